# Optimizing a Trainium2 kernel written in Bass

```python
import jax, jax.numpy as jnp
from jax import lax
import numpy as np

D_MODEL = 1024
BATCH = 4
SEQ = 8192
DEPTH = 2

GRID_W = 64
CTX_LEN = 256
N_EVEN = (DEPTH + 1) // 2
N_ODD = DEPTH // 2
LRU_W = D_MODEL // 2
LRU_HEADS = 8
LRU_HD = LRU_W // LRU_HEADS
LRU_C = 8.0
CONV_W = 4
CONV_LEFT = 2
FFT_W = D_MODEL - LRU_W
FFT_GROUPS = 8
FFT_GD = FFT_W // FFT_GROUPS
AB_IN = 2 * LRU_W + FFT_W
NA_HEADS = 16
NA_HD = D_MODEL // NA_HEADS
NA_KH = 8
NA_KW = 16
FFN_HIDDEN = -(-8 * D_MODEL // (3 * 256)) * 256
EPS = 1e-6

kernel_name = 'hybrid_rglru_fnet_natten_dit_block'


def _rmsnorm(x, g):
    xf = x.astype(jnp.float32)
    y = xf * lax.rsqrt(jnp.mean(xf * xf, axis=-1, keepdims=True) + EPS) * g.astype(jnp.float32)
    return y.astype(x.dtype)


def _modulate(x, g, shift, scale):
    return _rmsnorm(x, g) * (1 + scale) + shift


def _swiglu(h, wg, wu, wd):
    return (jax.nn.silu(h @ wg) * (h @ wu)) @ wd


def _dwconv(u, w, b):
    T = u.shape[1]
    up = jnp.pad(u, ((0, 0), (CONV_LEFT, CONV_W - 1 - CONV_LEFT), (0, 0)))
    out = b
    for k in range(CONV_W):
        out = out + up[:, k:k + T] * w[k]
    return out


def _rglru_coeffs(u, w_a, b_a, w_i, b_i, lam):
    B, T, C = u.shape
    uh = u.reshape(B, T, LRU_HEADS, LRU_HD)
    r = jax.nn.sigmoid(jnp.einsum('bthi,hij->bthj', uh, w_a).reshape(B, T, C) + b_a)
    i = jax.nn.sigmoid(jnp.einsum('bthi,hij->bthj', uh, w_i).reshape(B, T, C) + b_i)
    log_a = -LRU_C * r * jax.nn.softplus(-lam.astype(jnp.float32))
    a = jnp.exp(log_a)
    bterm = jnp.sqrt(-jnp.expm1(2.0 * log_a)) * (i * u)
    return a, bterm


def _affine_combine(l, r):
    return (l[0] * r[0], r[0] * l[1] + r[1])


def _linear_scan(a, b, h0, reverse):
    if reverse:
        a, b = jnp.flip(a, 1), jnp.flip(b, 1)
    A, Bc = lax.associative_scan(_affine_combine, (a, b), axis=1)
    h = Bc + A * h0[:, None, :]
    return jnp.flip(h, 1) if reverse else h


def _fourier(f):
    B, T, _ = f.shape
    fg = f.astype(jnp.float32).reshape(B, T, FFT_GROUPS, FFT_GD)
    out = jnp.fft.fft2(fg, axes=(1, 3), norm='ortho').real
    return out.reshape(B, T, FFT_W).astype(f.dtype)


def _ab_mixer(hx, hc, w_in, conv_w, conv_b, w_a, b_a, w_i, b_i, lam, w_out, need_ctx):
    zx = hx @ w_in
    zc = hc @ w_in
    ux, gx, fx = zx[..., :LRU_W], zx[..., LRU_W:2 * LRU_W], zx[..., 2 * LRU_W:]
    uc, gc, fc = zc[..., :LRU_W], zc[..., LRU_W:2 * LRU_W], zc[..., 2 * LRU_W:]
    ux = _dwconv(ux, conv_w, conv_b).astype(jnp.float32)
    uc = _dwconv(uc, conv_w, conv_b).astype(jnp.float32)
    rx = jnp.zeros_like(ux)
    hc_dirs = []
    for d in range(2):
        rev = d == 1
        ac, bc = _rglru_coeffs(uc, w_a[d], b_a[d], w_i[d], b_i[d], lam[d])
        hcs = _linear_scan(ac, bc, jnp.zeros_like(uc[:, 0]), rev)
        h_end = hcs[:, 0] if rev else hcs[:, -1]
        ax, bx = _rglru_coeffs(ux, w_a[d], b_a[d], w_i[d], b_i[d], lam[d])
        rx = rx + _linear_scan(ax, bx, h_end, rev)
        hc_dirs.append(hcs)
    yx = jnp.concatenate([rx.astype(hx.dtype) * jax.nn.gelu(gx), _fourier(fx)], axis=-1) @ w_out
    yc = None
    if need_ctx:
        rc = (hc_dirs[0] + hc_dirs[1]).astype(hc.dtype)
        yc = jnp.concatenate([rc * jax.nn.gelu(gc), _fourier(fc)], axis=-1) @ w_out
    return yx, yc


def _na_mixer(hx, hc, w_qkv, rpb, w_out, need_ctx):
    B, S, D = hx.shape
    L = hc.shape[1]
    R = S // GRID_W
    kh = min(NA_KH, R)
    scale = NA_HD ** -0.5
    qkv = (hx @ w_qkv).reshape(B, R, GRID_W, 3, NA_HEADS, NA_HD)
    q = qkv[:, :, :, 0] * scale
    k = qkv[:, :, :, 1]
    v = qkv[:, :, :, 2]
    qkv_c = (hc @ w_qkv).reshape(B, L, 3, NA_HEADS, NA_HD)
    qc, kc, vc = qkv_c[:, :, 0] * scale, qkv_c[:, :, 1], qkv_c[:, :, 2]
    row_start = jnp.clip(jnp.arange(R) - kh // 2, 0, R - kh)
    col_start = jnp.clip(jnp.arange(GRID_W) - NA_KW // 2, 0, GRID_W - NA_KW)
    col_idx = col_start[:, None] + jnp.arange(NA_KW)[None, :]
    dc = col_idx - jnp.arange(GRID_W)[:, None] + NA_KW - 1
    rpb_cols = rpb[:, :, dc]
    n_loc = kh * NA_KW

    def row_attend(args):
        r, q_r = args
        rs = row_start[r]
        k_r = lax.dynamic_slice_in_dim(k, rs, kh, axis=1)[:, :, col_idx]
        v_r = lax.dynamic_slice_in_dim(v, rs, kh, axis=1)[:, :, col_idx]
        dr = rs + jnp.arange(kh) - r + NA_KH - 1
        bias = jnp.transpose(rpb_cols[:, dr], (0, 2, 1, 3)).astype(jnp.float32)
        s_loc = jnp.einsum('bqhd,bkqjhd->bhqkj', q_r, k_r).astype(jnp.float32) + bias
        s_ctx = jnp.einsum('bqhd,blhd->bhql', q_r, kc).astype(jnp.float32)
        s = jnp.concatenate([s_loc.reshape(B, NA_HEADS, GRID_W, n_loc), s_ctx], axis=-1)
        p = jax.nn.softmax(s, axis=-1).astype(v.dtype)
        p_loc = p[..., :n_loc].reshape(B, NA_HEADS, GRID_W, kh, NA_KW)
        return (jnp.einsum('bhqkj,bkqjhd->bqhd', p_loc, v_r)
                + jnp.einsum('bhql,blhd->bqhd', p[..., n_loc:], vc))

    o = lax.map(row_attend, (jnp.arange(R), jnp.moveaxis(q, 1, 0)))
    yx = jnp.moveaxis(o, 0, 1).reshape(B, S, D) @ w_out
    yc = None
    if need_ctx:
        pc = jax.nn.softmax(jnp.einsum('bqhd,blhd->bhql', qc, kc).astype(jnp.float32), axis=-1)
        yc = jnp.einsum('bhql,blhd->bqhd', pc.astype(vc.dtype), vc).reshape(B, L, D) @ w_out
    return yx, yc


def _normal(k, shape, s):
    return jax.random.normal(k, shape, jnp.float32) * s


def setup_inputs(seed: int = 0) -> dict:
    key = jax.random.key(seed)
    ks = jax.random.split(key, 25)
    D, F = D_MODEL, FFN_HIDDEN
    u = jax.random.uniform(ks[20], (N_EVEN, 2, LRU_W), jnp.float32, minval=0.9, maxval=0.999)
    a0 = u ** (1.0 / LRU_C)
    return {
        'x': _normal(ks[0], (BATCH, SEQ, D), 1.0),
        'c': _normal(ks[1], (BATCH, D), 1.0),
        'ctx': _normal(ks[2], (BATCH, CTX_LEN, D), 1.0),
        'c_ctx': _normal(ks[3], (D,), 1.0),
        'w_mod': _normal(ks[4], (DEPTH, D, 6 * D), 0.5 * D ** -0.5),
        'b_mod': _normal(ks[5], (DEPTH, 6 * D), 0.02),
        'g_pre_mix': 1.0 + _normal(ks[6], (DEPTH, D), 0.02),
        'g_post_mix': 1.0 + _normal(ks[7], (DEPTH, D), 0.02),
        'g_pre_ffn': 1.0 + _normal(ks[8], (DEPTH, D), 0.02),
        'g_post_ffn': 1.0 + _normal(ks[9], (DEPTH, D), 0.02),
        'w_ffn_gate': _normal(ks[10], (DEPTH, D, F), D ** -0.5),
        'w_ffn_up': _normal(ks[11], (DEPTH, D, F), D ** -0.5),
        'w_ffn_down': _normal(ks[12], (DEPTH, F, D), F ** -0.5),
        'w_in_ab': _normal(ks[13], (N_EVEN, D, AB_IN), D ** -0.5),
        'conv_w': _normal(ks[14], (N_EVEN, CONV_W, LRU_W), CONV_W ** -0.5),
        'conv_b': _normal(ks[15], (N_EVEN, LRU_W), 0.02),
        'lru_w_a': _normal(ks[16], (N_EVEN, 2, LRU_HEADS, LRU_HD, LRU_HD), LRU_HD ** -0.5),
        'lru_b_a': _normal(ks[17], (N_EVEN, 2, LRU_W), 0.02),
        'lru_w_i': _normal(ks[18], (N_EVEN, 2, LRU_HEADS, LRU_HD, LRU_HD), LRU_HD ** -0.5),
        'lru_b_i': _normal(ks[19], (N_EVEN, 2, LRU_W), 0.02),
        'lru_lam': jnp.log(a0) - jnp.log1p(-a0),
        'w_out_ab': _normal(ks[21], (N_EVEN, LRU_W + FFT_W, D), (LRU_W + FFT_W) ** -0.5),
        'w_qkv_na': _normal(ks[22], (N_ODD, D, 3 * D), D ** -0.5),
        'rpb_na': _normal(ks[23], (N_ODD, NA_HEADS, 2 * NA_KH - 1, 2 * NA_KW - 1), 0.1),
        'w_out_na': _normal(ks[24], (N_ODD, D, D), D ** -0.5),
    }


def reference(x, c, ctx, c_ctx, w_mod, b_mod, g_pre_mix, g_post_mix, g_pre_ffn, g_post_ffn,
              w_ffn_gate, w_ffn_up, w_ffn_down, w_in_ab, conv_w, conv_b, lru_w_a, lru_b_a,
              lru_w_i, lru_b_i, lru_lam, w_out_ab, w_qkv_na, rpb_na, w_out_na):
    c_act = jax.nn.silu(c)
    cc_act = jax.nn.silu(c_ctx)
    for l in range(DEPTH):
        last = l == DEPTH - 1
        mx = (c_act @ w_mod[l] + b_mod[l])[:, None, :]
        mc = (cc_act @ w_mod[l] + b_mod[l])[None, None, :]
        sh1, sc1, gt1, sh2, sc2, gt2 = jnp.split(mx, 6, axis=-1)
        csh1, csc1, cgt1, csh2, csc2, cgt2 = jnp.split(mc, 6, axis=-1)
        hx = _modulate(x, g_pre_mix[l], sh1, sc1)
        hc = _modulate(ctx, g_pre_mix[l], csh1, csc1)
        if l % 2 == 0:
            e = l // 2
            yx, yc = _ab_mixer(hx, hc, w_in_ab[e], conv_w[e], conv_b[e], lru_w_a[e], lru_b_a[e],
                               lru_w_i[e], lru_b_i[e], lru_lam[e], w_out_ab[e], not last)
        else:
            o = l // 2
            yx, yc = _na_mixer(hx, hc, w_qkv_na[o], rpb_na[o], w_out_na[o], not last)
        x = x + gt1 * _rmsnorm(yx, g_post_mix[l])
        fx = _modulate(x, g_pre_ffn[l], sh2, sc2)
        x = x + gt2 * _rmsnorm(_swiglu(fx, w_ffn_gate[l], w_ffn_up[l], w_ffn_down[l]), g_post_ffn[l])
        if not last:
            ctx = ctx + cgt1 * _rmsnorm(yc, g_post_mix[l])
            fc = _modulate(ctx, g_pre_ffn[l], csh2, csc2)
            ctx = ctx + cgt2 * _rmsnorm(_swiglu(fc, w_ffn_gate[l], w_ffn_up[l], w_ffn_down[l]), g_post_ffn[l])
    return x
```

```python
import math
from contextlib import ExitStack
import numpy as np
import ml_dtypes
import concourse.bass as bass
import concourse.mybir as mybir
from concourse.bass_utils import run_bass_kernel_spmd

F32 = mybir.dt.float32
BF16 = mybir.dt.bfloat16
AF = mybir.ActivationFunctionType
ALU = mybir.AluOpType
NPBF = ml_dtypes.bfloat16

D = 1024
S = 8192
LC = 256
NT = S + LC
FH = 2816
NJ = FH // 128
EPS = 1e-6
NCORES = 4
NEG = -30000.0


class Em:
    LIMIT = 30000
    NDS = 8

    def __init__(self, nc, es):
        self.nc = nc
        self.es = es
        self.eng = dict(pe=nc.tensor, act=nc.scalar, dve=nc.vector, pool=nc.gpsimd, sp=nc.sync)
        self.sem = {}
        self.semkey = {}
        self.cnt = {}
        self.nsem = 0
        for e in self.eng:
            self._newsem(e)
        self.waited = {e: {} for e in self.eng}
        self.lastw = {}
        self.readers = {}
        self.dsem = {}
        self.ndma = {}
        for q in ("sp", "pool", "act"):
            self.dsem[q] = [es.enter_context(nc.semaphore(f"d{q}{i}")) for i in range(self.NDS)]
            self.ndma[q] = 0

    def _newsem(self, e):
        self.nsem += 1
        self.sem[e] = self.es.enter_context(self.nc.semaphore(f"s{e}{self.nsem}"))
        self.semkey[e] = (e, self.nsem)
        self.cnt[e] = 0

    def _deps(self, engine, R, W):
        deps = {}
        for k in list(R) + list(W):
            t = self.lastw.get(k)
            if t is not None:
                if t[0] not in deps or deps[t[0]][2] < t[2]:
                    deps[t[0]] = t
        for k in W:
            for t in self.readers.get(k, {}).values():
                if t[0] not in deps or deps[t[0]][2] < t[2]:
                    deps[t[0]] = t
        e = self.eng[engine]
        for sk, t in deps.items():
            if engine == "pe" and t[3] == "pe":
                continue
            if self.waited[engine].get(sk, 0) >= t[2]:
                continue
            e.wait_ge(t[1], t[2])
            self.waited[engine][sk] = t[2]

    def _record(self, tok, R, W):
        for k in W:
            self.lastw[k] = tok
            self.readers[k] = {}
        for k in R:
            d = self.readers.setdefault(k, {})
            if tok[0] not in d or d[tok[0]][2] < tok[2]:
                d[tok[0]] = tok

    def op(self, engine, fn, R=(), W=(), inc=True):
        self._deps(engine, R, W)
        ins = fn()
        if inc:
            ins.then_inc(self.sem[engine], 1)
            self.cnt[engine] += 1
            tok = (self.semkey[engine], self.sem[engine], self.cnt[engine], engine)
            self._record(tok, R, W)
            if self.cnt[engine] >= self.LIMIT:
                self._newsem(engine)
        else:
            tok = (self.semkey[engine], self.sem[engine], self.cnt[engine] + 1, engine)
            self._record(tok, R, W)
        return ins

    def dma(self, q, out, in_, R=(), W=(), **kw):
        self._deps(q, R, W)
        i = self.ndma[q]
        self.ndma[q] += 1
        sem = self.dsem[q][i % self.NDS]
        rnd = i // self.NDS
        sk = ("dma", q, i % self.NDS)
        if rnd > 0 and self.waited[q].get(sk, 0) < 16 * rnd:
            self.eng[q].wait_ge(sem, 16 * rnd)
            self.waited[q][sk] = 16 * rnd
        self.eng[q].dma_start(out=out, in_=in_, **kw).then_inc(sem, 16)
        tok = (sk, sem, 16 * (rnd + 1), None)
        self._record(tok, R, W)

    def finish(self):
        sp = self.eng["sp"]
        for q in self.dsem:
            n = self.ndma[q]
            for s in range(self.NDS):
                uses = (n - s + self.NDS - 1) // self.NDS if n > s else 0
                if uses > 0:
                    sp.wait_ge(self.dsem[q][s], 16 * uses)
        for e in ("pe", "act", "dve", "pool"):
            if self.cnt[e] > 0:
                sp.wait_ge(self.sem[e], self.cnt[e])


def _tables():
    t = {}
    t["ident"] = np.eye(128, dtype=np.float32)
    t["identb"] = np.eye(128, dtype=np.float32).astype(NPBF)
    a = np.arange(128, dtype=np.float64)
    ang = 2 * np.pi * np.outer(a, a) / 128.0
    t["T1"] = np.concatenate([np.cos(ang), -np.sin(ang)], axis=1).astype(NPBF)
    t2 = np.arange(64, dtype=np.float64)[:, None, None]
    k1 = np.arange(128, dtype=np.float64)[None, :, None]
    k2 = np.arange(64, dtype=np.float64)[None, None, :]
    ph = 2 * np.pi * (t2 * k2 / 64.0 + t2 * k1 / 8192.0)
    Mc, Ms = np.cos(ph), np.sin(ph)
    t["M2"] = np.concatenate([Ms, Mc, -Ms], axis=2).astype(NPBF)
    c = np.arange(64, dtype=np.float64)
    angc = 2 * np.pi * np.outer(c, c) / 64.0
    Cc, Sc = np.cos(angc), np.sin(angc)
    z = np.zeros((64, 64))
    Cbd = np.block([[Cc, z], [z, Cc]])
    Sbd = np.block([[Sc, z], [z, Sc]])
    sx = 1.0 / math.sqrt(8192.0 * 64.0)
    t["CSf"] = (np.concatenate([Cbd, Sbd], axis=1) * sx).astype(NPBF)
    sc_ = 1.0 / math.sqrt(256.0 * 64.0)
    t["CSc"] = (np.concatenate([Cbd, Sbd], axis=1) * sc_).astype(NPBF)
    p = np.arange(256, dtype=np.float64)
    angp = 2 * np.pi * np.outer(p, p) / 256.0
    T256 = np.concatenate([np.cos(angp), -np.sin(angp)], axis=1)
    t["T256"] = T256.reshape(2, 128, 512).transpose(1, 0, 2).copy().astype(NPBF)
    return t


_TABLES = None


def _bias_tables(rpb):
    H = 16
    out = np.full((5, 5 * 128, H, 128), NEG, dtype=np.float32)
    kr = np.arange(10)[:, None, None, None]
    kc = np.arange(64)[None, :, None, None]
    qr = np.arange(2)[None, None, :, None]
    qc = np.arange(64)[None, None, None, :]
    cs = np.clip(qc - 8, 0, 48)
    for vi, (gp, ks) in enumerate([(0, 0), (2, 0), (60, 56), (124, 118), (126, 118)]):
        gq = gp + qr
        rs = np.clip(gq - 4, 0, 120)
        gk = ks + kr
        valid = (gk >= rs) & (gk < rs + 8) & (kc >= cs) & (kc < cs + 16)
        dr = np.clip(gk - gq + 7, 0, 14)
        dc = np.clip(kc - qc + 15, 0, 30)
        valid, dr, dc = np.broadcast_arrays(valid, dr, dc)
        vals = rpb[:, dr, dc]
        vals = np.where(valid[None], vals, NEG)
        out[vi] = vals.transpose(1, 2, 0, 3, 4).reshape(640, H, 128)
    bt = out.reshape(5, 5, 128, H, 128).transpose(0, 2, 3, 1, 4)
    return np.ascontiguousarray(bt).astype(NPBF)


def _col(v, nchunk):
    return np.ascontiguousarray(np.asarray(v, np.float32).reshape(nchunk, 128).T)


def build(stop="all", debug=False):
    nc = bass.Bass("TRN2", target_bir_lowering=False)

    def din(name, shape, dt=F32):
        return nc.dram_tensor(name, list(shape), dt, kind="ExternalInput").ap()

    skind = "ExternalOutput" if debug else "Internal"

    def dscr(name, shape, dt):
        return nc.dram_tensor(name, list(shape), dt, kind=skind).ap()

    x_d = din("x", [S, D]); ctx_d = din("ctx", [LC, D]); ccols_d = din("ccols", [128, 16])
    wmod_d = din("w_mod", [2, D, 6 * D]); bmodc_d = din("bmodc", [128, 2, 48]); bmod_d = din("b_mod", [2, 6 * D])
    gcols_d = din("gcols", [128, 2, 2, 8]); gpm_d = din("g_post_mix", [2, D]); gpf_d = din("g_post_ffn", [2, D])
    win_d = din("w_in_ab", [D, 1536]); woab_d = din("w_out_ab", [D, D])
    wg_d = din("w_ffn_gate", [2, D, FH]); wu_d = din("w_ffn_up", [2, D, FH]); wd_d = din("w_ffn_down", [2, FH, D])
    wqkv_d = din("w_qkv_na", [D, 3 * D]); wona_d = din("w_out_na", [D, D])
    wbd_d = din("wbd", [2, 2, 4, 128, 128]); lcols_d = din("lcols", [128, 4, 2, 3]); convc_d = din("convc", [128, 4, 5])
    ident_d = din("ident", [128, 128]); identb_d = din("identb", [128, 128], BF16)
    T1_d = din("T1", [128, 256], BF16); M2_d = din("M2", [64, 128, 192], BF16); CSf_d = din("CSf", [128, 256], BF16)
    CSc_d = din("CSc", [128, 256], BF16); T256_d = din("T256", [128, 2, 512], BF16)
    BT_d = din("BT", [5, 128, 16, 5, 128], BF16)
    out_d = nc.dram_tensor("out", [S, D], F32, kind="ExternalOutput").ap()

    UG = dscr("UG", [8, 128, S], F32)
    UGc = dscr("UGc", [8, 128, LC], F32)
    MIXT = dscr("MIXT", [D, NT], BF16)
    GROW = dscr("GROW", [2, 2, 2, D], F32)
    X1 = dscr("X1", [S, D], F32); C1 = dscr("C1", [LC, D], F32)
    X2 = dscr("X2", [S, D], F32); C2 = dscr("C2", [LC, D], F32)
    X3 = dscr("X3", [S, D], F32)
    QT = dscr("QT", [D, S], BF16); KT = dscr("KT", [D, NT], BF16); VD = dscr("VD", [NT, D], BF16)

    with ExitStack() as es:
        em = Em(nc, es)

        def barrier():
            for e in ("pe", "act", "dve", "pool", "sp"):
                eng = em.eng[e]
                for f in ("pe", "act", "dve", "pool"):
                    if f != e and em.cnt[f] > 0 and em.waited[e].get(em.semkey[f], 0) < em.cnt[f]:
                        eng.wait_ge(em.sem[f], em.cnt[f]); em.waited[e][em.semkey[f]] = em.cnt[f]
                for q in em.dsem:
                    n = em.ndma[q]
                    for s_ in range(em.NDS):
                        uses = (n - s_ + em.NDS - 1) // em.NDS if n > s_ else 0
                        sk = ("dma", q, s_)
                        if uses > 0 and em.waited[e].get(sk, 0) < 16 * uses:
                            eng.wait_ge(em.dsem[q][s_], 16 * uses); em.waited[e][sk] = 16 * uses

        PS = es.enter_context(nc.psum_tensor("PS", [128, 8, 512], F32))
        ident = es.enter_context(nc.sbuf_tensor("ident_s", [128, 128], F32))
        identb = es.enter_context(nc.sbuf_tensor("identb_s", [128, 128], BF16))
        MODC = es.enter_context(nc.sbuf_tensor("MODC", [128, 2, 2, 2, 2, 8], F32))
        mhalf = es.enter_context(nc.sbuf_tensor("mhalf", [128, 8], F32))
        em.dma("sp", ident[:], ident_d, W=["ident"])
        em.dma("sp", identb[:], identb_d, W=["identb"])
        em.op("dve", lambda: nc.vector.memset(mhalf[:], -0.5), W=["mhalf"])

        V = nc.vector; A = nc.scalar; G = nc.gpsimd; T = nc.tensor

        def pbank(b):
            return ("ps", b)

        def rstd_from_ssq(ssq, rs, n, tag):
            em.op("dve", lambda: V.tensor_scalar(out=rs[:, 0:n], in0=ssq[:, 0:n], scalar1=1.0 / D, scalar2=EPS,
                                                 op0=ALU.mult, op1=ALU.add), R=[tag + "ssq"], W=[tag + "rs"])
            em.op("pool", lambda: G.tensor_tensor(out=rs[:, 0:n], in0=rs[:, 0:n], in1=mhalf[:, 0:n], op=ALU.pow),
                  R=[tag + "rs", "mhalf"], W=[tag + "rs"])

        def prologue(xt, xkey, nsub, xs, xskey, Acol, Bcol, hT, hkey, ssq, rs, junk, tag, tb):
            for s in range(nsub):
                em.op("act", lambda s=s: A.activation(out=junk[:, :], in_=xt[:, s, :], func=AF.Square,
                                                      accum_out=ssq[:, s:s + 1]),
                      R=[xkey], W=[tag + "junk", tag + "ssq"])
            rstd_from_ssq(ssq, rs, nsub, tag)
            for s in range(nsub):
                em.op("act", lambda s=s: A.activation(out=xs[:, s, :], in_=xt[:, s, :], func=AF.Copy,
                                                      scale=rs[:, s:s + 1]),
                      R=[xkey, tag + "rs"], W=[xskey])
            for k in range(8):
                b = tb[k % len(tb)]
                for s in range(nsub):
                    em.op("pe", lambda s=s, k=k, b=b: T.transpose(PS[:, b, s * 128:(s + 1) * 128],
                                                                  xs[:, s, k * 128:(k + 1) * 128], ident[:]),
                          R=[xskey, "ident"], W=[pbank(b)], inc=(s == nsub - 1))
                em.op("act", lambda k=k, b=b: A.activation(out=hT[:, k, 0:nsub * 128], in_=PS[:, b, 0:nsub * 128],
                                                           func=AF.Identity, scale=Acol[:, k:k + 1],
                                                           bias=Bcol[:, k:k + 1]),
                      R=[pbank(b), "MODC"], W=[hkey])

        def epilogue(psb, xsub, xkey, Gb, gkey, ssq, rs, junk, tt, tag):
            yv = PS[:, psb:psb + 2, :]
            em.op("act", lambda: A.activation(out=junk[:, :], in_=yv, func=AF.Square, accum_out=ssq[:, 0:1]),
                  R=[pbank(psb), pbank(psb + 1)], W=[tag + "junk", tag + "ssq"])
            rstd_from_ssq(ssq, rs, 1, tag)
            em.op("dve", lambda: V.tensor_tensor(out=tt[:, :], in0=yv, in1=Gb, op=ALU.mult),
                  R=[pbank(psb), pbank(psb + 1), gkey], W=[tag + "tt"])
            em.op("dve", lambda: V.scalar_tensor_tensor(out=xsub, in0=tt[:, :], scalar=rs[:, 0:1], in1=xsub,
                                                        op0=ALU.mult, op1=ALU.add),
                  R=[tag + "tt", tag + "rs", xkey], W=[xkey])

        with ExitStack() as pes:
            sb = lambda n, s, d=F32: pes.enter_context(nc.sbuf_tensor(n, list(s), d))
            cc = sb("cc", [128, 16]); scc = sb("scc", [128, 16]); rhs2 = sb("rhs2", [128, 8, 2], BF16)
            bmc = sb("bmc", [128, 2, 48]); bm1 = sb("bm1", [128, 2, 48]); gco = sb("gco", [128, 2, 2, 8])
            bmrow = sb("bmrow", [2, 2, 2, D]); grow = sb("grow", [2, 2, 2, D]); grt = sb("grt", [2, 2, 2, D])
            wm = [sb(f"wm{i}", [128, 8, 512], BF16) for i in range(3)]
            em.dma("sp", cc[:], ccols_d, W=["cc"])
            em.dma("sp", bmc[:], bmodc_d, W=["bmc"])
            em.dma("sp", gco[:], gcols_d, W=["gco"])
            for l in range(2):
                for w_, (src, off) in enumerate([(bmod_d, 2 * D), (bmod_d, 5 * D)]):
                    em.dma("sp", bmrow[:, l, w_, :], src[l, off:off + D].partition_broadcast(2), W=["bmrow"])
                em.dma("sp", grow[:, l, 0, :], gpm_d[l, :].partition_broadcast(2), W=["grow"])
                em.dma("sp", grow[:, l, 1, :], gpf_d[l, :].partition_broadcast(2), W=["grow"])
            em.op("act", lambda: A.activation(out=scc[:], in_=cc[:], func=AF.Silu), R=["cc"], W=["scc"])
            em.op("dve", lambda: V.tensor_copy(out=rhs2[:, :, 0], in_=scc[:, 0:8]), R=["scc"], W=["rhs2"])
            em.op("dve", lambda: V.tensor_copy(out=rhs2[:, :, 1], in_=scc[:, 8:16]), R=["scc"], W=["rhs2"])
            em.op("dve", lambda: V.tensor_scalar(out=bm1[:], in0=bmc[:], scalar1=1.0, scalar2=None, op0=ALU.add),
                  R=["bmc"], W=["bm1"])
            it = 0
            for l in range(2):
                wsrc = wmod_d[l].rearrange("(k p) n -> p k n", p=128)
                for nb in range(12):
                    slot = it % 3; it += 1
                    wt = wm[slot]; wk = f"wm{slot}"
                    em.dma("pool", wt[:], wsrc[:, :, nb * 512:(nb + 1) * 512], W=[wk])
                    v = nb // 2; half = nb % 2
                    b = it % 4
                    if v in (2, 5):
                        w_ = 0 if v == 2 else 1
                        for k in range(8):
                            em.op("pe", lambda k=k, b=b, wt=wt: T.matmul(PS[0:2, b, :], lhsT=rhs2[:, k, :], rhs=wt[:, k, :],
                                                                        start=(k == 0), stop=(k == 7)),
                                  R=["rhs2", wk], W=[pbank(b)], inc=(k == 7))
                        dst = grt[:, l, w_, half * 512:(half + 1) * 512]
                        em.op("dve", lambda b=b, dst=dst, l=l, w_=w_, half=half: V.tensor_tensor(
                            out=dst, in0=PS[0:2, b, :], in1=bmrow[:, l, w_, half * 512:(half + 1) * 512], op=ALU.add),
                              R=[pbank(b), "bmrow"], W=["grt"])
                        em.op("dve", lambda dst=dst, l=l, w_=w_, half=half: V.tensor_tensor(
                            out=dst, in0=dst, in1=grow[:, l, w_, half * 512:(half + 1) * 512], op=ALU.mult),
                              R=["grt", "grow"], W=["grt"])
                        if half == 1:
                            em.dma("sp", GROW[l, w_, :, :], grt[:, l, w_, :], R=["grt"], W=[("GROW", l, w_)])
                    else:
                        sub = 0 if v < 2 else 1
                        isA = v in (1, 4)
                        for m in range(4):
                            ch = half * 4 + m
                            for k in range(8):
                                em.op("pe", lambda k=k, b=b, m=m, wt=wt: T.matmul(
                                    PS[:, b, 2 * m:2 * m + 2], lhsT=wt[:, k, m * 128:(m + 1) * 128], rhs=rhs2[:, k, :],
                                    start=(k == 0), stop=(k == 7)), R=["rhs2", wk], W=[pbank(b)], inc=(k == 7))
                            dst = MODC[:, l, sub, 0 if isA else 1, :, ch]
                            if isA:
                                em.op("dve", lambda b=b, m=m, dst=dst, l=l, v=v, ch=ch, sub=sub: V.tensor_scalar(
                                    out=dst, in0=PS[:, b, 2 * m:2 * m + 2], scalar1=bm1[:, l, v * 8 + ch:v * 8 + ch + 1],
                                    scalar2=gco[:, l, sub, ch:ch + 1], op0=ALU.add, op1=ALU.mult),
                                      R=[pbank(b), "bm1", "gco"], W=["MODC"])
                            else:
                                em.op("dve", lambda b=b, m=m, dst=dst, l=l, v=v, ch=ch: V.tensor_scalar(
                                    out=dst, in0=PS[:, b, 2 * m:2 * m + 2], scalar1=bmc[:, l, v * 8 + ch:v * 8 + ch + 1],
                                    scalar2=None, op0=ALU.add), R=[pbank(b), "bmc"], W=["MODC"])
            if debug:
                MODCd = nc.dram_tensor("MODCd", [128, 128], F32, kind="ExternalOutput").ap()
                em.dma("sp", MODCd, MODC[:].rearrange("p a b c d e -> p (a b c d e)"), R=["MODC"], W=["MODCd"])
            barrier()
        if stop == "0":
            em.finish()
            return nc
        C = type("C", (), {})()
        C.__dict__.update(locals())
        for name, fn in PHASES:
            fn(C)
            barrier()
            if stop == name:
                break
        em.finish()
    return nc


PHASES = []


def _host_inputs(inp, b):
    global _TABLES
    if _TABLES is None:
        _TABLES = _tables()
    f = lambda a: np.ascontiguousarray(np.asarray(a, dtype=np.float32))
    m = {}
    m["x"] = f(inp["x"][b]); m["ctx"] = f(inp["ctx"][b])
    m["ccols"] = np.concatenate([_col(inp["c"][b], 8), _col(inp["c_ctx"], 8)], axis=1)
    m["w_mod"] = f(inp["w_mod"]); m["b_mod"] = f(inp["b_mod"])
    m["bmodc"] = np.ascontiguousarray(np.stack([_col(inp["b_mod"][l], 48) for l in range(2)], axis=1))
    m["gcols"] = np.ascontiguousarray(np.stack(
        [np.stack([_col(inp["g_pre_mix"][l], 8), _col(inp["g_pre_ffn"][l], 8)], axis=1) for l in range(2)], axis=1))
    m["g_post_mix"] = f(inp["g_post_mix"]); m["g_post_ffn"] = f(inp["g_post_ffn"])
    m["w_in_ab"] = f(inp["w_in_ab"][0]); m["w_out_ab"] = f(inp["w_out_ab"][0])
    m["w_ffn_gate"] = f(inp["w_ffn_gate"]); m["w_ffn_up"] = f(inp["w_ffn_up"]); m["w_ffn_down"] = f(inp["w_ffn_down"])
    m["w_qkv_na"] = f(inp["w_qkv_na"][0]); m["w_out_na"] = f(inp["w_out_na"][0])
    wbd = np.zeros((2, 2, 4, 128, 128), np.float32)
    for gi, key in enumerate(["lru_w_a", "lru_w_i"]):
        w = np.asarray(inp[key][0], np.float32)
        for d in range(2):
            for c in range(4):
                wbd[gi, d, c, 0:64, 0:64] = w[d, 2 * c]
                wbd[gi, d, c, 64:128, 64:128] = w[d, 2 * c + 1]
    m["wbd"] = wbd
    lc = np.zeros((128, 4, 2, 3), np.float32)
    for d in range(2):
        lc[:, :, d, 0] = _col(inp["lru_b_a"][0][d], 4)
        lc[:, :, d, 1] = _col(inp["lru_b_i"][0][d], 4)
        lc[:, :, d, 2] = _col(inp["lru_lam"][0][d], 4)
    m["lcols"] = lc
    cv = np.zeros((128, 4, 5), np.float32)
    for k in range(4):
        cv[:, :, k] = _col(inp["conv_w"][0][k], 4)
    cv[:, :, 4] = _col(inp["conv_b"][0], 4)
    m["convc"] = cv
    for k in ("ident", "identb", "T1", "M2", "CSf", "CSc", "T256"):
        m[k] = _TABLES[k]
    m["BT"] = _bias_tables(np.asarray(inp["rpb_na"][0], np.float32))
    return m


_NC_CACHE = {}


def kernel(**inputs):
    if "full" not in _NC_CACHE:
        _NC_CACHE["full"] = build()
    nc = _NC_CACHE["full"]
    in_maps = [_host_inputs(inputs, b) for b in range(NCORES)]
    res = run_bass_kernel_spmd(nc, in_maps, core_ids=list(range(NCORES)))
    return np.stack([np.asarray(res.results[b]["out"], dtype=np.float32) for b in range(NCORES)], axis=0)


def _wload(C, dst, src_view, key, nk, ncols, step=512):
    for c0 in range(0, ncols, step):
        c1 = min(ncols, c0 + step)
        C.em.dma("pool", dst[:, :, c0:c1], src_view[:, :, c0:c1], W=[key])


def phase_A(C):
    nc, em, PS = C.nc, C.em, C.PS
    V, A, G, T = nc.vector, nc.scalar, nc.gpsimd, nc.tensor
    pes = C.es_mix = ExitStack()
    C.F = pes.enter_context(nc.sbuf_tensor("Fbuf", [128, 64, 512], BF16))
    C.fTc = pes.enter_context(nc.sbuf_tensor("fTc", [128, 4, LC], BF16))
    F, fTc = C.F, C.fTc
    with ExitStack() as ps_:
        sb = lambda n, s, d=F32: ps_.enter_context(nc.sbuf_tensor(n, list(s), d))
        win = sb("win", [128, 8, 1536], BF16)
        xt = [sb(f"Axt{i}", [128, 4, D]) for i in range(2)]
        hT = [sb(f"AhT{i}", [128, 8, 512], BF16) for i in range(2)]
        ugst = [sb(f"Aug{i}", [128, 8, 512]) for i in range(2)]
        ssq = sb("Assq", [128, 4]); rs = sb("Ars", [128, 4]); junk = sb("Ajunk", [128, D], BF16)
        _wload(C, win, C.win_d.rearrange("(k p) n -> p k n", p=128), "win", 8, 1536)
        xsrc = C.x_d.rearrange("(t1 t2) d -> t1 t2 d", t2=64)
        Acol = C.MODC[:, 0, 0, 0, 0, :]; Bcol = C.MODC[:, 0, 0, 1, 0, :]
        AcolC = C.MODC[:, 0, 0, 0, 1, :]; BcolC = C.MODC[:, 0, 0, 1, 1, :]
        em.dma("sp", xt[0][:], xsrc[:, 0:4, :], W=["Axt0"])
        ev = 0
        for j in range(17):
            sl = j % 2
            isctx = j == 16
            nsub = 2 if isctx else 4
            n = nsub * 128
            if j + 1 < 16:
                em.dma("sp", xt[1 - sl][:], xsrc[:, 4 * (j + 1):4 * (j + 2), :], W=[f"Axt{1 - sl}"])
            elif j + 1 == 16:
                em.dma("sp", xt[1 - sl][:, 0:2, :], C.ctx_d.rearrange("(s p) d -> p s d", p=128), W=[f"Axt{1 - sl}"])
            C.prologue(xt[sl], f"Axt{sl}", nsub, xt[sl], f"Axt{sl}", AcolC if isctx else Acol, BcolC if isctx else Bcol,
                       hT[sl], f"AhT{sl}", ssq, rs, junk, "A", [0, 1])
            for oc in range(8):
                b = 2 + oc % 3
                for k in range(8):
                    em.op("pe", lambda k=k, b=b, oc=oc, sl=sl, n=n: T.matmul(
                        PS[:, b, 0:n], lhsT=win[:, k, oc * 128:(oc + 1) * 128], rhs=hT[sl][:, k, 0:n],
                        start=(k == 0), stop=(k == 7)), R=["win", f"AhT{sl}"], W=[("ps", b)], inc=(k == 7))
                ev += 1
                if ev % 2:
                    em.op("dve", lambda b=b, oc=oc, sl=sl, n=n: V.tensor_copy(out=ugst[sl][:, oc, 0:n], in_=PS[:, b, 0:n]),
                          R=[("ps", b)], W=[f"Aug{sl}"])
                else:
                    em.op("act", lambda b=b, oc=oc, sl=sl, n=n: A.copy(out=ugst[sl][:, oc, 0:n], in_=PS[:, b, 0:n]),
                          R=[("ps", b)], W=[f"Aug{sl}"])
            if isctx:
                em.dma("pool", C.UGc.rearrange("c p n -> p c n"), ugst[sl][:, :, 0:LC], R=[f"Aug{sl}"], W=["UGc"])
                for fc in range(4):
                    b = 5 + fc % 3
                    for k in range(8):
                        em.op("pe", lambda k=k, b=b, fc=fc, sl=sl: T.matmul(
                            PS[:, b, 0:LC], lhsT=win[:, k, 1024 + fc * 128:1024 + (fc + 1) * 128], rhs=hT[sl][:, k, 0:LC],
                            start=(k == 0), stop=(k == 7)), R=["win", f"AhT{sl}"], W=[("ps", b)], inc=(k == 7))
                    em.op("dve", lambda b=b, fc=fc: V.tensor_copy(out=fTc[:, fc, :], in_=PS[:, b, 0:LC]),
                          R=[("ps", b)], W=["fTc"])
            else:
                em.dma("pool", C.UG[:, :, j * 512:(j + 1) * 512].rearrange("c p n -> p c n"), ugst[sl][:],
                       R=[f"Aug{sl}"], W=[("UG", j)])
                for s in range(4):
                    b = 5 + s % 3
                    for k in range(8):
                        em.op("pe", lambda k=k, b=b, s=s, sl=sl: T.matmul(
                            PS[:, b, :], lhsT=hT[sl][:, k, s * 128:(s + 1) * 128], rhs=win[:, k, 1024:1536],
                            start=(k == 0), stop=(k == 7)), R=["win", f"AhT{sl}"], W=[("ps", b)], inc=(k == 7))
                    ev += 1
                    if ev % 2:
                        em.op("dve", lambda b=b, s=s, j=j: V.tensor_copy(out=F[:, 4 * j + s, :], in_=PS[:, b, :]),
                              R=[("ps", b)], W=["F"])
                    else:
                        em.op("act", lambda b=b, s=s, j=j: A.copy(out=F[:, 4 * j + s, :], in_=PS[:, b, :]),
                              R=[("ps", b)], W=["F"])


def phase_C(C):
    nc, em, PS, F, fTc = C.nc, C.em, C.PS, C.F, C.fTc
    V, A, G, T = nc.vector, nc.scalar, nc.gpsimd, nc.tensor
    with ExitStack() as ps_:
        sb = lambda n, s, d=F32: ps_.enter_context(nc.sbuf_tensor(n, list(s), d))
        T1 = sb("T1s", [128, 256], BF16); M2 = sb("M2s", [64, 128, 192], BF16); CSf = sb("CSfs", [128, 256], BF16)
        CSc = sb("CScs", [128, 256], BF16); T256 = sb("T256s", [128, 2, 512], BF16)
        Ast = sb("Ast", [64, 64, 256], BF16); Y = sb("Ybuf", [128, 2, S], BF16)
        fst = [sb(f"fst{i}", [128, 2048], BF16) for i in range(2)]
        Gc = sb("Gcb", [128, 2, 256], BF16); fcs = sb("fcs", [128, LC], BF16)
        for dst, src, k in ((T1, C.T1_d, "T1"), (M2, C.M2_d, "M2"), (CSf, C.CSf_d, "CSf"), (CSc, C.CSc_d, "CSc"),
                            (T256, C.T256_d, "T256")):
            em.dma("sp", dst[:], src, W=[k])
        ev = 0
        for cc in range(4):
            for hc in range(2):
                for g4 in range(16):
                    b = 2 * (g4 % 2)
                    for q in range(4):
                        ch = cc * 128 + hc * 64 + g4 * 4 + q
                        em.op("pe", lambda b=b, q=q, ch=ch: T.matmul(
                            PS[0:64, b + q // 2, (q % 2) * 256:(q % 2) * 256 + 256], lhsT=F[:, :, ch], rhs=T1[:, :],
                            start=True, stop=True), R=["F", "T1"], W=[("ps", b), ("ps", b + 1)], inc=(q == 3))
                    ev += 1
                    dst = Ast[:, g4 * 4:(g4 + 1) * 4, :].rearrange("p a b -> p (a b)")
                    src = PS[0:64, b:b + 2, :].rearrange("p a b -> p (a b)")
                    if ev % 2:
                        em.op("dve", lambda dst=dst, src=src: V.tensor_copy(out=dst, in_=src),
                              R=[("ps", b), ("ps", b + 1)], W=["Ast"])
                    else:
                        em.op("act", lambda dst=dst, src=src: A.copy(out=dst, in_=src),
                              R=[("ps", b), ("ps", b + 1)], W=["Ast"])
                for kb in range(32):
                    b = 4 + kb % 2
                    for q in range(4):
                        k1 = kb * 4 + q
                        o = PS[hc * 64:(hc + 1) * 64, b, q * 128:(q + 1) * 128]
                        em.op("pe", lambda o=o, k1=k1: T.matmul(o, lhsT=Ast[:, :, k1], rhs=M2[:, k1, 64:192],
                                                                start=True, stop=False),
                              R=["Ast", "M2"], W=[("ps", b)], inc=False)
                        em.op("pe", lambda o=o, k1=k1: T.matmul(o, lhsT=Ast[:, :, 128 + k1], rhs=M2[:, k1, 0:128],
                                                                start=False, stop=True),
                              R=["Ast", "M2"], W=[("ps", b)], inc=(q == 3))
                    ev += 1
                    src = PS[hc * 64:(hc + 1) * 64, b, :].rearrange("p (k r c) -> p r c k", k=4, r=2)
                    dst = Y[hc * 64:(hc + 1) * 64, :, :].rearrange("p r (c k) -> p r c k", k=128)[:, :, :, kb * 4:(kb + 1) * 4]
                    if ev % 2:
                        em.op("dve", lambda dst=dst, src=src: V.tensor_copy(out=dst, in_=src), R=[("ps", b)], W=["Y"])
                    else:
                        em.op("act", lambda dst=dst, src=src: A.copy(out=dst, in_=src), R=[("ps", b)], W=["Y"])
            for tl in range(16):
                b = 6 + tl % 2
                em.op("pe", lambda b=b, tl=tl: T.matmul(PS[:, b, :], lhsT=CSf[:, 0:128], rhs=Y[:, 0, tl * 512:(tl + 1) * 512],
                                                        start=True, stop=False), R=["CSf", "Y"], W=[("ps", b)], inc=False)
                em.op("pe", lambda b=b, tl=tl: T.matmul(PS[:, b, :], lhsT=CSf[:, 128:256], rhs=Y[:, 1, tl * 512:(tl + 1) * 512],
                                                        start=False, stop=True), R=["CSf", "Y"], W=[("ps", b)], inc=True)
                fs = (tl // 4) % 2
                em.op("act", lambda b=b, tl=tl, fs=fs: A.copy(out=fst[fs][:, (tl % 4) * 512:(tl % 4 + 1) * 512], in_=PS[:, b, :]),
                      R=[("ps", b)], W=[f"fst{fs}"])
                if tl % 4 == 3:
                    t0 = (tl // 4) * 2048
                    em.dma("pool", C.MIXT[512 + cc * 128:512 + (cc + 1) * 128, t0:t0 + 2048], fst[fs][:],
                           R=[f"fst{fs}"], W=[("MIXT", 4 + cc)])
            for tc in range(2):
                em.op("pe", lambda tc=tc, cc=cc: T.matmul(PS[:, 0, 0:256], lhsT=fTc[:, cc, tc * 128:(tc + 1) * 128], rhs=CSc[:, :],
                                                          start=True, stop=True), R=["fTc", "CSc"], W=[("ps", 0)])
                em.op("dve", lambda tc=tc: V.tensor_copy(out=Gc[:, tc, :], in_=PS[:, 0, 0:256]), R=[("ps", 0)], W=["Gc"])
            for i_, (tc, part) in enumerate([(0, 0), (0, 1), (1, 0), (1, 1)]):
                em.op("pe", lambda i_=i_, tc=tc, part=part: T.matmul(
                    PS[:, 1, 0:256], lhsT=Gc[:, tc, part * 128:(part + 1) * 128], rhs=T256[:, tc, part * 256:(part + 1) * 256],
                    start=(i_ == 0), stop=(i_ == 3)), R=["Gc", "T256"], W=[("ps", 1)], inc=(i_ == 3))
            em.op("dve", lambda: V.tensor_copy(out=fcs[:, :], in_=PS[:, 1, 0:256]), R=[("ps", 1)], W=["fcs"])
            em.dma("pool", C.MIXT[512 + cc * 128:512 + (cc + 1) * 128, S:NT], fcs[:], R=["fcs"], W=[("MIXTc", 4 + cc)])
    C.es_mix.close()


PHASES += [("A", phase_A), ("C", phase_C)]


def phase_B(C):
    nc, em, PS = C.nc, C.em, C.PS
    V, A, G, T = nc.vector, nc.scalar, nc.gpsimd, nc.tensor
    TW = 2048
    with ExitStack() as ps_:
        sb = lambda n, s, d=F32: ps_.enter_context(nc.sbuf_tensor(n, list(s), d))
        bufA = sb("bufA", [128, NT]); bufB = sb("bufB", [128, 8460]); uc = sb("ucb_", [128, NT]); ucb = sb("ucbb", [128, NT], BF16)
        rt = sb("rt", [128, TW]); it_ = sb("it", [128, TW]); at = sb("at", [128, TW]); st = sb("st", [128, TW])
        wbd = sb("wbds", [128, 16, 128], BF16)
        lco = sb("lco", [128, 4, 2, 3]); cvc = sb("cvc", [128, 4, 5]); cA = sb("cA", [128, 4, 2]); carry = sb("carry", [128, 2])
        em.dma("pool", wbd[:], C.wbd_d.rearrange("g d c p n -> p (g d c) n"), W=["wbd"])
        em.dma("sp", lco[:], C.lcols_d, W=["lco"])
        em.dma("sp", cvc[:], C.convc_d, W=["cvc"])
        em.op("act", lambda: A.activation(out=cA[:], in_=lco[:, :, :, 2], func=AF.Exp, scale=-1.0), R=["lco"], W=["cA"])
        em.op("act", lambda: A.activation(out=cA[:], in_=cA[:], func=AF.Ln, bias=1.0), R=["cA"], W=["cA"])
        em.op("dve", lambda: V.tensor_scalar(out=cA[:], in0=cA[:], scalar1=-8.0, scalar2=None, op0=ALU.mult), R=["cA"], W=["cA"])
        em.op("dve", lambda: V.memset(bufB[:], 0.0), W=["bufB"])
        XO = 8200
        tiles = [(S, NT)] + [(i * TW, (i + 1) * TW) for i in range(4)]
        for c in range(4):
            em.dma("sp", bufA[:, 0:S], C.UG[c], R=[("UG", j) for j in range(16)], W=["bufA"])
            em.dma("sp", bufA[:, S:NT], C.UGc[c], R=["UGc"], W=["bufA"])
            em.op("pool", lambda: G.tensor_copy(out=bufB[:, 2:2 + S].rearrange("p (a b) -> p a b", b=64),
                                                in_=bufA[:, 0:S].rearrange("p (b a) -> p a b", a=128)),
                  R=["bufA"], W=["bufB"])
            em.op("pool", lambda: G.tensor_copy(out=bufB[:, XO + 2:XO + 2 + LC], in_=bufA[:, S:NT]), R=["bufA"], W=["bufB"])
            em.dma("sp", bufA[:, 0:S], C.UG[4 + c], R=[("UG", j) for j in range(16)], W=["bufA"])
            em.dma("sp", bufA[:, S:NT], C.UGc[4 + c], R=["UGc"], W=["bufA"])
            for (o0, src0, n) in ((0, 0, S), (S, XO, LC)):
                em.op("act", lambda o0=o0, src0=src0, n=n, c=c: A.activation(
                    out=uc[:, o0:o0 + n], in_=bufB[:, src0:src0 + n], func=AF.Identity,
                    scale=cvc[:, c, 0:1], bias=cvc[:, c, 4:5]), R=["bufB", "cvc"], W=["uc"])
                for k in range(1, 4):
                    em.op("dve", lambda o0=o0, src0=src0, n=n, c=c, k=k: V.scalar_tensor_tensor(
                        out=uc[:, o0:o0 + n], in0=bufB[:, src0 + k:src0 + k + n], scalar=cvc[:, c, k:k + 1],
                        in1=uc[:, o0:o0 + n], op0=ALU.mult, op1=ALU.add), R=["bufB", "cvc", "uc"], W=["uc"])
            em.op("pool", lambda: G.tensor_copy(out=ucb[:, :], in_=uc[:, :]), R=["uc"], W=["ucb"])
            for (lo, hi) in tiles:
                n = hi - lo
                em.op("act", lambda lo=lo, hi=hi, n=n: A.activation(out=rt[:, 0:n], in_=bufA[:, lo:hi], func=AF.Square),
                      R=["bufA"], W=["rt"])
                em.op("dve", lambda n=n: V.tensor_scalar(out=rt[:, 0:n], in0=rt[:, 0:n], scalar1=0.044715, scalar2=1.0,
                                                         op0=ALU.mult, op1=ALU.add), R=["rt"], W=["rt"])
                em.op("pool", lambda lo=lo, hi=hi, n=n: G.tensor_tensor(out=rt[:, 0:n], in0=rt[:, 0:n], in1=bufA[:, lo:hi],
                                                                        op=ALU.mult), R=["rt", "bufA"], W=["rt"])
                em.op("act", lambda n=n: A.activation(out=it_[:, 0:n], in_=rt[:, 0:n], func=AF.Sigmoid, scale=1.5957691216057308),
                      R=["rt"], W=["it"])
                em.op("pool", lambda lo=lo, hi=hi, n=n: G.tensor_tensor(out=bufA[:, lo:hi], in0=it_[:, 0:n], in1=bufA[:, lo:hi],
                                                                        op=ALU.mult), R=["it", "bufA"], W=["bufA"])
            for d in range(2):
                order = [tiles[0]] + (tiles[1:] if d == 0 else tiles[1:][::-1])
                for ti, (lo, hi) in enumerate(order):
                    n = hi - lo
                    nq = (n + 511) // 512
                    for gi in range(2):
                        for q in range(nq):
                            w_ = min(512, n - q * 512)
                            em.op("pe", lambda gi=gi, q=q, w_=w_, lo=lo, d=d, c=c: T.matmul(
                                PS[:, gi * 4 + q, 0:w_], lhsT=wbd[:, gi * 8 + d * 4 + c, :], rhs=ucb[:, lo + q * 512:lo + q * 512 + w_],
                                start=True, stop=True), R=["wbd", "ucb"], W=[("ps", gi * 4 + q)], inc=(q == nq - 1))
                    pr = PS[:, 0:4, :].rearrange("p a b -> p (a b)")[:, 0:n]
                    pi = PS[:, 4:8, :].rearrange("p a b -> p (a b)")[:, 0:n]
                    em.op("act", lambda pr=pr, n=n, c=c, d=d: A.activation(out=rt[:, 0:n], in_=pr, func=AF.Sigmoid,
                                                                           bias=lco[:, c, d, 0:1]),
                          R=[("ps", q) for q in range(4)] + ["lco"], W=["rt"])
                    em.op("act", lambda pi=pi, n=n, c=c, d=d: A.activation(out=it_[:, 0:n], in_=pi, func=AF.Sigmoid,
                                                                           bias=lco[:, c, d, 1:2]),
                          R=[("ps", 4 + q) for q in range(4)] + ["lco"], W=["it"])
                    em.op("act", lambda n=n, c=c, d=d: A.activation(out=at[:, 0:n], in_=rt[:, 0:n], func=AF.Exp,
                                                                    scale=cA[:, c, d:d + 1]), R=["rt", "cA"], W=["at"])
                    em.op("act", lambda n=n: A.activation(out=st[:, 0:n], in_=at[:, 0:n], func=AF.Square), R=["at"], W=["st"])
                    em.op("act", lambda n=n: A.activation(out=st[:, 0:n], in_=st[:, 0:n], func=AF.Sqrt, scale=-1.0, bias=1.0),
                          R=["st"], W=["st"])
                    em.op("pool", lambda n=n: G.tensor_tensor(out=it_[:, 0:n], in0=it_[:, 0:n], in1=st[:, 0:n], op=ALU.mult),
                          R=["it", "st"], W=["it"])
                    em.op("pool", lambda n=n, lo=lo, hi=hi: G.tensor_tensor(out=it_[:, 0:n], in0=it_[:, 0:n], in1=uc[:, lo:hi],
                                                                            op=ALU.mult), R=["it", "uc"], W=["it"])
                    init = 0.0 if ti == 0 else carry[:, d:d + 1]
                    if d == 0:
                        em.op("dve", lambda n=n, lo=lo, hi=hi, init=init: V.tensor_tensor_scan(
                            out=bufB[:, lo:hi], data0=at[:, 0:n], data1=it_[:, 0:n], initial=init, op0=ALU.mult, op1=ALU.add),
                              R=["at", "it", "carry", "bufB", "uc"], W=["bufB"])
                        em.op("dve", lambda hi=hi: V.tensor_copy(out=carry[:, 0:1], in_=bufB[:, hi - 1:hi]), R=["bufB"], W=["carry"])
                    else:
                        em.op("dve", lambda n=n, init=init: V.tensor_tensor_scan(
                            out=rt[:, n - 1::-1] if False else rt[:, 0:n][:, ::-1], data0=at[:, 0:n][:, ::-1],
                            data1=it_[:, 0:n][:, ::-1], initial=init, op0=ALU.mult, op1=ALU.add),
                              R=["at", "it", "carry", "rt"], W=["rt"])
                        em.op("dve", lambda: V.tensor_copy(out=carry[:, 1:2], in_=rt[:, 0:1]), R=["rt"], W=["carry"])
                        em.op("pool", lambda n=n, lo=lo, hi=hi: G.tensor_tensor(out=bufB[:, lo:hi], in0=bufB[:, lo:hi], in1=rt[:, 0:n],
                                                                                op=ALU.add), R=["bufB", "rt"], W=["bufB"])
            em.op("dve", lambda: V.tensor_tensor(out=ucb[:, 0:S].rearrange("p (a b) -> p a b", b=64),
                                                 in0=bufB[:, 0:S].rearrange("p (a b) -> p a b", b=64),
                                                 in1=bufA[:, 0:S].rearrange("p (b a) -> p a b", a=128), op=ALU.mult),
                  R=["bufB", "bufA", "ucb"], W=["ucb"])
            em.op("dve", lambda: V.tensor_tensor(out=ucb[:, S:NT], in0=bufB[:, S:NT], in1=bufA[:, S:NT], op=ALU.mult),
                  R=["bufB", "bufA", "ucb"], W=["ucb"])
            em.dma("pool", C.MIXT[c * 128:(c + 1) * 128, :], ucb[:, :], R=["ucb"], W=[("MIXT", c)])
            if c < 3:
                em.op("dve", lambda: V.memset(bufB[:, 0:2], 0.0), R=["bufB"], W=["bufB"])
                em.op("dve", lambda: V.memset(bufB[:, S:8460], 0.0), R=["bufB"], W=["bufB"])


def _tok_tiles(ntok_tile):
    return None


def phase_D1(C, layer=0):
    nc, em, PS = C.nc, C.em, C.PS
    V, A, G, T = nc.vector, nc.scalar, nc.gpsimd, nc.tensor
    with ExitStack() as ps_:
        sb = lambda n, s, d=F32: ps_.enter_context(nc.sbuf_tensor(n, list(s), d))
        wo = sb("D1wo", [128, 8, D], BF16)
        xt = [sb(f"D1xt{i}", [128, 4, D]) for i in range(2)]
        mt = [sb(f"D1mt{i}", [128, 8, 512], BF16) for i in range(2)]
        Gx = sb("D1Gx", [128, D]); Gc = sb("D1Gc", [128, D])
        ssq = sb("D1ssq", [128, 4]); rs = sb("D1rs", [128, 4]); junk = sb("D1junk", [128, D], BF16); tt = sb("D1tt", [128, D])
        _wload(C, wo, C.woab_d.rearrange("(k p) n -> p k n", p=128), "D1wo", 8, D)
        em.dma("sp", Gx[:], C.GROW[0, 0, 0, :].partition_broadcast(128), R=[("GROW", 0, 0)], W=["D1Gx"])
        em.dma("sp", Gc[:], C.GROW[0, 0, 1, :].partition_broadcast(128), R=[("GROW", 0, 0)], W=["D1Gc"])
        mixv = C.MIXT.rearrange("(k p) t -> p k t", p=128)
        mixkeys = [("MIXT", i) for i in range(8)] + [("MIXTc", 4 + i) for i in range(4)]

        def load(j, sl):
            if j < 16:
                em.dma("sp", xt[sl][:], C.x_d[j * 512:(j + 1) * 512, :].rearrange("(s p) d -> p s d", p=128), W=[f"D1xt{sl}"])
                em.dma("sp", mt[sl][:], mixv[:, :, j * 512:(j + 1) * 512], R=mixkeys, W=[f"D1mt{sl}"])
            else:
                em.dma("sp", xt[sl][:, 0:2, :], C.ctx_d.rearrange("(s p) d -> p s d", p=128), W=[f"D1xt{sl}"])
                em.dma("sp", mt[sl][:, :, 0:LC], mixv[:, :, S:NT], R=mixkeys, W=[f"D1mt{sl}"])
        load(0, 0)
        for j in range(17):
            sl = j % 2
            if j + 1 < 17:
                load(j + 1, 1 - sl)
            nsub = 4 if j < 16 else 2
            for s in range(nsub):
                pb = 2 * (s % 4)
                for h in range(2):
                    for k in range(8):
                        em.op("pe", lambda k=k, h=h, s=s, sl=sl, pb=pb: T.matmul(
                            PS[:, pb + h, :], lhsT=mt[sl][:, k, s * 128:(s + 1) * 128], rhs=wo[:, k, h * 512:(h + 1) * 512],
                            start=(k == 0), stop=(k == 7)), R=["D1wo", f"D1mt{sl}"], W=[("ps", pb + h)], inc=(k == 7))
                C.epilogue(pb, xt[sl][:, s, :], f"D1xt{sl}", (Gx if j < 16 else Gc)[:, :], "D1Gx" if j < 16 else "D1Gc",
                           ssq, rs, junk, tt, "D1")
            if j < 16:
                em.dma("pool", C.X1[j * 512:(j + 1) * 512, :].rearrange("(s p) d -> p s d", p=128), xt[sl][:],
                       R=[f"D1xt{sl}"], W=[("X1", j)])
            else:
                em.dma("pool", C.C1.rearrange("(s p) d -> p s d", p=128), xt[sl][:, 0:2, :], R=[f"D1xt{sl}"], W=["C1"])


def phase_FFN(C, layer, Xin, Cin, Xout, Cout, inkey, outkey):
    nc, em, PS = C.nc, C.em, C.PS
    V, A, G, T = nc.vector, nc.scalar, nc.gpsimd, nc.tensor
    tg = f"F{layer}"
    with ExitStack() as ps_:
        sb = lambda n, s, d=F32: ps_.enter_context(nc.sbuf_tensor(tg + n, list(s), d))
        wg = sb("wg", [128, 8, FH], BF16); wu = sb("wu", [128, 8, FH], BF16); wd = sb("wd", [128, NJ, D], BF16)
        xt = [sb(f"xt{i}", [128, 2, D]) for i in range(2)]
        xs = sb("xs", [128, 2, D]); hT = sb("hT", [128, 8, 256], BF16); hh = sb("hh", [128, NJ, 256], BF16)
        sg = [sb(f"sg{i}", [128, 256]) for i in range(2)]
        Gx = sb("Gx", [128, D]); Gc = sb("Gc", [128, D])
        ssq = sb("ssq", [128, 4]); rs = sb("rs", [128, 4]); junk = sb("junk", [128, D], BF16); tt = sb("tt", [128, D])
        _wload(C, wg, C.wg_d[layer].rearrange("(k p) n -> p k n", p=128), tg + "wg", 8, FH, 704)
        _wload(C, wu, C.wu_d[layer].rearrange("(k p) n -> p k n", p=128), tg + "wu", 8, FH, 704)
        _wload(C, wd, C.wd_d[layer].rearrange("(k p) n -> p k n", p=128), tg + "wd", NJ, D, 512)
        em.dma("sp", Gx[:], C.GROW[layer, 1, 0, :].partition_broadcast(128), R=[("GROW", layer, 1)], W=[tg + "Gx"])
        em.dma("sp", Gc[:], C.GROW[layer, 1, 1, :].partition_broadcast(128), R=[("GROW", layer, 1)], W=[tg + "Gc"])
        ntile = 32 + (1 if Cin is not None else 0)

        def load(j, sl):
            if j < 32:
                em.dma("sp", xt[sl][:], Xin[j * 256:(j + 1) * 256, :].rearrange("(s p) d -> p s d", p=128),
                       R=[(inkey, j // 2)], W=[tg + f"xt{sl}"])
            else:
                em.dma("sp", xt[sl][:], Cin.rearrange("(s p) d -> p s d", p=128), R=[inkey + "c"], W=[tg + f"xt{sl}"])
        load(0, 0)
        for j in range(ntile):
            sl = j % 2
            if j + 1 < ntile:
                load(j + 1, 1 - sl)
            path = 0 if j < 32 else 1
            C.prologue(xt[sl], tg + f"xt{sl}", 2, xs, tg + "xs", C.MODC[:, layer, 1, 0, path, :], C.MODC[:, layer, 1, 1, path, :],
                       hT, tg + "hT", ssq, rs, junk, tg, [7])
            for jj in range(NJ):
                b = jj % 3
                for gi, w_ in enumerate((wg, wu)):
                    for k in range(8):
                        em.op("pe", lambda k=k, b=b, gi=gi, w_=w_, jj=jj: T.matmul(
                            PS[:, b, gi * 256:(gi + 1) * 256], lhsT=w_[:, k, jj * 128:(jj + 1) * 128], rhs=hT[:, k, :],
                            start=(k == 0), stop=(k == 7)), R=[tg + "wg", tg + "wu", tg + "hT"], W=[("ps", b)],
                              inc=(k == 7 and gi == 1))
                s2 = jj % 2
                em.op("act", lambda b=b, s2=s2: A.activation(out=sg[s2][:, :], in_=PS[:, b, 0:256], func=AF.Silu),
                      R=[("ps", b)], W=[tg + f"sg{s2}"])
                em.op("dve", lambda b=b, s2=s2, jj=jj: V.tensor_tensor(out=hh[:, jj, :], in0=sg[s2][:, :], in1=PS[:, b, 256:512],
                                                                      op=ALU.mult), R=[("ps", b), tg + f"sg{s2}"], W=[tg + "hh"])
            for s in range(2):
                pb = 3 + 2 * s
                for h in range(2):
                    for jj in range(NJ):
                        em.op("pe", lambda jj=jj, h=h, s=s, pb=pb: T.matmul(
                            PS[:, pb + h, :], lhsT=hh[:, jj, s * 128:(s + 1) * 128], rhs=wd[:, jj, h * 512:(h + 1) * 512],
                            start=(jj == 0), stop=(jj == NJ - 1)), R=[tg + "wd", tg + "hh"], W=[("ps", pb + h)], inc=(jj == NJ - 1))
                C.epilogue(pb, xt[sl][:, s, :], tg + f"xt{sl}", (Gx if path == 0 else Gc)[:, :], tg + ("Gx" if path == 0 else "Gc"),
                           ssq, rs, junk, tt, tg)
            if j < 32:
                em.dma("pool", Xout[j * 256:(j + 1) * 256, :].rearrange("(s p) d -> p s d", p=128), xt[sl][:],
                       R=[tg + f"xt{sl}"], W=[(outkey, j // 2)] if j % 2 else [(outkey + "h", j)])
            else:
                em.dma("pool", Cout.rearrange("(s p) d -> p s d", p=128), xt[sl][:], R=[tg + f"xt{sl}"], W=[outkey + "c"])


PHASES += [("B", phase_B), ("D1", phase_D1),
           ("D2", lambda C: phase_FFN(C, 0, C.X1, C.C1, C.X2, C.C2, "X1", "X2"))]


def phase_E(C):
    nc, em, PS = C.nc, C.em, C.PS
    V, A, G, T = nc.vector, nc.scalar, nc.gpsimd, nc.tensor
    with ExitStack() as ps_:
        sb = lambda n, s, d=F32: ps_.enter_context(nc.sbuf_tensor("E" + n, list(s), d))
        wq = sb("wq", [128, 8, 3 * D], BF16)
        xt = [sb(f"xt{i}", [128, 4, D]) for i in range(2)]
        hT = sb("hT", [128, 8, 512], BF16)
        qk = [sb(f"qk{i}", [128, 16, 512], BF16) for i in range(2)]
        vst = [sb(f"vst{i}", [128, 4, D], BF16) for i in range(2)]
        ssq = sb("ssq", [128, 4]); rs = sb("rs", [128, 4]); junk = sb("junk", [128, D], BF16)
        _wload(C, wq, C.wqkv_d.rearrange("(k p) n -> p k n", p=128), "Ewq", 8, 3 * D)
        QTv = C.QT.rearrange("(c p) t -> p c t", p=128); KTv = C.KT.rearrange("(c p) t -> p c t", p=128)

        def load(j, sl):
            if j < 16:
                em.dma("sp", xt[sl][:], C.X2[j * 512:(j + 1) * 512, :].rearrange("(s p) d -> p s d", p=128), W=[f"Ext{sl}"])
            else:
                em.dma("sp", xt[sl][:, 0:2, :], C.C2.rearrange("(s p) d -> p s d", p=128), W=[f"Ext{sl}"])
        load(0, 0)
        ev = 0
        for j in range(17):
            sl = j % 2
            if j + 1 < 17:
                load(j + 1, 1 - sl)
            isctx = j == 16
            nsub = 2 if isctx else 4
            n = nsub * 128
            path = 1 if isctx else 0
            C.prologue(xt[sl], f"Ext{sl}", nsub, xt[sl], f"Ext{sl}", C.MODC[:, 1, 0, 0, path, :], C.MODC[:, 1, 0, 1, path, :],
                       hT, "EhT", ssq, rs, junk, "E", [0, 1])
            for oc in range(8 if isctx else 0, 16) if isctx else range(16):
                b = 2 + oc % 2
                for k in range(8):
                    em.op("pe", lambda k=k, b=b, oc=oc, n=n: T.matmul(
                        PS[:, b, 0:n], lhsT=wq[:, k, oc * 128:(oc + 1) * 128], rhs=hT[:, k, 0:n],
                        start=(k == 0), stop=(k == 7)), R=["Ewq", "EhT"], W=[("ps", b)], inc=(k == 7))
                if oc < 8:
                    em.op("act", lambda b=b, oc=oc, sl=sl, n=n: A.activation(out=qk[sl][:, oc, 0:n], in_=PS[:, b, 0:n],
                                                                             func=AF.Copy, scale=0.125),
                          R=[("ps", b)], W=[f"Eqk{sl}"])
                else:
                    em.op("dve", lambda b=b, oc=oc, sl=sl, n=n: V.tensor_copy(out=qk[sl][:, oc, 0:n], in_=PS[:, b, 0:n]),
                          R=[("ps", b)], W=[f"Eqk{sl}"])
            if not isctx:
                em.dma("pool", QTv[:, :, j * 512:(j + 1) * 512], qk[sl][:, 0:8, :], R=[f"Eqk{sl}"], W=[("QT", j)])
                em.dma("pool", KTv[:, :, j * 512:(j + 1) * 512], qk[sl][:, 8:16, :], R=[f"Eqk{sl}"], W=[("KT", j)])
            else:
                em.dma("pool", KTv[:, :, S:NT], qk[sl][:, 8:16, 0:LC], R=[f"Eqk{sl}"], W=[("KT", j)])
            for s in range(nsub):
                pb = 4 + 2 * (s % 2)
                for h in range(2):
                    for k in range(8):
                        em.op("pe", lambda k=k, h=h, s=s, pb=pb: T.matmul(
                            PS[:, pb + h, :], lhsT=hT[:, k, s * 128:(s + 1) * 128], rhs=wq[:, k, 2048 + h * 512:2048 + (h + 1) * 512],
                            start=(k == 0), stop=(k == 7)), R=["Ewq", "EhT"], W=[("ps", pb + h)], inc=(k == 7))
                ev += 1
                src = PS[:, pb:pb + 2, :].rearrange("p a b -> p (a b)")
                if ev % 2:
                    em.op("dve", lambda s=s, sl=sl, src=src: V.tensor_copy(out=vst[sl][:, s, :], in_=src),
                          R=[("ps", pb), ("ps", pb + 1)], W=[f"Evst{sl}"])
                else:
                    em.op("act", lambda s=s, sl=sl, src=src: A.copy(out=vst[sl][:, s, :], in_=src),
                          R=[("ps", pb), ("ps", pb + 1)], W=[f"Evst{sl}"])
            t0 = S if isctx else j * 512
            em.dma("pool", C.VD[t0:t0 + n, :].rearrange("(s p) d -> p s d", p=128), vst[sl][:, 0:nsub, :],
                   R=[f"Evst{sl}"], W=[("VD", j)])


def phase_F(C):
    nc, em, PS = C.nc, C.em, C.PS
    V, A, G, T = nc.vector, nc.scalar, nc.gpsimd, nc.tensor
    with ExitStack() as ps_:
        sb = lambda n, s, d=F32: ps_.enter_context(nc.sbuf_tensor("AT" + n, list(s), d))
        wo = sb("wo", [128, 8, D], BF16)
        KTb = [sb(f"KTb{i}", [128, 8, 1024], BF16) for i in range(2)]
        Vb = [sb(f"Vb{i}", [128, 8, 16, 65], BF16) for i in range(2)]
        QTb = sb("QTb", [128, 8, 512], BF16); xt = sb("xt", [128, 4, D])
        KTc = sb("KTc", [128, 8, LC], BF16); Vc = sb("Vc", [128, 2, 16, 65], BF16)
        BTi = sb("BTi", [128, 16, 5, 128], BF16); BTe = sb("BTe", [128, 16, 5, 128], BF16)
        PT = [sb(f"PT{i}", [128, 896], BF16) for i in range(2)]
        Ot = sb("Ot", [128, D]); OTt = sb("OTt", [128, 8, 128], BF16); rden = sb("rden", [128, 4])
        Gx = sb("Gx", [128, D]); ssq = sb("ssq", [128, 4]); rs = sb("rs", [128, 4]); junk = sb("junk", [128, D], BF16)
        tt = sb("tt", [128, D])
        _wload(C, wo, C.wona_d.rearrange("(k p) n -> p k n", p=128), "Awo", 8, D)
        em.dma("sp", Gx[:], C.GROW[1, 0, 0, :].partition_broadcast(128), W=["AGx"])
        em.dma("sp", BTi[:], C.BT_d[2], W=["ABTi"])
        KTv = C.KT.rearrange("(c p) t -> p c t", p=128); QTv = C.QT.rearrange("(c p) t -> p c t", p=128)
        em.dma("sp", KTc[:], KTv[:, :, S:NT], W=["AKTc"])
        for i in range(2):
            em.op("pool", lambda i=i: G.memset(Vb[i][:, :, :, 64:65], 1.0), W=[f"AVb{i}"])
        em.op("pool", lambda: G.memset(Vc[:, :, :, 64:65], 1.0), W=["AVc"])
        for c in range(2):
            em.dma("sp", Vc[:, c, :, 0:64], C.VD[S + c * 128:S + (c + 1) * 128, :].rearrange("p (h d) -> p h d", d=64), W=["AVc"])

        def load(blk, sl):
            r0 = 8 * blk
            kb = min(max(r0 - 4, 0), 112)
            em.dma("sp", KTb[sl][:], KTv[:, :, kb * 64:kb * 64 + 1024], W=[f"AKTb{sl}"])
            for c in range(8):
                t0 = kb * 64 + c * 128
                em.dma("sp", Vb[sl][:, c, :, 0:64], C.VD[t0:t0 + 128, :].rearrange("p (h d) -> p h d", d=64), W=[f"AVb{sl}"])
        load(0, 0)
        for blk in range(16):
            sl = blk % 2
            r0 = 8 * blk
            kb = min(max(r0 - 4, 0), 112)
            em.dma("sp", QTb[:], QTv[:, :, r0 * 64:r0 * 64 + 512], W=["AQTb"])
            em.dma("sp", xt[:], C.X2[blk * 512:(blk + 1) * 512, :].rearrange("(s p) d -> p s d", p=128), W=["Axt"])
            if blk + 1 < 16:
                load(blk + 1, 1 - sl)
            for i in range(4):
                gp = r0 + 2 * i
                ks = min(max(gp - 4, 0), 118)
                off = (ks - kb) // 2
                var = {0: 0, 2: 1, 124: 3, 126: 4}.get(gp, 2)
                if var != 2:
                    em.dma("sp", BTe[:], C.BT_d[var], W=["ABTe"])
                BTv, btk = (BTi, "ABTi") if var == 2 else (BTe, "ABTe")
                for h in range(16):
                    j = h // 2; e = h % 2
                    p0, p1 = 64 * e, 64 * e + 64
                    sb_ = 2 * (h % 2)
                    for cl in range(7):
                        o = PS[:, sb_ + cl // 4, (cl % 4) * 128:(cl % 4 + 1) * 128]
                        if cl < 5:
                            em.op("pe", lambda o=o, cl=cl, j=j, p0=p0, p1=p1, sl=sl, off=off, i=i: T.matmul(
                                o, lhsT=KTb[sl][p0:p1, j, (off + cl) * 128:(off + cl + 1) * 128], rhs=QTb[p0:p1, j, i * 128:(i + 1) * 128],
                                start=True, stop=False), R=[f"AKTb{sl}", "AQTb"], W=[("ps", sb_), ("ps", sb_ + 1)], inc=False)
                            em.op("pe", lambda o=o, cl=cl, h=h, BTv=BTv: T.matmul(
                                o, lhsT=C.identb[:, :], rhs=BTv[:, h, cl, :], start=False, stop=True),
                                  R=["identb", btk], W=[("ps", sb_), ("ps", sb_ + 1)], inc=False)
                        else:
                            em.op("pe", lambda o=o, cl=cl, j=j, p0=p0, p1=p1, i=i: T.matmul(
                                o, lhsT=KTc[p0:p1, j, (cl - 5) * 128:(cl - 4) * 128], rhs=QTb[p0:p1, j, i * 128:(i + 1) * 128],
                                start=True, stop=True), R=["AKTc", "AQTb"], W=[("ps", sb_), ("ps", sb_ + 1)], inc=(cl == 6))
                    src = PS[:, sb_:sb_ + 2, :].rearrange("p a b -> p (a b)")[:, 0:896]
                    em.op("act", lambda src=src, h=h: A.activation(out=PT[h % 2][:, :], in_=src, func=AF.Exp),
                          R=[("ps", sb_), ("ps", sb_ + 1)], W=[f"APT{h % 2}"])
                    ob = 4 + (h // 4) % 2
                    so = (h % 4) * 128
                    for c in range(7):
                        rhs = Vb[sl][:, off + c, h, :] if c < 5 else Vc[:, c - 5, h, :]
                        em.op("pe", lambda c=c, rhs=rhs, h=h, ob=ob, so=so: T.matmul(
                            PS[:, ob, so:so + 65], lhsT=PT[h % 2][:, c * 128:(c + 1) * 128], rhs=rhs,
                            start=(c == 0), stop=(c == 6)), R=[f"APT{h % 2}", f"AVb{sl}", "AVc"], W=[("ps", ob)], inc=(c == 6))
                    if h % 4 == 3:
                        em.op("dve", lambda ob=ob: V.reciprocal(out=rden[:, 0:4], in_=PS[:, ob, 64:512:128]),
                              R=[("ps", ob)], W=["Arden"])
                        for hh in range(4):
                            hd = h - 3 + hh
                            em.op("dve", lambda ob=ob, hh=hh, hd=hd: V.tensor_scalar(
                                out=Ot[:, hd * 64:(hd + 1) * 64], in0=PS[:, ob, hh * 128:hh * 128 + 64],
                                scalar1=rden[:, hh:hh + 1], scalar2=None, op0=ALU.mult), R=[("ps", ob), "Arden"], W=["AOt"])
                for k in range(8):
                    em.op("pe", lambda k=k: T.transpose(PS[:, 6 + k // 4, (k % 4) * 128:(k % 4 + 1) * 128],
                                                        Ot[:, k * 128:(k + 1) * 128], C.ident[:]),
                          R=["AOt", "ident"], W=[("ps", 6 + k // 4)], inc=(k % 4 == 3))
                for hb in range(2):
                    em.op("act" if hb else "dve",
                          (lambda hb=hb: A.copy(out=OTt[:, 4 * hb:4 * hb + 4, :].rearrange("p a b -> p (a b)"), in_=PS[:, 6 + hb, :])) if hb else
                          (lambda hb=hb: V.tensor_copy(out=OTt[:, 4 * hb:4 * hb + 4, :].rearrange("p a b -> p (a b)"), in_=PS[:, 6 + hb, :])),
                          R=[("ps", 6 + hb)], W=["AOTt"])
                for hf in range(2):
                    for k in range(8):
                        em.op("pe", lambda k=k, hf=hf: T.matmul(PS[:, 6 + hf, :], lhsT=OTt[:, k, :], rhs=wo[:, k, hf * 512:(hf + 1) * 512],
                                                                start=(k == 0), stop=(k == 7)),
                              R=["AOTt", "Awo"], W=[("ps", 6 + hf)], inc=(k == 7))
                C.epilogue(6, xt[:, i, :], "Axt", Gx[:, :], "AGx", ssq, rs, junk, tt, "A")
            em.dma("pool", C.X3[blk * 512:(blk + 1) * 512, :].rearrange("(s p) d -> p s d", p=128), xt[:], R=["Axt"], W=[("X3", blk)])


PHASES += [("E", phase_E), ("F", phase_F),
           ("G", lambda C: phase_FFN(C, 1, C.X3, None, C.out_d, None, "X3", "OUT"))]
```

```python
import math
from contextlib import ExitStack
import numpy as np
import ml_dtypes
import concourse.bass as bass
import concourse.mybir as mybir
from concourse.bass_utils import run_bass_kernel_spmd

F32 = mybir.dt.float32
BF16 = mybir.dt.bfloat16
AF = mybir.ActivationFunctionType
ALU = mybir.AluOpType
NPBF = ml_dtypes.bfloat16

D = 1024
S = 8192
LC = 256
NT = S + LC
FH = 2816
NJ = FH // 128
EPS = 1e-6
NCORES = 4
NEG = -30000.0


class Em:
    LIMIT = 30000
    NDS = 8

    def __init__(self, nc, es):
        self.nc = nc
        self.es = es
        self.eng = dict(pe=nc.tensor, act=nc.scalar, dve=nc.vector, pool=nc.gpsimd, sp=nc.sync)
        self.sem = {}
        self.semkey = {}
        self.cnt = {}
        self.nsem = 0
        for e in self.eng:
            self._newsem(e)
        self.waited = {e: {} for e in self.eng}
        self.lastw = {}
        self.readers = {}
        self.dsem = {}
        self.ndma = {}
        for q in ("sp", "pool", "act"):
            self.dsem[q] = [es.enter_context(nc.semaphore(f"d{q}{i}")) for i in range(self.NDS)]
            self.ndma[q] = 0

    def _newsem(self, e):
        self.nsem += 1
        self.sem[e] = self.es.enter_context(self.nc.semaphore(f"s{e}{self.nsem}"))
        self.semkey[e] = (e, self.nsem)
        self.cnt[e] = 0

    def _deps(self, engine, R, W):
        deps = {}
        for k in list(R) + list(W):
            t = self.lastw.get(k)
            if t is not None:
                if t[0] not in deps or deps[t[0]][2] < t[2]:
                    deps[t[0]] = t
        for k in W:
            for t in self.readers.get(k, {}).values():
                if t[0] not in deps or deps[t[0]][2] < t[2]:
                    deps[t[0]] = t
        e = self.eng[engine]
        for sk, t in deps.items():
            if engine == "pe" and t[3] == "pe":
                continue
            if self.waited[engine].get(sk, 0) >= t[2]:
                continue
            e.wait_ge(t[1], t[2])
            self.waited[engine][sk] = t[2]

    def _record(self, tok, R, W):
        for k in W:
            self.lastw[k] = tok
            self.readers[k] = {}
        for k in R:
            d = self.readers.setdefault(k, {})
            if tok[0] not in d or d[tok[0]][2] < tok[2]:
                d[tok[0]] = tok

    def op(self, engine, fn, R=(), W=(), inc=True):
        self._deps(engine, R, W)
        ins = fn()
        if inc:
            ins.then_inc(self.sem[engine], 1)
            self.cnt[engine] += 1
            tok = (self.semkey[engine], self.sem[engine], self.cnt[engine], engine)
            self._record(tok, R, W)
            if self.cnt[engine] >= self.LIMIT:
                self._newsem(engine)
        else:
            tok = (self.semkey[engine], self.sem[engine], self.cnt[engine] + 1, engine)
            self._record(tok, R, W)
        return ins

    def dma(self, q, out, in_, R=(), W=(), **kw):
        self._deps(q, R, W)
        i = self.ndma[q]
        self.ndma[q] += 1
        sem = self.dsem[q][i % self.NDS]
        rnd = i // self.NDS
        sk = ("dma", q, i % self.NDS)
        if rnd > 0 and self.waited[q].get(sk, 0) < 16 * rnd:
            self.eng[q].wait_ge(sem, 16 * rnd)
            self.waited[q][sk] = 16 * rnd
        self.eng[q].dma_start(out=out, in_=in_, **kw).then_inc(sem, 16)
        tok = (sk, sem, 16 * (rnd + 1), None)
        self._record(tok, R, W)

    def finish(self):
        sp = self.eng["sp"]
        for q in self.dsem:
            n = self.ndma[q]
            for s in range(self.NDS):
                uses = (n - s + self.NDS - 1) // self.NDS if n > s else 0
                if uses > 0:
                    sp.wait_ge(self.dsem[q][s], 16 * uses)
        for e in ("pe", "act", "dve", "pool"):
            if self.cnt[e] > 0:
                sp.wait_ge(self.sem[e], self.cnt[e])


def _tables():
    t = {}
    t["ident"] = np.eye(128, dtype=np.float32)
    t["identb"] = np.eye(128, dtype=np.float32).astype(NPBF)
    a = np.arange(128, dtype=np.float64)
    ang = 2 * np.pi * np.outer(a, a) / 128.0
    t["T1"] = np.concatenate([np.cos(ang), -np.sin(ang)], axis=1).astype(NPBF)
    t2 = np.arange(64, dtype=np.float64)[:, None, None]
    k1 = np.arange(128, dtype=np.float64)[None, :, None]
    k2 = np.arange(64, dtype=np.float64)[None, None, :]
    ph = 2 * np.pi * (t2 * k2 / 64.0 + t2 * k1 / 8192.0)
    Mc, Ms = np.cos(ph), np.sin(ph)
    t["M2"] = np.concatenate([Ms, Mc, -Ms], axis=2).astype(NPBF)
    c = np.arange(64, dtype=np.float64)
    angc = 2 * np.pi * np.outer(c, c) / 64.0
    Cc, Sc = np.cos(angc), np.sin(angc)
    z = np.zeros((64, 64))
    Cbd = np.block([[Cc, z], [z, Cc]])
    Sbd = np.block([[Sc, z], [z, Sc]])
    sx = 1.0 / math.sqrt(8192.0 * 64.0)
    t["CSf"] = (np.concatenate([Cbd, Sbd], axis=1) * sx).astype(NPBF)
    sc_ = 1.0 / math.sqrt(256.0 * 64.0)
    t["CSc"] = (np.concatenate([Cbd, Sbd], axis=1) * sc_).astype(NPBF)
    p = np.arange(256, dtype=np.float64)
    angp = 2 * np.pi * np.outer(p, p) / 256.0
    T256 = np.concatenate([np.cos(angp), -np.sin(angp)], axis=1)
    t["T256"] = T256.reshape(2, 128, 512).transpose(1, 0, 2).copy().astype(NPBF)
    return t


_TABLES = None


def _bias_tables(rpb):
    H = 16
    out = np.full((5, 5 * 128, H, 128), NEG, dtype=np.float32)
    kr = np.arange(10)[:, None, None, None]
    kc = np.arange(64)[None, :, None, None]
    qr = np.arange(2)[None, None, :, None]
    qc = np.arange(64)[None, None, None, :]
    cs = np.clip(qc - 8, 0, 48)
    for vi, (gp, ks) in enumerate([(0, 0), (2, 0), (60, 56), (124, 118), (126, 118)]):
        gq = gp + qr
        rs = np.clip(gq - 4, 0, 120)
        gk = ks + kr
        valid = (gk >= rs) & (gk < rs + 8) & (kc >= cs) & (kc < cs + 16)
        dr = np.clip(gk - gq + 7, 0, 14)
        dc = np.clip(kc - qc + 15, 0, 30)
        valid, dr, dc = np.broadcast_arrays(valid, dr, dc)
        vals = rpb[:, dr, dc]
        vals = np.where(valid[None], vals, NEG)
        out[vi] = vals.transpose(1, 2, 0, 3, 4).reshape(640, H, 128)
    bt = out.reshape(5, 5, 128, H, 128).transpose(0, 2, 3, 1, 4)
    return np.ascontiguousarray(bt).astype(NPBF)


def _col(v, nchunk):
    return np.ascontiguousarray(np.asarray(v, np.float32).reshape(nchunk, 128).T)


def build(stop="all", debug=False):
    nc = bass.Bass("TRN2", target_bir_lowering=False)

    def din(name, shape, dt=F32):
        return nc.dram_tensor(name, list(shape), dt, kind="ExternalInput").ap()

    skind = "ExternalOutput" if debug else "Internal"

    def dscr(name, shape, dt):
        return nc.dram_tensor(name, list(shape), dt, kind=skind).ap()

    x_d = din("x", [S, D]); ctx_d = din("ctx", [LC, D]); ccols_d = din("ccols", [128, 16])
    wmod_d = din("w_mod", [2, D, 6 * D]); bmodc_d = din("bmodc", [128, 2, 48]); bmod_d = din("b_mod", [2, 6 * D])
    gcols_d = din("gcols", [128, 2, 2, 8]); gpm_d = din("g_post_mix", [2, D]); gpf_d = din("g_post_ffn", [2, D])
    win_d = din("w_in_ab", [D, 1536]); woab_d = din("w_out_ab", [D, D])
    wg_d = din("w_ffn_gate", [2, D, FH]); wu_d = din("w_ffn_up", [2, D, FH]); wd_d = din("w_ffn_down", [2, FH, D])
    wqkv_d = din("w_qkv_na", [D, 3 * D]); wona_d = din("w_out_na", [D, D])
    wbd_d = din("wbd", [2, 2, 4, 128, 128]); lcols_d = din("lcols", [128, 4, 2, 3]); convc_d = din("convc", [128, 4, 5])
    ident_d = din("ident", [128, 128]); identb_d = din("identb", [128, 128], BF16)
    T1_d = din("T1", [128, 256], BF16); M2_d = din("M2", [64, 128, 192], BF16); CSf_d = din("CSf", [128, 256], BF16)
    CSc_d = din("CSc", [128, 256], BF16); T256_d = din("T256", [128, 2, 512], BF16)
    BT_d = din("BT", [5, 128, 16, 5, 128], BF16)
    out_d = nc.dram_tensor("out", [S, D], F32, kind="ExternalOutput").ap()

    UG = dscr("UG", [8, 128, S], F32)
    UGc = dscr("UGc", [8, 128, LC], F32)
    MIXT = dscr("MIXT", [D, NT], BF16)
    GROW = dscr("GROW", [2, 2, 2, D], F32)
    X1 = dscr("X1", [S, D], F32); C1 = dscr("C1", [LC, D], F32)
    X2 = dscr("X2", [S, D], F32); C2 = dscr("C2", [LC, D], F32)
    X3 = dscr("X3", [S, D], F32)
    QT = dscr("QT", [D, S], BF16); KT = dscr("KT", [D, NT], BF16); VD = dscr("VD", [NT, D], BF16)

    with ExitStack() as es:
        em = Em(nc, es)

        def barrier():
            for e in ("pe", "act", "dve", "pool", "sp"):
                eng = em.eng[e]
                for f in ("pe", "act", "dve", "pool"):
                    if f != e and em.cnt[f] > 0 and em.waited[e].get(em.semkey[f], 0) < em.cnt[f]:
                        eng.wait_ge(em.sem[f], em.cnt[f]); em.waited[e][em.semkey[f]] = em.cnt[f]
                for q in em.dsem:
                    n = em.ndma[q]
                    for s_ in range(em.NDS):
                        uses = (n - s_ + em.NDS - 1) // em.NDS if n > s_ else 0
                        sk = ("dma", q, s_)
                        if uses > 0 and em.waited[e].get(sk, 0) < 16 * uses:
                            eng.wait_ge(em.dsem[q][s_], 16 * uses); em.waited[e][sk] = 16 * uses

        PS = es.enter_context(nc.psum_tensor("PS", [128, 8, 512], F32))
        ident = es.enter_context(nc.sbuf_tensor("ident_s", [128, 128], F32))
        identb = es.enter_context(nc.sbuf_tensor("identb_s", [128, 128], BF16))
        MODC = es.enter_context(nc.sbuf_tensor("MODC", [128, 2, 2, 2, 2, 8], F32))
        mhalf = es.enter_context(nc.sbuf_tensor("mhalf", [128, 8], F32))
        em.dma("sp", ident[:], ident_d, W=["ident"])
        em.dma("sp", identb[:], identb_d, W=["identb"])
        em.op("dve", lambda: nc.vector.memset(mhalf[:], -0.5), W=["mhalf"])

        V = nc.vector; A = nc.scalar; G = nc.gpsimd; T = nc.tensor

        def pbank(b):
            return ("ps", b)

        def rstd_from_ssq(ssq, rs, n, tag):
            em.op("dve", lambda: V.tensor_scalar(out=rs[:, 0:n], in0=ssq[:, 0:n], scalar1=1.0 / D, scalar2=EPS,
                                                 op0=ALU.mult, op1=ALU.add), R=[tag + "ssq"], W=[tag + "rs"])
            em.op("pool", lambda: G.tensor_tensor(out=rs[:, 0:n], in0=rs[:, 0:n], in1=mhalf[:, 0:n], op=ALU.pow),
                  R=[tag + "rs", "mhalf"], W=[tag + "rs"])

        def prologue(xt, xkey, nsub, xs, xskey, Acol, Bcol, hT, hkey, ssq, rs, junk, tag, tb):
            for s in range(nsub):
                em.op("act", lambda s=s: A.activation(out=junk[:, :], in_=xt[:, s, :], func=AF.Square,
                                                      accum_out=ssq[:, s:s + 1]),
                      R=[xkey], W=[tag + "junk", tag + "ssq"])
            rstd_from_ssq(ssq, rs, nsub, tag)
            for s in range(nsub):
                em.op("act", lambda s=s: A.activation(out=xs[:, s, :], in_=xt[:, s, :], func=AF.Copy,
                                                      scale=rs[:, s:s + 1]),
                      R=[xkey, tag + "rs"], W=[xskey])
            for k in range(8):
                b = tb[k % len(tb)]
                for s in range(nsub):
                    em.op("pe", lambda s=s, k=k, b=b: T.transpose(PS[:, b, s * 128:(s + 1) * 128],
                                                                  xs[:, s, k * 128:(k + 1) * 128], ident[:]),
                          R=[xskey, "ident"], W=[pbank(b)], inc=(s == nsub - 1))
                em.op("act", lambda k=k, b=b: A.activation(out=hT[:, k, 0:nsub * 128], in_=PS[:, b, 0:nsub * 128],
                                                           func=AF.Identity, scale=Acol[:, k:k + 1],
                                                           bias=Bcol[:, k:k + 1]),
                      R=[pbank(b), "MODC"], W=[hkey])

        def epilogue(psb, xsub, xkey, Gb, gkey, ssq, rs, junk, tt, tag):
            yv = PS[:, psb:psb + 2, :]
            em.op("act", lambda: A.activation(out=junk[:, :], in_=yv, func=AF.Square, accum_out=ssq[:, 0:1]),
                  R=[pbank(psb), pbank(psb + 1)], W=[tag + "junk", tag + "ssq"])
            rstd_from_ssq(ssq, rs, 1, tag)
            em.op("dve", lambda: V.tensor_tensor(out=tt[:, :], in0=yv, in1=Gb, op=ALU.mult),
                  R=[pbank(psb), pbank(psb + 1), gkey], W=[tag + "tt"])
            em.op("dve", lambda: V.scalar_tensor_tensor(out=xsub, in0=tt[:, :], scalar=rs[:, 0:1], in1=xsub,
                                                        op0=ALU.mult, op1=ALU.add),
                  R=[tag + "tt", tag + "rs", xkey], W=[xkey])

        with ExitStack() as pes:
            sb = lambda n, s, d=F32: pes.enter_context(nc.sbuf_tensor(n, list(s), d))
            cc = sb("cc", [128, 16]); scc = sb("scc", [128, 16]); rhs2 = sb("rhs2", [128, 8, 2], BF16)
            bmc = sb("bmc", [128, 2, 48]); bm1 = sb("bm1", [128, 2, 48]); gco = sb("gco", [128, 2, 2, 8])
            bmrow = sb("bmrow", [2, 2, 2, D]); grow = sb("grow", [2, 2, 2, D]); grt = sb("grt", [2, 2, 2, D])
            wm = [sb(f"wm{i}", [128, 8, 512], BF16) for i in range(3)]
            em.dma("sp", cc[:], ccols_d, W=["cc"])
            em.dma("sp", bmc[:], bmodc_d, W=["bmc"])
            em.dma("sp", gco[:], gcols_d, W=["gco"])
            for l in range(2):
                for w_, (src, off) in enumerate([(bmod_d, 2 * D), (bmod_d, 5 * D)]):
                    em.dma("sp", bmrow[:, l, w_, :], src[l, off:off + D].partition_broadcast(2), W=["bmrow"])
                em.dma("sp", grow[:, l, 0, :], gpm_d[l, :].partition_broadcast(2), W=["grow"])
                em.dma("sp", grow[:, l, 1, :], gpf_d[l, :].partition_broadcast(2), W=["grow"])
            em.op("act", lambda: A.activation(out=scc[:], in_=cc[:], func=AF.Silu), R=["cc"], W=["scc"])
            em.op("dve", lambda: V.tensor_copy(out=rhs2[:, :, 0], in_=scc[:, 0:8]), R=["scc"], W=["rhs2"])
            em.op("dve", lambda: V.tensor_copy(out=rhs2[:, :, 1], in_=scc[:, 8:16]), R=["scc"], W=["rhs2"])
            em.op("dve", lambda: V.tensor_scalar(out=bm1[:], in0=bmc[:], scalar1=1.0, scalar2=None, op0=ALU.add),
                  R=["bmc"], W=["bm1"])
            it = 0
            for l in range(2):
                wsrc = wmod_d[l].rearrange("(k p) n -> p k n", p=128)
                for nb in range(12):
                    slot = it % 3; it += 1
                    wt = wm[slot]; wk = f"wm{slot}"
                    em.dma("pool", wt[:], wsrc[:, :, nb * 512:(nb + 1) * 512], W=[wk])
                    v = nb // 2; half = nb % 2
                    b = it % 4
                    if v in (2, 5):
                        w_ = 0 if v == 2 else 1
                        for k in range(8):
                            em.op("pe", lambda k=k, b=b, wt=wt: T.matmul(PS[0:2, b, :], lhsT=rhs2[:, k, :], rhs=wt[:, k, :],
                                                                        start=(k == 0), stop=(k == 7)),
                                  R=["rhs2", wk], W=[pbank(b)], inc=(k == 7))
                        dst = grt[:, l, w_, half * 512:(half + 1) * 512]
                        em.op("dve", lambda b=b, dst=dst, l=l, w_=w_, half=half: V.tensor_tensor(
                            out=dst, in0=PS[0:2, b, :], in1=bmrow[:, l, w_, half * 512:(half + 1) * 512], op=ALU.add),
                              R=[pbank(b), "bmrow"], W=["grt"])
                        em.op("dve", lambda dst=dst, l=l, w_=w_, half=half: V.tensor_tensor(
                            out=dst, in0=dst, in1=grow[:, l, w_, half * 512:(half + 1) * 512], op=ALU.mult),
                              R=["grt", "grow"], W=["grt"])
                        if half == 1:
                            em.dma("sp", GROW[l, w_, :, :], grt[:, l, w_, :], R=["grt"], W=[("GROW", l, w_)])
                    else:
                        sub = 0 if v < 2 else 1
                        isA = v in (1, 4)
                        for m in range(4):
                            ch = half * 4 + m
                            for k in range(8):
                                em.op("pe", lambda k=k, b=b, m=m, wt=wt: T.matmul(
                                    PS[:, b, 2 * m:2 * m + 2], lhsT=wt[:, k, m * 128:(m + 1) * 128], rhs=rhs2[:, k, :],
                                    start=(k == 0), stop=(k == 7)), R=["rhs2", wk], W=[pbank(b)], inc=(k == 7))
                            dst = MODC[:, l, sub, 0 if isA else 1, :, ch]
                            if isA:
                                em.op("dve", lambda b=b, m=m, dst=dst, l=l, v=v, ch=ch, sub=sub: V.tensor_scalar(
                                    out=dst, in0=PS[:, b, 2 * m:2 * m + 2], scalar1=bm1[:, l, v * 8 + ch:v * 8 + ch + 1],
                                    scalar2=gco[:, l, sub, ch:ch + 1], op0=ALU.add, op1=ALU.mult),
                                      R=[pbank(b), "bm1", "gco"], W=["MODC"])
                            else:
                                em.op("dve", lambda b=b, m=m, dst=dst, l=l, v=v, ch=ch: V.tensor_scalar(
                                    out=dst, in0=PS[:, b, 2 * m:2 * m + 2], scalar1=bmc[:, l, v * 8 + ch:v * 8 + ch + 1],
                                    scalar2=None, op0=ALU.add), R=[pbank(b), "bmc"], W=["MODC"])
            if debug:
                MODCd = nc.dram_tensor("MODCd", [128, 128], F32, kind="ExternalOutput").ap()
                em.dma("sp", MODCd, MODC[:].rearrange("p a b c d e -> p (a b c d e)"), R=["MODC"], W=["MODCd"])
            barrier()
        if stop == "0":
            em.finish()
            return nc
        C = type("C", (), {})()
        C.__dict__.update(locals())
        for name, fn in PHASES:
            fn(C)
            barrier()
            if stop == name:
                if hasattr(C, 'es_mix'):
                    C.es_mix.close()
                break
        em.finish()
    return nc


PHASES = []


def _host_inputs(inp, b):
    global _TABLES
    if _TABLES is None:
        _TABLES = _tables()
    f = lambda a: np.ascontiguousarray(np.asarray(a, dtype=np.float32))
    m = {}
    m["x"] = f(inp["x"][b]); m["ctx"] = f(inp["ctx"][b])
    m["ccols"] = np.concatenate([_col(inp["c"][b], 8), _col(inp["c_ctx"], 8)], axis=1)
    m["w_mod"] = f(inp["w_mod"]); m["b_mod"] = f(inp["b_mod"])
    m["bmodc"] = np.ascontiguousarray(np.stack([_col(inp["b_mod"][l], 48) for l in range(2)], axis=1))
    m["gcols"] = np.ascontiguousarray(np.stack(
        [np.stack([_col(inp["g_pre_mix"][l], 8), _col(inp["g_pre_ffn"][l], 8)], axis=1) for l in range(2)], axis=1))
    m["g_post_mix"] = f(inp["g_post_mix"]); m["g_post_ffn"] = f(inp["g_post_ffn"])
    m["w_in_ab"] = f(inp["w_in_ab"][0]); m["w_out_ab"] = f(inp["w_out_ab"][0])
    m["w_ffn_gate"] = f(inp["w_ffn_gate"]); m["w_ffn_up"] = f(inp["w_ffn_up"]); m["w_ffn_down"] = f(inp["w_ffn_down"])
    m["w_qkv_na"] = f(inp["w_qkv_na"][0]); m["w_out_na"] = f(inp["w_out_na"][0])
    wbd = np.zeros((2, 2, 4, 128, 128), np.float32)
    for gi, key in enumerate(["lru_w_a", "lru_w_i"]):
        w = np.asarray(inp[key][0], np.float32)
        for d in range(2):
            for c in range(4):
                wbd[gi, d, c, 0:64, 0:64] = w[d, 2 * c]
                wbd[gi, d, c, 64:128, 64:128] = w[d, 2 * c + 1]
    m["wbd"] = wbd
    lc = np.zeros((128, 4, 2, 3), np.float32)
    for d in range(2):
        lc[:, :, d, 0] = _col(inp["lru_b_a"][0][d], 4)
        lc[:, :, d, 1] = _col(inp["lru_b_i"][0][d], 4)
        lc[:, :, d, 2] = _col(inp["lru_lam"][0][d], 4)
    m["lcols"] = lc
    cv = np.zeros((128, 4, 5), np.float32)
    for k in range(4):
        cv[:, :, k] = _col(inp["conv_w"][0][k], 4)
    cv[:, :, 4] = _col(inp["conv_b"][0], 4)
    m["convc"] = cv
    for k in ("ident", "identb", "T1", "M2", "CSf", "CSc", "T256"):
        m[k] = _TABLES[k]
    m["BT"] = _bias_tables(np.asarray(inp["rpb_na"][0], np.float32))
    return m


_NC_CACHE = {}


def kernel(**inputs):
    if "full" not in _NC_CACHE:
        _NC_CACHE["full"] = build()
    nc = _NC_CACHE["full"]
    in_maps = [_host_inputs(inputs, b) for b in range(NCORES)]
    res = run_bass_kernel_spmd(nc, in_maps, core_ids=list(range(NCORES)))
    return np.stack([np.asarray(res.results[b]["out"], dtype=np.float32) for b in range(NCORES)], axis=0)


def _wload(C, dst, src_view, key, nk, ncols, step=512):
    for c0 in range(0, ncols, step):
        c1 = min(ncols, c0 + step)
        C.em.dma("pool", dst[:, :, c0:c1], src_view[:, :, c0:c1], W=[key])


def phase_A(C):
    nc, em, PS = C.nc, C.em, C.PS
    V, A, G, T = nc.vector, nc.scalar, nc.gpsimd, nc.tensor
    pes = C.es_mix = ExitStack()
    C.F = pes.enter_context(nc.sbuf_tensor("Fbuf", [128, 64, 512], BF16))
    C.fTc = pes.enter_context(nc.sbuf_tensor("fTc", [128, 4, LC], BF16))
    F, fTc = C.F, C.fTc
    with ExitStack() as ps_:
        sb = lambda n, s, d=F32: ps_.enter_context(nc.sbuf_tensor(n, list(s), d))
        win = sb("win", [128, 8, 1536], BF16)
        xt = [sb(f"Axt{i}", [128, 4, D]) for i in range(2)]
        hT = [sb(f"AhT{i}", [128, 8, 512], BF16) for i in range(2)]
        ugst = [sb(f"Aug{i}", [128, 8, 512]) for i in range(2)]
        ssq = sb("Assq", [128, 4]); rs = sb("Ars", [128, 4]); junk = sb("Ajunk", [128, D], BF16)
        _wload(C, win, C.win_d.rearrange("(k p) n -> p k n", p=128), "win", 8, 1536)
        xsrc = C.x_d.rearrange("(t1 t2) d -> t1 t2 d", t2=64)
        Acol = C.MODC[:, 0, 0, 0, 0, :]; Bcol = C.MODC[:, 0, 0, 1, 0, :]
        AcolC = C.MODC[:, 0, 0, 0, 1, :]; BcolC = C.MODC[:, 0, 0, 1, 1, :]
        em.dma("sp", xt[0][:], xsrc[:, 0:4, :], W=["Axt0"])
        ev = 0
        for j in range(17):
            sl = j % 2
            isctx = j == 16
            nsub = 2 if isctx else 4
            n = nsub * 128
            if j + 1 < 16:
                em.dma("sp", xt[1 - sl][:], xsrc[:, 4 * (j + 1):4 * (j + 2), :], W=[f"Axt{1 - sl}"])
            elif j + 1 == 16:
                em.dma("sp", xt[1 - sl][:, 0:2, :], C.ctx_d.rearrange("(s p) d -> p s d", p=128), W=[f"Axt{1 - sl}"])
            C.prologue(xt[sl], f"Axt{sl}", nsub, xt[sl], f"Axt{sl}", AcolC if isctx else Acol, BcolC if isctx else Bcol,
                       hT[sl], f"AhT{sl}", ssq, rs, junk, "A", [0, 1])
            for oc in range(8):
                b = 2 + oc % 3
                for k in range(8):
                    em.op("pe", lambda k=k, b=b, oc=oc, sl=sl, n=n: T.matmul(
                        PS[:, b, 0:n], lhsT=win[:, k, oc * 128:(oc + 1) * 128], rhs=hT[sl][:, k, 0:n],
                        start=(k == 0), stop=(k == 7)), R=["win", f"AhT{sl}"], W=[("ps", b)], inc=(k == 7))
                ev += 1
                if ev % 2:
                    em.op("dve", lambda b=b, oc=oc, sl=sl, n=n: V.tensor_copy(out=ugst[sl][:, oc, 0:n], in_=PS[:, b, 0:n]),
                          R=[("ps", b)], W=[f"Aug{sl}"])
                else:
                    em.op("act", lambda b=b, oc=oc, sl=sl, n=n: A.copy(out=ugst[sl][:, oc, 0:n], in_=PS[:, b, 0:n]),
                          R=[("ps", b)], W=[f"Aug{sl}"])
            if isctx:
                em.dma("pool", C.UGc.rearrange("c p n -> p c n"), ugst[sl][:, :, 0:LC], R=[f"Aug{sl}"], W=["UGc"])
                for fc in range(4):
                    b = 5 + fc % 3
                    for k in range(8):
                        em.op("pe", lambda k=k, b=b, fc=fc, sl=sl: T.matmul(
                            PS[:, b, 0:LC], lhsT=win[:, k, 1024 + fc * 128:1024 + (fc + 1) * 128], rhs=hT[sl][:, k, 0:LC],
                            start=(k == 0), stop=(k == 7)), R=["win", f"AhT{sl}"], W=[("ps", b)], inc=(k == 7))
                    em.op("dve", lambda b=b, fc=fc: V.tensor_copy(out=fTc[:, fc, :], in_=PS[:, b, 0:LC]),
                          R=[("ps", b)], W=["fTc"])
            else:
                em.dma("pool", C.UG[:, :, j * 512:(j + 1) * 512].rearrange("c p n -> p c n"), ugst[sl][:],
                       R=[f"Aug{sl}"], W=[("UG", j)])
                for s in range(4):
                    b = 5 + s % 3
                    for k in range(8):
                        em.op("pe", lambda k=k, b=b, s=s, sl=sl: T.matmul(
                            PS[:, b, :], lhsT=hT[sl][:, k, s * 128:(s + 1) * 128], rhs=win[:, k, 1024:1536],
                            start=(k == 0), stop=(k == 7)), R=["win", f"AhT{sl}"], W=[("ps", b)], inc=(k == 7))
                    ev += 1
                    if ev % 2:
                        em.op("dve", lambda b=b, s=s, j=j: V.tensor_copy(out=F[:, 4 * j + s, :], in_=PS[:, b, :]),
                              R=[("ps", b)], W=["F"])
                    else:
                        em.op("act", lambda b=b, s=s, j=j: A.copy(out=F[:, 4 * j + s, :], in_=PS[:, b, :]),
                              R=[("ps", b)], W=["F"])


def phase_C(C):
    nc, em, PS, F, fTc = C.nc, C.em, C.PS, C.F, C.fTc
    V, A, G, T = nc.vector, nc.scalar, nc.gpsimd, nc.tensor
    with ExitStack() as ps_:
        sb = lambda n, s, d=F32: ps_.enter_context(nc.sbuf_tensor(n, list(s), d))
        T1 = sb("T1s", [128, 256], BF16); M2 = sb("M2s", [64, 128, 192], BF16); CSf = sb("CSfs", [128, 256], BF16)
        CSc = sb("CScs", [128, 256], BF16); T256 = sb("T256s", [128, 2, 512], BF16)
        Ast = sb("Ast", [64, 64, 256], BF16); Y = sb("Ybuf", [128, 2, S], BF16)
        fst = [sb(f"fst{i}", [128, 2048], BF16) for i in range(2)]
        Gc = sb("Gcb", [128, 2, 256], BF16); fcs = sb("fcs", [128, LC], BF16)
        for dst, src, k in ((T1, C.T1_d, "T1"), (M2, C.M2_d, "M2"), (CSf, C.CSf_d, "CSf"), (CSc, C.CSc_d, "CSc"),
                            (T256, C.T256_d, "T256")):
            em.dma("sp", dst[:], src, W=[k])
        ev = 0
        for cc in range(4):
            for hc in range(2):
                for g4 in range(16):
                    b = 2 * (g4 % 2)
                    for q in range(4):
                        ch = cc * 128 + hc * 64 + g4 * 4 + q
                        em.op("pe", lambda b=b, q=q, ch=ch: T.matmul(
                            PS[0:64, b + q // 2, (q % 2) * 256:(q % 2) * 256 + 256], lhsT=F[:, :, ch], rhs=T1[:, :],
                            start=True, stop=True), R=["F", "T1"], W=[("ps", b), ("ps", b + 1)], inc=(q == 3))
                    ev += 1
                    dst = Ast[:, g4 * 4:(g4 + 1) * 4, :].rearrange("p a b -> p (a b)")
                    src = PS[0:64, b:b + 2, :].rearrange("p a b -> p (a b)")
                    if ev % 2:
                        em.op("dve", lambda dst=dst, src=src: V.tensor_copy(out=dst, in_=src),
                              R=[("ps", b), ("ps", b + 1)], W=["Ast"])
                    else:
                        em.op("act", lambda dst=dst, src=src: A.copy(out=dst, in_=src),
                              R=[("ps", b), ("ps", b + 1)], W=["Ast"])
                for kb in range(32):
                    b = 4 + kb % 2
                    for q in range(4):
                        k1 = kb * 4 + q
                        o = PS[hc * 64:(hc + 1) * 64, b, q * 128:(q + 1) * 128]
                        em.op("pe", lambda o=o, k1=k1: T.matmul(o, lhsT=Ast[:, :, k1], rhs=M2[:, k1, 64:192],
                                                                start=True, stop=False),
                              R=["Ast", "M2"], W=[("ps", b)], inc=False)
                        em.op("pe", lambda o=o, k1=k1: T.matmul(o, lhsT=Ast[:, :, 128 + k1], rhs=M2[:, k1, 0:128],
                                                                start=False, stop=True),
                              R=["Ast", "M2"], W=[("ps", b)], inc=(q == 3))
                    ev += 1
                    src = PS[hc * 64:(hc + 1) * 64, b, :].rearrange("p (k r c) -> p r c k", k=4, r=2)
                    dst = Y[hc * 64:(hc + 1) * 64, :, :].rearrange("p r (c k) -> p r c k", k=128)[:, :, :, kb * 4:(kb + 1) * 4]
                    if ev % 2:
                        em.op("dve", lambda dst=dst, src=src: V.tensor_copy(out=dst, in_=src), R=[("ps", b)], W=["Y"])
                    else:
                        em.op("act", lambda dst=dst, src=src: A.copy(out=dst, in_=src), R=[("ps", b)], W=["Y"])
            for tl in range(16):
                b = 6 + tl % 2
                em.op("pe", lambda b=b, tl=tl: T.matmul(PS[:, b, :], lhsT=CSf[:, 0:128], rhs=Y[:, 0, tl * 512:(tl + 1) * 512],
                                                        start=True, stop=False), R=["CSf", "Y"], W=[("ps", b)], inc=False)
                em.op("pe", lambda b=b, tl=tl: T.matmul(PS[:, b, :], lhsT=CSf[:, 128:256], rhs=Y[:, 1, tl * 512:(tl + 1) * 512],
                                                        start=False, stop=True), R=["CSf", "Y"], W=[("ps", b)], inc=True)
                fs = (tl // 4) % 2
                em.op("act", lambda b=b, tl=tl, fs=fs: A.copy(out=fst[fs][:, (tl % 4) * 512:(tl % 4 + 1) * 512], in_=PS[:, b, :]),
                      R=[("ps", b)], W=[f"fst{fs}"])
                if tl % 4 == 3:
                    t0 = (tl // 4) * 2048
                    em.dma("pool", C.MIXT[512 + cc * 128:512 + (cc + 1) * 128, t0:t0 + 2048], fst[fs][:],
                           R=[f"fst{fs}"], W=[("MIXT", 4 + cc)])
            for tc in range(2):
                em.op("pe", lambda tc=tc, cc=cc: T.matmul(PS[:, 0, 0:256], lhsT=fTc[:, cc, tc * 128:(tc + 1) * 128], rhs=CSc[:, :],
                                                          start=True, stop=True), R=["fTc", "CSc"], W=[("ps", 0)])
                em.op("dve", lambda tc=tc: V.tensor_copy(out=Gc[:, tc, :], in_=PS[:, 0, 0:256]), R=[("ps", 0)], W=["Gc"])
            for i_, (tc, part) in enumerate([(0, 0), (0, 1), (1, 0), (1, 1)]):
                em.op("pe", lambda i_=i_, tc=tc, part=part: T.matmul(
                    PS[:, 1, 0:256], lhsT=Gc[:, tc, part * 128:(part + 1) * 128], rhs=T256[:, tc, part * 256:(part + 1) * 256],
                    start=(i_ == 0), stop=(i_ == 3)), R=["Gc", "T256"], W=[("ps", 1)], inc=(i_ == 3))
            em.op("dve", lambda: V.tensor_copy(out=fcs[:, :], in_=PS[:, 1, 0:256]), R=[("ps", 1)], W=["fcs"])
            em.dma("pool", C.MIXT[512 + cc * 128:512 + (cc + 1) * 128, S:NT], fcs[:], R=["fcs"], W=[("MIXTc", 4 + cc)])
    C.es_mix.close()


PHASES += [("A", phase_A), ("C", phase_C)]


def phase_B(C):
    nc, em, PS = C.nc, C.em, C.PS
    V, A, G, T = nc.vector, nc.scalar, nc.gpsimd, nc.tensor
    TW = 2048
    with ExitStack() as ps_:
        sb = lambda n, s, d=F32: ps_.enter_context(nc.sbuf_tensor(n, list(s), d))
        bufA = sb("bufA", [128, NT]); bufB = sb("bufB", [128, 8460]); uc = sb("ucb_", [128, NT]); ucb = sb("ucbb", [128, NT], BF16)
        rt = sb("rt", [128, TW]); it_ = sb("it", [128, TW]); at = sb("at", [128, TW]); st = sb("st", [128, TW])
        wbd = sb("wbds", [128, 16, 128], BF16)
        lco = sb("lco", [128, 4, 2, 3]); cvc = sb("cvc", [128, 4, 5]); cA = sb("cA", [128, 4, 2]); carry = sb("carry", [128, 2])
        em.dma("pool", wbd[:], C.wbd_d.rearrange("g d c p n -> p (g d c) n"), W=["wbd"])
        em.dma("sp", lco[:], C.lcols_d, W=["lco"])
        em.dma("sp", cvc[:], C.convc_d, W=["cvc"])
        em.op("act", lambda: A.activation(out=cA[:], in_=lco[:, :, :, 2], func=AF.Exp, scale=-1.0), R=["lco"], W=["cA"])
        em.op("act", lambda: A.activation(out=cA[:], in_=cA[:], func=AF.Ln, bias=1.0), R=["cA"], W=["cA"])
        em.op("dve", lambda: V.tensor_scalar(out=cA[:], in0=cA[:], scalar1=-8.0, scalar2=None, op0=ALU.mult), R=["cA"], W=["cA"])
        em.op("dve", lambda: V.memset(bufB[:], 0.0), W=["bufB"])
        XO = 8200
        tiles = [(S, NT)] + [(i * TW, (i + 1) * TW) for i in range(4)]
        for c in range(4):
            em.dma("sp", bufA[:, 0:S], C.UG[c], R=[("UG", j) for j in range(16)], W=["bufA"])
            em.dma("sp", bufA[:, S:NT], C.UGc[c], R=["UGc"], W=["bufA"])
            em.op("pool", lambda: G.tensor_copy(out=bufB[:, 2:2 + S].rearrange("p (a b) -> p a b", b=64),
                                                in_=bufA[:, 0:S].rearrange("p (b a) -> p a b", a=128)),
                  R=["bufA"], W=["bufB"])
            em.op("pool", lambda: G.tensor_copy(out=bufB[:, XO + 2:XO + 2 + LC], in_=bufA[:, S:NT]), R=["bufA"], W=["bufB"])
            em.dma("sp", bufA[:, 0:S], C.UG[4 + c], R=[("UG", j) for j in range(16)], W=["bufA"])
            em.dma("sp", bufA[:, S:NT], C.UGc[4 + c], R=["UGc"], W=["bufA"])
            for (o0, src0, n) in ((0, 0, S), (S, XO, LC)):
                em.op("act", lambda o0=o0, src0=src0, n=n, c=c: A.activation(
                    out=uc[:, o0:o0 + n], in_=bufB[:, src0:src0 + n], func=AF.Identity,
                    scale=cvc[:, c, 0:1], bias=cvc[:, c, 4:5]), R=["bufB", "cvc"], W=["uc"])
                for k in range(1, 4):
                    em.op("dve", lambda o0=o0, src0=src0, n=n, c=c, k=k: V.scalar_tensor_tensor(
                        out=uc[:, o0:o0 + n], in0=bufB[:, src0 + k:src0 + k + n], scalar=cvc[:, c, k:k + 1],
                        in1=uc[:, o0:o0 + n], op0=ALU.mult, op1=ALU.add), R=["bufB", "cvc", "uc"], W=["uc"])
            em.op("pool", lambda: G.tensor_copy(out=ucb[:, :], in_=uc[:, :]), R=["uc"], W=["ucb"])
            for (lo, hi) in tiles:
                n = hi - lo
                em.op("act", lambda lo=lo, hi=hi, n=n: A.activation(out=rt[:, 0:n], in_=bufA[:, lo:hi], func=AF.Square),
                      R=["bufA"], W=["rt"])
                em.op("dve", lambda n=n: V.tensor_scalar(out=rt[:, 0:n], in0=rt[:, 0:n], scalar1=0.044715, scalar2=1.0,
                                                         op0=ALU.mult, op1=ALU.add), R=["rt"], W=["rt"])
                em.op("pool", lambda lo=lo, hi=hi, n=n: G.tensor_tensor(out=rt[:, 0:n], in0=rt[:, 0:n], in1=bufA[:, lo:hi],
                                                                        op=ALU.mult), R=["rt", "bufA"], W=["rt"])
                em.op("act", lambda n=n: A.activation(out=it_[:, 0:n], in_=rt[:, 0:n], func=AF.Sigmoid, scale=1.5957691216057308),
                      R=["rt"], W=["it"])
                em.op("pool", lambda lo=lo, hi=hi, n=n: G.tensor_tensor(out=bufA[:, lo:hi], in0=it_[:, 0:n], in1=bufA[:, lo:hi],
                                                                        op=ALU.mult), R=["it", "bufA"], W=["bufA"])
            for d in range(2):
                order = [tiles[0]] + (tiles[1:] if d == 0 else tiles[1:][::-1])
                for ti, (lo, hi) in enumerate(order):
                    n = hi - lo
                    nq = (n + 511) // 512
                    for gi in range(2):
                        for q in range(nq):
                            w_ = min(512, n - q * 512)
                            em.op("pe", lambda gi=gi, q=q, w_=w_, lo=lo, d=d, c=c: T.matmul(
                                PS[:, gi * 4 + q, 0:w_], lhsT=wbd[:, gi * 8 + d * 4 + c, :], rhs=ucb[:, lo + q * 512:lo + q * 512 + w_],
                                start=True, stop=True), R=["wbd", "ucb"], W=[("ps", gi * 4 + q)], inc=(q == nq - 1))
                    pr = PS[:, 0:4, :].rearrange("p a b -> p (a b)")[:, 0:n]
                    pi = PS[:, 4:8, :].rearrange("p a b -> p (a b)")[:, 0:n]
                    em.op("act", lambda pr=pr, n=n, c=c, d=d: A.activation(out=rt[:, 0:n], in_=pr, func=AF.Sigmoid,
                                                                           bias=lco[:, c, d, 0:1]),
                          R=[("ps", q) for q in range(4)] + ["lco"], W=["rt"])
                    em.op("act", lambda pi=pi, n=n, c=c, d=d: A.activation(out=it_[:, 0:n], in_=pi, func=AF.Sigmoid,
                                                                           bias=lco[:, c, d, 1:2]),
                          R=[("ps", 4 + q) for q in range(4)] + ["lco"], W=["it"])
                    em.op("act", lambda n=n, c=c, d=d: A.activation(out=at[:, 0:n], in_=rt[:, 0:n], func=AF.Exp,
                                                                    scale=cA[:, c, d:d + 1]), R=["rt", "cA"], W=["at"])
                    em.op("act", lambda n=n: A.activation(out=st[:, 0:n], in_=at[:, 0:n], func=AF.Square), R=["at"], W=["st"])
                    em.op("act", lambda n=n: A.activation(out=st[:, 0:n], in_=st[:, 0:n], func=AF.Sqrt, scale=-1.0, bias=1.0),
                          R=["st"], W=["st"])
                    em.op("pool", lambda n=n: G.tensor_tensor(out=it_[:, 0:n], in0=it_[:, 0:n], in1=st[:, 0:n], op=ALU.mult),
                          R=["it", "st"], W=["it"])
                    em.op("pool", lambda n=n, lo=lo, hi=hi: G.tensor_tensor(out=it_[:, 0:n], in0=it_[:, 0:n], in1=uc[:, lo:hi],
                                                                            op=ALU.mult), R=["it", "uc"], W=["it"])
                    init = 0.0 if ti == 0 else carry[:, d:d + 1]
                    if d == 0:
                        em.op("dve", lambda n=n, lo=lo, hi=hi, init=init: V.tensor_tensor_scan(
                            out=bufB[:, lo:hi], data0=at[:, 0:n], data1=it_[:, 0:n], initial=init, op0=ALU.mult, op1=ALU.add),
                              R=["at", "it", "carry", "bufB", "uc"], W=["bufB"])
                        em.op("dve", lambda hi=hi: V.tensor_copy(out=carry[:, 0:1], in_=bufB[:, hi - 1:hi]), R=["bufB"], W=["carry"])
                    else:
                        em.op("dve", lambda n=n, init=init: V.tensor_tensor_scan(
                            out=rt[:, n - 1::-1] if False else rt[:, 0:n][:, ::-1], data0=at[:, 0:n][:, ::-1],
                            data1=it_[:, 0:n][:, ::-1], initial=init, op0=ALU.mult, op1=ALU.add),
                              R=["at", "it", "carry", "rt"], W=["rt"])
                        em.op("dve", lambda: V.tensor_copy(out=carry[:, 1:2], in_=rt[:, 0:1]), R=["rt"], W=["carry"])
                        em.op("pool", lambda n=n, lo=lo, hi=hi: G.tensor_tensor(out=bufB[:, lo:hi], in0=bufB[:, lo:hi], in1=rt[:, 0:n],
                                                                                op=ALU.add), R=["bufB", "rt"], W=["bufB"])
            em.op("dve", lambda: V.tensor_tensor(out=ucb[:, 0:S].rearrange("p (a b) -> p a b", b=64),
                                                 in0=bufB[:, 0:S].rearrange("p (a b) -> p a b", b=64),
                                                 in1=bufA[:, 0:S].rearrange("p (b a) -> p a b", a=128), op=ALU.mult),
                  R=["bufB", "bufA", "ucb"], W=["ucb"])
            em.op("dve", lambda: V.tensor_tensor(out=ucb[:, S:NT], in0=bufB[:, S:NT], in1=bufA[:, S:NT], op=ALU.mult),
                  R=["bufB", "bufA", "ucb"], W=["ucb"])
            em.dma("pool", C.MIXT[c * 128:(c + 1) * 128, :], ucb[:, :], R=["ucb"], W=[("MIXT", c)])
            if c < 3:
                em.op("dve", lambda: V.memset(bufB[:, 0:2], 0.0), R=["bufB"], W=["bufB"])
                em.op("dve", lambda: V.memset(bufB[:, S:8460], 0.0), R=["bufB"], W=["bufB"])


def _tok_tiles(ntok_tile):
    return None


def phase_D1(C, layer=0):
    nc, em, PS = C.nc, C.em, C.PS
    V, A, G, T = nc.vector, nc.scalar, nc.gpsimd, nc.tensor
    with ExitStack() as ps_:
        sb = lambda n, s, d=F32: ps_.enter_context(nc.sbuf_tensor(n, list(s), d))
        wo = sb("D1wo", [128, 8, D], BF16)
        xt = [sb(f"D1xt{i}", [128, 4, D]) for i in range(2)]
        mt = [sb(f"D1mt{i}", [128, 8, 512], BF16) for i in range(2)]
        Gx = sb("D1Gx", [128, D]); Gc = sb("D1Gc", [128, D])
        ssq = sb("D1ssq", [128, 4]); rs = sb("D1rs", [128, 4]); junk = sb("D1junk", [128, D], BF16); tt = sb("D1tt", [128, D])
        _wload(C, wo, C.woab_d.rearrange("(k p) n -> p k n", p=128), "D1wo", 8, D)
        em.dma("sp", Gx[:], C.GROW[0, 0, 0, :].partition_broadcast(128), R=[("GROW", 0, 0)], W=["D1Gx"])
        em.dma("sp", Gc[:], C.GROW[0, 0, 1, :].partition_broadcast(128), R=[("GROW", 0, 0)], W=["D1Gc"])
        mixv = C.MIXT.rearrange("(k p) t -> p k t", p=128)
        mixkeys = [("MIXT", i) for i in range(8)] + [("MIXTc", 4 + i) for i in range(4)]

        def load(j, sl):
            if j < 16:
                em.dma("sp", xt[sl][:], C.x_d[j * 512:(j + 1) * 512, :].rearrange("(s p) d -> p s d", p=128), W=[f"D1xt{sl}"])
                em.dma("sp", mt[sl][:], mixv[:, :, j * 512:(j + 1) * 512], R=mixkeys, W=[f"D1mt{sl}"])
            else:
                em.dma("sp", xt[sl][:, 0:2, :], C.ctx_d.rearrange("(s p) d -> p s d", p=128), W=[f"D1xt{sl}"])
                em.dma("sp", mt[sl][:, :, 0:LC], mixv[:, :, S:NT], R=mixkeys, W=[f"D1mt{sl}"])
        load(0, 0)
        for j in range(17):
            sl = j % 2
            if j + 1 < 17:
                load(j + 1, 1 - sl)
            nsub = 4 if j < 16 else 2
            for s in range(nsub):
                pb = 2 * (s % 4)
                for h in range(2):
                    for k in range(8):
                        em.op("pe", lambda k=k, h=h, s=s, sl=sl, pb=pb: T.matmul(
                            PS[:, pb + h, :], lhsT=mt[sl][:, k, s * 128:(s + 1) * 128], rhs=wo[:, k, h * 512:(h + 1) * 512],
                            start=(k == 0), stop=(k == 7)), R=["D1wo", f"D1mt{sl}"], W=[("ps", pb + h)], inc=(k == 7))
                C.epilogue(pb, xt[sl][:, s, :], f"D1xt{sl}", (Gx if j < 16 else Gc)[:, :], "D1Gx" if j < 16 else "D1Gc",
                           ssq, rs, junk, tt, "D1")
            if j < 16:
                em.dma("pool", C.X1[j * 512:(j + 1) * 512, :].rearrange("(s p) d -> p s d", p=128), xt[sl][:],
                       R=[f"D1xt{sl}"], W=[("X1", j)])
            else:
                em.dma("pool", C.C1.rearrange("(s p) d -> p s d", p=128), xt[sl][:, 0:2, :], R=[f"D1xt{sl}"], W=["C1"])


def phase_FFN(C, layer, Xin, Cin, Xout, Cout, inkey, outkey):
    nc, em, PS = C.nc, C.em, C.PS
    V, A, G, T = nc.vector, nc.scalar, nc.gpsimd, nc.tensor
    tg = f"F{layer}"
    with ExitStack() as ps_:
        sb = lambda n, s, d=F32: ps_.enter_context(nc.sbuf_tensor(tg + n, list(s), d))
        wg = sb("wg", [128, 8, FH], BF16); wu = sb("wu", [128, 8, FH], BF16); wd = sb("wd", [128, NJ, D], BF16)
        xt = [sb(f"xt{i}", [128, 2, D]) for i in range(2)]
        xs = sb("xs", [128, 2, D]); hT = sb("hT", [128, 8, 256], BF16); hh = sb("hh", [128, NJ, 256], BF16)
        sg = [sb(f"sg{i}", [128, 256]) for i in range(2)]
        Gx = sb("Gx", [128, D]); Gc = sb("Gc", [128, D])
        ssq = sb("ssq", [128, 4]); rs = sb("rs", [128, 4]); junk = sb("junk", [128, D], BF16); tt = sb("tt", [128, D])
        _wload(C, wg, C.wg_d[layer].rearrange("(k p) n -> p k n", p=128), tg + "wg", 8, FH, 704)
        _wload(C, wu, C.wu_d[layer].rearrange("(k p) n -> p k n", p=128), tg + "wu", 8, FH, 704)
        _wload(C, wd, C.wd_d[layer].rearrange("(k p) n -> p k n", p=128), tg + "wd", NJ, D, 512)
        em.dma("sp", Gx[:], C.GROW[layer, 1, 0, :].partition_broadcast(128), R=[("GROW", layer, 1)], W=[tg + "Gx"])
        em.dma("sp", Gc[:], C.GROW[layer, 1, 1, :].partition_broadcast(128), R=[("GROW", layer, 1)], W=[tg + "Gc"])
        ntile = 32 + (1 if Cin is not None else 0)

        def load(j, sl):
            if j < 32:
                em.dma("sp", xt[sl][:], Xin[j * 256:(j + 1) * 256, :].rearrange("(s p) d -> p s d", p=128),
                       R=[(inkey, j // 2)], W=[tg + f"xt{sl}"])
            else:
                em.dma("sp", xt[sl][:], Cin.rearrange("(s p) d -> p s d", p=128), R=[inkey + "c"], W=[tg + f"xt{sl}"])
        load(0, 0)
        for j in range(ntile):
            sl = j % 2
            if j + 1 < ntile:
                load(j + 1, 1 - sl)
            path = 0 if j < 32 else 1
            C.prologue(xt[sl], tg + f"xt{sl}", 2, xs, tg + "xs", C.MODC[:, layer, 1, 0, path, :], C.MODC[:, layer, 1, 1, path, :],
                       hT, tg + "hT", ssq, rs, junk, tg, [7])
            for jj in range(NJ):
                b = jj % 3
                for gi, w_ in enumerate((wg, wu)):
                    for k in range(8):
                        em.op("pe", lambda k=k, b=b, gi=gi, w_=w_, jj=jj: T.matmul(
                            PS[:, b, gi * 256:(gi + 1) * 256], lhsT=w_[:, k, jj * 128:(jj + 1) * 128], rhs=hT[:, k, :],
                            start=(k == 0), stop=(k == 7)), R=[tg + "wg", tg + "wu", tg + "hT"], W=[("ps", b)],
                              inc=(k == 7 and gi == 1))
                s2 = jj % 2
                em.op("act", lambda b=b, s2=s2: A.activation(out=sg[s2][:, :], in_=PS[:, b, 0:256], func=AF.Silu),
                      R=[("ps", b)], W=[tg + f"sg{s2}"])
                em.op("dve", lambda b=b, s2=s2, jj=jj: V.tensor_tensor(out=hh[:, jj, :], in0=sg[s2][:, :], in1=PS[:, b, 256:512],
                                                                      op=ALU.mult), R=[("ps", b), tg + f"sg{s2}"], W=[tg + "hh"])
            for s in range(2):
                pb = 3 + 2 * s
                for h in range(2):
                    for jj in range(NJ):
                        em.op("pe", lambda jj=jj, h=h, s=s, pb=pb: T.matmul(
                            PS[:, pb + h, :], lhsT=hh[:, jj, s * 128:(s + 1) * 128], rhs=wd[:, jj, h * 512:(h + 1) * 512],
                            start=(jj == 0), stop=(jj == NJ - 1)), R=[tg + "wd", tg + "hh"], W=[("ps", pb + h)], inc=(jj == NJ - 1))
                C.epilogue(pb, xt[sl][:, s, :], tg + f"xt{sl}", (Gx if path == 0 else Gc)[:, :], tg + ("Gx" if path == 0 else "Gc"),
                           ssq, rs, junk, tt, tg)
            if j < 32:
                em.dma("pool", Xout[j * 256:(j + 1) * 256, :].rearrange("(s p) d -> p s d", p=128), xt[sl][:],
                       R=[tg + f"xt{sl}"], W=[(outkey, j // 2)] if j % 2 else [(outkey + "h", j)])
            else:
                em.dma("pool", Cout.rearrange("(s p) d -> p s d", p=128), xt[sl][:], R=[tg + f"xt{sl}"], W=[outkey + "c"])


PHASES += [("B", phase_B), ("D1", phase_D1),
           ("D2", lambda C: phase_FFN(C, 0, C.X1, C.C1, C.X2, C.C2, "X1", "X2"))]


def phase_E(C):
    nc, em, PS = C.nc, C.em, C.PS
    V, A, G, T = nc.vector, nc.scalar, nc.gpsimd, nc.tensor
    with ExitStack() as ps_:
        sb = lambda n, s, d=F32: ps_.enter_context(nc.sbuf_tensor("E" + n, list(s), d))
        wq = sb("wq", [128, 8, 3 * D], BF16)
        xt = [sb(f"xt{i}", [128, 4, D]) for i in range(2)]
        hT = sb("hT", [128, 8, 512], BF16)
        qk = [sb(f"qk{i}", [128, 16, 512], BF16) for i in range(2)]
        vst = [sb(f"vst{i}", [128, 4, D], BF16) for i in range(2)]
        ssq = sb("ssq", [128, 4]); rs = sb("rs", [128, 4]); junk = sb("junk", [128, D], BF16)
        _wload(C, wq, C.wqkv_d.rearrange("(k p) n -> p k n", p=128), "Ewq", 8, 3 * D)
        QTv = C.QT.rearrange("(c p) t -> p c t", p=128); KTv = C.KT.rearrange("(c p) t -> p c t", p=128)

        def load(j, sl):
            if j < 16:
                em.dma("sp", xt[sl][:], C.X2[j * 512:(j + 1) * 512, :].rearrange("(s p) d -> p s d", p=128), W=[f"Ext{sl}"])
            else:
                em.dma("sp", xt[sl][:, 0:2, :], C.C2.rearrange("(s p) d -> p s d", p=128), W=[f"Ext{sl}"])
        load(0, 0)
        ev = 0
        for j in range(17):
            sl = j % 2
            if j + 1 < 17:
                load(j + 1, 1 - sl)
            isctx = j == 16
            nsub = 2 if isctx else 4
            n = nsub * 128
            path = 1 if isctx else 0
            C.prologue(xt[sl], f"Ext{sl}", nsub, xt[sl], f"Ext{sl}", C.MODC[:, 1, 0, 0, path, :], C.MODC[:, 1, 0, 1, path, :],
                       hT, "EhT", ssq, rs, junk, "E", [0, 1])
            for oc in range(8 if isctx else 0, 16) if isctx else range(16):
                b = 2 + oc % 2
                for k in range(8):
                    em.op("pe", lambda k=k, b=b, oc=oc, n=n: T.matmul(
                        PS[:, b, 0:n], lhsT=wq[:, k, oc * 128:(oc + 1) * 128], rhs=hT[:, k, 0:n],
                        start=(k == 0), stop=(k == 7)), R=["Ewq", "EhT"], W=[("ps", b)], inc=(k == 7))
                if oc < 8:
                    em.op("act", lambda b=b, oc=oc, sl=sl, n=n: A.activation(out=qk[sl][:, oc, 0:n], in_=PS[:, b, 0:n],
                                                                             func=AF.Copy, scale=0.125),
                          R=[("ps", b)], W=[f"Eqk{sl}"])
                else:
                    em.op("dve", lambda b=b, oc=oc, sl=sl, n=n: V.tensor_copy(out=qk[sl][:, oc, 0:n], in_=PS[:, b, 0:n]),
                          R=[("ps", b)], W=[f"Eqk{sl}"])
            if not isctx:
                em.dma("pool", QTv[:, :, j * 512:(j + 1) * 512], qk[sl][:, 0:8, :], R=[f"Eqk{sl}"], W=[("QT", j)])
                em.dma("pool", KTv[:, :, j * 512:(j + 1) * 512], qk[sl][:, 8:16, :], R=[f"Eqk{sl}"], W=[("KT", j)])
            else:
                em.dma("pool", KTv[:, :, S:NT], qk[sl][:, 8:16, 0:LC], R=[f"Eqk{sl}"], W=[("KT", j)])
            for s in range(nsub):
                pb = 4 + 2 * (s % 2)
                for h in range(2):
                    for k in range(8):
                        em.op("pe", lambda k=k, h=h, s=s, pb=pb: T.matmul(
                            PS[:, pb + h, :], lhsT=hT[:, k, s * 128:(s + 1) * 128], rhs=wq[:, k, 2048 + h * 512:2048 + (h + 1) * 512],
                            start=(k == 0), stop=(k == 7)), R=["Ewq", "EhT"], W=[("ps", pb + h)], inc=(k == 7))
                ev += 1
                src = PS[:, pb:pb + 2, :].rearrange("p a b -> p (a b)")
                if ev % 2:
                    em.op("dve", lambda s=s, sl=sl, src=src: V.tensor_copy(out=vst[sl][:, s, :], in_=src),
                          R=[("ps", pb), ("ps", pb + 1)], W=[f"Evst{sl}"])
                else:
                    em.op("act", lambda s=s, sl=sl, src=src: A.copy(out=vst[sl][:, s, :], in_=src),
                          R=[("ps", pb), ("ps", pb + 1)], W=[f"Evst{sl}"])
            t0 = S if isctx else j * 512
            em.dma("pool", C.VD[t0:t0 + n, :].rearrange("(s p) d -> p s d", p=128), vst[sl][:, 0:nsub, :],
                   R=[f"Evst{sl}"], W=[("VD", j)])


def phase_F(C):
    nc, em, PS = C.nc, C.em, C.PS
    V, A, G, T = nc.vector, nc.scalar, nc.gpsimd, nc.tensor
    with ExitStack() as ps_:
        sb = lambda n, s, d=F32: ps_.enter_context(nc.sbuf_tensor("AT" + n, list(s), d))
        wo = sb("wo", [128, 8, D], BF16)
        KTb = [sb(f"KTb{i}", [128, 8, 1024], BF16) for i in range(2)]
        Vb = [sb(f"Vb{i}", [128, 8, 16, 65], BF16) for i in range(2)]
        QTb = sb("QTb", [128, 8, 512], BF16); xt = sb("xt", [128, 4, D])
        KTc = sb("KTc", [128, 8, LC], BF16); Vc = sb("Vc", [128, 2, 16, 65], BF16)
        BTi = sb("BTi", [128, 16, 5, 128], BF16); BTe = sb("BTe", [128, 16, 5, 128], BF16)
        PT = [sb(f"PT{i}", [128, 896], BF16) for i in range(2)]
        Ot = sb("Ot", [128, D]); OTt = sb("OTt", [128, 8, 128], BF16); rden = sb("rden", [128, 4])
        Gx = sb("Gx", [128, D]); ssq = sb("ssq", [128, 4]); rs = sb("rs", [128, 4]); junk = sb("junk", [128, D], BF16)
        tt = sb("tt", [128, D])
        _wload(C, wo, C.wona_d.rearrange("(k p) n -> p k n", p=128), "Awo", 8, D)
        em.dma("sp", Gx[:], C.GROW[1, 0, 0, :].partition_broadcast(128), W=["AGx"])
        em.dma("sp", BTi[:], C.BT_d[2], W=["ABTi"])
        KTv = C.KT.rearrange("(c p) t -> p c t", p=128); QTv = C.QT.rearrange("(c p) t -> p c t", p=128)
        em.dma("sp", KTc[:], KTv[:, :, S:NT], W=["AKTc"])
        for i in range(2):
            em.op("pool", lambda i=i: G.memset(Vb[i][:, :, :, 64:65], 1.0), W=[f"AVb{i}"])
        em.op("pool", lambda: G.memset(Vc[:, :, :, 64:65], 1.0), W=["AVc"])
        for c in range(2):
            em.dma("sp", Vc[:, c, :, 0:64], C.VD[S + c * 128:S + (c + 1) * 128, :].rearrange("p (h d) -> p h d", d=64), W=["AVc"])

        def load(blk, sl):
            r0 = 8 * blk
            kb = min(max(r0 - 4, 0), 112)
            em.dma("sp", KTb[sl][:], KTv[:, :, kb * 64:kb * 64 + 1024], W=[f"AKTb{sl}"])
            for c in range(8):
                t0 = kb * 64 + c * 128
                em.dma("sp", Vb[sl][:, c, :, 0:64], C.VD[t0:t0 + 128, :].rearrange("p (h d) -> p h d", d=64), W=[f"AVb{sl}"])
        load(0, 0)
        for blk in range(16):
            sl = blk % 2
            r0 = 8 * blk
            kb = min(max(r0 - 4, 0), 112)
            em.dma("sp", QTb[:], QTv[:, :, r0 * 64:r0 * 64 + 512], W=["AQTb"])
            em.dma("sp", xt[:], C.X2[blk * 512:(blk + 1) * 512, :].rearrange("(s p) d -> p s d", p=128), W=["Axt"])
            if blk + 1 < 16:
                load(blk + 1, 1 - sl)
            pinfo = []
            for i in range(4):
                gp = r0 + 2 * i
                ks = min(max(gp - 4, 0), 118)
                pinfo.append(((ks - kb) // 2, {0: 0, 2: 1, 124: 3, 126: 4}.get(gp, 2)))

            def qk(i, h):
                off, var = pinfo[i]
                if var != 2 and h == 0:
                    em.dma("sp", BTe[:], C.BT_d[var], W=["ABTe"])
                BTv, btk = (BTi, "ABTi") if var == 2 else (BTe, "ABTe")
                j = h // 2; e = h % 2
                p0, p1 = 64 * e, 64 * e + 64
                sb_ = 2 * (h % 2)
                for cl in range(7):
                    o = PS[:, sb_ + cl // 4, (cl % 4) * 128:(cl % 4 + 1) * 128]
                    if cl < 5:
                        lt = KTb[sl][p0:p1, j, (off + cl) * 128:(off + cl + 1) * 128]
                    else:
                        lt = KTc[p0:p1, j, (cl - 5) * 128:(cl - 4) * 128]
                    em.op("pe", lambda o=o, lt=lt, cl=cl, j=j, p0=p0, p1=p1, i=i: T.matmul(
                        o, lhsT=lt, rhs=QTb[p0:p1, j, i * 128:(i + 1) * 128], start=(cl % 4 == 0), stop=False),
                          R=[f"AKTb{sl}", "AKTc", "AQTb"], W=[("ps", sb_ + cl // 4)], inc=False)
                    if cl == 3:
                        em.op("pe", lambda h=h, BTv=BTv, sb_=sb_: T.matmul(
                            PS[:, sb_, :], lhsT=C.identb[:, :], rhs=BTv[:, h, 0:4, :].rearrange("p a b -> p (a b)"),
                            start=False, stop=True), R=["identb", btk], W=[("ps", sb_)], inc=False)
                    if cl == 6:
                        em.op("pe", lambda h=h, BTv=BTv, sb_=sb_: T.matmul(
                            PS[:, sb_ + 1, 0:128], lhsT=C.identb[:, :], rhs=BTv[:, h, 4, :],
                            start=False, stop=True), R=["identb", btk], W=[("ps", sb_ + 1)], inc=True)

            def ex(i, h):
                sb_ = 2 * (h % 2)
                src = PS[:, sb_:sb_ + 2, :].rearrange("p a b -> p (a b)")[:, 0:896]
                em.op("act", lambda src=src, h=h: A.activation(out=PT[h % 2][:, :], in_=src, func=AF.Exp),
                      R=[("ps", sb_), ("ps", sb_ + 1)], W=[f"APT{h % 2}"])

            def pv(i, h):
                off, var = pinfo[i]
                ob = 4 + (h // 4) % 2
                so = (h % 4) * 128
                for c in range(7):
                    rhs = Vb[sl][:, off + c, h, :] if c < 5 else Vc[:, c - 5, h, :]
                    em.op("pe", lambda c=c, rhs=rhs, h=h, ob=ob, so=so: T.matmul(
                        PS[:, ob, so:so + 65], lhsT=PT[h % 2][:, c * 128:(c + 1) * 128], rhs=rhs,
                        start=(c == 0), stop=(c == 6)), R=[f"APT{h % 2}", f"AVb{sl}", "AVc"], W=[("ps", ob)], inc=(c == 6))
                if h % 4 == 3:
                    em.op("dve", lambda ob=ob: V.reciprocal(out=rden[:, 0:4], in_=PS[:, ob, 64:512:128]),
                          R=[("ps", ob)], W=["Arden"])
                    for hh in range(4):
                        hd = h - 3 + hh
                        em.op("dve", lambda ob=ob, hh=hh, hd=hd: V.tensor_scalar(
                            out=Ot[:, hd * 64:(hd + 1) * 64], in0=PS[:, ob, hh * 128:hh * 128 + 64],
                            scalar1=rden[:, hh:hh + 1], scalar2=None, op0=ALU.mult), R=[("ps", ob), "Arden"], W=["AOt"])

            def fin(i):
                for k in range(8):
                    em.op("pe", lambda k=k: T.transpose(PS[:, 6 + k // 4, (k % 4) * 128:(k % 4 + 1) * 128],
                                                        Ot[:, k * 128:(k + 1) * 128], C.ident[:]),
                          R=["AOt", "ident"], W=[("ps", 6 + k // 4)], inc=(k % 4 == 3))
                for hb in range(2):
                    em.op("act" if hb else "dve",
                          (lambda hb=hb: A.copy(out=OTt[:, 4 * hb:4 * hb + 4, :].rearrange("p a b -> p (a b)"), in_=PS[:, 6 + hb, :])) if hb else
                          (lambda hb=hb: V.tensor_copy(out=OTt[:, 4 * hb:4 * hb + 4, :].rearrange("p a b -> p (a b)"), in_=PS[:, 6 + hb, :])),
                          R=[("ps", 6 + hb)], W=["AOTt"])
                for hf in range(2):
                    for k in range(8):
                        em.op("pe", lambda k=k, hf=hf: T.matmul(PS[:, 6 + hf, :], lhsT=OTt[:, k, :], rhs=wo[:, k, hf * 512:(hf + 1) * 512],
                                                                start=(k == 0), stop=(k == 7)),
                              R=["AOTt", "Awo"], W=[("ps", 6 + hf)], inc=(k == 7))
                C.epilogue(6, xt[:, i, :], "Axt", Gx[:, :], "AGx", ssq, rs, junk, tt, "A")

            items = [(i, h) for i in range(4) for h in range(16)]
            qk(*items[0])
            for n_, (i, h) in enumerate(items):
                if n_ + 1 < len(items):
                    qk(*items[n_ + 1])
                ex(i, h)
                pv(i, h)
                if h == 15:
                    fin(i)
            em.dma("pool", C.X3[blk * 512:(blk + 1) * 512, :].rearrange("(s p) d -> p s d", p=128), xt[:], R=["Axt"], W=[("X3", blk)])


PHASES += [("E", phase_E), ("F", phase_F),
           ("G", lambda C: phase_FFN(C, 1, C.X3, None, C.out_d, None, "X3", "OUT"))]
```

```python
import math
from contextlib import ExitStack
import numpy as np
import ml_dtypes
import concourse.bass as bass
import concourse.mybir as mybir
from concourse.bass_utils import run_bass_kernel_spmd

F32 = mybir.dt.float32
BF16 = mybir.dt.bfloat16
AF = mybir.ActivationFunctionType
ALU = mybir.AluOpType
NPBF = ml_dtypes.bfloat16

D = 1024
S = 8192
LC = 256
NT = S + LC
FH = 2816
NJ = FH // 128
EPS = 1e-6
NCORES = 8
NL = 4608
NLT = NL // 512
LBASE = 3584
NEG = -30000.0


class Em:
    LIMIT = 30000
    NDS = 8

    def __init__(self, nc, es):
        self.nc = nc
        self.es = es
        self.eng = dict(pe=nc.tensor, act=nc.scalar, dve=nc.vector, pool=nc.gpsimd, sp=nc.sync)
        self.sem = {}
        self.semkey = {}
        self.cnt = {}
        self.nsem = 0
        for e in self.eng:
            self._newsem(e)
        self.waited = {e: {} for e in self.eng}
        self.lastw = {}
        self.readers = {}
        self.dsem = {}
        self.ndma = {}
        for q in ("sp", "pool", "act"):
            self.dsem[q] = [es.enter_context(nc.semaphore(f"d{q}{i}")) for i in range(self.NDS)]
            self.ndma[q] = 0

    def _newsem(self, e):
        self.nsem += 1
        self.sem[e] = self.es.enter_context(self.nc.semaphore(f"s{e}{self.nsem}"))
        self.semkey[e] = (e, self.nsem)
        self.cnt[e] = 0

    def _deps(self, engine, R, W):
        deps = {}
        for k in list(R) + list(W):
            t = self.lastw.get(k)
            if t is not None:
                if t[0] not in deps or deps[t[0]][2] < t[2]:
                    deps[t[0]] = t
        for k in W:
            for t in self.readers.get(k, {}).values():
                if t[0] not in deps or deps[t[0]][2] < t[2]:
                    deps[t[0]] = t
        e = self.eng[engine]
        for sk, t in deps.items():
            if engine == "pe" and t[3] == "pe":
                continue
            if self.waited[engine].get(sk, 0) >= t[2]:
                continue
            e.wait_ge(t[1], t[2])
            self.waited[engine][sk] = t[2]

    def _record(self, tok, R, W):
        for k in W:
            self.lastw[k] = tok
            self.readers[k] = {}
        for k in R:
            d = self.readers.setdefault(k, {})
            if tok[0] not in d or d[tok[0]][2] < tok[2]:
                d[tok[0]] = tok

    def op(self, engine, fn, R=(), W=(), inc=True):
        self._deps(engine, R, W)
        ins = fn()
        if inc:
            ins.then_inc(self.sem[engine], 1)
            self.cnt[engine] += 1
            tok = (self.semkey[engine], self.sem[engine], self.cnt[engine], engine)
            self._record(tok, R, W)
            if self.cnt[engine] >= self.LIMIT:
                self._newsem(engine)
        else:
            tok = (self.semkey[engine], self.sem[engine], self.cnt[engine] + 1, engine)
            self._record(tok, R, W)
        return ins

    def dma(self, q, out, in_, R=(), W=(), **kw):
        self._deps(q, R, W)
        i = self.ndma[q]
        self.ndma[q] += 1
        sem = self.dsem[q][i % self.NDS]
        rnd = i // self.NDS
        sk = ("dma", q, i % self.NDS)
        if rnd > 0 and self.waited[q].get(sk, 0) < 16 * rnd:
            self.eng[q].wait_ge(sem, 16 * rnd)
            self.waited[q][sk] = 16 * rnd
        self.eng[q].dma_start(out=out, in_=in_, **kw).then_inc(sem, 16)
        tok = (sk, sem, 16 * (rnd + 1), None)
        self._record(tok, R, W)

    def finish(self):
        sp = self.eng["sp"]
        for q in self.dsem:
            n = self.ndma[q]
            for s in range(self.NDS):
                uses = (n - s + self.NDS - 1) // self.NDS if n > s else 0
                if uses > 0:
                    sp.wait_ge(self.dsem[q][s], 16 * uses)
        for e in ("pe", "act", "dve", "pool"):
            if self.cnt[e] > 0:
                sp.wait_ge(self.sem[e], self.cnt[e])


def _tables():
    t = {}
    t["ident"] = np.eye(128, dtype=np.float32)
    t["identb"] = np.eye(128, dtype=np.float32).astype(NPBF)
    a = np.arange(128, dtype=np.float64)
    ang = 2 * np.pi * np.outer(a, a) / 128.0
    t["T1"] = np.concatenate([np.cos(ang), -np.sin(ang)], axis=1).astype(NPBF)
    t2 = np.arange(64, dtype=np.float64)[:, None, None]
    k1 = np.arange(128, dtype=np.float64)[None, :, None]
    k2 = np.arange(64, dtype=np.float64)[None, None, :]
    ph = 2 * np.pi * (t2 * k2 / 64.0 + t2 * k1 / 8192.0)
    Mc, Ms = np.cos(ph), np.sin(ph)
    t["M2"] = np.concatenate([Ms, Mc, -Ms], axis=2).astype(NPBF)
    c = np.arange(64, dtype=np.float64)
    angc = 2 * np.pi * np.outer(c, c) / 64.0
    Cc, Sc = np.cos(angc), np.sin(angc)
    z = np.zeros((64, 64))
    Cbd = np.block([[Cc, z], [z, Cc]])
    Sbd = np.block([[Sc, z], [z, Sc]])
    sx = 1.0 / math.sqrt(8192.0 * 64.0)
    t["CSf"] = (np.concatenate([Cbd, Sbd], axis=1) * sx).astype(NPBF)
    sc_ = 1.0 / math.sqrt(256.0 * 64.0)
    t["CSc"] = (np.concatenate([Cbd, Sbd], axis=1) * sc_).astype(NPBF)
    p = np.arange(256, dtype=np.float64)
    angp = 2 * np.pi * np.outer(p, p) / 256.0
    T256 = np.concatenate([np.cos(angp), -np.sin(angp)], axis=1)
    t["T256"] = T256.reshape(2, 128, 512).transpose(1, 0, 2).copy().astype(NPBF)
    return t


_TABLES = None


def _bias_tables(rpb):
    H = 16
    out = np.full((5, 5 * 128, H, 128), NEG, dtype=np.float32)
    kr = np.arange(10)[:, None, None, None]
    kc = np.arange(64)[None, :, None, None]
    qr = np.arange(2)[None, None, :, None]
    qc = np.arange(64)[None, None, None, :]
    cs = np.clip(qc - 8, 0, 48)
    for vi, (gp, ks) in enumerate([(0, 0), (2, 0), (60, 56), (124, 118), (126, 118)]):
        gq = gp + qr
        rs = np.clip(gq - 4, 0, 120)
        gk = ks + kr
        valid = (gk >= rs) & (gk < rs + 8) & (kc >= cs) & (kc < cs + 16)
        dr = np.clip(gk - gq + 7, 0, 14)
        dc = np.clip(kc - qc + 15, 0, 30)
        valid, dr, dc = np.broadcast_arrays(valid, dr, dc)
        vals = rpb[:, dr, dc]
        vals = np.where(valid[None], vals, NEG)
        out[vi] = vals.transpose(1, 2, 0, 3, 4).reshape(640, H, 128)
    bt = out.reshape(5, 5, 128, H, 128).transpose(0, 2, 3, 1, 4)
    return np.ascontiguousarray(bt).astype(NPBF)


def _edge_tables(rpb, h):
    H = 16
    out = np.empty((2, 4, 128, H, 8, 128), dtype=NPBF)
    kr = np.arange(16)[:, None, None, None]
    kc = np.arange(64)[None, :, None, None]
    qr = np.arange(2)[None, None, :, None]
    qc = np.arange(64)[None, None, None, :]
    cs = np.clip(qc - 8, 0, 48)
    for eb, (r0, kb) in enumerate([(0, 0), (64, 56)]):
        for i in range(4):
            gq = r0 + 2 * i + qr + 56 * h
            gk = kb + kr + 56 * h
            rs = np.clip(gq - 4, 0, 120)
            valid = (gk >= rs) & (gk < rs + 8) & (kc >= cs) & (kc < cs + 16)
            dr = np.clip(gk - gq + 7, 0, 14)
            dc = np.clip(kc - qc + 15, 0, 30)
            valid, dr, dc = np.broadcast_arrays(valid, dr, dc)
            vals = np.where(valid[None], rpb[:, dr, dc], NEG)
            t = vals.transpose(1, 2, 0, 3, 4).reshape(8, 128, H, 128)
            out[eb, i] = t.transpose(1, 2, 0, 3).astype(NPBF)
    return out


def _col(v, nchunk):
    return np.ascontiguousarray(np.asarray(v, np.float32).reshape(nchunk, 128).T)


def build(stop="all", debug=False):
    nc = bass.Bass("TRN2", target_bir_lowering=False)

    def din(name, shape, dt=F32):
        return nc.dram_tensor(name, list(shape), dt, kind="ExternalInput").ap()

    skind = "ExternalOutput" if debug else "Internal"

    def dscr(name, shape, dt):
        return nc.dram_tensor(name, list(shape), dt, kind=skind).ap()

    x_d = din("x", [S, D]); xloc_d = din("xloc", [NL, D]); hsel_d = din("hsel", [128, 2]); ctx_d = din("ctx", [LC, D]); ccols_d = din("ccols", [128, 16])
    wmod_d = din("w_mod", [2, D, 6 * D]); bmodc_d = din("bmodc", [128, 2, 48]); bmod_d = din("b_mod", [2, 6 * D])
    gcols_d = din("gcols", [128, 2, 2, 8]); gpm_d = din("g_post_mix", [2, D]); gpf_d = din("g_post_ffn", [2, D])
    win_d = din("w_in_ab", [D, 1536]); woab_d = din("w_out_ab", [D, D])
    wg_d = din("w_ffn_gate", [2, D, FH]); wu_d = din("w_ffn_up", [2, D, FH]); wd_d = din("w_ffn_down", [2, FH, D])
    wqkv_d = din("w_qkv_na", [D, 3 * D]); wona_d = din("w_out_na", [D, D])
    wbd_d = din("wbd", [2, 2, 4, 128, 128]); lcols_d = din("lcols", [128, 4, 2, 3]); convc_d = din("convc", [128, 4, 5])
    ident_d = din("ident", [128, 128]); identb_d = din("identb", [128, 128], BF16)
    T1_d = din("T1", [128, 256], BF16); M2_d = din("M2", [64, 128, 192], BF16); CSf_d = din("CSf", [128, 256], BF16)
    CSc_d = din("CSc", [128, 256], BF16); T256_d = din("T256", [128, 2, 512], BF16)
    BT_d = din("BT", [128, 16, 5, 128], BF16); BTE_d = din("BTE", [2, 4, 128, 16, 8, 128], BF16)
    out_d = nc.dram_tensor("out", [NL, D], F32, kind="ExternalOutput").ap()

    UG = dscr("UG", [8, 128, S], F32)
    UGc = dscr("UGc", [8, 128, LC], F32)
    MIXT = dscr("MIXT", [D, NT], BF16)
    GROW = dscr("GROW", [2, 2, 2, D], F32)
    X1 = dscr("X1", [NL, D], F32); C1 = dscr("C1", [LC, D], F32)
    X2 = dscr("X2", [NL, D], F32); C2 = dscr("C2", [LC, D], F32)
    X3 = dscr("X3", [NL, D], F32)
    QT = dscr("QT", [D, NL], BF16); KT = dscr("KT", [D, NL + LC], BF16); VD = dscr("VD", [NL + LC, D], BF16)

    with ExitStack() as es:
        em = Em(nc, es)

        def barrier():
            for e in ("pe", "act", "dve", "pool", "sp"):
                eng = em.eng[e]
                for f in ("pe", "act", "dve", "pool"):
                    if f != e and em.cnt[f] > 0 and em.waited[e].get(em.semkey[f], 0) < em.cnt[f]:
                        eng.wait_ge(em.sem[f], em.cnt[f]); em.waited[e][em.semkey[f]] = em.cnt[f]
                for q in em.dsem:
                    n = em.ndma[q]
                    for s_ in range(em.NDS):
                        uses = (n - s_ + em.NDS - 1) // em.NDS if n > s_ else 0
                        sk = ("dma", q, s_)
                        if uses > 0 and em.waited[e].get(sk, 0) < 16 * uses:
                            eng.wait_ge(em.dsem[q][s_], 16 * uses); em.waited[e][sk] = 16 * uses

        PS = es.enter_context(nc.psum_tensor("PS", [128, 8, 512], F32))
        ident = es.enter_context(nc.sbuf_tensor("ident_s", [128, 128], F32))
        identb = es.enter_context(nc.sbuf_tensor("identb_s", [128, 128], BF16))
        MODC = es.enter_context(nc.sbuf_tensor("MODC", [128, 2, 2, 2, 2, 8], F32))
        mhalf = es.enter_context(nc.sbuf_tensor("mhalf", [128, 8], F32))
        em.dma("sp", ident[:], ident_d, W=["ident"])
        em.dma("sp", identb[:], identb_d, W=["identb"])
        em.op("dve", lambda: nc.vector.memset(mhalf[:], -0.5), W=["mhalf"])

        V = nc.vector; A = nc.scalar; G = nc.gpsimd; T = nc.tensor

        def pbank(b):
            return ("ps", b)

        def rstd_from_ssq(ssq, rs, n, tag):
            em.op("dve", lambda: V.tensor_scalar(out=rs[:, 0:n], in0=ssq[:, 0:n], scalar1=1.0 / D, scalar2=EPS,
                                                 op0=ALU.mult, op1=ALU.add), R=[tag + "ssq"], W=[tag + "rs"])
            em.op("pool", lambda: G.tensor_tensor(out=rs[:, 0:n], in0=rs[:, 0:n], in1=mhalf[:, 0:n], op=ALU.pow),
                  R=[tag + "rs", "mhalf"], W=[tag + "rs"])

        def prologue(xt, xkey, nsub, xs, xskey, Acol, Bcol, hT, hkey, ssq, rs, junk, tag, tb):
            for s in range(nsub):
                em.op("act", lambda s=s: A.activation(out=junk[:, :], in_=xt[:, s, :], func=AF.Square,
                                                      accum_out=ssq[:, s:s + 1]),
                      R=[xkey], W=[tag + "junk", tag + "ssq"])
            rstd_from_ssq(ssq, rs, nsub, tag)
            for s in range(nsub):
                em.op("act", lambda s=s: A.activation(out=xs[:, s, :], in_=xt[:, s, :], func=AF.Copy,
                                                      scale=rs[:, s:s + 1]),
                      R=[xkey, tag + "rs"], W=[xskey])
            for k in range(8):
                b = tb[k % len(tb)]
                for s in range(nsub):
                    em.op("pe", lambda s=s, k=k, b=b: T.transpose(PS[:, b, s * 128:(s + 1) * 128],
                                                                  xs[:, s, k * 128:(k + 1) * 128], ident[:]),
                          R=[xskey, "ident"], W=[pbank(b)], inc=(s == nsub - 1))
                em.op("act", lambda k=k, b=b: A.activation(out=hT[:, k, 0:nsub * 128], in_=PS[:, b, 0:nsub * 128],
                                                           func=AF.Identity, scale=Acol[:, k:k + 1],
                                                           bias=Bcol[:, k:k + 1]),
                      R=[pbank(b), "MODC"], W=[hkey])

        def epilogue(psb, xsub, xkey, Gb, gkey, ssq, rs, junk, tt, tag):
            yv = PS[:, psb:psb + 2, :]
            em.op("act", lambda: A.activation(out=junk[:, :], in_=yv, func=AF.Square, accum_out=ssq[:, 0:1]),
                  R=[pbank(psb), pbank(psb + 1)], W=[tag + "junk", tag + "ssq"])
            rstd_from_ssq(ssq, rs, 1, tag)
            em.op("dve", lambda: V.tensor_tensor(out=tt[:, :], in0=yv, in1=Gb, op=ALU.mult),
                  R=[pbank(psb), pbank(psb + 1), gkey], W=[tag + "tt"])
            em.op("dve", lambda: V.scalar_tensor_tensor(out=xsub, in0=tt[:, :], scalar=rs[:, 0:1], in1=xsub,
                                                        op0=ALU.mult, op1=ALU.add),
                  R=[tag + "tt", tag + "rs", xkey], W=[xkey])

        with ExitStack() as pes:
            sb = lambda n, s, d=F32: pes.enter_context(nc.sbuf_tensor(n, list(s), d))
            cc = sb("cc", [128, 16]); scc = sb("scc", [128, 16]); rhs2 = sb("rhs2", [128, 8, 2], BF16)
            bmc = sb("bmc", [128, 2, 48]); bm1 = sb("bm1", [128, 2, 48]); gco = sb("gco", [128, 2, 2, 8])
            bmrow = sb("bmrow", [2, 2, 2, D]); grow = sb("grow", [2, 2, 2, D]); grt = sb("grt", [2, 2, 2, D])
            wm = [sb(f"wm{i}", [128, 8, 512], BF16) for i in range(3)]
            em.dma("sp", cc[:], ccols_d, W=["cc"])
            em.dma("sp", bmc[:], bmodc_d, W=["bmc"])
            em.dma("sp", gco[:], gcols_d, W=["gco"])
            for l in range(2):
                for w_, (src, off) in enumerate([(bmod_d, 2 * D), (bmod_d, 5 * D)]):
                    em.dma("sp", bmrow[:, l, w_, :], src[l, off:off + D].partition_broadcast(2), W=["bmrow"])
                em.dma("sp", grow[:, l, 0, :], gpm_d[l, :].partition_broadcast(2), W=["grow"])
                em.dma("sp", grow[:, l, 1, :], gpf_d[l, :].partition_broadcast(2), W=["grow"])
            em.op("act", lambda: A.activation(out=scc[:], in_=cc[:], func=AF.Silu), R=["cc"], W=["scc"])
            em.op("dve", lambda: V.tensor_copy(out=rhs2[:, :, 0], in_=scc[:, 0:8]), R=["scc"], W=["rhs2"])
            em.op("dve", lambda: V.tensor_copy(out=rhs2[:, :, 1], in_=scc[:, 8:16]), R=["scc"], W=["rhs2"])
            em.op("dve", lambda: V.tensor_scalar(out=bm1[:], in0=bmc[:], scalar1=1.0, scalar2=None, op0=ALU.add),
                  R=["bmc"], W=["bm1"])
            it = 0
            for l in range(2):
                wsrc = wmod_d[l].rearrange("(k p) n -> p k n", p=128)
                for nb in range(12):
                    slot = it % 3; it += 1
                    wt = wm[slot]; wk = f"wm{slot}"
                    em.dma("pool", wt[:], wsrc[:, :, nb * 512:(nb + 1) * 512], W=[wk])
                    v = nb // 2; half = nb % 2
                    b = it % 4
                    if v in (2, 5):
                        w_ = 0 if v == 2 else 1
                        for k in range(8):
                            em.op("pe", lambda k=k, b=b, wt=wt: T.matmul(PS[0:2, b, :], lhsT=rhs2[:, k, :], rhs=wt[:, k, :],
                                                                        start=(k == 0), stop=(k == 7)),
                                  R=["rhs2", wk], W=[pbank(b)], inc=(k == 7))
                        dst = grt[:, l, w_, half * 512:(half + 1) * 512]
                        em.op("dve", lambda b=b, dst=dst, l=l, w_=w_, half=half: V.tensor_tensor(
                            out=dst, in0=PS[0:2, b, :], in1=bmrow[:, l, w_, half * 512:(half + 1) * 512], op=ALU.add),
                              R=[pbank(b), "bmrow"], W=["grt"])
                        em.op("dve", lambda dst=dst, l=l, w_=w_, half=half: V.tensor_tensor(
                            out=dst, in0=dst, in1=grow[:, l, w_, half * 512:(half + 1) * 512], op=ALU.mult),
                              R=["grt", "grow"], W=["grt"])
                        if half == 1:
                            em.dma("sp", GROW[l, w_, :, :], grt[:, l, w_, :], R=["grt"], W=[("GROW", l, w_)])
                    else:
                        sub = 0 if v < 2 else 1
                        isA = v in (1, 4)
                        for m in range(4):
                            ch = half * 4 + m
                            for k in range(8):
                                em.op("pe", lambda k=k, b=b, m=m, wt=wt: T.matmul(
                                    PS[:, b, 2 * m:2 * m + 2], lhsT=wt[:, k, m * 128:(m + 1) * 128], rhs=rhs2[:, k, :],
                                    start=(k == 0), stop=(k == 7)), R=["rhs2", wk], W=[pbank(b)], inc=(k == 7))
                            dst = MODC[:, l, sub, 0 if isA else 1, :, ch]
                            if isA:
                                em.op("dve", lambda b=b, m=m, dst=dst, l=l, v=v, ch=ch, sub=sub: V.tensor_scalar(
                                    out=dst, in0=PS[:, b, 2 * m:2 * m + 2], scalar1=bm1[:, l, v * 8 + ch:v * 8 + ch + 1],
                                    scalar2=gco[:, l, sub, ch:ch + 1], op0=ALU.add, op1=ALU.mult),
                                      R=[pbank(b), "bm1", "gco"], W=["MODC"])
                            else:
                                em.op("dve", lambda b=b, m=m, dst=dst, l=l, v=v, ch=ch: V.tensor_scalar(
                                    out=dst, in0=PS[:, b, 2 * m:2 * m + 2], scalar1=bmc[:, l, v * 8 + ch:v * 8 + ch + 1],
                                    scalar2=None, op0=ALU.add), R=[pbank(b), "bmc"], W=["MODC"])
            if debug:
                MODCd = nc.dram_tensor("MODCd", [128, 128], F32, kind="ExternalOutput").ap()
                em.dma("sp", MODCd, MODC[:].rearrange("p a b c d e -> p (a b c d e)"), R=["MODC"], W=["MODCd"])
            barrier()
        if stop == "0":
            em.finish()
            return nc
        C = type("C", (), {})()
        C.__dict__.update(locals())
        for name, fn in PHASES:
            fn(C)
            barrier()
            if stop == name:
                if hasattr(C, 'es_mix'):
                    C.es_mix.close()
                break
        em.finish()
    return nc


PHASES = []


def _host_inputs(inp, b):
    global _TABLES
    if _TABLES is None:
        _TABLES = _tables()
    f = lambda a: np.ascontiguousarray(np.asarray(a, dtype=np.float32))
    m = {}
    b, h = b // 2, b % 2
    m["x"] = f(inp["x"][b]); m["ctx"] = f(inp["ctx"][b])
    m["xloc"] = f(inp["x"][b][LBASE * h:LBASE * h + NL])
    m["hsel"] = np.tile(np.array([[1.0 - h, float(h)]], np.float32), (128, 1))
    m["ccols"] = np.concatenate([_col(inp["c"][b], 8), _col(inp["c_ctx"], 8)], axis=1)
    m["w_mod"] = f(inp["w_mod"]); m["b_mod"] = f(inp["b_mod"])
    m["bmodc"] = np.ascontiguousarray(np.stack([_col(inp["b_mod"][l], 48) for l in range(2)], axis=1))
    m["gcols"] = np.ascontiguousarray(np.stack(
        [np.stack([_col(inp["g_pre_mix"][l], 8), _col(inp["g_pre_ffn"][l], 8)], axis=1) for l in range(2)], axis=1))
    m["g_post_mix"] = f(inp["g_post_mix"]); m["g_post_ffn"] = f(inp["g_post_ffn"])
    m["w_in_ab"] = f(inp["w_in_ab"][0]); m["w_out_ab"] = f(inp["w_out_ab"][0])
    m["w_ffn_gate"] = f(inp["w_ffn_gate"]); m["w_ffn_up"] = f(inp["w_ffn_up"]); m["w_ffn_down"] = f(inp["w_ffn_down"])
    m["w_qkv_na"] = f(inp["w_qkv_na"][0]); m["w_out_na"] = f(inp["w_out_na"][0])
    wbd = np.zeros((2, 2, 4, 128, 128), np.float32)
    for gi, key in enumerate(["lru_w_a", "lru_w_i"]):
        w = np.asarray(inp[key][0], np.float32)
        for d in range(2):
            for c in range(4):
                wbd[gi, d, c, 0:64, 0:64] = w[d, 2 * c]
                wbd[gi, d, c, 64:128, 64:128] = w[d, 2 * c + 1]
    m["wbd"] = wbd
    lc = np.zeros((128, 4, 2, 3), np.float32)
    for d in range(2):
        lc[:, :, d, 0] = _col(inp["lru_b_a"][0][d], 4)
        lc[:, :, d, 1] = _col(inp["lru_b_i"][0][d], 4)
        lc[:, :, d, 2] = _col(inp["lru_lam"][0][d], 4)
    m["lcols"] = lc
    cv = np.zeros((128, 4, 5), np.float32)
    for k in range(4):
        cv[:, :, k] = _col(inp["conv_w"][0][k], 4)
    cv[:, :, 4] = _col(inp["conv_b"][0], 4)
    m["convc"] = cv
    for k in ("ident", "identb", "T1", "M2", "CSf", "CSc", "T256"):
        m[k] = _TABLES[k]
    rpb = np.asarray(inp["rpb_na"][0], np.float32)
    m["BT"] = np.ascontiguousarray(_bias_tables(rpb)[2])
    m["BTE"] = _edge_tables(rpb, h)
    return m


_NC_CACHE = {}


def kernel(**inputs):
    if "full" not in _NC_CACHE:
        _NC_CACHE["full"] = build()
    nc = _NC_CACHE["full"]
    in_maps = [_host_inputs(inputs, b) for b in range(NCORES)]
    res = run_bass_kernel_spmd(nc, in_maps, core_ids=list(range(NCORES)))
    out = np.empty((4, S, D), np.float32)
    for c in range(NCORES):
        b, h = c // 2, c % 2
        o = np.asarray(res.results[c]["out"], dtype=np.float32)
        out[b, 4096 * h:4096 * (h + 1)] = o[512 * h:512 * h + 4096]
    return out


def _wload(C, dst, src_view, key, nk, ncols, step=512):
    for c0 in range(0, ncols, step):
        c1 = min(ncols, c0 + step)
        C.em.dma("pool", dst[:, :, c0:c1], src_view[:, :, c0:c1], W=[key])


def phase_A(C):
    nc, em, PS = C.nc, C.em, C.PS
    V, A, G, T = nc.vector, nc.scalar, nc.gpsimd, nc.tensor
    pes = C.es_mix = ExitStack()
    C.F = pes.enter_context(nc.sbuf_tensor("Fbuf", [128, 64, 512], BF16))
    C.fTc = pes.enter_context(nc.sbuf_tensor("fTc", [128, 4, LC], BF16))
    F, fTc = C.F, C.fTc
    with ExitStack() as ps_:
        sb = lambda n, s, d=F32: ps_.enter_context(nc.sbuf_tensor(n, list(s), d))
        win = sb("win", [128, 8, 1536], BF16)
        xt = [sb(f"Axt{i}", [128, 4, D]) for i in range(2)]
        hT = [sb(f"AhT{i}", [128, 8, 512], BF16) for i in range(2)]
        ugst = [sb(f"Aug{i}", [128, 8, 512]) for i in range(2)]
        ssq = sb("Assq", [128, 4]); rs = sb("Ars", [128, 4]); junk = sb("Ajunk", [128, D], BF16)
        _wload(C, win, C.win_d.rearrange("(k p) n -> p k n", p=128), "win", 8, 1536)
        xsrc = C.x_d.rearrange("(t1 t2) d -> t1 t2 d", t2=64)
        Acol = C.MODC[:, 0, 0, 0, 0, :]; Bcol = C.MODC[:, 0, 0, 1, 0, :]
        AcolC = C.MODC[:, 0, 0, 0, 1, :]; BcolC = C.MODC[:, 0, 0, 1, 1, :]
        em.dma("sp", xt[0][:], xsrc[:, 0:4, :], W=["Axt0"])
        ev = 0
        for j in range(17):
            sl = j % 2
            isctx = j == 16
            nsub = 2 if isctx else 4
            n = nsub * 128
            if j + 1 < 16:
                em.dma("sp", xt[1 - sl][:], xsrc[:, 4 * (j + 1):4 * (j + 2), :], W=[f"Axt{1 - sl}"])
            elif j + 1 == 16:
                em.dma("sp", xt[1 - sl][:, 0:2, :], C.ctx_d.rearrange("(s p) d -> p s d", p=128), W=[f"Axt{1 - sl}"])
            C.prologue(xt[sl], f"Axt{sl}", nsub, xt[sl], f"Axt{sl}", AcolC if isctx else Acol, BcolC if isctx else Bcol,
                       hT[sl], f"AhT{sl}", ssq, rs, junk, "A", [0, 1])
            for oc in range(8):
                b = 2 + oc % 3
                for k in range(8):
                    em.op("pe", lambda k=k, b=b, oc=oc, sl=sl, n=n: T.matmul(
                        PS[:, b, 0:n], lhsT=win[:, k, oc * 128:(oc + 1) * 128], rhs=hT[sl][:, k, 0:n],
                        start=(k == 0), stop=(k == 7)), R=["win", f"AhT{sl}"], W=[("ps", b)], inc=(k == 7))
                ev += 1
                if ev % 2:
                    em.op("dve", lambda b=b, oc=oc, sl=sl, n=n: V.tensor_copy(out=ugst[sl][:, oc, 0:n], in_=PS[:, b, 0:n]),
                          R=[("ps", b)], W=[f"Aug{sl}"])
                else:
                    em.op("act", lambda b=b, oc=oc, sl=sl, n=n: A.copy(out=ugst[sl][:, oc, 0:n], in_=PS[:, b, 0:n]),
                          R=[("ps", b)], W=[f"Aug{sl}"])
            if isctx:
                em.dma("pool", C.UGc.rearrange("c p n -> p c n"), ugst[sl][:, :, 0:LC], R=[f"Aug{sl}"], W=["UGc"])
                for fc in range(4):
                    b = 5 + fc % 3
                    for k in range(8):
                        em.op("pe", lambda k=k, b=b, fc=fc, sl=sl: T.matmul(
                            PS[:, b, 0:LC], lhsT=win[:, k, 1024 + fc * 128:1024 + (fc + 1) * 128], rhs=hT[sl][:, k, 0:LC],
                            start=(k == 0), stop=(k == 7)), R=["win", f"AhT{sl}"], W=[("ps", b)], inc=(k == 7))
                    em.op("dve", lambda b=b, fc=fc: V.tensor_copy(out=fTc[:, fc, :], in_=PS[:, b, 0:LC]),
                          R=[("ps", b)], W=["fTc"])
            else:
                em.dma("pool", C.UG[:, :, j * 512:(j + 1) * 512].rearrange("c p n -> p c n"), ugst[sl][:],
                       R=[f"Aug{sl}"], W=[("UG", j)])
                for s in range(4):
                    b = 5 + s % 3
                    for k in range(8):
                        em.op("pe", lambda k=k, b=b, s=s, sl=sl: T.matmul(
                            PS[:, b, :], lhsT=hT[sl][:, k, s * 128:(s + 1) * 128], rhs=win[:, k, 1024:1536],
                            start=(k == 0), stop=(k == 7)), R=["win", f"AhT{sl}"], W=[("ps", b)], inc=(k == 7))
                    ev += 1
                    if ev % 2:
                        em.op("dve", lambda b=b, s=s, j=j: V.tensor_copy(out=F[:, 4 * j + s, :], in_=PS[:, b, :]),
                              R=[("ps", b)], W=["F"])
                    else:
                        em.op("act", lambda b=b, s=s, j=j: A.copy(out=F[:, 4 * j + s, :], in_=PS[:, b, :]),
                              R=[("ps", b)], W=["F"])


def phase_C(C):
    nc, em, PS, F, fTc = C.nc, C.em, C.PS, C.F, C.fTc
    V, A, G, T = nc.vector, nc.scalar, nc.gpsimd, nc.tensor
    with ExitStack() as ps_:
        sb = lambda n, s, d=F32: ps_.enter_context(nc.sbuf_tensor(n, list(s), d))
        T1 = sb("T1s", [128, 256], BF16); M2 = sb("M2s", [64, 128, 192], BF16); CSf = sb("CSfs", [128, 256], BF16)
        CSc = sb("CScs", [128, 256], BF16); T256 = sb("T256s", [128, 2, 512], BF16)
        Ast = sb("Ast", [64, 64, 256], BF16); Y = sb("Ybuf", [128, 2, S], BF16)
        fst = [sb(f"fst{i}", [128, 2048], BF16) for i in range(2)]
        Gc = sb("Gcb", [128, 2, 256], BF16); fcs = sb("fcs", [128, LC], BF16)
        for dst, src, k in ((T1, C.T1_d, "T1"), (M2, C.M2_d, "M2"), (CSf, C.CSf_d, "CSf"), (CSc, C.CSc_d, "CSc"),
                            (T256, C.T256_d, "T256")):
            em.dma("sp", dst[:], src, W=[k])
        ev = 0
        for cc in range(4):
            for hc in range(2):
                for g4 in range(16):
                    b = 2 * (g4 % 2)
                    for q in range(4):
                        ch = cc * 128 + hc * 64 + g4 * 4 + q
                        em.op("pe", lambda b=b, q=q, ch=ch: T.matmul(
                            PS[0:64, b + q // 2, (q % 2) * 256:(q % 2) * 256 + 256], lhsT=F[:, :, ch], rhs=T1[:, :],
                            start=True, stop=True), R=["F", "T1"], W=[("ps", b), ("ps", b + 1)], inc=(q == 3))
                    ev += 1
                    dst = Ast[:, g4 * 4:(g4 + 1) * 4, :].rearrange("p a b -> p (a b)")
                    src = PS[0:64, b:b + 2, :].rearrange("p a b -> p (a b)")
                    if ev % 2:
                        em.op("dve", lambda dst=dst, src=src: V.tensor_copy(out=dst, in_=src),
                              R=[("ps", b), ("ps", b + 1)], W=["Ast"])
                    else:
                        em.op("act", lambda dst=dst, src=src: A.copy(out=dst, in_=src),
                              R=[("ps", b), ("ps", b + 1)], W=["Ast"])
                for kb in range(32):
                    b = 4 + kb % 2
                    for q in range(4):
                        k1 = kb * 4 + q
                        o = PS[hc * 64:(hc + 1) * 64, b, q * 128:(q + 1) * 128]
                        em.op("pe", lambda o=o, k1=k1: T.matmul(o, lhsT=Ast[:, :, k1], rhs=M2[:, k1, 64:192],
                                                                start=True, stop=False),
                              R=["Ast", "M2"], W=[("ps", b)], inc=False)
                        em.op("pe", lambda o=o, k1=k1: T.matmul(o, lhsT=Ast[:, :, 128 + k1], rhs=M2[:, k1, 0:128],
                                                                start=False, stop=True),
                              R=["Ast", "M2"], W=[("ps", b)], inc=(q == 3))
                    ev += 1
                    src = PS[hc * 64:(hc + 1) * 64, b, :].rearrange("p (k r c) -> p r c k", k=4, r=2)
                    dst = Y[hc * 64:(hc + 1) * 64, :, :].rearrange("p r (c k) -> p r c k", k=128)[:, :, :, kb * 4:(kb + 1) * 4]
                    if ev % 2:
                        em.op("dve", lambda dst=dst, src=src: V.tensor_copy(out=dst, in_=src), R=[("ps", b)], W=["Y"])
                    else:
                        em.op("act", lambda dst=dst, src=src: A.copy(out=dst, in_=src), R=[("ps", b)], W=["Y"])
            for tl in range(16):
                b = 6 + tl % 2
                em.op("pe", lambda b=b, tl=tl: T.matmul(PS[:, b, :], lhsT=CSf[:, 0:128], rhs=Y[:, 0, tl * 512:(tl + 1) * 512],
                                                        start=True, stop=False), R=["CSf", "Y"], W=[("ps", b)], inc=False)
                em.op("pe", lambda b=b, tl=tl: T.matmul(PS[:, b, :], lhsT=CSf[:, 128:256], rhs=Y[:, 1, tl * 512:(tl + 1) * 512],
                                                        start=False, stop=True), R=["CSf", "Y"], W=[("ps", b)], inc=True)
                fs = (tl // 4) % 2
                em.op("act", lambda b=b, tl=tl, fs=fs: A.copy(out=fst[fs][:, (tl % 4) * 512:(tl % 4 + 1) * 512], in_=PS[:, b, :]),
                      R=[("ps", b)], W=[f"fst{fs}"])
                if tl % 4 == 3:
                    t0 = (tl // 4) * 2048
                    em.dma("pool", C.MIXT[512 + cc * 128:512 + (cc + 1) * 128, t0:t0 + 2048], fst[fs][:],
                           R=[f"fst{fs}"], W=[("MIXT", 4 + cc)])
            for tc in range(2):
                em.op("pe", lambda tc=tc, cc=cc: T.matmul(PS[:, 0, 0:256], lhsT=fTc[:, cc, tc * 128:(tc + 1) * 128], rhs=CSc[:, :],
                                                          start=True, stop=True), R=["fTc", "CSc"], W=[("ps", 0)])
                em.op("dve", lambda tc=tc: V.tensor_copy(out=Gc[:, tc, :], in_=PS[:, 0, 0:256]), R=[("ps", 0)], W=["Gc"])
            for i_, (tc, part) in enumerate([(0, 0), (0, 1), (1, 0), (1, 1)]):
                em.op("pe", lambda i_=i_, tc=tc, part=part: T.matmul(
                    PS[:, 1, 0:256], lhsT=Gc[:, tc, part * 128:(part + 1) * 128], rhs=T256[:, tc, part * 256:(part + 1) * 256],
                    start=(i_ == 0), stop=(i_ == 3)), R=["Gc", "T256"], W=[("ps", 1)], inc=(i_ == 3))
            em.op("dve", lambda: V.tensor_copy(out=fcs[:, :], in_=PS[:, 1, 0:256]), R=[("ps", 1)], W=["fcs"])
            em.dma("pool", C.MIXT[512 + cc * 128:512 + (cc + 1) * 128, S:NT], fcs[:], R=["fcs"], W=[("MIXTc", 4 + cc)])
    C.es_mix.close()


PHASES += [("A", phase_A), ("C", phase_C)]


def phase_B(C):
    nc, em, PS = C.nc, C.em, C.PS
    V, A, G, T = nc.vector, nc.scalar, nc.gpsimd, nc.tensor
    TW = 2048
    with ExitStack() as ps_:
        sb = lambda n, s, d=F32: ps_.enter_context(nc.sbuf_tensor(n, list(s), d))
        bufA = sb("bufA", [128, NT]); bufB = sb("bufB", [128, 8460]); uc = sb("ucb_", [128, NT]); ucb = sb("ucbb", [128, NT], BF16)
        rt = sb("rt", [128, TW]); it_ = sb("it", [128, TW]); at = sb("at", [128, TW]); st = sb("st", [128, TW])
        wbd = sb("wbds", [128, 16, 128], BF16)
        lco = sb("lco", [128, 4, 2, 3]); cvc = sb("cvc", [128, 4, 5]); cA = sb("cA", [128, 4, 2]); carry = sb("carry", [128, 2])
        em.dma("pool", wbd[:], C.wbd_d.rearrange("g d c p n -> p (g d c) n"), W=["wbd"])
        em.dma("sp", lco[:], C.lcols_d, W=["lco"])
        em.dma("sp", cvc[:], C.convc_d, W=["cvc"])
        em.op("act", lambda: A.activation(out=cA[:], in_=lco[:, :, :, 2], func=AF.Exp, scale=-1.0), R=["lco"], W=["cA"])
        em.op("act", lambda: A.activation(out=cA[:], in_=cA[:], func=AF.Ln, bias=1.0), R=["cA"], W=["cA"])
        em.op("dve", lambda: V.tensor_scalar(out=cA[:], in0=cA[:], scalar1=-8.0, scalar2=None, op0=ALU.mult), R=["cA"], W=["cA"])
        em.op("dve", lambda: V.memset(bufB[:], 0.0), W=["bufB"])
        XO = 8200
        tiles = [(S, NT)] + [(i * TW, (i + 1) * TW) for i in range(4)]
        for c in range(4):
            em.dma("sp", bufA[:, 0:S], C.UG[c], R=[("UG", j) for j in range(16)], W=["bufA"])
            em.dma("sp", bufA[:, S:NT], C.UGc[c], R=["UGc"], W=["bufA"])
            em.op("pool", lambda: G.tensor_copy(out=bufB[:, 2:2 + S].rearrange("p (a b) -> p a b", b=64),
                                                in_=bufA[:, 0:S].rearrange("p (b a) -> p a b", a=128)),
                  R=["bufA"], W=["bufB"])
            em.op("pool", lambda: G.tensor_copy(out=bufB[:, XO + 2:XO + 2 + LC], in_=bufA[:, S:NT]), R=["bufA"], W=["bufB"])
            em.dma("sp", bufA[:, 0:S], C.UG[4 + c], R=[("UG", j) for j in range(16)], W=["bufA"])
            em.dma("sp", bufA[:, S:NT], C.UGc[4 + c], R=["UGc"], W=["bufA"])
            for (o0, src0, n) in ((0, 0, S), (S, XO, LC)):
                em.op("act", lambda o0=o0, src0=src0, n=n, c=c: A.activation(
                    out=uc[:, o0:o0 + n], in_=bufB[:, src0:src0 + n], func=AF.Identity,
                    scale=cvc[:, c, 0:1], bias=cvc[:, c, 4:5]), R=["bufB", "cvc"], W=["uc"])
                for k in range(1, 4):
                    em.op("dve", lambda o0=o0, src0=src0, n=n, c=c, k=k: V.scalar_tensor_tensor(
                        out=uc[:, o0:o0 + n], in0=bufB[:, src0 + k:src0 + k + n], scalar=cvc[:, c, k:k + 1],
                        in1=uc[:, o0:o0 + n], op0=ALU.mult, op1=ALU.add), R=["bufB", "cvc", "uc"], W=["uc"])
            em.op("pool", lambda: G.tensor_copy(out=ucb[:, :], in_=uc[:, :]), R=["uc"], W=["ucb"])
            for (lo, hi) in tiles:
                n = hi - lo
                em.op("act", lambda lo=lo, hi=hi, n=n: A.activation(out=rt[:, 0:n], in_=bufA[:, lo:hi], func=AF.Square),
                      R=["bufA"], W=["rt"])
                em.op("dve", lambda n=n: V.tensor_scalar(out=rt[:, 0:n], in0=rt[:, 0:n], scalar1=0.044715, scalar2=1.0,
                                                         op0=ALU.mult, op1=ALU.add), R=["rt"], W=["rt"])
                em.op("pool", lambda lo=lo, hi=hi, n=n: G.tensor_tensor(out=rt[:, 0:n], in0=rt[:, 0:n], in1=bufA[:, lo:hi],
                                                                        op=ALU.mult), R=["rt", "bufA"], W=["rt"])
                em.op("act", lambda n=n: A.activation(out=it_[:, 0:n], in_=rt[:, 0:n], func=AF.Sigmoid, scale=1.5957691216057308),
                      R=["rt"], W=["it"])
                em.op("pool", lambda lo=lo, hi=hi, n=n: G.tensor_tensor(out=bufA[:, lo:hi], in0=it_[:, 0:n], in1=bufA[:, lo:hi],
                                                                        op=ALU.mult), R=["it", "bufA"], W=["bufA"])
            for d in range(2):
                order = [tiles[0]] + (tiles[1:] if d == 0 else tiles[1:][::-1])
                for ti, (lo, hi) in enumerate(order):
                    n = hi - lo
                    nq = (n + 511) // 512
                    for gi in range(2):
                        for q in range(nq):
                            w_ = min(512, n - q * 512)
                            em.op("pe", lambda gi=gi, q=q, w_=w_, lo=lo, d=d, c=c: T.matmul(
                                PS[:, gi * 4 + q, 0:w_], lhsT=wbd[:, gi * 8 + d * 4 + c, :], rhs=ucb[:, lo + q * 512:lo + q * 512 + w_],
                                start=True, stop=True), R=["wbd", "ucb"], W=[("ps", gi * 4 + q)], inc=(q == nq - 1))
                    pr = PS[:, 0:4, :].rearrange("p a b -> p (a b)")[:, 0:n]
                    pi = PS[:, 4:8, :].rearrange("p a b -> p (a b)")[:, 0:n]
                    em.op("act", lambda pr=pr, n=n, c=c, d=d: A.activation(out=rt[:, 0:n], in_=pr, func=AF.Sigmoid,
                                                                           bias=lco[:, c, d, 0:1]),
                          R=[("ps", q) for q in range(4)] + ["lco"], W=["rt"])
                    em.op("act", lambda pi=pi, n=n, c=c, d=d: A.activation(out=it_[:, 0:n], in_=pi, func=AF.Sigmoid,
                                                                           bias=lco[:, c, d, 1:2]),
                          R=[("ps", 4 + q) for q in range(4)] + ["lco"], W=["it"])
                    em.op("act", lambda n=n, c=c, d=d: A.activation(out=at[:, 0:n], in_=rt[:, 0:n], func=AF.Exp,
                                                                    scale=cA[:, c, d:d + 1]), R=["rt", "cA"], W=["at"])
                    em.op("act", lambda n=n: A.activation(out=st[:, 0:n], in_=at[:, 0:n], func=AF.Square), R=["at"], W=["st"])
                    em.op("act", lambda n=n: A.activation(out=st[:, 0:n], in_=st[:, 0:n], func=AF.Sqrt, scale=-1.0, bias=1.0),
                          R=["st"], W=["st"])
                    em.op("pool", lambda n=n: G.tensor_tensor(out=it_[:, 0:n], in0=it_[:, 0:n], in1=st[:, 0:n], op=ALU.mult),
                          R=["it", "st"], W=["it"])
                    em.op("pool", lambda n=n, lo=lo, hi=hi: G.tensor_tensor(out=it_[:, 0:n], in0=it_[:, 0:n], in1=uc[:, lo:hi],
                                                                            op=ALU.mult), R=["it", "uc"], W=["it"])
                    init = 0.0 if ti == 0 else carry[:, d:d + 1]
                    if d == 0:
                        em.op("dve", lambda n=n, lo=lo, hi=hi, init=init: V.tensor_tensor_scan(
                            out=bufB[:, lo:hi], data0=at[:, 0:n], data1=it_[:, 0:n], initial=init, op0=ALU.mult, op1=ALU.add),
                              R=["at", "it", "carry", "bufB", "uc"], W=["bufB"])
                        em.op("dve", lambda hi=hi: V.tensor_copy(out=carry[:, 0:1], in_=bufB[:, hi - 1:hi]), R=["bufB"], W=["carry"])
                    else:
                        em.op("dve", lambda n=n, init=init: V.tensor_tensor_scan(
                            out=rt[:, n - 1::-1] if False else rt[:, 0:n][:, ::-1], data0=at[:, 0:n][:, ::-1],
                            data1=it_[:, 0:n][:, ::-1], initial=init, op0=ALU.mult, op1=ALU.add),
                              R=["at", "it", "carry", "rt"], W=["rt"])
                        em.op("dve", lambda: V.tensor_copy(out=carry[:, 1:2], in_=rt[:, 0:1]), R=["rt"], W=["carry"])
                        em.op("pool", lambda n=n, lo=lo, hi=hi: G.tensor_tensor(out=bufB[:, lo:hi], in0=bufB[:, lo:hi], in1=rt[:, 0:n],
                                                                                op=ALU.add), R=["bufB", "rt"], W=["bufB"])
            em.op("dve", lambda: V.tensor_tensor(out=ucb[:, 0:S].rearrange("p (a b) -> p a b", b=64),
                                                 in0=bufB[:, 0:S].rearrange("p (a b) -> p a b", b=64),
                                                 in1=bufA[:, 0:S].rearrange("p (b a) -> p a b", a=128), op=ALU.mult),
                  R=["bufB", "bufA", "ucb"], W=["ucb"])
            em.op("dve", lambda: V.tensor_tensor(out=ucb[:, S:NT], in0=bufB[:, S:NT], in1=bufA[:, S:NT], op=ALU.mult),
                  R=["bufB", "bufA", "ucb"], W=["ucb"])
            em.dma("pool", C.MIXT[c * 128:(c + 1) * 128, :], ucb[:, :], R=["ucb"], W=[("MIXT", c)])
            if c < 3:
                em.op("dve", lambda: V.memset(bufB[:, 0:2], 0.0), R=["bufB"], W=["bufB"])
                em.op("dve", lambda: V.memset(bufB[:, S:8460], 0.0), R=["bufB"], W=["bufB"])


def _tok_tiles(ntok_tile):
    return None


def phase_D1(C, layer=0):
    nc, em, PS = C.nc, C.em, C.PS
    V, A, G, T = nc.vector, nc.scalar, nc.gpsimd, nc.tensor
    with ExitStack() as ps_:
        sb = lambda n, s, d=F32: ps_.enter_context(nc.sbuf_tensor(n, list(s), d))
        wo = sb("D1wo", [128, 8, D], BF16)
        xt = [sb(f"D1xt{i}", [128, 4, D]) for i in range(2)]
        mt = [sb(f"D1mt{i}", [128, 8, 512], BF16) for i in range(2)]
        mb = [sb(f"D1mb{i}", [128, 8, 512], BF16) for i in range(2)]
        hs = sb("D1hs", [128, 2])
        Gx = sb("D1Gx", [128, D]); Gc = sb("D1Gc", [128, D])
        ssq = sb("D1ssq", [128, 4]); rs = sb("D1rs", [128, 4]); junk = sb("D1junk", [128, D], BF16); tt = sb("D1tt", [128, D])
        _wload(C, wo, C.woab_d.rearrange("(k p) n -> p k n", p=128), "D1wo", 8, D)
        em.dma("sp", hs[:], C.hsel_d, W=["D1hs"])
        em.dma("sp", Gx[:], C.GROW[0, 0, 0, :].partition_broadcast(128), R=[("GROW", 0, 0)], W=["D1Gx"])
        em.dma("sp", Gc[:], C.GROW[0, 0, 1, :].partition_broadcast(128), R=[("GROW", 0, 0)], W=["D1Gc"])
        mixv = C.MIXT.rearrange("(k p) t -> p k t", p=128)
        NTL = NLT + 1

        def load(j, sl):
            if j < NLT:
                em.dma("sp", xt[sl][:], C.xloc_d[j * 512:(j + 1) * 512, :].rearrange("(s p) d -> p s d", p=128), W=[f"D1xt{sl}"])
                em.dma("sp", mt[sl][:], mixv[:, :, j * 512:(j + 1) * 512], W=[f"D1mt{sl}"])
                em.dma("sp", mb[sl][:], mixv[:, :, LBASE + j * 512:LBASE + (j + 1) * 512], W=[f"D1mb{sl}"])
                em.op("pool", lambda sl=sl: G.tensor_scalar(out=mt[sl][:], in0=mt[sl][:], scalar1=hs[:, 0:1], scalar2=None,
                                                            op0=ALU.mult), R=[f"D1mt{sl}", "D1hs"], W=[f"D1mt{sl}"])
                em.op("dve", lambda sl=sl: V.scalar_tensor_tensor(
                    out=mt[sl][:].rearrange("p a b -> p (a b)"), in0=mb[sl][:].rearrange("p a b -> p (a b)"), scalar=hs[:, 1:2],
                    in1=mt[sl][:].rearrange("p a b -> p (a b)"), op0=ALU.mult, op1=ALU.add),
                      R=[f"D1mt{sl}", f"D1mb{sl}", "D1hs"], W=[f"D1mt{sl}"])
            else:
                em.dma("sp", xt[sl][:, 0:2, :], C.ctx_d.rearrange("(s p) d -> p s d", p=128), W=[f"D1xt{sl}"])
                em.dma("sp", mt[sl][:, :, 0:LC], mixv[:, :, S:NT], W=[f"D1mt{sl}"])
        load(0, 0)
        for j in range(NTL):
            sl = j % 2
            if j + 1 < NTL:
                load(j + 1, 1 - sl)
            isx = j < NLT
            nsub = 4 if isx else 2
            for s in range(nsub):
                pb = 2 * (s % 4)
                for h in range(2):
                    for k in range(8):
                        em.op("pe", lambda k=k, h=h, s=s, sl=sl, pb=pb: T.matmul(
                            PS[:, pb + h, :], lhsT=mt[sl][:, k, s * 128:(s + 1) * 128], rhs=wo[:, k, h * 512:(h + 1) * 512],
                            start=(k == 0), stop=(k == 7)), R=["D1wo", f"D1mt{sl}"], W=[("ps", pb + h)], inc=(k == 7))
                C.epilogue(pb, xt[sl][:, s, :], f"D1xt{sl}", (Gx if isx else Gc)[:, :], "D1Gx" if isx else "D1Gc",
                           ssq, rs, junk, tt, "D1")
            if isx:
                em.dma("pool", C.X1[j * 512:(j + 1) * 512, :].rearrange("(s p) d -> p s d", p=128), xt[sl][:],
                       R=[f"D1xt{sl}"], W=[("X1", j)])
            else:
                em.dma("pool", C.C1.rearrange("(s p) d -> p s d", p=128), xt[sl][:, 0:2, :], R=[f"D1xt{sl}"], W=["C1"])


def phase_FFN(C, layer, Xin, Cin, Xout, Cout, inkey, outkey):
    nc, em, PS = C.nc, C.em, C.PS
    V, A, G, T = nc.vector, nc.scalar, nc.gpsimd, nc.tensor
    tg = f"F{layer}"
    with ExitStack() as ps_:
        sb = lambda n, s, d=F32: ps_.enter_context(nc.sbuf_tensor(tg + n, list(s), d))
        wg = sb("wg", [128, 8, FH], BF16); wu = sb("wu", [128, 8, FH], BF16); wd = sb("wd", [128, NJ, D], BF16)
        xt = [sb(f"xt{i}", [128, 2, D]) for i in range(2)]
        xs = sb("xs", [128, 2, D]); hT = sb("hT", [128, 8, 256], BF16); hh = sb("hh", [128, NJ, 256], BF16)
        sg = [sb(f"sg{i}", [128, 256]) for i in range(2)]
        Gx = sb("Gx", [128, D]); Gc = sb("Gc", [128, D])
        ssq = sb("ssq", [128, 4]); rs = sb("rs", [128, 4]); junk = sb("junk", [128, D], BF16); tt = sb("tt", [128, D])
        _wload(C, wg, C.wg_d[layer].rearrange("(k p) n -> p k n", p=128), tg + "wg", 8, FH, 704)
        _wload(C, wu, C.wu_d[layer].rearrange("(k p) n -> p k n", p=128), tg + "wu", 8, FH, 704)
        _wload(C, wd, C.wd_d[layer].rearrange("(k p) n -> p k n", p=128), tg + "wd", NJ, D, 512)
        em.dma("sp", Gx[:], C.GROW[layer, 1, 0, :].partition_broadcast(128), R=[("GROW", layer, 1)], W=[tg + "Gx"])
        em.dma("sp", Gc[:], C.GROW[layer, 1, 1, :].partition_broadcast(128), R=[("GROW", layer, 1)], W=[tg + "Gc"])
        NX = NL // 256
        ntile = NX + (1 if Cin is not None else 0)

        def load(j, sl):
            if j < NX:
                em.dma("sp", xt[sl][:], Xin[j * 256:(j + 1) * 256, :].rearrange("(s p) d -> p s d", p=128),
                       R=[(inkey, j // 2)], W=[tg + f"xt{sl}"])
            else:
                em.dma("sp", xt[sl][:], Cin.rearrange("(s p) d -> p s d", p=128), R=[inkey + "c"], W=[tg + f"xt{sl}"])
        load(0, 0)
        for j in range(ntile):
            sl = j % 2
            if j + 1 < ntile:
                load(j + 1, 1 - sl)
            path = 0 if j < NX else 1
            C.prologue(xt[sl], tg + f"xt{sl}", 2, xs, tg + "xs", C.MODC[:, layer, 1, 0, path, :], C.MODC[:, layer, 1, 1, path, :],
                       hT, tg + "hT", ssq, rs, junk, tg, [7])
            for jj in range(NJ):
                b = jj % 3
                for gi, w_ in enumerate((wg, wu)):
                    for k in range(8):
                        em.op("pe", lambda k=k, b=b, gi=gi, w_=w_, jj=jj: T.matmul(
                            PS[:, b, gi * 256:(gi + 1) * 256], lhsT=w_[:, k, jj * 128:(jj + 1) * 128], rhs=hT[:, k, :],
                            start=(k == 0), stop=(k == 7)), R=[tg + "wg", tg + "wu", tg + "hT"], W=[("ps", b)],
                              inc=(k == 7 and gi == 1))
                s2 = jj % 2
                em.op("act", lambda b=b, s2=s2: A.activation(out=sg[s2][:, :], in_=PS[:, b, 0:256], func=AF.Silu),
                      R=[("ps", b)], W=[tg + f"sg{s2}"])
                em.op("dve", lambda b=b, s2=s2, jj=jj: V.tensor_tensor(out=hh[:, jj, :], in0=sg[s2][:, :], in1=PS[:, b, 256:512],
                                                                      op=ALU.mult), R=[("ps", b), tg + f"sg{s2}"], W=[tg + "hh"])
            for s in range(2):
                pb = 3 + 2 * s
                for h in range(2):
                    for jj in range(NJ):
                        em.op("pe", lambda jj=jj, h=h, s=s, pb=pb: T.matmul(
                            PS[:, pb + h, :], lhsT=hh[:, jj, s * 128:(s + 1) * 128], rhs=wd[:, jj, h * 512:(h + 1) * 512],
                            start=(jj == 0), stop=(jj == NJ - 1)), R=[tg + "wd", tg + "hh"], W=[("ps", pb + h)], inc=(jj == NJ - 1))
                C.epilogue(pb, xt[sl][:, s, :], tg + f"xt{sl}", (Gx if path == 0 else Gc)[:, :], tg + ("Gx" if path == 0 else "Gc"),
                           ssq, rs, junk, tt, tg)
            if j < NX:
                em.dma("pool", Xout[j * 256:(j + 1) * 256, :].rearrange("(s p) d -> p s d", p=128), xt[sl][:],
                       R=[tg + f"xt{sl}"], W=[(outkey, j // 2)] if j % 2 else [(outkey + "h", j)])
            else:
                em.dma("pool", Cout.rearrange("(s p) d -> p s d", p=128), xt[sl][:], R=[tg + f"xt{sl}"], W=[outkey + "c"])


PHASES += [("B", phase_B), ("D1", phase_D1),
           ("D2", lambda C: phase_FFN(C, 0, C.X1, C.C1, C.X2, C.C2, "X1", "X2"))]


def phase_E(C):
    nc, em, PS = C.nc, C.em, C.PS
    V, A, G, T = nc.vector, nc.scalar, nc.gpsimd, nc.tensor
    with ExitStack() as ps_:
        sb = lambda n, s, d=F32: ps_.enter_context(nc.sbuf_tensor("E" + n, list(s), d))
        wq = sb("wq", [128, 8, 3 * D], BF16)
        xt = [sb(f"xt{i}", [128, 4, D]) for i in range(2)]
        hT = sb("hT", [128, 8, 512], BF16)
        qk = [sb(f"qk{i}", [128, 16, 512], BF16) for i in range(2)]
        vst = [sb(f"vst{i}", [128, 4, D], BF16) for i in range(2)]
        ssq = sb("ssq", [128, 4]); rs = sb("rs", [128, 4]); junk = sb("junk", [128, D], BF16)
        _wload(C, wq, C.wqkv_d.rearrange("(k p) n -> p k n", p=128), "Ewq", 8, 3 * D)
        QTv = C.QT.rearrange("(c p) t -> p c t", p=128); KTv = C.KT.rearrange("(c p) t -> p c t", p=128)

        def load(j, sl):
            if j < NLT:
                em.dma("sp", xt[sl][:], C.X2[j * 512:(j + 1) * 512, :].rearrange("(s p) d -> p s d", p=128), W=[f"Ext{sl}"])
            else:
                em.dma("sp", xt[sl][:, 0:2, :], C.C2.rearrange("(s p) d -> p s d", p=128), W=[f"Ext{sl}"])
        load(0, 0)
        ev = 0
        for j in range(NLT + 1):
            sl = j % 2
            if j + 1 < NLT + 1:
                load(j + 1, 1 - sl)
            isctx = j == NLT
            nsub = 2 if isctx else 4
            n = nsub * 128
            path = 1 if isctx else 0
            C.prologue(xt[sl], f"Ext{sl}", nsub, xt[sl], f"Ext{sl}", C.MODC[:, 1, 0, 0, path, :], C.MODC[:, 1, 0, 1, path, :],
                       hT, "EhT", ssq, rs, junk, "E", [0, 1])
            for oc in range(8 if isctx else 0, 16) if isctx else range(16):
                b = 2 + oc % 2
                for k in range(8):
                    em.op("pe", lambda k=k, b=b, oc=oc, n=n: T.matmul(
                        PS[:, b, 0:n], lhsT=wq[:, k, oc * 128:(oc + 1) * 128], rhs=hT[:, k, 0:n],
                        start=(k == 0), stop=(k == 7)), R=["Ewq", "EhT"], W=[("ps", b)], inc=(k == 7))
                if oc < 8:
                    em.op("act", lambda b=b, oc=oc, sl=sl, n=n: A.activation(out=qk[sl][:, oc, 0:n], in_=PS[:, b, 0:n],
                                                                             func=AF.Copy, scale=0.125),
                          R=[("ps", b)], W=[f"Eqk{sl}"])
                else:
                    em.op("dve", lambda b=b, oc=oc, sl=sl, n=n: V.tensor_copy(out=qk[sl][:, oc, 0:n], in_=PS[:, b, 0:n]),
                          R=[("ps", b)], W=[f"Eqk{sl}"])
            if not isctx:
                em.dma("pool", QTv[:, :, j * 512:(j + 1) * 512], qk[sl][:, 0:8, :], R=[f"Eqk{sl}"], W=[("QT", j)])
                em.dma("pool", KTv[:, :, j * 512:(j + 1) * 512], qk[sl][:, 8:16, :], R=[f"Eqk{sl}"], W=[("KT", j)])
            else:
                em.dma("pool", KTv[:, :, NL:NL + LC], qk[sl][:, 8:16, 0:LC], R=[f"Eqk{sl}"], W=[("KT", j)])
            for s in range(nsub):
                pb = 4 + 2 * (s % 2)
                for h in range(2):
                    for k in range(8):
                        em.op("pe", lambda k=k, h=h, s=s, pb=pb: T.matmul(
                            PS[:, pb + h, :], lhsT=hT[:, k, s * 128:(s + 1) * 128], rhs=wq[:, k, 2048 + h * 512:2048 + (h + 1) * 512],
                            start=(k == 0), stop=(k == 7)), R=["Ewq", "EhT"], W=[("ps", pb + h)], inc=(k == 7))
                ev += 1
                src = PS[:, pb:pb + 2, :].rearrange("p a b -> p (a b)")
                if ev % 2:
                    em.op("dve", lambda s=s, sl=sl, src=src: V.tensor_copy(out=vst[sl][:, s, :], in_=src),
                          R=[("ps", pb), ("ps", pb + 1)], W=[f"Evst{sl}"])
                else:
                    em.op("act", lambda s=s, sl=sl, src=src: A.copy(out=vst[sl][:, s, :], in_=src),
                          R=[("ps", pb), ("ps", pb + 1)], W=[f"Evst{sl}"])
            t0 = NL if isctx else j * 512
            em.dma("pool", C.VD[t0:t0 + n, :].rearrange("(s p) d -> p s d", p=128), vst[sl][:, 0:nsub, :],
                   R=[f"Evst{sl}"], W=[("VD", j)])


def phase_F(C):
    nc, em, PS = C.nc, C.em, C.PS
    V, A, G, T = nc.vector, nc.scalar, nc.gpsimd, nc.tensor
    with ExitStack() as ps_:
        sb = lambda n, s, d=F32: ps_.enter_context(nc.sbuf_tensor("AT" + n, list(s), d))
        wo = sb("wo", [128, 8, D], BF16)
        KTb = [sb(f"KTb{i}", [128, 8, 1024], BF16) for i in range(2)]
        Vb = [sb(f"Vb{i}", [128, 8, 16, 65], BF16) for i in range(2)]
        QTb = sb("QTb", [128, 8, 512], BF16); xt = sb("xt", [128, 4, D])
        KTc = sb("KTc", [128, 8, LC], BF16); Vc = sb("Vc", [128, 2, 16, 65], BF16)
        BTi = sb("BTi", [128, 16, 5, 128], BF16); BTe = sb("BTe", [128, 16, 8, 128], BF16)
        PT = [sb(f"PT{i}", [128, 1280], BF16) for i in range(2)]
        Ot = sb("Ot", [128, D]); OTt = sb("OTt", [128, 8, 128], BF16); rden = sb("rden", [128, 4])
        Gx = sb("Gx", [128, D]); ssq = sb("ssq", [128, 4]); rs = sb("rs", [128, 4]); junk = sb("junk", [128, D], BF16)
        tt = sb("tt", [128, D])
        _wload(C, wo, C.wona_d.rearrange("(k p) n -> p k n", p=128), "Awo", 8, D)
        em.dma("sp", Gx[:], C.GROW[1, 0, 0, :].partition_broadcast(128), W=["AGx"])
        em.dma("sp", BTi[:], C.BT_d, W=["ABTi"])
        KTv = C.KT.rearrange("(c p) t -> p c t", p=128); QTv = C.QT.rearrange("(c p) t -> p c t", p=128)
        em.dma("sp", KTc[:], KTv[:, :, NL:NL + LC], W=["AKTc"])
        for i in range(2):
            em.op("pool", lambda i=i: G.memset(Vb[i][:, :, :, 64:65], 1.0), W=[f"AVb{i}"])
        em.op("pool", lambda: G.memset(Vc[:, :, :, 64:65], 1.0), W=["AVc"])
        for c in range(2):
            em.dma("sp", Vc[:, c, :, 0:64], C.VD[NL + c * 128:NL + (c + 1) * 128, :].rearrange("p (h d) -> p h d", d=64), W=["AVc"])

        def kbof(blk):
            return 0 if blk == 0 else (56 if blk == NLT - 1 else 8 * blk - 4)

        def load(blk, sl):
            kb = kbof(blk)
            em.dma("sp", KTb[sl][:], KTv[:, :, kb * 64:kb * 64 + 1024], W=[f"AKTb{sl}"])
            for c in range(8):
                t0 = kb * 64 + c * 128
                em.dma("sp", Vb[sl][:, c, :, 0:64], C.VD[t0:t0 + 128, :].rearrange("p (h d) -> p h d", d=64), W=[f"AVb{sl}"])
        load(0, 0)
        for blk in range(NLT):
            sl = blk % 2
            r0 = 8 * blk
            edge = blk in (0, NLT - 1)
            eb = 0 if blk == 0 else 1
            nloc = 8 if edge else 5
            nch = nloc + 2
            em.dma("sp", QTb[:], QTv[:, :, r0 * 64:r0 * 64 + 512], W=["AQTb"])
            em.dma("sp", xt[:], C.X2[blk * 512:(blk + 1) * 512, :].rearrange("(s p) d -> p s d", p=128), W=["Axt"])
            if blk + 1 < NLT:
                load(blk + 1, 1 - sl)

            def sbase(h):
                return 0 if edge else 2 * (h % 2)

            def cpos(cl, h):
                if edge:
                    return (cl // 4, (cl % 4) * 128)
                return (2 * (h % 2) + cl // 4, (cl % 4) * 128)

            def qk(i, h):
                off = 0 if edge else i
                if edge and h == 0:
                    em.dma("sp", BTe[:], C.BTE_d[eb, i], W=["ABTe"])
                j = h // 2; e = h % 2
                p0, p1 = 64 * e, 64 * e + 64
                for cl in range(nch):
                    bk, co = cpos(cl, h)
                    o = PS[:, bk, co:co + 128]
                    if cl < nloc:
                        lt = KTb[sl][p0:p1, j, (off + cl) * 128:(off + cl + 1) * 128]
                    else:
                        lt = KTc[p0:p1, j, (cl - nloc) * 128:(cl - nloc + 1) * 128]
                    last = cl == nch - 1
                    em.op("pe", lambda o=o, lt=lt, j=j, p0=p0, p1=p1, i=i, cl=cl, last=last: T.matmul(
                        o, lhsT=lt, rhs=QTb[p0:p1, j, i * 128:(i + 1) * 128], start=(cl % 4 == 0),
                        stop=(edge and last)), R=[f"AKTb{sl}", "AKTc", "AQTb"], W=[("ps", bk)], inc=(edge and last))
                    if edge:
                        if cl in (3, 7):
                            em.op("pe", lambda h=h, bk=bk, cl=cl: T.matmul(
                                PS[:, bk, :], lhsT=C.identb[:, :], rhs=BTe[:, h, cl - 3:cl + 1, :].rearrange("p a b -> p (a b)"),
                                start=False, stop=True), R=["identb", "ABTe"], W=[("ps", bk)], inc=False)
                    else:
                        if cl == 3:
                            em.op("pe", lambda h=h, bk=bk: T.matmul(
                                PS[:, bk, :], lhsT=C.identb[:, :], rhs=BTi[:, h, 0:4, :].rearrange("p a b -> p (a b)"),
                                start=False, stop=True), R=["identb", "ABTi"], W=[("ps", bk)], inc=False)
                        if cl == 6:
                            em.op("pe", lambda h=h, bk=bk: T.matmul(
                                PS[:, bk, 0:128], lhsT=C.identb[:, :], rhs=BTi[:, h, 4, :],
                                start=False, stop=True), R=["identb", "ABTi"], W=[("ps", bk)], inc=True)

            def ex(i, h):
                b0 = sbase(h)
                nb = 3 if edge else 2
                src = PS[:, b0:b0 + nb, :].rearrange("p a b -> p (a b)")[:, 0:nch * 128]
                em.op("act", lambda src=src, h=h: A.activation(out=PT[h % 2][:, 0:nch * 128], in_=src, func=AF.Exp),
                      R=[("ps", b0 + q) for q in range(nb)], W=[f"APT{h % 2}"])

            def pv(i, h):
                off = 0 if edge else i
                ob = 4 + (h // 4) % 2
                so = (h % 4) * 128
                for c in range(nch):
                    rhs = Vb[sl][:, off + c, h, :] if c < nloc else Vc[:, c - nloc, h, :]
                    em.op("pe", lambda c=c, rhs=rhs, h=h, ob=ob, so=so: T.matmul(
                        PS[:, ob, so:so + 65], lhsT=PT[h % 2][:, c * 128:(c + 1) * 128], rhs=rhs,
                        start=(c == 0), stop=(c == nch - 1)), R=[f"APT{h % 2}", f"AVb{sl}", "AVc"], W=[("ps", ob)],
                          inc=(c == nch - 1))
                if h % 4 == 3:
                    em.op("dve", lambda ob=ob: V.reciprocal(out=rden[:, 0:4], in_=PS[:, ob, 64:512:128]),
                          R=[("ps", ob)], W=["Arden"])
                    for hh in range(4):
                        hd = h - 3 + hh
                        em.op("dve", lambda ob=ob, hh=hh, hd=hd: V.tensor_scalar(
                            out=Ot[:, hd * 64:(hd + 1) * 64], in0=PS[:, ob, hh * 128:hh * 128 + 64],
                            scalar1=rden[:, hh:hh + 1], scalar2=None, op0=ALU.mult), R=[("ps", ob), "Arden"], W=["AOt"])

            def fin(i):
                for k in range(8):
                    em.op("pe", lambda k=k: T.transpose(PS[:, 6 + k // 4, (k % 4) * 128:(k % 4 + 1) * 128],
                                                        Ot[:, k * 128:(k + 1) * 128], C.ident[:]),
                          R=["AOt", "ident"], W=[("ps", 6 + k // 4)], inc=(k % 4 == 3))
                for hb in range(2):
                    em.op("act" if hb else "dve",
                          (lambda hb=hb: A.copy(out=OTt[:, 4 * hb:4 * hb + 4, :].rearrange("p a b -> p (a b)"), in_=PS[:, 6 + hb, :])) if hb else
                          (lambda hb=hb: V.tensor_copy(out=OTt[:, 4 * hb:4 * hb + 4, :].rearrange("p a b -> p (a b)"), in_=PS[:, 6 + hb, :])),
                          R=[("ps", 6 + hb)], W=["AOTt"])
                for hf in range(2):
                    for k in range(8):
                        em.op("pe", lambda k=k, hf=hf: T.matmul(PS[:, 6 + hf, :], lhsT=OTt[:, k, :], rhs=wo[:, k, hf * 512:(hf + 1) * 512],
                                                                start=(k == 0), stop=(k == 7)),
                              R=["AOTt", "Awo"], W=[("ps", 6 + hf)], inc=(k == 7))
                C.epilogue(6, xt[:, i, :], "Axt", Gx[:, :], "AGx", ssq, rs, junk, tt, "A")

            items = [(i, h) for i in range(4) for h in range(16)]
            qk(*items[0])
            for n_, (i, h) in enumerate(items):
                nxt = items[n_ + 1] if n_ + 1 < len(items) else None
                if edge:
                    ex(i, h)
                    if nxt:
                        qk(*nxt)
                else:
                    if nxt:
                        qk(*nxt)
                    ex(i, h)
                pv(i, h)
                if h == 15:
                    fin(i)
            em.dma("pool", C.X3[blk * 512:(blk + 1) * 512, :].rearrange("(s p) d -> p s d", p=128), xt[:], R=["Axt"], W=[("X3", blk)])


PHASES += [("E", phase_E), ("F", phase_F),
           ("G", lambda C: phase_FFN(C, 1, C.X3, None, C.out_d, None, "X3", "OUT"))]
```

```python
import math
from contextlib import ExitStack
import numpy as np
import ml_dtypes
import concourse.bass as bass
import concourse.mybir as mybir
from concourse.bass_utils import run_bass_kernel_spmd

F32 = mybir.dt.float32
BF16 = mybir.dt.bfloat16
AF = mybir.ActivationFunctionType
ALU = mybir.AluOpType
NPBF = ml_dtypes.bfloat16

D = 1024
S = 8192
LC = 256
NT = S + LC
FH = 2816
NJ = FH // 128
EPS = 1e-6
NCORES = 8
NL = 4608
NLT = NL // 512
LBASE = 3584
NEG = -30000.0


class Em:
    LIMIT = 30000
    NDS = 8

    def __init__(self, nc, es):
        self.nc = nc
        self.es = es
        self.eng = dict(pe=nc.tensor, act=nc.scalar, dve=nc.vector, pool=nc.gpsimd, sp=nc.sync)
        self.sem = {}
        self.semkey = {}
        self.cnt = {}
        self.nsem = 0
        for e in self.eng:
            self._newsem(e)
        self.waited = {e: {} for e in self.eng}
        self.lastw = {}
        self.readers = {}
        self.dsem = {}
        self.ndma = {}
        for q in ("sp", "pool", "act"):
            self.dsem[q] = [es.enter_context(nc.semaphore(f"d{q}{i}")) for i in range(self.NDS)]
            self.ndma[q] = 0

    def _newsem(self, e):
        self.nsem += 1
        self.sem[e] = self.es.enter_context(self.nc.semaphore(f"s{e}{self.nsem}"))
        self.semkey[e] = (e, self.nsem)
        self.cnt[e] = 0

    def _deps(self, engine, R, W):
        deps = {}
        for k in list(R) + list(W):
            t = self.lastw.get(k)
            if t is not None:
                if t[0] not in deps or deps[t[0]][2] < t[2]:
                    deps[t[0]] = t
        for k in W:
            for t in self.readers.get(k, {}).values():
                if t[0] not in deps or deps[t[0]][2] < t[2]:
                    deps[t[0]] = t
        e = self.eng[engine]
        for sk, t in deps.items():
            if engine == "pe" and t[3] == "pe":
                continue
            if self.waited[engine].get(sk, 0) >= t[2]:
                continue
            e.wait_ge(t[1], t[2])
            self.waited[engine][sk] = t[2]

    def _record(self, tok, R, W):
        for k in W:
            self.lastw[k] = tok
            self.readers[k] = {}
        for k in R:
            d = self.readers.setdefault(k, {})
            if tok[0] not in d or d[tok[0]][2] < tok[2]:
                d[tok[0]] = tok

    def op(self, engine, fn, R=(), W=(), inc=True):
        self._deps(engine, R, W)
        ins = fn()
        if inc:
            ins.then_inc(self.sem[engine], 1)
            self.cnt[engine] += 1
            tok = (self.semkey[engine], self.sem[engine], self.cnt[engine], engine)
            self._record(tok, R, W)
            if self.cnt[engine] >= self.LIMIT:
                self._newsem(engine)
        else:
            tok = (self.semkey[engine], self.sem[engine], self.cnt[engine] + 1, engine)
            self._record(tok, R, W)
        return ins

    def dma(self, q, out, in_, R=(), W=(), **kw):
        self._deps(q, R, W)
        i = self.ndma[q]
        self.ndma[q] += 1
        sem = self.dsem[q][i % self.NDS]
        rnd = i // self.NDS
        sk = ("dma", q, i % self.NDS)
        if rnd > 0 and self.waited[q].get(sk, 0) < 16 * rnd:
            self.eng[q].wait_ge(sem, 16 * rnd)
            self.waited[q][sk] = 16 * rnd
        self.eng[q].dma_start(out=out, in_=in_, **kw).then_inc(sem, 16)
        tok = (sk, sem, 16 * (rnd + 1), None)
        self._record(tok, R, W)

    def finish(self):
        sp = self.eng["sp"]
        for q in self.dsem:
            n = self.ndma[q]
            for s in range(self.NDS):
                uses = (n - s + self.NDS - 1) // self.NDS if n > s else 0
                if uses > 0:
                    sp.wait_ge(self.dsem[q][s], 16 * uses)
        for e in ("pe", "act", "dve", "pool"):
            if self.cnt[e] > 0:
                sp.wait_ge(self.sem[e], self.cnt[e])


def _tables():
    t = {}
    t["ident"] = np.eye(128, dtype=np.float32)
    t["identb"] = np.eye(128, dtype=np.float32).astype(NPBF)
    a = np.arange(128, dtype=np.float64)
    ang = 2 * np.pi * np.outer(a, a) / 128.0
    t["T1"] = np.concatenate([np.cos(ang), -np.sin(ang)], axis=1).astype(NPBF)
    t2 = np.arange(64, dtype=np.float64)[:, None, None]
    k1 = np.arange(128, dtype=np.float64)[None, :, None]
    k2 = np.arange(64, dtype=np.float64)[None, None, :]
    ph = 2 * np.pi * (t2 * k2 / 64.0 + t2 * k1 / 8192.0)
    Mc, Ms = np.cos(ph), np.sin(ph)
    t["M2"] = np.concatenate([Ms, Mc, -Ms], axis=2).astype(NPBF)
    c = np.arange(64, dtype=np.float64)
    angc = 2 * np.pi * np.outer(c, c) / 64.0
    Cc, Sc = np.cos(angc), np.sin(angc)
    z = np.zeros((64, 64))
    Cbd = np.block([[Cc, z], [z, Cc]])
    Sbd = np.block([[Sc, z], [z, Sc]])
    sx = 1.0 / math.sqrt(8192.0 * 64.0)
    t["CSf"] = (np.concatenate([Cbd, Sbd], axis=1) * sx).astype(NPBF)
    sc_ = 1.0 / math.sqrt(256.0 * 64.0)
    t["CSc"] = (np.concatenate([Cbd, Sbd], axis=1) * sc_).astype(NPBF)
    p = np.arange(256, dtype=np.float64)
    angp = 2 * np.pi * np.outer(p, p) / 256.0
    T256 = np.concatenate([np.cos(angp), -np.sin(angp)], axis=1)
    t["T256"] = T256.reshape(2, 128, 512).transpose(1, 0, 2).copy().astype(NPBF)
    return t


_TABLES = None


def _bias_tables(rpb):
    H = 16
    out = np.full((5, 5 * 128, H, 128), NEG, dtype=np.float32)
    kr = np.arange(10)[:, None, None, None]
    kc = np.arange(64)[None, :, None, None]
    qr = np.arange(2)[None, None, :, None]
    qc = np.arange(64)[None, None, None, :]
    cs = np.clip(qc - 8, 0, 48)
    for vi, (gp, ks) in enumerate([(0, 0), (2, 0), (60, 56), (124, 118), (126, 118)]):
        gq = gp + qr
        rs = np.clip(gq - 4, 0, 120)
        gk = ks + kr
        valid = (gk >= rs) & (gk < rs + 8) & (kc >= cs) & (kc < cs + 16)
        dr = np.clip(gk - gq + 7, 0, 14)
        dc = np.clip(kc - qc + 15, 0, 30)
        valid, dr, dc = np.broadcast_arrays(valid, dr, dc)
        vals = rpb[:, dr, dc]
        vals = np.where(valid[None], vals, NEG)
        out[vi] = vals.transpose(1, 2, 0, 3, 4).reshape(640, H, 128)
    bt = out.reshape(5, 5, 128, H, 128).transpose(0, 2, 3, 1, 4)
    return np.ascontiguousarray(bt).astype(NPBF)


def _edge_tables(rpb, h):
    H = 16
    out = np.empty((2, 4, 128, H, 8, 128), dtype=NPBF)
    kr = np.arange(16)[:, None, None, None]
    kc = np.arange(64)[None, :, None, None]
    qr = np.arange(2)[None, None, :, None]
    qc = np.arange(64)[None, None, None, :]
    cs = np.clip(qc - 8, 0, 48)
    for eb, (r0, kb) in enumerate([(0, 0), (64, 56)]):
        for i in range(4):
            gq = r0 + 2 * i + qr + 56 * h
            gk = kb + kr + 56 * h
            rs = np.clip(gq - 4, 0, 120)
            valid = (gk >= rs) & (gk < rs + 8) & (kc >= cs) & (kc < cs + 16)
            dr = np.clip(gk - gq + 7, 0, 14)
            dc = np.clip(kc - qc + 15, 0, 30)
            valid, dr, dc = np.broadcast_arrays(valid, dr, dc)
            vals = np.where(valid[None], rpb[:, dr, dc], NEG)
            t = vals.transpose(1, 2, 0, 3, 4).reshape(8, 128, H, 128)
            out[eb, i] = t.transpose(1, 2, 0, 3).astype(NPBF)
    return out


def _col(v, nchunk):
    return np.ascontiguousarray(np.asarray(v, np.float32).reshape(nchunk, 128).T)


def build(stop="all", debug=False):
    nc = bass.Bass("TRN2", target_bir_lowering=False)

    def din(name, shape, dt=F32):
        return nc.dram_tensor(name, list(shape), dt, kind="ExternalInput").ap()

    skind = "ExternalOutput" if debug else "Internal"

    def dscr(name, shape, dt):
        return nc.dram_tensor(name, list(shape), dt, kind=skind).ap()

    x_d = din("x", [S, D]); xloc_d = din("xloc", [NL, D]); hsel_d = din("hsel", [128, 2]); ctx_d = din("ctx", [LC, D]); ccols_d = din("ccols", [128, 16])
    wmod_d = din("w_mod", [2, D, 6 * D]); bmodc_d = din("bmodc", [128, 2, 48]); bmod_d = din("b_mod", [2, 6 * D])
    gcols_d = din("gcols", [128, 2, 2, 8]); gpm_d = din("g_post_mix", [2, D]); gpf_d = din("g_post_ffn", [2, D])
    win_d = din("w_in_ab", [D, 1536]); woab_d = din("w_out_ab", [D, D])
    wg_d = din("w_ffn_gate", [2, D, FH]); wu_d = din("w_ffn_up", [2, D, FH]); wd_d = din("w_ffn_down", [2, FH, D])
    wqkv_d = din("w_qkv_na", [D, 3 * D]); wona_d = din("w_out_na", [D, D])
    wbd_d = din("wbd", [2, 2, 4, 128, 128]); lcols_d = din("lcols", [128, 4, 2, 3]); convc_d = din("convc", [128, 4, 5]); cdiag_d = din("cdiag", [4, 4, 128, 128])
    ident_d = din("ident", [128, 128]); identb_d = din("identb", [128, 128], BF16)
    T1_d = din("T1", [128, 256], BF16); M2_d = din("M2", [64, 128, 192], BF16); CSf_d = din("CSf", [128, 256], BF16)
    CSc_d = din("CSc", [128, 256], BF16); T256_d = din("T256", [128, 2, 512], BF16)
    BT_d = din("BT", [128, 16, 5, 128], BF16); BTE_d = din("BTE", [2, 4, 128, 16, 8, 128], BF16)
    out_d = nc.dram_tensor("out", [NL, D], F32, kind="ExternalOutput").ap()

    UG = dscr("UG", [8, 128, S], F32)
    UGc = dscr("UGc", [8, 128, LC], F32)
    MIXT = dscr("MIXT", [D, NT], BF16)
    GROW = dscr("GROW", [2, 2, 2, D], F32)
    X1 = dscr("X1", [NL, D], F32); C1 = dscr("C1", [LC, D], F32)
    X2 = dscr("X2", [NL, D], F32); C2 = dscr("C2", [LC, D], F32)
    X3 = dscr("X3", [NL, D], F32)
    QT = dscr("QT", [D, NL], BF16); KT = dscr("KT", [D, NL + LC], BF16); VD = dscr("VD", [NL + LC, D], BF16)

    with ExitStack() as es:
        em = Em(nc, es)

        def barrier():
            for e in ("pe", "act", "dve", "pool", "sp"):
                eng = em.eng[e]
                for f in ("pe", "act", "dve", "pool"):
                    if f != e and em.cnt[f] > 0 and em.waited[e].get(em.semkey[f], 0) < em.cnt[f]:
                        eng.wait_ge(em.sem[f], em.cnt[f]); em.waited[e][em.semkey[f]] = em.cnt[f]
                for q in em.dsem:
                    n = em.ndma[q]
                    for s_ in range(em.NDS):
                        uses = (n - s_ + em.NDS - 1) // em.NDS if n > s_ else 0
                        sk = ("dma", q, s_)
                        if uses > 0 and em.waited[e].get(sk, 0) < 16 * uses:
                            eng.wait_ge(em.dsem[q][s_], 16 * uses); em.waited[e][sk] = 16 * uses

        PS = es.enter_context(nc.psum_tensor("PS", [128, 8, 512], F32))
        ident = es.enter_context(nc.sbuf_tensor("ident_s", [128, 128], F32))
        identb = es.enter_context(nc.sbuf_tensor("identb_s", [128, 128], BF16))
        MODC = es.enter_context(nc.sbuf_tensor("MODC", [128, 2, 2, 2, 2, 8], F32))
        mhalf = es.enter_context(nc.sbuf_tensor("mhalf", [128, 8], F32))
        em.dma("sp", ident[:], ident_d, W=["ident"])
        em.dma("sp", identb[:], identb_d, W=["identb"])
        em.op("dve", lambda: nc.vector.memset(mhalf[:], -0.5), W=["mhalf"])

        V = nc.vector; A = nc.scalar; G = nc.gpsimd; T = nc.tensor

        def pbank(b):
            return ("ps", b)

        def rstd_from_ssq(ssq, rs, n, tag):
            em.op("dve", lambda: V.tensor_scalar(out=rs[:, 0:n], in0=ssq[:, 0:n], scalar1=1.0 / D, scalar2=EPS,
                                                 op0=ALU.mult, op1=ALU.add), R=[tag + "ssq"], W=[tag + "rs"])
            em.op("pool", lambda: G.tensor_tensor(out=rs[:, 0:n], in0=rs[:, 0:n], in1=mhalf[:, 0:n], op=ALU.pow),
                  R=[tag + "rs", "mhalf"], W=[tag + "rs"])

        def prologue_a(xt, xkey, nsub, xs, xskey, ssq, rs, junk, tag):
            for s in range(nsub):
                em.op("act", lambda s=s: A.activation(out=junk[:, :], in_=xt[:, s, :], func=AF.Square,
                                                      accum_out=ssq[:, s:s + 1]),
                      R=[xkey], W=[tag + "junk", tag + "ssq"])
            rstd_from_ssq(ssq, rs, nsub, tag)
            for s in range(nsub):
                em.op("act", lambda s=s: A.activation(out=xs[:, s, :], in_=xt[:, s, :], func=AF.Copy,
                                                      scale=rs[:, s:s + 1]),
                      R=[xkey, tag + "rs"], W=[xskey])

        def prologue_b(nsub, xs, xskey, Acol, Bcol, hT, hkey, tb):
            for k in range(8):
                b = tb[k % len(tb)]
                for s in range(nsub):
                    em.op("pe", lambda s=s, k=k, b=b: T.transpose(PS[:, b, s * 128:(s + 1) * 128],
                                                                  xs[:, s, k * 128:(k + 1) * 128], ident[:]),
                          R=[xskey, "ident"], W=[pbank(b)], inc=(s == nsub - 1))
                em.op("act", lambda k=k, b=b: A.activation(out=hT[:, k, 0:nsub * 128], in_=PS[:, b, 0:nsub * 128],
                                                           func=AF.Identity, scale=Acol[:, k:k + 1],
                                                           bias=Bcol[:, k:k + 1]),
                      R=[pbank(b), "MODC"], W=[hkey])

        def prologue(xt, xkey, nsub, xs, xskey, Acol, Bcol, hT, hkey, ssq, rs, junk, tag, tb):
            prologue_a(xt, xkey, nsub, xs, xskey, ssq, rs, junk, tag)
            prologue_b(nsub, xs, xskey, Acol, Bcol, hT, hkey, tb)

        def epilogue(psb, xsub, xkey, Gb, gkey, ssq, rs, junk, tt, tag):
            yv = PS[:, psb:psb + 2, :]
            em.op("act", lambda: A.activation(out=junk[:, :], in_=yv, func=AF.Square, accum_out=ssq[:, 0:1]),
                  R=[pbank(psb), pbank(psb + 1)], W=[tag + "junk", tag + "ssq"])
            rstd_from_ssq(ssq, rs, 1, tag)
            em.op("dve", lambda: V.tensor_tensor(out=tt[:, :], in0=yv, in1=Gb, op=ALU.mult),
                  R=[pbank(psb), pbank(psb + 1), gkey], W=[tag + "tt"])
            em.op("dve", lambda: V.scalar_tensor_tensor(out=xsub, in0=tt[:, :], scalar=rs[:, 0:1], in1=xsub,
                                                        op0=ALU.mult, op1=ALU.add),
                  R=[tag + "tt", tag + "rs", xkey], W=[xkey])

        with ExitStack() as pes:
            sb = lambda n, s, d=F32: pes.enter_context(nc.sbuf_tensor(n, list(s), d))
            cc = sb("cc", [128, 16]); scc = sb("scc", [128, 16]); rhs2 = sb("rhs2", [128, 8, 2], BF16)
            bmc = sb("bmc", [128, 2, 48]); bm1 = sb("bm1", [128, 2, 48]); gco = sb("gco", [128, 2, 2, 8])
            bmrow = sb("bmrow", [2, 2, 2, D]); grow = sb("grow", [2, 2, 2, D]); grt = sb("grt", [2, 2, 2, D])
            wm = [sb(f"wm{i}", [128, 8, 512], BF16) for i in range(3)]
            em.dma("sp", cc[:], ccols_d, W=["cc"])
            em.dma("sp", bmc[:], bmodc_d, W=["bmc"])
            em.dma("sp", gco[:], gcols_d, W=["gco"])
            for l in range(2):
                for w_, (src, off) in enumerate([(bmod_d, 2 * D), (bmod_d, 5 * D)]):
                    em.dma("sp", bmrow[:, l, w_, :], src[l, off:off + D].partition_broadcast(2), W=["bmrow"])
                em.dma("sp", grow[:, l, 0, :], gpm_d[l, :].partition_broadcast(2), W=["grow"])
                em.dma("sp", grow[:, l, 1, :], gpf_d[l, :].partition_broadcast(2), W=["grow"])
            em.op("act", lambda: A.activation(out=scc[:], in_=cc[:], func=AF.Silu), R=["cc"], W=["scc"])
            em.op("dve", lambda: V.tensor_copy(out=rhs2[:, :, 0], in_=scc[:, 0:8]), R=["scc"], W=["rhs2"])
            em.op("dve", lambda: V.tensor_copy(out=rhs2[:, :, 1], in_=scc[:, 8:16]), R=["scc"], W=["rhs2"])
            em.op("dve", lambda: V.tensor_scalar(out=bm1[:], in0=bmc[:], scalar1=1.0, scalar2=None, op0=ALU.add),
                  R=["bmc"], W=["bm1"])
            it = 0
            for l in range(2):
                wsrc = wmod_d[l].rearrange("(k p) n -> p k n", p=128)
                for nb in range(12):
                    slot = it % 3; it += 1
                    wt = wm[slot]; wk = f"wm{slot}"
                    em.dma("pool", wt[:], wsrc[:, :, nb * 512:(nb + 1) * 512], W=[wk])
                    v = nb // 2; half = nb % 2
                    b = it % 4
                    if v in (2, 5):
                        w_ = 0 if v == 2 else 1
                        for k in range(8):
                            em.op("pe", lambda k=k, b=b, wt=wt: T.matmul(PS[0:2, b, :], lhsT=rhs2[:, k, :], rhs=wt[:, k, :],
                                                                        start=(k == 0), stop=(k == 7)),
                                  R=["rhs2", wk], W=[pbank(b)], inc=(k == 7))
                        dst = grt[:, l, w_, half * 512:(half + 1) * 512]
                        em.op("dve", lambda b=b, dst=dst, l=l, w_=w_, half=half: V.tensor_tensor(
                            out=dst, in0=PS[0:2, b, :], in1=bmrow[:, l, w_, half * 512:(half + 1) * 512], op=ALU.add),
                              R=[pbank(b), "bmrow"], W=["grt"])
                        em.op("dve", lambda dst=dst, l=l, w_=w_, half=half: V.tensor_tensor(
                            out=dst, in0=dst, in1=grow[:, l, w_, half * 512:(half + 1) * 512], op=ALU.mult),
                              R=["grt", "grow"], W=["grt"])
                        if half == 1:
                            em.dma("sp", GROW[l, w_, :, :], grt[:, l, w_, :], R=["grt"], W=[("GROW", l, w_)])
                    else:
                        sub = 0 if v < 2 else 1
                        isA = v in (1, 4)
                        for m in range(4):
                            ch = half * 4 + m
                            for k in range(8):
                                em.op("pe", lambda k=k, b=b, m=m, wt=wt: T.matmul(
                                    PS[:, b, 2 * m:2 * m + 2], lhsT=wt[:, k, m * 128:(m + 1) * 128], rhs=rhs2[:, k, :],
                                    start=(k == 0), stop=(k == 7)), R=["rhs2", wk], W=[pbank(b)], inc=(k == 7))
                            dst = MODC[:, l, sub, 0 if isA else 1, :, ch]
                            if isA:
                                em.op("dve", lambda b=b, m=m, dst=dst, l=l, v=v, ch=ch, sub=sub: V.tensor_scalar(
                                    out=dst, in0=PS[:, b, 2 * m:2 * m + 2], scalar1=bm1[:, l, v * 8 + ch:v * 8 + ch + 1],
                                    scalar2=gco[:, l, sub, ch:ch + 1], op0=ALU.add, op1=ALU.mult),
                                      R=[pbank(b), "bm1", "gco"], W=["MODC"])
                            else:
                                em.op("dve", lambda b=b, m=m, dst=dst, l=l, v=v, ch=ch: V.tensor_scalar(
                                    out=dst, in0=PS[:, b, 2 * m:2 * m + 2], scalar1=bmc[:, l, v * 8 + ch:v * 8 + ch + 1],
                                    scalar2=None, op0=ALU.add), R=[pbank(b), "bmc"], W=["MODC"])
            if debug:
                MODCd = nc.dram_tensor("MODCd", [128, 128], F32, kind="ExternalOutput").ap()
                em.dma("sp", MODCd, MODC[:].rearrange("p a b c d e -> p (a b c d e)"), R=["MODC"], W=["MODCd"])
            barrier()
        if stop == "0":
            em.finish()
            return nc
        C = type("C", (), {})()
        C.__dict__.update(locals())
        for name, fn in PHASES:
            fn(C)
            barrier()
            if stop == name:
                if hasattr(C, 'es_mix'):
                    C.es_mix.close()
                break
        em.finish()
    return nc


PHASES = []


def _host_inputs(inp, b):
    global _TABLES
    if _TABLES is None:
        _TABLES = _tables()
    f = lambda a: np.ascontiguousarray(np.asarray(a, dtype=np.float32))
    m = {}
    b, h = b // 2, b % 2
    m["x"] = f(inp["x"][b]); m["ctx"] = f(inp["ctx"][b])
    m["xloc"] = f(inp["x"][b][LBASE * h:LBASE * h + NL])
    m["hsel"] = np.tile(np.array([[1.0 - h, float(h)]], np.float32), (128, 1))
    m["ccols"] = np.concatenate([_col(inp["c"][b], 8), _col(inp["c_ctx"], 8)], axis=1)
    m["w_mod"] = f(inp["w_mod"]); m["b_mod"] = f(inp["b_mod"])
    m["bmodc"] = np.ascontiguousarray(np.stack([_col(inp["b_mod"][l], 48) for l in range(2)], axis=1))
    m["gcols"] = np.ascontiguousarray(np.stack(
        [np.stack([_col(inp["g_pre_mix"][l], 8), _col(inp["g_pre_ffn"][l], 8)], axis=1) for l in range(2)], axis=1))
    m["g_post_mix"] = f(inp["g_post_mix"]); m["g_post_ffn"] = f(inp["g_post_ffn"])
    m["w_in_ab"] = f(inp["w_in_ab"][0]); m["w_out_ab"] = f(inp["w_out_ab"][0])
    m["w_ffn_gate"] = f(inp["w_ffn_gate"]); m["w_ffn_up"] = f(inp["w_ffn_up"]); m["w_ffn_down"] = f(inp["w_ffn_down"])
    m["w_qkv_na"] = f(inp["w_qkv_na"][0]); m["w_out_na"] = f(inp["w_out_na"][0])
    wbd = np.zeros((2, 2, 4, 128, 128), np.float32)
    for gi, key in enumerate(["lru_w_a", "lru_w_i"]):
        w = np.asarray(inp[key][0], np.float32)
        for d in range(2):
            for c in range(4):
                wbd[gi, d, c, 0:64, 0:64] = w[d, 2 * c]
                wbd[gi, d, c, 64:128, 64:128] = w[d, 2 * c + 1]
    m["wbd"] = wbd
    lc = np.zeros((128, 4, 2, 3), np.float32)
    for d in range(2):
        lc[:, :, d, 0] = _col(inp["lru_b_a"][0][d], 4)
        lc[:, :, d, 1] = _col(inp["lru_b_i"][0][d], 4)
        lc[:, :, d, 2] = _col(inp["lru_lam"][0][d], 4)
    m["lcols"] = lc
    cv = np.zeros((128, 4, 5), np.float32)
    for k in range(4):
        cv[:, :, k] = _col(inp["conv_w"][0][k], 4)
    cv[:, :, 4] = _col(inp["conv_b"][0], 4)
    m["convc"] = cv
    cd = np.zeros((4, 4, 128, 128), np.float32)
    ii = np.arange(128)
    for c in range(4):
        for k in range(4):
            cd[c, k, ii, ii] = cv[:, c, k]
    m["cdiag"] = cd
    for k in ("ident", "identb", "T1", "M2", "CSf", "CSc", "T256"):
        m[k] = _TABLES[k]
    rpb = np.asarray(inp["rpb_na"][0], np.float32)
    m["BT"] = np.ascontiguousarray(_bias_tables(rpb)[2])
    m["BTE"] = _edge_tables(rpb, h)
    return m


_NC_CACHE = {}


def kernel(**inputs):
    if "full" not in _NC_CACHE:
        _NC_CACHE["full"] = build()
    nc = _NC_CACHE["full"]
    in_maps = [_host_inputs(inputs, b) for b in range(NCORES)]
    res = run_bass_kernel_spmd(nc, in_maps, core_ids=list(range(NCORES)))
    out = np.empty((4, S, D), np.float32)
    for c in range(NCORES):
        b, h = c // 2, c % 2
        o = np.asarray(res.results[c]["out"], dtype=np.float32)
        out[b, 4096 * h:4096 * (h + 1)] = o[512 * h:512 * h + 4096]
    return out


def _wload(C, dst, src_view, key, nk, ncols, step=512):
    for c0 in range(0, ncols, step):
        c1 = min(ncols, c0 + step)
        C.em.dma("pool", dst[:, :, c0:c1], src_view[:, :, c0:c1], W=[key])


def phase_A(C):
    nc, em, PS = C.nc, C.em, C.PS
    V, A, G, T = nc.vector, nc.scalar, nc.gpsimd, nc.tensor
    pes = C.es_mix = ExitStack()
    C.F = pes.enter_context(nc.sbuf_tensor("Fbuf", [128, 64, 512], BF16))
    C.fTc = pes.enter_context(nc.sbuf_tensor("fTc", [128, 4, LC], BF16))
    F, fTc = C.F, C.fTc
    with ExitStack() as ps_:
        sb = lambda n, s, d=F32: ps_.enter_context(nc.sbuf_tensor(n, list(s), d))
        win = sb("win", [128, 8, 1536], BF16)
        xt = [sb(f"Axt{i}", [128, 4, D]) for i in range(2)]
        hT = [sb(f"AhT{i}", [128, 8, 512], BF16) for i in range(2)]
        ugst = [sb(f"Aug{i}", [128, 8, 512]) for i in range(2)]
        ssq = sb("Assq", [128, 4]); rs = sb("Ars", [128, 4]); junk = sb("Ajunk", [128, D], BF16)
        _wload(C, win, C.win_d.rearrange("(k p) n -> p k n", p=128), "win", 8, 1536)
        xsrc = C.x_d.rearrange("(t1 t2) d -> t1 t2 d", t2=64)
        Acol = C.MODC[:, 0, 0, 0, 0, :]; Bcol = C.MODC[:, 0, 0, 1, 0, :]
        AcolC = C.MODC[:, 0, 0, 0, 1, :]; BcolC = C.MODC[:, 0, 0, 1, 1, :]
        em.dma("sp", xt[0][:], xsrc[:, 0:4, :], W=["Axt0"])
        ev = 0
        for j in range(17):
            sl = j % 2
            isctx = j == 16
            nsub = 2 if isctx else 4
            n = nsub * 128
            if j + 1 < 16:
                em.dma("sp", xt[1 - sl][:], xsrc[:, 4 * (j + 1):4 * (j + 2), :], W=[f"Axt{1 - sl}"])
            elif j + 1 == 16:
                em.dma("sp", xt[1 - sl][:, 0:2, :], C.ctx_d.rearrange("(s p) d -> p s d", p=128), W=[f"Axt{1 - sl}"])
            C.prologue(xt[sl], f"Axt{sl}", nsub, xt[sl], f"Axt{sl}", AcolC if isctx else Acol, BcolC if isctx else Bcol,
                       hT[sl], f"AhT{sl}", ssq, rs, junk, "A", [0, 1])
            for oc in range(8):
                b = 2 + oc % 3
                for k in range(8):
                    em.op("pe", lambda k=k, b=b, oc=oc, sl=sl, n=n: T.matmul(
                        PS[:, b, 0:n], lhsT=win[:, k, oc * 128:(oc + 1) * 128], rhs=hT[sl][:, k, 0:n],
                        start=(k == 0), stop=(k == 7)), R=["win", f"AhT{sl}"], W=[("ps", b)], inc=(k == 7))
                ev += 1
                if ev % 2:
                    em.op("dve", lambda b=b, oc=oc, sl=sl, n=n: V.tensor_copy(out=ugst[sl][:, oc, 0:n], in_=PS[:, b, 0:n]),
                          R=[("ps", b)], W=[f"Aug{sl}"])
                else:
                    em.op("act", lambda b=b, oc=oc, sl=sl, n=n: A.copy(out=ugst[sl][:, oc, 0:n], in_=PS[:, b, 0:n]),
                          R=[("ps", b)], W=[f"Aug{sl}"])
            if isctx:
                em.dma("pool", C.UGc.rearrange("c p n -> p c n"), ugst[sl][:, :, 0:LC], R=[f"Aug{sl}"], W=["UGc"])
                for fc in range(4):
                    b = 5 + fc % 3
                    for k in range(8):
                        em.op("pe", lambda k=k, b=b, fc=fc, sl=sl: T.matmul(
                            PS[:, b, 0:LC], lhsT=win[:, k, 1024 + fc * 128:1024 + (fc + 1) * 128], rhs=hT[sl][:, k, 0:LC],
                            start=(k == 0), stop=(k == 7)), R=["win", f"AhT{sl}"], W=[("ps", b)], inc=(k == 7))
                    em.op("dve", lambda b=b, fc=fc: V.tensor_copy(out=fTc[:, fc, :], in_=PS[:, b, 0:LC]),
                          R=[("ps", b)], W=["fTc"])
            else:
                em.dma("pool", C.UG[:, :, j * 512:(j + 1) * 512].rearrange("c p n -> p c n"), ugst[sl][:],
                       R=[f"Aug{sl}"], W=[("UG", j)])
                for s in range(4):
                    b = 5 + s % 3
                    for k in range(8):
                        em.op("pe", lambda k=k, b=b, s=s, sl=sl: T.matmul(
                            PS[:, b, :], lhsT=hT[sl][:, k, s * 128:(s + 1) * 128], rhs=win[:, k, 1024:1536],
                            start=(k == 0), stop=(k == 7)), R=["win", f"AhT{sl}"], W=[("ps", b)], inc=(k == 7))
                    ev += 1
                    if ev % 2:
                        em.op("dve", lambda b=b, s=s, j=j: V.tensor_copy(out=F[:, 4 * j + s, :], in_=PS[:, b, :]),
                              R=[("ps", b)], W=["F"])
                    else:
                        em.op("act", lambda b=b, s=s, j=j: A.copy(out=F[:, 4 * j + s, :], in_=PS[:, b, :]),
                              R=[("ps", b)], W=["F"])


def phase_C(C):
    nc, em, PS, F, fTc = C.nc, C.em, C.PS, C.F, C.fTc
    V, A, G, T = nc.vector, nc.scalar, nc.gpsimd, nc.tensor
    with ExitStack() as ps_:
        sb = lambda n, s, d=F32: ps_.enter_context(nc.sbuf_tensor(n, list(s), d))
        T1 = sb("T1s", [128, 256], BF16); M2 = sb("M2s", [64, 128, 192], BF16); CSf = sb("CSfs", [128, 256], BF16)
        CSc = sb("CScs", [128, 256], BF16); T256 = sb("T256s", [128, 2, 512], BF16)
        Ast = sb("Ast", [64, 64, 256], BF16); Y = sb("Ybuf", [128, 2, S], BF16)
        fst = [sb(f"fst{i}", [128, 2048], BF16) for i in range(2)]
        Gc = sb("Gcb", [128, 2, 256], BF16); fcs = sb("fcs", [128, LC], BF16)
        for dst, src, k in ((T1, C.T1_d, "T1"), (M2, C.M2_d, "M2"), (CSf, C.CSf_d, "CSf"), (CSc, C.CSc_d, "CSc"),
                            (T256, C.T256_d, "T256")):
            em.dma("sp", dst[:], src, W=[k])
        ev = 0
        for cc in range(4):
            for hc in range(2):
                for g4 in range(16):
                    b = 2 * (g4 % 2)
                    for q in range(4):
                        ch = cc * 128 + hc * 64 + g4 * 4 + q
                        em.op("pe", lambda b=b, q=q, ch=ch: T.matmul(
                            PS[0:64, b + q // 2, (q % 2) * 256:(q % 2) * 256 + 256], lhsT=F[:, :, ch], rhs=T1[:, :],
                            start=True, stop=True), R=["F", "T1"], W=[("ps", b), ("ps", b + 1)], inc=(q == 3))
                    ev += 1
                    dst = Ast[:, g4 * 4:(g4 + 1) * 4, :].rearrange("p a b -> p (a b)")
                    src = PS[0:64, b:b + 2, :].rearrange("p a b -> p (a b)")
                    if ev % 2:
                        em.op("dve", lambda dst=dst, src=src: V.tensor_copy(out=dst, in_=src),
                              R=[("ps", b), ("ps", b + 1)], W=["Ast"])
                    else:
                        em.op("act", lambda dst=dst, src=src: A.copy(out=dst, in_=src),
                              R=[("ps", b), ("ps", b + 1)], W=["Ast"])
                for kb in range(32):
                    b = 4 + kb % 2
                    for q in range(4):
                        k1 = kb * 4 + q
                        o = PS[hc * 64:(hc + 1) * 64, b, q * 128:(q + 1) * 128]
                        em.op("pe", lambda o=o, k1=k1: T.matmul(o, lhsT=Ast[:, :, k1], rhs=M2[:, k1, 64:192],
                                                                start=True, stop=False),
                              R=["Ast", "M2"], W=[("ps", b)], inc=False)
                        em.op("pe", lambda o=o, k1=k1: T.matmul(o, lhsT=Ast[:, :, 128 + k1], rhs=M2[:, k1, 0:128],
                                                                start=False, stop=True),
                              R=["Ast", "M2"], W=[("ps", b)], inc=(q == 3))
                    ev += 1
                    src = PS[hc * 64:(hc + 1) * 64, b, :].rearrange("p (k r c) -> p r c k", k=4, r=2)
                    dst = Y[hc * 64:(hc + 1) * 64, :, :].rearrange("p r (c k) -> p r c k", k=128)[:, :, :, kb * 4:(kb + 1) * 4]
                    if ev % 2:
                        em.op("dve", lambda dst=dst, src=src: V.tensor_copy(out=dst, in_=src), R=[("ps", b)], W=["Y"])
                    else:
                        em.op("act", lambda dst=dst, src=src: A.copy(out=dst, in_=src), R=[("ps", b)], W=["Y"])
            for tl in range(16):
                b = 6 + tl % 2
                em.op("pe", lambda b=b, tl=tl: T.matmul(PS[:, b, :], lhsT=CSf[:, 0:128], rhs=Y[:, 0, tl * 512:(tl + 1) * 512],
                                                        start=True, stop=False), R=["CSf", "Y"], W=[("ps", b)], inc=False)
                em.op("pe", lambda b=b, tl=tl: T.matmul(PS[:, b, :], lhsT=CSf[:, 128:256], rhs=Y[:, 1, tl * 512:(tl + 1) * 512],
                                                        start=False, stop=True), R=["CSf", "Y"], W=[("ps", b)], inc=True)
                fs = (tl // 4) % 2
                em.op("act", lambda b=b, tl=tl, fs=fs: A.copy(out=fst[fs][:, (tl % 4) * 512:(tl % 4 + 1) * 512], in_=PS[:, b, :]),
                      R=[("ps", b)], W=[f"fst{fs}"])
                if tl % 4 == 3:
                    t0 = (tl // 4) * 2048
                    em.dma("pool", C.MIXT[512 + cc * 128:512 + (cc + 1) * 128, t0:t0 + 2048], fst[fs][:],
                           R=[f"fst{fs}"], W=[("MIXT", 4 + cc)])
            for tc in range(2):
                em.op("pe", lambda tc=tc, cc=cc: T.matmul(PS[:, 0, 0:256], lhsT=fTc[:, cc, tc * 128:(tc + 1) * 128], rhs=CSc[:, :],
                                                          start=True, stop=True), R=["fTc", "CSc"], W=[("ps", 0)])
                em.op("dve", lambda tc=tc: V.tensor_copy(out=Gc[:, tc, :], in_=PS[:, 0, 0:256]), R=[("ps", 0)], W=["Gc"])
            for i_, (tc, part) in enumerate([(0, 0), (0, 1), (1, 0), (1, 1)]):
                em.op("pe", lambda i_=i_, tc=tc, part=part: T.matmul(
                    PS[:, 1, 0:256], lhsT=Gc[:, tc, part * 128:(part + 1) * 128], rhs=T256[:, tc, part * 256:(part + 1) * 256],
                    start=(i_ == 0), stop=(i_ == 3)), R=["Gc", "T256"], W=[("ps", 1)], inc=(i_ == 3))
            em.op("dve", lambda: V.tensor_copy(out=fcs[:, :], in_=PS[:, 1, 0:256]), R=[("ps", 1)], W=["fcs"])
            em.dma("pool", C.MIXT[512 + cc * 128:512 + (cc + 1) * 128, S:NT], fcs[:], R=["fcs"], W=[("MIXTc", 4 + cc)])
    C.es_mix.close()


PHASES += [("A", phase_A), ("C", phase_C)]


def phase_B(C):
    nc, em, PS = C.nc, C.em, C.PS
    V, A, G, T = nc.vector, nc.scalar, nc.gpsimd, nc.tensor
    TW = 2048
    with ExitStack() as ps_:
        sb = lambda n, s, d=F32: ps_.enter_context(nc.sbuf_tensor(n, list(s), d))
        bufA = sb("bufA", [128, NT]); bufB = sb("bufB", [128, 8460]); uc = sb("ucb_", [128, NT]); ucb = sb("ucbb", [128, NT], BF16)
        rt = sb("rt", [128, TW])
        it_ = [sb(f"it{i}", [128, TW]) for i in range(2)]; at = [sb(f"at{i}", [128, TW]) for i in range(2)]
        st = [sb(f"st{i}", [128, TW]) for i in range(2)]
        wbd = sb("wbds", [128, 16, 128], BF16)
        cdg = sb("cdg", [128, 16, 128])
        lco = sb("lco", [128, 4, 2, 3]); cvc = sb("cvc", [128, 4, 5]); cA = sb("cA", [128, 4, 2]); carry = sb("carry", [128, 2])
        em.dma("pool", wbd[:], C.wbd_d.rearrange("g d c p n -> p (g d c) n"), W=["wbd"])
        em.dma("sp", cdg[:], C.cdiag_d.rearrange("c k p n -> p (c k) n"), W=["cdg"])
        em.dma("sp", lco[:], C.lcols_d, W=["lco"])
        em.dma("sp", cvc[:], C.convc_d, W=["cvc"])
        em.op("act", lambda: A.activation(out=cA[:], in_=lco[:, :, :, 2], func=AF.Exp, scale=-1.0), R=["lco"], W=["cA"])
        em.op("act", lambda: A.activation(out=cA[:], in_=cA[:], func=AF.Ln, bias=1.0), R=["cA"], W=["cA"])
        em.op("dve", lambda: V.tensor_scalar(out=cA[:], in0=cA[:], scalar1=-8.0, scalar2=None, op0=ALU.mult), R=["cA"], W=["cA"])
        em.op("dve", lambda: V.memset(bufB[:], 0.0), W=["bufB"])
        XO = 8200
        tiles = [(S, NT)] + [(i * TW, (i + 1) * TW) for i in range(4)]
        tix = 0
        for c in range(4):
            em.dma("sp", bufA[:, 0:S], C.UG[c], W=["bufA"])
            em.dma("sp", bufA[:, S:NT], C.UGc[c], W=["bufA"])
            em.op("pool", lambda: G.tensor_copy(out=bufB[:, 2:2 + S].rearrange("p (a b) -> p a b", b=64),
                                                in_=bufA[:, 0:S].rearrange("p (b a) -> p a b", a=128)),
                  R=["bufA"], W=["bufB"])
            em.op("pool", lambda: G.tensor_copy(out=bufB[:, XO + 2:XO + 2 + LC], in_=bufA[:, S:NT]), R=["bufA"], W=["bufB"])
            em.dma("sp", bufA[:, 0:S], C.UG[4 + c], W=["bufA"])
            em.dma("sp", bufA[:, S:NT], C.UGc[4 + c], W=["bufA"])
            for (o0, src0, n) in ((0, 0, S), (S, XO, LC)):
                for q0 in range(0, n, 2048):
                    nn = min(2048, n - q0)
                    nq = (nn + 511) // 512
                    pb = 4 * ((q0 // 2048) % 2)
                    for q in range(nq):
                        w_ = min(512, nn - q * 512)
                        for k in range(4):
                            em.op("pe", lambda q=q, k=k, w_=w_, pb=pb, src0=src0, q0=q0, c=c: T.matmul(
                                PS[:, pb + q, 0:w_], lhsT=cdg[:, c * 4 + k, :],
                                rhs=bufB[:, src0 + q0 + q * 512 + k:src0 + q0 + q * 512 + k + w_],
                                start=(k == 0), stop=(k == 3)), R=["cdg", "bufB"], W=[("ps", pb + q)], inc=(k == 3))
                    src = PS[:, pb:pb + 4, :].rearrange("p a b -> p (a b)")[:, 0:nn]
                    em.op("act", lambda src=src, o0=o0, q0=q0, nn=nn, c=c: A.activation(
                        out=uc[:, o0 + q0:o0 + q0 + nn], in_=src, func=AF.Identity, bias=cvc[:, c, 4:5]),
                          R=[("ps", pb + q) for q in range(4)] + ["cvc"], W=["uc"])
            em.op("pool", lambda: G.tensor_copy(out=ucb[:, :], in_=uc[:, :]), R=["uc"], W=["ucb"])

            def gelu_tile(lo, hi, p):
                n = hi - lo
                em.op("act", lambda: A.activation(out=st[p][:, 0:n], in_=bufA[:, lo:hi], func=AF.Square),
                      R=["bufA"], W=[f"st{p}"])
                em.op("pool", lambda: G.tensor_scalar(out=st[p][:, 0:n], in0=st[p][:, 0:n], scalar1=0.044715, scalar2=1.0,
                                                      op0=ALU.mult, op1=ALU.add), R=[f"st{p}"], W=[f"st{p}"])
                em.op("dve", lambda: V.tensor_tensor(out=st[p][:, 0:n], in0=st[p][:, 0:n], in1=bufA[:, lo:hi], op=ALU.mult),
                      R=[f"st{p}", "bufA"], W=[f"st{p}"])
                em.op("act", lambda: A.activation(out=st[p][:, 0:n], in_=st[p][:, 0:n], func=AF.Sigmoid,
                                                  scale=1.5957691216057308), R=[f"st{p}"], W=[f"st{p}"])
                em.op("dve", lambda: V.tensor_tensor(out=bufA[:, lo:hi], in0=st[p][:, 0:n], in1=bufA[:, lo:hi], op=ALU.mult),
                      R=[f"st{p}", "bufA"], W=["bufA"])

            for d in range(2):
                order = [tiles[0]] + (tiles[1:] if d == 0 else tiles[1:][::-1])
                for ti, (lo, hi) in enumerate(order):
                    p = tix % 2
                    tix += 1
                    n = hi - lo
                    nq = (n + 511) // 512
                    for gi in range(2):
                        for q in range(nq):
                            w_ = min(512, n - q * 512)
                            em.op("pe", lambda gi=gi, q=q, w_=w_, lo=lo, d=d, c=c: T.matmul(
                                PS[:, gi * 4 + q, 0:w_], lhsT=wbd[:, gi * 8 + d * 4 + c, :], rhs=ucb[:, lo + q * 512:lo + q * 512 + w_],
                                start=True, stop=True), R=["wbd", "ucb"], W=[("ps", gi * 4 + q)], inc=(q == nq - 1))
                    pr = PS[:, 0:4, :].rearrange("p a b -> p (a b)")[:, 0:n]
                    pi = PS[:, 4:8, :].rearrange("p a b -> p (a b)")[:, 0:n]
                    em.op("act", lambda pr=pr, n=n, c=c, d=d: A.activation(out=rt[:, 0:n], in_=pr, func=AF.Sigmoid,
                                                                           bias=lco[:, c, d, 0:1]),
                          R=[("ps", q) for q in range(4)] + ["lco"], W=["rt"])
                    em.op("act", lambda pi=pi, n=n, c=c, d=d, p=p: A.activation(out=it_[p][:, 0:n], in_=pi, func=AF.Sigmoid,
                                                                                bias=lco[:, c, d, 1:2]),
                          R=[("ps", 4 + q) for q in range(4)] + ["lco"], W=[f"it{p}"])
                    em.op("act", lambda n=n, c=c, d=d, p=p: A.activation(out=at[p][:, 0:n], in_=rt[:, 0:n], func=AF.Exp,
                                                                         scale=cA[:, c, d:d + 1]), R=["rt", "cA"], W=[f"at{p}"])
                    em.op("act", lambda n=n, p=p: A.activation(out=st[p][:, 0:n], in_=at[p][:, 0:n], func=AF.Square),
                          R=[f"at{p}"], W=[f"st{p}"])
                    em.op("act", lambda n=n, p=p: A.activation(out=st[p][:, 0:n], in_=st[p][:, 0:n], func=AF.Sqrt, scale=-1.0, bias=1.0),
                          R=[f"st{p}"], W=[f"st{p}"])
                    em.op("pool", lambda n=n, p=p: G.tensor_tensor(out=it_[p][:, 0:n], in0=it_[p][:, 0:n], in1=st[p][:, 0:n], op=ALU.mult),
                          R=[f"it{p}", f"st{p}"], W=[f"it{p}"])
                    em.op("dve", lambda n=n, lo=lo, hi=hi, p=p: V.tensor_tensor(out=it_[p][:, 0:n], in0=it_[p][:, 0:n], in1=uc[:, lo:hi],
                                                                               op=ALU.mult), R=[f"it{p}", "uc"], W=[f"it{p}"])
                    init = 0.0 if ti == 0 else carry[:, d:d + 1]
                    if d == 0:
                        em.op("dve", lambda n=n, lo=lo, hi=hi, init=init, p=p: V.tensor_tensor_scan(
                            out=bufB[:, lo:hi], data0=at[p][:, 0:n], data1=it_[p][:, 0:n], initial=init, op0=ALU.mult, op1=ALU.add),
                              R=[f"at{p}", f"it{p}", "carry", "bufB", "uc"], W=["bufB"])
                        em.op("dve", lambda hi=hi: V.tensor_copy(out=carry[:, 0:1], in_=bufB[:, hi - 1:hi]), R=["bufB"], W=["carry"])
                        gelu_tile(lo, hi, p)
                    else:
                        em.op("dve", lambda n=n, init=init, p=p: V.tensor_tensor_scan(
                            out=st[p][:, 0:n][:, ::-1], data0=at[p][:, 0:n][:, ::-1],
                            data1=it_[p][:, 0:n][:, ::-1], initial=init, op0=ALU.mult, op1=ALU.add),
                              R=[f"at{p}", f"it{p}", "carry", f"st{p}"], W=[f"st{p}"])
                        em.op("dve", lambda p=p: V.tensor_copy(out=carry[:, 1:2], in_=st[p][:, 0:1]), R=[f"st{p}"], W=["carry"])
                        em.op("pool", lambda n=n, lo=lo, hi=hi, p=p: G.tensor_tensor(out=bufB[:, lo:hi], in0=bufB[:, lo:hi], in1=st[p][:, 0:n],
                                                                                    op=ALU.add), R=["bufB", f"st{p}"], W=["bufB"])
            em.op("dve", lambda: V.tensor_tensor(out=ucb[:, 0:S].rearrange("p (a b) -> p a b", b=64),
                                                 in0=bufB[:, 0:S].rearrange("p (a b) -> p a b", b=64),
                                                 in1=bufA[:, 0:S].rearrange("p (b a) -> p a b", a=128), op=ALU.mult),
                  R=["bufB", "bufA", "ucb"], W=["ucb"])
            em.op("dve", lambda: V.tensor_tensor(out=ucb[:, S:NT], in0=bufB[:, S:NT], in1=bufA[:, S:NT], op=ALU.mult),
                  R=["bufB", "bufA", "ucb"], W=["ucb"])
            em.dma("pool", C.MIXT[c * 128:(c + 1) * 128, :], ucb[:, :], R=["ucb"], W=[("MIXT", c)])
            if c < 3:
                em.op("dve", lambda: V.memset(bufB[:, 0:2], 0.0), R=["bufB"], W=["bufB"])
                em.op("dve", lambda: V.memset(bufB[:, S:8460], 0.0), R=["bufB"], W=["bufB"])


def _tok_tiles(ntok_tile):
    return None


def phase_D1(C, layer=0):
    nc, em, PS = C.nc, C.em, C.PS
    V, A, G, T = nc.vector, nc.scalar, nc.gpsimd, nc.tensor
    with ExitStack() as ps_:
        sb = lambda n, s, d=F32: ps_.enter_context(nc.sbuf_tensor(n, list(s), d))
        wo = sb("D1wo", [128, 8, D], BF16)
        xt = [sb(f"D1xt{i}", [128, 4, D]) for i in range(2)]
        mt = [sb(f"D1mt{i}", [128, 8, 512], BF16) for i in range(2)]
        mb = [sb(f"D1mb{i}", [128, 8, 512], BF16) for i in range(2)]
        hs = sb("D1hs", [128, 2])
        Gx = sb("D1Gx", [128, D]); Gc = sb("D1Gc", [128, D])
        ssq = sb("D1ssq", [128, 4]); rs = sb("D1rs", [128, 4]); junk = sb("D1junk", [128, D], BF16); tt = sb("D1tt", [128, D])
        _wload(C, wo, C.woab_d.rearrange("(k p) n -> p k n", p=128), "D1wo", 8, D)
        em.dma("sp", hs[:], C.hsel_d, W=["D1hs"])
        em.dma("sp", Gx[:], C.GROW[0, 0, 0, :].partition_broadcast(128), R=[("GROW", 0, 0)], W=["D1Gx"])
        em.dma("sp", Gc[:], C.GROW[0, 0, 1, :].partition_broadcast(128), R=[("GROW", 0, 0)], W=["D1Gc"])
        mixv = C.MIXT.rearrange("(k p) t -> p k t", p=128)
        NTL = NLT + 1

        def load(j, sl):
            if j < NLT:
                em.dma("sp", xt[sl][:], C.xloc_d[j * 512:(j + 1) * 512, :].rearrange("(s p) d -> p s d", p=128), W=[f"D1xt{sl}"])
                em.dma("sp", mt[sl][:], mixv[:, :, j * 512:(j + 1) * 512], W=[f"D1mt{sl}"])
                em.dma("sp", mb[sl][:], mixv[:, :, LBASE + j * 512:LBASE + (j + 1) * 512], W=[f"D1mb{sl}"])
                em.op("act", lambda sl=sl: A.activation(out=mt[sl][:].rearrange("p a b -> p (a b)"),
                                                        in_=mt[sl][:].rearrange("p a b -> p (a b)"), func=AF.Copy,
                                                        scale=hs[:, 0:1]), R=[f"D1mt{sl}", "D1hs"], W=[f"D1mt{sl}"])
                em.op("dve", lambda sl=sl: V.scalar_tensor_tensor(
                    out=mt[sl][:].rearrange("p a b -> p (a b)"), in0=mb[sl][:].rearrange("p a b -> p (a b)"), scalar=hs[:, 1:2],
                    in1=mt[sl][:].rearrange("p a b -> p (a b)"), op0=ALU.mult, op1=ALU.add),
                      R=[f"D1mt{sl}", f"D1mb{sl}", "D1hs"], W=[f"D1mt{sl}"])
            else:
                em.dma("sp", xt[sl][:, 0:2, :], C.ctx_d.rearrange("(s p) d -> p s d", p=128), W=[f"D1xt{sl}"])
                em.dma("sp", mt[sl][:, :, 0:LC], mixv[:, :, S:NT], W=[f"D1mt{sl}"])
        load(0, 0)
        for j in range(NTL):
            sl = j % 2
            if j + 1 < NTL:
                load(j + 1, 1 - sl)
            isx = j < NLT
            nsub = 4 if isx else 2
            for s in range(nsub):
                pb = 2 * (s % 4)
                for h in range(2):
                    for k in range(8):
                        em.op("pe", lambda k=k, h=h, s=s, sl=sl, pb=pb: T.matmul(
                            PS[:, pb + h, :], lhsT=mt[sl][:, k, s * 128:(s + 1) * 128], rhs=wo[:, k, h * 512:(h + 1) * 512],
                            start=(k == 0), stop=(k == 7)), R=["D1wo", f"D1mt{sl}"], W=[("ps", pb + h)], inc=(k == 7))
                C.epilogue(pb, xt[sl][:, s, :], f"D1xt{sl}", (Gx if isx else Gc)[:, :], "D1Gx" if isx else "D1Gc",
                           ssq, rs, junk, tt, "D1")
            if isx:
                em.dma("pool", C.X1[j * 512:(j + 1) * 512, :].rearrange("(s p) d -> p s d", p=128), xt[sl][:],
                       R=[f"D1xt{sl}"], W=[("X1", j)])
            else:
                em.dma("pool", C.C1.rearrange("(s p) d -> p s d", p=128), xt[sl][:, 0:2, :], R=[f"D1xt{sl}"], W=["C1"])


def phase_FFN(C, layer, Xin, Cin, Xout, Cout, inkey, outkey):
    nc, em, PS = C.nc, C.em, C.PS
    V, A, G, T = nc.vector, nc.scalar, nc.gpsimd, nc.tensor
    tg = f"F{layer}"
    with ExitStack() as ps_:
        sb = lambda n, s, d=F32: ps_.enter_context(nc.sbuf_tensor(tg + n, list(s), d))
        wg = sb("wg", [128, 8, FH], BF16); wu = sb("wu", [128, 8, FH], BF16); wd = sb("wd", [128, NJ, D], BF16)
        xt = [sb(f"xt{i}", [128, 2, D]) for i in range(2)]
        xs = sb("xs", [128, 2, D]); hT = sb("hT", [128, 8, 256], BF16); hh = sb("hh", [128, NJ, 256], BF16)
        sg = [sb(f"sg{i}", [128, 256]) for i in range(2)]
        Gx = sb("Gx", [128, D]); Gc = sb("Gc", [128, D])
        ssq = sb("ssq", [128, 4]); rs = sb("rs", [128, 4]); junk = sb("junk", [128, D], BF16); tt = sb("tt", [128, D])
        _wload(C, wg, C.wg_d[layer].rearrange("(k p) n -> p k n", p=128), tg + "wg", 8, FH, 704)
        _wload(C, wu, C.wu_d[layer].rearrange("(k p) n -> p k n", p=128), tg + "wu", 8, FH, 704)
        _wload(C, wd, C.wd_d[layer].rearrange("(k p) n -> p k n", p=128), tg + "wd", NJ, D, 512)
        em.dma("sp", Gx[:], C.GROW[layer, 1, 0, :].partition_broadcast(128), R=[("GROW", layer, 1)], W=[tg + "Gx"])
        em.dma("sp", Gc[:], C.GROW[layer, 1, 1, :].partition_broadcast(128), R=[("GROW", layer, 1)], W=[tg + "Gc"])
        NX = NL // 256
        ntile = NX + (1 if Cin is not None else 0)

        def load(j, sl):
            if j < NX:
                em.dma("sp", xt[sl][:], Xin[j * 256:(j + 1) * 256, :].rearrange("(s p) d -> p s d", p=128),
                       R=[(inkey, j // 2)], W=[tg + f"xt{sl}"])
            else:
                em.dma("sp", xt[sl][:], Cin.rearrange("(s p) d -> p s d", p=128), R=[inkey + "c"], W=[tg + f"xt{sl}"])
        ssq2 = sb("ssq2", [128, 4]); rs2 = sb("rs2", [128, 4]); junk2 = sb("junk2", [128, D], BF16)

        def pro_a(j):
            sl = j % 2
            C.prologue_a(xt[sl], tg + f"xt{sl}", 2, xs, tg + "xs", ssq2, rs2, junk2, tg + "p")

        def pro_b(j):
            path = 0 if j < NX else 1
            C.prologue_b(2, xs, tg + "xs", C.MODC[:, layer, 1, 0, path, :], C.MODC[:, layer, 1, 1, path, :], hT, tg + "hT", [7])

        def gateup(j):
            for jj in range(NJ):
                b = jj % 3
                for gi, w_ in enumerate((wg, wu)):
                    for k in range(8):
                        em.op("pe", lambda k=k, b=b, gi=gi, w_=w_, jj=jj: T.matmul(
                            PS[:, b, gi * 256:(gi + 1) * 256], lhsT=w_[:, k, jj * 128:(jj + 1) * 128], rhs=hT[:, k, :],
                            start=(k == 0), stop=(k == 7)), R=[tg + "wg", tg + "wu", tg + "hT"], W=[("ps", b)],
                              inc=(k == 7 and gi == 1))
                s2 = jj % 2
                em.op("act", lambda b=b, s2=s2: A.activation(out=sg[s2][:, :], in_=PS[:, b, 0:256], func=AF.Silu),
                      R=[("ps", b)], W=[tg + f"sg{s2}"])
                em.op("dve", lambda b=b, s2=s2, jj=jj: V.tensor_tensor(out=hh[:, jj, :], in0=sg[s2][:, :], in1=PS[:, b, 256:512],
                                                                      op=ALU.mult), R=[("ps", b), tg + f"sg{s2}"], W=[tg + "hh"])

        def down(j):
            for s in range(2):
                pb = 3 + 2 * s
                for h in range(2):
                    for jj in range(NJ):
                        em.op("pe", lambda jj=jj, h=h, s=s, pb=pb: T.matmul(
                            PS[:, pb + h, :], lhsT=hh[:, jj, s * 128:(s + 1) * 128], rhs=wd[:, jj, h * 512:(h + 1) * 512],
                            start=(jj == 0), stop=(jj == NJ - 1)), R=[tg + "wd", tg + "hh"], W=[("ps", pb + h)], inc=(jj == NJ - 1))

        def epi(j):
            sl = j % 2
            path = 0 if j < NX else 1
            for s in range(2):
                pb = 3 + 2 * s
                C.epilogue(pb, xt[sl][:, s, :], tg + f"xt{sl}", (Gx if path == 0 else Gc)[:, :], tg + ("Gx" if path == 0 else "Gc"),
                           ssq, rs, junk, tt, tg)
            if j < NX:
                em.dma("pool", Xout[j * 256:(j + 1) * 256, :].rearrange("(s p) d -> p s d", p=128), xt[sl][:],
                       R=[tg + f"xt{sl}"], W=[(outkey, j // 2)] if j % 2 else [(outkey + "h", j)])
            else:
                em.dma("pool", Cout.rearrange("(s p) d -> p s d", p=128), xt[sl][:], R=[tg + f"xt{sl}"], W=[outkey + "c"])

        load(0, 0)
        pro_a(0)
        pro_b(0)
        for j in range(ntile):
            sl = j % 2
            gateup(j)
            if j + 1 < ntile:
                load(j + 1, 1 - sl)
                pro_a(j + 1)
            down(j)
            if j + 1 < ntile:
                pro_b(j + 1)
            epi(j)


PHASES += [("B", phase_B), ("D1", phase_D1),
           ("D2", lambda C: phase_FFN(C, 0, C.X1, C.C1, C.X2, C.C2, "X1", "X2"))]


def phase_E(C):
    nc, em, PS = C.nc, C.em, C.PS
    V, A, G, T = nc.vector, nc.scalar, nc.gpsimd, nc.tensor
    with ExitStack() as ps_:
        sb = lambda n, s, d=F32: ps_.enter_context(nc.sbuf_tensor("E" + n, list(s), d))
        wq = sb("wq", [128, 8, 3 * D], BF16)
        xt = [sb(f"xt{i}", [128, 4, D]) for i in range(2)]
        hT = sb("hT", [128, 8, 512], BF16)
        qk = [sb(f"qk{i}", [128, 16, 512], BF16) for i in range(2)]
        vst = [sb(f"vst{i}", [128, 4, D], BF16) for i in range(2)]
        ssq = sb("ssq", [128, 4]); rs = sb("rs", [128, 4]); junk = sb("junk", [128, D], BF16)
        _wload(C, wq, C.wqkv_d.rearrange("(k p) n -> p k n", p=128), "Ewq", 8, 3 * D)
        QTv = C.QT.rearrange("(c p) t -> p c t", p=128); KTv = C.KT.rearrange("(c p) t -> p c t", p=128)

        def load(j, sl):
            if j < NLT:
                em.dma("sp", xt[sl][:], C.X2[j * 512:(j + 1) * 512, :].rearrange("(s p) d -> p s d", p=128), W=[f"Ext{sl}"])
            else:
                em.dma("sp", xt[sl][:, 0:2, :], C.C2.rearrange("(s p) d -> p s d", p=128), W=[f"Ext{sl}"])
        load(0, 0)
        ev = 0
        for j in range(NLT + 1):
            sl = j % 2
            if j + 1 < NLT + 1:
                load(j + 1, 1 - sl)
            isctx = j == NLT
            nsub = 2 if isctx else 4
            n = nsub * 128
            path = 1 if isctx else 0
            C.prologue(xt[sl], f"Ext{sl}", nsub, xt[sl], f"Ext{sl}", C.MODC[:, 1, 0, 0, path, :], C.MODC[:, 1, 0, 1, path, :],
                       hT, "EhT", ssq, rs, junk, "E", [0, 1])
            for oc in range(8 if isctx else 0, 16) if isctx else range(16):
                b = 2 + oc % 2
                for k in range(8):
                    em.op("pe", lambda k=k, b=b, oc=oc, n=n: T.matmul(
                        PS[:, b, 0:n], lhsT=wq[:, k, oc * 128:(oc + 1) * 128], rhs=hT[:, k, 0:n],
                        start=(k == 0), stop=(k == 7)), R=["Ewq", "EhT"], W=[("ps", b)], inc=(k == 7))
                if oc < 8:
                    em.op("act", lambda b=b, oc=oc, sl=sl, n=n: A.activation(out=qk[sl][:, oc, 0:n], in_=PS[:, b, 0:n],
                                                                             func=AF.Copy, scale=0.125),
                          R=[("ps", b)], W=[f"Eqk{sl}"])
                else:
                    em.op("dve", lambda b=b, oc=oc, sl=sl, n=n: V.tensor_copy(out=qk[sl][:, oc, 0:n], in_=PS[:, b, 0:n]),
                          R=[("ps", b)], W=[f"Eqk{sl}"])
            if not isctx:
                em.dma("pool", QTv[:, :, j * 512:(j + 1) * 512], qk[sl][:, 0:8, :], R=[f"Eqk{sl}"], W=[("QT", j)])
                em.dma("pool", KTv[:, :, j * 512:(j + 1) * 512], qk[sl][:, 8:16, :], R=[f"Eqk{sl}"], W=[("KT", j)])
            else:
                em.dma("pool", KTv[:, :, NL:NL + LC], qk[sl][:, 8:16, 0:LC], R=[f"Eqk{sl}"], W=[("KT", j)])
            for s in range(nsub):
                pb = 4 + 2 * (s % 2)
                for h in range(2):
                    for k in range(8):
                        em.op("pe", lambda k=k, h=h, s=s, pb=pb: T.matmul(
                            PS[:, pb + h, :], lhsT=hT[:, k, s * 128:(s + 1) * 128], rhs=wq[:, k, 2048 + h * 512:2048 + (h + 1) * 512],
                            start=(k == 0), stop=(k == 7)), R=["Ewq", "EhT"], W=[("ps", pb + h)], inc=(k == 7))
                ev += 1
                src = PS[:, pb:pb + 2, :].rearrange("p a b -> p (a b)")
                if ev % 2:
                    em.op("dve", lambda s=s, sl=sl, src=src: V.tensor_copy(out=vst[sl][:, s, :], in_=src),
                          R=[("ps", pb), ("ps", pb + 1)], W=[f"Evst{sl}"])
                else:
                    em.op("act", lambda s=s, sl=sl, src=src: A.copy(out=vst[sl][:, s, :], in_=src),
                          R=[("ps", pb), ("ps", pb + 1)], W=[f"Evst{sl}"])
            t0 = NL if isctx else j * 512
            em.dma("pool", C.VD[t0:t0 + n, :].rearrange("(s p) d -> p s d", p=128), vst[sl][:, 0:nsub, :],
                   R=[f"Evst{sl}"], W=[("VD", j)])


def phase_F(C):
    nc, em, PS = C.nc, C.em, C.PS
    V, A, G, T = nc.vector, nc.scalar, nc.gpsimd, nc.tensor
    with ExitStack() as ps_:
        sb = lambda n, s, d=F32: ps_.enter_context(nc.sbuf_tensor("AT" + n, list(s), d))
        wo = sb("wo", [128, 8, D], BF16)
        KTb = [sb(f"KTb{i}", [128, 8, 1024], BF16) for i in range(2)]
        Vb = [sb(f"Vb{i}", [128, 8, 16, 65], BF16) for i in range(2)]
        QTb = sb("QTb", [128, 8, 512], BF16); xt = sb("xt", [128, 4, D])
        KTc = sb("KTc", [128, 8, LC], BF16); Vc = sb("Vc", [128, 2, 16, 65], BF16)
        BTi = sb("BTi", [128, 16, 5, 128], BF16); BTe = sb("BTe", [128, 16, 8, 128], BF16)
        PT = [sb(f"PT{i}", [128, 1280], BF16) for i in range(2)]
        Ot = sb("Ot", [128, D]); OTt = sb("OTt", [128, 8, 128], BF16); rden = sb("rden", [128, 4])
        Gx = sb("Gx", [128, D]); ssq = sb("ssq", [128, 4]); rs = sb("rs", [128, 4]); junk = sb("junk", [128, D], BF16)
        tt = sb("tt", [128, D])
        _wload(C, wo, C.wona_d.rearrange("(k p) n -> p k n", p=128), "Awo", 8, D)
        em.dma("sp", Gx[:], C.GROW[1, 0, 0, :].partition_broadcast(128), W=["AGx"])
        em.dma("sp", BTi[:], C.BT_d, W=["ABTi"])
        KTv = C.KT.rearrange("(c p) t -> p c t", p=128); QTv = C.QT.rearrange("(c p) t -> p c t", p=128)
        em.dma("sp", KTc[:], KTv[:, :, NL:NL + LC], W=["AKTc"])
        for i in range(2):
            em.op("pool", lambda i=i: G.memset(Vb[i][:, :, :, 64:65], 1.0), W=[f"AVb{i}"])
        em.op("pool", lambda: G.memset(Vc[:, :, :, 64:65], 1.0), W=["AVc"])
        for c in range(2):
            em.dma("sp", Vc[:, c, :, 0:64], C.VD[NL + c * 128:NL + (c + 1) * 128, :].rearrange("p (h d) -> p h d", d=64), W=["AVc"])

        def kbof(blk):
            return 0 if blk == 0 else (56 if blk == NLT - 1 else 8 * blk - 4)

        def load(blk, sl):
            kb = kbof(blk)
            em.dma("sp", KTb[sl][:], KTv[:, :, kb * 64:kb * 64 + 1024], W=[f"AKTb{sl}"])
            for c in range(8):
                t0 = kb * 64 + c * 128
                em.dma("sp", Vb[sl][:, c, :, 0:64], C.VD[t0:t0 + 128, :].rearrange("p (h d) -> p h d", d=64), W=[f"AVb{sl}"])
        load(0, 0)
        for blk in range(NLT):
            sl = blk % 2
            r0 = 8 * blk
            edge = blk in (0, NLT - 1)
            eb = 0 if blk == 0 else 1
            nloc = 8 if edge else 5
            nch = nloc + 2
            em.dma("sp", QTb[:], QTv[:, :, r0 * 64:r0 * 64 + 512], W=["AQTb"])
            em.dma("sp", xt[:], C.X2[blk * 512:(blk + 1) * 512, :].rearrange("(s p) d -> p s d", p=128), W=["Axt"])
            if blk + 1 < NLT:
                load(blk + 1, 1 - sl)

            def sbase(h):
                return 0 if edge else 2 * (h % 2)

            def cpos(cl, h):
                if edge:
                    return (cl // 4, (cl % 4) * 128)
                return (2 * (h % 2) + cl // 4, (cl % 4) * 128)

            def qk(i, h):
                off = 0 if edge else i
                if edge and h == 0:
                    em.dma("sp", BTe[:], C.BTE_d[eb, i], W=["ABTe"])
                j = h // 2; e = h % 2
                p0, p1 = 64 * e, 64 * e + 64
                for cl in range(nch):
                    bk, co = cpos(cl, h)
                    o = PS[:, bk, co:co + 128]
                    if cl < nloc:
                        lt = KTb[sl][p0:p1, j, (off + cl) * 128:(off + cl + 1) * 128]
                    else:
                        lt = KTc[p0:p1, j, (cl - nloc) * 128:(cl - nloc + 1) * 128]
                    last = cl == nch - 1
                    em.op("pe", lambda o=o, lt=lt, j=j, p0=p0, p1=p1, i=i, cl=cl, last=last: T.matmul(
                        o, lhsT=lt, rhs=QTb[p0:p1, j, i * 128:(i + 1) * 128], start=(cl % 4 == 0),
                        stop=(edge and last)), R=[f"AKTb{sl}", "AKTc", "AQTb"], W=[("ps", bk)], inc=(edge and last))
                    if edge:
                        if cl in (3, 7):
                            em.op("pe", lambda h=h, bk=bk, cl=cl: T.matmul(
                                PS[:, bk, :], lhsT=C.identb[:, :], rhs=BTe[:, h, cl - 3:cl + 1, :].rearrange("p a b -> p (a b)"),
                                start=False, stop=True), R=["identb", "ABTe"], W=[("ps", bk)], inc=False)
                    else:
                        if cl == 3:
                            em.op("pe", lambda h=h, bk=bk: T.matmul(
                                PS[:, bk, :], lhsT=C.identb[:, :], rhs=BTi[:, h, 0:4, :].rearrange("p a b -> p (a b)"),
                                start=False, stop=True), R=["identb", "ABTi"], W=[("ps", bk)], inc=False)
                        if cl == 6:
                            em.op("pe", lambda h=h, bk=bk: T.matmul(
                                PS[:, bk, 0:128], lhsT=C.identb[:, :], rhs=BTi[:, h, 4, :],
                                start=False, stop=True), R=["identb", "ABTi"], W=[("ps", bk)], inc=True)

            def ex(i, h):
                b0 = sbase(h)
                nb = 3 if edge else 2
                src = PS[:, b0:b0 + nb, :].rearrange("p a b -> p (a b)")[:, 0:nch * 128]
                em.op("act", lambda src=src, h=h: A.activation(out=PT[h % 2][:, 0:nch * 128], in_=src, func=AF.Exp),
                      R=[("ps", b0 + q) for q in range(nb)], W=[f"APT{h % 2}"])

            def pv(i, h):
                off = 0 if edge else i
                ob = 4 + (h // 4) % 2
                so = (h % 4) * 128
                for c in range(nch):
                    rhs = Vb[sl][:, off + c, h, :] if c < nloc else Vc[:, c - nloc, h, :]
                    em.op("pe", lambda c=c, rhs=rhs, h=h, ob=ob, so=so: T.matmul(
                        PS[:, ob, so:so + 65], lhsT=PT[h % 2][:, c * 128:(c + 1) * 128], rhs=rhs,
                        start=(c == 0), stop=(c == nch - 1)), R=[f"APT{h % 2}", f"AVb{sl}", "AVc"], W=[("ps", ob)],
                          inc=(c == nch - 1))
                if h % 4 == 3:
                    em.op("dve", lambda ob=ob: V.reciprocal(out=rden[:, 0:4], in_=PS[:, ob, 64:512:128]),
                          R=[("ps", ob)], W=["Arden"])
                    for hh in range(4):
                        hd = h - 3 + hh
                        em.op("dve", lambda ob=ob, hh=hh, hd=hd: V.tensor_scalar(
                            out=Ot[:, hd * 64:(hd + 1) * 64], in0=PS[:, ob, hh * 128:hh * 128 + 64],
                            scalar1=rden[:, hh:hh + 1], scalar2=None, op0=ALU.mult), R=[("ps", ob), "Arden"], W=["AOt"])

            def fin(i):
                for k in range(8):
                    em.op("pe", lambda k=k: T.transpose(PS[:, 6 + k // 4, (k % 4) * 128:(k % 4 + 1) * 128],
                                                        Ot[:, k * 128:(k + 1) * 128], C.ident[:]),
                          R=["AOt", "ident"], W=[("ps", 6 + k // 4)], inc=(k % 4 == 3))
                for hb in range(2):
                    em.op("act" if hb else "dve",
                          (lambda hb=hb: A.copy(out=OTt[:, 4 * hb:4 * hb + 4, :].rearrange("p a b -> p (a b)"), in_=PS[:, 6 + hb, :])) if hb else
                          (lambda hb=hb: V.tensor_copy(out=OTt[:, 4 * hb:4 * hb + 4, :].rearrange("p a b -> p (a b)"), in_=PS[:, 6 + hb, :])),
                          R=[("ps", 6 + hb)], W=["AOTt"])
                for hf in range(2):
                    for k in range(8):
                        em.op("pe", lambda k=k, hf=hf: T.matmul(PS[:, 6 + hf, :], lhsT=OTt[:, k, :], rhs=wo[:, k, hf * 512:(hf + 1) * 512],
                                                                start=(k == 0), stop=(k == 7)),
                              R=["AOTt", "Awo"], W=[("ps", 6 + hf)], inc=(k == 7))
                C.epilogue(6, xt[:, i, :], "Axt", Gx[:, :], "AGx", ssq, rs, junk, tt, "A")

            items = [(i, h) for i in range(4) for h in range(16)]
            qk(*items[0])
            for n_, (i, h) in enumerate(items):
                nxt = items[n_ + 1] if n_ + 1 < len(items) else None
                if edge:
                    ex(i, h)
                    if nxt:
                        qk(*nxt)
                else:
                    if nxt:
                        qk(*nxt)
                    ex(i, h)
                pv(i, h)
                if h == 15:
                    fin(i)
            em.dma("pool", C.X3[blk * 512:(blk + 1) * 512, :].rearrange("(s p) d -> p s d", p=128), xt[:], R=["Axt"], W=[("X3", blk)])


PHASES += [("E", phase_E), ("F", phase_F),
           ("G", lambda C: phase_FFN(C, 1, C.X3, None, C.out_d, None, "X3", "OUT"))]
```

```python
import math
from contextlib import ExitStack
import numpy as np
import ml_dtypes
import concourse.bass as bass
import concourse.mybir as mybir
from concourse.bass_utils import run_bass_kernel_spmd

F32 = mybir.dt.float32
BF16 = mybir.dt.bfloat16
AF = mybir.ActivationFunctionType
ALU = mybir.AluOpType
NPBF = ml_dtypes.bfloat16

D = 1024
S = 8192
LC = 256
NT = S + LC
FH = 2816
NJ = FH // 128
EPS = 1e-6
NCORES = 8
NL = 4608
NLT = NL // 512
LBASE = 3584
NEG = -30000.0
QENG = ("pool", "dve", "pool", "dve")


class Em:
    LIMIT = 30000
    NDS = 8

    def __init__(self, nc, es):
        self.nc = nc
        self.es = es
        self.eng = dict(pe=nc.tensor, act=nc.scalar, dve=nc.vector, pool=nc.gpsimd, sp=nc.sync)
        self.sem = {}
        self.semkey = {}
        self.cnt = {}
        self.nsem = 0
        for e in self.eng:
            self._newsem(e)
        self.waited = {e: {} for e in self.eng}
        self.lastw = {}
        self.readers = {}
        self.dsem = {}
        self.ndma = {}
        for q in ("sp", "pool", "act"):
            self.dsem[q] = [es.enter_context(nc.semaphore(f"d{q}{i}")) for i in range(self.NDS)]
            self.ndma[q] = 0
        self.bg = []

    def _newsem(self, e):
        self.nsem += 1
        self.sem[e] = self.es.enter_context(self.nc.semaphore(f"s{e}{self.nsem}"))
        self.semkey[e] = (e, self.nsem)
        self.cnt[e] = 0

    def _deps(self, engine, R, W):
        deps = {}
        for k in list(R) + list(W):
            t = self.lastw.get(k)
            if t is not None:
                if t[0] not in deps or deps[t[0]][2] < t[2]:
                    deps[t[0]] = t
        for k in W:
            for t in self.readers.get(k, {}).values():
                if t[0] not in deps or deps[t[0]][2] < t[2]:
                    deps[t[0]] = t
        e = self.eng[engine]
        for sk, t in deps.items():
            if engine == "pe" and t[3] == "pe":
                continue
            if self.waited[engine].get(sk, 0) >= t[2]:
                continue
            e.wait_ge(t[1], t[2])
            self.waited[engine][sk] = t[2]

    def _record(self, tok, R, W):
        for k in W:
            self.lastw[k] = tok
            self.readers[k] = {}
        for k in R:
            d = self.readers.setdefault(k, {})
            if tok[0] not in d or d[tok[0]][2] < tok[2]:
                d[tok[0]] = tok

    def op(self, engine, fn, R=(), W=(), inc=True):
        self._deps(engine, R, W)
        ins = fn()
        if inc:
            ins.then_inc(self.sem[engine], 1)
            self.cnt[engine] += 1
            tok = (self.semkey[engine], self.sem[engine], self.cnt[engine], engine)
            self._record(tok, R, W)
            if self.cnt[engine] >= self.LIMIT:
                self._newsem(engine)
        else:
            tok = (self.semkey[engine], self.sem[engine], self.cnt[engine] + 1, engine)
            self._record(tok, R, W)
        return ins

    def dma(self, q, out, in_, R=(), W=(), **kw):
        self._deps(q, R, W)
        i = self.ndma[q]
        self.ndma[q] += 1
        sem = self.dsem[q][i % self.NDS]
        rnd = i // self.NDS
        sk = ("dma", q, i % self.NDS)
        if rnd > 0 and self.waited[q].get(sk, 0) < 16 * rnd:
            self.eng[q].wait_ge(sem, 16 * rnd)
            self.waited[q][sk] = 16 * rnd
        self.eng[q].dma_start(out=out, in_=in_, **kw).then_inc(sem, 16)
        tok = (sk, sem, 16 * (rnd + 1), None)
        self._record(tok, R, W)

    def dma_bg(self, q, out, in_, R=(), W=(), **kw):
        self._deps(q, R, W)
        sem = self.es.enter_context(self.nc.semaphore(f"bg{len(self.bg)}"))
        self.bg.append(sem)
        self.eng[q].dma_start(out=out, in_=in_, **kw).then_inc(sem, 16)
        self._record((("bg", len(self.bg)), sem, 16, None), R, W)

    def finish(self):
        sp = self.eng["sp"]
        for sem in self.bg:
            sp.wait_ge(sem, 16)
        for q in self.dsem:
            n = self.ndma[q]
            for s in range(self.NDS):
                uses = (n - s + self.NDS - 1) // self.NDS if n > s else 0
                if uses > 0:
                    sp.wait_ge(self.dsem[q][s], 16 * uses)
        for e in ("pe", "act", "dve", "pool"):
            if self.cnt[e] > 0:
                sp.wait_ge(self.sem[e], self.cnt[e])


def _tables():
    t = {}
    t["ident"] = np.eye(128, dtype=np.float32)
    t["identb"] = np.eye(128, dtype=np.float32).astype(NPBF)
    a = np.arange(128, dtype=np.float64)
    ang = 2 * np.pi * np.outer(a, a) / 128.0
    t["T1"] = np.concatenate([np.cos(ang), -np.sin(ang)], axis=1).astype(NPBF)
    t2 = np.arange(64, dtype=np.float64)[:, None, None]
    k1 = np.arange(128, dtype=np.float64)[None, :, None]
    k2 = np.arange(64, dtype=np.float64)[None, None, :]
    ph = 2 * np.pi * (t2 * k2 / 64.0 + t2 * k1 / 8192.0)
    Mc, Ms = np.cos(ph), np.sin(ph)
    t["M2"] = np.concatenate([Ms, Mc, -Ms], axis=2).astype(NPBF)
    c = np.arange(64, dtype=np.float64)
    angc = 2 * np.pi * np.outer(c, c) / 64.0
    Cc, Sc = np.cos(angc), np.sin(angc)
    z = np.zeros((64, 64))
    Cbd = np.block([[Cc, z], [z, Cc]])
    Sbd = np.block([[Sc, z], [z, Sc]])
    sx = 1.0 / math.sqrt(8192.0 * 64.0)
    t["CSf"] = (np.concatenate([Cbd, Sbd], axis=1) * sx).astype(NPBF)
    sc_ = 1.0 / math.sqrt(256.0 * 64.0)
    t["CSc"] = (np.concatenate([Cbd, Sbd], axis=1) * sc_).astype(NPBF)
    p = np.arange(256, dtype=np.float64)
    angp = 2 * np.pi * np.outer(p, p) / 256.0
    T256 = np.concatenate([np.cos(angp), -np.sin(angp)], axis=1)
    t["T256"] = T256.reshape(2, 128, 512).transpose(1, 0, 2).copy().astype(NPBF)
    return t


_TABLES = None


def _bias_tables(rpb):
    H = 16
    out = np.full((5, 5 * 128, H, 128), NEG, dtype=np.float32)
    kr = np.arange(10)[:, None, None, None]
    kc = np.arange(64)[None, :, None, None]
    qr = np.arange(2)[None, None, :, None]
    qc = np.arange(64)[None, None, None, :]
    cs = np.clip(qc - 8, 0, 48)
    for vi, (gp, ks) in enumerate([(0, 0), (2, 0), (60, 56), (124, 118), (126, 118)]):
        gq = gp + qr
        rs = np.clip(gq - 4, 0, 120)
        gk = ks + kr
        valid = (gk >= rs) & (gk < rs + 8) & (kc >= cs) & (kc < cs + 16)
        dr = np.clip(gk - gq + 7, 0, 14)
        dc = np.clip(kc - qc + 15, 0, 30)
        valid, dr, dc = np.broadcast_arrays(valid, dr, dc)
        vals = rpb[:, dr, dc]
        vals = np.where(valid[None], vals, NEG)
        out[vi] = vals.transpose(1, 2, 0, 3, 4).reshape(640, H, 128)
    bt = out.reshape(5, 5, 128, H, 128).transpose(0, 2, 3, 1, 4)
    return np.ascontiguousarray(bt).astype(NPBF)


def _edge_tables(rpb, h):
    H = 16
    out = np.empty((2, 4, 128, H, 8, 128), dtype=NPBF)
    kr = np.arange(16)[:, None, None, None]
    kc = np.arange(64)[None, :, None, None]
    qr = np.arange(2)[None, None, :, None]
    qc = np.arange(64)[None, None, None, :]
    cs = np.clip(qc - 8, 0, 48)
    for eb, (r0, kb) in enumerate([(0, 0), (64, 56)]):
        for i in range(4):
            gq = r0 + 2 * i + qr + 56 * h
            gk = kb + kr + 56 * h
            rs = np.clip(gq - 4, 0, 120)
            valid = (gk >= rs) & (gk < rs + 8) & (kc >= cs) & (kc < cs + 16)
            dr = np.clip(gk - gq + 7, 0, 14)
            dc = np.clip(kc - qc + 15, 0, 30)
            valid, dr, dc = np.broadcast_arrays(valid, dr, dc)
            vals = np.where(valid[None], rpb[:, dr, dc], NEG)
            t = vals.transpose(1, 2, 0, 3, 4).reshape(8, 128, H, 128)
            out[eb, i] = t.transpose(1, 2, 0, 3).astype(NPBF)
    return out


def _col(v, nchunk):
    return np.ascontiguousarray(np.asarray(v, np.float32).reshape(nchunk, 128).T)


def build(stop="all", debug=False):
    nc = bass.Bass("TRN2", target_bir_lowering=False)

    def din(name, shape, dt=F32):
        return nc.dram_tensor(name, list(shape), dt, kind="ExternalInput").ap()

    skind = "ExternalOutput" if debug else "Internal"

    def dscr(name, shape, dt):
        return nc.dram_tensor(name, list(shape), dt, kind=skind).ap()

    x_d = din("x", [S, D]); xloc_d = din("xloc", [NL, D]); hsel_d = din("hsel", [128, 2]); ctx_d = din("ctx", [LC, D]); ccols_d = din("ccols", [128, 16])
    wmod_d = din("w_mod", [2, D, 6 * D]); bmodc_d = din("bmodc", [128, 2, 48]); bmod_d = din("b_mod", [2, 6 * D])
    gcols_d = din("gcols", [128, 2, 2, 8]); gpm_d = din("g_post_mix", [2, D]); gpf_d = din("g_post_ffn", [2, D])
    win_d = din("w_in_ab", [D, 1536]); woab_d = din("w_out_ab", [D, D])
    wg_d = din("w_ffn_gate", [2, D, FH]); wu_d = din("w_ffn_up", [2, D, FH]); wd_d = din("w_ffn_down", [2, FH, D])
    wqkv_d = din("w_qkv_na", [D, 3 * D]); wona_d = din("w_out_na", [D, D])
    wbd_d = din("wbd", [2, 2, 4, 128, 128]); lcols_d = din("lcols", [128, 4, 2, 3]); convc_d = din("convc", [128, 4, 5]); cdiag_d = din("cdiag", [4, 4, 128, 128])
    ident_d = din("ident", [128, 128]); identb_d = din("identb", [128, 128], BF16)
    T1_d = din("T1", [128, 256], BF16); M2_d = din("M2", [64, 128, 192], BF16); CSf_d = din("CSf", [128, 256], BF16)
    CSc_d = din("CSc", [128, 256], BF16); T256_d = din("T256", [128, 2, 512], BF16)
    BT_d = din("BT", [128, 16, 5, 128], BF16); BTE_d = din("BTE", [2, 4, 128, 16, 8, 128], BF16)
    out_d = nc.dram_tensor("out", [NL, D], F32, kind="ExternalOutput").ap()

    UG = dscr("UG", [8, 128, S], F32)
    UGc = dscr("UGc", [8, 128, LC], F32)
    MIXT = dscr("MIXT", [D, NT], BF16)
    GROW = dscr("GROW", [2, 2, 2, D], F32)
    X1 = dscr("X1", [NL, D], F32); C1 = dscr("C1", [LC, D], F32)
    X2 = dscr("X2", [NL, D], F32); C2 = dscr("C2", [LC, D], F32)
    X3 = dscr("X3", [NL, D], F32)
    QT = dscr("QT", [D, NL], BF16); KT = dscr("KT", [D, NL + LC], BF16); VD = dscr("VD", [NL + LC, D], BF16)

    WBF = {"woab": dscr("woab_bf", [D, D], BF16), "wona": dscr("wona_bf", [D, D], BF16),
           "wqkv": dscr("wqkv_bf", [D, 3 * D], BF16),
           "wg": dscr("wg_bf", [2, D, FH], BF16), "wu": dscr("wu_bf", [2, D, FH], BF16), "wd": dscr("wd_bf", [2, FH, D], BF16)}

    with ExitStack() as es:
        em = Em(nc, es)

        def barrier():
            for e in ("pe", "act", "dve", "pool", "sp"):
                eng = em.eng[e]
                for f in ("pe", "act", "dve", "pool"):
                    if f != e and em.cnt[f] > 0 and em.waited[e].get(em.semkey[f], 0) < em.cnt[f]:
                        eng.wait_ge(em.sem[f], em.cnt[f]); em.waited[e][em.semkey[f]] = em.cnt[f]
                for q in em.dsem:
                    n = em.ndma[q]
                    for s_ in range(em.NDS):
                        uses = (n - s_ + em.NDS - 1) // em.NDS if n > s_ else 0
                        sk = ("dma", q, s_)
                        if uses > 0 and em.waited[e].get(sk, 0) < 16 * uses:
                            eng.wait_ge(em.dsem[q][s_], 16 * uses); em.waited[e][sk] = 16 * uses

        PS = es.enter_context(nc.psum_tensor("PS", [128, 8, 512], F32))
        ident = es.enter_context(nc.sbuf_tensor("ident_s", [128, 128], F32))
        identb = es.enter_context(nc.sbuf_tensor("identb_s", [128, 128], BF16))
        MODC = es.enter_context(nc.sbuf_tensor("MODC", [128, 2, 2, 2, 2, 8], F32))
        mhalf = es.enter_context(nc.sbuf_tensor("mhalf", [128, 8], F32))
        em.dma("sp", ident[:], ident_d, W=["ident"])
        em.dma("sp", identb[:], identb_d, W=["identb"])
        em.op("dve", lambda: nc.vector.memset(mhalf[:], -0.5), W=["mhalf"])

        V = nc.vector; A = nc.scalar; G = nc.gpsimd; T = nc.tensor

        def pbank(b):
            return ("ps", b)

        def rstd_from_ssq(ssq, rs, n, tag):
            em.op("dve", lambda: V.tensor_scalar(out=rs[:, 0:n], in0=ssq[:, 0:n], scalar1=1.0 / D, scalar2=EPS,
                                                 op0=ALU.mult, op1=ALU.add), R=[tag + "ssq"], W=[tag + "rs"])
            em.op("pool", lambda: G.tensor_tensor(out=rs[:, 0:n], in0=rs[:, 0:n], in1=mhalf[:, 0:n], op=ALU.pow),
                  R=[tag + "rs", "mhalf"], W=[tag + "rs"])

        def prologue_a(xt, xkey, nsub, xs, xskey, ssq, rs, junk, tag):
            for s in range(nsub):
                em.op("act", lambda s=s: A.activation(out=junk[:, :], in_=xt[:, s, :], func=AF.Square,
                                                      accum_out=ssq[:, s:s + 1]),
                      R=[xkey], W=[tag + "junk", tag + "ssq"])
            rstd_from_ssq(ssq, rs, nsub, tag)
            for s in range(nsub):
                em.op("act", lambda s=s: A.activation(out=xs[:, s, :], in_=xt[:, s, :], func=AF.Copy,
                                                      scale=rs[:, s:s + 1]),
                      R=[xkey, tag + "rs"], W=[xskey])

        def prologue_b(nsub, xs, xskey, Acol, Bcol, hT, hkey, tb):
            for k in range(8):
                b = tb[k % len(tb)]
                for s in range(nsub):
                    em.op("pe", lambda s=s, k=k, b=b: T.transpose(PS[:, b, s * 128:(s + 1) * 128],
                                                                  xs[:, s, k * 128:(k + 1) * 128], ident[:]),
                          R=[xskey, "ident"], W=[pbank(b)], inc=(s == nsub - 1))
                em.op("act", lambda k=k, b=b: A.activation(out=hT[:, k, 0:nsub * 128], in_=PS[:, b, 0:nsub * 128],
                                                           func=AF.Identity, scale=Acol[:, k:k + 1],
                                                           bias=Bcol[:, k:k + 1]),
                      R=[pbank(b), "MODC"], W=[hkey])

        def prologue(xt, xkey, nsub, xs, xskey, Acol, Bcol, hT, hkey, ssq, rs, junk, tag, tb):
            prologue_a(xt, xkey, nsub, xs, xskey, ssq, rs, junk, tag)
            prologue_b(nsub, xs, xskey, Acol, Bcol, hT, hkey, tb)

        def epilogue(psb, xsub, xkey, Gb, gkey, ssq, rs, junk, tt, tag):
            yv = PS[:, psb:psb + 2, :]
            em.op("act", lambda: A.activation(out=junk[:, :], in_=yv, func=AF.Square, accum_out=ssq[:, 0:1]),
                  R=[pbank(psb), pbank(psb + 1)], W=[tag + "junk", tag + "ssq"])
            rstd_from_ssq(ssq, rs, 1, tag)
            em.op("dve", lambda: V.tensor_tensor(out=tt[:, :], in0=yv, in1=Gb, op=ALU.mult),
                  R=[pbank(psb), pbank(psb + 1), gkey], W=[tag + "tt"])
            em.op("dve", lambda: V.scalar_tensor_tensor(out=xsub, in0=tt[:, :], scalar=rs[:, 0:1], in1=xsub,
                                                        op0=ALU.mult, op1=ALU.add),
                  R=[tag + "tt", tag + "rs", xkey], W=[xkey])

        with ExitStack() as pes:
            sb = lambda n, s, d=F32: pes.enter_context(nc.sbuf_tensor(n, list(s), d))
            cc = sb("cc", [128, 16]); scc = sb("scc", [128, 16]); rhs2 = sb("rhs2", [128, 8, 2], BF16)
            bmc = sb("bmc", [128, 2, 48]); bm1 = sb("bm1", [128, 2, 48]); gco = sb("gco", [128, 2, 2, 8])
            bmrow = sb("bmrow", [2, 2, 2, D]); grow = sb("grow", [2, 2, 2, D]); grt = sb("grt", [2, 2, 2, D])
            wm = [sb(f"wm{i}", [128, 8, 512], BF16) for i in range(3)]
            em.dma("sp", cc[:], ccols_d, W=["cc"])
            em.dma("sp", bmc[:], bmodc_d, W=["bmc"])
            em.dma("sp", gco[:], gcols_d, W=["gco"])
            for l in range(2):
                for w_, (src, off) in enumerate([(bmod_d, 2 * D), (bmod_d, 5 * D)]):
                    em.dma("sp", bmrow[:, l, w_, :], src[l, off:off + D].partition_broadcast(2), W=["bmrow"])
                em.dma("sp", grow[:, l, 0, :], gpm_d[l, :].partition_broadcast(2), W=["grow"])
                em.dma("sp", grow[:, l, 1, :], gpf_d[l, :].partition_broadcast(2), W=["grow"])
            em.op("act", lambda: A.activation(out=scc[:], in_=cc[:], func=AF.Silu), R=["cc"], W=["scc"])
            em.op("dve", lambda: V.tensor_copy(out=rhs2[:, :, 0], in_=scc[:, 0:8]), R=["scc"], W=["rhs2"])
            em.op("dve", lambda: V.tensor_copy(out=rhs2[:, :, 1], in_=scc[:, 8:16]), R=["scc"], W=["rhs2"])
            em.op("dve", lambda: V.tensor_scalar(out=bm1[:], in0=bmc[:], scalar1=1.0, scalar2=None, op0=ALU.add),
                  R=["bmc"], W=["bm1"])
            it = 0
            for l in range(2):
                wsrc = wmod_d[l].rearrange("(k p) n -> p k n", p=128)
                for nb in range(12):
                    slot = it % 3; it += 1
                    wt = wm[slot]; wk = f"wm{slot}"
                    em.dma("pool", wt[:], wsrc[:, :, nb * 512:(nb + 1) * 512], W=[wk])
                    v = nb // 2; half = nb % 2
                    b = it % 4
                    if v in (2, 5):
                        w_ = 0 if v == 2 else 1
                        for k in range(8):
                            em.op("pe", lambda k=k, b=b, wt=wt: T.matmul(PS[0:2, b, :], lhsT=rhs2[:, k, :], rhs=wt[:, k, :],
                                                                        start=(k == 0), stop=(k == 7)),
                                  R=["rhs2", wk], W=[pbank(b)], inc=(k == 7))
                        dst = grt[:, l, w_, half * 512:(half + 1) * 512]
                        em.op("dve", lambda b=b, dst=dst, l=l, w_=w_, half=half: V.tensor_tensor(
                            out=dst, in0=PS[0:2, b, :], in1=bmrow[:, l, w_, half * 512:(half + 1) * 512], op=ALU.add),
                              R=[pbank(b), "bmrow"], W=["grt"])
                        em.op("dve", lambda dst=dst, l=l, w_=w_, half=half: V.tensor_tensor(
                            out=dst, in0=dst, in1=grow[:, l, w_, half * 512:(half + 1) * 512], op=ALU.mult),
                              R=["grt", "grow"], W=["grt"])
                        if half == 1:
                            em.dma("sp", GROW[l, w_, :, :], grt[:, l, w_, :], R=["grt"], W=[("GROW", l, w_)])
                    else:
                        sub = 0 if v < 2 else 1
                        isA = v in (1, 4)
                        for m in range(4):
                            ch = half * 4 + m
                            for k in range(8):
                                em.op("pe", lambda k=k, b=b, m=m, wt=wt: T.matmul(
                                    PS[:, b, 2 * m:2 * m + 2], lhsT=wt[:, k, m * 128:(m + 1) * 128], rhs=rhs2[:, k, :],
                                    start=(k == 0), stop=(k == 7)), R=["rhs2", wk], W=[pbank(b)], inc=(k == 7))
                            dst = MODC[:, l, sub, 0 if isA else 1, :, ch]
                            if isA:
                                em.op("dve", lambda b=b, m=m, dst=dst, l=l, v=v, ch=ch, sub=sub: V.tensor_scalar(
                                    out=dst, in0=PS[:, b, 2 * m:2 * m + 2], scalar1=bm1[:, l, v * 8 + ch:v * 8 + ch + 1],
                                    scalar2=gco[:, l, sub, ch:ch + 1], op0=ALU.add, op1=ALU.mult),
                                      R=[pbank(b), "bm1", "gco"], W=["MODC"])
                            else:
                                em.op("dve", lambda b=b, m=m, dst=dst, l=l, v=v, ch=ch: V.tensor_scalar(
                                    out=dst, in0=PS[:, b, 2 * m:2 * m + 2], scalar1=bmc[:, l, v * 8 + ch:v * 8 + ch + 1],
                                    scalar2=None, op0=ALU.add), R=[pbank(b), "bmc"], W=["MODC"])
            if debug:
                MODCd = nc.dram_tensor("MODCd", [128, 128], F32, kind="ExternalOutput").ap()
                em.dma("sp", MODCd, MODC[:].rearrange("p a b c d e -> p (a b c d e)"), R=["MODC"], W=["MODCd"])
            barrier()
        if stop == "0":
            em.finish()
            return nc
        C = type("C", (), {})()
        C.__dict__.update(locals())
        for name, fn in PHASES:
            fn(C)
            barrier()
            if stop == name:
                if hasattr(C, 'es_mix'):
                    C.es_mix.close()
                break
        em.finish()
    return nc


PHASES = []


def _host_inputs(inp, b):
    global _TABLES
    if _TABLES is None:
        _TABLES = _tables()
    f = lambda a: np.ascontiguousarray(np.asarray(a, dtype=np.float32))
    m = {}
    b, h = b // 2, b % 2
    m["x"] = f(inp["x"][b]); m["ctx"] = f(inp["ctx"][b])
    m["xloc"] = f(inp["x"][b][LBASE * h:LBASE * h + NL])
    m["hsel"] = np.tile(np.array([[1.0 - h, float(h)]], np.float32), (128, 1))
    m["ccols"] = np.concatenate([_col(inp["c"][b], 8), _col(inp["c_ctx"], 8)], axis=1)
    m["w_mod"] = f(inp["w_mod"]); m["b_mod"] = f(inp["b_mod"])
    m["bmodc"] = np.ascontiguousarray(np.stack([_col(inp["b_mod"][l], 48) for l in range(2)], axis=1))
    m["gcols"] = np.ascontiguousarray(np.stack(
        [np.stack([_col(inp["g_pre_mix"][l], 8), _col(inp["g_pre_ffn"][l], 8)], axis=1) for l in range(2)], axis=1))
    m["g_post_mix"] = f(inp["g_post_mix"]); m["g_post_ffn"] = f(inp["g_post_ffn"])
    m["w_in_ab"] = f(inp["w_in_ab"][0]); m["w_out_ab"] = f(inp["w_out_ab"][0])
    m["w_ffn_gate"] = f(inp["w_ffn_gate"]); m["w_ffn_up"] = f(inp["w_ffn_up"]); m["w_ffn_down"] = f(inp["w_ffn_down"])
    m["w_qkv_na"] = f(inp["w_qkv_na"][0]); m["w_out_na"] = f(inp["w_out_na"][0])
    wbd = np.zeros((2, 2, 4, 128, 128), np.float32)
    for gi, key in enumerate(["lru_w_a", "lru_w_i"]):
        w = np.asarray(inp[key][0], np.float32)
        for d in range(2):
            for c in range(4):
                wbd[gi, d, c, 0:64, 0:64] = w[d, 2 * c]
                wbd[gi, d, c, 64:128, 64:128] = w[d, 2 * c + 1]
    m["wbd"] = wbd
    lc = np.zeros((128, 4, 2, 3), np.float32)
    for d in range(2):
        lc[:, :, d, 0] = _col(inp["lru_b_a"][0][d], 4)
        lc[:, :, d, 1] = _col(inp["lru_b_i"][0][d], 4)
        lc[:, :, d, 2] = _col(inp["lru_lam"][0][d], 4)
    m["lcols"] = lc
    cv = np.zeros((128, 4, 5), np.float32)
    for k in range(4):
        cv[:, :, k] = _col(inp["conv_w"][0][k], 4)
    cv[:, :, 4] = _col(inp["conv_b"][0], 4)
    m["convc"] = cv
    cd = np.zeros((4, 4, 128, 128), np.float32)
    ii = np.arange(128)
    for c in range(4):
        for k in range(4):
            cd[c, k, ii, ii] = cv[:, c, k]
    m["cdiag"] = cd
    for k in ("ident", "identb", "T1", "M2", "CSf", "CSc", "T256"):
        m[k] = _TABLES[k]
    rpb = np.asarray(inp["rpb_na"][0], np.float32)
    m["BT"] = np.ascontiguousarray(_bias_tables(rpb)[2])
    m["BTE"] = _edge_tables(rpb, h)
    return m


_NC_CACHE = {}


def kernel(**inputs):
    if "full" not in _NC_CACHE:
        _NC_CACHE["full"] = build()
    nc = _NC_CACHE["full"]
    in_maps = [_host_inputs(inputs, b) for b in range(NCORES)]
    res = run_bass_kernel_spmd(nc, in_maps, core_ids=list(range(NCORES)))
    out = np.empty((4, S, D), np.float32)
    for c in range(NCORES):
        b, h = c // 2, c % 2
        o = np.asarray(res.results[c]["out"], dtype=np.float32)
        out[b, 4096 * h:4096 * (h + 1)] = o[512 * h:512 * h + 4096]
    return out


def _wload(C, dst, src_view, key, nk, ncols, step=512):
    for c0 in range(0, ncols, step):
        c1 = min(ncols, c0 + step)
        C.em.dma("pool", dst[:, :, c0:c1], src_view[:, :, c0:c1], W=[key])


def _wload_bf(C, dst, src_view, key, ncols, srckey, step=1024):
    for c0 in range(0, ncols, step):
        c1 = min(ncols, c0 + step)
        C.em.dma("sp", dst[:, :, c0:c1], src_view[:, :, c0:c1], R=C.cvkeys[srckey], W=[key])


def preconvert(C):
    em = C.em
    jobs = [("woab", C.woab_d, C.WBF["woab"]), ("wg0", C.wg_d[0], C.WBF["wg"][0]), ("wu0", C.wu_d[0], C.WBF["wu"][0]),
            ("wd0", C.wd_d[0], C.WBF["wd"][0]), ("wqkv", C.wqkv_d, C.WBF["wqkv"]), ("wona", C.wona_d, C.WBF["wona"]),
            ("wg1", C.wg_d[1], C.WBF["wg"][1]), ("wu1", C.wu_d[1], C.WBF["wu"][1]), ("wd1", C.wd_d[1], C.WBF["wd"][1])]
    C.cvkeys = {}
    for key, src, dst in jobs:
        rows, cols = src.shape
        rstep = 512 if rows % 512 == 0 else 704
        C.cvkeys["cv" + key] = []
        for r0 in range(0, rows, rstep):
            for c0 in range(0, cols, 1024):
                c1 = min(cols, c0 + 1024)
                k_ = f"cv{key}_{r0}_{c0}"
                C.cvkeys["cv" + key].append(k_)
                em.dma_bg("pool", dst[r0:r0 + rstep, c0:c1], src[r0:r0 + rstep, c0:c1], W=[k_])


def phase_A(C):
    nc, em, PS = C.nc, C.em, C.PS
    V, A, G, T = nc.vector, nc.scalar, nc.gpsimd, nc.tensor
    pes = C.es_mix = ExitStack()
    C.F = pes.enter_context(nc.sbuf_tensor("Fbuf", [128, 64, 512], BF16))
    C.fTc = pes.enter_context(nc.sbuf_tensor("fTc", [128, 4, LC], BF16))
    F, fTc = C.F, C.fTc
    with ExitStack() as ps_:
        sb = lambda n, s, d=F32: ps_.enter_context(nc.sbuf_tensor(n, list(s), d))
        win = sb("win", [128, 8, 1536], BF16)
        xt = [sb(f"Axt{i}", [128, 4, D]) for i in range(2)]
        hT = [sb(f"AhT{i}", [128, 8, 512], BF16) for i in range(2)]
        ugst = [sb(f"Aug{i}", [128, 8, 512]) for i in range(2)]
        ssq = sb("Assq", [128, 4]); rs = sb("Ars", [128, 4]); junk = sb("Ajunk", [128, D], BF16)
        _wload(C, win, C.win_d.rearrange("(k p) n -> p k n", p=128), "win", 8, 1536)
        xsrc = C.x_d.rearrange("(t1 t2) d -> t1 t2 d", t2=64)
        Acol = C.MODC[:, 0, 0, 0, 0, :]; Bcol = C.MODC[:, 0, 0, 1, 0, :]
        AcolC = C.MODC[:, 0, 0, 0, 1, :]; BcolC = C.MODC[:, 0, 0, 1, 1, :]
        em.dma("sp", xt[0][:], xsrc[:, 0:4, :], W=["Axt0"])
        ev = 0
        for j in range(17):
            sl = j % 2
            isctx = j == 16
            nsub = 2 if isctx else 4
            n = nsub * 128
            if j + 1 < 16:
                em.dma("sp", xt[1 - sl][:], xsrc[:, 4 * (j + 1):4 * (j + 2), :], W=[f"Axt{1 - sl}"])
            elif j + 1 == 16:
                em.dma("sp", xt[1 - sl][:, 0:2, :], C.ctx_d.rearrange("(s p) d -> p s d", p=128), W=[f"Axt{1 - sl}"])
            C.prologue(xt[sl], f"Axt{sl}", nsub, xt[sl], f"Axt{sl}", AcolC if isctx else Acol, BcolC if isctx else Bcol,
                       hT[sl], f"AhT{sl}", ssq, rs, junk, "A", [0, 1])
            for oc in range(8):
                b = 2 + oc % 3
                for k in range(8):
                    em.op("pe", lambda k=k, b=b, oc=oc, sl=sl, n=n: T.matmul(
                        PS[:, b, 0:n], lhsT=win[:, k, oc * 128:(oc + 1) * 128], rhs=hT[sl][:, k, 0:n],
                        start=(k == 0), stop=(k == 7)), R=["win", f"AhT{sl}"], W=[("ps", b)], inc=(k == 7))
                ev += 1
                if ev % 2:
                    em.op("dve", lambda b=b, oc=oc, sl=sl, n=n: V.tensor_copy(out=ugst[sl][:, oc, 0:n], in_=PS[:, b, 0:n]),
                          R=[("ps", b)], W=[f"Aug{sl}"])
                else:
                    em.op("act", lambda b=b, oc=oc, sl=sl, n=n: A.copy(out=ugst[sl][:, oc, 0:n], in_=PS[:, b, 0:n]),
                          R=[("ps", b)], W=[f"Aug{sl}"])
            if isctx:
                em.dma("pool", C.UGc.rearrange("c p n -> p c n"), ugst[sl][:, :, 0:LC], R=[f"Aug{sl}"], W=["UGc"])
                for fc in range(4):
                    b = 5 + fc % 3
                    for k in range(8):
                        em.op("pe", lambda k=k, b=b, fc=fc, sl=sl: T.matmul(
                            PS[:, b, 0:LC], lhsT=win[:, k, 1024 + fc * 128:1024 + (fc + 1) * 128], rhs=hT[sl][:, k, 0:LC],
                            start=(k == 0), stop=(k == 7)), R=["win", f"AhT{sl}"], W=[("ps", b)], inc=(k == 7))
                    em.op("dve", lambda b=b, fc=fc: V.tensor_copy(out=fTc[:, fc, :], in_=PS[:, b, 0:LC]),
                          R=[("ps", b)], W=["fTc"])
            else:
                em.dma("pool", C.UG[:, :, j * 512:(j + 1) * 512].rearrange("c p n -> p c n"), ugst[sl][:],
                       R=[f"Aug{sl}"], W=[("UG", j)])
                for s in range(4):
                    b = 5 + s % 3
                    for k in range(8):
                        em.op("pe", lambda k=k, b=b, s=s, sl=sl: T.matmul(
                            PS[:, b, :], lhsT=hT[sl][:, k, s * 128:(s + 1) * 128], rhs=win[:, k, 1024:1536],
                            start=(k == 0), stop=(k == 7)), R=["win", f"AhT{sl}"], W=[("ps", b)], inc=(k == 7))
                    ev += 1
                    if ev % 2:
                        em.op("dve", lambda b=b, s=s, j=j: V.tensor_copy(out=F[:, 4 * j + s, :], in_=PS[:, b, :]),
                              R=[("ps", b)], W=["F"])
                    else:
                        em.op("act", lambda b=b, s=s, j=j: A.copy(out=F[:, 4 * j + s, :], in_=PS[:, b, :]),
                              R=[("ps", b)], W=["F"])


def phase_C(C):
    nc, em, PS, F, fTc = C.nc, C.em, C.PS, C.F, C.fTc
    V, A, G, T = nc.vector, nc.scalar, nc.gpsimd, nc.tensor
    preconvert(C)
    with ExitStack() as ps_:
        sb = lambda n, s, d=F32: ps_.enter_context(nc.sbuf_tensor(n, list(s), d))
        T1 = sb("T1s", [128, 256], BF16); M2 = sb("M2s", [64, 128, 192], BF16); CSf = sb("CSfs", [128, 256], BF16)
        CSc = sb("CScs", [128, 256], BF16); T256 = sb("T256s", [128, 2, 512], BF16)
        Ast = sb("Ast", [64, 64, 256], BF16); Y = sb("Ybuf", [128, 2, S], BF16)
        fst = [sb(f"fst{i}", [128, 2048], BF16) for i in range(2)]
        Gc = sb("Gcb", [128, 2, 256], BF16); fcs = sb("fcs", [128, LC], BF16)
        for dst, src, k in ((T1, C.T1_d, "T1"), (M2, C.M2_d, "M2"), (CSf, C.CSf_d, "CSf"), (CSc, C.CSc_d, "CSc"),
                            (T256, C.T256_d, "T256")):
            em.dma("sp", dst[:], src, W=[k])
        ev = 0
        for cc in range(4):
            for hc in range(2):
                for g4 in range(16):
                    b = 2 * (g4 % 2)
                    for q in range(4):
                        ch = cc * 128 + hc * 64 + g4 * 4 + q
                        em.op("pe", lambda b=b, q=q, ch=ch: T.matmul(
                            PS[0:64, b + q // 2, (q % 2) * 256:(q % 2) * 256 + 256], lhsT=F[:, :, ch], rhs=T1[:, :],
                            start=True, stop=True), R=["F", "T1"], W=[("ps", b), ("ps", b + 1)], inc=(q == 3))
                    ev += 1
                    dst = Ast[:, g4 * 4:(g4 + 1) * 4, :].rearrange("p a b -> p (a b)")
                    src = PS[0:64, b:b + 2, :].rearrange("p a b -> p (a b)")
                    if ev % 2:
                        em.op("dve", lambda dst=dst, src=src: V.tensor_copy(out=dst, in_=src),
                              R=[("ps", b), ("ps", b + 1)], W=["Ast"])
                    else:
                        em.op("act", lambda dst=dst, src=src: A.copy(out=dst, in_=src),
                              R=[("ps", b), ("ps", b + 1)], W=["Ast"])
                for kb in range(32):
                    b = 4 + kb % 2
                    for q in range(4):
                        k1 = kb * 4 + q
                        o = PS[hc * 64:(hc + 1) * 64, b, q * 128:(q + 1) * 128]
                        em.op("pe", lambda o=o, k1=k1: T.matmul(o, lhsT=Ast[:, :, k1], rhs=M2[:, k1, 64:192],
                                                                start=True, stop=False),
                              R=["Ast", "M2"], W=[("ps", b)], inc=False)
                        em.op("pe", lambda o=o, k1=k1: T.matmul(o, lhsT=Ast[:, :, 128 + k1], rhs=M2[:, k1, 0:128],
                                                                start=False, stop=True),
                              R=["Ast", "M2"], W=[("ps", b)], inc=(q == 3))
                    ev += 1
                    src = PS[hc * 64:(hc + 1) * 64, b, :].rearrange("p (k r c) -> p r c k", k=4, r=2)
                    dst = Y[hc * 64:(hc + 1) * 64, :, :].rearrange("p r (c k) -> p r c k", k=128)[:, :, :, kb * 4:(kb + 1) * 4]
                    if ev % 2:
                        em.op("dve", lambda dst=dst, src=src: V.tensor_copy(out=dst, in_=src), R=[("ps", b)], W=["Y"])
                    else:
                        em.op("act", lambda dst=dst, src=src: A.copy(out=dst, in_=src), R=[("ps", b)], W=["Y"])
            for tl in range(16):
                b = 6 + tl % 2
                em.op("pe", lambda b=b, tl=tl: T.matmul(PS[:, b, :], lhsT=CSf[:, 0:128], rhs=Y[:, 0, tl * 512:(tl + 1) * 512],
                                                        start=True, stop=False), R=["CSf", "Y"], W=[("ps", b)], inc=False)
                em.op("pe", lambda b=b, tl=tl: T.matmul(PS[:, b, :], lhsT=CSf[:, 128:256], rhs=Y[:, 1, tl * 512:(tl + 1) * 512],
                                                        start=False, stop=True), R=["CSf", "Y"], W=[("ps", b)], inc=True)
                fs = (tl // 4) % 2
                em.op("act", lambda b=b, tl=tl, fs=fs: A.copy(out=fst[fs][:, (tl % 4) * 512:(tl % 4 + 1) * 512], in_=PS[:, b, :]),
                      R=[("ps", b)], W=[f"fst{fs}"])
                if tl % 4 == 3:
                    t0 = (tl // 4) * 2048
                    em.dma("pool", C.MIXT[512 + cc * 128:512 + (cc + 1) * 128, t0:t0 + 2048], fst[fs][:],
                           R=[f"fst{fs}"], W=[("MIXT", 4 + cc)])
            for tc in range(2):
                em.op("pe", lambda tc=tc, cc=cc: T.matmul(PS[:, 0, 0:256], lhsT=fTc[:, cc, tc * 128:(tc + 1) * 128], rhs=CSc[:, :],
                                                          start=True, stop=True), R=["fTc", "CSc"], W=[("ps", 0)])
                em.op("dve", lambda tc=tc: V.tensor_copy(out=Gc[:, tc, :], in_=PS[:, 0, 0:256]), R=[("ps", 0)], W=["Gc"])
            for i_, (tc, part) in enumerate([(0, 0), (0, 1), (1, 0), (1, 1)]):
                em.op("pe", lambda i_=i_, tc=tc, part=part: T.matmul(
                    PS[:, 1, 0:256], lhsT=Gc[:, tc, part * 128:(part + 1) * 128], rhs=T256[:, tc, part * 256:(part + 1) * 256],
                    start=(i_ == 0), stop=(i_ == 3)), R=["Gc", "T256"], W=[("ps", 1)], inc=(i_ == 3))
            em.op("dve", lambda: V.tensor_copy(out=fcs[:, :], in_=PS[:, 1, 0:256]), R=[("ps", 1)], W=["fcs"])
            em.dma("pool", C.MIXT[512 + cc * 128:512 + (cc + 1) * 128, S:NT], fcs[:], R=["fcs"], W=[("MIXTc", 4 + cc)])
    C.es_mix.close()


PHASES += [("A", phase_A), ("C", phase_C)]


def phase_B(C):
    nc, em, PS = C.nc, C.em, C.PS
    V, A, G, T = nc.vector, nc.scalar, nc.gpsimd, nc.tensor
    TW = 2048
    with ExitStack() as ps_:
        sb = lambda n, s, d=F32: ps_.enter_context(nc.sbuf_tensor(n, list(s), d))
        bufA = sb("bufA", [128, NT]); bufB = sb("bufB", [128, 8460]); uc = sb("ucb_", [128, NT]); ucb = sb("ucbb", [128, NT], BF16)
        rt = sb("rt", [128, TW])
        it_ = [sb(f"it{i}", [128, TW]) for i in range(2)]; at = [sb(f"at{i}", [128, TW]) for i in range(2)]
        st = [sb(f"st{i}", [128, TW]) for i in range(2)]
        gtmp = [sb(f"gtmp{i}", [128, 1024]) for i in range(2)]
        wbd = sb("wbds", [128, 16, 128], BF16)
        cdg = sb("cdg", [128, 16, 128])
        lco = sb("lco", [128, 4, 2, 3]); cvc = sb("cvc", [128, 4, 5]); cA = sb("cA", [128, 4, 2]); carry = sb("carry", [128, 2])
        em.dma("pool", wbd[:], C.wbd_d.rearrange("g d c p n -> p (g d c) n"), W=["wbd"])
        em.dma("sp", cdg[:], C.cdiag_d.rearrange("c k p n -> p (c k) n"), W=["cdg"])
        em.dma("sp", lco[:], C.lcols_d, W=["lco"])
        em.dma("sp", cvc[:], C.convc_d, W=["cvc"])
        em.op("act", lambda: A.activation(out=cA[:], in_=lco[:, :, :, 2], func=AF.Exp, scale=-1.0), R=["lco"], W=["cA"])
        em.op("act", lambda: A.activation(out=cA[:], in_=cA[:], func=AF.Ln, bias=1.0), R=["cA"], W=["cA"])
        em.op("dve", lambda: V.tensor_scalar(out=cA[:], in0=cA[:], scalar1=-8.0, scalar2=None, op0=ALU.mult), R=["cA"], W=["cA"])
        em.op("dve", lambda: V.memset(bufB[:], 0.0), W=["bufB"])
        XO = 8200
        tiles = [(S, NT)] + [(i * TW, (i + 1) * TW) for i in range(4)]
        tix = 0
        for c in range(4):
            em.dma("sp", bufA[:, 0:S], C.UG[c], W=["bufA"])
            em.dma("sp", bufA[:, S:NT], C.UGc[c], W=["bufA"])
            for qd in range(4):
                o_ = bufB[:, 2 + qd * 2048:2 + (qd + 1) * 2048].rearrange("p (a b) -> p a b", b=64)
                i_ = bufA[:, 0:S].rearrange("p (b a) -> p a b", a=128)[:, qd * 32:(qd + 1) * 32, :]
                eng = QENG[qd]
                fn = {"act": (lambda o_=o_, i_=i_: A.copy(out=o_, in_=i_)),
                      "dve": (lambda o_=o_, i_=i_: V.tensor_copy(out=o_, in_=i_)),
                      "pool": (lambda o_=o_, i_=i_: G.tensor_copy(out=o_, in_=i_))}[eng]
                em.op(eng, fn, R=["bufA"], W=[f"bufBq{qd}"])
            em.op("pool", lambda: G.tensor_copy(out=bufB[:, XO + 2:XO + 2 + LC], in_=bufA[:, S:NT]), R=["bufA"], W=["bufBc"])
            em.dma("sp", bufA[:, 0:S], C.UG[4 + c], W=["bufA"])
            em.dma("sp", bufA[:, S:NT], C.UGc[4 + c], W=["bufA"])
            for (o0, src0, n) in ((0, 0, S), (S, XO, LC)):
                for q0 in range(0, n, 2048):
                    nn = min(2048, n - q0)
                    nq = (nn + 511) // 512
                    pb = 4 * ((q0 // 2048) % 2)
                    qd = q0 // 2048
                    rk = ["bufBc", "bufB"] if n == LC else [f"bufBq{x}" for x in (qd - 1, qd, qd + 1) if 0 <= x < 4] + ["bufB"]
                    for q in range(nq):
                        w_ = min(512, nn - q * 512)
                        for k in range(4):
                            em.op("pe", lambda q=q, k=k, w_=w_, pb=pb, src0=src0, q0=q0, c=c: T.matmul(
                                PS[:, pb + q, 0:w_], lhsT=cdg[:, c * 4 + k, :],
                                rhs=bufB[:, src0 + q0 + q * 512 + k:src0 + q0 + q * 512 + k + w_],
                                start=(k == 0), stop=(k == 3)), R=["cdg"] + rk, W=[("ps", pb + q)], inc=(k == 3))
                    src = PS[:, pb:pb + 4, :].rearrange("p a b -> p (a b)")[:, 0:nn]
                    em.op("act", lambda src=src, o0=o0, q0=q0, nn=nn, c=c: A.activation(
                        out=uc[:, o0 + q0:o0 + q0 + nn], in_=src, func=AF.Identity, bias=cvc[:, c, 4:5]),
                          R=[("ps", pb + q) for q in range(4)] + ["cvc"], W=["uc"])
                    em.op("act", lambda src=src, o0=o0, q0=q0, nn=nn, c=c: A.activation(
                        out=ucb[:, o0 + q0:o0 + q0 + nn], in_=src, func=AF.Identity, bias=cvc[:, c, 4:5]),
                          R=[("ps", pb + q) for q in range(4)] + ["cvc"], W=["ucb"])

            def gelu_s1(lo, hi, p):
                n = hi - lo
                em.op("act", lambda: A.activation(out=gtmp[p][:, 0:n], in_=bufA[:, lo:hi], func=AF.Square),
                      R=["bufA"], W=[f"gt{p}"])
                em.op("pool", lambda: G.tensor_scalar(out=gtmp[p][:, 0:n], in0=gtmp[p][:, 0:n], scalar1=0.044715, scalar2=1.0,
                                                      op0=ALU.mult, op1=ALU.add), R=[f"gt{p}"], W=[f"gt{p}"])
                em.op("dve", lambda: V.tensor_tensor(out=gtmp[p][:, 0:n], in0=gtmp[p][:, 0:n], in1=bufA[:, lo:hi], op=ALU.mult),
                      R=[f"gt{p}", "bufA"], W=[f"gt{p}"])

            def gelu_s2(lo, hi, p):
                n = hi - lo
                em.op("act", lambda: A.activation(out=gtmp[p][:, 0:n], in_=gtmp[p][:, 0:n], func=AF.Sigmoid,
                                                  scale=1.5957691216057308), R=[f"gt{p}"], W=[f"gt{p}"])
                em.op("dve", lambda: V.tensor_tensor(out=bufA[:, lo:hi], in0=gtmp[p][:, 0:n], in1=bufA[:, lo:hi], op=ALU.mult),
                      R=[f"gt{p}", "bufA"], W=["bufA"])

            gpieces = [(S, NT)] + [(i * 1024, (i + 1) * 1024) for i in range(8)]
            gstate = {"s1": 0, "s2": 0}

            def gelu_push1():
                k = gstate["s1"]
                if k < len(gpieces):
                    gelu_s1(gpieces[k][0], gpieces[k][1], k % 2)
                    gstate["s1"] += 1

            def gelu_push2():
                k = gstate["s2"]
                if k < gstate["s1"]:
                    gelu_s2(gpieces[k][0], gpieces[k][1], k % 2)
                    gstate["s2"] += 1

            for d in range(2):
                order = [tiles[0]] + (tiles[1:] if d == 0 else tiles[1:][::-1])
                for ti, (lo, hi) in enumerate(order):
                    p = tix % 2
                    tix += 1
                    n = hi - lo
                    nq = (n + 511) // 512
                    for gi in range(2):
                        for q in range(nq):
                            w_ = min(512, n - q * 512)
                            em.op("pe", lambda gi=gi, q=q, w_=w_, lo=lo, d=d, c=c: T.matmul(
                                PS[:, gi * 4 + q, 0:w_], lhsT=wbd[:, gi * 8 + d * 4 + c, :], rhs=ucb[:, lo + q * 512:lo + q * 512 + w_],
                                start=True, stop=True), R=["wbd", "ucb"], W=[("ps", gi * 4 + q)], inc=(q == nq - 1))
                    pr = PS[:, 0:4, :].rearrange("p a b -> p (a b)")[:, 0:n]
                    pi = PS[:, 4:8, :].rearrange("p a b -> p (a b)")[:, 0:n]
                    if d == 0:
                        gelu_push2()
                        gelu_push2()
                    em.op("act", lambda pr=pr, n=n, c=c, d=d: A.activation(out=rt[:, 0:n], in_=pr, func=AF.Sigmoid,
                                                                           bias=lco[:, c, d, 0:1]),
                          R=[("ps", q) for q in range(4)] + ["lco"], W=["rt"])
                    em.op("act", lambda pi=pi, n=n, c=c, d=d, p=p: A.activation(out=it_[p][:, 0:n], in_=pi, func=AF.Sigmoid,
                                                                                bias=lco[:, c, d, 1:2]),
                          R=[("ps", 4 + q) for q in range(4)] + ["lco"], W=[f"it{p}"])
                    em.op("act", lambda n=n, c=c, d=d, p=p: A.activation(out=at[p][:, 0:n], in_=rt[:, 0:n], func=AF.Exp,
                                                                         scale=cA[:, c, d:d + 1]), R=["rt", "cA"], W=[f"at{p}"])
                    em.op("dve", lambda n=n, p=p: V.tensor_tensor(out=st[p][:, 0:n], in0=at[p][:, 0:n], in1=at[p][:, 0:n], op=ALU.mult),
                          R=[f"at{p}"], W=[f"st{p}"])
                    if d == 0:
                        gelu_push1()
                        gelu_push1()
                    em.op("act", lambda n=n, p=p: A.activation(out=st[p][:, 0:n], in_=st[p][:, 0:n], func=AF.Sqrt, scale=-1.0, bias=1.0),
                          R=[f"st{p}"], W=[f"st{p}"])
                    em.op("dve", lambda n=n, p=p: V.tensor_tensor(out=it_[p][:, 0:n], in0=it_[p][:, 0:n], in1=st[p][:, 0:n], op=ALU.mult),
                          R=[f"it{p}", f"st{p}"], W=[f"it{p}"])
                    em.op("dve", lambda n=n, lo=lo, hi=hi, p=p: V.tensor_tensor(out=it_[p][:, 0:n], in0=it_[p][:, 0:n], in1=uc[:, lo:hi],
                                                                               op=ALU.mult), R=[f"it{p}", "uc"], W=[f"it{p}"])
                    init = 0.0 if ti == 0 else carry[:, d:d + 1]
                    if d == 0:
                        em.op("dve", lambda n=n, lo=lo, hi=hi, init=init, p=p: V.tensor_tensor_scan(
                            out=bufB[:, lo:hi], data0=at[p][:, 0:n], data1=it_[p][:, 0:n], initial=init, op0=ALU.mult, op1=ALU.add),
                              R=[f"at{p}", f"it{p}", "carry", "bufB", "uc"], W=["bufB", "bufBq0", "bufBq1", "bufBq2", "bufBq3", "bufBc"])
                        em.op("dve", lambda hi=hi: V.tensor_copy(out=carry[:, 0:1], in_=bufB[:, hi - 1:hi]), R=["bufB"], W=["carry"])
                    else:
                        em.op("dve", lambda n=n, init=init, p=p: V.tensor_tensor_scan(
                            out=st[p][:, 0:n][:, ::-1], data0=at[p][:, 0:n][:, ::-1],
                            data1=it_[p][:, 0:n][:, ::-1], initial=init, op0=ALU.mult, op1=ALU.add),
                              R=[f"at{p}", f"it{p}", "carry", f"st{p}"], W=[f"st{p}"])
                        em.op("dve", lambda p=p: V.tensor_copy(out=carry[:, 1:2], in_=st[p][:, 0:1]), R=[f"st{p}"], W=["carry"])
                        em.op("dve", lambda n=n, lo=lo, hi=hi, p=p: V.tensor_tensor(out=bufB[:, lo:hi], in0=bufB[:, lo:hi], in1=st[p][:, 0:n],
                                                                                    op=ALU.add), R=["bufB", f"st{p}"], W=["bufB"])
            while gstate["s2"] < len(gpieces):
                gelu_push1()
                gelu_push2()
            em.op("dve", lambda: V.tensor_tensor(out=ucb[:, 0:S].rearrange("p (a b) -> p a b", b=64),
                                                 in0=bufB[:, 0:S].rearrange("p (a b) -> p a b", b=64),
                                                 in1=bufA[:, 0:S].rearrange("p (b a) -> p a b", a=128), op=ALU.mult),
                  R=["bufB", "bufBq0", "bufBq1", "bufBq2", "bufBq3", "bufBc", "bufA", "ucb"], W=["ucb"])
            em.op("dve", lambda: V.tensor_tensor(out=ucb[:, S:NT], in0=bufB[:, S:NT], in1=bufA[:, S:NT], op=ALU.mult),
                  R=["bufB", "bufBq0", "bufBq1", "bufBq2", "bufBq3", "bufBc", "bufA", "ucb"], W=["ucb"])
            em.dma("pool", C.MIXT[c * 128:(c + 1) * 128, :], ucb[:, :], R=["ucb"], W=[("MIXT", c)])
            if c < 3:
                em.op("dve", lambda: V.memset(bufB[:, 0:2], 0.0), R=["bufB"], W=["bufB", "bufBq0"])
                em.op("dve", lambda: V.memset(bufB[:, S:8460], 0.0), R=["bufB"], W=["bufB", "bufBq3", "bufBc"])


def _tok_tiles(ntok_tile):
    return None


def phase_D1(C, layer=0):
    nc, em, PS = C.nc, C.em, C.PS
    V, A, G, T = nc.vector, nc.scalar, nc.gpsimd, nc.tensor
    with ExitStack() as ps_:
        sb = lambda n, s, d=F32: ps_.enter_context(nc.sbuf_tensor(n, list(s), d))
        wo = sb("D1wo", [128, 8, D], BF16)
        xt = [sb(f"D1xt{i}", [128, 4, D]) for i in range(2)]
        mt = [sb(f"D1mt{i}", [128, 8, 512], BF16) for i in range(2)]
        mb = [sb(f"D1mb{i}", [128, 8, 512], BF16) for i in range(2)]
        hs = sb("D1hs", [128, 2])
        Gx = sb("D1Gx", [128, D]); Gc = sb("D1Gc", [128, D])
        ssq = sb("D1ssq", [128, 4]); rs = sb("D1rs", [128, 4]); junk = sb("D1junk", [128, D], BF16); tt = sb("D1tt", [128, D])
        _wload_bf(C, wo, C.WBF["woab"].rearrange("(k p) n -> p k n", p=128), "D1wo", D, "cvwoab")
        em.dma("sp", hs[:], C.hsel_d, W=["D1hs"])
        em.dma("sp", Gx[:], C.GROW[0, 0, 0, :].partition_broadcast(128), R=[("GROW", 0, 0)], W=["D1Gx"])
        em.dma("sp", Gc[:], C.GROW[0, 0, 1, :].partition_broadcast(128), R=[("GROW", 0, 0)], W=["D1Gc"])
        mixv = C.MIXT.rearrange("(k p) t -> p k t", p=128)
        NTL = NLT + 1

        def load(j, sl):
            if j < NLT:
                em.dma("sp", xt[sl][:], C.xloc_d[j * 512:(j + 1) * 512, :].rearrange("(s p) d -> p s d", p=128), W=[f"D1xt{sl}"])
                em.dma("sp", mt[sl][:], mixv[:, :, j * 512:(j + 1) * 512], W=[f"D1mt{sl}"])
                em.dma("sp", mb[sl][:], mixv[:, :, LBASE + j * 512:LBASE + (j + 1) * 512], W=[f"D1mb{sl}"])
                em.op("act", lambda sl=sl: A.activation(out=mt[sl][:].rearrange("p a b -> p (a b)"),
                                                        in_=mt[sl][:].rearrange("p a b -> p (a b)"), func=AF.Copy,
                                                        scale=hs[:, 0:1]), R=[f"D1mt{sl}", "D1hs"], W=[f"D1mt{sl}"])
                em.op("dve", lambda sl=sl: V.scalar_tensor_tensor(
                    out=mt[sl][:].rearrange("p a b -> p (a b)"), in0=mb[sl][:].rearrange("p a b -> p (a b)"), scalar=hs[:, 1:2],
                    in1=mt[sl][:].rearrange("p a b -> p (a b)"), op0=ALU.mult, op1=ALU.add),
                      R=[f"D1mt{sl}", f"D1mb{sl}", "D1hs"], W=[f"D1mt{sl}"])
            else:
                em.dma("sp", xt[sl][:, 0:2, :], C.ctx_d.rearrange("(s p) d -> p s d", p=128), W=[f"D1xt{sl}"])
                em.dma("sp", mt[sl][:, :, 0:LC], mixv[:, :, S:NT], W=[f"D1mt{sl}"])
        load(0, 0)
        for j in range(NTL):
            sl = j % 2
            if j + 1 < NTL:
                load(j + 1, 1 - sl)
            isx = j < NLT
            nsub = 4 if isx else 2
            for s in range(nsub):
                pb = 2 * (s % 4)
                for h in range(2):
                    for k in range(8):
                        em.op("pe", lambda k=k, h=h, s=s, sl=sl, pb=pb: T.matmul(
                            PS[:, pb + h, :], lhsT=mt[sl][:, k, s * 128:(s + 1) * 128], rhs=wo[:, k, h * 512:(h + 1) * 512],
                            start=(k == 0), stop=(k == 7)), R=["D1wo", f"D1mt{sl}"], W=[("ps", pb + h)], inc=(k == 7))
                C.epilogue(pb, xt[sl][:, s, :], f"D1xt{sl}", (Gx if isx else Gc)[:, :], "D1Gx" if isx else "D1Gc",
                           ssq, rs, junk, tt, "D1")
            if isx:
                em.dma("pool", C.X1[j * 512:(j + 1) * 512, :].rearrange("(s p) d -> p s d", p=128), xt[sl][:],
                       R=[f"D1xt{sl}"], W=[("X1", j)])
            else:
                em.dma("pool", C.C1.rearrange("(s p) d -> p s d", p=128), xt[sl][:, 0:2, :], R=[f"D1xt{sl}"], W=["C1"])


def phase_FFN(C, layer, Xin, Cin, Xout, Cout, inkey, outkey):
    nc, em, PS = C.nc, C.em, C.PS
    V, A, G, T = nc.vector, nc.scalar, nc.gpsimd, nc.tensor
    tg = f"F{layer}"
    with ExitStack() as ps_:
        sb = lambda n, s, d=F32: ps_.enter_context(nc.sbuf_tensor(tg + n, list(s), d))
        wg = sb("wg", [128, 8, FH], BF16); wu = sb("wu", [128, 8, FH], BF16); wd = sb("wd", [128, NJ, D], BF16)
        xt = sb("xt", [128, 4, D]); hT = sb("hT", [128, 8, 512], BF16); hh = sb("hh", [128, NJ, 512], BF16)
        est = [sb(f"est{i}", [128, D]) for i in range(2)]
        Gx = sb("Gx", [128, D]); tt = sb("tt", [128, D]); junk = sb("junk", [128, D], BF16)
        ssq = sb("ssq", [128, 4]); rs = sb("rs", [128, 4]); ssq2 = sb("ssq2", [128, 4]); rs2 = sb("rs2", [128, 4])
        _wload_bf(C, wg, C.WBF["wg"][layer].rearrange("(k p) n -> p k n", p=128), tg + "wg", FH, f"cvwg{layer}", 1408)
        _wload_bf(C, wu, C.WBF["wu"][layer].rearrange("(k p) n -> p k n", p=128), tg + "wu", FH, f"cvwu{layer}", 1408)
        _wload_bf(C, wd, C.WBF["wd"][layer].rearrange("(k p) n -> p k n", p=128), tg + "wd", D, f"cvwd{layer}", 512)
        em.dma("sp", Gx[:], C.GROW[layer, 1, 0, :].partition_broadcast(128), W=[tg + "Gx"])
        NX = NL // 512
        ntile = NX + (1 if Cin is not None else 0)

        def nsub_of(j):
            return 4 if j < NX else 2

        def rows(j, s):
            if j < NX:
                r0 = j * 512 + s * 128
                return Xin[r0:r0 + 128, :], Xout[r0:r0 + 128, :]
            return Cin[s * 128:(s + 1) * 128, :], Cout[s * 128:(s + 1) * 128, :]

        def load(j):
            ns = nsub_of(j)
            src = Xin[j * 512:(j + 1) * 512, :] if j < NX else Cin
            em.dma("sp", xt[:, 0:ns, :], src.rearrange("(s p) d -> p s d", p=128), W=[tg + "xt"])

        def pro_a(j):
            ns = nsub_of(j)
            C.prologue_a(xt, tg + "xt", ns, xt, tg + "xt", ssq2, rs2, junk, tg + "p")

        def pro_b(j):
            path = 0 if j < NX else 1
            C.prologue_b(nsub_of(j), xt, tg + "xt", C.MODC[:, layer, 1, 0, path, :], C.MODC[:, layer, 1, 1, path, :],
                         hT, tg + "hT", [0, 1, 2, 3])

        def gateup(j):
            n = nsub_of(j) * 128
            for jj in range(NJ):
                b = 2 * (jj % 2)
                for gi, w_ in enumerate((wg, wu)):
                    for k in range(8):
                        em.op("pe", lambda k=k, b=b, gi=gi, w_=w_, jj=jj: T.matmul(
                            PS[:, b + gi, 0:n], lhsT=w_[:, k, jj * 128:(jj + 1) * 128], rhs=hT[:, k, 0:n],
                            start=(k == 0), stop=(k == 7)), R=[tg + "wg", tg + "wu", tg + "hT"], W=[("ps", b + gi)],
                              inc=(k == 7))
                em.op("act", lambda b=b, jj=jj: A.activation(out=hh[:, jj, 0:n], in_=PS[:, b, 0:n], func=AF.Silu),
                      R=[("ps", b)], W=[tg + f"hh{jj}"])
                em.op("dve", lambda b=b, jj=jj: V.tensor_tensor(out=hh[:, jj, 0:n], in0=hh[:, jj, 0:n], in1=PS[:, b + 1, 0:n],
                                                               op=ALU.mult), R=[("ps", b + 1), tg + f"hh{jj}"], W=[tg + f"hh{jj}"])

        def down(j, s):
            pb = 4 + 2 * (s % 2)
            for h in range(2):
                for jj in range(NJ):
                    em.op("pe", lambda jj=jj, h=h, s=s, pb=pb: T.matmul(
                        PS[:, pb + h, :], lhsT=hh[:, jj, s * 128:(s + 1) * 128], rhs=wd[:, jj, h * 512:(h + 1) * 512],
                        start=(jj == 0), stop=(jj == NJ - 1)), R=[tg + "wd", tg + f"hh{jj}"], W=[("ps", pb + h)], inc=(jj == NJ - 1))

        def eload(j, s):
            em.dma("sp", est[s % 2][:], rows(j, s)[0], W=[tg + f"est{s % 2}"])

        def epi(j, s):
            pb = 4 + 2 * (s % 2)
            C.epilogue(pb, est[s % 2][:, :], tg + f"est{s % 2}", Gx[:, :], tg + "Gx", ssq, rs, junk, tt, tg)
            em.dma("pool", rows(j, s)[1], est[s % 2][:], R=[tg + f"est{s % 2}"], W=[(outkey, j, s)])

        load(0)
        pro_a(0)
        pro_b(0)
        for j in range(ntile):
            ns = nsub_of(j)
            if j == NX:
                em.dma("sp", Gx[:], C.GROW[layer, 1, 1, :].partition_broadcast(128), W=[tg + "Gx"])
            gateup(j)
            if j + 1 < ntile:
                load(j + 1)
                pro_a(j + 1)
            eload(j, 0)
            eload(j, 1)
            down(j, 0)
            down(j, 1)
            if j + 1 < ntile:
                pro_b(j + 1)
            epi(j, 0)
            epi(j, 1)
            if ns == 4:
                eload(j, 2)
                eload(j, 3)
                down(j, 2)
                down(j, 3)
                epi(j, 2)
                epi(j, 3)


PHASES += [("B", phase_B), ("D1", phase_D1),
           ("D2", lambda C: phase_FFN(C, 0, C.X1, C.C1, C.X2, C.C2, "X1", "X2"))]


def phase_E(C):
    nc, em, PS = C.nc, C.em, C.PS
    V, A, G, T = nc.vector, nc.scalar, nc.gpsimd, nc.tensor
    with ExitStack() as ps_:
        sb = lambda n, s, d=F32: ps_.enter_context(nc.sbuf_tensor("E" + n, list(s), d))
        wq = sb("wq", [128, 8, 3 * D], BF16)
        xt = [sb(f"xt{i}", [128, 4, D]) for i in range(2)]
        hT = sb("hT", [128, 8, 512], BF16)
        qk = [sb(f"qk{i}", [128, 16, 512], BF16) for i in range(2)]
        vst = [sb(f"vst{i}", [128, 4, D], BF16) for i in range(2)]
        ssq = sb("ssq", [128, 4]); rs = sb("rs", [128, 4]); junk = sb("junk", [128, D], BF16)
        _wload_bf(C, wq, C.WBF["wqkv"].rearrange("(k p) n -> p k n", p=128), "Ewq", 3 * D, "cvwqkv")
        QTv = C.QT.rearrange("(c p) t -> p c t", p=128); KTv = C.KT.rearrange("(c p) t -> p c t", p=128)

        def load(j, sl):
            if j < NLT:
                em.dma("sp", xt[sl][:], C.X2[j * 512:(j + 1) * 512, :].rearrange("(s p) d -> p s d", p=128), W=[f"Ext{sl}"])
            else:
                em.dma("sp", xt[sl][:, 0:2, :], C.C2.rearrange("(s p) d -> p s d", p=128), W=[f"Ext{sl}"])
        load(0, 0)
        ev = 0
        for j in range(NLT + 1):
            sl = j % 2
            if j + 1 < NLT + 1:
                load(j + 1, 1 - sl)
            isctx = j == NLT
            nsub = 2 if isctx else 4
            n = nsub * 128
            path = 1 if isctx else 0
            C.prologue(xt[sl], f"Ext{sl}", nsub, xt[sl], f"Ext{sl}", C.MODC[:, 1, 0, 0, path, :], C.MODC[:, 1, 0, 1, path, :],
                       hT, "EhT", ssq, rs, junk, "E", [0, 1])
            for oc in range(8 if isctx else 0, 16) if isctx else range(16):
                b = 2 + oc % 2
                for k in range(8):
                    em.op("pe", lambda k=k, b=b, oc=oc, n=n: T.matmul(
                        PS[:, b, 0:n], lhsT=wq[:, k, oc * 128:(oc + 1) * 128], rhs=hT[:, k, 0:n],
                        start=(k == 0), stop=(k == 7)), R=["Ewq", "EhT"], W=[("ps", b)], inc=(k == 7))
                if oc < 8:
                    em.op("act", lambda b=b, oc=oc, sl=sl, n=n: A.activation(out=qk[sl][:, oc, 0:n], in_=PS[:, b, 0:n],
                                                                             func=AF.Copy, scale=0.125),
                          R=[("ps", b)], W=[f"Eqk{sl}"])
                else:
                    em.op("dve", lambda b=b, oc=oc, sl=sl, n=n: V.tensor_copy(out=qk[sl][:, oc, 0:n], in_=PS[:, b, 0:n]),
                          R=[("ps", b)], W=[f"Eqk{sl}"])
            if not isctx:
                em.dma("pool", QTv[:, :, j * 512:(j + 1) * 512], qk[sl][:, 0:8, :], R=[f"Eqk{sl}"], W=[("QT", j)])
                em.dma("pool", KTv[:, :, j * 512:(j + 1) * 512], qk[sl][:, 8:16, :], R=[f"Eqk{sl}"], W=[("KT", j)])
            else:
                em.dma("pool", KTv[:, :, NL:NL + LC], qk[sl][:, 8:16, 0:LC], R=[f"Eqk{sl}"], W=[("KT", j)])
            for s in range(nsub):
                pb = 4 + 2 * (s % 2)
                for h in range(2):
                    for k in range(8):
                        em.op("pe", lambda k=k, h=h, s=s, pb=pb: T.matmul(
                            PS[:, pb + h, :], lhsT=hT[:, k, s * 128:(s + 1) * 128], rhs=wq[:, k, 2048 + h * 512:2048 + (h + 1) * 512],
                            start=(k == 0), stop=(k == 7)), R=["Ewq", "EhT"], W=[("ps", pb + h)], inc=(k == 7))
                ev += 1
                src = PS[:, pb:pb + 2, :].rearrange("p a b -> p (a b)")
                if ev % 2:
                    em.op("dve", lambda s=s, sl=sl, src=src: V.tensor_copy(out=vst[sl][:, s, :], in_=src),
                          R=[("ps", pb), ("ps", pb + 1)], W=[f"Evst{sl}"])
                else:
                    em.op("act", lambda s=s, sl=sl, src=src: A.copy(out=vst[sl][:, s, :], in_=src),
                          R=[("ps", pb), ("ps", pb + 1)], W=[f"Evst{sl}"])
            t0 = NL if isctx else j * 512
            em.dma("pool", C.VD[t0:t0 + n, :].rearrange("(s p) d -> p s d", p=128), vst[sl][:, 0:nsub, :],
                   R=[f"Evst{sl}"], W=[("VD", j)])


def phase_F(C):
    nc, em, PS = C.nc, C.em, C.PS
    V, A, G, T = nc.vector, nc.scalar, nc.gpsimd, nc.tensor
    with ExitStack() as ps_:
        sb = lambda n, s, d=F32: ps_.enter_context(nc.sbuf_tensor("AT" + n, list(s), d))
        wo = sb("wo", [128, 8, D], BF16)
        KTb = [sb(f"KTb{i}", [128, 8, 1024], BF16) for i in range(2)]
        Vb = [sb(f"Vb{i}", [128, 8, 16, 65], BF16) for i in range(2)]
        QTb = sb("QTb", [128, 8, 512], BF16); xt = sb("xt", [128, 4, D])
        KTc = sb("KTc", [128, 8, LC], BF16); Vc = sb("Vc", [128, 2, 16, 65], BF16)
        BTi = sb("BTi", [128, 16, 5, 128], BF16); BTe = sb("BTe", [128, 16, 8, 128], BF16)
        PT = [sb(f"PT{i}", [128, 1280], BF16) for i in range(2)]
        Ot = sb("Ot", [128, D]); OTt = sb("OTt", [128, 8, 128], BF16); rden = sb("rden", [128, 4])
        Gx = sb("Gx", [128, D]); ssq = sb("ssq", [128, 4]); rs = sb("rs", [128, 4]); junk = sb("junk", [128, D], BF16)
        tt = sb("tt", [128, D])
        _wload_bf(C, wo, C.WBF["wona"].rearrange("(k p) n -> p k n", p=128), "Awo", D, "cvwona")
        em.dma("sp", Gx[:], C.GROW[1, 0, 0, :].partition_broadcast(128), W=["AGx"])
        em.dma("sp", BTi[:], C.BT_d, W=["ABTi"])
        KTv = C.KT.rearrange("(c p) t -> p c t", p=128); QTv = C.QT.rearrange("(c p) t -> p c t", p=128)
        em.dma("sp", KTc[:], KTv[:, :, NL:NL + LC], W=["AKTc"])
        for i in range(2):
            em.op("pool", lambda i=i: G.memset(Vb[i][:, :, :, 64:65], 1.0), W=[f"AVb{i}"])
        em.op("pool", lambda: G.memset(Vc[:, :, :, 64:65], 1.0), W=["AVc"])
        for c in range(2):
            em.dma("sp", Vc[:, c, :, 0:64], C.VD[NL + c * 128:NL + (c + 1) * 128, :].rearrange("p (h d) -> p h d", d=64), W=["AVc"])

        def kbof(blk):
            return 0 if blk == 0 else (56 if blk == NLT - 1 else 8 * blk - 4)

        def load(blk, sl):
            kb = kbof(blk)
            em.dma("sp", KTb[sl][:], KTv[:, :, kb * 64:kb * 64 + 1024], W=[f"AKTb{sl}"])
            for c in range(8):
                t0 = kb * 64 + c * 128
                em.dma("sp", Vb[sl][:, c, :, 0:64], C.VD[t0:t0 + 128, :].rearrange("p (h d) -> p h d", d=64), W=[f"AVb{sl}"])
        load(0, 0)
        for blk in range(NLT):
            sl = blk % 2
            r0 = 8 * blk
            edge = blk in (0, NLT - 1)
            eb = 0 if blk == 0 else 1
            nloc = 8 if edge else 5
            nch = nloc + 2
            em.dma("sp", QTb[:], QTv[:, :, r0 * 64:r0 * 64 + 512], W=["AQTb"])
            em.dma("sp", xt[:], C.X2[blk * 512:(blk + 1) * 512, :].rearrange("(s p) d -> p s d", p=128), W=["Axt"])
            if blk + 1 < NLT:
                load(blk + 1, 1 - sl)

            def sbase(h):
                return 0 if edge else 2 * (h % 2)

            def cpos(cl, h):
                if edge:
                    return (cl // 4, (cl % 4) * 128)
                return (2 * (h % 2) + cl // 4, (cl % 4) * 128)

            def qk(i, h):
                off = 0 if edge else i
                if edge and h == 0:
                    em.dma("sp", BTe[:], C.BTE_d[eb, i], W=["ABTe"])
                j = h // 2; e = h % 2
                p0, p1 = 64 * e, 64 * e + 64
                for cl in range(nch):
                    bk, co = cpos(cl, h)
                    o = PS[:, bk, co:co + 128]
                    if cl < nloc:
                        lt = KTb[sl][p0:p1, j, (off + cl) * 128:(off + cl + 1) * 128]
                    else:
                        lt = KTc[p0:p1, j, (cl - nloc) * 128:(cl - nloc + 1) * 128]
                    last = cl == nch - 1
                    em.op("pe", lambda o=o, lt=lt, j=j, p0=p0, p1=p1, i=i, cl=cl, last=last: T.matmul(
                        o, lhsT=lt, rhs=QTb[p0:p1, j, i * 128:(i + 1) * 128], start=(cl % 4 == 0),
                        stop=(edge and last)), R=[f"AKTb{sl}", "AKTc", "AQTb"], W=[("ps", bk)], inc=(edge and last))
                    if edge:
                        if cl in (3, 7):
                            em.op("pe", lambda h=h, bk=bk, cl=cl: T.matmul(
                                PS[:, bk, :], lhsT=C.identb[:, :], rhs=BTe[:, h, cl - 3:cl + 1, :].rearrange("p a b -> p (a b)"),
                                start=False, stop=True), R=["identb", "ABTe"], W=[("ps", bk)], inc=False)
                    else:
                        if cl == 3:
                            em.op("pe", lambda h=h, bk=bk: T.matmul(
                                PS[:, bk, :], lhsT=C.identb[:, :], rhs=BTi[:, h, 0:4, :].rearrange("p a b -> p (a b)"),
                                start=False, stop=True), R=["identb", "ABTi"], W=[("ps", bk)], inc=False)
                        if cl == 6:
                            em.op("pe", lambda h=h, bk=bk: T.matmul(
                                PS[:, bk, 0:128], lhsT=C.identb[:, :], rhs=BTi[:, h, 4, :],
                                start=False, stop=True), R=["identb", "ABTi"], W=[("ps", bk)], inc=True)

            def ex(i, h):
                b0 = sbase(h)
                nb = 3 if edge else 2
                src = PS[:, b0:b0 + nb, :].rearrange("p a b -> p (a b)")[:, 0:nch * 128]
                em.op("act", lambda src=src, h=h: A.activation(out=PT[h % 2][:, 0:nch * 128], in_=src, func=AF.Exp),
                      R=[("ps", b0 + q) for q in range(nb)], W=[f"APT{h % 2}"])

            def pv(i, h):
                off = 0 if edge else i
                ob = 4 + (h // 4) % 2
                so = (h % 4) * 128
                for c in range(nch):
                    rhs = Vb[sl][:, off + c, h, :] if c < nloc else Vc[:, c - nloc, h, :]
                    em.op("pe", lambda c=c, rhs=rhs, h=h, ob=ob, so=so: T.matmul(
                        PS[:, ob, so:so + 65], lhsT=PT[h % 2][:, c * 128:(c + 1) * 128], rhs=rhs,
                        start=(c == 0), stop=(c == nch - 1)), R=[f"APT{h % 2}", f"AVb{sl}", "AVc"], W=[("ps", ob)],
                          inc=(c == nch - 1))
                if h % 4 == 3:
                    em.op("dve", lambda ob=ob: V.reciprocal(out=rden[:, 0:4], in_=PS[:, ob, 64:512:128]),
                          R=[("ps", ob)], W=["Arden"])
                    for hh in range(4):
                        hd = h - 3 + hh
                        em.op("dve", lambda ob=ob, hh=hh, hd=hd: V.tensor_scalar(
                            out=Ot[:, hd * 64:(hd + 1) * 64], in0=PS[:, ob, hh * 128:hh * 128 + 64],
                            scalar1=rden[:, hh:hh + 1], scalar2=None, op0=ALU.mult), R=[("ps", ob), "Arden"], W=["AOt"])

            def fin(i):
                for k in range(8):
                    em.op("pe", lambda k=k: T.transpose(PS[:, 6 + k // 4, (k % 4) * 128:(k % 4 + 1) * 128],
                                                        Ot[:, k * 128:(k + 1) * 128], C.ident[:]),
                          R=["AOt", "ident"], W=[("ps", 6 + k // 4)], inc=(k % 4 == 3))
                for hb in range(2):
                    em.op("act" if hb else "dve",
                          (lambda hb=hb: A.copy(out=OTt[:, 4 * hb:4 * hb + 4, :].rearrange("p a b -> p (a b)"), in_=PS[:, 6 + hb, :])) if hb else
                          (lambda hb=hb: V.tensor_copy(out=OTt[:, 4 * hb:4 * hb + 4, :].rearrange("p a b -> p (a b)"), in_=PS[:, 6 + hb, :])),
                          R=[("ps", 6 + hb)], W=["AOTt"])
                for hf in range(2):
                    for k in range(8):
                        em.op("pe", lambda k=k, hf=hf: T.matmul(PS[:, 6 + hf, :], lhsT=OTt[:, k, :], rhs=wo[:, k, hf * 512:(hf + 1) * 512],
                                                                start=(k == 0), stop=(k == 7)),
                              R=["AOTt", "Awo"], W=[("ps", 6 + hf)], inc=(k == 7))
                C.epilogue(6, xt[:, i, :], "Axt", Gx[:, :], "AGx", ssq, rs, junk, tt, "A")

            items = [(i, h) for i in range(4) for h in range(16)]
            qk(*items[0])
            for n_, (i, h) in enumerate(items):
                nxt = items[n_ + 1] if n_ + 1 < len(items) else None
                if edge:
                    ex(i, h)
                    if nxt:
                        qk(*nxt)
                else:
                    if nxt:
                        qk(*nxt)
                    ex(i, h)
                pv(i, h)
                if h == 15:
                    fin(i)
            em.dma("pool", C.X3[blk * 512:(blk + 1) * 512, :].rearrange("(s p) d -> p s d", p=128), xt[:], R=["Axt"], W=[("X3", blk)])


PHASES += [("E", phase_E), ("F", phase_F),
           ("G", lambda C: phase_FFN(C, 1, C.X3, None, C.out_d, None, "X3", "OUT"))]
```

```python
import math
from contextlib import ExitStack
import numpy as np
import ml_dtypes
import concourse.bass as bass
import concourse.mybir as mybir
from concourse.bass_utils import run_bass_kernel_spmd

F32 = mybir.dt.float32
BF16 = mybir.dt.bfloat16
AF = mybir.ActivationFunctionType
ALU = mybir.AluOpType
NPBF = ml_dtypes.bfloat16

D = 1024
S = 8192
LC = 256
NT = S + LC
FH = 2816
NJ = FH // 128
EPS = 1e-6
NCORES = 8
NL = 4608
NLT = NL // 512
LBASE = 3584
NEG = -30000.0
QENG = ("pool", "dve", "pool", "dve")


class Em:
    LIMIT = 30000
    NDS = 8

    def __init__(self, nc, es):
        self.nc = nc
        self.es = es
        self.eng = dict(pe=nc.tensor, act=nc.scalar, dve=nc.vector, pool=nc.gpsimd, sp=nc.sync)
        self.sem = {}
        self.semkey = {}
        self.cnt = {}
        self.nsem = 0
        for e in self.eng:
            self._newsem(e)
        self.waited = {e: {} for e in self.eng}
        self.lastw = {}
        self.readers = {}
        self.dsem = {}
        self.ndma = {}
        for q in ("sp", "pool", "act"):
            self.dsem[q] = [es.enter_context(nc.semaphore(f"d{q}{i}")) for i in range(self.NDS)]
            self.ndma[q] = 0
        self.bg = []

    def _newsem(self, e):
        self.nsem += 1
        self.sem[e] = self.es.enter_context(self.nc.semaphore(f"s{e}{self.nsem}"))
        self.semkey[e] = (e, self.nsem)
        self.cnt[e] = 0

    def _deps(self, engine, R, W):
        deps = {}
        for k in list(R) + list(W):
            t = self.lastw.get(k)
            if t is not None:
                if t[0] not in deps or deps[t[0]][2] < t[2]:
                    deps[t[0]] = t
        for k in W:
            for t in self.readers.get(k, {}).values():
                if t[0] not in deps or deps[t[0]][2] < t[2]:
                    deps[t[0]] = t
        e = self.eng[engine]
        for sk, t in deps.items():
            if engine == "pe" and t[3] == "pe":
                continue
            if self.waited[engine].get(sk, 0) >= t[2]:
                continue
            e.wait_ge(t[1], t[2])
            self.waited[engine][sk] = t[2]

    def _record(self, tok, R, W):
        for k in W:
            self.lastw[k] = tok
            self.readers[k] = {}
        for k in R:
            d = self.readers.setdefault(k, {})
            if tok[0] not in d or d[tok[0]][2] < tok[2]:
                d[tok[0]] = tok

    def op(self, engine, fn, R=(), W=(), inc=True):
        self._deps(engine, R, W)
        ins = fn()
        if inc:
            ins.then_inc(self.sem[engine], 1)
            self.cnt[engine] += 1
            tok = (self.semkey[engine], self.sem[engine], self.cnt[engine], engine)
            self._record(tok, R, W)
            if self.cnt[engine] >= self.LIMIT:
                self._newsem(engine)
        else:
            tok = (self.semkey[engine], self.sem[engine], self.cnt[engine] + 1, engine)
            self._record(tok, R, W)
        return ins

    def dma(self, q, out, in_, R=(), W=(), **kw):
        self._deps(q, R, W)
        i = self.ndma[q]
        self.ndma[q] += 1
        sem = self.dsem[q][i % self.NDS]
        rnd = i // self.NDS
        sk = ("dma", q, i % self.NDS)
        if rnd > 0 and self.waited[q].get(sk, 0) < 16 * rnd:
            self.eng[q].wait_ge(sem, 16 * rnd)
            self.waited[q][sk] = 16 * rnd
        self.eng[q].dma_start(out=out, in_=in_, **kw).then_inc(sem, 16)
        tok = (sk, sem, 16 * (rnd + 1), None)
        self._record(tok, R, W)

    def dma_bg(self, q, out, in_, R=(), W=(), **kw):
        self._deps(q, R, W)
        sem = self.es.enter_context(self.nc.semaphore(f"bg{len(self.bg)}"))
        self.bg.append(sem)
        self.eng[q].dma_start(out=out, in_=in_, **kw).then_inc(sem, 16)
        self._record((("bg", len(self.bg)), sem, 16, None), R, W)

    def finish(self):
        sp = self.eng["sp"]
        for sem in self.bg:
            sp.wait_ge(sem, 16)
        for q in self.dsem:
            n = self.ndma[q]
            for s in range(self.NDS):
                uses = (n - s + self.NDS - 1) // self.NDS if n > s else 0
                if uses > 0:
                    sp.wait_ge(self.dsem[q][s], 16 * uses)
        for e in ("pe", "act", "dve", "pool"):
            if self.cnt[e] > 0:
                sp.wait_ge(self.sem[e], self.cnt[e])


def _tables():
    t = {}
    t["ident"] = np.eye(128, dtype=np.float32)
    t["identb"] = np.eye(128, dtype=np.float32).astype(NPBF)
    a = np.arange(128, dtype=np.float64)
    ang = 2 * np.pi * np.outer(a, a) / 128.0
    t["T1"] = np.concatenate([np.cos(ang), -np.sin(ang)], axis=1).astype(NPBF)
    t2 = np.arange(64, dtype=np.float64)[:, None, None]
    k1 = np.arange(128, dtype=np.float64)[None, :, None]
    k2 = np.arange(64, dtype=np.float64)[None, None, :]
    ph = 2 * np.pi * (t2 * k2 / 64.0 + t2 * k1 / 8192.0)
    Mc, Ms = np.cos(ph), np.sin(ph)
    t["M2"] = np.concatenate([Ms, Mc, -Ms], axis=2).astype(NPBF)
    c = np.arange(64, dtype=np.float64)
    angc = 2 * np.pi * np.outer(c, c) / 64.0
    Cc, Sc = np.cos(angc), np.sin(angc)
    z = np.zeros((64, 64))
    Cbd = np.block([[Cc, z], [z, Cc]])
    Sbd = np.block([[Sc, z], [z, Sc]])
    sx = 1.0 / math.sqrt(8192.0 * 64.0)
    t["CSf"] = (np.concatenate([Cbd, Sbd], axis=1) * sx).astype(NPBF)
    sc_ = 1.0 / math.sqrt(256.0 * 64.0)
    t["CSc"] = (np.concatenate([Cbd, Sbd], axis=1) * sc_).astype(NPBF)
    p = np.arange(256, dtype=np.float64)
    angp = 2 * np.pi * np.outer(p, p) / 256.0
    T256 = np.concatenate([np.cos(angp), -np.sin(angp)], axis=1)
    t["T256"] = T256.reshape(2, 128, 512).transpose(1, 0, 2).copy().astype(NPBF)
    return t


_TABLES = None


def _bias_tables(rpb):
    H = 16
    out = np.full((5, 5 * 128, H, 128), NEG, dtype=np.float32)
    kr = np.arange(10)[:, None, None, None]
    kc = np.arange(64)[None, :, None, None]
    qr = np.arange(2)[None, None, :, None]
    qc = np.arange(64)[None, None, None, :]
    cs = np.clip(qc - 8, 0, 48)
    for vi, (gp, ks) in enumerate([(0, 0), (2, 0), (60, 56), (124, 118), (126, 118)]):
        gq = gp + qr
        rs = np.clip(gq - 4, 0, 120)
        gk = ks + kr
        valid = (gk >= rs) & (gk < rs + 8) & (kc >= cs) & (kc < cs + 16)
        dr = np.clip(gk - gq + 7, 0, 14)
        dc = np.clip(kc - qc + 15, 0, 30)
        valid, dr, dc = np.broadcast_arrays(valid, dr, dc)
        vals = rpb[:, dr, dc]
        vals = np.where(valid[None], vals, NEG)
        out[vi] = vals.transpose(1, 2, 0, 3, 4).reshape(640, H, 128)
    bt = out.reshape(5, 5, 128, H, 128).transpose(0, 2, 3, 1, 4)
    return np.ascontiguousarray(bt).astype(NPBF)


def _edge_tables(rpb, h):
    H = 16
    out = np.empty((2, 4, 128, H, 8, 128), dtype=NPBF)
    kr = np.arange(16)[:, None, None, None]
    kc = np.arange(64)[None, :, None, None]
    qr = np.arange(2)[None, None, :, None]
    qc = np.arange(64)[None, None, None, :]
    cs = np.clip(qc - 8, 0, 48)
    for eb, (r0, kb) in enumerate([(0, 0), (64, 56)]):
        for i in range(4):
            gq = r0 + 2 * i + qr + 56 * h
            gk = kb + kr + 56 * h
            rs = np.clip(gq - 4, 0, 120)
            valid = (gk >= rs) & (gk < rs + 8) & (kc >= cs) & (kc < cs + 16)
            dr = np.clip(gk - gq + 7, 0, 14)
            dc = np.clip(kc - qc + 15, 0, 30)
            valid, dr, dc = np.broadcast_arrays(valid, dr, dc)
            vals = np.where(valid[None], rpb[:, dr, dc], NEG)
            t = vals.transpose(1, 2, 0, 3, 4).reshape(8, 128, H, 128)
            out[eb, i] = t.transpose(1, 2, 0, 3).astype(NPBF)
    return out


def _col(v, nchunk):
    return np.ascontiguousarray(np.asarray(v, np.float32).reshape(nchunk, 128).T)


def build(stop="all", debug=False):
    nc = bass.Bass("TRN2", target_bir_lowering=False)

    def din(name, shape, dt=F32):
        return nc.dram_tensor(name, list(shape), dt, kind="ExternalInput").ap()

    skind = "ExternalOutput" if debug else "Internal"

    def dscr(name, shape, dt):
        return nc.dram_tensor(name, list(shape), dt, kind=skind).ap()

    x_d = din("x", [S, D]); xloc_d = din("xloc", [NL, D]); hsel_d = din("hsel", [128, 2]); ctx_d = din("ctx", [LC, D]); ccols_d = din("ccols", [128, 16])
    wmod_d = din("w_mod", [2, D, 6 * D]); bmodc_d = din("bmodc", [128, 2, 48]); bmod_d = din("b_mod", [2, 6 * D])
    gcols_d = din("gcols", [128, 2, 2, 8]); gpm_d = din("g_post_mix", [2, D]); gpf_d = din("g_post_ffn", [2, D])
    win_d = din("w_in_ab", [D, 1536]); woab_d = din("w_out_ab", [D, D])
    wg_d = din("w_ffn_gate", [2, D, FH]); wu_d = din("w_ffn_up", [2, D, FH]); wd_d = din("w_ffn_down", [2, FH, D])
    wqkv_d = din("w_qkv_na", [D, 3 * D]); wona_d = din("w_out_na", [D, D])
    wbd_d = din("wbd", [2, 2, 4, 128, 128]); lcols_d = din("lcols", [128, 4, 2, 3]); convc_d = din("convc", [128, 4, 5]); cdiag_d = din("cdiag", [4, 4, 128, 128])
    ident_d = din("ident", [128, 128]); identb_d = din("identb", [128, 128], BF16)
    T1_d = din("T1", [128, 256], BF16); M2_d = din("M2", [64, 128, 192], BF16); CSf_d = din("CSf", [128, 256], BF16)
    CSc_d = din("CSc", [128, 256], BF16); T256_d = din("T256", [128, 2, 512], BF16)
    BT_d = din("BT", [128, 16, 5, 128], BF16); BTE_d = din("BTE", [2, 4, 128, 16, 8, 128], BF16)
    out_d = nc.dram_tensor("out", [NL, D], F32, kind="ExternalOutput").ap()

    UG = dscr("UG", [8, 128, S], F32)
    UGc = dscr("UGc", [8, 128, LC], F32)
    MIXT = dscr("MIXT", [D, NT], BF16)
    GROW = dscr("GROW", [2, 2, 2, D], F32)
    X1 = dscr("X1", [NL, D], F32); C1 = dscr("C1", [LC, D], F32)
    X2 = dscr("X2", [NL, D], F32); C2 = dscr("C2", [LC, D], F32)
    X3 = dscr("X3", [NL, D], F32)
    QT = dscr("QT", [D, NL], BF16); KT = dscr("KT", [D, NL + LC], BF16); VD = dscr("VD", [NL + LC, D], BF16)

    WBF = {"woab": dscr("woab_bf", [D, D], BF16), "wona": dscr("wona_bf", [D, D], BF16),
           "wqkv": dscr("wqkv_bf", [D, 3 * D], BF16),
           "wg": dscr("wg_bf", [2, D, FH], BF16), "wu": dscr("wu_bf", [2, D, FH], BF16), "wd": dscr("wd_bf", [2, FH, D], BF16)}

    with ExitStack() as es:
        em = Em(nc, es)

        def barrier():
            for e in ("pe", "act", "dve", "pool", "sp"):
                eng = em.eng[e]
                for f in ("pe", "act", "dve", "pool"):
                    if f != e and em.cnt[f] > 0 and em.waited[e].get(em.semkey[f], 0) < em.cnt[f]:
                        eng.wait_ge(em.sem[f], em.cnt[f]); em.waited[e][em.semkey[f]] = em.cnt[f]
                for q in em.dsem:
                    n = em.ndma[q]
                    for s_ in range(em.NDS):
                        uses = (n - s_ + em.NDS - 1) // em.NDS if n > s_ else 0
                        sk = ("dma", q, s_)
                        if uses > 0 and em.waited[e].get(sk, 0) < 16 * uses:
                            eng.wait_ge(em.dsem[q][s_], 16 * uses); em.waited[e][sk] = 16 * uses

        PS = es.enter_context(nc.psum_tensor("PS", [128, 8, 512], F32))
        ident = es.enter_context(nc.sbuf_tensor("ident_s", [128, 128], F32))
        identb = es.enter_context(nc.sbuf_tensor("identb_s", [128, 128], BF16))
        MODC = es.enter_context(nc.sbuf_tensor("MODC", [128, 2, 2, 2, 2, 8], F32))
        mhalf = es.enter_context(nc.sbuf_tensor("mhalf", [128, 8], F32))
        em.dma("sp", ident[:], ident_d, W=["ident"])
        em.dma("sp", identb[:], identb_d, W=["identb"])
        em.op("dve", lambda: nc.vector.memset(mhalf[:], -0.5), W=["mhalf"])

        V = nc.vector; A = nc.scalar; G = nc.gpsimd; T = nc.tensor

        def pbank(b):
            return ("ps", b)

        def rstd_from_ssq(ssq, rs, n, tag):
            em.op("dve", lambda: V.tensor_scalar(out=rs[:, 0:n], in0=ssq[:, 0:n], scalar1=1.0 / D, scalar2=EPS,
                                                 op0=ALU.mult, op1=ALU.add), R=[tag + "ssq"], W=[tag + "rs"])
            em.op("pool", lambda: G.tensor_tensor(out=rs[:, 0:n], in0=rs[:, 0:n], in1=mhalf[:, 0:n], op=ALU.pow),
                  R=[tag + "rs", "mhalf"], W=[tag + "rs"])

        def prologue_a(xt, xkey, nsub, xs, xskey, ssq, rs, junk, tag):
            for s in range(nsub):
                em.op("act", lambda s=s: A.activation(out=junk[:, :], in_=xt[:, s, :], func=AF.Square,
                                                      accum_out=ssq[:, s:s + 1]),
                      R=[xkey], W=[tag + "junk", tag + "ssq"])
            rstd_from_ssq(ssq, rs, nsub, tag)
            for s in range(nsub):
                em.op("act", lambda s=s: A.activation(out=xs[:, s, :], in_=xt[:, s, :], func=AF.Copy,
                                                      scale=rs[:, s:s + 1]),
                      R=[xkey, tag + "rs"], W=[xskey])

        def prologue_b(nsub, xs, xskey, Acol, Bcol, hT, hkey, tb):
            for k in range(8):
                b = tb[k % len(tb)]
                for s in range(nsub):
                    em.op("pe", lambda s=s, k=k, b=b: T.transpose(PS[:, b, s * 128:(s + 1) * 128],
                                                                  xs[:, s, k * 128:(k + 1) * 128], ident[:]),
                          R=[xskey, "ident"], W=[pbank(b)], inc=(s == nsub - 1))
                em.op("act", lambda k=k, b=b: A.activation(out=hT[:, k, 0:nsub * 128], in_=PS[:, b, 0:nsub * 128],
                                                           func=AF.Identity, scale=Acol[:, k:k + 1],
                                                           bias=Bcol[:, k:k + 1]),
                      R=[pbank(b), "MODC"], W=[hkey])

        def prologue(xt, xkey, nsub, xs, xskey, Acol, Bcol, hT, hkey, ssq, rs, junk, tag, tb):
            prologue_a(xt, xkey, nsub, xs, xskey, ssq, rs, junk, tag)
            prologue_b(nsub, xs, xskey, Acol, Bcol, hT, hkey, tb)

        def epilogue(psb, xsub, xkey, Gb, gkey, ssq, rs, junk, tt, tag):
            yv = PS[:, psb:psb + 2, :]
            em.op("act", lambda: A.activation(out=junk[:, :], in_=yv, func=AF.Square, accum_out=ssq[:, 0:1]),
                  R=[pbank(psb), pbank(psb + 1)], W=[tag + "junk", tag + "ssq"])
            rstd_from_ssq(ssq, rs, 1, tag)
            em.op("dve", lambda: V.tensor_tensor(out=tt[:, :], in0=yv, in1=Gb, op=ALU.mult),
                  R=[pbank(psb), pbank(psb + 1), gkey], W=[tag + "tt"])
            em.op("dve", lambda: V.scalar_tensor_tensor(out=xsub, in0=tt[:, :], scalar=rs[:, 0:1], in1=xsub,
                                                        op0=ALU.mult, op1=ALU.add),
                  R=[tag + "tt", tag + "rs", xkey], W=[xkey])

        with ExitStack() as pes:
            sb = lambda n, s, d=F32: pes.enter_context(nc.sbuf_tensor(n, list(s), d))
            cc = sb("cc", [128, 16]); scc = sb("scc", [128, 16]); rhs2 = sb("rhs2", [128, 8, 2], BF16)
            bmc = sb("bmc", [128, 2, 48]); bm1 = sb("bm1", [128, 2, 48]); gco = sb("gco", [128, 2, 2, 8])
            bmrow = sb("bmrow", [2, 2, 2, D]); grow = sb("grow", [2, 2, 2, D]); grt = sb("grt", [2, 2, 2, D])
            wm = [sb(f"wm{i}", [128, 8, 512], BF16) for i in range(3)]
            em.dma("sp", cc[:], ccols_d, W=["cc"])
            em.dma("sp", bmc[:], bmodc_d, W=["bmc"])
            em.dma("sp", gco[:], gcols_d, W=["gco"])
            for l in range(2):
                for w_, (src, off) in enumerate([(bmod_d, 2 * D), (bmod_d, 5 * D)]):
                    em.dma("sp", bmrow[:, l, w_, :], src[l, off:off + D].partition_broadcast(2), W=["bmrow"])
                em.dma("sp", grow[:, l, 0, :], gpm_d[l, :].partition_broadcast(2), W=["grow"])
                em.dma("sp", grow[:, l, 1, :], gpf_d[l, :].partition_broadcast(2), W=["grow"])
            em.op("act", lambda: A.activation(out=scc[:], in_=cc[:], func=AF.Silu), R=["cc"], W=["scc"])
            em.op("dve", lambda: V.tensor_copy(out=rhs2[:, :, 0], in_=scc[:, 0:8]), R=["scc"], W=["rhs2"])
            em.op("dve", lambda: V.tensor_copy(out=rhs2[:, :, 1], in_=scc[:, 8:16]), R=["scc"], W=["rhs2"])
            em.op("dve", lambda: V.tensor_scalar(out=bm1[:], in0=bmc[:], scalar1=1.0, scalar2=None, op0=ALU.add),
                  R=["bmc"], W=["bm1"])
            it = 0
            for l in range(2):
                wsrc = wmod_d[l].rearrange("(k p) n -> p k n", p=128)
                for nb in range(12):
                    slot = it % 3; it += 1
                    wt = wm[slot]; wk = f"wm{slot}"
                    em.dma("pool", wt[:], wsrc[:, :, nb * 512:(nb + 1) * 512], W=[wk])
                    v = nb // 2; half = nb % 2
                    b = it % 4
                    if v in (2, 5):
                        w_ = 0 if v == 2 else 1
                        for k in range(8):
                            em.op("pe", lambda k=k, b=b, wt=wt: T.matmul(PS[0:2, b, :], lhsT=rhs2[:, k, :], rhs=wt[:, k, :],
                                                                        start=(k == 0), stop=(k == 7)),
                                  R=["rhs2", wk], W=[pbank(b)], inc=(k == 7))
                        dst = grt[:, l, w_, half * 512:(half + 1) * 512]
                        em.op("dve", lambda b=b, dst=dst, l=l, w_=w_, half=half: V.tensor_tensor(
                            out=dst, in0=PS[0:2, b, :], in1=bmrow[:, l, w_, half * 512:(half + 1) * 512], op=ALU.add),
                              R=[pbank(b), "bmrow"], W=["grt"])
                        em.op("dve", lambda dst=dst, l=l, w_=w_, half=half: V.tensor_tensor(
                            out=dst, in0=dst, in1=grow[:, l, w_, half * 512:(half + 1) * 512], op=ALU.mult),
                              R=["grt", "grow"], W=["grt"])
                        if half == 1:
                            em.dma("sp", GROW[l, w_, :, :], grt[:, l, w_, :], R=["grt"], W=[("GROW", l, w_)])
                    else:
                        sub = 0 if v < 2 else 1
                        isA = v in (1, 4)
                        for m in range(4):
                            ch = half * 4 + m
                            for k in range(8):
                                em.op("pe", lambda k=k, b=b, m=m, wt=wt: T.matmul(
                                    PS[:, b, 2 * m:2 * m + 2], lhsT=wt[:, k, m * 128:(m + 1) * 128], rhs=rhs2[:, k, :],
                                    start=(k == 0), stop=(k == 7)), R=["rhs2", wk], W=[pbank(b)], inc=(k == 7))
                            dst = MODC[:, l, sub, 0 if isA else 1, :, ch]
                            if isA:
                                em.op("dve", lambda b=b, m=m, dst=dst, l=l, v=v, ch=ch, sub=sub: V.tensor_scalar(
                                    out=dst, in0=PS[:, b, 2 * m:2 * m + 2], scalar1=bm1[:, l, v * 8 + ch:v * 8 + ch + 1],
                                    scalar2=gco[:, l, sub, ch:ch + 1], op0=ALU.add, op1=ALU.mult),
                                      R=[pbank(b), "bm1", "gco"], W=["MODC"])
                            else:
                                em.op("dve", lambda b=b, m=m, dst=dst, l=l, v=v, ch=ch: V.tensor_scalar(
                                    out=dst, in0=PS[:, b, 2 * m:2 * m + 2], scalar1=bmc[:, l, v * 8 + ch:v * 8 + ch + 1],
                                    scalar2=None, op0=ALU.add), R=[pbank(b), "bmc"], W=["MODC"])
            if debug:
                MODCd = nc.dram_tensor("MODCd", [128, 128], F32, kind="ExternalOutput").ap()
                em.dma("sp", MODCd, MODC[:].rearrange("p a b c d e -> p (a b c d e)"), R=["MODC"], W=["MODCd"])
            barrier()
        if stop == "0":
            em.finish()
            return nc
        C = type("C", (), {})()
        C.__dict__.update(locals())
        for name, fn in PHASES:
            fn(C)
            barrier()
            if stop == name:
                if hasattr(C, 'es_mix'):
                    C.es_mix.close()
                break
        em.finish()
    return nc


PHASES = []


def _host_inputs(inp, b):
    global _TABLES
    if _TABLES is None:
        _TABLES = _tables()
    f = lambda a: np.ascontiguousarray(np.asarray(a, dtype=np.float32))
    m = {}
    b, h = b // 2, b % 2
    m["x"] = f(inp["x"][b]); m["ctx"] = f(inp["ctx"][b])
    m["xloc"] = f(inp["x"][b][LBASE * h:LBASE * h + NL])
    m["hsel"] = np.tile(np.array([[1.0 - h, float(h)]], np.float32), (128, 1))
    m["ccols"] = np.concatenate([_col(inp["c"][b], 8), _col(inp["c_ctx"], 8)], axis=1)
    m["w_mod"] = f(inp["w_mod"]); m["b_mod"] = f(inp["b_mod"])
    m["bmodc"] = np.ascontiguousarray(np.stack([_col(inp["b_mod"][l], 48) for l in range(2)], axis=1))
    m["gcols"] = np.ascontiguousarray(np.stack(
        [np.stack([_col(inp["g_pre_mix"][l], 8), _col(inp["g_pre_ffn"][l], 8)], axis=1) for l in range(2)], axis=1))
    m["g_post_mix"] = f(inp["g_post_mix"]); m["g_post_ffn"] = f(inp["g_post_ffn"])
    m["w_in_ab"] = f(inp["w_in_ab"][0]); m["w_out_ab"] = f(inp["w_out_ab"][0])
    m["w_ffn_gate"] = f(inp["w_ffn_gate"]); m["w_ffn_up"] = f(inp["w_ffn_up"]); m["w_ffn_down"] = f(inp["w_ffn_down"])
    m["w_qkv_na"] = f(inp["w_qkv_na"][0]); m["w_out_na"] = f(inp["w_out_na"][0])
    wbd = np.zeros((2, 2, 4, 128, 128), np.float32)
    for gi, key in enumerate(["lru_w_a", "lru_w_i"]):
        w = np.asarray(inp[key][0], np.float32)
        for d in range(2):
            for c in range(4):
                wbd[gi, d, c, 0:64, 0:64] = w[d, 2 * c]
                wbd[gi, d, c, 64:128, 64:128] = w[d, 2 * c + 1]
    m["wbd"] = wbd
    lc = np.zeros((128, 4, 2, 3), np.float32)
    for d in range(2):
        lc[:, :, d, 0] = _col(inp["lru_b_a"][0][d], 4)
        lc[:, :, d, 1] = _col(inp["lru_b_i"][0][d], 4)
        lc[:, :, d, 2] = _col(inp["lru_lam"][0][d], 4)
    m["lcols"] = lc
    cv = np.zeros((128, 4, 5), np.float32)
    for k in range(4):
        cv[:, :, k] = _col(inp["conv_w"][0][k], 4)
    cv[:, :, 4] = _col(inp["conv_b"][0], 4)
    m["convc"] = cv
    cd = np.zeros((4, 4, 128, 128), np.float32)
    ii = np.arange(128)
    for c in range(4):
        for k in range(4):
            cd[c, k, ii, ii] = cv[:, c, k]
    m["cdiag"] = cd
    for k in ("ident", "identb", "T1", "M2", "CSf", "CSc", "T256"):
        m[k] = _TABLES[k]
    rpb = np.asarray(inp["rpb_na"][0], np.float32)
    m["BT"] = np.ascontiguousarray(_bias_tables(rpb)[2])
    m["BTE"] = _edge_tables(rpb, h)
    return m


_NC_CACHE = {}


def kernel(**inputs):
    if "full" not in _NC_CACHE:
        _NC_CACHE["full"] = build()
    nc = _NC_CACHE["full"]
    in_maps = [_host_inputs(inputs, b) for b in range(NCORES)]
    res = run_bass_kernel_spmd(nc, in_maps, core_ids=list(range(NCORES)))
    out = np.empty((4, S, D), np.float32)
    for c in range(NCORES):
        b, h = c // 2, c % 2
        o = np.asarray(res.results[c]["out"], dtype=np.float32)
        out[b, 4096 * h:4096 * (h + 1)] = o[512 * h:512 * h + 4096]
    return out


def _wload(C, dst, src_view, key, nk, ncols, step=512):
    for c0 in range(0, ncols, step):
        c1 = min(ncols, c0 + step)
        C.em.dma("pool", dst[:, :, c0:c1], src_view[:, :, c0:c1], W=[key])


def _wload_bf(C, dst, src_view, key, ncols, srckey, step=1024):
    for c0 in range(0, ncols, step):
        c1 = min(ncols, c0 + step)
        C.em.dma("sp", dst[:, :, c0:c1], src_view[:, :, c0:c1], R=C.cvkeys[srckey], W=[key])


def preconvert(C):
    em = C.em
    jobs = [("woab", C.woab_d, C.WBF["woab"]), ("wg0", C.wg_d[0], C.WBF["wg"][0]), ("wu0", C.wu_d[0], C.WBF["wu"][0]),
            ("wd0", C.wd_d[0], C.WBF["wd"][0]), ("wqkv", C.wqkv_d, C.WBF["wqkv"]), ("wona", C.wona_d, C.WBF["wona"]),
            ("wg1", C.wg_d[1], C.WBF["wg"][1]), ("wu1", C.wu_d[1], C.WBF["wu"][1]), ("wd1", C.wd_d[1], C.WBF["wd"][1])]
    C.cvkeys = {}
    for key, src, dst in jobs:
        rows, cols = src.shape
        rstep = 512 if rows % 512 == 0 else 704
        C.cvkeys["cv" + key] = []
        for r0 in range(0, rows, rstep):
            for c0 in range(0, cols, 1024):
                c1 = min(cols, c0 + 1024)
                k_ = f"cv{key}_{r0}_{c0}"
                C.cvkeys["cv" + key].append(k_)
                em.dma_bg("pool", dst[r0:r0 + rstep, c0:c1], src[r0:r0 + rstep, c0:c1], W=[k_])


def phase_A(C):
    nc, em, PS = C.nc, C.em, C.PS
    V, A, G, T = nc.vector, nc.scalar, nc.gpsimd, nc.tensor
    pes = C.es_mix = ExitStack()
    C.F = pes.enter_context(nc.sbuf_tensor("Fbuf", [128, 64, 512], BF16))
    C.fTc = pes.enter_context(nc.sbuf_tensor("fTc", [128, 4, LC], BF16))
    F, fTc = C.F, C.fTc
    with ExitStack() as ps_:
        sb = lambda n, s, d=F32: ps_.enter_context(nc.sbuf_tensor(n, list(s), d))
        win = sb("win", [128, 8, 1536], BF16)
        xt = [sb(f"Axt{i}", [128, 4, D]) for i in range(2)]
        hT = [sb(f"AhT{i}", [128, 8, 512], BF16) for i in range(2)]
        ugst = [sb(f"Aug{i}", [128, 8, 512]) for i in range(2)]
        ssq = sb("Assq", [128, 4]); rs = sb("Ars", [128, 4]); junk = sb("Ajunk", [128, D], BF16)
        _wload(C, win, C.win_d.rearrange("(k p) n -> p k n", p=128), "win", 8, 1536)
        xsrc = C.x_d.rearrange("(t1 t2) d -> t1 t2 d", t2=64)
        Acol = C.MODC[:, 0, 0, 0, 0, :]; Bcol = C.MODC[:, 0, 0, 1, 0, :]
        AcolC = C.MODC[:, 0, 0, 0, 1, :]; BcolC = C.MODC[:, 0, 0, 1, 1, :]
        evc = [0]

        def loadA(j):
            sl = j % 2
            if j < 16:
                em.dma("sp", xt[sl][:], xsrc[:, 4 * j:4 * (j + 1), :], W=[f"Axt{sl}"])
            elif j == 16:
                em.dma("sp", xt[sl][:, 0:2, :], C.ctx_d.rearrange("(s p) d -> p s d", p=128), W=[f"Axt{sl}"])

        def nsubA(j):
            return 2 if j == 16 else 4

        def pro_a(j):
            sl = j % 2
            C.prologue_a(xt[sl], f"Axt{sl}", nsubA(j), xt[sl], f"Axt{sl}", ssq, rs, junk, "A")

        def pro_b(j):
            sl = j % 2
            isctx = j == 16
            C.prologue_b(nsubA(j), xt[sl], f"Axt{sl}", AcolC if isctx else Acol, BcolC if isctx else Bcol,
                         hT[sl], f"AhT{sl}", [0, 1])

        def ugA(j):
            sl = j % 2
            isctx = j == 16
            n = nsubA(j) * 128
            for oc in range(8):
                b = 2 + oc % 3
                for k in range(8):
                    em.op("pe", lambda k=k, b=b, oc=oc, sl=sl, n=n: T.matmul(
                        PS[:, b, 0:n], lhsT=win[:, k, oc * 128:(oc + 1) * 128], rhs=hT[sl][:, k, 0:n],
                        start=(k == 0), stop=(k == 7)), R=["win", f"AhT{sl}"], W=[("ps", b)], inc=(k == 7))
                evc[0] += 1
                if evc[0] % 2:
                    em.op("dve", lambda b=b, oc=oc, sl=sl, n=n: V.tensor_copy(out=ugst[sl][:, oc, 0:n], in_=PS[:, b, 0:n]),
                          R=[("ps", b)], W=[f"Aug{sl}"])
                else:
                    em.op("act", lambda b=b, oc=oc, sl=sl, n=n: A.copy(out=ugst[sl][:, oc, 0:n], in_=PS[:, b, 0:n]),
                          R=[("ps", b)], W=[f"Aug{sl}"])
            if isctx:
                em.dma("pool", C.UGc.rearrange("c p n -> p c n"), ugst[sl][:, :, 0:LC], R=[f"Aug{sl}"], W=["UGc"])
            else:
                em.dma("pool", C.UG[:, :, j * 512:(j + 1) * 512].rearrange("c p n -> p c n"), ugst[sl][:],
                       R=[f"Aug{sl}"], W=[("UG", j)])

        def fA(j):
            sl = j % 2
            if j == 16:
                for fc in range(4):
                    b = 5 + fc % 3
                    for k in range(8):
                        em.op("pe", lambda k=k, b=b, fc=fc, sl=sl: T.matmul(
                            PS[:, b, 0:LC], lhsT=win[:, k, 1024 + fc * 128:1024 + (fc + 1) * 128], rhs=hT[sl][:, k, 0:LC],
                            start=(k == 0), stop=(k == 7)), R=["win", f"AhT{sl}"], W=[("ps", b)], inc=(k == 7))
                    em.op("dve", lambda b=b, fc=fc: V.tensor_copy(out=fTc[:, fc, :], in_=PS[:, b, 0:LC]),
                          R=[("ps", b)], W=["fTc"])
            else:
                for s in range(4):
                    b = 5 + s % 3
                    for k in range(8):
                        em.op("pe", lambda k=k, b=b, s=s, sl=sl: T.matmul(
                            PS[:, b, :], lhsT=hT[sl][:, k, s * 128:(s + 1) * 128], rhs=win[:, k, 1024:1536],
                            start=(k == 0), stop=(k == 7)), R=["win", f"AhT{sl}"], W=[("ps", b)], inc=(k == 7))
                    evc[0] += 1
                    if evc[0] % 2:
                        em.op("dve", lambda b=b, s=s, j=j: V.tensor_copy(out=F[:, 4 * j + s, :], in_=PS[:, b, :]),
                              R=[("ps", b)], W=["F"])
                    else:
                        em.op("act", lambda b=b, s=s, j=j: A.copy(out=F[:, 4 * j + s, :], in_=PS[:, b, :]),
                              R=[("ps", b)], W=["F"])

        loadA(0)
        loadA(1)
        pro_a(0)
        pro_b(0)
        for j in range(17):
            ugA(j)
            if j + 2 < 17:
                loadA(j + 2)
            if j + 1 < 17:
                pro_a(j + 1)
            fA(j)
            if j + 1 < 17:
                pro_b(j + 1)


def phase_C(C):
    nc, em, PS, F, fTc = C.nc, C.em, C.PS, C.F, C.fTc
    V, A, G, T = nc.vector, nc.scalar, nc.gpsimd, nc.tensor
    preconvert(C)
    with ExitStack() as ps_:
        sb = lambda n, s, d=F32: ps_.enter_context(nc.sbuf_tensor(n, list(s), d))
        T1 = sb("T1s", [128, 256], BF16); M2 = sb("M2s", [64, 128, 192], BF16); CSf = sb("CSfs", [128, 256], BF16)
        CSc = sb("CScs", [128, 256], BF16); T256 = sb("T256s", [128, 2, 512], BF16)
        Ast = sb("Ast", [64, 64, 256], BF16); Y = sb("Ybuf", [128, 2, S], BF16)
        fst = [sb(f"fst{i}", [128, 2048], BF16) for i in range(2)]
        Gc = sb("Gcb", [128, 2, 256], BF16); fcs = sb("fcs", [128, LC], BF16)
        for dst, src, k in ((T1, C.T1_d, "T1"), (M2, C.M2_d, "M2"), (CSf, C.CSf_d, "CSf"), (CSc, C.CSc_d, "CSc"),
                            (T256, C.T256_d, "T256")):
            em.dma("sp", dst[:], src, W=[k])
        ev = 0
        for cc in range(4):
            for hc in range(2):
                for g4 in range(16):
                    b = 2 * (g4 % 2)
                    for q in range(4):
                        ch = cc * 128 + hc * 64 + g4 * 4 + q
                        em.op("pe", lambda b=b, q=q, ch=ch: T.matmul(
                            PS[0:64, b + q // 2, (q % 2) * 256:(q % 2) * 256 + 256], lhsT=F[:, :, ch], rhs=T1[:, :],
                            start=True, stop=True), R=["F", "T1"], W=[("ps", b), ("ps", b + 1)], inc=(q == 3))
                    ev += 1
                    dst = Ast[:, g4 * 4:(g4 + 1) * 4, :].rearrange("p a b -> p (a b)")
                    src = PS[0:64, b:b + 2, :].rearrange("p a b -> p (a b)")
                    if ev % 2:
                        em.op("dve", lambda dst=dst, src=src: V.tensor_copy(out=dst, in_=src),
                              R=[("ps", b), ("ps", b + 1)], W=["Ast"])
                    else:
                        em.op("act", lambda dst=dst, src=src: A.copy(out=dst, in_=src),
                              R=[("ps", b), ("ps", b + 1)], W=["Ast"])
                for kb in range(32):
                    b = 4 + kb % 2
                    for q in range(4):
                        k1 = kb * 4 + q
                        o = PS[hc * 64:(hc + 1) * 64, b, q * 128:(q + 1) * 128]
                        em.op("pe", lambda o=o, k1=k1: T.matmul(o, lhsT=Ast[:, :, k1], rhs=M2[:, k1, 64:192],
                                                                start=True, stop=False),
                              R=["Ast", "M2"], W=[("ps", b)], inc=False)
                        em.op("pe", lambda o=o, k1=k1: T.matmul(o, lhsT=Ast[:, :, 128 + k1], rhs=M2[:, k1, 0:128],
                                                                start=False, stop=True),
                              R=["Ast", "M2"], W=[("ps", b)], inc=(q == 3))
                    ev += 1
                    src = PS[hc * 64:(hc + 1) * 64, b, :].rearrange("p (k r c) -> p r c k", k=4, r=2)
                    dst = Y[hc * 64:(hc + 1) * 64, :, :].rearrange("p r (c k) -> p r c k", k=128)[:, :, :, kb * 4:(kb + 1) * 4]
                    if ev % 2:
                        em.op("dve", lambda dst=dst, src=src: V.tensor_copy(out=dst, in_=src), R=[("ps", b)], W=["Y"])
                    else:
                        em.op("act", lambda dst=dst, src=src: A.copy(out=dst, in_=src), R=[("ps", b)], W=["Y"])
            for tl in range(16):
                b = 6 + tl % 2
                em.op("pe", lambda b=b, tl=tl: T.matmul(PS[:, b, :], lhsT=CSf[:, 0:128], rhs=Y[:, 0, tl * 512:(tl + 1) * 512],
                                                        start=True, stop=False), R=["CSf", "Y"], W=[("ps", b)], inc=False)
                em.op("pe", lambda b=b, tl=tl: T.matmul(PS[:, b, :], lhsT=CSf[:, 128:256], rhs=Y[:, 1, tl * 512:(tl + 1) * 512],
                                                        start=False, stop=True), R=["CSf", "Y"], W=[("ps", b)], inc=True)
                fs = (tl // 4) % 2
                em.op("act", lambda b=b, tl=tl, fs=fs: A.copy(out=fst[fs][:, (tl % 4) * 512:(tl % 4 + 1) * 512], in_=PS[:, b, :]),
                      R=[("ps", b)], W=[f"fst{fs}"])
                if tl % 4 == 3:
                    t0 = (tl // 4) * 2048
                    em.dma("pool", C.MIXT[512 + cc * 128:512 + (cc + 1) * 128, t0:t0 + 2048], fst[fs][:],
                           R=[f"fst{fs}"], W=[("MIXT", 4 + cc)])
            for tc in range(2):
                em.op("pe", lambda tc=tc, cc=cc: T.matmul(PS[:, 0, 0:256], lhsT=fTc[:, cc, tc * 128:(tc + 1) * 128], rhs=CSc[:, :],
                                                          start=True, stop=True), R=["fTc", "CSc"], W=[("ps", 0)])
                em.op("dve", lambda tc=tc: V.tensor_copy(out=Gc[:, tc, :], in_=PS[:, 0, 0:256]), R=[("ps", 0)], W=["Gc"])
            for i_, (tc, part) in enumerate([(0, 0), (0, 1), (1, 0), (1, 1)]):
                em.op("pe", lambda i_=i_, tc=tc, part=part: T.matmul(
                    PS[:, 1, 0:256], lhsT=Gc[:, tc, part * 128:(part + 1) * 128], rhs=T256[:, tc, part * 256:(part + 1) * 256],
                    start=(i_ == 0), stop=(i_ == 3)), R=["Gc", "T256"], W=[("ps", 1)], inc=(i_ == 3))
            em.op("dve", lambda: V.tensor_copy(out=fcs[:, :], in_=PS[:, 1, 0:256]), R=[("ps", 1)], W=["fcs"])
            em.dma("pool", C.MIXT[512 + cc * 128:512 + (cc + 1) * 128, S:NT], fcs[:], R=["fcs"], W=[("MIXTc", 4 + cc)])
    C.es_mix.close()


PHASES += [("A", phase_A), ("C", phase_C)]


def phase_B(C):
    nc, em, PS = C.nc, C.em, C.PS
    V, A, G, T = nc.vector, nc.scalar, nc.gpsimd, nc.tensor
    TW = 2048
    with ExitStack() as ps_:
        sb = lambda n, s, d=F32: ps_.enter_context(nc.sbuf_tensor(n, list(s), d))
        bufA = sb("bufA", [128, NT]); bufB = sb("bufB", [128, 8460]); uc = sb("ucb_", [128, NT]); ucb = sb("ucbb", [128, NT], BF16)
        rt = sb("rt", [128, TW])
        it_ = [sb(f"it{i}", [128, TW]) for i in range(2)]; at = [sb(f"at{i}", [128, TW]) for i in range(2)]
        st = [sb(f"st{i}", [128, TW]) for i in range(2)]
        gtmp = [sb(f"gtmp{i}", [128, 1024]) for i in range(2)]
        wbd = sb("wbds", [128, 16, 128], BF16)
        cdg = sb("cdg", [128, 16, 128])
        lco = sb("lco", [128, 4, 2, 3]); cvc = sb("cvc", [128, 4, 5]); cA = sb("cA", [128, 4, 2]); carry = sb("carry", [128, 2])
        em.dma("pool", wbd[:], C.wbd_d.rearrange("g d c p n -> p (g d c) n"), W=["wbd"])
        em.dma("sp", cdg[:], C.cdiag_d.rearrange("c k p n -> p (c k) n"), W=["cdg"])
        em.dma("sp", lco[:], C.lcols_d, W=["lco"])
        em.dma("sp", cvc[:], C.convc_d, W=["cvc"])
        em.op("act", lambda: A.activation(out=cA[:], in_=lco[:, :, :, 2], func=AF.Exp, scale=-1.0), R=["lco"], W=["cA"])
        em.op("act", lambda: A.activation(out=cA[:], in_=cA[:], func=AF.Ln, bias=1.0), R=["cA"], W=["cA"])
        em.op("dve", lambda: V.tensor_scalar(out=cA[:], in0=cA[:], scalar1=-8.0, scalar2=None, op0=ALU.mult), R=["cA"], W=["cA"])
        em.op("dve", lambda: V.memset(bufB[:], 0.0), W=["bufB"])
        XO = 8200
        tiles = [(S, NT)] + [(i * TW, (i + 1) * TW) for i in range(4)]
        tix = 0
        for c in range(4):
            em.dma("sp", bufA[:, 0:S], C.UG[c], W=["bufA"])
            em.dma("sp", bufA[:, S:NT], C.UGc[c], W=["bufA"])
            for qd in range(4):
                o_ = bufB[:, 2 + qd * 2048:2 + (qd + 1) * 2048].rearrange("p (a b) -> p a b", b=64)
                i_ = bufA[:, 0:S].rearrange("p (b a) -> p a b", a=128)[:, qd * 32:(qd + 1) * 32, :]
                eng = QENG[qd]
                fn = {"act": (lambda o_=o_, i_=i_: A.copy(out=o_, in_=i_)),
                      "dve": (lambda o_=o_, i_=i_: V.tensor_copy(out=o_, in_=i_)),
                      "pool": (lambda o_=o_, i_=i_: G.tensor_copy(out=o_, in_=i_))}[eng]
                em.op(eng, fn, R=["bufA"], W=[f"bufBq{qd}"])
            em.op("pool", lambda: G.tensor_copy(out=bufB[:, XO + 2:XO + 2 + LC], in_=bufA[:, S:NT]), R=["bufA"], W=["bufBc"])
            em.dma("sp", bufA[:, 0:S], C.UG[4 + c], W=["bufA"])
            em.dma("sp", bufA[:, S:NT], C.UGc[4 + c], W=["bufA"])
            for (o0, src0, n) in ((0, 0, S), (S, XO, LC)):
                for q0 in range(0, n, 2048):
                    nn = min(2048, n - q0)
                    nq = (nn + 511) // 512
                    pb = 4 * ((q0 // 2048) % 2)
                    qd = q0 // 2048
                    rk = ["bufBc", "bufB"] if n == LC else [f"bufBq{x}" for x in (qd - 1, qd, qd + 1) if 0 <= x < 4] + ["bufB"]
                    for q in range(nq):
                        w_ = min(512, nn - q * 512)
                        for k in range(4):
                            em.op("pe", lambda q=q, k=k, w_=w_, pb=pb, src0=src0, q0=q0, c=c: T.matmul(
                                PS[:, pb + q, 0:w_], lhsT=cdg[:, c * 4 + k, :],
                                rhs=bufB[:, src0 + q0 + q * 512 + k:src0 + q0 + q * 512 + k + w_],
                                start=(k == 0), stop=(k == 3)), R=["cdg"] + rk, W=[("ps", pb + q)], inc=(k == 3))
                    src = PS[:, pb:pb + 4, :].rearrange("p a b -> p (a b)")[:, 0:nn]
                    em.op("act", lambda src=src, o0=o0, q0=q0, nn=nn, c=c: A.activation(
                        out=uc[:, o0 + q0:o0 + q0 + nn], in_=src, func=AF.Identity, bias=cvc[:, c, 4:5]),
                          R=[("ps", pb + q) for q in range(4)] + ["cvc"], W=["uc"])
                    em.op("act", lambda src=src, o0=o0, q0=q0, nn=nn, c=c: A.activation(
                        out=ucb[:, o0 + q0:o0 + q0 + nn], in_=src, func=AF.Identity, bias=cvc[:, c, 4:5]),
                          R=[("ps", pb + q) for q in range(4)] + ["cvc"], W=["ucb"])

            def gelu_s1(lo, hi, p):
                n = hi - lo
                em.op("act", lambda: A.activation(out=gtmp[p][:, 0:n], in_=bufA[:, lo:hi], func=AF.Square),
                      R=["bufA"], W=[f"gt{p}"])
                em.op("pool", lambda: G.tensor_scalar(out=gtmp[p][:, 0:n], in0=gtmp[p][:, 0:n], scalar1=0.044715, scalar2=1.0,
                                                      op0=ALU.mult, op1=ALU.add), R=[f"gt{p}"], W=[f"gt{p}"])
                em.op("dve", lambda: V.tensor_tensor(out=gtmp[p][:, 0:n], in0=gtmp[p][:, 0:n], in1=bufA[:, lo:hi], op=ALU.mult),
                      R=[f"gt{p}", "bufA"], W=[f"gt{p}"])

            def gelu_s2(lo, hi, p):
                n = hi - lo
                em.op("act", lambda: A.activation(out=gtmp[p][:, 0:n], in_=gtmp[p][:, 0:n], func=AF.Sigmoid,
                                                  scale=1.5957691216057308), R=[f"gt{p}"], W=[f"gt{p}"])
                em.op("dve", lambda: V.tensor_tensor(out=bufA[:, lo:hi], in0=gtmp[p][:, 0:n], in1=bufA[:, lo:hi], op=ALU.mult),
                      R=[f"gt{p}", "bufA"], W=["bufA"])

            gpieces = [(S, NT)] + [(i * 1024, (i + 1) * 1024) for i in range(8)]
            gstate = {"s1": 0, "s2": 0}

            def gelu_push1():
                k = gstate["s1"]
                if k < len(gpieces):
                    gelu_s1(gpieces[k][0], gpieces[k][1], k % 2)
                    gstate["s1"] += 1

            def gelu_push2():
                k = gstate["s2"]
                if k < gstate["s1"]:
                    gelu_s2(gpieces[k][0], gpieces[k][1], k % 2)
                    gstate["s2"] += 1

            for d in range(2):
                order = [tiles[0]] + (tiles[1:] if d == 0 else tiles[1:][::-1])
                for ti, (lo, hi) in enumerate(order):
                    p = tix % 2
                    tix += 1
                    n = hi - lo
                    nq = (n + 511) // 512
                    for gi in range(2):
                        for q in range(nq):
                            w_ = min(512, n - q * 512)
                            em.op("pe", lambda gi=gi, q=q, w_=w_, lo=lo, d=d, c=c: T.matmul(
                                PS[:, gi * 4 + q, 0:w_], lhsT=wbd[:, gi * 8 + d * 4 + c, :], rhs=ucb[:, lo + q * 512:lo + q * 512 + w_],
                                start=True, stop=True), R=["wbd", "ucb"], W=[("ps", gi * 4 + q)], inc=(q == nq - 1))
                    pr = PS[:, 0:4, :].rearrange("p a b -> p (a b)")[:, 0:n]
                    pi = PS[:, 4:8, :].rearrange("p a b -> p (a b)")[:, 0:n]
                    if d == 0:
                        gelu_push2()
                        gelu_push2()
                    em.op("act", lambda pr=pr, n=n, c=c, d=d: A.activation(out=rt[:, 0:n], in_=pr, func=AF.Sigmoid,
                                                                           bias=lco[:, c, d, 0:1]),
                          R=[("ps", q) for q in range(4)] + ["lco"], W=["rt"])
                    em.op("act", lambda pi=pi, n=n, c=c, d=d, p=p: A.activation(out=it_[p][:, 0:n], in_=pi, func=AF.Sigmoid,
                                                                                bias=lco[:, c, d, 1:2]),
                          R=[("ps", 4 + q) for q in range(4)] + ["lco"], W=[f"it{p}"])
                    em.op("act", lambda n=n, c=c, d=d, p=p: A.activation(out=at[p][:, 0:n], in_=rt[:, 0:n], func=AF.Exp,
                                                                         scale=cA[:, c, d:d + 1]), R=["rt", "cA"], W=[f"at{p}"])
                    em.op("dve", lambda n=n, p=p: V.tensor_tensor(out=st[p][:, 0:n], in0=at[p][:, 0:n], in1=at[p][:, 0:n], op=ALU.mult),
                          R=[f"at{p}"], W=[f"st{p}"])
                    if d == 0:
                        gelu_push1()
                        gelu_push1()
                    em.op("act", lambda n=n, p=p: A.activation(out=st[p][:, 0:n], in_=st[p][:, 0:n], func=AF.Sqrt, scale=-1.0, bias=1.0),
                          R=[f"st{p}"], W=[f"st{p}"])
                    em.op("dve", lambda n=n, p=p: V.tensor_tensor(out=it_[p][:, 0:n], in0=it_[p][:, 0:n], in1=st[p][:, 0:n], op=ALU.mult),
                          R=[f"it{p}", f"st{p}"], W=[f"it{p}"])
                    em.op("dve", lambda n=n, lo=lo, hi=hi, p=p: V.tensor_tensor(out=it_[p][:, 0:n], in0=it_[p][:, 0:n], in1=uc[:, lo:hi],
                                                                               op=ALU.mult), R=[f"it{p}", "uc"], W=[f"it{p}"])
                    init = 0.0 if ti == 0 else carry[:, d:d + 1]
                    if d == 0:
                        em.op("dve", lambda n=n, lo=lo, hi=hi, init=init, p=p: V.tensor_tensor_scan(
                            out=bufB[:, lo:hi], data0=at[p][:, 0:n], data1=it_[p][:, 0:n], initial=init, op0=ALU.mult, op1=ALU.add),
                              R=[f"at{p}", f"it{p}", "carry", "bufB", "uc"], W=["bufB", "bufBq0", "bufBq1", "bufBq2", "bufBq3", "bufBc"])
                        em.op("dve", lambda hi=hi: V.tensor_copy(out=carry[:, 0:1], in_=bufB[:, hi - 1:hi]), R=["bufB"], W=["carry"])
                    else:
                        em.op("dve", lambda n=n, init=init, p=p: V.tensor_tensor_scan(
                            out=st[p][:, 0:n][:, ::-1], data0=at[p][:, 0:n][:, ::-1],
                            data1=it_[p][:, 0:n][:, ::-1], initial=init, op0=ALU.mult, op1=ALU.add),
                              R=[f"at{p}", f"it{p}", "carry", f"st{p}"], W=[f"st{p}"])
                        em.op("dve", lambda p=p: V.tensor_copy(out=carry[:, 1:2], in_=st[p][:, 0:1]), R=[f"st{p}"], W=["carry"])
                        em.op("dve", lambda n=n, lo=lo, hi=hi, p=p: V.tensor_tensor(out=bufB[:, lo:hi], in0=bufB[:, lo:hi], in1=st[p][:, 0:n],
                                                                                    op=ALU.add), R=["bufB", f"st{p}"], W=["bufB"])
            while gstate["s2"] < len(gpieces):
                gelu_push1()
                gelu_push2()
            em.op("dve", lambda: V.tensor_tensor(out=ucb[:, 0:S].rearrange("p (a b) -> p a b", b=64),
                                                 in0=bufB[:, 0:S].rearrange("p (a b) -> p a b", b=64),
                                                 in1=bufA[:, 0:S].rearrange("p (b a) -> p a b", a=128), op=ALU.mult),
                  R=["bufB", "bufBq0", "bufBq1", "bufBq2", "bufBq3", "bufBc", "bufA", "ucb"], W=["ucb"])
            em.op("dve", lambda: V.tensor_tensor(out=ucb[:, S:NT], in0=bufB[:, S:NT], in1=bufA[:, S:NT], op=ALU.mult),
                  R=["bufB", "bufBq0", "bufBq1", "bufBq2", "bufBq3", "bufBc", "bufA", "ucb"], W=["ucb"])
            em.dma("pool", C.MIXT[c * 128:(c + 1) * 128, :], ucb[:, :], R=["ucb"], W=[("MIXT", c)])
            if c < 3:
                em.op("dve", lambda: V.memset(bufB[:, 0:2], 0.0), R=["bufB"], W=["bufB", "bufBq0"])
                em.op("dve", lambda: V.memset(bufB[:, S:8460], 0.0), R=["bufB"], W=["bufB", "bufBq3", "bufBc"])


def _tok_tiles(ntok_tile):
    return None


def phase_D1(C, layer=0):
    nc, em, PS = C.nc, C.em, C.PS
    V, A, G, T = nc.vector, nc.scalar, nc.gpsimd, nc.tensor
    with ExitStack() as ps_:
        sb = lambda n, s, d=F32: ps_.enter_context(nc.sbuf_tensor(n, list(s), d))
        wo = sb("D1wo", [128, 8, D], BF16)
        xt = [sb(f"D1xt{i}", [128, 4, D]) for i in range(2)]
        mt = [sb(f"D1mt{i}", [128, 8, 512], BF16) for i in range(2)]
        mb = [sb(f"D1mb{i}", [128, 8, 512], BF16) for i in range(2)]
        hs = sb("D1hs", [128, 2])
        Gx = sb("D1Gx", [128, D]); Gc = sb("D1Gc", [128, D])
        ssq = sb("D1ssq", [128, 4]); rs = sb("D1rs", [128, 4]); junk = sb("D1junk", [128, D], BF16); tt = sb("D1tt", [128, D])
        _wload_bf(C, wo, C.WBF["woab"].rearrange("(k p) n -> p k n", p=128), "D1wo", D, "cvwoab")
        em.dma("sp", hs[:], C.hsel_d, W=["D1hs"])
        em.dma("sp", Gx[:], C.GROW[0, 0, 0, :].partition_broadcast(128), R=[("GROW", 0, 0)], W=["D1Gx"])
        em.dma("sp", Gc[:], C.GROW[0, 0, 1, :].partition_broadcast(128), R=[("GROW", 0, 0)], W=["D1Gc"])
        mixv = C.MIXT.rearrange("(k p) t -> p k t", p=128)
        NTL = NLT + 1

        def load(j, sl):
            if j < NLT:
                em.dma("sp", xt[sl][:], C.xloc_d[j * 512:(j + 1) * 512, :].rearrange("(s p) d -> p s d", p=128), W=[f"D1xt{sl}"])
                em.dma("sp", mt[sl][:], mixv[:, :, j * 512:(j + 1) * 512], W=[f"D1mt{sl}"])
                em.dma("sp", mb[sl][:], mixv[:, :, LBASE + j * 512:LBASE + (j + 1) * 512], W=[f"D1mb{sl}"])
                em.op("act", lambda sl=sl: A.activation(out=mt[sl][:].rearrange("p a b -> p (a b)"),
                                                        in_=mt[sl][:].rearrange("p a b -> p (a b)"), func=AF.Copy,
                                                        scale=hs[:, 0:1]), R=[f"D1mt{sl}", "D1hs"], W=[f"D1mt{sl}"])
                em.op("dve", lambda sl=sl: V.scalar_tensor_tensor(
                    out=mt[sl][:].rearrange("p a b -> p (a b)"), in0=mb[sl][:].rearrange("p a b -> p (a b)"), scalar=hs[:, 1:2],
                    in1=mt[sl][:].rearrange("p a b -> p (a b)"), op0=ALU.mult, op1=ALU.add),
                      R=[f"D1mt{sl}", f"D1mb{sl}", "D1hs"], W=[f"D1mt{sl}"])
            else:
                em.dma("sp", xt[sl][:, 0:2, :], C.ctx_d.rearrange("(s p) d -> p s d", p=128), W=[f"D1xt{sl}"])
                em.dma("sp", mt[sl][:, :, 0:LC], mixv[:, :, S:NT], W=[f"D1mt{sl}"])
        load(0, 0)
        for j in range(NTL):
            sl = j % 2
            if j + 1 < NTL:
                load(j + 1, 1 - sl)
            isx = j < NLT
            nsub = 4 if isx else 2
            for s in range(nsub):
                pb = 2 * (s % 4)
                for h in range(2):
                    for k in range(8):
                        em.op("pe", lambda k=k, h=h, s=s, sl=sl, pb=pb: T.matmul(
                            PS[:, pb + h, :], lhsT=mt[sl][:, k, s * 128:(s + 1) * 128], rhs=wo[:, k, h * 512:(h + 1) * 512],
                            start=(k == 0), stop=(k == 7)), R=["D1wo", f"D1mt{sl}"], W=[("ps", pb + h)], inc=(k == 7))
                C.epilogue(pb, xt[sl][:, s, :], f"D1xt{sl}", (Gx if isx else Gc)[:, :], "D1Gx" if isx else "D1Gc",
                           ssq, rs, junk, tt, "D1")
            if isx:
                em.dma("pool", C.X1[j * 512:(j + 1) * 512, :].rearrange("(s p) d -> p s d", p=128), xt[sl][:],
                       R=[f"D1xt{sl}"], W=[("X1", j)])
            else:
                em.dma("pool", C.C1.rearrange("(s p) d -> p s d", p=128), xt[sl][:, 0:2, :], R=[f"D1xt{sl}"], W=["C1"])


def phase_FFN(C, layer, Xin, Cin, Xout, Cout, inkey, outkey):
    nc, em, PS = C.nc, C.em, C.PS
    V, A, G, T = nc.vector, nc.scalar, nc.gpsimd, nc.tensor
    tg = f"F{layer}"
    with ExitStack() as ps_:
        sb = lambda n, s, d=F32: ps_.enter_context(nc.sbuf_tensor(tg + n, list(s), d))
        wg = sb("wg", [128, 8, FH], BF16); wu = sb("wu", [128, 8, FH], BF16); wd = sb("wd", [128, NJ, D], BF16)
        xt = sb("xt", [128, 4, D]); hT = sb("hT", [128, 8, 512], BF16); hh = sb("hh", [128, NJ, 512], BF16)
        est = [sb(f"est{i}", [128, D]) for i in range(2)]
        Gx = sb("Gx", [128, D]); tt = sb("tt", [128, D]); junk = sb("junk", [128, D], BF16)
        ssq = sb("ssq", [128, 4]); rs = sb("rs", [128, 4]); ssq2 = sb("ssq2", [128, 4]); rs2 = sb("rs2", [128, 4])
        _wload_bf(C, wg, C.WBF["wg"][layer].rearrange("(k p) n -> p k n", p=128), tg + "wg", FH, f"cvwg{layer}", 1408)
        _wload_bf(C, wu, C.WBF["wu"][layer].rearrange("(k p) n -> p k n", p=128), tg + "wu", FH, f"cvwu{layer}", 1408)
        _wload_bf(C, wd, C.WBF["wd"][layer].rearrange("(k p) n -> p k n", p=128), tg + "wd", D, f"cvwd{layer}", 512)
        em.dma("sp", Gx[:], C.GROW[layer, 1, 0, :].partition_broadcast(128), W=[tg + "Gx"])
        NX = NL // 512
        ntile = NX + (1 if Cin is not None else 0)

        def nsub_of(j):
            return 4 if j < NX else 2

        def rows(j, s):
            if j < NX:
                r0 = j * 512 + s * 128
                return Xin[r0:r0 + 128, :], Xout[r0:r0 + 128, :]
            return Cin[s * 128:(s + 1) * 128, :], Cout[s * 128:(s + 1) * 128, :]

        def load(j):
            ns = nsub_of(j)
            src = Xin[j * 512:(j + 1) * 512, :] if j < NX else Cin
            em.dma("sp", xt[:, 0:ns, :], src.rearrange("(s p) d -> p s d", p=128), W=[tg + "xt"])

        def pro_a(j):
            ns = nsub_of(j)
            C.prologue_a(xt, tg + "xt", ns, xt, tg + "xt", ssq2, rs2, junk, tg + "p")

        def pro_b(j):
            path = 0 if j < NX else 1
            C.prologue_b(nsub_of(j), xt, tg + "xt", C.MODC[:, layer, 1, 0, path, :], C.MODC[:, layer, 1, 1, path, :],
                         hT, tg + "hT", [0, 1, 2, 3])

        def gateup(j):
            n = nsub_of(j) * 128
            for jj in range(NJ):
                b = 2 * (jj % 2)
                for gi, w_ in enumerate((wg, wu)):
                    for k in range(8):
                        em.op("pe", lambda k=k, b=b, gi=gi, w_=w_, jj=jj: T.matmul(
                            PS[:, b + gi, 0:n], lhsT=w_[:, k, jj * 128:(jj + 1) * 128], rhs=hT[:, k, 0:n],
                            start=(k == 0), stop=(k == 7)), R=[tg + "wg", tg + "wu", tg + "hT"], W=[("ps", b + gi)],
                              inc=(k == 7))
                em.op("act", lambda b=b, jj=jj: A.activation(out=hh[:, jj, 0:n], in_=PS[:, b, 0:n], func=AF.Silu),
                      R=[("ps", b)], W=[tg + f"hh{jj}"])
                em.op("dve", lambda b=b, jj=jj: V.tensor_tensor(out=hh[:, jj, 0:n], in0=hh[:, jj, 0:n], in1=PS[:, b + 1, 0:n],
                                                               op=ALU.mult), R=[("ps", b + 1), tg + f"hh{jj}"], W=[tg + f"hh{jj}"])

        def down(j, s):
            pb = 4 + 2 * (s % 2)
            for h in range(2):
                for jj in range(NJ):
                    em.op("pe", lambda jj=jj, h=h, s=s, pb=pb: T.matmul(
                        PS[:, pb + h, :], lhsT=hh[:, jj, s * 128:(s + 1) * 128], rhs=wd[:, jj, h * 512:(h + 1) * 512],
                        start=(jj == 0), stop=(jj == NJ - 1)), R=[tg + "wd", tg + f"hh{jj}"], W=[("ps", pb + h)], inc=(jj == NJ - 1))

        def eload(j, s):
            em.dma("sp", est[s % 2][:], rows(j, s)[0], W=[tg + f"est{s % 2}"])

        def epi(j, s):
            pb = 4 + 2 * (s % 2)
            C.epilogue(pb, est[s % 2][:, :], tg + f"est{s % 2}", Gx[:, :], tg + "Gx", ssq, rs, junk, tt, tg)
            em.dma("pool", rows(j, s)[1], est[s % 2][:], R=[tg + f"est{s % 2}"], W=[(outkey, j, s)])

        load(0)
        pro_a(0)
        pro_b(0)
        for j in range(ntile):
            ns = nsub_of(j)
            if j == NX:
                em.dma("sp", Gx[:], C.GROW[layer, 1, 1, :].partition_broadcast(128), W=[tg + "Gx"])
            gateup(j)
            if j + 1 < ntile:
                load(j + 1)
                pro_a(j + 1)
            eload(j, 0)
            eload(j, 1)
            down(j, 0)
            down(j, 1)
            if j + 1 < ntile:
                pro_b(j + 1)
            epi(j, 0)
            epi(j, 1)
            if ns == 4:
                eload(j, 2)
                eload(j, 3)
                down(j, 2)
                down(j, 3)
                epi(j, 2)
                epi(j, 3)


PHASES += [("B", phase_B), ("D1", phase_D1),
           ("D2", lambda C: phase_FFN(C, 0, C.X1, C.C1, C.X2, C.C2, "X1", "X2"))]


def phase_E(C):
    nc, em, PS = C.nc, C.em, C.PS
    V, A, G, T = nc.vector, nc.scalar, nc.gpsimd, nc.tensor
    with ExitStack() as ps_:
        sb = lambda n, s, d=F32: ps_.enter_context(nc.sbuf_tensor("E" + n, list(s), d))
        wq = sb("wq", [128, 8, 3 * D], BF16)
        xt = [sb(f"xt{i}", [128, 4, D]) for i in range(2)]
        hT = [sb(f"hT{i}", [128, 8, 512], BF16) for i in range(2)]
        qk = [sb(f"qk{i}", [128, 16, 512], BF16) for i in range(2)]
        vst = [sb(f"vst{i}", [128, 4, D], BF16) for i in range(2)]
        ssq = sb("ssq", [128, 4]); rs = sb("rs", [128, 4]); junk = sb("junk", [128, D], BF16)
        _wload_bf(C, wq, C.WBF["wqkv"].rearrange("(k p) n -> p k n", p=128), "Ewq", 3 * D, "cvwqkv")
        QTv = C.QT.rearrange("(c p) t -> p c t", p=128); KTv = C.KT.rearrange("(c p) t -> p c t", p=128)
        NTL = NLT + 1
        evc = [0]

        def nsubE(j):
            return 2 if j == NLT else 4

        def load(j):
            sl = j % 2
            if j < NLT:
                em.dma("sp", xt[sl][:], C.X2[j * 512:(j + 1) * 512, :].rearrange("(s p) d -> p s d", p=128), W=[f"Ext{sl}"])
            else:
                em.dma("sp", xt[sl][:, 0:2, :], C.C2.rearrange("(s p) d -> p s d", p=128), W=[f"Ext{sl}"])

        def pro_a(j):
            sl = j % 2
            C.prologue_a(xt[sl], f"Ext{sl}", nsubE(j), xt[sl], f"Ext{sl}", ssq, rs, junk, "E")

        def pro_b(j):
            sl = j % 2
            path = 1 if j == NLT else 0
            C.prologue_b(nsubE(j), xt[sl], f"Ext{sl}", C.MODC[:, 1, 0, 0, path, :], C.MODC[:, 1, 0, 1, path, :],
                         hT[sl], f"EhT{sl}", [0, 1])

        def qkE(j):
            sl = j % 2
            isctx = j == NLT
            n = nsubE(j) * 128
            for oc in (range(8, 16) if isctx else range(16)):
                b = 2 + oc % 2
                for k in range(8):
                    em.op("pe", lambda k=k, b=b, oc=oc, n=n, sl=sl: T.matmul(
                        PS[:, b, 0:n], lhsT=wq[:, k, oc * 128:(oc + 1) * 128], rhs=hT[sl][:, k, 0:n],
                        start=(k == 0), stop=(k == 7)), R=["Ewq", f"EhT{sl}"], W=[("ps", b)], inc=(k == 7))
                if oc < 8:
                    em.op("act", lambda b=b, oc=oc, sl=sl, n=n: A.activation(out=qk[sl][:, oc, 0:n], in_=PS[:, b, 0:n],
                                                                             func=AF.Copy, scale=0.125),
                          R=[("ps", b)], W=[f"Eqk{sl}"])
                else:
                    em.op("dve", lambda b=b, oc=oc, sl=sl, n=n: V.tensor_copy(out=qk[sl][:, oc, 0:n], in_=PS[:, b, 0:n]),
                          R=[("ps", b)], W=[f"Eqk{sl}"])
            if not isctx:
                em.dma("pool", QTv[:, :, j * 512:(j + 1) * 512], qk[sl][:, 0:8, :], R=[f"Eqk{sl}"], W=[("QT", j)])
                em.dma("pool", KTv[:, :, j * 512:(j + 1) * 512], qk[sl][:, 8:16, :], R=[f"Eqk{sl}"], W=[("KT", j)])
            else:
                em.dma("pool", KTv[:, :, NL:NL + LC], qk[sl][:, 8:16, 0:LC], R=[f"Eqk{sl}"], W=[("KT", j)])

        def vE(j):
            sl = j % 2
            isctx = j == NLT
            nsub = nsubE(j)
            for s in range(nsub):
                pb = 4 + 2 * (s % 2)
                for h in range(2):
                    for k in range(8):
                        em.op("pe", lambda k=k, h=h, s=s, pb=pb, sl=sl: T.matmul(
                            PS[:, pb + h, :], lhsT=hT[sl][:, k, s * 128:(s + 1) * 128], rhs=wq[:, k, 2048 + h * 512:2048 + (h + 1) * 512],
                            start=(k == 0), stop=(k == 7)), R=["Ewq", f"EhT{sl}"], W=[("ps", pb + h)], inc=(k == 7))
                evc[0] += 1
                src = PS[:, pb:pb + 2, :].rearrange("p a b -> p (a b)")
                if evc[0] % 2:
                    em.op("dve", lambda s=s, sl=sl, src=src: V.tensor_copy(out=vst[sl][:, s, :], in_=src),
                          R=[("ps", pb), ("ps", pb + 1)], W=[f"Evst{sl}"])
                else:
                    em.op("act", lambda s=s, sl=sl, src=src: A.copy(out=vst[sl][:, s, :], in_=src),
                          R=[("ps", pb), ("ps", pb + 1)], W=[f"Evst{sl}"])
            t0 = NL if isctx else j * 512
            em.dma("pool", C.VD[t0:t0 + nsub * 128, :].rearrange("(s p) d -> p s d", p=128), vst[sl][:, 0:nsub, :],
                   R=[f"Evst{sl}"], W=[("VD", j)])

        load(0)
        load(1)
        pro_a(0)
        pro_b(0)
        for j in range(NTL):
            qkE(j)
            if j + 2 < NTL:
                load(j + 2)
            if j + 1 < NTL:
                pro_a(j + 1)
            vE(j)
            if j + 1 < NTL:
                pro_b(j + 1)


def phase_F(C):
    nc, em, PS = C.nc, C.em, C.PS
    V, A, G, T = nc.vector, nc.scalar, nc.gpsimd, nc.tensor
    with ExitStack() as ps_:
        sb = lambda n, s, d=F32: ps_.enter_context(nc.sbuf_tensor("AT" + n, list(s), d))
        wo = sb("wo", [128, 8, D], BF16)
        KTb = [sb(f"KTb{i}", [128, 8, 1024], BF16) for i in range(2)]
        Vb = [sb(f"Vb{i}", [128, 8, 16, 65], BF16) for i in range(2)]
        QTb = sb("QTb", [128, 8, 512], BF16); xt = sb("xt", [128, 4, D])
        KTc = sb("KTc", [128, 8, LC], BF16); Vc = sb("Vc", [128, 2, 16, 65], BF16)
        BTi = sb("BTi", [128, 16, 5, 128], BF16); BTe = sb("BTe", [128, 16, 8, 128], BF16)
        PT = [sb(f"PT{i}", [128, 1280], BF16) for i in range(2)]
        Ot = sb("Ot", [128, D]); OTt = sb("OTt", [128, 8, 128], BF16); rden = sb("rden", [128, 4])
        Gx = sb("Gx", [128, D]); ssq = sb("ssq", [128, 4]); rs = sb("rs", [128, 4]); junk = sb("junk", [128, D], BF16)
        tt = sb("tt", [128, D])
        _wload_bf(C, wo, C.WBF["wona"].rearrange("(k p) n -> p k n", p=128), "Awo", D, "cvwona")
        em.dma("sp", Gx[:], C.GROW[1, 0, 0, :].partition_broadcast(128), W=["AGx"])
        em.dma("sp", BTi[:], C.BT_d, W=["ABTi"])
        KTv = C.KT.rearrange("(c p) t -> p c t", p=128); QTv = C.QT.rearrange("(c p) t -> p c t", p=128)
        em.dma("sp", KTc[:], KTv[:, :, NL:NL + LC], W=["AKTc"])
        for i in range(2):
            em.op("pool", lambda i=i: G.memset(Vb[i][:, :, :, 64:65], 1.0), W=[f"AVb{i}"])
        em.op("pool", lambda: G.memset(Vc[:, :, :, 64:65], 1.0), W=["AVc"])
        for c in range(2):
            em.dma("sp", Vc[:, c, :, 0:64], C.VD[NL + c * 128:NL + (c + 1) * 128, :].rearrange("p (h d) -> p h d", d=64), W=["AVc"])

        def kbof(blk):
            return 0 if blk == 0 else (56 if blk == NLT - 1 else 8 * blk - 4)

        def load(blk, sl):
            kb = kbof(blk)
            em.dma("sp", KTb[sl][:], KTv[:, :, kb * 64:kb * 64 + 1024], W=[f"AKTb{sl}"])
            for c in range(8):
                t0 = kb * 64 + c * 128
                em.dma("sp", Vb[sl][:, c, :, 0:64], C.VD[t0:t0 + 128, :].rearrange("p (h d) -> p h d", d=64), W=[f"AVb{sl}"])
        load(0, 0)
        for blk in range(NLT):
            sl = blk % 2
            r0 = 8 * blk
            edge = blk in (0, NLT - 1)
            eb = 0 if blk == 0 else 1
            nloc = 8 if edge else 5
            nch = nloc + 2
            em.dma("sp", QTb[:], QTv[:, :, r0 * 64:r0 * 64 + 512], W=["AQTb"])
            em.dma("sp", xt[:], C.X2[blk * 512:(blk + 1) * 512, :].rearrange("(s p) d -> p s d", p=128), W=["Axt"])
            if blk + 1 < NLT:
                load(blk + 1, 1 - sl)

            def sbase(h):
                return 0 if edge else 2 * (h % 2)

            def cpos(cl, h):
                if edge:
                    return (cl // 4, (cl % 4) * 128)
                return (2 * (h % 2) + cl // 4, (cl % 4) * 128)

            def qk(i, h):
                off = 0 if edge else i
                if edge and h == 0:
                    em.dma("sp", BTe[:], C.BTE_d[eb, i], W=["ABTe"])
                j = h // 2; e = h % 2
                p0, p1 = 64 * e, 64 * e + 64
                for cl in range(nch):
                    bk, co = cpos(cl, h)
                    o = PS[:, bk, co:co + 128]
                    if cl < nloc:
                        lt = KTb[sl][p0:p1, j, (off + cl) * 128:(off + cl + 1) * 128]
                    else:
                        lt = KTc[p0:p1, j, (cl - nloc) * 128:(cl - nloc + 1) * 128]
                    last = cl == nch - 1
                    em.op("pe", lambda o=o, lt=lt, j=j, p0=p0, p1=p1, i=i, cl=cl, last=last: T.matmul(
                        o, lhsT=lt, rhs=QTb[p0:p1, j, i * 128:(i + 1) * 128], start=(cl % 4 == 0),
                        stop=(edge and last)), R=[f"AKTb{sl}", "AKTc", "AQTb"], W=[("ps", bk)], inc=(edge and last))
                    if edge:
                        if cl in (3, 7):
                            em.op("pe", lambda h=h, bk=bk, cl=cl: T.matmul(
                                PS[:, bk, :], lhsT=C.identb[:, :], rhs=BTe[:, h, cl - 3:cl + 1, :].rearrange("p a b -> p (a b)"),
                                start=False, stop=True), R=["identb", "ABTe"], W=[("ps", bk)], inc=False)
                    else:
                        if cl == 3:
                            em.op("pe", lambda h=h, bk=bk: T.matmul(
                                PS[:, bk, :], lhsT=C.identb[:, :], rhs=BTi[:, h, 0:4, :].rearrange("p a b -> p (a b)"),
                                start=False, stop=True), R=["identb", "ABTi"], W=[("ps", bk)], inc=False)
                        if cl == 6:
                            em.op("pe", lambda h=h, bk=bk: T.matmul(
                                PS[:, bk, 0:128], lhsT=C.identb[:, :], rhs=BTi[:, h, 4, :],
                                start=False, stop=True), R=["identb", "ABTi"], W=[("ps", bk)], inc=True)

            def ex(i, h):
                b0 = sbase(h)
                nb = 3 if edge else 2
                src = PS[:, b0:b0 + nb, :].rearrange("p a b -> p (a b)")[:, 0:nch * 128]
                em.op("act", lambda src=src, h=h: A.activation(out=PT[h % 2][:, 0:nch * 128], in_=src, func=AF.Exp),
                      R=[("ps", b0 + q) for q in range(nb)], W=[f"APT{h % 2}"])

            def pv(i, h):
                off = 0 if edge else i
                ob = 4 + (h // 4) % 2
                so = (h % 4) * 128
                for c in range(nch):
                    rhs = Vb[sl][:, off + c, h, :] if c < nloc else Vc[:, c - nloc, h, :]
                    em.op("pe", lambda c=c, rhs=rhs, h=h, ob=ob, so=so: T.matmul(
                        PS[:, ob, so:so + 65], lhsT=PT[h % 2][:, c * 128:(c + 1) * 128], rhs=rhs,
                        start=(c == 0), stop=(c == nch - 1)), R=[f"APT{h % 2}", f"AVb{sl}", "AVc"], W=[("ps", ob)],
                          inc=(c == nch - 1))
                if h % 4 == 3:
                    em.op("dve", lambda ob=ob: V.reciprocal(out=rden[:, 0:4], in_=PS[:, ob, 64:512:128]),
                          R=[("ps", ob)], W=["Arden"])
                    for hh in range(4):
                        hd = h - 3 + hh
                        em.op("dve", lambda ob=ob, hh=hh, hd=hd: V.tensor_scalar(
                            out=Ot[:, hd * 64:(hd + 1) * 64], in0=PS[:, ob, hh * 128:hh * 128 + 64],
                            scalar1=rden[:, hh:hh + 1], scalar2=None, op0=ALU.mult), R=[("ps", ob), "Arden"], W=["AOt"])

            def fin(i):
                for k in range(8):
                    em.op("pe", lambda k=k: T.transpose(PS[:, 6 + k // 4, (k % 4) * 128:(k % 4 + 1) * 128],
                                                        Ot[:, k * 128:(k + 1) * 128], C.ident[:]),
                          R=["AOt", "ident"], W=[("ps", 6 + k // 4)], inc=(k % 4 == 3))
                for hb in range(2):
                    em.op("act" if hb else "dve",
                          (lambda hb=hb: A.copy(out=OTt[:, 4 * hb:4 * hb + 4, :].rearrange("p a b -> p (a b)"), in_=PS[:, 6 + hb, :])) if hb else
                          (lambda hb=hb: V.tensor_copy(out=OTt[:, 4 * hb:4 * hb + 4, :].rearrange("p a b -> p (a b)"), in_=PS[:, 6 + hb, :])),
                          R=[("ps", 6 + hb)], W=["AOTt"])
                for hf in range(2):
                    for k in range(8):
                        em.op("pe", lambda k=k, hf=hf: T.matmul(PS[:, 6 + hf, :], lhsT=OTt[:, k, :], rhs=wo[:, k, hf * 512:(hf + 1) * 512],
                                                                start=(k == 0), stop=(k == 7)),
                              R=["AOTt", "Awo"], W=[("ps", 6 + hf)], inc=(k == 7))
                C.epilogue(6, xt[:, i, :], "Axt", Gx[:, :], "AGx", ssq, rs, junk, tt, "A")

            items = [(i, h) for i in range(4) for h in range(16)]
            qk(*items[0])
            for n_, (i, h) in enumerate(items):
                nxt = items[n_ + 1] if n_ + 1 < len(items) else None
                if edge:
                    ex(i, h)
                    if nxt:
                        qk(*nxt)
                else:
                    if nxt:
                        qk(*nxt)
                    ex(i, h)
                pv(i, h)
                if h == 15:
                    fin(i)
            em.dma("pool", C.X3[blk * 512:(blk + 1) * 512, :].rearrange("(s p) d -> p s d", p=128), xt[:], R=["Axt"], W=[("X3", blk)])


PHASES += [("E", phase_E), ("F", phase_F),
           ("G", lambda C: phase_FFN(C, 1, C.X3, None, C.out_d, None, "X3", "OUT"))]
```

```python
import math
from contextlib import ExitStack
import numpy as np
import ml_dtypes
import concourse.bass as bass
import concourse.mybir as mybir
from concourse.bass_utils import run_bass_kernel_spmd

F32 = mybir.dt.float32
BF16 = mybir.dt.bfloat16
AF = mybir.ActivationFunctionType
ALU = mybir.AluOpType
NPBF = ml_dtypes.bfloat16

D = 1024
S = 8192
LC = 256
NT = S + LC
FH = 2816
NJ = FH // 128
EPS = 1e-6
NCORES = 8
NL = 4608
NLT = NL // 512
LBASE = 3584
NEG = -30000.0
QENG = ("pool", "dve", "pool", "dve")


class Em:
    LIMIT = 30000
    NDS = 8

    def __init__(self, nc, es):
        self.nc = nc
        self.es = es
        self.eng = dict(pe=nc.tensor, act=nc.scalar, dve=nc.vector, pool=nc.gpsimd, sp=nc.sync)
        self.sem = {}
        self.semkey = {}
        self.cnt = {}
        self.nsem = 0
        for e in self.eng:
            self._newsem(e)
        self.waited = {e: {} for e in self.eng}
        self.lastw = {}
        self.readers = {}
        self.dsem = {}
        self.ndma = {}
        for q in ("sp", "pool", "act"):
            self.dsem[q] = [es.enter_context(nc.semaphore(f"d{q}{i}")) for i in range(self.NDS)]
            self.ndma[q] = 0
        self.bg = []

    def _newsem(self, e):
        self.nsem += 1
        self.sem[e] = self.es.enter_context(self.nc.semaphore(f"s{e}{self.nsem}"))
        self.semkey[e] = (e, self.nsem)
        self.cnt[e] = 0

    def _deps(self, engine, R, W):
        deps = {}
        for k in list(R) + list(W):
            t = self.lastw.get(k)
            if t is not None:
                if t[0] not in deps or deps[t[0]][2] < t[2]:
                    deps[t[0]] = t
        for k in W:
            for t in self.readers.get(k, {}).values():
                if t[0] not in deps or deps[t[0]][2] < t[2]:
                    deps[t[0]] = t
        e = self.eng[engine]
        for sk, t in deps.items():
            if engine == "pe" and t[3] == "pe":
                continue
            if self.waited[engine].get(sk, 0) >= t[2]:
                continue
            e.wait_ge(t[1], t[2])
            self.waited[engine][sk] = t[2]

    def _record(self, tok, R, W):
        for k in W:
            self.lastw[k] = tok
            self.readers[k] = {}
        for k in R:
            d = self.readers.setdefault(k, {})
            if tok[0] not in d or d[tok[0]][2] < tok[2]:
                d[tok[0]] = tok

    def op(self, engine, fn, R=(), W=(), inc=True):
        self._deps(engine, R, W)
        ins = fn()
        if inc:
            ins.then_inc(self.sem[engine], 1)
            self.cnt[engine] += 1
            tok = (self.semkey[engine], self.sem[engine], self.cnt[engine], engine)
            self._record(tok, R, W)
            if self.cnt[engine] >= self.LIMIT:
                self._newsem(engine)
        else:
            tok = (self.semkey[engine], self.sem[engine], self.cnt[engine] + 1, engine)
            self._record(tok, R, W)
        return ins

    def dma(self, q, out, in_, R=(), W=(), **kw):
        self._deps(q, R, W)
        i = self.ndma[q]
        self.ndma[q] += 1
        sem = self.dsem[q][i % self.NDS]
        rnd = i // self.NDS
        sk = ("dma", q, i % self.NDS)
        if rnd > 0 and self.waited[q].get(sk, 0) < 16 * rnd:
            self.eng[q].wait_ge(sem, 16 * rnd)
            self.waited[q][sk] = 16 * rnd
        self.eng[q].dma_start(out=out, in_=in_, **kw).then_inc(sem, 16)
        tok = (sk, sem, 16 * (rnd + 1), None)
        self._record(tok, R, W)

    def dma_bg(self, q, out, in_, R=(), W=(), **kw):
        self._deps(q, R, W)
        sem = self.es.enter_context(self.nc.semaphore(f"bg{len(self.bg)}"))
        self.bg.append(sem)
        self.eng[q].dma_start(out=out, in_=in_, **kw).then_inc(sem, 16)
        self._record((("bg", len(self.bg)), sem, 16, None), R, W)

    def finish(self):
        sp = self.eng["sp"]
        for sem in self.bg:
            sp.wait_ge(sem, 16)
        for q in self.dsem:
            n = self.ndma[q]
            for s in range(self.NDS):
                uses = (n - s + self.NDS - 1) // self.NDS if n > s else 0
                if uses > 0:
                    sp.wait_ge(self.dsem[q][s], 16 * uses)
        for e in ("pe", "act", "dve", "pool"):
            if self.cnt[e] > 0:
                sp.wait_ge(self.sem[e], self.cnt[e])


def _tables():
    t = {}
    t["ident"] = np.eye(128, dtype=np.float32)
    t["identb"] = np.eye(128, dtype=np.float32).astype(NPBF)
    a = np.arange(128, dtype=np.float64)
    ang = 2 * np.pi * np.outer(a, a) / 128.0
    t["T1"] = np.concatenate([np.cos(ang), -np.sin(ang)], axis=1).astype(NPBF)
    t2 = np.arange(64, dtype=np.float64)[:, None, None]
    k1 = np.arange(128, dtype=np.float64)[None, :, None]
    k2 = np.arange(64, dtype=np.float64)[None, None, :]
    ph = 2 * np.pi * (t2 * k2 / 64.0 + t2 * k1 / 8192.0)
    Mc, Ms = np.cos(ph), np.sin(ph)
    t["M2"] = np.concatenate([Ms, Mc, -Ms], axis=2).astype(NPBF)
    c = np.arange(64, dtype=np.float64)
    angc = 2 * np.pi * np.outer(c, c) / 64.0
    Cc, Sc = np.cos(angc), np.sin(angc)
    z = np.zeros((64, 64))
    Cbd = np.block([[Cc, z], [z, Cc]])
    Sbd = np.block([[Sc, z], [z, Sc]])
    sx = 1.0 / math.sqrt(8192.0 * 64.0)
    t["CSf"] = (np.concatenate([Cbd, Sbd], axis=1) * sx).astype(NPBF)
    sc_ = 1.0 / math.sqrt(256.0 * 64.0)
    t["CSc"] = (np.concatenate([Cbd, Sbd], axis=1) * sc_).astype(NPBF)
    p = np.arange(256, dtype=np.float64)
    angp = 2 * np.pi * np.outer(p, p) / 256.0
    T256 = np.concatenate([np.cos(angp), -np.sin(angp)], axis=1)
    t["T256"] = T256.reshape(2, 128, 512).transpose(1, 0, 2).copy().astype(NPBF)
    return t


_TABLES = None


def _bias_tables(rpb):
    H = 16
    out = np.full((5, 5 * 128, H, 128), NEG, dtype=np.float32)
    kr = np.arange(10)[:, None, None, None]
    kc = np.arange(64)[None, :, None, None]
    qr = np.arange(2)[None, None, :, None]
    qc = np.arange(64)[None, None, None, :]
    cs = np.clip(qc - 8, 0, 48)
    for vi, (gp, ks) in enumerate([(0, 0), (2, 0), (60, 56), (124, 118), (126, 118)]):
        gq = gp + qr
        rs = np.clip(gq - 4, 0, 120)
        gk = ks + kr
        valid = (gk >= rs) & (gk < rs + 8) & (kc >= cs) & (kc < cs + 16)
        dr = np.clip(gk - gq + 7, 0, 14)
        dc = np.clip(kc - qc + 15, 0, 30)
        valid, dr, dc = np.broadcast_arrays(valid, dr, dc)
        vals = rpb[:, dr, dc]
        vals = np.where(valid[None], vals, NEG)
        out[vi] = vals.transpose(1, 2, 0, 3, 4).reshape(640, H, 128)
    bt = out.reshape(5, 5, 128, H, 128).transpose(0, 2, 3, 1, 4)
    return np.ascontiguousarray(bt).astype(NPBF)


def _edge_tables(rpb, h):
    H = 16
    out = np.empty((2, 4, 128, H, 8, 128), dtype=NPBF)
    kr = np.arange(16)[:, None, None, None]
    kc = np.arange(64)[None, :, None, None]
    qr = np.arange(2)[None, None, :, None]
    qc = np.arange(64)[None, None, None, :]
    cs = np.clip(qc - 8, 0, 48)
    for eb, (r0, kb) in enumerate([(0, 0), (64, 56)]):
        for i in range(4):
            gq = r0 + 2 * i + qr + 56 * h
            gk = kb + kr + 56 * h
            rs = np.clip(gq - 4, 0, 120)
            valid = (gk >= rs) & (gk < rs + 8) & (kc >= cs) & (kc < cs + 16)
            dr = np.clip(gk - gq + 7, 0, 14)
            dc = np.clip(kc - qc + 15, 0, 30)
            valid, dr, dc = np.broadcast_arrays(valid, dr, dc)
            vals = np.where(valid[None], rpb[:, dr, dc], NEG)
            t = vals.transpose(1, 2, 0, 3, 4).reshape(8, 128, H, 128)
            out[eb, i] = t.transpose(1, 2, 0, 3).astype(NPBF)
    return out


def _col(v, nchunk):
    return np.ascontiguousarray(np.asarray(v, np.float32).reshape(nchunk, 128).T)


def build(stop="all", debug=False):
    nc = bass.Bass("TRN2", target_bir_lowering=False)

    def din(name, shape, dt=F32):
        return nc.dram_tensor(name, list(shape), dt, kind="ExternalInput").ap()

    skind = "ExternalOutput" if debug else "Internal"

    def dscr(name, shape, dt):
        return nc.dram_tensor(name, list(shape), dt, kind=skind).ap()

    x_d = din("x", [S, D]); xloc_d = din("xloc", [NL, D]); hsel_d = din("hsel", [128, 2]); ctx_d = din("ctx", [LC, D]); ccols_d = din("ccols", [128, 16])
    wmod_d = din("w_mod", [2, D, 6 * D]); bmodc_d = din("bmodc", [128, 2, 48]); bmod_d = din("b_mod", [2, 6 * D])
    gcols_d = din("gcols", [128, 2, 2, 8]); gpm_d = din("g_post_mix", [2, D]); gpf_d = din("g_post_ffn", [2, D])
    win_d = din("w_in_ab", [D, 1536]); woab_d = din("w_out_ab", [D, D])
    wg_d = din("w_ffn_gate", [2, D, FH]); wu_d = din("w_ffn_up", [2, D, FH]); wd_d = din("w_ffn_down", [2, FH, D])
    wqkv_d = din("w_qkv_na", [D, 3 * D]); wona_d = din("w_out_na", [D, D])
    wbd_d = din("wbd", [2, 2, 4, 128, 128]); lcols_d = din("lcols", [128, 4, 2, 3]); convc_d = din("convc", [128, 4, 5]); cdiag_d = din("cdiag", [4, 4, 128, 128])
    ident_d = din("ident", [128, 128]); identb_d = din("identb", [128, 128], BF16)
    T1_d = din("T1", [128, 256], BF16); M2_d = din("M2", [64, 128, 192], BF16); CSf_d = din("CSf", [128, 256], BF16)
    CSc_d = din("CSc", [128, 256], BF16); T256_d = din("T256", [128, 2, 512], BF16)
    BT_d = din("BT", [128, 16, 5, 128], BF16); BTE_d = din("BTE", [2, 4, 128, 16, 8, 128], BF16)
    out_d = nc.dram_tensor("out", [NL, D], F32, kind="ExternalOutput").ap()

    UG = dscr("UG", [8, 128, S], F32)
    UGc = dscr("UGc", [8, 128, LC], F32)
    MIXT = dscr("MIXT", [D, NT], BF16)
    GROW = dscr("GROW", [2, 2, 2, D], F32)
    X1 = dscr("X1", [NL, D], F32); C1 = dscr("C1", [LC, D], F32)
    X2 = dscr("X2", [NL, D], F32); C2 = dscr("C2", [LC, D], F32)
    X3 = dscr("X3", [NL, D], F32)
    QT = dscr("QT", [D, NL], BF16); KT = dscr("KT", [D, NL + LC], BF16); VD = dscr("VD", [NL + LC, D], BF16)

    WBF = {"woab": dscr("woab_bf", [D, D], BF16), "wona": dscr("wona_bf", [D, D], BF16),
           "wqkv": dscr("wqkv_bf", [D, 3 * D], BF16),
           "wg": dscr("wg_bf", [2, D, FH], BF16), "wu": dscr("wu_bf", [2, D, FH], BF16), "wd": dscr("wd_bf", [2, FH, D], BF16)}

    with ExitStack() as es:
        em = Em(nc, es)

        def barrier():
            for e in ("pe", "act", "dve", "pool", "sp"):
                eng = em.eng[e]
                for f in ("pe", "act", "dve", "pool"):
                    if f != e and em.cnt[f] > 0 and em.waited[e].get(em.semkey[f], 0) < em.cnt[f]:
                        eng.wait_ge(em.sem[f], em.cnt[f]); em.waited[e][em.semkey[f]] = em.cnt[f]
                for q in em.dsem:
                    n = em.ndma[q]
                    for s_ in range(em.NDS):
                        uses = (n - s_ + em.NDS - 1) // em.NDS if n > s_ else 0
                        sk = ("dma", q, s_)
                        if uses > 0 and em.waited[e].get(sk, 0) < 16 * uses:
                            eng.wait_ge(em.dsem[q][s_], 16 * uses); em.waited[e][sk] = 16 * uses

        PS = es.enter_context(nc.psum_tensor("PS", [128, 8, 512], F32))
        ident = es.enter_context(nc.sbuf_tensor("ident_s", [128, 128], F32))
        identb = es.enter_context(nc.sbuf_tensor("identb_s", [128, 128], BF16))
        MODC = es.enter_context(nc.sbuf_tensor("MODC", [128, 2, 2, 2, 2, 8], F32))
        mhalf = es.enter_context(nc.sbuf_tensor("mhalf", [128, 8], F32))
        em.dma("sp", ident[:], ident_d, W=["ident"])
        em.dma("sp", identb[:], identb_d, W=["identb"])
        em.op("dve", lambda: nc.vector.memset(mhalf[:], -0.5), W=["mhalf"])

        V = nc.vector; A = nc.scalar; G = nc.gpsimd; T = nc.tensor

        def pbank(b):
            return ("ps", b)

        def rstd_from_ssq(ssq, rs, n, tag):
            em.op("dve", lambda: V.tensor_scalar(out=rs[:, 0:n], in0=ssq[:, 0:n], scalar1=1.0 / D, scalar2=EPS,
                                                 op0=ALU.mult, op1=ALU.add), R=[tag + "ssq"], W=[tag + "rs"])
            em.op("pool", lambda: G.tensor_tensor(out=rs[:, 0:n], in0=rs[:, 0:n], in1=mhalf[:, 0:n], op=ALU.pow),
                  R=[tag + "rs", "mhalf"], W=[tag + "rs"])

        def prologue_a(xt, xkey, nsub, xs, xskey, ssq, rs, junk, tag):
            for s in range(nsub):
                em.op("act", lambda s=s: A.activation(out=junk[:, :], in_=xt[:, s, :], func=AF.Square,
                                                      accum_out=ssq[:, s:s + 1]),
                      R=[xkey], W=[tag + "junk", tag + "ssq"])
            rstd_from_ssq(ssq, rs, nsub, tag)
            for s in range(nsub):
                em.op("act", lambda s=s: A.activation(out=xs[:, s, :], in_=xt[:, s, :], func=AF.Copy,
                                                      scale=rs[:, s:s + 1]),
                      R=[xkey, tag + "rs"], W=[xskey])

        def prologue_b(nsub, xs, xskey, Acol, Bcol, hT, hkey, tb):
            for k in range(8):
                b = tb[k % len(tb)]
                for s in range(nsub):
                    em.op("pe", lambda s=s, k=k, b=b: T.transpose(PS[:, b, s * 128:(s + 1) * 128],
                                                                  xs[:, s, k * 128:(k + 1) * 128], ident[:]),
                          R=[xskey, "ident"], W=[pbank(b)], inc=(s == nsub - 1))
                em.op("act", lambda k=k, b=b: A.activation(out=hT[:, k, 0:nsub * 128], in_=PS[:, b, 0:nsub * 128],
                                                           func=AF.Identity, scale=Acol[:, k:k + 1],
                                                           bias=Bcol[:, k:k + 1]),
                      R=[pbank(b), "MODC"], W=[hkey])

        def prologue(xt, xkey, nsub, xs, xskey, Acol, Bcol, hT, hkey, ssq, rs, junk, tag, tb):
            prologue_a(xt, xkey, nsub, xs, xskey, ssq, rs, junk, tag)
            prologue_b(nsub, xs, xskey, Acol, Bcol, hT, hkey, tb)

        def epilogue(psb, xsub, xkey, Gb, gkey, ssq, rs, junk, tt, tag):
            yv = PS[:, psb:psb + 2, :]
            em.op("act", lambda: A.activation(out=junk[:, :], in_=yv, func=AF.Square, accum_out=ssq[:, 0:1]),
                  R=[pbank(psb), pbank(psb + 1)], W=[tag + "junk", tag + "ssq"])
            rstd_from_ssq(ssq, rs, 1, tag)
            em.op("dve", lambda: V.tensor_tensor(out=tt[:, :], in0=yv, in1=Gb, op=ALU.mult),
                  R=[pbank(psb), pbank(psb + 1), gkey], W=[tag + "tt"])
            em.op("dve", lambda: V.scalar_tensor_tensor(out=xsub, in0=tt[:, :], scalar=rs[:, 0:1], in1=xsub,
                                                        op0=ALU.mult, op1=ALU.add),
                  R=[tag + "tt", tag + "rs", xkey], W=[xkey])

        with ExitStack() as pes:
            sb = lambda n, s, d=F32: pes.enter_context(nc.sbuf_tensor(n, list(s), d))
            cc = sb("cc", [128, 16]); scc = sb("scc", [128, 16]); rhs2 = sb("rhs2", [128, 8, 2], BF16)
            bmc = sb("bmc", [128, 2, 48]); bm1 = sb("bm1", [128, 2, 48]); gco = sb("gco", [128, 2, 2, 8])
            bmrow = sb("bmrow", [2, 2, 2, D]); grow = sb("grow", [2, 2, 2, D]); grt = sb("grt", [2, 2, 2, D])
            wm = [sb(f"wm{i}", [128, 8, 512], BF16) for i in range(3)]
            em.dma("sp", cc[:], ccols_d, W=["cc"])
            em.dma("sp", bmc[:], bmodc_d, W=["bmc"])
            em.dma("sp", gco[:], gcols_d, W=["gco"])
            for l in range(2):
                for w_, (src, off) in enumerate([(bmod_d, 2 * D), (bmod_d, 5 * D)]):
                    em.dma("sp", bmrow[:, l, w_, :], src[l, off:off + D].partition_broadcast(2), W=["bmrow"])
                em.dma("sp", grow[:, l, 0, :], gpm_d[l, :].partition_broadcast(2), W=["grow"])
                em.dma("sp", grow[:, l, 1, :], gpf_d[l, :].partition_broadcast(2), W=["grow"])
            em.op("act", lambda: A.activation(out=scc[:], in_=cc[:], func=AF.Silu), R=["cc"], W=["scc"])
            em.op("dve", lambda: V.tensor_copy(out=rhs2[:, :, 0], in_=scc[:, 0:8]), R=["scc"], W=["rhs2"])
            em.op("dve", lambda: V.tensor_copy(out=rhs2[:, :, 1], in_=scc[:, 8:16]), R=["scc"], W=["rhs2"])
            em.op("dve", lambda: V.tensor_scalar(out=bm1[:], in0=bmc[:], scalar1=1.0, scalar2=None, op0=ALU.add),
                  R=["bmc"], W=["bm1"])
            it = 0
            for l in range(2):
                wsrc = wmod_d[l].rearrange("(k p) n -> p k n", p=128)
                for nb in range(12):
                    slot = it % 3; it += 1
                    wt = wm[slot]; wk = f"wm{slot}"
                    em.dma("pool", wt[:], wsrc[:, :, nb * 512:(nb + 1) * 512], W=[wk])
                    v = nb // 2; half = nb % 2
                    b = it % 4
                    if v in (2, 5):
                        w_ = 0 if v == 2 else 1
                        for k in range(8):
                            em.op("pe", lambda k=k, b=b, wt=wt: T.matmul(PS[0:2, b, :], lhsT=rhs2[:, k, :], rhs=wt[:, k, :],
                                                                        start=(k == 0), stop=(k == 7)),
                                  R=["rhs2", wk], W=[pbank(b)], inc=(k == 7))
                        dst = grt[:, l, w_, half * 512:(half + 1) * 512]
                        em.op("dve", lambda b=b, dst=dst, l=l, w_=w_, half=half: V.tensor_tensor(
                            out=dst, in0=PS[0:2, b, :], in1=bmrow[:, l, w_, half * 512:(half + 1) * 512], op=ALU.add),
                              R=[pbank(b), "bmrow"], W=["grt"])
                        em.op("dve", lambda dst=dst, l=l, w_=w_, half=half: V.tensor_tensor(
                            out=dst, in0=dst, in1=grow[:, l, w_, half * 512:(half + 1) * 512], op=ALU.mult),
                              R=["grt", "grow"], W=["grt"])
                        if half == 1:
                            em.dma("sp", GROW[l, w_, :, :], grt[:, l, w_, :], R=["grt"], W=[("GROW", l, w_)])
                    else:
                        sub = 0 if v < 2 else 1
                        isA = v in (1, 4)
                        for m in range(4):
                            ch = half * 4 + m
                            for k in range(8):
                                em.op("pe", lambda k=k, b=b, m=m, wt=wt: T.matmul(
                                    PS[:, b, 2 * m:2 * m + 2], lhsT=wt[:, k, m * 128:(m + 1) * 128], rhs=rhs2[:, k, :],
                                    start=(k == 0), stop=(k == 7)), R=["rhs2", wk], W=[pbank(b)], inc=(k == 7))
                            dst = MODC[:, l, sub, 0 if isA else 1, :, ch]
                            if isA:
                                em.op("dve", lambda b=b, m=m, dst=dst, l=l, v=v, ch=ch, sub=sub: V.tensor_scalar(
                                    out=dst, in0=PS[:, b, 2 * m:2 * m + 2], scalar1=bm1[:, l, v * 8 + ch:v * 8 + ch + 1],
                                    scalar2=gco[:, l, sub, ch:ch + 1], op0=ALU.add, op1=ALU.mult),
                                      R=[pbank(b), "bm1", "gco"], W=["MODC"])
                            else:
                                em.op("dve", lambda b=b, m=m, dst=dst, l=l, v=v, ch=ch: V.tensor_scalar(
                                    out=dst, in0=PS[:, b, 2 * m:2 * m + 2], scalar1=bmc[:, l, v * 8 + ch:v * 8 + ch + 1],
                                    scalar2=None, op0=ALU.add), R=[pbank(b), "bmc"], W=["MODC"])
            if debug:
                MODCd = nc.dram_tensor("MODCd", [128, 128], F32, kind="ExternalOutput").ap()
                em.dma("sp", MODCd, MODC[:].rearrange("p a b c d e -> p (a b c d e)"), R=["MODC"], W=["MODCd"])
            barrier()
        if stop == "0":
            em.finish()
            return nc
        C = type("C", (), {})()
        C.__dict__.update(locals())
        for name, fn in PHASES:
            fn(C)
            barrier()
            if stop == name:
                if hasattr(C, 'es_mix'):
                    C.es_mix.close()
                break
        em.finish()
    return nc


PHASES = []


def _host_inputs(inp, b):
    global _TABLES
    if _TABLES is None:
        _TABLES = _tables()
    f = lambda a: np.ascontiguousarray(np.asarray(a, dtype=np.float32))
    m = {}
    b, h = b // 2, b % 2
    m["x"] = f(inp["x"][b]); m["ctx"] = f(inp["ctx"][b])
    m["xloc"] = f(inp["x"][b][LBASE * h:LBASE * h + NL])
    m["hsel"] = np.tile(np.array([[1.0 - h, float(h)]], np.float32), (128, 1))
    m["ccols"] = np.concatenate([_col(inp["c"][b], 8), _col(inp["c_ctx"], 8)], axis=1)
    m["w_mod"] = f(inp["w_mod"]); m["b_mod"] = f(inp["b_mod"])
    m["bmodc"] = np.ascontiguousarray(np.stack([_col(inp["b_mod"][l], 48) for l in range(2)], axis=1))
    m["gcols"] = np.ascontiguousarray(np.stack(
        [np.stack([_col(inp["g_pre_mix"][l], 8), _col(inp["g_pre_ffn"][l], 8)], axis=1) for l in range(2)], axis=1))
    m["g_post_mix"] = f(inp["g_post_mix"]); m["g_post_ffn"] = f(inp["g_post_ffn"])
    m["w_in_ab"] = f(inp["w_in_ab"][0]); m["w_out_ab"] = f(inp["w_out_ab"][0])
    m["w_ffn_gate"] = f(inp["w_ffn_gate"]); m["w_ffn_up"] = f(inp["w_ffn_up"]); m["w_ffn_down"] = f(inp["w_ffn_down"])
    m["w_qkv_na"] = f(inp["w_qkv_na"][0]); m["w_out_na"] = f(inp["w_out_na"][0])
    wbd = np.zeros((2, 2, 4, 128, 128), np.float32)
    for gi, key in enumerate(["lru_w_a", "lru_w_i"]):
        w = np.asarray(inp[key][0], np.float32)
        for d in range(2):
            for c in range(4):
                wbd[gi, d, c, 0:64, 0:64] = w[d, 2 * c]
                wbd[gi, d, c, 64:128, 64:128] = w[d, 2 * c + 1]
    m["wbd"] = wbd
    lc = np.zeros((128, 4, 2, 3), np.float32)
    for d in range(2):
        lc[:, :, d, 0] = _col(inp["lru_b_a"][0][d], 4)
        lc[:, :, d, 1] = _col(inp["lru_b_i"][0][d], 4)
        lc[:, :, d, 2] = _col(inp["lru_lam"][0][d], 4)
    m["lcols"] = lc
    cv = np.zeros((128, 4, 5), np.float32)
    for k in range(4):
        cv[:, :, k] = _col(inp["conv_w"][0][k], 4)
    cv[:, :, 4] = _col(inp["conv_b"][0], 4)
    m["convc"] = cv
    cd = np.zeros((4, 4, 128, 128), np.float32)
    ii = np.arange(128)
    for c in range(4):
        for k in range(4):
            cd[c, k, ii, ii] = cv[:, c, k]
    m["cdiag"] = cd
    for k in ("ident", "identb", "T1", "M2", "CSf", "CSc", "T256"):
        m[k] = _TABLES[k]
    rpb = np.asarray(inp["rpb_na"][0], np.float32)
    m["BT"] = np.ascontiguousarray(_bias_tables(rpb)[2])
    m["BTE"] = _edge_tables(rpb, h)
    return m


_NC_CACHE = {}


def kernel(**inputs):
    if "full" not in _NC_CACHE:
        _NC_CACHE["full"] = build()
    nc = _NC_CACHE["full"]
    in_maps = [_host_inputs(inputs, b) for b in range(NCORES)]
    res = run_bass_kernel_spmd(nc, in_maps, core_ids=list(range(NCORES)))
    out = np.empty((4, S, D), np.float32)
    for c in range(NCORES):
        b, h = c // 2, c % 2
        o = np.asarray(res.results[c]["out"], dtype=np.float32)
        out[b, 4096 * h:4096 * (h + 1)] = o[512 * h:512 * h + 4096]
    return out


def _wload(C, dst, src_view, key, nk, ncols, step=512):
    for c0 in range(0, ncols, step):
        c1 = min(ncols, c0 + step)
        C.em.dma("pool", dst[:, :, c0:c1], src_view[:, :, c0:c1], W=[key])


def _wload_bf(C, dst, src_view, key, ncols, srckey, step=1024):
    cv_emit(C, 0, need=srckey)
    for c0 in range(0, ncols, step):
        c1 = min(ncols, c0 + step)
        C.em.dma("sp", dst[:, :, c0:c1], src_view[:, :, c0:c1], R=C.cvkeys[srckey], W=[key])


def preconvert_init(C):
    jobs = [("woab", C.woab_d, C.WBF["woab"]), ("wg0", C.wg_d[0], C.WBF["wg"][0]), ("wu0", C.wu_d[0], C.WBF["wu"][0]),
            ("wd0", C.wd_d[0], C.WBF["wd"][0]), ("wqkv", C.wqkv_d, C.WBF["wqkv"]), ("wona", C.wona_d, C.WBF["wona"]),
            ("wg1", C.wg_d[1], C.WBF["wg"][1]), ("wu1", C.wu_d[1], C.WBF["wu"][1]), ("wd1", C.wd_d[1], C.WBF["wd"][1])]
    C.cvkeys = {}
    C.cvq = []
    for key, src, dst in jobs:
        rows, cols = src.shape
        rstep = 512 if rows % 512 == 0 else 704
        C.cvkeys["cv" + key] = []
        for r0 in range(0, rows, rstep):
            for c0 in range(0, cols, 1024):
                c1 = min(cols, c0 + 1024)
                k_ = f"cv{key}_{r0}_{c0}"
                C.cvkeys["cv" + key].append(k_)
                C.cvq.append(("cv" + key, k_, dst[r0:r0 + rstep, c0:c1], src[r0:r0 + rstep, c0:c1]))


def cv_emit(C, n=1, need=None):
    while C.cvq and (n > 0 or (need is not None and any(j[0] == need for j in C.cvq))):
        big, k_, dst, src = C.cvq.pop(0)
        C.em.dma_bg("pool", dst, src, W=[k_])
        n -= 1


def phase_A(C):
    nc, em, PS = C.nc, C.em, C.PS
    V, A, G, T = nc.vector, nc.scalar, nc.gpsimd, nc.tensor
    pes = C.es_mix = ExitStack()
    preconvert_init(C)
    C.F = pes.enter_context(nc.sbuf_tensor("Fbuf", [128, 64, 512], BF16))
    C.fTc = pes.enter_context(nc.sbuf_tensor("fTc", [128, 4, LC], BF16))
    F, fTc = C.F, C.fTc
    with ExitStack() as ps_:
        sb = lambda n, s, d=F32: ps_.enter_context(nc.sbuf_tensor(n, list(s), d))
        win = sb("win", [128, 8, 1536], BF16)
        xt = [sb(f"Axt{i}", [128, 4, D]) for i in range(2)]
        hT = [sb(f"AhT{i}", [128, 8, 512], BF16) for i in range(2)]
        ugst = [sb(f"Aug{i}", [128, 8, 512]) for i in range(2)]
        ssq = sb("Assq", [128, 4]); rs = sb("Ars", [128, 4]); junk = sb("Ajunk", [128, D], BF16)
        _wload(C, win, C.win_d.rearrange("(k p) n -> p k n", p=128), "win", 8, 1536)
        xsrc = C.x_d.rearrange("(t1 t2) d -> t1 t2 d", t2=64)
        Acol = C.MODC[:, 0, 0, 0, 0, :]; Bcol = C.MODC[:, 0, 0, 1, 0, :]
        AcolC = C.MODC[:, 0, 0, 0, 1, :]; BcolC = C.MODC[:, 0, 0, 1, 1, :]
        evc = [0]

        def loadA(j):
            sl = j % 2
            if j < 16:
                em.dma("sp", xt[sl][:], xsrc[:, 4 * j:4 * (j + 1), :], W=[f"Axt{sl}"])
            elif j == 16:
                em.dma("sp", xt[sl][:, 0:2, :], C.ctx_d.rearrange("(s p) d -> p s d", p=128), W=[f"Axt{sl}"])

        def nsubA(j):
            return 2 if j == 16 else 4

        def pro_a(j):
            sl = j % 2
            C.prologue_a(xt[sl], f"Axt{sl}", nsubA(j), xt[sl], f"Axt{sl}", ssq, rs, junk, "A")

        def pro_b(j):
            sl = j % 2
            isctx = j == 16
            C.prologue_b(nsubA(j), xt[sl], f"Axt{sl}", AcolC if isctx else Acol, BcolC if isctx else Bcol,
                         hT[sl], f"AhT{sl}", [0, 1])

        def ugA(j):
            sl = j % 2
            isctx = j == 16
            n = nsubA(j) * 128
            for oc in range(8):
                b = 2 + oc % 3
                for k in range(8):
                    em.op("pe", lambda k=k, b=b, oc=oc, sl=sl, n=n: T.matmul(
                        PS[:, b, 0:n], lhsT=win[:, k, oc * 128:(oc + 1) * 128], rhs=hT[sl][:, k, 0:n],
                        start=(k == 0), stop=(k == 7)), R=["win", f"AhT{sl}"], W=[("ps", b)], inc=(k == 7))
                evc[0] += 1
                if evc[0] % 2:
                    em.op("dve", lambda b=b, oc=oc, sl=sl, n=n: V.tensor_copy(out=ugst[sl][:, oc, 0:n], in_=PS[:, b, 0:n]),
                          R=[("ps", b)], W=[f"Aug{sl}"])
                else:
                    em.op("act", lambda b=b, oc=oc, sl=sl, n=n: A.copy(out=ugst[sl][:, oc, 0:n], in_=PS[:, b, 0:n]),
                          R=[("ps", b)], W=[f"Aug{sl}"])
            if isctx:
                em.dma("pool", C.UGc.rearrange("c p n -> p c n"), ugst[sl][:, :, 0:LC], R=[f"Aug{sl}"], W=["UGc"])
            else:
                em.dma("pool", C.UG[:, :, j * 512:(j + 1) * 512].rearrange("c p n -> p c n"), ugst[sl][:],
                       R=[f"Aug{sl}"], W=[("UG", j)])

        def fA(j):
            sl = j % 2
            if j == 16:
                for fc in range(4):
                    b = 5 + fc % 3
                    for k in range(8):
                        em.op("pe", lambda k=k, b=b, fc=fc, sl=sl: T.matmul(
                            PS[:, b, 0:LC], lhsT=win[:, k, 1024 + fc * 128:1024 + (fc + 1) * 128], rhs=hT[sl][:, k, 0:LC],
                            start=(k == 0), stop=(k == 7)), R=["win", f"AhT{sl}"], W=[("ps", b)], inc=(k == 7))
                    em.op("dve", lambda b=b, fc=fc: V.tensor_copy(out=fTc[:, fc, :], in_=PS[:, b, 0:LC]),
                          R=[("ps", b)], W=["fTc"])
            else:
                for s in range(4):
                    b = 5 + s % 3
                    for k in range(8):
                        em.op("pe", lambda k=k, b=b, s=s, sl=sl: T.matmul(
                            PS[:, b, :], lhsT=hT[sl][:, k, s * 128:(s + 1) * 128], rhs=win[:, k, 1024:1536],
                            start=(k == 0), stop=(k == 7)), R=["win", f"AhT{sl}"], W=[("ps", b)], inc=(k == 7))
                    evc[0] += 1
                    if evc[0] % 2:
                        em.op("dve", lambda b=b, s=s, j=j: V.tensor_copy(out=F[:, 4 * j + s, :], in_=PS[:, b, :]),
                              R=[("ps", b)], W=["F"])
                    else:
                        em.op("act", lambda b=b, s=s, j=j: A.copy(out=F[:, 4 * j + s, :], in_=PS[:, b, :]),
                              R=[("ps", b)], W=["F"])

        loadA(0)
        loadA(1)
        pro_a(0)
        pro_b(0)
        for j in range(17):
            ugA(j)
            cv_emit(C, 1)
            if j + 2 < 17:
                loadA(j + 2)
            if j + 1 < 17:
                pro_a(j + 1)
            fA(j)
            if j + 1 < 17:
                pro_b(j + 1)


def phase_C(C):
    nc, em, PS, F, fTc = C.nc, C.em, C.PS, C.F, C.fTc
    V, A, G, T = nc.vector, nc.scalar, nc.gpsimd, nc.tensor
    with ExitStack() as ps_:
        sb = lambda n, s, d=F32: ps_.enter_context(nc.sbuf_tensor(n, list(s), d))
        T1 = sb("T1s", [128, 256], BF16); M2 = sb("M2s", [64, 128, 192], BF16); CSf = sb("CSfs", [128, 256], BF16)
        CSc = sb("CScs", [128, 256], BF16); T256 = sb("T256s", [128, 2, 512], BF16)
        Ast = sb("Ast", [64, 64, 256], BF16); Y = sb("Ybuf", [128, 2, S], BF16)
        fst = [sb(f"fst{i}", [128, 2048], BF16) for i in range(2)]
        Gc = sb("Gcb", [128, 2, 256], BF16); fcs = sb("fcs", [128, LC], BF16)
        for dst, src, k in ((T1, C.T1_d, "T1"), (M2, C.M2_d, "M2"), (CSf, C.CSf_d, "CSf"), (CSc, C.CSc_d, "CSc"),
                            (T256, C.T256_d, "T256")):
            em.dma("sp", dst[:], src, W=[k])
        ev = 0
        for cc in range(4):
            for hc in range(2):
                for g4 in range(16):
                    b = 2 * (g4 % 2)
                    for q in range(4):
                        ch = cc * 128 + hc * 64 + g4 * 4 + q
                        em.op("pe", lambda b=b, q=q, ch=ch: T.matmul(
                            PS[0:64, b + q // 2, (q % 2) * 256:(q % 2) * 256 + 256], lhsT=F[:, :, ch], rhs=T1[:, :],
                            start=True, stop=True), R=["F", "T1"], W=[("ps", b), ("ps", b + 1)], inc=(q == 3))
                    ev += 1
                    dst = Ast[:, g4 * 4:(g4 + 1) * 4, :].rearrange("p a b -> p (a b)")
                    src = PS[0:64, b:b + 2, :].rearrange("p a b -> p (a b)")
                    if ev % 2:
                        em.op("dve", lambda dst=dst, src=src: V.tensor_copy(out=dst, in_=src),
                              R=[("ps", b), ("ps", b + 1)], W=["Ast"])
                    else:
                        em.op("act", lambda dst=dst, src=src: A.copy(out=dst, in_=src),
                              R=[("ps", b), ("ps", b + 1)], W=["Ast"])
                for kb in range(32):
                    b = 4 + kb % 2
                    for q in range(4):
                        k1 = kb * 4 + q
                        o = PS[hc * 64:(hc + 1) * 64, b, q * 128:(q + 1) * 128]
                        em.op("pe", lambda o=o, k1=k1: T.matmul(o, lhsT=Ast[:, :, k1], rhs=M2[:, k1, 64:192],
                                                                start=True, stop=False),
                              R=["Ast", "M2"], W=[("ps", b)], inc=False)
                        em.op("pe", lambda o=o, k1=k1: T.matmul(o, lhsT=Ast[:, :, 128 + k1], rhs=M2[:, k1, 0:128],
                                                                start=False, stop=True),
                              R=["Ast", "M2"], W=[("ps", b)], inc=(q == 3))
                    ev += 1
                    src = PS[hc * 64:(hc + 1) * 64, b, :].rearrange("p (k r c) -> p r c k", k=4, r=2)
                    dst = Y[hc * 64:(hc + 1) * 64, :, :].rearrange("p r (c k) -> p r c k", k=128)[:, :, :, kb * 4:(kb + 1) * 4]
                    if ev % 2:
                        em.op("dve", lambda dst=dst, src=src: V.tensor_copy(out=dst, in_=src), R=[("ps", b)], W=["Y"])
                    else:
                        em.op("act", lambda dst=dst, src=src: A.copy(out=dst, in_=src), R=[("ps", b)], W=["Y"])
            for tl in range(16):
                b = 6 + tl % 2
                em.op("pe", lambda b=b, tl=tl: T.matmul(PS[:, b, :], lhsT=CSf[:, 0:128], rhs=Y[:, 0, tl * 512:(tl + 1) * 512],
                                                        start=True, stop=False), R=["CSf", "Y"], W=[("ps", b)], inc=False)
                em.op("pe", lambda b=b, tl=tl: T.matmul(PS[:, b, :], lhsT=CSf[:, 128:256], rhs=Y[:, 1, tl * 512:(tl + 1) * 512],
                                                        start=False, stop=True), R=["CSf", "Y"], W=[("ps", b)], inc=True)
                fs = (tl // 4) % 2
                em.op("act", lambda b=b, tl=tl, fs=fs: A.copy(out=fst[fs][:, (tl % 4) * 512:(tl % 4 + 1) * 512], in_=PS[:, b, :]),
                      R=[("ps", b)], W=[f"fst{fs}"])
                if tl % 4 == 3:
                    t0 = (tl // 4) * 2048
                    em.dma("pool", C.MIXT[512 + cc * 128:512 + (cc + 1) * 128, t0:t0 + 2048], fst[fs][:],
                           R=[f"fst{fs}"], W=[("MIXT", 4 + cc)])
            for tc in range(2):
                em.op("pe", lambda tc=tc, cc=cc: T.matmul(PS[:, 0, 0:256], lhsT=fTc[:, cc, tc * 128:(tc + 1) * 128], rhs=CSc[:, :],
                                                          start=True, stop=True), R=["fTc", "CSc"], W=[("ps", 0)])
                em.op("dve", lambda tc=tc: V.tensor_copy(out=Gc[:, tc, :], in_=PS[:, 0, 0:256]), R=[("ps", 0)], W=["Gc"])
            for i_, (tc, part) in enumerate([(0, 0), (0, 1), (1, 0), (1, 1)]):
                em.op("pe", lambda i_=i_, tc=tc, part=part: T.matmul(
                    PS[:, 1, 0:256], lhsT=Gc[:, tc, part * 128:(part + 1) * 128], rhs=T256[:, tc, part * 256:(part + 1) * 256],
                    start=(i_ == 0), stop=(i_ == 3)), R=["Gc", "T256"], W=[("ps", 1)], inc=(i_ == 3))
            em.op("dve", lambda: V.tensor_copy(out=fcs[:, :], in_=PS[:, 1, 0:256]), R=[("ps", 1)], W=["fcs"])
            em.dma("pool", C.MIXT[512 + cc * 128:512 + (cc + 1) * 128, S:NT], fcs[:], R=["fcs"], W=[("MIXTc", 4 + cc)])
    C.es_mix.close()


PHASES += [("A", phase_A), ("C", phase_C)]


def phase_B(C):
    nc, em, PS = C.nc, C.em, C.PS
    V, A, G, T = nc.vector, nc.scalar, nc.gpsimd, nc.tensor
    TW = 2048
    with ExitStack() as ps_:
        sb = lambda n, s, d=F32: ps_.enter_context(nc.sbuf_tensor(n, list(s), d))
        bufA = sb("bufA", [128, NT]); bufB = sb("bufB", [128, 8460]); uc = sb("ucb_", [128, NT]); ucb = sb("ucbb", [128, NT], BF16)
        rt = sb("rt", [128, TW])
        it_ = [sb(f"it{i}", [128, TW]) for i in range(2)]; at = [sb(f"at{i}", [128, TW]) for i in range(2)]
        st = [sb(f"st{i}", [128, TW]) for i in range(2)]
        gtmp = [sb(f"gtmp{i}", [128, 1024]) for i in range(2)]
        wbd = sb("wbds", [128, 16, 128], BF16)
        cdg = sb("cdg", [128, 16, 128])
        lco = sb("lco", [128, 4, 2, 3]); cvc = sb("cvc", [128, 4, 5]); cA = sb("cA", [128, 4, 2]); carry = sb("carry", [128, 2])
        em.dma("pool", wbd[:], C.wbd_d.rearrange("g d c p n -> p (g d c) n"), W=["wbd"])
        em.dma("sp", cdg[:], C.cdiag_d.rearrange("c k p n -> p (c k) n"), W=["cdg"])
        em.dma("sp", lco[:], C.lcols_d, W=["lco"])
        em.dma("sp", cvc[:], C.convc_d, W=["cvc"])
        em.op("act", lambda: A.activation(out=cA[:], in_=lco[:, :, :, 2], func=AF.Exp, scale=-1.0), R=["lco"], W=["cA"])
        em.op("act", lambda: A.activation(out=cA[:], in_=cA[:], func=AF.Ln, bias=1.0), R=["cA"], W=["cA"])
        em.op("dve", lambda: V.tensor_scalar(out=cA[:], in0=cA[:], scalar1=-8.0, scalar2=None, op0=ALU.mult), R=["cA"], W=["cA"])
        em.op("dve", lambda: V.memset(bufB[:], 0.0), W=["bufB"])
        XO = 8200
        tiles = [(S, NT)] + [(i * TW, (i + 1) * TW) for i in range(4)]
        tix = 0
        for c in range(4):
            cv_emit(C, 2)
            em.dma("sp", bufA[:, 0:S], C.UG[c], W=["bufA"])
            em.dma("sp", bufA[:, S:NT], C.UGc[c], W=["bufA"])
            for qd in range(4):
                o_ = bufB[:, 2 + qd * 2048:2 + (qd + 1) * 2048].rearrange("p (a b) -> p a b", b=64)
                i_ = bufA[:, 0:S].rearrange("p (b a) -> p a b", a=128)[:, qd * 32:(qd + 1) * 32, :]
                eng = QENG[qd]
                fn = {"act": (lambda o_=o_, i_=i_: A.copy(out=o_, in_=i_)),
                      "dve": (lambda o_=o_, i_=i_: V.tensor_copy(out=o_, in_=i_)),
                      "pool": (lambda o_=o_, i_=i_: G.tensor_copy(out=o_, in_=i_))}[eng]
                em.op(eng, fn, R=["bufA"], W=[f"bufBq{qd}"])
            em.op("pool", lambda: G.tensor_copy(out=bufB[:, XO + 2:XO + 2 + LC], in_=bufA[:, S:NT]), R=["bufA"], W=["bufBc"])
            em.dma("sp", bufA[:, 0:S], C.UG[4 + c], W=["bufA"])
            em.dma("sp", bufA[:, S:NT], C.UGc[4 + c], W=["bufA"])
            for (o0, src0, n) in ((0, 0, S), (S, XO, LC)):
                for q0 in range(0, n, 2048):
                    nn = min(2048, n - q0)
                    nq = (nn + 511) // 512
                    pb = 4 * ((q0 // 2048) % 2)
                    qd = q0 // 2048
                    rk = ["bufBc", "bufB"] if n == LC else [f"bufBq{x}" for x in (qd - 1, qd, qd + 1) if 0 <= x < 4] + ["bufB"]
                    for q in range(nq):
                        w_ = min(512, nn - q * 512)
                        for k in range(4):
                            em.op("pe", lambda q=q, k=k, w_=w_, pb=pb, src0=src0, q0=q0, c=c: T.matmul(
                                PS[:, pb + q, 0:w_], lhsT=cdg[:, c * 4 + k, :],
                                rhs=bufB[:, src0 + q0 + q * 512 + k:src0 + q0 + q * 512 + k + w_],
                                start=(k == 0), stop=(k == 3)), R=["cdg"] + rk, W=[("ps", pb + q)], inc=(k == 3))
                    src = PS[:, pb:pb + 4, :].rearrange("p a b -> p (a b)")[:, 0:nn]
                    em.op("act", lambda src=src, o0=o0, q0=q0, nn=nn, c=c: A.activation(
                        out=uc[:, o0 + q0:o0 + q0 + nn], in_=src, func=AF.Identity, bias=cvc[:, c, 4:5]),
                          R=[("ps", pb + q) for q in range(4)] + ["cvc"], W=["uc"])
                    em.op("act", lambda src=src, o0=o0, q0=q0, nn=nn, c=c: A.activation(
                        out=ucb[:, o0 + q0:o0 + q0 + nn], in_=src, func=AF.Identity, bias=cvc[:, c, 4:5]),
                          R=[("ps", pb + q) for q in range(4)] + ["cvc"], W=["ucb"])

            def gelu_s1(lo, hi, p):
                n = hi - lo
                em.op("act", lambda: A.activation(out=gtmp[p][:, 0:n], in_=bufA[:, lo:hi], func=AF.Square),
                      R=["bufA"], W=[f"gt{p}"])
                em.op("pool", lambda: G.tensor_scalar(out=gtmp[p][:, 0:n], in0=gtmp[p][:, 0:n], scalar1=0.044715, scalar2=1.0,
                                                      op0=ALU.mult, op1=ALU.add), R=[f"gt{p}"], W=[f"gt{p}"])
                em.op("dve", lambda: V.tensor_tensor(out=gtmp[p][:, 0:n], in0=gtmp[p][:, 0:n], in1=bufA[:, lo:hi], op=ALU.mult),
                      R=[f"gt{p}", "bufA"], W=[f"gt{p}"])

            def gelu_s2(lo, hi, p):
                n = hi - lo
                em.op("act", lambda: A.activation(out=gtmp[p][:, 0:n], in_=gtmp[p][:, 0:n], func=AF.Sigmoid,
                                                  scale=1.5957691216057308), R=[f"gt{p}"], W=[f"gt{p}"])
                em.op("dve", lambda: V.tensor_tensor(out=bufA[:, lo:hi], in0=gtmp[p][:, 0:n], in1=bufA[:, lo:hi], op=ALU.mult),
                      R=[f"gt{p}", "bufA"], W=["bufA"])

            gpieces = [(S, NT)] + [(i * 1024, (i + 1) * 1024) for i in range(8)]
            gstate = {"s1": 0, "s2": 0}

            def gelu_push1():
                k = gstate["s1"]
                if k < len(gpieces):
                    gelu_s1(gpieces[k][0], gpieces[k][1], k % 2)
                    gstate["s1"] += 1

            def gelu_push2():
                k = gstate["s2"]
                if k < gstate["s1"]:
                    gelu_s2(gpieces[k][0], gpieces[k][1], k % 2)
                    gstate["s2"] += 1

            for d in range(2):
                order = [tiles[0]] + (tiles[1:] if d == 0 else tiles[1:][::-1])
                for ti, (lo, hi) in enumerate(order):
                    p = tix % 2
                    tix += 1
                    n = hi - lo
                    nq = (n + 511) // 512
                    for gi in range(2):
                        for q in range(nq):
                            w_ = min(512, n - q * 512)
                            em.op("pe", lambda gi=gi, q=q, w_=w_, lo=lo, d=d, c=c: T.matmul(
                                PS[:, gi * 4 + q, 0:w_], lhsT=wbd[:, gi * 8 + d * 4 + c, :], rhs=ucb[:, lo + q * 512:lo + q * 512 + w_],
                                start=True, stop=True), R=["wbd", "ucb"], W=[("ps", gi * 4 + q)], inc=(q == nq - 1))
                    pr = PS[:, 0:4, :].rearrange("p a b -> p (a b)")[:, 0:n]
                    pi = PS[:, 4:8, :].rearrange("p a b -> p (a b)")[:, 0:n]
                    if d == 0:
                        gelu_push2()
                        gelu_push2()
                    em.op("act", lambda pr=pr, n=n, c=c, d=d: A.activation(out=rt[:, 0:n], in_=pr, func=AF.Sigmoid,
                                                                           bias=lco[:, c, d, 0:1]),
                          R=[("ps", q) for q in range(4)] + ["lco"], W=["rt"])
                    em.op("act", lambda pi=pi, n=n, c=c, d=d, p=p: A.activation(out=it_[p][:, 0:n], in_=pi, func=AF.Sigmoid,
                                                                                bias=lco[:, c, d, 1:2]),
                          R=[("ps", 4 + q) for q in range(4)] + ["lco"], W=[f"it{p}"])
                    em.op("act", lambda n=n, c=c, d=d, p=p: A.activation(out=at[p][:, 0:n], in_=rt[:, 0:n], func=AF.Exp,
                                                                         scale=cA[:, c, d:d + 1]), R=["rt", "cA"], W=[f"at{p}"])
                    em.op("dve", lambda n=n, p=p: V.tensor_tensor(out=st[p][:, 0:n], in0=at[p][:, 0:n], in1=at[p][:, 0:n], op=ALU.mult),
                          R=[f"at{p}"], W=[f"st{p}"])
                    if d == 0:
                        gelu_push1()
                        gelu_push1()
                    em.op("act", lambda n=n, p=p: A.activation(out=st[p][:, 0:n], in_=st[p][:, 0:n], func=AF.Sqrt, scale=-1.0, bias=1.0),
                          R=[f"st{p}"], W=[f"st{p}"])
                    em.op("dve", lambda n=n, p=p: V.tensor_tensor(out=it_[p][:, 0:n], in0=it_[p][:, 0:n], in1=st[p][:, 0:n], op=ALU.mult),
                          R=[f"it{p}", f"st{p}"], W=[f"it{p}"])
                    em.op("dve", lambda n=n, lo=lo, hi=hi, p=p: V.tensor_tensor(out=it_[p][:, 0:n], in0=it_[p][:, 0:n], in1=uc[:, lo:hi],
                                                                               op=ALU.mult), R=[f"it{p}", "uc"], W=[f"it{p}"])
                    init = 0.0 if ti == 0 else carry[:, d:d + 1]
                    if d == 0:
                        em.op("dve", lambda n=n, lo=lo, hi=hi, init=init, p=p: V.tensor_tensor_scan(
                            out=bufB[:, lo:hi], data0=at[p][:, 0:n], data1=it_[p][:, 0:n], initial=init, op0=ALU.mult, op1=ALU.add),
                              R=[f"at{p}", f"it{p}", "carry", "bufB", "uc"], W=["bufB", "bufBq0", "bufBq1", "bufBq2", "bufBq3", "bufBc"])
                        em.op("dve", lambda hi=hi: V.tensor_copy(out=carry[:, 0:1], in_=bufB[:, hi - 1:hi]), R=["bufB"], W=["carry"])
                    else:
                        em.op("dve", lambda n=n, init=init, p=p: V.tensor_tensor_scan(
                            out=st[p][:, 0:n][:, ::-1], data0=at[p][:, 0:n][:, ::-1],
                            data1=it_[p][:, 0:n][:, ::-1], initial=init, op0=ALU.mult, op1=ALU.add),
                              R=[f"at{p}", f"it{p}", "carry", f"st{p}"], W=[f"st{p}"])
                        em.op("dve", lambda p=p: V.tensor_copy(out=carry[:, 1:2], in_=st[p][:, 0:1]), R=[f"st{p}"], W=["carry"])
                        em.op("dve", lambda n=n, lo=lo, hi=hi, p=p: V.tensor_tensor(out=bufB[:, lo:hi], in0=bufB[:, lo:hi], in1=st[p][:, 0:n],
                                                                                    op=ALU.add), R=["bufB", f"st{p}"], W=["bufB"])
            while gstate["s2"] < len(gpieces):
                gelu_push1()
                gelu_push2()
            em.op("dve", lambda: V.tensor_tensor(out=ucb[:, 0:S].rearrange("p (a b) -> p a b", b=64),
                                                 in0=bufB[:, 0:S].rearrange("p (a b) -> p a b", b=64),
                                                 in1=bufA[:, 0:S].rearrange("p (b a) -> p a b", a=128), op=ALU.mult),
                  R=["bufB", "bufBq0", "bufBq1", "bufBq2", "bufBq3", "bufBc", "bufA", "ucb"], W=["ucb"])
            em.op("dve", lambda: V.tensor_tensor(out=ucb[:, S:NT], in0=bufB[:, S:NT], in1=bufA[:, S:NT], op=ALU.mult),
                  R=["bufB", "bufBq0", "bufBq1", "bufBq2", "bufBq3", "bufBc", "bufA", "ucb"], W=["ucb"])
            em.dma("pool", C.MIXT[c * 128:(c + 1) * 128, :], ucb[:, :], R=["ucb"], W=[("MIXT", c)])
            if c < 3:
                em.op("dve", lambda: V.memset(bufB[:, 0:2], 0.0), R=["bufB"], W=["bufB", "bufBq0"])
                em.op("dve", lambda: V.memset(bufB[:, S:8460], 0.0), R=["bufB"], W=["bufB", "bufBq3", "bufBc"])


def _tok_tiles(ntok_tile):
    return None


def phase_D1(C, layer=0):
    nc, em, PS = C.nc, C.em, C.PS
    V, A, G, T = nc.vector, nc.scalar, nc.gpsimd, nc.tensor
    with ExitStack() as ps_:
        sb = lambda n, s, d=F32: ps_.enter_context(nc.sbuf_tensor(n, list(s), d))
        wo = sb("D1wo", [128, 8, D], BF16)
        xt = [sb(f"D1xt{i}", [128, 4, D]) for i in range(2)]
        mt = [sb(f"D1mt{i}", [128, 8, 512], BF16) for i in range(2)]
        mb = [sb(f"D1mb{i}", [128, 8, 512], BF16) for i in range(2)]
        hs = sb("D1hs", [128, 2])
        Gx = sb("D1Gx", [128, D]); Gc = sb("D1Gc", [128, D])
        ssq = sb("D1ssq", [128, 4]); rs = sb("D1rs", [128, 4]); junk = sb("D1junk", [128, D], BF16); tt = sb("D1tt", [128, D])
        _wload_bf(C, wo, C.WBF["woab"].rearrange("(k p) n -> p k n", p=128), "D1wo", D, "cvwoab")
        em.dma("sp", hs[:], C.hsel_d, W=["D1hs"])
        em.dma("sp", Gx[:], C.GROW[0, 0, 0, :].partition_broadcast(128), R=[("GROW", 0, 0)], W=["D1Gx"])
        em.dma("sp", Gc[:], C.GROW[0, 0, 1, :].partition_broadcast(128), R=[("GROW", 0, 0)], W=["D1Gc"])
        mixv = C.MIXT.rearrange("(k p) t -> p k t", p=128)
        NTL = NLT + 1

        def load(j, sl):
            if j < NLT:
                em.dma("sp", xt[sl][:], C.xloc_d[j * 512:(j + 1) * 512, :].rearrange("(s p) d -> p s d", p=128), W=[f"D1xt{sl}"])
                em.dma("sp", mt[sl][:], mixv[:, :, j * 512:(j + 1) * 512], W=[f"D1mt{sl}"])
                em.dma("sp", mb[sl][:], mixv[:, :, LBASE + j * 512:LBASE + (j + 1) * 512], W=[f"D1mb{sl}"])
                em.op("act", lambda sl=sl: A.activation(out=mt[sl][:].rearrange("p a b -> p (a b)"),
                                                        in_=mt[sl][:].rearrange("p a b -> p (a b)"), func=AF.Copy,
                                                        scale=hs[:, 0:1]), R=[f"D1mt{sl}", "D1hs"], W=[f"D1mt{sl}"])
                em.op("dve", lambda sl=sl: V.scalar_tensor_tensor(
                    out=mt[sl][:].rearrange("p a b -> p (a b)"), in0=mb[sl][:].rearrange("p a b -> p (a b)"), scalar=hs[:, 1:2],
                    in1=mt[sl][:].rearrange("p a b -> p (a b)"), op0=ALU.mult, op1=ALU.add),
                      R=[f"D1mt{sl}", f"D1mb{sl}", "D1hs"], W=[f"D1mt{sl}"])
            else:
                em.dma("sp", xt[sl][:, 0:2, :], C.ctx_d.rearrange("(s p) d -> p s d", p=128), W=[f"D1xt{sl}"])
                em.dma("sp", mt[sl][:, :, 0:LC], mixv[:, :, S:NT], W=[f"D1mt{sl}"])
        load(0, 0)
        for j in range(NTL):
            sl = j % 2
            if j + 1 < NTL:
                load(j + 1, 1 - sl)
            isx = j < NLT
            nsub = 4 if isx else 2
            for s in range(nsub):
                pb = 2 * (s % 4)
                for h in range(2):
                    for k in range(8):
                        em.op("pe", lambda k=k, h=h, s=s, sl=sl, pb=pb: T.matmul(
                            PS[:, pb + h, :], lhsT=mt[sl][:, k, s * 128:(s + 1) * 128], rhs=wo[:, k, h * 512:(h + 1) * 512],
                            start=(k == 0), stop=(k == 7)), R=["D1wo", f"D1mt{sl}"], W=[("ps", pb + h)], inc=(k == 7))
                C.epilogue(pb, xt[sl][:, s, :], f"D1xt{sl}", (Gx if isx else Gc)[:, :], "D1Gx" if isx else "D1Gc",
                           ssq, rs, junk, tt, "D1")
            if isx:
                em.dma("pool", C.X1[j * 512:(j + 1) * 512, :].rearrange("(s p) d -> p s d", p=128), xt[sl][:],
                       R=[f"D1xt{sl}"], W=[("X1", j)])
            else:
                em.dma("pool", C.C1.rearrange("(s p) d -> p s d", p=128), xt[sl][:, 0:2, :], R=[f"D1xt{sl}"], W=["C1"])


def phase_FFN(C, layer, Xin, Cin, Xout, Cout, inkey, outkey):
    nc, em, PS = C.nc, C.em, C.PS
    V, A, G, T = nc.vector, nc.scalar, nc.gpsimd, nc.tensor
    tg = f"F{layer}"
    with ExitStack() as ps_:
        sb = lambda n, s, d=F32: ps_.enter_context(nc.sbuf_tensor(tg + n, list(s), d))
        wg = sb("wg", [128, 8, FH], BF16); wu = sb("wu", [128, 8, FH], BF16); wd = sb("wd", [128, NJ, D], BF16)
        xt = sb("xt", [128, 4, D]); hT = sb("hT", [128, 8, 512], BF16); hh = sb("hh", [128, NJ, 512], BF16)
        est = [sb(f"est{i}", [128, D]) for i in range(2)]
        Gx = sb("Gx", [128, D]); tt = sb("tt", [128, D]); junk = sb("junk", [128, D], BF16)
        ssq = sb("ssq", [128, 4]); rs = sb("rs", [128, 4]); ssq2 = sb("ssq2", [128, 4]); rs2 = sb("rs2", [128, 4])
        _wload_bf(C, wg, C.WBF["wg"][layer].rearrange("(k p) n -> p k n", p=128), tg + "wg", FH, f"cvwg{layer}", 1408)
        _wload_bf(C, wu, C.WBF["wu"][layer].rearrange("(k p) n -> p k n", p=128), tg + "wu", FH, f"cvwu{layer}", 1408)
        _wload_bf(C, wd, C.WBF["wd"][layer].rearrange("(k p) n -> p k n", p=128), tg + "wd", D, f"cvwd{layer}", 512)
        em.dma("sp", Gx[:], C.GROW[layer, 1, 0, :].partition_broadcast(128), W=[tg + "Gx"])
        NX = NL // 512
        ntile = NX + (1 if Cin is not None else 0)

        def nsub_of(j):
            return 4 if j < NX else 2

        def rows(j, s):
            if j < NX:
                r0 = j * 512 + s * 128
                return Xin[r0:r0 + 128, :], Xout[r0:r0 + 128, :]
            return Cin[s * 128:(s + 1) * 128, :], Cout[s * 128:(s + 1) * 128, :]

        def load(j):
            ns = nsub_of(j)
            src = Xin[j * 512:(j + 1) * 512, :] if j < NX else Cin
            em.dma("sp", xt[:, 0:ns, :], src.rearrange("(s p) d -> p s d", p=128), W=[tg + "xt"])

        def pro_a(j):
            ns = nsub_of(j)
            C.prologue_a(xt, tg + "xt", ns, xt, tg + "xt", ssq2, rs2, junk, tg + "p")

        def pro_b(j):
            path = 0 if j < NX else 1
            C.prologue_b(nsub_of(j), xt, tg + "xt", C.MODC[:, layer, 1, 0, path, :], C.MODC[:, layer, 1, 1, path, :],
                         hT, tg + "hT", [0, 1, 2, 3])

        def gateup(j):
            n = nsub_of(j) * 128
            for jj in range(NJ):
                b = 2 * (jj % 2)
                for gi, w_ in enumerate((wg, wu)):
                    for k in range(8):
                        em.op("pe", lambda k=k, b=b, gi=gi, w_=w_, jj=jj: T.matmul(
                            PS[:, b + gi, 0:n], lhsT=w_[:, k, jj * 128:(jj + 1) * 128], rhs=hT[:, k, 0:n],
                            start=(k == 0), stop=(k == 7)), R=[tg + "wg", tg + "wu", tg + "hT"], W=[("ps", b + gi)],
                              inc=(k == 7))
                em.op("act", lambda b=b, jj=jj: A.activation(out=hh[:, jj, 0:n], in_=PS[:, b, 0:n], func=AF.Silu),
                      R=[("ps", b)], W=[tg + f"hh{jj}"])
                em.op("dve", lambda b=b, jj=jj: V.tensor_tensor(out=hh[:, jj, 0:n], in0=hh[:, jj, 0:n], in1=PS[:, b + 1, 0:n],
                                                               op=ALU.mult), R=[("ps", b + 1), tg + f"hh{jj}"], W=[tg + f"hh{jj}"])

        def down(j, s):
            pb = 4 + 2 * (s % 2)
            for h in range(2):
                for jj in range(NJ):
                    em.op("pe", lambda jj=jj, h=h, s=s, pb=pb: T.matmul(
                        PS[:, pb + h, :], lhsT=hh[:, jj, s * 128:(s + 1) * 128], rhs=wd[:, jj, h * 512:(h + 1) * 512],
                        start=(jj == 0), stop=(jj == NJ - 1)), R=[tg + "wd", tg + f"hh{jj}"], W=[("ps", pb + h)], inc=(jj == NJ - 1))

        def eload(j, s):
            em.dma("sp", est[s % 2][:], rows(j, s)[0], W=[tg + f"est{s % 2}"])

        def epi(j, s):
            pb = 4 + 2 * (s % 2)
            C.epilogue(pb, est[s % 2][:, :], tg + f"est{s % 2}", Gx[:, :], tg + "Gx", ssq, rs, junk, tt, tg)
            em.dma("pool", rows(j, s)[1], est[s % 2][:], R=[tg + f"est{s % 2}"], W=[(outkey, j, s)])

        load(0)
        pro_a(0)
        pro_b(0)
        for j in range(ntile):
            ns = nsub_of(j)
            if j == NX:
                em.dma("sp", Gx[:], C.GROW[layer, 1, 1, :].partition_broadcast(128), W=[tg + "Gx"])
            gateup(j)
            if layer == 0:
                cv_emit(C, 2)
            if j + 1 < ntile:
                load(j + 1)
                pro_a(j + 1)
            eload(j, 0)
            eload(j, 1)
            down(j, 0)
            down(j, 1)
            if j + 1 < ntile:
                pro_b(j + 1)
            epi(j, 0)
            epi(j, 1)
            if ns == 4:
                eload(j, 2)
                eload(j, 3)
                down(j, 2)
                down(j, 3)
                epi(j, 2)
                epi(j, 3)


PHASES += [("B", phase_B), ("D1", phase_D1),
           ("D2", lambda C: phase_FFN(C, 0, C.X1, C.C1, C.X2, C.C2, "X1", "X2"))]


def phase_E(C):
    nc, em, PS = C.nc, C.em, C.PS
    V, A, G, T = nc.vector, nc.scalar, nc.gpsimd, nc.tensor
    with ExitStack() as ps_:
        sb = lambda n, s, d=F32: ps_.enter_context(nc.sbuf_tensor("E" + n, list(s), d))
        wq = sb("wq", [128, 8, 3 * D], BF16)
        xt = [sb(f"xt{i}", [128, 4, D]) for i in range(2)]
        hT = [sb(f"hT{i}", [128, 8, 512], BF16) for i in range(2)]
        qk = [sb(f"qk{i}", [128, 16, 512], BF16) for i in range(2)]
        vst = [sb(f"vst{i}", [128, 4, D], BF16) for i in range(2)]
        ssq = sb("ssq", [128, 4]); rs = sb("rs", [128, 4]); junk = sb("junk", [128, D], BF16)
        _wload_bf(C, wq, C.WBF["wqkv"].rearrange("(k p) n -> p k n", p=128), "Ewq", 3 * D, "cvwqkv")
        QTv = C.QT.rearrange("(c p) t -> p c t", p=128); KTv = C.KT.rearrange("(c p) t -> p c t", p=128)
        NTL = NLT + 1
        evc = [0]

        def nsubE(j):
            return 2 if j == NLT else 4

        def load(j):
            sl = j % 2
            if j < NLT:
                em.dma("sp", xt[sl][:], C.X2[j * 512:(j + 1) * 512, :].rearrange("(s p) d -> p s d", p=128), W=[f"Ext{sl}"])
            else:
                em.dma("sp", xt[sl][:, 0:2, :], C.C2.rearrange("(s p) d -> p s d", p=128), W=[f"Ext{sl}"])

        def pro_a(j):
            sl = j % 2
            C.prologue_a(xt[sl], f"Ext{sl}", nsubE(j), xt[sl], f"Ext{sl}", ssq, rs, junk, "E")

        def pro_b(j):
            sl = j % 2
            path = 1 if j == NLT else 0
            C.prologue_b(nsubE(j), xt[sl], f"Ext{sl}", C.MODC[:, 1, 0, 0, path, :], C.MODC[:, 1, 0, 1, path, :],
                         hT[sl], f"EhT{sl}", [0, 1])

        def qkE(j):
            sl = j % 2
            isctx = j == NLT
            n = nsubE(j) * 128
            for oc in (range(8, 16) if isctx else range(16)):
                b = 2 + oc % 2
                for k in range(8):
                    em.op("pe", lambda k=k, b=b, oc=oc, n=n, sl=sl: T.matmul(
                        PS[:, b, 0:n], lhsT=wq[:, k, oc * 128:(oc + 1) * 128], rhs=hT[sl][:, k, 0:n],
                        start=(k == 0), stop=(k == 7)), R=["Ewq", f"EhT{sl}"], W=[("ps", b)], inc=(k == 7))
                if oc < 8:
                    em.op("act", lambda b=b, oc=oc, sl=sl, n=n: A.activation(out=qk[sl][:, oc, 0:n], in_=PS[:, b, 0:n],
                                                                             func=AF.Copy, scale=0.125),
                          R=[("ps", b)], W=[f"Eqk{sl}"])
                else:
                    em.op("dve", lambda b=b, oc=oc, sl=sl, n=n: V.tensor_copy(out=qk[sl][:, oc, 0:n], in_=PS[:, b, 0:n]),
                          R=[("ps", b)], W=[f"Eqk{sl}"])
            if not isctx:
                em.dma("pool", QTv[:, :, j * 512:(j + 1) * 512], qk[sl][:, 0:8, :], R=[f"Eqk{sl}"], W=[("QT", j)])
                em.dma("pool", KTv[:, :, j * 512:(j + 1) * 512], qk[sl][:, 8:16, :], R=[f"Eqk{sl}"], W=[("KT", j)])
            else:
                em.dma("pool", KTv[:, :, NL:NL + LC], qk[sl][:, 8:16, 0:LC], R=[f"Eqk{sl}"], W=[("KT", j)])

        def vE(j):
            sl = j % 2
            isctx = j == NLT
            nsub = nsubE(j)
            for s in range(nsub):
                pb = 4 + 2 * (s % 2)
                for h in range(2):
                    for k in range(8):
                        em.op("pe", lambda k=k, h=h, s=s, pb=pb, sl=sl: T.matmul(
                            PS[:, pb + h, :], lhsT=hT[sl][:, k, s * 128:(s + 1) * 128], rhs=wq[:, k, 2048 + h * 512:2048 + (h + 1) * 512],
                            start=(k == 0), stop=(k == 7)), R=["Ewq", f"EhT{sl}"], W=[("ps", pb + h)], inc=(k == 7))
                evc[0] += 1
                src = PS[:, pb:pb + 2, :].rearrange("p a b -> p (a b)")
                if evc[0] % 2:
                    em.op("dve", lambda s=s, sl=sl, src=src: V.tensor_copy(out=vst[sl][:, s, :], in_=src),
                          R=[("ps", pb), ("ps", pb + 1)], W=[f"Evst{sl}"])
                else:
                    em.op("act", lambda s=s, sl=sl, src=src: A.copy(out=vst[sl][:, s, :], in_=src),
                          R=[("ps", pb), ("ps", pb + 1)], W=[f"Evst{sl}"])
            t0 = NL if isctx else j * 512
            em.dma("pool", C.VD[t0:t0 + nsub * 128, :].rearrange("(s p) d -> p s d", p=128), vst[sl][:, 0:nsub, :],
                   R=[f"Evst{sl}"], W=[("VD", j)])

        load(0)
        load(1)
        pro_a(0)
        pro_b(0)
        for j in range(NTL):
            qkE(j)
            if j + 2 < NTL:
                load(j + 2)
            if j + 1 < NTL:
                pro_a(j + 1)
            vE(j)
            if j + 1 < NTL:
                pro_b(j + 1)


def phase_F(C):
    nc, em, PS = C.nc, C.em, C.PS
    V, A, G, T = nc.vector, nc.scalar, nc.gpsimd, nc.tensor
    with ExitStack() as ps_:
        sb = lambda n, s, d=F32: ps_.enter_context(nc.sbuf_tensor("AT" + n, list(s), d))
        wo = sb("wo", [128, 8, D], BF16)
        KTb = [sb(f"KTb{i}", [128, 8, 1024], BF16) for i in range(2)]
        Vb = [sb(f"Vb{i}", [128, 8, 16, 65], BF16) for i in range(2)]
        QTb = sb("QTb", [128, 8, 512], BF16); xt = sb("xt", [128, 4, D])
        KTc = sb("KTc", [128, 8, LC], BF16); Vc = sb("Vc", [128, 2, 16, 65], BF16)
        BTi = sb("BTi", [128, 16, 5, 128], BF16); BTe = sb("BTe", [128, 16, 8, 128], BF16)
        PT = [sb(f"PT{i}", [128, 1280], BF16) for i in range(2)]
        Ot = sb("Ot", [128, D]); OTt = sb("OTt", [128, 8, 128], BF16); rden = sb("rden", [128, 4])
        Gx = sb("Gx", [128, D]); ssq = sb("ssq", [128, 4]); rs = sb("rs", [128, 4]); junk = sb("junk", [128, D], BF16)
        tt = sb("tt", [128, D])
        _wload_bf(C, wo, C.WBF["wona"].rearrange("(k p) n -> p k n", p=128), "Awo", D, "cvwona")
        em.dma("sp", Gx[:], C.GROW[1, 0, 0, :].partition_broadcast(128), W=["AGx"])
        em.dma("sp", BTi[:], C.BT_d, W=["ABTi"])
        KTv = C.KT.rearrange("(c p) t -> p c t", p=128); QTv = C.QT.rearrange("(c p) t -> p c t", p=128)
        em.dma("sp", KTc[:], KTv[:, :, NL:NL + LC], W=["AKTc"])
        for i in range(2):
            em.op("pool", lambda i=i: G.memset(Vb[i][:, :, :, 64:65], 1.0), W=[f"AVb{i}"])
        em.op("pool", lambda: G.memset(Vc[:, :, :, 64:65], 1.0), W=["AVc"])
        for c in range(2):
            em.dma("sp", Vc[:, c, :, 0:64], C.VD[NL + c * 128:NL + (c + 1) * 128, :].rearrange("p (h d) -> p h d", d=64), W=["AVc"])

        def kbof(blk):
            return 0 if blk == 0 else (56 if blk == NLT - 1 else 8 * blk - 4)

        def load(blk, sl):
            kb = kbof(blk)
            em.dma("sp", KTb[sl][:], KTv[:, :, kb * 64:kb * 64 + 1024], W=[f"AKTb{sl}"])
            for c in range(8):
                t0 = kb * 64 + c * 128
                em.dma("sp", Vb[sl][:, c, :, 0:64], C.VD[t0:t0 + 128, :].rearrange("p (h d) -> p h d", d=64), W=[f"AVb{sl}"])
        load(0, 0)
        for blk in range(NLT):
            sl = blk % 2
            r0 = 8 * blk
            edge = blk in (0, NLT - 1)
            eb = 0 if blk == 0 else 1
            nloc = 8 if edge else 5
            nch = nloc + 2
            em.dma("sp", QTb[:], QTv[:, :, r0 * 64:r0 * 64 + 512], W=["AQTb"])
            em.dma("sp", xt[:], C.X2[blk * 512:(blk + 1) * 512, :].rearrange("(s p) d -> p s d", p=128), W=["Axt"])
            if blk + 1 < NLT:
                load(blk + 1, 1 - sl)

            def sbase(h):
                return 0 if edge else 2 * (h % 2)

            def cpos(cl, h):
                if edge:
                    return (cl // 4, (cl % 4) * 128)
                return (2 * (h % 2) + cl // 4, (cl % 4) * 128)

            def qk(i, h):
                off = 0 if edge else i
                if edge and h == 0:
                    em.dma("sp", BTe[:], C.BTE_d[eb, i], W=["ABTe"])
                j = h // 2; e = h % 2
                p0, p1 = 64 * e, 64 * e + 64
                for cl in range(nch):
                    bk, co = cpos(cl, h)
                    o = PS[:, bk, co:co + 128]
                    if cl < nloc:
                        lt = KTb[sl][p0:p1, j, (off + cl) * 128:(off + cl + 1) * 128]
                    else:
                        lt = KTc[p0:p1, j, (cl - nloc) * 128:(cl - nloc + 1) * 128]
                    last = cl == nch - 1
                    em.op("pe", lambda o=o, lt=lt, j=j, p0=p0, p1=p1, i=i, cl=cl, last=last: T.matmul(
                        o, lhsT=lt, rhs=QTb[p0:p1, j, i * 128:(i + 1) * 128], start=(cl % 4 == 0),
                        stop=(edge and last)), R=[f"AKTb{sl}", "AKTc", "AQTb"], W=[("ps", bk)], inc=(edge and last))
                    if edge:
                        if cl in (3, 7):
                            em.op("pe", lambda h=h, bk=bk, cl=cl: T.matmul(
                                PS[:, bk, :], lhsT=C.identb[:, :], rhs=BTe[:, h, cl - 3:cl + 1, :].rearrange("p a b -> p (a b)"),
                                start=False, stop=True), R=["identb", "ABTe"], W=[("ps", bk)], inc=False)
                    else:
                        if cl == 3:
                            em.op("pe", lambda h=h, bk=bk: T.matmul(
                                PS[:, bk, :], lhsT=C.identb[:, :], rhs=BTi[:, h, 0:4, :].rearrange("p a b -> p (a b)"),
                                start=False, stop=True), R=["identb", "ABTi"], W=[("ps", bk)], inc=False)
                        if cl == 6:
                            em.op("pe", lambda h=h, bk=bk: T.matmul(
                                PS[:, bk, 0:128], lhsT=C.identb[:, :], rhs=BTi[:, h, 4, :],
                                start=False, stop=True), R=["identb", "ABTi"], W=[("ps", bk)], inc=True)

            def ex(i, h):
                b0 = sbase(h)
                nb = 3 if edge else 2
                src = PS[:, b0:b0 + nb, :].rearrange("p a b -> p (a b)")[:, 0:nch * 128]
                em.op("act", lambda src=src, h=h: A.activation(out=PT[h % 2][:, 0:nch * 128], in_=src, func=AF.Exp),
                      R=[("ps", b0 + q) for q in range(nb)], W=[f"APT{h % 2}"])

            def pv(i, h):
                off = 0 if edge else i
                ob = 4 + (h // 4) % 2
                so = (h % 4) * 128
                for c in range(nch):
                    rhs = Vb[sl][:, off + c, h, :] if c < nloc else Vc[:, c - nloc, h, :]
                    em.op("pe", lambda c=c, rhs=rhs, h=h, ob=ob, so=so: T.matmul(
                        PS[:, ob, so:so + 65], lhsT=PT[h % 2][:, c * 128:(c + 1) * 128], rhs=rhs,
                        start=(c == 0), stop=(c == nch - 1)), R=[f"APT{h % 2}", f"AVb{sl}", "AVc"], W=[("ps", ob)],
                          inc=(c == nch - 1))
                if h % 4 == 3:
                    em.op("dve", lambda ob=ob: V.reciprocal(out=rden[:, 0:4], in_=PS[:, ob, 64:512:128]),
                          R=[("ps", ob)], W=["Arden"])
                    for hh in range(4):
                        hd = h - 3 + hh
                        em.op("dve", lambda ob=ob, hh=hh, hd=hd: V.tensor_scalar(
                            out=Ot[:, hd * 64:(hd + 1) * 64], in0=PS[:, ob, hh * 128:hh * 128 + 64],
                            scalar1=rden[:, hh:hh + 1], scalar2=None, op0=ALU.mult), R=[("ps", ob), "Arden"], W=["AOt"])

            def fin(i):
                for k in range(8):
                    em.op("pe", lambda k=k: T.transpose(PS[:, 6 + k // 4, (k % 4) * 128:(k % 4 + 1) * 128],
                                                        Ot[:, k * 128:(k + 1) * 128], C.ident[:]),
                          R=["AOt", "ident"], W=[("ps", 6 + k // 4)], inc=(k % 4 == 3))
                for hb in range(2):
                    em.op("act" if hb else "dve",
                          (lambda hb=hb: A.copy(out=OTt[:, 4 * hb:4 * hb + 4, :].rearrange("p a b -> p (a b)"), in_=PS[:, 6 + hb, :])) if hb else
                          (lambda hb=hb: V.tensor_copy(out=OTt[:, 4 * hb:4 * hb + 4, :].rearrange("p a b -> p (a b)"), in_=PS[:, 6 + hb, :])),
                          R=[("ps", 6 + hb)], W=["AOTt"])
                for hf in range(2):
                    for k in range(8):
                        em.op("pe", lambda k=k, hf=hf: T.matmul(PS[:, 6 + hf, :], lhsT=OTt[:, k, :], rhs=wo[:, k, hf * 512:(hf + 1) * 512],
                                                                start=(k == 0), stop=(k == 7)),
                              R=["AOTt", "Awo"], W=[("ps", 6 + hf)], inc=(k == 7))
                C.epilogue(6, xt[:, i, :], "Axt", Gx[:, :], "AGx", ssq, rs, junk, tt, "A")

            items = [(i, h) for i in range(4) for h in range(16)]
            qk(*items[0])
            for n_, (i, h) in enumerate(items):
                nxt = items[n_ + 1] if n_ + 1 < len(items) else None
                if edge:
                    ex(i, h)
                    if nxt:
                        qk(*nxt)
                else:
                    if nxt:
                        qk(*nxt)
                    ex(i, h)
                pv(i, h)
                if h == 15:
                    fin(i)
            em.dma("pool", C.X3[blk * 512:(blk + 1) * 512, :].rearrange("(s p) d -> p s d", p=128), xt[:], R=["Axt"], W=[("X3", blk)])


PHASES += [("E", phase_E), ("F", phase_F),
           ("G", lambda C: phase_FFN(C, 1, C.X3, None, C.out_d, None, "X3", "OUT"))]
```

```python
import math
from contextlib import ExitStack
import numpy as np
import ml_dtypes
import concourse.bass as bass
import concourse.mybir as mybir
from concourse.bass_utils import run_bass_kernel_spmd

F32 = mybir.dt.float32
BF16 = mybir.dt.bfloat16
AF = mybir.ActivationFunctionType
ALU = mybir.AluOpType
NPBF = ml_dtypes.bfloat16

D = 1024
S = 8192
LC = 256
NT = S + LC
FH = 2816
NJ = FH // 128
EPS = 1e-6
NCORES = 8
NL = 4608
NLT = NL // 512
LBASE = 3584
NEG = -30000.0
QENG = ("pool", "dve", "pool", "dve")


class Em:
    LIMIT = 30000
    NDS = 8

    def __init__(self, nc, es):
        self.nc = nc
        self.es = es
        self.eng = dict(pe=nc.tensor, act=nc.scalar, dve=nc.vector, pool=nc.gpsimd, sp=nc.sync)
        self.sem = {}
        self.semkey = {}
        self.cnt = {}
        self.nsem = 0
        for e in self.eng:
            self._newsem(e)
        self.waited = {e: {} for e in self.eng}
        self.lastw = {}
        self.readers = {}
        self.dsem = {}
        self.ndma = {}
        for q in ("sp", "pool", "act"):
            self.dsem[q] = [es.enter_context(nc.semaphore(f"d{q}{i}")) for i in range(self.NDS)]
            self.ndma[q] = 0
        self.bg = []

    def _newsem(self, e):
        self.nsem += 1
        self.sem[e] = self.es.enter_context(self.nc.semaphore(f"s{e}{self.nsem}"))
        self.semkey[e] = (e, self.nsem)
        self.cnt[e] = 0

    def _deps(self, engine, R, W):
        deps = {}
        for k in list(R) + list(W):
            t = self.lastw.get(k)
            if t is not None:
                if t[0] not in deps or deps[t[0]][2] < t[2]:
                    deps[t[0]] = t
        for k in W:
            for t in self.readers.get(k, {}).values():
                if t[0] not in deps or deps[t[0]][2] < t[2]:
                    deps[t[0]] = t
        e = self.eng[engine]
        for sk, t in deps.items():
            if engine == "pe" and t[3] == "pe":
                continue
            if self.waited[engine].get(sk, 0) >= t[2]:
                continue
            e.wait_ge(t[1], t[2])
            self.waited[engine][sk] = t[2]

    def _record(self, tok, R, W):
        for k in W:
            self.lastw[k] = tok
            self.readers[k] = {}
        for k in R:
            d = self.readers.setdefault(k, {})
            if tok[0] not in d or d[tok[0]][2] < tok[2]:
                d[tok[0]] = tok

    def op(self, engine, fn, R=(), W=(), inc=True):
        self._deps(engine, R, W)
        ins = fn()
        if inc:
            ins.then_inc(self.sem[engine], 1)
            self.cnt[engine] += 1
            tok = (self.semkey[engine], self.sem[engine], self.cnt[engine], engine)
            self._record(tok, R, W)
            if self.cnt[engine] >= self.LIMIT:
                self._newsem(engine)
        else:
            tok = (self.semkey[engine], self.sem[engine], self.cnt[engine] + 1, engine)
            self._record(tok, R, W)
        return ins

    def dma(self, q, out, in_, R=(), W=(), **kw):
        self._deps(q, R, W)
        i = self.ndma[q]
        self.ndma[q] += 1
        sem = self.dsem[q][i % self.NDS]
        rnd = i // self.NDS
        sk = ("dma", q, i % self.NDS)
        if rnd > 0 and self.waited[q].get(sk, 0) < 16 * rnd:
            self.eng[q].wait_ge(sem, 16 * rnd)
            self.waited[q][sk] = 16 * rnd
        self.eng[q].dma_start(out=out, in_=in_, **kw).then_inc(sem, 16)
        tok = (sk, sem, 16 * (rnd + 1), None)
        self._record(tok, R, W)

    def dma_bg(self, q, out, in_, R=(), W=(), **kw):
        self._deps(q, R, W)
        sem = self.es.enter_context(self.nc.semaphore(f"bg{len(self.bg)}"))
        self.bg.append(sem)
        self.eng[q].dma_start(out=out, in_=in_, **kw).then_inc(sem, 16)
        self._record((("bg", len(self.bg)), sem, 16, None), R, W)

    def finish(self):
        sp = self.eng["sp"]
        for sem in self.bg:
            sp.wait_ge(sem, 16)
        for q in self.dsem:
            n = self.ndma[q]
            for s in range(self.NDS):
                uses = (n - s + self.NDS - 1) // self.NDS if n > s else 0
                if uses > 0:
                    sp.wait_ge(self.dsem[q][s], 16 * uses)
        for e in ("pe", "act", "dve", "pool"):
            if self.cnt[e] > 0:
                sp.wait_ge(self.sem[e], self.cnt[e])


def _tables():
    t = {}
    t["ident"] = np.eye(128, dtype=np.float32)
    t["identb"] = np.eye(128, dtype=np.float32).astype(NPBF)
    a = np.arange(128, dtype=np.float64)
    ang = 2 * np.pi * np.outer(a, a) / 128.0
    t["T1"] = np.concatenate([np.cos(ang), -np.sin(ang)], axis=1).astype(NPBF)
    t2 = np.arange(64, dtype=np.float64)[:, None, None]
    k1 = np.arange(128, dtype=np.float64)[None, :, None]
    k2 = np.arange(64, dtype=np.float64)[None, None, :]
    ph = 2 * np.pi * (t2 * k2 / 64.0 + t2 * k1 / 8192.0)
    Mc, Ms = np.cos(ph), np.sin(ph)
    t["M2"] = np.concatenate([Ms, Mc, -Ms], axis=2).astype(NPBF)
    c = np.arange(64, dtype=np.float64)
    angc = 2 * np.pi * np.outer(c, c) / 64.0
    Cc, Sc = np.cos(angc), np.sin(angc)
    z = np.zeros((64, 64))
    Cbd = np.block([[Cc, z], [z, Cc]])
    Sbd = np.block([[Sc, z], [z, Sc]])
    sx = 1.0 / math.sqrt(8192.0 * 64.0)
    t["CSf"] = (np.concatenate([Cbd, Sbd], axis=1) * sx).astype(NPBF)
    sc_ = 1.0 / math.sqrt(256.0 * 64.0)
    t["CSc"] = (np.concatenate([Cbd, Sbd], axis=1) * sc_).astype(NPBF)
    p = np.arange(256, dtype=np.float64)
    angp = 2 * np.pi * np.outer(p, p) / 256.0
    T256 = np.concatenate([np.cos(angp), -np.sin(angp)], axis=1)
    t["T256"] = T256.reshape(2, 128, 512).transpose(1, 0, 2).copy().astype(NPBF)
    return t


_TABLES = None


def _bias_tables(rpb):
    H = 16
    out = np.full((5, 5 * 128, H, 128), NEG, dtype=np.float32)
    kr = np.arange(10)[:, None, None, None]
    kc = np.arange(64)[None, :, None, None]
    qr = np.arange(2)[None, None, :, None]
    qc = np.arange(64)[None, None, None, :]
    cs = np.clip(qc - 8, 0, 48)
    for vi, (gp, ks) in enumerate([(0, 0), (2, 0), (60, 56), (124, 118), (126, 118)]):
        gq = gp + qr
        rs = np.clip(gq - 4, 0, 120)
        gk = ks + kr
        valid = (gk >= rs) & (gk < rs + 8) & (kc >= cs) & (kc < cs + 16)
        dr = np.clip(gk - gq + 7, 0, 14)
        dc = np.clip(kc - qc + 15, 0, 30)
        valid, dr, dc = np.broadcast_arrays(valid, dr, dc)
        vals = rpb[:, dr, dc]
        vals = np.where(valid[None], vals, NEG)
        out[vi] = vals.transpose(1, 2, 0, 3, 4).reshape(640, H, 128)
    bt = out.reshape(5, 5, 128, H, 128).transpose(0, 2, 3, 1, 4)
    return np.ascontiguousarray(bt).astype(NPBF)


def _local_rows(h):
    if h == 0:
        return np.arange(72)
    return np.concatenate([np.arange(64, 128), np.arange(56, 64)])


def _edge_tables(rpb, h):
    H = 16
    lr = _local_rows(h)
    out = np.empty((2, 4, 128, H, 8, 128), dtype=NPBF)
    kc = np.arange(64)[None, :, None, None]
    qr = np.arange(2)[None, None, :, None]
    qc = np.arange(64)[None, None, None, :]
    cs = np.clip(qc - 8, 0, 48)
    keyrows = [np.concatenate([np.arange(68, 72), np.arange(0, 12)]), np.arange(52, 68)]
    for eb, r0 in enumerate([0, 56]):
        gk = lr[keyrows[eb]][:, None, None, None]
        for i in range(4):
            gq = lr[r0 + 2 * i + np.arange(2)][None, None, :, None]
            rs = np.clip(gq - 4, 0, 120)
            valid = (gk >= rs) & (gk < rs + 8) & (kc >= cs) & (kc < cs + 16)
            dr = np.clip(gk - gq + 7, 0, 14)
            dc = np.clip(kc - qc + 15, 0, 30)
            valid, dr, dc = np.broadcast_arrays(valid, dr, dc)
            vals = np.where(valid[None], rpb[:, dr, dc], NEG)
            t = vals.transpose(1, 2, 0, 3, 4).reshape(8, 128, H, 128)
            out[eb, i] = t.transpose(1, 2, 0, 3).astype(NPBF)
    return out


def _col(v, nchunk):
    return np.ascontiguousarray(np.asarray(v, np.float32).reshape(nchunk, 128).T)


def build(stop="all", debug=False):
    nc = bass.Bass("TRN2", target_bir_lowering=False)

    def din(name, shape, dt=F32):
        return nc.dram_tensor(name, list(shape), dt, kind="ExternalInput").ap()

    skind = "ExternalOutput" if debug else "Internal"

    def dscr(name, shape, dt):
        return nc.dram_tensor(name, list(shape), dt, kind=skind).ap()

    x_d = din("x", [S, D]); xloc_d = din("xloc", [NL, D]); hsel_d = din("hsel", [128, 2]); ctx_d = din("ctx", [LC, D]); ccols_d = din("ccols", [128, 16])
    wmod_d = din("w_mod", [2, D, 6 * D]); bmodc_d = din("bmodc", [128, 2, 48]); bmod_d = din("b_mod", [2, 6 * D])
    gcols_d = din("gcols", [128, 2, 2, 8]); gpm_d = din("g_post_mix", [2, D]); gpf_d = din("g_post_ffn", [2, D])
    win_d = din("w_in_ab", [D, 1536]); woab_d = din("w_out_ab", [D, D])
    wg_d = din("w_ffn_gate", [2, D, FH]); wu_d = din("w_ffn_up", [2, D, FH]); wd_d = din("w_ffn_down", [2, FH, D])
    wqkv_d = din("w_qkv_na", [D, 3 * D]); wona_d = din("w_out_na", [D, D])
    wbd_d = din("wbd", [2, 2, 4, 128, 128]); lcols_d = din("lcols", [128, 4, 2, 3]); convc_d = din("convc", [128, 4, 5]); cdiag_d = din("cdiag", [4, 4, 128, 128])
    ident_d = din("ident", [128, 128]); identb_d = din("identb", [128, 128], BF16)
    T1_d = din("T1", [128, 256], BF16); M2_d = din("M2", [64, 128, 192], BF16); CSf_d = din("CSf", [128, 256], BF16)
    CSc_d = din("CSc", [128, 256], BF16); T256_d = din("T256", [128, 2, 512], BF16)
    BT_d = din("BT", [128, 16, 5, 128], BF16); BTE_d = din("BTE", [2, 4, 128, 16, 8, 128], BF16)
    out_d = nc.dram_tensor("out", [4096, D], F32, kind="ExternalOutput").ap()

    UG = dscr("UG", [8, 128, S], F32)
    UGc = dscr("UGc", [8, 128, LC], F32)
    MIXT = dscr("MIXT", [D, NT], BF16)
    GROW = dscr("GROW", [2, 2, 2, D], F32)
    X1 = dscr("X1", [NL, D], F32); C1 = dscr("C1", [LC, D], F32)
    X2 = dscr("X2", [NL, D], F32); C2 = dscr("C2", [LC, D], F32)
    X3 = dscr("X3", [NL, D], F32)
    QT = dscr("QT", [D, NL], BF16); KT = dscr("KT", [D, NL + LC], BF16); VD = dscr("VD", [NL + LC, D], BF16)

    WBF = {"woab": dscr("woab_bf", [D, D], BF16), "wona": dscr("wona_bf", [D, D], BF16),
           "wqkv": dscr("wqkv_bf", [D, 3 * D], BF16),
           "wg": dscr("wg_bf", [2, D, FH], BF16), "wu": dscr("wu_bf", [2, D, FH], BF16), "wd": dscr("wd_bf", [2, FH, D], BF16)}

    with ExitStack() as es:
        em = Em(nc, es)

        def barrier():
            for e in ("pe", "act", "dve", "pool", "sp"):
                eng = em.eng[e]
                for f in ("pe", "act", "dve", "pool"):
                    if f != e and em.cnt[f] > 0 and em.waited[e].get(em.semkey[f], 0) < em.cnt[f]:
                        eng.wait_ge(em.sem[f], em.cnt[f]); em.waited[e][em.semkey[f]] = em.cnt[f]
                for q in em.dsem:
                    n = em.ndma[q]
                    for s_ in range(em.NDS):
                        uses = (n - s_ + em.NDS - 1) // em.NDS if n > s_ else 0
                        sk = ("dma", q, s_)
                        if uses > 0 and em.waited[e].get(sk, 0) < 16 * uses:
                            eng.wait_ge(em.dsem[q][s_], 16 * uses); em.waited[e][sk] = 16 * uses

        PS = es.enter_context(nc.psum_tensor("PS", [128, 8, 512], F32))
        ident = es.enter_context(nc.sbuf_tensor("ident_s", [128, 128], F32))
        identb = es.enter_context(nc.sbuf_tensor("identb_s", [128, 128], BF16))
        MODC = es.enter_context(nc.sbuf_tensor("MODC", [128, 2, 2, 2, 2, 8], F32))
        mhalf = es.enter_context(nc.sbuf_tensor("mhalf", [128, 8], F32))
        em.dma("sp", ident[:], ident_d, W=["ident"])
        em.dma("sp", identb[:], identb_d, W=["identb"])
        em.op("dve", lambda: nc.vector.memset(mhalf[:], -0.5), W=["mhalf"])

        V = nc.vector; A = nc.scalar; G = nc.gpsimd; T = nc.tensor

        def pbank(b):
            return ("ps", b)

        def rstd_from_ssq(ssq, rs, n, tag):
            em.op("dve", lambda: V.tensor_scalar(out=rs[:, 0:n], in0=ssq[:, 0:n], scalar1=1.0 / D, scalar2=EPS,
                                                 op0=ALU.mult, op1=ALU.add), R=[tag + "ssq"], W=[tag + "rs"])
            em.op("pool", lambda: G.tensor_tensor(out=rs[:, 0:n], in0=rs[:, 0:n], in1=mhalf[:, 0:n], op=ALU.pow),
                  R=[tag + "rs", "mhalf"], W=[tag + "rs"])

        def prologue_a(xt, xkey, nsub, xs, xskey, ssq, rs, junk, tag):
            for s in range(nsub):
                em.op("act", lambda s=s: A.activation(out=junk[:, :], in_=xt[:, s, :], func=AF.Square,
                                                      accum_out=ssq[:, s:s + 1]),
                      R=[xkey], W=[tag + "junk", tag + "ssq"])
            rstd_from_ssq(ssq, rs, nsub, tag)
            for s in range(nsub):
                em.op("act", lambda s=s: A.activation(out=xs[:, s, :], in_=xt[:, s, :], func=AF.Copy,
                                                      scale=rs[:, s:s + 1]),
                      R=[xkey, tag + "rs"], W=[xskey])

        def prologue_b(nsub, xs, xskey, Acol, Bcol, hT, hkey, tb):
            for k in range(8):
                b = tb[k % len(tb)]
                for s in range(nsub):
                    em.op("pe", lambda s=s, k=k, b=b: T.transpose(PS[:, b, s * 128:(s + 1) * 128],
                                                                  xs[:, s, k * 128:(k + 1) * 128], ident[:]),
                          R=[xskey, "ident"], W=[pbank(b)], inc=(s == nsub - 1))
                em.op("act", lambda k=k, b=b: A.activation(out=hT[:, k, 0:nsub * 128], in_=PS[:, b, 0:nsub * 128],
                                                           func=AF.Identity, scale=Acol[:, k:k + 1],
                                                           bias=Bcol[:, k:k + 1]),
                      R=[pbank(b), "MODC"], W=[hkey])

        def prologue(xt, xkey, nsub, xs, xskey, Acol, Bcol, hT, hkey, ssq, rs, junk, tag, tb):
            prologue_a(xt, xkey, nsub, xs, xskey, ssq, rs, junk, tag)
            prologue_b(nsub, xs, xskey, Acol, Bcol, hT, hkey, tb)

        def epilogue(psb, xsub, xkey, Gb, gkey, ssq, rs, junk, tt, tag):
            yv = PS[:, psb:psb + 2, :]
            em.op("act", lambda: A.activation(out=junk[:, :], in_=yv, func=AF.Square, accum_out=ssq[:, 0:1]),
                  R=[pbank(psb), pbank(psb + 1)], W=[tag + "junk", tag + "ssq"])
            rstd_from_ssq(ssq, rs, 1, tag)
            em.op("dve", lambda: V.tensor_tensor(out=tt[:, :], in0=yv, in1=Gb, op=ALU.mult),
                  R=[pbank(psb), pbank(psb + 1), gkey], W=[tag + "tt"])
            em.op("dve", lambda: V.scalar_tensor_tensor(out=xsub, in0=tt[:, :], scalar=rs[:, 0:1], in1=xsub,
                                                        op0=ALU.mult, op1=ALU.add),
                  R=[tag + "tt", tag + "rs", xkey], W=[xkey])

        with ExitStack() as pes:
            sb = lambda n, s, d=F32: pes.enter_context(nc.sbuf_tensor(n, list(s), d))
            cc = sb("cc", [128, 16]); scc = sb("scc", [128, 16]); rhs2 = sb("rhs2", [128, 8, 2], BF16)
            bmc = sb("bmc", [128, 2, 48]); bm1 = sb("bm1", [128, 2, 48]); gco = sb("gco", [128, 2, 2, 8])
            bmrow = sb("bmrow", [2, 2, 2, D]); grow = sb("grow", [2, 2, 2, D]); grt = sb("grt", [2, 2, 2, D])
            wm = [sb(f"wm{i}", [128, 8, 512], BF16) for i in range(3)]
            em.dma("sp", cc[:], ccols_d, W=["cc"])
            em.dma("sp", bmc[:], bmodc_d, W=["bmc"])
            em.dma("sp", gco[:], gcols_d, W=["gco"])
            for l in range(2):
                for w_, (src, off) in enumerate([(bmod_d, 2 * D), (bmod_d, 5 * D)]):
                    em.dma("sp", bmrow[:, l, w_, :], src[l, off:off + D].partition_broadcast(2), W=["bmrow"])
                em.dma("sp", grow[:, l, 0, :], gpm_d[l, :].partition_broadcast(2), W=["grow"])
                em.dma("sp", grow[:, l, 1, :], gpf_d[l, :].partition_broadcast(2), W=["grow"])
            em.op("act", lambda: A.activation(out=scc[:], in_=cc[:], func=AF.Silu), R=["cc"], W=["scc"])
            em.op("dve", lambda: V.tensor_copy(out=rhs2[:, :, 0], in_=scc[:, 0:8]), R=["scc"], W=["rhs2"])
            em.op("dve", lambda: V.tensor_copy(out=rhs2[:, :, 1], in_=scc[:, 8:16]), R=["scc"], W=["rhs2"])
            em.op("dve", lambda: V.tensor_scalar(out=bm1[:], in0=bmc[:], scalar1=1.0, scalar2=None, op0=ALU.add),
                  R=["bmc"], W=["bm1"])
            it = 0
            for l in range(2):
                wsrc = wmod_d[l].rearrange("(k p) n -> p k n", p=128)
                for nb in range(12):
                    slot = it % 3; it += 1
                    wt = wm[slot]; wk = f"wm{slot}"
                    em.dma("pool", wt[:], wsrc[:, :, nb * 512:(nb + 1) * 512], W=[wk])
                    v = nb // 2; half = nb % 2
                    b = it % 4
                    if v in (2, 5):
                        w_ = 0 if v == 2 else 1
                        for k in range(8):
                            em.op("pe", lambda k=k, b=b, wt=wt: T.matmul(PS[0:2, b, :], lhsT=rhs2[:, k, :], rhs=wt[:, k, :],
                                                                        start=(k == 0), stop=(k == 7)),
                                  R=["rhs2", wk], W=[pbank(b)], inc=(k == 7))
                        dst = grt[:, l, w_, half * 512:(half + 1) * 512]
                        em.op("dve", lambda b=b, dst=dst, l=l, w_=w_, half=half: V.tensor_tensor(
                            out=dst, in0=PS[0:2, b, :], in1=bmrow[:, l, w_, half * 512:(half + 1) * 512], op=ALU.add),
                              R=[pbank(b), "bmrow"], W=["grt"])
                        em.op("dve", lambda dst=dst, l=l, w_=w_, half=half: V.tensor_tensor(
                            out=dst, in0=dst, in1=grow[:, l, w_, half * 512:(half + 1) * 512], op=ALU.mult),
                              R=["grt", "grow"], W=["grt"])
                        if half == 1:
                            em.dma("sp", GROW[l, w_, :, :], grt[:, l, w_, :], R=["grt"], W=[("GROW", l, w_)])
                    else:
                        sub = 0 if v < 2 else 1
                        isA = v in (1, 4)
                        for m in range(4):
                            ch = half * 4 + m
                            for k in range(8):
                                em.op("pe", lambda k=k, b=b, m=m, wt=wt: T.matmul(
                                    PS[:, b, 2 * m:2 * m + 2], lhsT=wt[:, k, m * 128:(m + 1) * 128], rhs=rhs2[:, k, :],
                                    start=(k == 0), stop=(k == 7)), R=["rhs2", wk], W=[pbank(b)], inc=(k == 7))
                            dst = MODC[:, l, sub, 0 if isA else 1, :, ch]
                            if isA:
                                em.op("dve", lambda b=b, m=m, dst=dst, l=l, v=v, ch=ch, sub=sub: V.tensor_scalar(
                                    out=dst, in0=PS[:, b, 2 * m:2 * m + 2], scalar1=bm1[:, l, v * 8 + ch:v * 8 + ch + 1],
                                    scalar2=gco[:, l, sub, ch:ch + 1], op0=ALU.add, op1=ALU.mult),
                                      R=[pbank(b), "bm1", "gco"], W=["MODC"])
                            else:
                                em.op("dve", lambda b=b, m=m, dst=dst, l=l, v=v, ch=ch: V.tensor_scalar(
                                    out=dst, in0=PS[:, b, 2 * m:2 * m + 2], scalar1=bmc[:, l, v * 8 + ch:v * 8 + ch + 1],
                                    scalar2=None, op0=ALU.add), R=[pbank(b), "bmc"], W=["MODC"])
            if debug:
                MODCd = nc.dram_tensor("MODCd", [128, 128], F32, kind="ExternalOutput").ap()
                em.dma("sp", MODCd, MODC[:].rearrange("p a b c d e -> p (a b c d e)"), R=["MODC"], W=["MODCd"])
            barrier()
        if stop == "0":
            em.finish()
            return nc
        C = type("C", (), {})()
        C.__dict__.update(locals())
        for name, fn in PHASES:
            fn(C)
            barrier()
            if stop == name:
                if hasattr(C, 'es_mix'):
                    C.es_mix.close()
                break
        em.finish()
    return nc


PHASES = []


def _host_inputs(inp, b):
    global _TABLES
    if _TABLES is None:
        _TABLES = _tables()
    f = lambda a: np.ascontiguousarray(np.asarray(a, dtype=np.float32))
    m = {}
    b, h = b // 2, b % 2
    m["x"] = f(inp["x"][b]); m["ctx"] = f(inp["ctx"][b])
    m["xloc"] = f(inp["x"][b].reshape(128, 64, D)[_local_rows(h)].reshape(NL, D))
    m["hsel"] = np.tile(np.array([[1.0 - h, float(h)]], np.float32), (128, 1))
    m["ccols"] = np.concatenate([_col(inp["c"][b], 8), _col(inp["c_ctx"], 8)], axis=1)
    m["w_mod"] = f(inp["w_mod"]); m["b_mod"] = f(inp["b_mod"])
    m["bmodc"] = np.ascontiguousarray(np.stack([_col(inp["b_mod"][l], 48) for l in range(2)], axis=1))
    m["gcols"] = np.ascontiguousarray(np.stack(
        [np.stack([_col(inp["g_pre_mix"][l], 8), _col(inp["g_pre_ffn"][l], 8)], axis=1) for l in range(2)], axis=1))
    m["g_post_mix"] = f(inp["g_post_mix"]); m["g_post_ffn"] = f(inp["g_post_ffn"])
    m["w_in_ab"] = f(inp["w_in_ab"][0]); m["w_out_ab"] = f(inp["w_out_ab"][0])
    m["w_ffn_gate"] = f(inp["w_ffn_gate"]); m["w_ffn_up"] = f(inp["w_ffn_up"]); m["w_ffn_down"] = f(inp["w_ffn_down"])
    m["w_qkv_na"] = f(inp["w_qkv_na"][0]); m["w_out_na"] = f(inp["w_out_na"][0])
    wbd = np.zeros((2, 2, 4, 128, 128), np.float32)
    for gi, key in enumerate(["lru_w_a", "lru_w_i"]):
        w = np.asarray(inp[key][0], np.float32)
        for d in range(2):
            for c in range(4):
                wbd[gi, d, c, 0:64, 0:64] = w[d, 2 * c]
                wbd[gi, d, c, 64:128, 64:128] = w[d, 2 * c + 1]
    m["wbd"] = wbd
    lc = np.zeros((128, 4, 2, 3), np.float32)
    for d in range(2):
        lc[:, :, d, 0] = _col(inp["lru_b_a"][0][d], 4)
        lc[:, :, d, 1] = _col(inp["lru_b_i"][0][d], 4)
        lc[:, :, d, 2] = _col(inp["lru_lam"][0][d], 4)
    m["lcols"] = lc
    cv = np.zeros((128, 4, 5), np.float32)
    for k in range(4):
        cv[:, :, k] = _col(inp["conv_w"][0][k], 4)
    cv[:, :, 4] = _col(inp["conv_b"][0], 4)
    m["convc"] = cv
    cd = np.zeros((4, 4, 128, 128), np.float32)
    ii = np.arange(128)
    for c in range(4):
        for k in range(4):
            cd[c, k, ii, ii] = cv[:, c, k]
    m["cdiag"] = cd
    for k in ("ident", "identb", "T1", "M2", "CSf", "CSc", "T256"):
        m[k] = _TABLES[k]
    rpb = np.asarray(inp["rpb_na"][0], np.float32)
    m["BT"] = np.ascontiguousarray(_bias_tables(rpb)[2])
    m["BTE"] = _edge_tables(rpb, h)
    return m


_NC_CACHE = {}


def kernel(**inputs):
    if "full" not in _NC_CACHE:
        _NC_CACHE["full"] = build()
    nc = _NC_CACHE["full"]
    in_maps = [_host_inputs(inputs, b) for b in range(NCORES)]
    res = run_bass_kernel_spmd(nc, in_maps, core_ids=list(range(NCORES)))
    out = np.empty((4, S, D), np.float32)
    for c in range(NCORES):
        b, h = c // 2, c % 2
        o = np.asarray(res.results[c]["out"], dtype=np.float32)
        out[b, 4096 * h:4096 * (h + 1)] = o
    return out


def _wload(C, dst, src_view, key, nk, ncols, step=512):
    for c0 in range(0, ncols, step):
        c1 = min(ncols, c0 + step)
        C.em.dma("pool", dst[:, :, c0:c1], src_view[:, :, c0:c1], W=[key])


def _wload_bf(C, dst, src_view, key, ncols, srckey, step=1024):
    cv_emit(C, 0, need=srckey)
    for c0 in range(0, ncols, step):
        c1 = min(ncols, c0 + step)
        C.em.dma("sp", dst[:, :, c0:c1], src_view[:, :, c0:c1], R=C.cvkeys[srckey], W=[key])


def preconvert_init(C):
    jobs = [("woab", C.woab_d, C.WBF["woab"]), ("wg0", C.wg_d[0], C.WBF["wg"][0]), ("wu0", C.wu_d[0], C.WBF["wu"][0]),
            ("wd0", C.wd_d[0], C.WBF["wd"][0]), ("wqkv", C.wqkv_d, C.WBF["wqkv"]), ("wona", C.wona_d, C.WBF["wona"]),
            ("wg1", C.wg_d[1], C.WBF["wg"][1]), ("wu1", C.wu_d[1], C.WBF["wu"][1]), ("wd1", C.wd_d[1], C.WBF["wd"][1])]
    C.cvkeys = {}
    C.cvq = []
    for key, src, dst in jobs:
        rows, cols = src.shape
        rstep = 512 if rows % 512 == 0 else 704
        C.cvkeys["cv" + key] = []
        for r0 in range(0, rows, rstep):
            for c0 in range(0, cols, 1024):
                c1 = min(cols, c0 + 1024)
                k_ = f"cv{key}_{r0}_{c0}"
                C.cvkeys["cv" + key].append(k_)
                C.cvq.append(("cv" + key, k_, dst[r0:r0 + rstep, c0:c1], src[r0:r0 + rstep, c0:c1]))


def cv_emit(C, n=1, need=None):
    while C.cvq and (n > 0 or (need is not None and any(j[0] == need for j in C.cvq))):
        big, k_, dst, src = C.cvq.pop(0)
        C.em.dma_bg("pool", dst, src, W=[k_])
        n -= 1


def phase_A(C):
    nc, em, PS = C.nc, C.em, C.PS
    V, A, G, T = nc.vector, nc.scalar, nc.gpsimd, nc.tensor
    pes = C.es_mix = ExitStack()
    preconvert_init(C)
    C.F = pes.enter_context(nc.sbuf_tensor("Fbuf", [128, 64, 512], BF16))
    C.fTc = pes.enter_context(nc.sbuf_tensor("fTc", [128, 4, LC], BF16))
    F, fTc = C.F, C.fTc
    with ExitStack() as ps_:
        sb = lambda n, s, d=F32: ps_.enter_context(nc.sbuf_tensor(n, list(s), d))
        win = sb("win", [128, 8, 1536], BF16)
        xt = [sb(f"Axt{i}", [128, 4, D]) for i in range(2)]
        hT = [sb(f"AhT{i}", [128, 8, 512], BF16) for i in range(2)]
        ugst = [sb(f"Aug{i}", [128, 8, 512]) for i in range(2)]
        ssq = sb("Assq", [128, 4]); rs = sb("Ars", [128, 4]); junk = sb("Ajunk", [128, D], BF16)
        _wload(C, win, C.win_d.rearrange("(k p) n -> p k n", p=128), "win", 8, 1536)
        xsrc = C.x_d.rearrange("(t1 t2) d -> t1 t2 d", t2=64)
        Acol = C.MODC[:, 0, 0, 0, 0, :]; Bcol = C.MODC[:, 0, 0, 1, 0, :]
        AcolC = C.MODC[:, 0, 0, 0, 1, :]; BcolC = C.MODC[:, 0, 0, 1, 1, :]
        evc = [0]

        def loadA(j):
            sl = j % 2
            if j < 16:
                em.dma("sp", xt[sl][:], xsrc[:, 4 * j:4 * (j + 1), :], W=[f"Axt{sl}"])
            elif j == 16:
                em.dma("sp", xt[sl][:, 0:2, :], C.ctx_d.rearrange("(s p) d -> p s d", p=128), W=[f"Axt{sl}"])

        def nsubA(j):
            return 2 if j == 16 else 4

        def pro_a(j):
            sl = j % 2
            C.prologue_a(xt[sl], f"Axt{sl}", nsubA(j), xt[sl], f"Axt{sl}", ssq, rs, junk, "A")

        def pro_b(j):
            sl = j % 2
            isctx = j == 16
            C.prologue_b(nsubA(j), xt[sl], f"Axt{sl}", AcolC if isctx else Acol, BcolC if isctx else Bcol,
                         hT[sl], f"AhT{sl}", [0, 1])

        def ugA(j):
            sl = j % 2
            isctx = j == 16
            n = nsubA(j) * 128
            for oc in range(8):
                b = 2 + oc % 3
                for k in range(8):
                    em.op("pe", lambda k=k, b=b, oc=oc, sl=sl, n=n: T.matmul(
                        PS[:, b, 0:n], lhsT=win[:, k, oc * 128:(oc + 1) * 128], rhs=hT[sl][:, k, 0:n],
                        start=(k == 0), stop=(k == 7)), R=["win", f"AhT{sl}"], W=[("ps", b)], inc=(k == 7))
                evc[0] += 1
                if evc[0] % 2:
                    em.op("dve", lambda b=b, oc=oc, sl=sl, n=n: V.tensor_copy(out=ugst[sl][:, oc, 0:n], in_=PS[:, b, 0:n]),
                          R=[("ps", b)], W=[f"Aug{sl}"])
                else:
                    em.op("act", lambda b=b, oc=oc, sl=sl, n=n: A.copy(out=ugst[sl][:, oc, 0:n], in_=PS[:, b, 0:n]),
                          R=[("ps", b)], W=[f"Aug{sl}"])
            if isctx:
                em.dma("pool", C.UGc.rearrange("c p n -> p c n"), ugst[sl][:, :, 0:LC], R=[f"Aug{sl}"], W=["UGc"])
            else:
                em.dma("pool", C.UG[:, :, j * 512:(j + 1) * 512].rearrange("c p n -> p c n"), ugst[sl][:],
                       R=[f"Aug{sl}"], W=[("UG", j)])

        def fA(j):
            sl = j % 2
            if j == 16:
                for fc in range(4):
                    b = 5 + fc % 3
                    for k in range(8):
                        em.op("pe", lambda k=k, b=b, fc=fc, sl=sl: T.matmul(
                            PS[:, b, 0:LC], lhsT=win[:, k, 1024 + fc * 128:1024 + (fc + 1) * 128], rhs=hT[sl][:, k, 0:LC],
                            start=(k == 0), stop=(k == 7)), R=["win", f"AhT{sl}"], W=[("ps", b)], inc=(k == 7))
                    em.op("dve", lambda b=b, fc=fc: V.tensor_copy(out=fTc[:, fc, :], in_=PS[:, b, 0:LC]),
                          R=[("ps", b)], W=["fTc"])
            else:
                for s in range(4):
                    b = 5 + s % 3
                    for k in range(8):
                        em.op("pe", lambda k=k, b=b, s=s, sl=sl: T.matmul(
                            PS[:, b, :], lhsT=hT[sl][:, k, s * 128:(s + 1) * 128], rhs=win[:, k, 1024:1536],
                            start=(k == 0), stop=(k == 7)), R=["win", f"AhT{sl}"], W=[("ps", b)], inc=(k == 7))
                    evc[0] += 1
                    if evc[0] % 2:
                        em.op("dve", lambda b=b, s=s, j=j: V.tensor_copy(out=F[:, 4 * j + s, :], in_=PS[:, b, :]),
                              R=[("ps", b)], W=["F"])
                    else:
                        em.op("act", lambda b=b, s=s, j=j: A.copy(out=F[:, 4 * j + s, :], in_=PS[:, b, :]),
                              R=[("ps", b)], W=["F"])

        loadA(0)
        loadA(1)
        pro_a(0)
        pro_b(0)
        for j in range(17):
            ugA(j)
            cv_emit(C, 1)
            if j + 2 < 17:
                loadA(j + 2)
            if j + 1 < 17:
                pro_a(j + 1)
            fA(j)
            if j + 1 < 17:
                pro_b(j + 1)


def phase_C(C):
    nc, em, PS, F, fTc = C.nc, C.em, C.PS, C.F, C.fTc
    V, A, G, T = nc.vector, nc.scalar, nc.gpsimd, nc.tensor
    with ExitStack() as ps_:
        sb = lambda n, s, d=F32: ps_.enter_context(nc.sbuf_tensor(n, list(s), d))
        T1 = sb("T1s", [128, 256], BF16); M2 = sb("M2s", [64, 128, 192], BF16); CSf = sb("CSfs", [128, 256], BF16)
        CSc = sb("CScs", [128, 256], BF16); T256 = sb("T256s", [128, 2, 512], BF16)
        Ast = sb("Ast", [64, 64, 256], BF16); Y = sb("Ybuf", [128, 2, S], BF16)
        fst = [sb(f"fst{i}", [128, 2048], BF16) for i in range(2)]
        Gc = sb("Gcb", [128, 2, 256], BF16); fcs = sb("fcs", [128, LC], BF16)
        for dst, src, k in ((T1, C.T1_d, "T1"), (M2, C.M2_d, "M2"), (CSf, C.CSf_d, "CSf"), (CSc, C.CSc_d, "CSc"),
                            (T256, C.T256_d, "T256")):
            em.dma("sp", dst[:], src, W=[k])
        ev = 0
        for cc in range(4):
            for hc in range(2):
                for g4 in range(16):
                    b = 2 * (g4 % 2)
                    for q in range(4):
                        ch = cc * 128 + hc * 64 + g4 * 4 + q
                        em.op("pe", lambda b=b, q=q, ch=ch: T.matmul(
                            PS[0:64, b + q // 2, (q % 2) * 256:(q % 2) * 256 + 256], lhsT=F[:, :, ch], rhs=T1[:, :],
                            start=True, stop=True), R=["F", "T1"], W=[("ps", b), ("ps", b + 1)], inc=(q == 3))
                    ev += 1
                    dst = Ast[:, g4 * 4:(g4 + 1) * 4, :].rearrange("p a b -> p (a b)")
                    src = PS[0:64, b:b + 2, :].rearrange("p a b -> p (a b)")
                    if ev % 2:
                        em.op("dve", lambda dst=dst, src=src: V.tensor_copy(out=dst, in_=src),
                              R=[("ps", b), ("ps", b + 1)], W=["Ast"])
                    else:
                        em.op("act", lambda dst=dst, src=src: A.copy(out=dst, in_=src),
                              R=[("ps", b), ("ps", b + 1)], W=["Ast"])
                for kb in range(32):
                    b = 4 + kb % 2
                    for q in range(4):
                        k1 = kb * 4 + q
                        o = PS[hc * 64:(hc + 1) * 64, b, q * 128:(q + 1) * 128]
                        em.op("pe", lambda o=o, k1=k1: T.matmul(o, lhsT=Ast[:, :, k1], rhs=M2[:, k1, 64:192],
                                                                start=True, stop=False),
                              R=["Ast", "M2"], W=[("ps", b)], inc=False)
                        em.op("pe", lambda o=o, k1=k1: T.matmul(o, lhsT=Ast[:, :, 128 + k1], rhs=M2[:, k1, 0:128],
                                                                start=False, stop=True),
                              R=["Ast", "M2"], W=[("ps", b)], inc=(q == 3))
                    ev += 1
                    src = PS[hc * 64:(hc + 1) * 64, b, :].rearrange("p (k r c) -> p r c k", k=4, r=2)
                    dst = Y[hc * 64:(hc + 1) * 64, :, :].rearrange("p r (c k) -> p r c k", k=128)[:, :, :, kb * 4:(kb + 1) * 4]
                    if ev % 2:
                        em.op("dve", lambda dst=dst, src=src: V.tensor_copy(out=dst, in_=src), R=[("ps", b)], W=["Y"])
                    else:
                        em.op("act", lambda dst=dst, src=src: A.copy(out=dst, in_=src), R=[("ps", b)], W=["Y"])
            for tl in range(16):
                b = 6 + tl % 2
                em.op("pe", lambda b=b, tl=tl: T.matmul(PS[:, b, :], lhsT=CSf[:, 0:128], rhs=Y[:, 0, tl * 512:(tl + 1) * 512],
                                                        start=True, stop=False), R=["CSf", "Y"], W=[("ps", b)], inc=False)
                em.op("pe", lambda b=b, tl=tl: T.matmul(PS[:, b, :], lhsT=CSf[:, 128:256], rhs=Y[:, 1, tl * 512:(tl + 1) * 512],
                                                        start=False, stop=True), R=["CSf", "Y"], W=[("ps", b)], inc=True)
                fs = (tl // 4) % 2
                em.op("act", lambda b=b, tl=tl, fs=fs: A.copy(out=fst[fs][:, (tl % 4) * 512:(tl % 4 + 1) * 512], in_=PS[:, b, :]),
                      R=[("ps", b)], W=[f"fst{fs}"])
                if tl % 4 == 3:
                    t0 = (tl // 4) * 2048
                    em.dma("pool", C.MIXT[512 + cc * 128:512 + (cc + 1) * 128, t0:t0 + 2048], fst[fs][:],
                           R=[f"fst{fs}"], W=[("MIXT", 4 + cc)])
            for tc in range(2):
                em.op("pe", lambda tc=tc, cc=cc: T.matmul(PS[:, 0, 0:256], lhsT=fTc[:, cc, tc * 128:(tc + 1) * 128], rhs=CSc[:, :],
                                                          start=True, stop=True), R=["fTc", "CSc"], W=[("ps", 0)])
                em.op("dve", lambda tc=tc: V.tensor_copy(out=Gc[:, tc, :], in_=PS[:, 0, 0:256]), R=[("ps", 0)], W=["Gc"])
            for i_, (tc, part) in enumerate([(0, 0), (0, 1), (1, 0), (1, 1)]):
                em.op("pe", lambda i_=i_, tc=tc, part=part: T.matmul(
                    PS[:, 1, 0:256], lhsT=Gc[:, tc, part * 128:(part + 1) * 128], rhs=T256[:, tc, part * 256:(part + 1) * 256],
                    start=(i_ == 0), stop=(i_ == 3)), R=["Gc", "T256"], W=[("ps", 1)], inc=(i_ == 3))
            em.op("dve", lambda: V.tensor_copy(out=fcs[:, :], in_=PS[:, 1, 0:256]), R=[("ps", 1)], W=["fcs"])
            em.dma("pool", C.MIXT[512 + cc * 128:512 + (cc + 1) * 128, S:NT], fcs[:], R=["fcs"], W=[("MIXTc", 4 + cc)])
    C.es_mix.close()


PHASES += [("A", phase_A), ("C", phase_C)]


def phase_B(C):
    nc, em, PS = C.nc, C.em, C.PS
    V, A, G, T = nc.vector, nc.scalar, nc.gpsimd, nc.tensor
    TW = 2048
    with ExitStack() as ps_:
        sb = lambda n, s, d=F32: ps_.enter_context(nc.sbuf_tensor(n, list(s), d))
        bufA = sb("bufA", [128, NT]); bufB = sb("bufB", [128, 8460]); uc = sb("ucb_", [128, NT]); ucb = sb("ucbb", [128, NT], BF16)
        rt = sb("rt", [128, TW])
        it_ = [sb(f"it{i}", [128, TW]) for i in range(2)]; at = [sb(f"at{i}", [128, TW]) for i in range(2)]
        st = [sb(f"st{i}", [128, TW]) for i in range(2)]
        gtmp = [sb(f"gtmp{i}", [128, 1024]) for i in range(2)]
        wbd = sb("wbds", [128, 16, 128], BF16)
        cdg = sb("cdg", [128, 16, 128])
        lco = sb("lco", [128, 4, 2, 3]); cvc = sb("cvc", [128, 4, 5]); cA = sb("cA", [128, 4, 2]); carry = sb("carry", [128, 2])
        em.dma("pool", wbd[:], C.wbd_d.rearrange("g d c p n -> p (g d c) n"), W=["wbd"])
        em.dma("sp", cdg[:], C.cdiag_d.rearrange("c k p n -> p (c k) n"), W=["cdg"])
        em.dma("sp", lco[:], C.lcols_d, W=["lco"])
        em.dma("sp", cvc[:], C.convc_d, W=["cvc"])
        em.op("act", lambda: A.activation(out=cA[:], in_=lco[:, :, :, 2], func=AF.Exp, scale=-1.0), R=["lco"], W=["cA"])
        em.op("act", lambda: A.activation(out=cA[:], in_=cA[:], func=AF.Ln, bias=1.0), R=["cA"], W=["cA"])
        em.op("dve", lambda: V.tensor_scalar(out=cA[:], in0=cA[:], scalar1=-8.0, scalar2=None, op0=ALU.mult), R=["cA"], W=["cA"])
        em.op("dve", lambda: V.memset(bufB[:], 0.0), W=["bufB"])
        XO = 8200
        tiles = [(S, NT)] + [(i * TW, (i + 1) * TW) for i in range(4)]
        tix = 0
        for c in range(4):
            cv_emit(C, 2)
            em.dma("sp", bufA[:, 0:S], C.UG[c], W=["bufA"])
            em.dma("sp", bufA[:, S:NT], C.UGc[c], W=["bufA"])
            for qd in range(4):
                o_ = bufB[:, 2 + qd * 2048:2 + (qd + 1) * 2048].rearrange("p (a b) -> p a b", b=64)
                i_ = bufA[:, 0:S].rearrange("p (b a) -> p a b", a=128)[:, qd * 32:(qd + 1) * 32, :]
                eng = QENG[qd]
                fn = {"act": (lambda o_=o_, i_=i_: A.copy(out=o_, in_=i_)),
                      "dve": (lambda o_=o_, i_=i_: V.tensor_copy(out=o_, in_=i_)),
                      "pool": (lambda o_=o_, i_=i_: G.tensor_copy(out=o_, in_=i_))}[eng]
                em.op(eng, fn, R=["bufA"], W=[f"bufBq{qd}"])
            em.op("pool", lambda: G.tensor_copy(out=bufB[:, XO + 2:XO + 2 + LC], in_=bufA[:, S:NT]), R=["bufA"], W=["bufBc"])
            em.dma("sp", bufA[:, 0:S], C.UG[4 + c], W=["bufA"])
            em.dma("sp", bufA[:, S:NT], C.UGc[4 + c], W=["bufA"])
            for (o0, src0, n) in ((0, 0, S), (S, XO, LC)):
                for q0 in range(0, n, 2048):
                    nn = min(2048, n - q0)
                    nq = (nn + 511) // 512
                    pb = 4 * ((q0 // 2048) % 2)
                    qd = q0 // 2048
                    rk = ["bufBc", "bufB"] if n == LC else [f"bufBq{x}" for x in (qd - 1, qd, qd + 1) if 0 <= x < 4] + ["bufB"]
                    for q in range(nq):
                        w_ = min(512, nn - q * 512)
                        for k in range(4):
                            em.op("pe", lambda q=q, k=k, w_=w_, pb=pb, src0=src0, q0=q0, c=c: T.matmul(
                                PS[:, pb + q, 0:w_], lhsT=cdg[:, c * 4 + k, :],
                                rhs=bufB[:, src0 + q0 + q * 512 + k:src0 + q0 + q * 512 + k + w_],
                                start=(k == 0), stop=(k == 3)), R=["cdg"] + rk, W=[("ps", pb + q)], inc=(k == 3))
                    src = PS[:, pb:pb + 4, :].rearrange("p a b -> p (a b)")[:, 0:nn]
                    em.op("act", lambda src=src, o0=o0, q0=q0, nn=nn, c=c: A.activation(
                        out=uc[:, o0 + q0:o0 + q0 + nn], in_=src, func=AF.Identity, bias=cvc[:, c, 4:5]),
                          R=[("ps", pb + q) for q in range(4)] + ["cvc"], W=["uc"])
                    em.op("act", lambda src=src, o0=o0, q0=q0, nn=nn, c=c: A.activation(
                        out=ucb[:, o0 + q0:o0 + q0 + nn], in_=src, func=AF.Identity, bias=cvc[:, c, 4:5]),
                          R=[("ps", pb + q) for q in range(4)] + ["cvc"], W=["ucb"])

            def gelu_s1(lo, hi, p):
                n = hi - lo
                em.op("act", lambda: A.activation(out=gtmp[p][:, 0:n], in_=bufA[:, lo:hi], func=AF.Square),
                      R=["bufA"], W=[f"gt{p}"])
                em.op("pool", lambda: G.tensor_scalar(out=gtmp[p][:, 0:n], in0=gtmp[p][:, 0:n], scalar1=0.044715, scalar2=1.0,
                                                      op0=ALU.mult, op1=ALU.add), R=[f"gt{p}"], W=[f"gt{p}"])
                em.op("dve", lambda: V.tensor_tensor(out=gtmp[p][:, 0:n], in0=gtmp[p][:, 0:n], in1=bufA[:, lo:hi], op=ALU.mult),
                      R=[f"gt{p}", "bufA"], W=[f"gt{p}"])

            def gelu_s2(lo, hi, p):
                n = hi - lo
                em.op("act", lambda: A.activation(out=gtmp[p][:, 0:n], in_=gtmp[p][:, 0:n], func=AF.Sigmoid,
                                                  scale=1.5957691216057308), R=[f"gt{p}"], W=[f"gt{p}"])
                em.op("dve", lambda: V.tensor_tensor(out=bufA[:, lo:hi], in0=gtmp[p][:, 0:n], in1=bufA[:, lo:hi], op=ALU.mult),
                      R=[f"gt{p}", "bufA"], W=["bufA"])

            gpieces = [(S, NT)] + [(i * 1024, (i + 1) * 1024) for i in range(8)]
            gstate = {"s1": 0, "s2": 0}

            def gelu_push1():
                k = gstate["s1"]
                if k < len(gpieces):
                    gelu_s1(gpieces[k][0], gpieces[k][1], k % 2)
                    gstate["s1"] += 1

            def gelu_push2():
                k = gstate["s2"]
                if k < gstate["s1"]:
                    gelu_s2(gpieces[k][0], gpieces[k][1], k % 2)
                    gstate["s2"] += 1

            for d in range(2):
                order = [tiles[0]] + (tiles[1:] if d == 0 else tiles[1:][::-1])
                for ti, (lo, hi) in enumerate(order):
                    p = tix % 2
                    tix += 1
                    n = hi - lo
                    nq = (n + 511) // 512
                    for gi in range(2):
                        for q in range(nq):
                            w_ = min(512, n - q * 512)
                            em.op("pe", lambda gi=gi, q=q, w_=w_, lo=lo, d=d, c=c: T.matmul(
                                PS[:, gi * 4 + q, 0:w_], lhsT=wbd[:, gi * 8 + d * 4 + c, :], rhs=ucb[:, lo + q * 512:lo + q * 512 + w_],
                                start=True, stop=True), R=["wbd", "ucb"], W=[("ps", gi * 4 + q)], inc=(q == nq - 1))
                    pr = PS[:, 0:4, :].rearrange("p a b -> p (a b)")[:, 0:n]
                    pi = PS[:, 4:8, :].rearrange("p a b -> p (a b)")[:, 0:n]
                    if d == 0:
                        gelu_push2()
                        gelu_push2()
                    em.op("act", lambda pr=pr, n=n, c=c, d=d: A.activation(out=rt[:, 0:n], in_=pr, func=AF.Sigmoid,
                                                                           bias=lco[:, c, d, 0:1]),
                          R=[("ps", q) for q in range(4)] + ["lco"], W=["rt"])
                    em.op("act", lambda pi=pi, n=n, c=c, d=d, p=p: A.activation(out=it_[p][:, 0:n], in_=pi, func=AF.Sigmoid,
                                                                                bias=lco[:, c, d, 1:2]),
                          R=[("ps", 4 + q) for q in range(4)] + ["lco"], W=[f"it{p}"])
                    em.op("act", lambda n=n, c=c, d=d, p=p: A.activation(out=at[p][:, 0:n], in_=rt[:, 0:n], func=AF.Exp,
                                                                         scale=cA[:, c, d:d + 1]), R=["rt", "cA"], W=[f"at{p}"])
                    em.op("dve", lambda n=n, p=p: V.tensor_tensor(out=st[p][:, 0:n], in0=at[p][:, 0:n], in1=at[p][:, 0:n], op=ALU.mult),
                          R=[f"at{p}"], W=[f"st{p}"])
                    if d == 0:
                        gelu_push1()
                        gelu_push1()
                    em.op("act", lambda n=n, p=p: A.activation(out=st[p][:, 0:n], in_=st[p][:, 0:n], func=AF.Sqrt, scale=-1.0, bias=1.0),
                          R=[f"st{p}"], W=[f"st{p}"])
                    em.op("dve", lambda n=n, p=p: V.tensor_tensor(out=it_[p][:, 0:n], in0=it_[p][:, 0:n], in1=st[p][:, 0:n], op=ALU.mult),
                          R=[f"it{p}", f"st{p}"], W=[f"it{p}"])
                    em.op("dve", lambda n=n, lo=lo, hi=hi, p=p: V.tensor_tensor(out=it_[p][:, 0:n], in0=it_[p][:, 0:n], in1=uc[:, lo:hi],
                                                                               op=ALU.mult), R=[f"it{p}", "uc"], W=[f"it{p}"])
                    init = 0.0 if ti == 0 else carry[:, d:d + 1]
                    if d == 0:
                        em.op("dve", lambda n=n, lo=lo, hi=hi, init=init, p=p: V.tensor_tensor_scan(
                            out=bufB[:, lo:hi], data0=at[p][:, 0:n], data1=it_[p][:, 0:n], initial=init, op0=ALU.mult, op1=ALU.add),
                              R=[f"at{p}", f"it{p}", "carry", "bufB", "uc"], W=["bufB", "bufBq0", "bufBq1", "bufBq2", "bufBq3", "bufBc"])
                        em.op("dve", lambda hi=hi: V.tensor_copy(out=carry[:, 0:1], in_=bufB[:, hi - 1:hi]), R=["bufB"], W=["carry"])
                    else:
                        em.op("dve", lambda n=n, init=init, p=p: V.tensor_tensor_scan(
                            out=st[p][:, 0:n][:, ::-1], data0=at[p][:, 0:n][:, ::-1],
                            data1=it_[p][:, 0:n][:, ::-1], initial=init, op0=ALU.mult, op1=ALU.add),
                              R=[f"at{p}", f"it{p}", "carry", f"st{p}"], W=[f"st{p}"])
                        em.op("dve", lambda p=p: V.tensor_copy(out=carry[:, 1:2], in_=st[p][:, 0:1]), R=[f"st{p}"], W=["carry"])
                        em.op("dve", lambda n=n, lo=lo, hi=hi, p=p: V.tensor_tensor(out=bufB[:, lo:hi], in0=bufB[:, lo:hi], in1=st[p][:, 0:n],
                                                                                    op=ALU.add), R=["bufB", f"st{p}"], W=["bufB"])
            while gstate["s2"] < len(gpieces):
                gelu_push1()
                gelu_push2()
            em.op("dve", lambda: V.tensor_tensor(out=ucb[:, 0:S].rearrange("p (a b) -> p a b", b=64),
                                                 in0=bufB[:, 0:S].rearrange("p (a b) -> p a b", b=64),
                                                 in1=bufA[:, 0:S].rearrange("p (b a) -> p a b", a=128), op=ALU.mult),
                  R=["bufB", "bufBq0", "bufBq1", "bufBq2", "bufBq3", "bufBc", "bufA", "ucb"], W=["ucb"])
            em.op("dve", lambda: V.tensor_tensor(out=ucb[:, S:NT], in0=bufB[:, S:NT], in1=bufA[:, S:NT], op=ALU.mult),
                  R=["bufB", "bufBq0", "bufBq1", "bufBq2", "bufBq3", "bufBc", "bufA", "ucb"], W=["ucb"])
            em.dma("pool", C.MIXT[c * 128:(c + 1) * 128, :], ucb[:, :], R=["ucb"], W=[("MIXT", c)])
            if c < 3:
                em.op("dve", lambda: V.memset(bufB[:, 0:2], 0.0), R=["bufB"], W=["bufB", "bufBq0"])
                em.op("dve", lambda: V.memset(bufB[:, S:8460], 0.0), R=["bufB"], W=["bufB", "bufBq3", "bufBc"])


def _tok_tiles(ntok_tile):
    return None


def phase_D1(C, layer=0):
    nc, em, PS = C.nc, C.em, C.PS
    V, A, G, T = nc.vector, nc.scalar, nc.gpsimd, nc.tensor
    with ExitStack() as ps_:
        sb = lambda n, s, d=F32: ps_.enter_context(nc.sbuf_tensor(n, list(s), d))
        wo = sb("D1wo", [128, 8, D], BF16)
        xt = [sb(f"D1xt{i}", [128, 4, D]) for i in range(2)]
        mt = [sb(f"D1mt{i}", [128, 8, 512], BF16) for i in range(2)]
        mb = [sb(f"D1mb{i}", [128, 8, 512], BF16) for i in range(2)]
        hs = sb("D1hs", [128, 2])
        Gx = sb("D1Gx", [128, D]); Gc = sb("D1Gc", [128, D])
        ssq = sb("D1ssq", [128, 4]); rs = sb("D1rs", [128, 4]); junk = sb("D1junk", [128, D], BF16); tt = sb("D1tt", [128, D])
        _wload_bf(C, wo, C.WBF["woab"].rearrange("(k p) n -> p k n", p=128), "D1wo", D, "cvwoab")
        em.dma("sp", hs[:], C.hsel_d, W=["D1hs"])
        em.dma("sp", Gx[:], C.GROW[0, 0, 0, :].partition_broadcast(128), R=[("GROW", 0, 0)], W=["D1Gx"])
        em.dma("sp", Gc[:], C.GROW[0, 0, 1, :].partition_broadcast(128), R=[("GROW", 0, 0)], W=["D1Gc"])
        mixv = C.MIXT.rearrange("(k p) t -> p k t", p=128)
        NTL = NLT + 1

        def load(j, sl):
            if j < NLT:
                em.dma("sp", xt[sl][:], C.xloc_d[j * 512:(j + 1) * 512, :].rearrange("(s p) d -> p s d", p=128), W=[f"D1xt{sl}"])
                em.dma("sp", mt[sl][:], mixv[:, :, j * 512:(j + 1) * 512], W=[f"D1mt{sl}"])
                lb = 4096 + j * 512 if j < 8 else LBASE
                em.dma("sp", mb[sl][:], mixv[:, :, lb:lb + 512], W=[f"D1mb{sl}"])
                em.op("act", lambda sl=sl: A.activation(out=mt[sl][:].rearrange("p a b -> p (a b)"),
                                                        in_=mt[sl][:].rearrange("p a b -> p (a b)"), func=AF.Copy,
                                                        scale=hs[:, 0:1]), R=[f"D1mt{sl}", "D1hs"], W=[f"D1mt{sl}"])
                em.op("dve", lambda sl=sl: V.scalar_tensor_tensor(
                    out=mt[sl][:].rearrange("p a b -> p (a b)"), in0=mb[sl][:].rearrange("p a b -> p (a b)"), scalar=hs[:, 1:2],
                    in1=mt[sl][:].rearrange("p a b -> p (a b)"), op0=ALU.mult, op1=ALU.add),
                      R=[f"D1mt{sl}", f"D1mb{sl}", "D1hs"], W=[f"D1mt{sl}"])
            else:
                em.dma("sp", xt[sl][:, 0:2, :], C.ctx_d.rearrange("(s p) d -> p s d", p=128), W=[f"D1xt{sl}"])
                em.dma("sp", mt[sl][:, :, 0:LC], mixv[:, :, S:NT], W=[f"D1mt{sl}"])
        load(0, 0)
        for j in range(NTL):
            sl = j % 2
            if j + 1 < NTL:
                load(j + 1, 1 - sl)
            isx = j < NLT
            nsub = 4 if isx else 2
            for s in range(nsub):
                pb = 2 * (s % 4)
                for h in range(2):
                    for k in range(8):
                        em.op("pe", lambda k=k, h=h, s=s, sl=sl, pb=pb: T.matmul(
                            PS[:, pb + h, :], lhsT=mt[sl][:, k, s * 128:(s + 1) * 128], rhs=wo[:, k, h * 512:(h + 1) * 512],
                            start=(k == 0), stop=(k == 7)), R=["D1wo", f"D1mt{sl}"], W=[("ps", pb + h)], inc=(k == 7))
                C.epilogue(pb, xt[sl][:, s, :], f"D1xt{sl}", (Gx if isx else Gc)[:, :], "D1Gx" if isx else "D1Gc",
                           ssq, rs, junk, tt, "D1")
            if isx:
                em.dma("pool", C.X1[j * 512:(j + 1) * 512, :].rearrange("(s p) d -> p s d", p=128), xt[sl][:],
                       R=[f"D1xt{sl}"], W=[("X1", j)])
            else:
                em.dma("pool", C.C1.rearrange("(s p) d -> p s d", p=128), xt[sl][:, 0:2, :], R=[f"D1xt{sl}"], W=["C1"])


def phase_FFN(C, layer, Xin, Cin, Xout, Cout, inkey, outkey, ntok=NL):
    nc, em, PS = C.nc, C.em, C.PS
    V, A, G, T = nc.vector, nc.scalar, nc.gpsimd, nc.tensor
    tg = f"F{layer}"
    with ExitStack() as ps_:
        sb = lambda n, s, d=F32: ps_.enter_context(nc.sbuf_tensor(tg + n, list(s), d))
        wg = sb("wg", [128, 8, FH], BF16); wu = sb("wu", [128, 8, FH], BF16); wd = sb("wd", [128, NJ, D], BF16)
        xt = sb("xt", [128, 4, D]); hT = sb("hT", [128, 8, 512], BF16); hh = sb("hh", [128, NJ, 512], BF16)
        est = [sb(f"est{i}", [128, D]) for i in range(2)]
        Gx = sb("Gx", [128, D]); tt = sb("tt", [128, D]); junk = sb("junk", [128, D], BF16)
        ssq = sb("ssq", [128, 4]); rs = sb("rs", [128, 4]); ssq2 = sb("ssq2", [128, 4]); rs2 = sb("rs2", [128, 4])
        _wload_bf(C, wg, C.WBF["wg"][layer].rearrange("(k p) n -> p k n", p=128), tg + "wg", FH, f"cvwg{layer}", 1408)
        _wload_bf(C, wu, C.WBF["wu"][layer].rearrange("(k p) n -> p k n", p=128), tg + "wu", FH, f"cvwu{layer}", 1408)
        _wload_bf(C, wd, C.WBF["wd"][layer].rearrange("(k p) n -> p k n", p=128), tg + "wd", D, f"cvwd{layer}", 512)
        em.dma("sp", Gx[:], C.GROW[layer, 1, 0, :].partition_broadcast(128), W=[tg + "Gx"])
        NX = ntok // 512
        ntile = NX + (1 if Cin is not None else 0)

        def nsub_of(j):
            return 4 if j < NX else 2

        def rows(j, s):
            if j < NX:
                r0 = j * 512 + s * 128
                return Xin[r0:r0 + 128, :], Xout[r0:r0 + 128, :]
            return Cin[s * 128:(s + 1) * 128, :], Cout[s * 128:(s + 1) * 128, :]

        def load(j):
            ns = nsub_of(j)
            src = Xin[j * 512:(j + 1) * 512, :] if j < NX else Cin
            em.dma("sp", xt[:, 0:ns, :], src.rearrange("(s p) d -> p s d", p=128), W=[tg + "xt"])

        def pro_a(j):
            ns = nsub_of(j)
            C.prologue_a(xt, tg + "xt", ns, xt, tg + "xt", ssq2, rs2, junk, tg + "p")

        def pro_b(j):
            path = 0 if j < NX else 1
            C.prologue_b(nsub_of(j), xt, tg + "xt", C.MODC[:, layer, 1, 0, path, :], C.MODC[:, layer, 1, 1, path, :],
                         hT, tg + "hT", [0, 1, 2, 3])

        def gateup(j):
            n = nsub_of(j) * 128
            for jj in range(NJ):
                b = 2 * (jj % 2)
                for gi, w_ in enumerate((wg, wu)):
                    for k in range(8):
                        em.op("pe", lambda k=k, b=b, gi=gi, w_=w_, jj=jj: T.matmul(
                            PS[:, b + gi, 0:n], lhsT=w_[:, k, jj * 128:(jj + 1) * 128], rhs=hT[:, k, 0:n],
                            start=(k == 0), stop=(k == 7)), R=[tg + "wg", tg + "wu", tg + "hT"], W=[("ps", b + gi)],
                              inc=(k == 7))
                em.op("act", lambda b=b, jj=jj: A.activation(out=hh[:, jj, 0:n], in_=PS[:, b, 0:n], func=AF.Silu),
                      R=[("ps", b)], W=[tg + f"hh{jj}"])
                em.op("dve", lambda b=b, jj=jj: V.tensor_tensor(out=hh[:, jj, 0:n], in0=hh[:, jj, 0:n], in1=PS[:, b + 1, 0:n],
                                                               op=ALU.mult), R=[("ps", b + 1), tg + f"hh{jj}"], W=[tg + f"hh{jj}"])

        def down(j, s):
            pb = 4 + 2 * (s % 2)
            for h in range(2):
                for jj in range(NJ):
                    em.op("pe", lambda jj=jj, h=h, s=s, pb=pb: T.matmul(
                        PS[:, pb + h, :], lhsT=hh[:, jj, s * 128:(s + 1) * 128], rhs=wd[:, jj, h * 512:(h + 1) * 512],
                        start=(jj == 0), stop=(jj == NJ - 1)), R=[tg + "wd", tg + f"hh{jj}"], W=[("ps", pb + h)], inc=(jj == NJ - 1))

        def eload(j, s):
            em.dma("sp", est[s % 2][:], rows(j, s)[0], W=[tg + f"est{s % 2}"])

        def epi(j, s):
            pb = 4 + 2 * (s % 2)
            C.epilogue(pb, est[s % 2][:, :], tg + f"est{s % 2}", Gx[:, :], tg + "Gx", ssq, rs, junk, tt, tg)
            em.dma("pool", rows(j, s)[1], est[s % 2][:], R=[tg + f"est{s % 2}"], W=[(outkey, j, s)])

        load(0)
        pro_a(0)
        pro_b(0)
        for j in range(ntile):
            ns = nsub_of(j)
            if j == NX:
                em.dma("sp", Gx[:], C.GROW[layer, 1, 1, :].partition_broadcast(128), W=[tg + "Gx"])
            gateup(j)
            if layer == 0:
                cv_emit(C, 2)
            if j + 1 < ntile:
                load(j + 1)
                pro_a(j + 1)
            eload(j, 0)
            eload(j, 1)
            down(j, 0)
            down(j, 1)
            if j + 1 < ntile:
                pro_b(j + 1)
            epi(j, 0)
            epi(j, 1)
            if ns == 4:
                eload(j, 2)
                eload(j, 3)
                down(j, 2)
                down(j, 3)
                epi(j, 2)
                epi(j, 3)


PHASES += [("B", phase_B), ("D1", phase_D1),
           ("D2", lambda C: phase_FFN(C, 0, C.X1, C.C1, C.X2, C.C2, "X1", "X2"))]


def phase_E(C):
    nc, em, PS = C.nc, C.em, C.PS
    V, A, G, T = nc.vector, nc.scalar, nc.gpsimd, nc.tensor
    with ExitStack() as ps_:
        sb = lambda n, s, d=F32: ps_.enter_context(nc.sbuf_tensor("E" + n, list(s), d))
        wq = sb("wq", [128, 8, 3 * D], BF16)
        xt = [sb(f"xt{i}", [128, 4, D]) for i in range(2)]
        hT = [sb(f"hT{i}", [128, 8, 512], BF16) for i in range(2)]
        qk = [sb(f"qk{i}", [128, 16, 512], BF16) for i in range(2)]
        vst = [sb(f"vst{i}", [128, 4, D], BF16) for i in range(2)]
        ssq = sb("ssq", [128, 4]); rs = sb("rs", [128, 4]); junk = sb("junk", [128, D], BF16)
        _wload_bf(C, wq, C.WBF["wqkv"].rearrange("(k p) n -> p k n", p=128), "Ewq", 3 * D, "cvwqkv")
        QTv = C.QT.rearrange("(c p) t -> p c t", p=128); KTv = C.KT.rearrange("(c p) t -> p c t", p=128)
        NTL = NLT + 1
        evc = [0]

        def nsubE(j):
            return 2 if j == NLT else 4

        def load(j):
            sl = j % 2
            if j < NLT:
                em.dma("sp", xt[sl][:], C.X2[j * 512:(j + 1) * 512, :].rearrange("(s p) d -> p s d", p=128), W=[f"Ext{sl}"])
            else:
                em.dma("sp", xt[sl][:, 0:2, :], C.C2.rearrange("(s p) d -> p s d", p=128), W=[f"Ext{sl}"])

        def pro_a(j):
            sl = j % 2
            C.prologue_a(xt[sl], f"Ext{sl}", nsubE(j), xt[sl], f"Ext{sl}", ssq, rs, junk, "E")

        def pro_b(j):
            sl = j % 2
            path = 1 if j == NLT else 0
            C.prologue_b(nsubE(j), xt[sl], f"Ext{sl}", C.MODC[:, 1, 0, 0, path, :], C.MODC[:, 1, 0, 1, path, :],
                         hT[sl], f"EhT{sl}", [0, 1])

        def qkE(j):
            sl = j % 2
            isctx = j == NLT
            n = nsubE(j) * 128
            for oc in (range(8, 16) if isctx else range(16)):
                b = 2 + oc % 2
                for k in range(8):
                    em.op("pe", lambda k=k, b=b, oc=oc, n=n, sl=sl: T.matmul(
                        PS[:, b, 0:n], lhsT=wq[:, k, oc * 128:(oc + 1) * 128], rhs=hT[sl][:, k, 0:n],
                        start=(k == 0), stop=(k == 7)), R=["Ewq", f"EhT{sl}"], W=[("ps", b)], inc=(k == 7))
                if oc < 8:
                    em.op("act", lambda b=b, oc=oc, sl=sl, n=n: A.activation(out=qk[sl][:, oc, 0:n], in_=PS[:, b, 0:n],
                                                                             func=AF.Copy, scale=0.125),
                          R=[("ps", b)], W=[f"Eqk{sl}"])
                else:
                    em.op("dve", lambda b=b, oc=oc, sl=sl, n=n: V.tensor_copy(out=qk[sl][:, oc, 0:n], in_=PS[:, b, 0:n]),
                          R=[("ps", b)], W=[f"Eqk{sl}"])
            if not isctx:
                em.dma("pool", QTv[:, :, j * 512:(j + 1) * 512], qk[sl][:, 0:8, :], R=[f"Eqk{sl}"], W=[("QT", j)])
                em.dma("pool", KTv[:, :, j * 512:(j + 1) * 512], qk[sl][:, 8:16, :], R=[f"Eqk{sl}"], W=[("KT", j)])
            else:
                em.dma("pool", KTv[:, :, NL:NL + LC], qk[sl][:, 8:16, 0:LC], R=[f"Eqk{sl}"], W=[("KT", j)])

        def vE(j):
            sl = j % 2
            isctx = j == NLT
            nsub = nsubE(j)
            for s in range(nsub):
                pb = 4 + 2 * (s % 2)
                for h in range(2):
                    for k in range(8):
                        em.op("pe", lambda k=k, h=h, s=s, pb=pb, sl=sl: T.matmul(
                            PS[:, pb + h, :], lhsT=hT[sl][:, k, s * 128:(s + 1) * 128], rhs=wq[:, k, 2048 + h * 512:2048 + (h + 1) * 512],
                            start=(k == 0), stop=(k == 7)), R=["Ewq", f"EhT{sl}"], W=[("ps", pb + h)], inc=(k == 7))
                evc[0] += 1
                src = PS[:, pb:pb + 2, :].rearrange("p a b -> p (a b)")
                if evc[0] % 2:
                    em.op("dve", lambda s=s, sl=sl, src=src: V.tensor_copy(out=vst[sl][:, s, :], in_=src),
                          R=[("ps", pb), ("ps", pb + 1)], W=[f"Evst{sl}"])
                else:
                    em.op("act", lambda s=s, sl=sl, src=src: A.copy(out=vst[sl][:, s, :], in_=src),
                          R=[("ps", pb), ("ps", pb + 1)], W=[f"Evst{sl}"])
            t0 = NL if isctx else j * 512
            em.dma("pool", C.VD[t0:t0 + nsub * 128, :].rearrange("(s p) d -> p s d", p=128), vst[sl][:, 0:nsub, :],
                   R=[f"Evst{sl}"], W=[("VD", j)])

        load(0)
        load(1)
        pro_a(0)
        pro_b(0)
        for j in range(NTL):
            qkE(j)
            if j + 2 < NTL:
                load(j + 2)
            if j + 1 < NTL:
                pro_a(j + 1)
            vE(j)
            if j + 1 < NTL:
                pro_b(j + 1)


def phase_F(C):
    nc, em, PS = C.nc, C.em, C.PS
    V, A, G, T = nc.vector, nc.scalar, nc.gpsimd, nc.tensor
    with ExitStack() as ps_:
        sb = lambda n, s, d=F32: ps_.enter_context(nc.sbuf_tensor("AT" + n, list(s), d))
        wo = sb("wo", [128, 8, D], BF16)
        KTb = [sb(f"KTb{i}", [128, 8, 1024], BF16) for i in range(2)]
        Vb = [sb(f"Vb{i}", [128, 8, 16, 65], BF16) for i in range(2)]
        QTb = sb("QTb", [128, 8, 512], BF16); xt = sb("xt", [128, 4, D])
        KTc = sb("KTc", [128, 8, LC], BF16); Vc = sb("Vc", [128, 2, 16, 65], BF16)
        BTi = sb("BTi", [128, 16, 5, 128], BF16); BTe = sb("BTe", [128, 16, 8, 128], BF16)
        PT = [sb(f"PT{i}", [128, 1280], BF16) for i in range(2)]
        Ot = sb("Ot", [128, D]); OTt = sb("OTt", [128, 8, 128], BF16); rden = sb("rden", [128, 4])
        Gx = sb("Gx", [128, D]); ssq = sb("ssq", [128, 4]); rs = sb("rs", [128, 4]); junk = sb("junk", [128, D], BF16)
        tt = sb("tt", [128, D])
        _wload_bf(C, wo, C.WBF["wona"].rearrange("(k p) n -> p k n", p=128), "Awo", D, "cvwona")
        em.dma("sp", Gx[:], C.GROW[1, 0, 0, :].partition_broadcast(128), W=["AGx"])
        em.dma("sp", BTi[:], C.BT_d, W=["ABTi"])
        KTv = C.KT.rearrange("(c p) t -> p c t", p=128); QTv = C.QT.rearrange("(c p) t -> p c t", p=128)
        em.dma("sp", KTc[:], KTv[:, :, NL:NL + LC], W=["AKTc"])
        for i in range(2):
            em.op("pool", lambda i=i: G.memset(Vb[i][:, :, :, 64:65], 1.0), W=[f"AVb{i}"])
        em.op("pool", lambda: G.memset(Vc[:, :, :, 64:65], 1.0), W=["AVc"])
        for c in range(2):
            em.dma("sp", Vc[:, c, :, 0:64], C.VD[NL + c * 128:NL + (c + 1) * 128, :].rearrange("p (h d) -> p h d", d=64), W=["AVc"])

        NB = 8

        def kbof(blk):
            return 0 if blk == 0 else (52 if blk == NB - 1 else 8 * blk - 4)

        def load(blk, sl):
            if blk == 0:
                pieces = [(0, 2, 68 * 64), (2, 6, 0)]
            else:
                pieces = [(0, 8, kbof(blk) * 64)]
            for (c0, ncn, t0) in pieces:
                em.dma("sp", KTb[sl][:, :, c0 * 128:(c0 + ncn) * 128], KTv[:, :, t0:t0 + ncn * 128], W=[f"AKTb{sl}"])
                for c in range(ncn):
                    tt0 = t0 + c * 128
                    em.dma("sp", Vb[sl][:, c0 + c, :, 0:64], C.VD[tt0:tt0 + 128, :].rearrange("p (h d) -> p h d", d=64),
                           W=[f"AVb{sl}"])
        load(0, 0)
        for blk in range(NB):
            sl = blk % 2
            r0 = 8 * blk
            edge = blk in (0, NB - 1)
            eb = 0 if blk == 0 else 1
            nloc = 8 if edge else 5
            nch = nloc + 2
            em.dma("sp", QTb[:], QTv[:, :, r0 * 64:r0 * 64 + 512], W=["AQTb"])
            em.dma("sp", xt[:], C.X2[blk * 512:(blk + 1) * 512, :].rearrange("(s p) d -> p s d", p=128), W=["Axt"])
            if blk + 1 < NB:
                load(blk + 1, 1 - sl)

            def sbase(h):
                return 0 if edge else 2 * (h % 2)

            def cpos(cl, h):
                if edge:
                    return (cl // 4, (cl % 4) * 128)
                return (2 * (h % 2) + cl // 4, (cl % 4) * 128)

            def qk(i, h):
                off = 0 if edge else i
                if edge and h == 0:
                    em.dma("sp", BTe[:], C.BTE_d[eb, i], W=["ABTe"])
                j = h // 2; e = h % 2
                p0, p1 = 64 * e, 64 * e + 64
                for cl in range(nch):
                    bk, co = cpos(cl, h)
                    o = PS[:, bk, co:co + 128]
                    if cl < nloc:
                        lt = KTb[sl][p0:p1, j, (off + cl) * 128:(off + cl + 1) * 128]
                    else:
                        lt = KTc[p0:p1, j, (cl - nloc) * 128:(cl - nloc + 1) * 128]
                    last = cl == nch - 1
                    em.op("pe", lambda o=o, lt=lt, j=j, p0=p0, p1=p1, i=i, cl=cl, last=last: T.matmul(
                        o, lhsT=lt, rhs=QTb[p0:p1, j, i * 128:(i + 1) * 128], start=(cl % 4 == 0),
                        stop=(edge and last)), R=[f"AKTb{sl}", "AKTc", "AQTb"], W=[("ps", bk)], inc=(edge and last))
                    if edge:
                        if cl in (3, 7):
                            em.op("pe", lambda h=h, bk=bk, cl=cl: T.matmul(
                                PS[:, bk, :], lhsT=C.identb[:, :], rhs=BTe[:, h, cl - 3:cl + 1, :].rearrange("p a b -> p (a b)"),
                                start=False, stop=True), R=["identb", "ABTe"], W=[("ps", bk)], inc=False)
                    else:
                        if cl == 3:
                            em.op("pe", lambda h=h, bk=bk: T.matmul(
                                PS[:, bk, :], lhsT=C.identb[:, :], rhs=BTi[:, h, 0:4, :].rearrange("p a b -> p (a b)"),
                                start=False, stop=True), R=["identb", "ABTi"], W=[("ps", bk)], inc=False)
                        if cl == 6:
                            em.op("pe", lambda h=h, bk=bk: T.matmul(
                                PS[:, bk, 0:128], lhsT=C.identb[:, :], rhs=BTi[:, h, 4, :],
                                start=False, stop=True), R=["identb", "ABTi"], W=[("ps", bk)], inc=True)

            def ex(i, h):
                b0 = sbase(h)
                nb = 3 if edge else 2
                src = PS[:, b0:b0 + nb, :].rearrange("p a b -> p (a b)")[:, 0:nch * 128]
                em.op("act", lambda src=src, h=h: A.activation(out=PT[h % 2][:, 0:nch * 128], in_=src, func=AF.Exp),
                      R=[("ps", b0 + q) for q in range(nb)], W=[f"APT{h % 2}"])

            def pv(i, h):
                off = 0 if edge else i
                ob = 4 + (h // 4) % 2
                so = (h % 4) * 128
                for c in range(nch):
                    rhs = Vb[sl][:, off + c, h, :] if c < nloc else Vc[:, c - nloc, h, :]
                    em.op("pe", lambda c=c, rhs=rhs, h=h, ob=ob, so=so: T.matmul(
                        PS[:, ob, so:so + 65], lhsT=PT[h % 2][:, c * 128:(c + 1) * 128], rhs=rhs,
                        start=(c == 0), stop=(c == nch - 1)), R=[f"APT{h % 2}", f"AVb{sl}", "AVc"], W=[("ps", ob)],
                          inc=(c == nch - 1))
                if h % 4 == 3:
                    em.op("dve", lambda ob=ob: V.reciprocal(out=rden[:, 0:4], in_=PS[:, ob, 64:512:128]),
                          R=[("ps", ob)], W=["Arden"])
                    for hh in range(4):
                        hd = h - 3 + hh
                        em.op("dve", lambda ob=ob, hh=hh, hd=hd: V.tensor_scalar(
                            out=Ot[:, hd * 64:(hd + 1) * 64], in0=PS[:, ob, hh * 128:hh * 128 + 64],
                            scalar1=rden[:, hh:hh + 1], scalar2=None, op0=ALU.mult), R=[("ps", ob), "Arden"], W=["AOt"])

            def fin(i):
                for k in range(8):
                    em.op("pe", lambda k=k: T.transpose(PS[:, 6 + k // 4, (k % 4) * 128:(k % 4 + 1) * 128],
                                                        Ot[:, k * 128:(k + 1) * 128], C.ident[:]),
                          R=["AOt", "ident"], W=[("ps", 6 + k // 4)], inc=(k % 4 == 3))
                for hb in range(2):
                    em.op("act" if hb else "dve",
                          (lambda hb=hb: A.copy(out=OTt[:, 4 * hb:4 * hb + 4, :].rearrange("p a b -> p (a b)"), in_=PS[:, 6 + hb, :])) if hb else
                          (lambda hb=hb: V.tensor_copy(out=OTt[:, 4 * hb:4 * hb + 4, :].rearrange("p a b -> p (a b)"), in_=PS[:, 6 + hb, :])),
                          R=[("ps", 6 + hb)], W=["AOTt"])
                for hf in range(2):
                    for k in range(8):
                        em.op("pe", lambda k=k, hf=hf: T.matmul(PS[:, 6 + hf, :], lhsT=OTt[:, k, :], rhs=wo[:, k, hf * 512:(hf + 1) * 512],
                                                                start=(k == 0), stop=(k == 7)),
                              R=["AOTt", "Awo"], W=[("ps", 6 + hf)], inc=(k == 7))
                C.epilogue(6, xt[:, i, :], "Axt", Gx[:, :], "AGx", ssq, rs, junk, tt, "A")

            items = [(i, h) for i in range(4) for h in range(16)]
            qk(*items[0])
            for n_, (i, h) in enumerate(items):
                nxt = items[n_ + 1] if n_ + 1 < len(items) else None
                if edge:
                    ex(i, h)
                    if nxt:
                        qk(*nxt)
                else:
                    if nxt:
                        qk(*nxt)
                    ex(i, h)
                pv(i, h)
                if h == 15:
                    fin(i)
            em.dma("pool", C.X3[blk * 512:(blk + 1) * 512, :].rearrange("(s p) d -> p s d", p=128), xt[:], R=["Axt"], W=[("X3", blk)])


PHASES += [("E", phase_E), ("F", phase_F),
           ("G", lambda C: phase_FFN(C, 1, C.X3, None, C.out_d, None, "X3", "OUT", ntok=4096))]
```

```python
import math
from contextlib import ExitStack
import numpy as np
import ml_dtypes
import concourse.bass as bass
import concourse.mybir as mybir
from concourse.bass_utils import run_bass_kernel_spmd

F32 = mybir.dt.float32
BF16 = mybir.dt.bfloat16
AF = mybir.ActivationFunctionType
ALU = mybir.AluOpType
NPBF = ml_dtypes.bfloat16

D = 1024
S = 8192
LC = 256
NT = S + LC
FH = 2816
NJ = FH // 128
EPS = 1e-6
NCORES = 8
NL = 4608
NLT = NL // 512
LBASE = 3584
NEG = -30000.0
QENG = ("pool", "dve", "pool", "dve")


class Em:
    LIMIT = 30000
    NDS = 8

    def __init__(self, nc, es):
        self.nc = nc
        self.es = es
        self.eng = dict(pe=nc.tensor, act=nc.scalar, dve=nc.vector, pool=nc.gpsimd, sp=nc.sync)
        self.sem = {}
        self.semkey = {}
        self.cnt = {}
        self.nsem = 0
        for e in self.eng:
            self._newsem(e)
        self.waited = {e: {} for e in self.eng}
        self.lastw = {}
        self.readers = {}
        self.dsem = {}
        self.ndma = {}
        for q in ("sp", "pool", "act"):
            self.dsem[q] = [es.enter_context(nc.semaphore(f"d{q}{i}")) for i in range(self.NDS)]
            self.ndma[q] = 0
        self.bg = []

    def _newsem(self, e):
        self.nsem += 1
        self.sem[e] = self.es.enter_context(self.nc.semaphore(f"s{e}{self.nsem}"))
        self.semkey[e] = (e, self.nsem)
        self.cnt[e] = 0

    def _deps(self, engine, R, W):
        deps = {}
        for k in list(R) + list(W):
            t = self.lastw.get(k)
            if t is not None:
                if t[0] not in deps or deps[t[0]][2] < t[2]:
                    deps[t[0]] = t
        for k in W:
            for t in self.readers.get(k, {}).values():
                if t[0] not in deps or deps[t[0]][2] < t[2]:
                    deps[t[0]] = t
        e = self.eng[engine]
        for sk, t in deps.items():
            if engine == "pe" and t[3] == "pe":
                continue
            if self.waited[engine].get(sk, 0) >= t[2]:
                continue
            e.wait_ge(t[1], t[2])
            self.waited[engine][sk] = t[2]

    def _record(self, tok, R, W):
        for k in W:
            self.lastw[k] = tok
            self.readers[k] = {}
        for k in R:
            d = self.readers.setdefault(k, {})
            if tok[0] not in d or d[tok[0]][2] < tok[2]:
                d[tok[0]] = tok

    def op(self, engine, fn, R=(), W=(), inc=True):
        self._deps(engine, R, W)
        ins = fn()
        if inc:
            ins.then_inc(self.sem[engine], 1)
            self.cnt[engine] += 1
            tok = (self.semkey[engine], self.sem[engine], self.cnt[engine], engine)
            self._record(tok, R, W)
            if self.cnt[engine] >= self.LIMIT:
                self._newsem(engine)
        else:
            tok = (self.semkey[engine], self.sem[engine], self.cnt[engine] + 1, engine)
            self._record(tok, R, W)
        return ins

    def dma(self, q, out, in_, R=(), W=(), **kw):
        self._deps(q, R, W)
        i = self.ndma[q]
        self.ndma[q] += 1
        sem = self.dsem[q][i % self.NDS]
        rnd = i // self.NDS
        sk = ("dma", q, i % self.NDS)
        if rnd > 0 and self.waited[q].get(sk, 0) < 16 * rnd:
            self.eng[q].wait_ge(sem, 16 * rnd)
            self.waited[q][sk] = 16 * rnd
        self.eng[q].dma_start(out=out, in_=in_, **kw).then_inc(sem, 16)
        tok = (sk, sem, 16 * (rnd + 1), None)
        self._record(tok, R, W)

    def dma_bg(self, q, out, in_, R=(), W=(), **kw):
        self._deps(q, R, W)
        sem = self.es.enter_context(self.nc.semaphore(f"bg{len(self.bg)}"))
        self.bg.append(sem)
        self.eng[q].dma_start(out=out, in_=in_, **kw).then_inc(sem, 16)
        self._record((("bg", len(self.bg)), sem, 16, None), R, W)

    def finish(self):
        sp = self.eng["sp"]
        for sem in self.bg:
            sp.wait_ge(sem, 16)
        for q in self.dsem:
            n = self.ndma[q]
            for s in range(self.NDS):
                uses = (n - s + self.NDS - 1) // self.NDS if n > s else 0
                if uses > 0:
                    sp.wait_ge(self.dsem[q][s], 16 * uses)
        for e in ("pe", "act", "dve", "pool"):
            if self.cnt[e] > 0:
                sp.wait_ge(self.sem[e], self.cnt[e])


def _tables():
    t = {}
    t["ident"] = np.eye(128, dtype=np.float32)
    t["identb"] = np.eye(128, dtype=np.float32).astype(NPBF)
    a = np.arange(128, dtype=np.float64)
    ang = 2 * np.pi * np.outer(a, a) / 128.0
    t["T1"] = np.concatenate([np.cos(ang), -np.sin(ang)], axis=1).astype(NPBF)
    t2 = np.arange(64, dtype=np.float64)[:, None, None]
    k1 = np.arange(128, dtype=np.float64)[None, :, None]
    k2 = np.arange(64, dtype=np.float64)[None, None, :]
    ph = 2 * np.pi * (t2 * k2 / 64.0 + t2 * k1 / 8192.0)
    Mc, Ms = np.cos(ph), np.sin(ph)
    t["M2"] = np.concatenate([Ms, Mc, -Ms], axis=2).astype(NPBF)
    c = np.arange(64, dtype=np.float64)
    angc = 2 * np.pi * np.outer(c, c) / 64.0
    Cc, Sc = np.cos(angc), np.sin(angc)
    z = np.zeros((64, 64))
    Cbd = np.block([[Cc, z], [z, Cc]])
    Sbd = np.block([[Sc, z], [z, Sc]])
    sx = 1.0 / math.sqrt(8192.0 * 64.0)
    t["CSf"] = (np.concatenate([Cbd, Sbd], axis=1) * sx).astype(NPBF)
    sc_ = 1.0 / math.sqrt(256.0 * 64.0)
    t["CSc"] = (np.concatenate([Cbd, Sbd], axis=1) * sc_).astype(NPBF)
    p = np.arange(256, dtype=np.float64)
    angp = 2 * np.pi * np.outer(p, p) / 256.0
    T256 = np.concatenate([np.cos(angp), -np.sin(angp)], axis=1)
    t["T256"] = T256.reshape(2, 128, 512).transpose(1, 0, 2).copy().astype(NPBF)
    return t


_TABLES = None


def _bias_tables(rpb):
    H = 16
    out = np.full((5, 5 * 128, H, 128), NEG, dtype=np.float32)
    kr = np.arange(10)[:, None, None, None]
    kc = np.arange(64)[None, :, None, None]
    qr = np.arange(2)[None, None, :, None]
    qc = np.arange(64)[None, None, None, :]
    cs = np.clip(qc - 8, 0, 48)
    for vi, (gp, ks) in enumerate([(0, 0), (2, 0), (60, 56), (124, 118), (126, 118)]):
        gq = gp + qr
        rs = np.clip(gq - 4, 0, 120)
        gk = ks + kr
        valid = (gk >= rs) & (gk < rs + 8) & (kc >= cs) & (kc < cs + 16)
        dr = np.clip(gk - gq + 7, 0, 14)
        dc = np.clip(kc - qc + 15, 0, 30)
        valid, dr, dc = np.broadcast_arrays(valid, dr, dc)
        vals = rpb[:, dr, dc]
        vals = np.where(valid[None], vals, NEG)
        out[vi] = vals.transpose(1, 2, 0, 3, 4).reshape(640, H, 128)
    bt = out.reshape(5, 5, 128, H, 128).transpose(0, 2, 3, 1, 4)
    return np.ascontiguousarray(bt).astype(NPBF)


def _local_rows(h):
    if h == 0:
        return np.arange(72)
    return np.concatenate([np.arange(64, 128), np.arange(56, 64)])


def _edge_tables(rpb, h):
    H = 16
    lr = _local_rows(h)
    out = np.empty((2, 4, 128, H, 8, 128), dtype=NPBF)
    kc = np.arange(64)[None, :, None, None]
    qr = np.arange(2)[None, None, :, None]
    qc = np.arange(64)[None, None, None, :]
    cs = np.clip(qc - 8, 0, 48)
    keyrows = [np.concatenate([np.arange(68, 72), np.arange(0, 12)]), np.arange(52, 68)]
    for eb, r0 in enumerate([0, 56]):
        gk = lr[keyrows[eb]][:, None, None, None]
        for i in range(4):
            gq = lr[r0 + 2 * i + np.arange(2)][None, None, :, None]
            rs = np.clip(gq - 4, 0, 120)
            valid = (gk >= rs) & (gk < rs + 8) & (kc >= cs) & (kc < cs + 16)
            dr = np.clip(gk - gq + 7, 0, 14)
            dc = np.clip(kc - qc + 15, 0, 30)
            valid, dr, dc = np.broadcast_arrays(valid, dr, dc)
            vals = np.where(valid[None], rpb[:, dr, dc], NEG)
            t = vals.transpose(1, 2, 0, 3, 4).reshape(8, 128, H, 128)
            out[eb, i] = t.transpose(1, 2, 0, 3).astype(NPBF)
    return out


def _col(v, nchunk):
    return np.ascontiguousarray(np.asarray(v, np.float32).reshape(nchunk, 128).T)


def build(stop="all", debug=False):
    nc = bass.Bass("TRN2", target_bir_lowering=False)

    def din(name, shape, dt=F32):
        return nc.dram_tensor(name, list(shape), dt, kind="ExternalInput").ap()

    skind = "ExternalOutput" if debug else "Internal"

    def dscr(name, shape, dt):
        return nc.dram_tensor(name, list(shape), dt, kind=skind).ap()

    x_d = din("x", [S, D]); xloc_d = din("xloc", [NL, D]); hsel_d = din("hsel", [128, 2]); ctx_d = din("ctx", [LC, D]); ccols_d = din("ccols", [128, 16])
    wmod_d = din("w_mod", [2, D, 6 * D]); bmodc_d = din("bmodc", [128, 2, 48]); bmod_d = din("b_mod", [2, 6 * D])
    gcols_d = din("gcols", [128, 2, 2, 8]); gpm_d = din("g_post_mix", [2, D]); gpf_d = din("g_post_ffn", [2, D])
    win_d = din("w_in_ab", [D, 1536]); woab_d = din("w_out_ab", [D, D])
    wg_d = din("w_ffn_gate", [2, D, FH]); wu_d = din("w_ffn_up", [2, D, FH]); wd_d = din("w_ffn_down", [2, FH, D])
    wqkv_d = din("w_qkv_na", [D, 3 * D]); wona_d = din("w_out_na", [D, D])
    wbd_d = din("wbd", [2, 2, 4, 128, 128]); lcols_d = din("lcols", [128, 4, 2, 3]); convc_d = din("convc", [128, 4, 5]); cdiag_d = din("cdiag", [4, 4, 128, 128])
    ident_d = din("ident", [128, 128]); identb_d = din("identb", [128, 128], BF16)
    T1_d = din("T1", [128, 256], BF16); M2_d = din("M2", [64, 128, 192], BF16); CSf_d = din("CSf", [128, 256], BF16)
    CSc_d = din("CSc", [128, 256], BF16); T256_d = din("T256", [128, 2, 512], BF16)
    BT_d = din("BT", [128, 16, 5, 128], BF16); BTE_d = din("BTE", [2, 4, 128, 16, 8, 128], BF16)
    out_d = nc.dram_tensor("out", [4096, D], F32, kind="ExternalOutput").ap()

    UG = dscr("UG", [8, 128, S], F32)
    UGc = dscr("UGc", [8, 128, LC], F32)
    MIXT = dscr("MIXT", [D, NT], BF16)
    GROW = dscr("GROW", [2, 2, 2, D], F32)
    X1 = dscr("X1", [NL, D], F32); C1 = dscr("C1", [LC, D], F32)
    X2 = dscr("X2", [NL, D], F32); C2 = dscr("C2", [LC, D], F32)
    X3 = dscr("X3", [NL, D], F32)
    QT = dscr("QT", [D, NL], BF16); KT = dscr("KT", [D, NL + LC], BF16); VD = dscr("VD", [NL + LC, D], BF16)

    WBF = {"woab": dscr("woab_bf", [D, D], BF16), "wona": dscr("wona_bf", [D, D], BF16),
           "wqkv": dscr("wqkv_bf", [D, 3 * D], BF16),
           "wg": dscr("wg_bf", [2, D, FH], BF16), "wu": dscr("wu_bf", [2, D, FH], BF16), "wd": dscr("wd_bf", [2, FH, D], BF16)}

    with ExitStack() as es:
        em = Em(nc, es)

        def barrier():
            for e in ("pe", "act", "dve", "pool", "sp"):
                eng = em.eng[e]
                for f in ("pe", "act", "dve", "pool"):
                    if f != e and em.cnt[f] > 0 and em.waited[e].get(em.semkey[f], 0) < em.cnt[f]:
                        eng.wait_ge(em.sem[f], em.cnt[f]); em.waited[e][em.semkey[f]] = em.cnt[f]
                for q in em.dsem:
                    n = em.ndma[q]
                    for s_ in range(em.NDS):
                        uses = (n - s_ + em.NDS - 1) // em.NDS if n > s_ else 0
                        sk = ("dma", q, s_)
                        if uses > 0 and em.waited[e].get(sk, 0) < 16 * uses:
                            eng.wait_ge(em.dsem[q][s_], 16 * uses); em.waited[e][sk] = 16 * uses

        PS = es.enter_context(nc.psum_tensor("PS", [128, 8, 512], F32))
        ident = es.enter_context(nc.sbuf_tensor("ident_s", [128, 128], F32))
        identb = es.enter_context(nc.sbuf_tensor("identb_s", [128, 128], BF16))
        MODC = es.enter_context(nc.sbuf_tensor("MODC", [128, 2, 2, 2, 2, 8], F32))
        mhalf = es.enter_context(nc.sbuf_tensor("mhalf", [128, 8], F32))
        em.dma("sp", ident[:], ident_d, W=["ident"])
        em.dma("sp", identb[:], identb_d, W=["identb"])
        em.op("dve", lambda: nc.vector.memset(mhalf[:], -0.5), W=["mhalf"])

        V = nc.vector; A = nc.scalar; G = nc.gpsimd; T = nc.tensor

        def pbank(b):
            return ("ps", b)

        def rstd_from_ssq(ssq, rs, n, tag):
            em.op("dve", lambda: V.tensor_scalar(out=rs[:, 0:n], in0=ssq[:, 0:n], scalar1=1.0 / D, scalar2=EPS,
                                                 op0=ALU.mult, op1=ALU.add), R=[tag + "ssq"], W=[tag + "rs"])
            em.op("pool", lambda: G.tensor_tensor(out=rs[:, 0:n], in0=rs[:, 0:n], in1=mhalf[:, 0:n], op=ALU.pow),
                  R=[tag + "rs", "mhalf"], W=[tag + "rs"])

        def prologue_a(xt, xkey, nsub, xs, xskey, ssq, rs, junk, tag):
            for s in range(nsub):
                em.op("act", lambda s=s: A.activation(out=junk[:, :], in_=xt[:, s, :], func=AF.Square,
                                                      accum_out=ssq[:, s:s + 1]),
                      R=[xkey], W=[tag + "junk", tag + "ssq"])
            rstd_from_ssq(ssq, rs, nsub, tag)
            for s in range(nsub):
                em.op("act", lambda s=s: A.activation(out=xs[:, s, :], in_=xt[:, s, :], func=AF.Copy,
                                                      scale=rs[:, s:s + 1]),
                      R=[xkey, tag + "rs"], W=[xskey])

        def prologue_b(nsub, xs, xskey, Acol, Bcol, hT, hkey, tb):
            for k in range(8):
                b = tb[k % len(tb)]
                for s in range(nsub):
                    em.op("pe", lambda s=s, k=k, b=b: T.transpose(PS[:, b, s * 128:(s + 1) * 128],
                                                                  xs[:, s, k * 128:(k + 1) * 128], ident[:]),
                          R=[xskey, "ident"], W=[pbank(b)], inc=(s == nsub - 1))
                em.op("act", lambda k=k, b=b: A.activation(out=hT[:, k, 0:nsub * 128], in_=PS[:, b, 0:nsub * 128],
                                                           func=AF.Identity, scale=Acol[:, k:k + 1],
                                                           bias=Bcol[:, k:k + 1]),
                      R=[pbank(b), "MODC"], W=[hkey])

        def prologue(xt, xkey, nsub, xs, xskey, Acol, Bcol, hT, hkey, ssq, rs, junk, tag, tb):
            prologue_a(xt, xkey, nsub, xs, xskey, ssq, rs, junk, tag)
            prologue_b(nsub, xs, xskey, Acol, Bcol, hT, hkey, tb)

        def epilogue(psb, xsub, xkey, Gb, gkey, ssq, rs, junk, tt, tag):
            yv = PS[:, psb:psb + 2, :]
            em.op("act", lambda: A.activation(out=junk[:, :], in_=yv, func=AF.Square, accum_out=ssq[:, 0:1]),
                  R=[pbank(psb), pbank(psb + 1)], W=[tag + "junk", tag + "ssq"])
            rstd_from_ssq(ssq, rs, 1, tag)
            em.op("dve", lambda: V.tensor_tensor(out=tt[:, :], in0=yv, in1=Gb, op=ALU.mult),
                  R=[pbank(psb), pbank(psb + 1), gkey], W=[tag + "tt"])
            em.op("dve", lambda: V.scalar_tensor_tensor(out=xsub, in0=tt[:, :], scalar=rs[:, 0:1], in1=xsub,
                                                        op0=ALU.mult, op1=ALU.add),
                  R=[tag + "tt", tag + "rs", xkey], W=[xkey])

        with ExitStack() as pes:
            sb = lambda n, s, d=F32: pes.enter_context(nc.sbuf_tensor(n, list(s), d))
            cc = sb("cc", [128, 16]); scc = sb("scc", [128, 16]); rhs2 = sb("rhs2", [128, 8, 2], BF16)
            bmc = sb("bmc", [128, 2, 48]); bm1 = sb("bm1", [128, 2, 48]); gco = sb("gco", [128, 2, 2, 8])
            bmrow = sb("bmrow", [2, 2, 2, D]); grow = sb("grow", [2, 2, 2, D]); grt = sb("grt", [2, 2, 2, D])
            wm = [sb(f"wm{i}", [128, 8, 512], BF16) for i in range(3)]
            em.dma("sp", cc[:], ccols_d, W=["cc"])
            em.dma("sp", bmc[:], bmodc_d, W=["bmc"])
            em.dma("sp", gco[:], gcols_d, W=["gco"])
            for l in range(2):
                for w_, (src, off) in enumerate([(bmod_d, 2 * D), (bmod_d, 5 * D)]):
                    em.dma("sp", bmrow[:, l, w_, :], src[l, off:off + D].partition_broadcast(2), W=["bmrow"])
                em.dma("sp", grow[:, l, 0, :], gpm_d[l, :].partition_broadcast(2), W=["grow"])
                em.dma("sp", grow[:, l, 1, :], gpf_d[l, :].partition_broadcast(2), W=["grow"])
            em.op("act", lambda: A.activation(out=scc[:], in_=cc[:], func=AF.Silu), R=["cc"], W=["scc"])
            em.op("dve", lambda: V.tensor_copy(out=rhs2[:, :, 0], in_=scc[:, 0:8]), R=["scc"], W=["rhs2"])
            em.op("dve", lambda: V.tensor_copy(out=rhs2[:, :, 1], in_=scc[:, 8:16]), R=["scc"], W=["rhs2"])
            em.op("dve", lambda: V.tensor_scalar(out=bm1[:], in0=bmc[:], scalar1=1.0, scalar2=None, op0=ALU.add),
                  R=["bmc"], W=["bm1"])
            it = 0
            for l in range(2):
                wsrc = wmod_d[l].rearrange("(k p) n -> p k n", p=128)
                for nb in range(12):
                    slot = it % 3; it += 1
                    wt = wm[slot]; wk = f"wm{slot}"
                    em.dma("pool", wt[:], wsrc[:, :, nb * 512:(nb + 1) * 512], W=[wk])
                    v = nb // 2; half = nb % 2
                    b = it % 4
                    if v in (2, 5):
                        w_ = 0 if v == 2 else 1
                        for k in range(8):
                            em.op("pe", lambda k=k, b=b, wt=wt: T.matmul(PS[0:2, b, :], lhsT=rhs2[:, k, :], rhs=wt[:, k, :],
                                                                        start=(k == 0), stop=(k == 7)),
                                  R=["rhs2", wk], W=[pbank(b)], inc=(k == 7))
                        dst = grt[:, l, w_, half * 512:(half + 1) * 512]
                        em.op("dve", lambda b=b, dst=dst, l=l, w_=w_, half=half: V.tensor_tensor(
                            out=dst, in0=PS[0:2, b, :], in1=bmrow[:, l, w_, half * 512:(half + 1) * 512], op=ALU.add),
                              R=[pbank(b), "bmrow"], W=["grt"])
                        em.op("dve", lambda dst=dst, l=l, w_=w_, half=half: V.tensor_tensor(
                            out=dst, in0=dst, in1=grow[:, l, w_, half * 512:(half + 1) * 512], op=ALU.mult),
                              R=["grt", "grow"], W=["grt"])
                        if half == 1:
                            em.dma("sp", GROW[l, w_, :, :], grt[:, l, w_, :], R=["grt"], W=[("GROW", l, w_)])
                    else:
                        sub = 0 if v < 2 else 1
                        isA = v in (1, 4)
                        for m in range(4):
                            ch = half * 4 + m
                            for k in range(8):
                                em.op("pe", lambda k=k, b=b, m=m, wt=wt: T.matmul(
                                    PS[:, b, 2 * m:2 * m + 2], lhsT=wt[:, k, m * 128:(m + 1) * 128], rhs=rhs2[:, k, :],
                                    start=(k == 0), stop=(k == 7)), R=["rhs2", wk], W=[pbank(b)], inc=(k == 7))
                            dst = MODC[:, l, sub, 0 if isA else 1, :, ch]
                            if isA:
                                em.op("dve", lambda b=b, m=m, dst=dst, l=l, v=v, ch=ch, sub=sub: V.tensor_scalar(
                                    out=dst, in0=PS[:, b, 2 * m:2 * m + 2], scalar1=bm1[:, l, v * 8 + ch:v * 8 + ch + 1],
                                    scalar2=gco[:, l, sub, ch:ch + 1], op0=ALU.add, op1=ALU.mult),
                                      R=[pbank(b), "bm1", "gco"], W=["MODC"])
                            else:
                                em.op("dve", lambda b=b, m=m, dst=dst, l=l, v=v, ch=ch: V.tensor_scalar(
                                    out=dst, in0=PS[:, b, 2 * m:2 * m + 2], scalar1=bmc[:, l, v * 8 + ch:v * 8 + ch + 1],
                                    scalar2=None, op0=ALU.add), R=[pbank(b), "bmc"], W=["MODC"])
            if debug:
                MODCd = nc.dram_tensor("MODCd", [128, 128], F32, kind="ExternalOutput").ap()
                em.dma("sp", MODCd, MODC[:].rearrange("p a b c d e -> p (a b c d e)"), R=["MODC"], W=["MODCd"])
            barrier()
        if stop == "0":
            em.finish()
            return nc
        C = type("C", (), {})()
        C.__dict__.update(locals())
        for name, fn in PHASES:
            fn(C)
            barrier()
            if stop == name:
                if hasattr(C, 'es_mix'):
                    C.es_mix.close()
                break
        em.finish()
    return nc


PHASES = []


def _host_inputs(inp, b):
    global _TABLES
    if _TABLES is None:
        _TABLES = _tables()
    f = lambda a: np.ascontiguousarray(np.asarray(a, dtype=np.float32))
    m = {}
    b, h = b // 2, b % 2
    m["x"] = f(inp["x"][b]); m["ctx"] = f(inp["ctx"][b])
    m["xloc"] = f(inp["x"][b].reshape(128, 64, D)[_local_rows(h)].reshape(NL, D))
    m["hsel"] = np.tile(np.array([[1.0 - h, float(h)]], np.float32), (128, 1))
    m["ccols"] = np.concatenate([_col(inp["c"][b], 8), _col(inp["c_ctx"], 8)], axis=1)
    m["w_mod"] = f(inp["w_mod"]); m["b_mod"] = f(inp["b_mod"])
    m["bmodc"] = np.ascontiguousarray(np.stack([_col(inp["b_mod"][l], 48) for l in range(2)], axis=1))
    m["gcols"] = np.ascontiguousarray(np.stack(
        [np.stack([_col(inp["g_pre_mix"][l], 8), _col(inp["g_pre_ffn"][l], 8)], axis=1) for l in range(2)], axis=1))
    m["g_post_mix"] = f(inp["g_post_mix"]); m["g_post_ffn"] = f(inp["g_post_ffn"])
    m["w_in_ab"] = f(inp["w_in_ab"][0]); m["w_out_ab"] = f(inp["w_out_ab"][0])
    m["w_ffn_gate"] = f(inp["w_ffn_gate"]); m["w_ffn_up"] = f(inp["w_ffn_up"]); m["w_ffn_down"] = f(inp["w_ffn_down"])
    m["w_qkv_na"] = f(inp["w_qkv_na"][0]); m["w_out_na"] = f(inp["w_out_na"][0])
    wbd = np.zeros((2, 2, 4, 128, 128), np.float32)
    for gi, key in enumerate(["lru_w_a", "lru_w_i"]):
        w = np.asarray(inp[key][0], np.float32)
        for d in range(2):
            for c in range(4):
                wbd[gi, d, c, 0:64, 0:64] = w[d, 2 * c]
                wbd[gi, d, c, 64:128, 64:128] = w[d, 2 * c + 1]
    m["wbd"] = wbd
    lc = np.zeros((128, 4, 2, 3), np.float32)
    for d in range(2):
        lc[:, :, d, 0] = _col(inp["lru_b_a"][0][d], 4)
        lc[:, :, d, 1] = _col(inp["lru_b_i"][0][d], 4)
        lc[:, :, d, 2] = _col(inp["lru_lam"][0][d], 4)
    m["lcols"] = lc
    cv = np.zeros((128, 4, 5), np.float32)
    for k in range(4):
        cv[:, :, k] = _col(inp["conv_w"][0][k], 4)
    cv[:, :, 4] = _col(inp["conv_b"][0], 4)
    m["convc"] = cv
    cd = np.zeros((4, 4, 128, 128), np.float32)
    ii = np.arange(128)
    for c in range(4):
        for k in range(4):
            cd[c, k, ii, ii] = cv[:, c, k]
    m["cdiag"] = cd
    for k in ("ident", "identb", "T1", "M2", "CSf", "CSc", "T256"):
        m[k] = _TABLES[k]
    rpb = np.asarray(inp["rpb_na"][0], np.float32)
    m["BT"] = np.ascontiguousarray(_bias_tables(rpb)[2])
    m["BTE"] = _edge_tables(rpb, h)
    return m


_NC_CACHE = {}


def kernel(**inputs):
    if "full" not in _NC_CACHE:
        _NC_CACHE["full"] = build()
    nc = _NC_CACHE["full"]
    in_maps = [_host_inputs(inputs, b) for b in range(NCORES)]
    res = run_bass_kernel_spmd(nc, in_maps, core_ids=list(range(NCORES)))
    out = np.empty((4, S, D), np.float32)
    for c in range(NCORES):
        b, h = c // 2, c % 2
        o = np.asarray(res.results[c]["out"], dtype=np.float32)
        out[b, 4096 * h:4096 * (h + 1)] = o
    return out


def _wload(C, dst, src_view, key, nk, ncols, step=512):
    for c0 in range(0, ncols, step):
        c1 = min(ncols, c0 + step)
        C.em.dma("pool", dst[:, :, c0:c1], src_view[:, :, c0:c1], W=[key])


def _wload_bf(C, dst, src_view, key, ncols, srckey, step=1024):
    cv_emit(C, 0, need=srckey)
    for bi, c0 in enumerate(range(0, ncols, step)):
        c1 = min(ncols, c0 + step)
        C.em.dma("sp", dst[:, :, c0:c1], src_view[:, :, c0:c1], R=C.cvkeys[srckey], W=[key + str(bi)])


def preconvert_init(C):
    jobs = [("woab", C.woab_d, C.WBF["woab"]), ("wg0", C.wg_d[0], C.WBF["wg"][0]), ("wu0", C.wu_d[0], C.WBF["wu"][0]),
            ("wd0", C.wd_d[0], C.WBF["wd"][0]), ("wqkv", C.wqkv_d, C.WBF["wqkv"]), ("wona", C.wona_d, C.WBF["wona"]),
            ("wg1", C.wg_d[1], C.WBF["wg"][1]), ("wu1", C.wu_d[1], C.WBF["wu"][1]), ("wd1", C.wd_d[1], C.WBF["wd"][1])]
    C.cvkeys = {}
    C.cvq = []
    for key, src, dst in jobs:
        rows, cols = src.shape
        rstep = 512 if rows % 512 == 0 else 704
        C.cvkeys["cv" + key] = []
        for r0 in range(0, rows, rstep):
            for c0 in range(0, cols, 1024):
                c1 = min(cols, c0 + 1024)
                k_ = f"cv{key}_{r0}_{c0}"
                C.cvkeys["cv" + key].append(k_)
                C.cvq.append(("cv" + key, k_, dst[r0:r0 + rstep, c0:c1], src[r0:r0 + rstep, c0:c1]))


def cv_emit(C, n=1, need=None):
    while C.cvq and (n > 0 or (need is not None and any(j[0] == need for j in C.cvq))):
        big, k_, dst, src = C.cvq.pop(0)
        C.em.dma_bg("pool", dst, src, W=[k_])
        n -= 1


def phase_A(C):
    nc, em, PS = C.nc, C.em, C.PS
    V, A, G, T = nc.vector, nc.scalar, nc.gpsimd, nc.tensor
    pes = C.es_mix = ExitStack()
    preconvert_init(C)
    C.F = pes.enter_context(nc.sbuf_tensor("Fbuf", [128, 64, 512], BF16))
    C.fTc = pes.enter_context(nc.sbuf_tensor("fTc", [128, 4, LC], BF16))
    F, fTc = C.F, C.fTc
    with ExitStack() as ps_:
        sb = lambda n, s, d=F32: ps_.enter_context(nc.sbuf_tensor(n, list(s), d))
        win = sb("win", [128, 8, 1536], BF16)
        xt = [sb(f"Axt{i}", [128, 4, D]) for i in range(2)]
        hT = [sb(f"AhT{i}", [128, 8, 512], BF16) for i in range(2)]
        ugst = [sb(f"Aug{i}", [128, 8, 512]) for i in range(2)]
        ssq = sb("Assq", [128, 4]); rs = sb("Ars", [128, 4]); junk = sb("Ajunk", [128, D], BF16)
        _wload(C, win, C.win_d.rearrange("(k p) n -> p k n", p=128), "win", 8, 1536)
        xsrc = C.x_d.rearrange("(t1 t2) d -> t1 t2 d", t2=64)
        Acol = C.MODC[:, 0, 0, 0, 0, :]; Bcol = C.MODC[:, 0, 0, 1, 0, :]
        AcolC = C.MODC[:, 0, 0, 0, 1, :]; BcolC = C.MODC[:, 0, 0, 1, 1, :]
        evc = [0]

        def loadA(j):
            sl = j % 2
            if j < 16:
                em.dma("sp", xt[sl][:], xsrc[:, 4 * j:4 * (j + 1), :], W=[f"Axt{sl}"])
            elif j == 16:
                em.dma("sp", xt[sl][:, 0:2, :], C.ctx_d.rearrange("(s p) d -> p s d", p=128), W=[f"Axt{sl}"])

        def nsubA(j):
            return 2 if j == 16 else 4

        def pro_a(j):
            sl = j % 2
            C.prologue_a(xt[sl], f"Axt{sl}", nsubA(j), xt[sl], f"Axt{sl}", ssq, rs, junk, "A")

        def pro_b(j):
            sl = j % 2
            isctx = j == 16
            C.prologue_b(nsubA(j), xt[sl], f"Axt{sl}", AcolC if isctx else Acol, BcolC if isctx else Bcol,
                         hT[sl], f"AhT{sl}", [0, 1])

        def ugA(j):
            sl = j % 2
            isctx = j == 16
            n = nsubA(j) * 128
            for oc in range(8):
                b = 2 + oc % 3
                for k in range(8):
                    em.op("pe", lambda k=k, b=b, oc=oc, sl=sl, n=n: T.matmul(
                        PS[:, b, 0:n], lhsT=win[:, k, oc * 128:(oc + 1) * 128], rhs=hT[sl][:, k, 0:n],
                        start=(k == 0), stop=(k == 7)), R=["win", f"AhT{sl}"], W=[("ps", b)], inc=(k == 7))
                evc[0] += 1
                if evc[0] % 2:
                    em.op("dve", lambda b=b, oc=oc, sl=sl, n=n: V.tensor_copy(out=ugst[sl][:, oc, 0:n], in_=PS[:, b, 0:n]),
                          R=[("ps", b)], W=[f"Aug{sl}"])
                else:
                    em.op("act", lambda b=b, oc=oc, sl=sl, n=n: A.copy(out=ugst[sl][:, oc, 0:n], in_=PS[:, b, 0:n]),
                          R=[("ps", b)], W=[f"Aug{sl}"])
            if isctx:
                em.dma("pool", C.UGc.rearrange("c p n -> p c n"), ugst[sl][:, :, 0:LC], R=[f"Aug{sl}"], W=["UGc"])
            else:
                em.dma("pool", C.UG[:, :, j * 512:(j + 1) * 512].rearrange("c p n -> p c n"), ugst[sl][:],
                       R=[f"Aug{sl}"], W=[("UG", j)])

        def fA(j):
            sl = j % 2
            if j == 16:
                for fc in range(4):
                    b = 5 + fc % 3
                    for k in range(8):
                        em.op("pe", lambda k=k, b=b, fc=fc, sl=sl: T.matmul(
                            PS[:, b, 0:LC], lhsT=win[:, k, 1024 + fc * 128:1024 + (fc + 1) * 128], rhs=hT[sl][:, k, 0:LC],
                            start=(k == 0), stop=(k == 7)), R=["win", f"AhT{sl}"], W=[("ps", b)], inc=(k == 7))
                    em.op("dve", lambda b=b, fc=fc: V.tensor_copy(out=fTc[:, fc, :], in_=PS[:, b, 0:LC]),
                          R=[("ps", b)], W=["fTc"])
            else:
                for s in range(4):
                    b = 5 + s % 3
                    for k in range(8):
                        em.op("pe", lambda k=k, b=b, s=s, sl=sl: T.matmul(
                            PS[:, b, :], lhsT=hT[sl][:, k, s * 128:(s + 1) * 128], rhs=win[:, k, 1024:1536],
                            start=(k == 0), stop=(k == 7)), R=["win", f"AhT{sl}"], W=[("ps", b)], inc=(k == 7))
                    evc[0] += 1
                    if evc[0] % 2:
                        em.op("dve", lambda b=b, s=s, j=j: V.tensor_copy(out=F[:, 4 * j + s, :], in_=PS[:, b, :]),
                              R=[("ps", b)], W=["F"])
                    else:
                        em.op("act", lambda b=b, s=s, j=j: A.copy(out=F[:, 4 * j + s, :], in_=PS[:, b, :]),
                              R=[("ps", b)], W=["F"])

        loadA(0)
        loadA(1)
        pro_a(0)
        pro_b(0)
        for j in range(17):
            ugA(j)
            cv_emit(C, 1)
            if j + 2 < 17:
                loadA(j + 2)
            if j + 1 < 17:
                pro_a(j + 1)
            fA(j)
            if j + 1 < 17:
                pro_b(j + 1)


def phase_C(C):
    nc, em, PS, F, fTc = C.nc, C.em, C.PS, C.F, C.fTc
    V, A, G, T = nc.vector, nc.scalar, nc.gpsimd, nc.tensor
    with ExitStack() as ps_:
        sb = lambda n, s, d=F32: ps_.enter_context(nc.sbuf_tensor(n, list(s), d))
        T1 = sb("T1s", [128, 256], BF16); M2 = sb("M2s", [64, 128, 192], BF16); CSf = sb("CSfs", [128, 256], BF16)
        CSc = sb("CScs", [128, 256], BF16); T256 = sb("T256s", [128, 2, 512], BF16)
        Ast = sb("Ast", [64, 64, 256], BF16); Y = sb("Ybuf", [128, 2, S], BF16)
        fst = [sb(f"fst{i}", [128, 2048], BF16) for i in range(2)]
        Gc = sb("Gcb", [128, 2, 256], BF16); fcs = sb("fcs", [128, LC], BF16)
        for dst, src, k in ((T1, C.T1_d, "T1"), (M2, C.M2_d, "M2"), (CSf, C.CSf_d, "CSf"), (CSc, C.CSc_d, "CSc"),
                            (T256, C.T256_d, "T256")):
            em.dma("sp", dst[:], src, W=[k])
        ev = 0
        for cc in range(4):
            for hc in range(2):
                for g4 in range(16):
                    b = 2 * (g4 % 2)
                    for q in range(4):
                        ch = cc * 128 + hc * 64 + g4 * 4 + q
                        em.op("pe", lambda b=b, q=q, ch=ch: T.matmul(
                            PS[0:64, b + q // 2, (q % 2) * 256:(q % 2) * 256 + 256], lhsT=F[:, :, ch], rhs=T1[:, :],
                            start=True, stop=True), R=["F", "T1"], W=[("ps", b), ("ps", b + 1)], inc=(q == 3))
                    ev += 1
                    dst = Ast[:, g4 * 4:(g4 + 1) * 4, :].rearrange("p a b -> p (a b)")
                    src = PS[0:64, b:b + 2, :].rearrange("p a b -> p (a b)")
                    if ev % 2:
                        em.op("dve", lambda dst=dst, src=src: V.tensor_copy(out=dst, in_=src),
                              R=[("ps", b), ("ps", b + 1)], W=["Ast"])
                    else:
                        em.op("act", lambda dst=dst, src=src: A.copy(out=dst, in_=src),
                              R=[("ps", b), ("ps", b + 1)], W=["Ast"])
                for kb in range(32):
                    b = 4 + kb % 2
                    for q in range(4):
                        k1 = kb * 4 + q
                        o = PS[hc * 64:(hc + 1) * 64, b, q * 128:(q + 1) * 128]
                        em.op("pe", lambda o=o, k1=k1: T.matmul(o, lhsT=Ast[:, :, k1], rhs=M2[:, k1, 64:192],
                                                                start=True, stop=False),
                              R=["Ast", "M2"], W=[("ps", b)], inc=False)
                        em.op("pe", lambda o=o, k1=k1: T.matmul(o, lhsT=Ast[:, :, 128 + k1], rhs=M2[:, k1, 0:128],
                                                                start=False, stop=True),
                              R=["Ast", "M2"], W=[("ps", b)], inc=(q == 3))
                    ev += 1
                    src = PS[hc * 64:(hc + 1) * 64, b, :].rearrange("p (k r c) -> p r c k", k=4, r=2)
                    dst = Y[hc * 64:(hc + 1) * 64, :, :].rearrange("p r (c k) -> p r c k", k=128)[:, :, :, kb * 4:(kb + 1) * 4]
                    if ev % 2:
                        em.op("dve", lambda dst=dst, src=src: V.tensor_copy(out=dst, in_=src), R=[("ps", b)], W=["Y"])
                    else:
                        em.op("act", lambda dst=dst, src=src: A.copy(out=dst, in_=src), R=[("ps", b)], W=["Y"])
            for tl in range(16):
                b = 6 + tl % 2
                em.op("pe", lambda b=b, tl=tl: T.matmul(PS[:, b, :], lhsT=CSf[:, 0:128], rhs=Y[:, 0, tl * 512:(tl + 1) * 512],
                                                        start=True, stop=False), R=["CSf", "Y"], W=[("ps", b)], inc=False)
                em.op("pe", lambda b=b, tl=tl: T.matmul(PS[:, b, :], lhsT=CSf[:, 128:256], rhs=Y[:, 1, tl * 512:(tl + 1) * 512],
                                                        start=False, stop=True), R=["CSf", "Y"], W=[("ps", b)], inc=True)
                fs = (tl // 4) % 2
                em.op("act", lambda b=b, tl=tl, fs=fs: A.copy(out=fst[fs][:, (tl % 4) * 512:(tl % 4 + 1) * 512], in_=PS[:, b, :]),
                      R=[("ps", b)], W=[f"fst{fs}"])
                if tl % 4 == 3:
                    t0 = (tl // 4) * 2048
                    em.dma("pool", C.MIXT[512 + cc * 128:512 + (cc + 1) * 128, t0:t0 + 2048], fst[fs][:],
                           R=[f"fst{fs}"], W=[("MIXT", 4 + cc)])
            for tc in range(2):
                em.op("pe", lambda tc=tc, cc=cc: T.matmul(PS[:, 0, 0:256], lhsT=fTc[:, cc, tc * 128:(tc + 1) * 128], rhs=CSc[:, :],
                                                          start=True, stop=True), R=["fTc", "CSc"], W=[("ps", 0)])
                em.op("dve", lambda tc=tc: V.tensor_copy(out=Gc[:, tc, :], in_=PS[:, 0, 0:256]), R=[("ps", 0)], W=["Gc"])
            for i_, (tc, part) in enumerate([(0, 0), (0, 1), (1, 0), (1, 1)]):
                em.op("pe", lambda i_=i_, tc=tc, part=part: T.matmul(
                    PS[:, 1, 0:256], lhsT=Gc[:, tc, part * 128:(part + 1) * 128], rhs=T256[:, tc, part * 256:(part + 1) * 256],
                    start=(i_ == 0), stop=(i_ == 3)), R=["Gc", "T256"], W=[("ps", 1)], inc=(i_ == 3))
            em.op("dve", lambda: V.tensor_copy(out=fcs[:, :], in_=PS[:, 1, 0:256]), R=[("ps", 1)], W=["fcs"])
            em.dma("pool", C.MIXT[512 + cc * 128:512 + (cc + 1) * 128, S:NT], fcs[:], R=["fcs"], W=[("MIXTc", 4 + cc)])
    C.es_mix.close()


PHASES += [("A", phase_A), ("C", phase_C)]


def phase_B(C):
    nc, em, PS = C.nc, C.em, C.PS
    V, A, G, T = nc.vector, nc.scalar, nc.gpsimd, nc.tensor
    TW = 2048
    with ExitStack() as ps_:
        sb = lambda n, s, d=F32: ps_.enter_context(nc.sbuf_tensor(n, list(s), d))
        bufA = sb("bufA", [128, NT]); bufB = sb("bufB", [128, 8460]); uc = sb("ucb_", [128, NT]); ucb = sb("ucbb", [128, NT], BF16)
        rt = sb("rt", [128, TW])
        it_ = [sb(f"it{i}", [128, TW]) for i in range(2)]; at = [sb(f"at{i}", [128, TW]) for i in range(2)]
        st = [sb(f"st{i}", [128, TW]) for i in range(2)]
        gtmp = [sb(f"gtmp{i}", [128, 1024]) for i in range(2)]
        wbd = sb("wbds", [128, 16, 128], BF16)
        cdg = sb("cdg", [128, 16, 128])
        lco = sb("lco", [128, 4, 2, 3]); cvc = sb("cvc", [128, 4, 5]); cA = sb("cA", [128, 4, 2]); carry = sb("carry", [128, 2])
        em.dma("pool", wbd[:], C.wbd_d.rearrange("g d c p n -> p (g d c) n"), W=["wbd"])
        em.dma("sp", cdg[:], C.cdiag_d.rearrange("c k p n -> p (c k) n"), W=["cdg"])
        em.dma("sp", lco[:], C.lcols_d, W=["lco"])
        em.dma("sp", cvc[:], C.convc_d, W=["cvc"])
        em.op("act", lambda: A.activation(out=cA[:], in_=lco[:, :, :, 2], func=AF.Exp, scale=-1.0), R=["lco"], W=["cA"])
        em.op("act", lambda: A.activation(out=cA[:], in_=cA[:], func=AF.Ln, bias=1.0), R=["cA"], W=["cA"])
        em.op("dve", lambda: V.tensor_scalar(out=cA[:], in0=cA[:], scalar1=-8.0, scalar2=None, op0=ALU.mult), R=["cA"], W=["cA"])
        em.op("dve", lambda: V.memset(bufB[:], 0.0), W=["bufB"])
        XO = 8200
        tiles = [(S, NT)] + [(i * TW, (i + 1) * TW) for i in range(4)]
        tix = 0
        for c in range(4):
            cv_emit(C, 2)
            em.dma("sp", bufA[:, 0:S], C.UG[c], W=["bufA"])
            em.dma("sp", bufA[:, S:NT], C.UGc[c], W=["bufA"])
            for qd in range(4):
                o_ = bufB[:, 2 + qd * 2048:2 + (qd + 1) * 2048].rearrange("p (a b) -> p a b", b=64)
                i_ = bufA[:, 0:S].rearrange("p (b a) -> p a b", a=128)[:, qd * 32:(qd + 1) * 32, :]
                eng = QENG[qd]
                fn = {"act": (lambda o_=o_, i_=i_: A.copy(out=o_, in_=i_)),
                      "dve": (lambda o_=o_, i_=i_: V.tensor_copy(out=o_, in_=i_)),
                      "pool": (lambda o_=o_, i_=i_: G.tensor_copy(out=o_, in_=i_))}[eng]
                em.op(eng, fn, R=["bufA"], W=[f"bufBq{qd}"])
            em.op("pool", lambda: G.tensor_copy(out=bufB[:, XO + 2:XO + 2 + LC], in_=bufA[:, S:NT]), R=["bufA"], W=["bufBc"])
            em.dma("sp", bufA[:, 0:S], C.UG[4 + c], W=["bufA"])
            em.dma("sp", bufA[:, S:NT], C.UGc[4 + c], W=["bufA"])
            for (o0, src0, n) in ((0, 0, S), (S, XO, LC)):
                for q0 in range(0, n, 2048):
                    nn = min(2048, n - q0)
                    nq = (nn + 511) // 512
                    pb = 4 * ((q0 // 2048) % 2)
                    qd = q0 // 2048
                    rk = ["bufBc", "bufB"] if n == LC else [f"bufBq{x}" for x in (qd - 1, qd, qd + 1) if 0 <= x < 4] + ["bufB"]
                    for q in range(nq):
                        w_ = min(512, nn - q * 512)
                        for k in range(4):
                            em.op("pe", lambda q=q, k=k, w_=w_, pb=pb, src0=src0, q0=q0, c=c: T.matmul(
                                PS[:, pb + q, 0:w_], lhsT=cdg[:, c * 4 + k, :],
                                rhs=bufB[:, src0 + q0 + q * 512 + k:src0 + q0 + q * 512 + k + w_],
                                start=(k == 0), stop=(k == 3)), R=["cdg"] + rk, W=[("ps", pb + q)], inc=(k == 3))
                    src = PS[:, pb:pb + 4, :].rearrange("p a b -> p (a b)")[:, 0:nn]
                    em.op("act", lambda src=src, o0=o0, q0=q0, nn=nn, c=c: A.activation(
                        out=uc[:, o0 + q0:o0 + q0 + nn], in_=src, func=AF.Identity, bias=cvc[:, c, 4:5]),
                          R=[("ps", pb + q) for q in range(4)] + ["cvc"], W=["uc"])
                    em.op("act", lambda src=src, o0=o0, q0=q0, nn=nn, c=c: A.activation(
                        out=ucb[:, o0 + q0:o0 + q0 + nn], in_=src, func=AF.Identity, bias=cvc[:, c, 4:5]),
                          R=[("ps", pb + q) for q in range(4)] + ["cvc"], W=["ucb"])

            def gelu_s1(lo, hi, p):
                n = hi - lo
                em.op("act", lambda: A.activation(out=gtmp[p][:, 0:n], in_=bufA[:, lo:hi], func=AF.Square),
                      R=["bufA"], W=[f"gt{p}"])
                em.op("pool", lambda: G.tensor_scalar(out=gtmp[p][:, 0:n], in0=gtmp[p][:, 0:n], scalar1=0.044715, scalar2=1.0,
                                                      op0=ALU.mult, op1=ALU.add), R=[f"gt{p}"], W=[f"gt{p}"])
                em.op("dve", lambda: V.tensor_tensor(out=gtmp[p][:, 0:n], in0=gtmp[p][:, 0:n], in1=bufA[:, lo:hi], op=ALU.mult),
                      R=[f"gt{p}", "bufA"], W=[f"gt{p}"])

            def gelu_s2(lo, hi, p):
                n = hi - lo
                em.op("act", lambda: A.activation(out=gtmp[p][:, 0:n], in_=gtmp[p][:, 0:n], func=AF.Sigmoid,
                                                  scale=1.5957691216057308), R=[f"gt{p}"], W=[f"gt{p}"])
                em.op("dve", lambda: V.tensor_tensor(out=bufA[:, lo:hi], in0=gtmp[p][:, 0:n], in1=bufA[:, lo:hi], op=ALU.mult),
                      R=[f"gt{p}", "bufA"], W=["bufA"])

            gpieces = [(S, NT)] + [(i * 1024, (i + 1) * 1024) for i in range(8)]
            gstate = {"s1": 0, "s2": 0}

            def gelu_push1():
                k = gstate["s1"]
                if k < len(gpieces):
                    gelu_s1(gpieces[k][0], gpieces[k][1], k % 2)
                    gstate["s1"] += 1

            def gelu_push2():
                k = gstate["s2"]
                if k < gstate["s1"]:
                    gelu_s2(gpieces[k][0], gpieces[k][1], k % 2)
                    gstate["s2"] += 1

            for d in range(2):
                order = [tiles[0]] + (tiles[1:] if d == 0 else tiles[1:][::-1])
                for ti, (lo, hi) in enumerate(order):
                    p = tix % 2
                    tix += 1
                    n = hi - lo
                    nq = (n + 511) // 512
                    for gi in range(2):
                        for q in range(nq):
                            w_ = min(512, n - q * 512)
                            em.op("pe", lambda gi=gi, q=q, w_=w_, lo=lo, d=d, c=c: T.matmul(
                                PS[:, gi * 4 + q, 0:w_], lhsT=wbd[:, gi * 8 + d * 4 + c, :], rhs=ucb[:, lo + q * 512:lo + q * 512 + w_],
                                start=True, stop=True), R=["wbd", "ucb"], W=[("ps", gi * 4 + q)], inc=(q == nq - 1))
                    pr = PS[:, 0:4, :].rearrange("p a b -> p (a b)")[:, 0:n]
                    pi = PS[:, 4:8, :].rearrange("p a b -> p (a b)")[:, 0:n]
                    if d == 0:
                        gelu_push2()
                        gelu_push2()
                    em.op("act", lambda pr=pr, n=n, c=c, d=d: A.activation(out=rt[:, 0:n], in_=pr, func=AF.Sigmoid,
                                                                           bias=lco[:, c, d, 0:1]),
                          R=[("ps", q) for q in range(4)] + ["lco"], W=["rt"])
                    em.op("act", lambda pi=pi, n=n, c=c, d=d, p=p: A.activation(out=it_[p][:, 0:n], in_=pi, func=AF.Sigmoid,
                                                                                bias=lco[:, c, d, 1:2]),
                          R=[("ps", 4 + q) for q in range(4)] + ["lco"], W=[f"it{p}"])
                    em.op("act", lambda n=n, c=c, d=d, p=p: A.activation(out=at[p][:, 0:n], in_=rt[:, 0:n], func=AF.Exp,
                                                                         scale=cA[:, c, d:d + 1]), R=["rt", "cA"], W=[f"at{p}"])
                    em.op("dve", lambda n=n, p=p: V.tensor_tensor(out=st[p][:, 0:n], in0=at[p][:, 0:n], in1=at[p][:, 0:n], op=ALU.mult),
                          R=[f"at{p}"], W=[f"st{p}"])
                    if d == 0:
                        gelu_push1()
                        gelu_push1()
                    em.op("act", lambda n=n, p=p: A.activation(out=st[p][:, 0:n], in_=st[p][:, 0:n], func=AF.Sqrt, scale=-1.0, bias=1.0),
                          R=[f"st{p}"], W=[f"st{p}"])
                    em.op("dve", lambda n=n, p=p: V.tensor_tensor(out=it_[p][:, 0:n], in0=it_[p][:, 0:n], in1=st[p][:, 0:n], op=ALU.mult),
                          R=[f"it{p}", f"st{p}"], W=[f"it{p}"])
                    em.op("dve", lambda n=n, lo=lo, hi=hi, p=p: V.tensor_tensor(out=it_[p][:, 0:n], in0=it_[p][:, 0:n], in1=uc[:, lo:hi],
                                                                               op=ALU.mult), R=[f"it{p}", "uc"], W=[f"it{p}"])
                    init = 0.0 if ti == 0 else carry[:, d:d + 1]
                    if d == 0:
                        em.op("dve", lambda n=n, lo=lo, hi=hi, init=init, p=p: V.tensor_tensor_scan(
                            out=bufB[:, lo:hi], data0=at[p][:, 0:n], data1=it_[p][:, 0:n], initial=init, op0=ALU.mult, op1=ALU.add),
                              R=[f"at{p}", f"it{p}", "carry", "bufB", "uc"], W=["bufB", "bufBq0", "bufBq1", "bufBq2", "bufBq3", "bufBc"])
                        em.op("dve", lambda hi=hi: V.tensor_copy(out=carry[:, 0:1], in_=bufB[:, hi - 1:hi]), R=["bufB"], W=["carry"])
                    else:
                        em.op("dve", lambda n=n, init=init, p=p: V.tensor_tensor_scan(
                            out=st[p][:, 0:n][:, ::-1], data0=at[p][:, 0:n][:, ::-1],
                            data1=it_[p][:, 0:n][:, ::-1], initial=init, op0=ALU.mult, op1=ALU.add),
                              R=[f"at{p}", f"it{p}", "carry", f"st{p}"], W=[f"st{p}"])
                        em.op("dve", lambda p=p: V.tensor_copy(out=carry[:, 1:2], in_=st[p][:, 0:1]), R=[f"st{p}"], W=["carry"])
                        em.op("dve", lambda n=n, lo=lo, hi=hi, p=p: V.tensor_tensor(out=bufB[:, lo:hi], in0=bufB[:, lo:hi], in1=st[p][:, 0:n],
                                                                                    op=ALU.add), R=["bufB", f"st{p}"], W=["bufB"])
            while gstate["s2"] < len(gpieces):
                gelu_push1()
                gelu_push2()
            em.op("dve", lambda: V.tensor_tensor(out=ucb[:, 0:S].rearrange("p (a b) -> p a b", b=64),
                                                 in0=bufB[:, 0:S].rearrange("p (a b) -> p a b", b=64),
                                                 in1=bufA[:, 0:S].rearrange("p (b a) -> p a b", a=128), op=ALU.mult),
                  R=["bufB", "bufBq0", "bufBq1", "bufBq2", "bufBq3", "bufBc", "bufA", "ucb"], W=["ucb"])
            em.op("dve", lambda: V.tensor_tensor(out=ucb[:, S:NT], in0=bufB[:, S:NT], in1=bufA[:, S:NT], op=ALU.mult),
                  R=["bufB", "bufBq0", "bufBq1", "bufBq2", "bufBq3", "bufBc", "bufA", "ucb"], W=["ucb"])
            em.dma("pool", C.MIXT[c * 128:(c + 1) * 128, :], ucb[:, :], R=["ucb"], W=[("MIXT", c)])
            if c < 3:
                em.op("dve", lambda: V.memset(bufB[:, 0:2], 0.0), R=["bufB"], W=["bufB", "bufBq0"])
                em.op("dve", lambda: V.memset(bufB[:, S:8460], 0.0), R=["bufB"], W=["bufB", "bufBq3", "bufBc"])


def _tok_tiles(ntok_tile):
    return None


def phase_D1(C, layer=0):
    nc, em, PS = C.nc, C.em, C.PS
    V, A, G, T = nc.vector, nc.scalar, nc.gpsimd, nc.tensor
    with ExitStack() as ps_:
        sb = lambda n, s, d=F32: ps_.enter_context(nc.sbuf_tensor(n, list(s), d))
        wo = sb("D1wo", [128, 8, D], BF16)
        xt = [sb(f"D1xt{i}", [128, 4, D]) for i in range(3)]
        mt = [sb(f"D1mt{i}", [128, 8, 512], BF16) for i in range(3)]
        mb = [sb(f"D1mb{i}", [128, 8, 512], BF16) for i in range(3)]
        hs = sb("D1hs", [128, 2])
        Gx = sb("D1Gx", [128, D]); Gc = sb("D1Gc", [128, D])
        ssq = sb("D1ssq", [128, 4]); rs = sb("D1rs", [128, 4]); junk = sb("D1junk", [128, D], BF16); tt = sb("D1tt", [128, D])
        _wload_bf(C, wo, C.WBF["woab"].rearrange("(k p) n -> p k n", p=128), "D1wo", D, "cvwoab")
        em.dma("sp", hs[:], C.hsel_d, W=["D1hs"])
        em.dma("sp", Gx[:], C.GROW[0, 0, 0, :].partition_broadcast(128), R=[("GROW", 0, 0)], W=["D1Gx"])
        em.dma("sp", Gc[:], C.GROW[0, 0, 1, :].partition_broadcast(128), R=[("GROW", 0, 0)], W=["D1Gc"])
        mixv = C.MIXT.rearrange("(k p) t -> p k t", p=128)
        NTL = NLT + 1

        def load(j, sl):
            if j < NLT:
                em.dma("sp", xt[sl][:], C.xloc_d[j * 512:(j + 1) * 512, :].rearrange("(s p) d -> p s d", p=128), W=[f"D1xt{sl}"])
                em.dma("sp", mt[sl][:], mixv[:, :, j * 512:(j + 1) * 512], W=[f"D1mt{sl}"])
                lb = 4096 + j * 512 if j < 8 else LBASE
                em.dma("sp", mb[sl][:], mixv[:, :, lb:lb + 512], W=[f"D1mb{sl}"])
                em.op("act", lambda sl=sl: A.activation(out=mt[sl][:].rearrange("p a b -> p (a b)"),
                                                        in_=mt[sl][:].rearrange("p a b -> p (a b)"), func=AF.Copy,
                                                        scale=hs[:, 0:1]), R=[f"D1mt{sl}", "D1hs"], W=[f"D1mt{sl}"])
                em.op("dve", lambda sl=sl: V.scalar_tensor_tensor(
                    out=mt[sl][:].rearrange("p a b -> p (a b)"), in0=mb[sl][:].rearrange("p a b -> p (a b)"), scalar=hs[:, 1:2],
                    in1=mt[sl][:].rearrange("p a b -> p (a b)"), op0=ALU.mult, op1=ALU.add),
                      R=[f"D1mt{sl}", f"D1mb{sl}", "D1hs"], W=[f"D1mt{sl}"])
            else:
                em.dma("sp", xt[sl][:, 0:2, :], C.ctx_d.rearrange("(s p) d -> p s d", p=128), W=[f"D1xt{sl}"])
                em.dma("sp", mt[sl][:, :, 0:LC], mixv[:, :, S:NT], W=[f"D1mt{sl}"])
        load(0, 0)
        load(1, 1)
        for j in range(NTL):
            sl = j % 3
            if j + 2 < NTL:
                load(j + 2, (j + 2) % 3)
            isx = j < NLT
            nsub = 4 if isx else 2
            for s in range(nsub):
                pb = 2 * (s % 4)
                for h in range(2):
                    for k in range(8):
                        em.op("pe", lambda k=k, h=h, s=s, sl=sl, pb=pb: T.matmul(
                            PS[:, pb + h, :], lhsT=mt[sl][:, k, s * 128:(s + 1) * 128], rhs=wo[:, k, h * 512:(h + 1) * 512],
                            start=(k == 0), stop=(k == 7)), R=["D1wo0", f"D1mt{sl}"], W=[("ps", pb + h)], inc=(k == 7))
                C.epilogue(pb, xt[sl][:, s, :], f"D1xt{sl}", (Gx if isx else Gc)[:, :], "D1Gx" if isx else "D1Gc",
                           ssq, rs, junk, tt, "D1")
            if isx:
                em.dma("pool", C.X1[j * 512:(j + 1) * 512, :].rearrange("(s p) d -> p s d", p=128), xt[sl][:],
                       R=[f"D1xt{sl}"], W=[("X1", j)])
            else:
                em.dma("pool", C.C1.rearrange("(s p) d -> p s d", p=128), xt[sl][:, 0:2, :], R=[f"D1xt{sl}"], W=["C1"])


def phase_FFN(C, layer, Xin, Cin, Xout, Cout, inkey, outkey, ntok=NL):
    nc, em, PS = C.nc, C.em, C.PS
    V, A, G, T = nc.vector, nc.scalar, nc.gpsimd, nc.tensor
    tg = f"F{layer}"
    with ExitStack() as ps_:
        sb = lambda n, s, d=F32: ps_.enter_context(nc.sbuf_tensor(tg + n, list(s), d))
        wg = sb("wg", [128, 8, FH], BF16); wu = sb("wu", [128, 8, FH], BF16); wd = sb("wd", [128, NJ, D], BF16)
        xt = sb("xt", [128, 4, D]); hT = sb("hT", [128, 8, 512], BF16); hh = sb("hh", [128, NJ, 512], BF16)
        est = [sb(f"est{i}", [128, D]) for i in range(2)]
        Gx = sb("Gx", [128, D]); tt = sb("tt", [128, D]); junk = sb("junk", [128, D], BF16)
        ssq = sb("ssq", [128, 4]); rs = sb("rs", [128, 4]); ssq2 = sb("ssq2", [128, 4]); rs2 = sb("rs2", [128, 4])
        _wload_bf(C, wg, C.WBF["wg"][layer].rearrange("(k p) n -> p k n", p=128), tg + "wg", FH, f"cvwg{layer}", 704)
        _wload_bf(C, wu, C.WBF["wu"][layer].rearrange("(k p) n -> p k n", p=128), tg + "wu", FH, f"cvwu{layer}", 704)
        _wload_bf(C, wd, C.WBF["wd"][layer].rearrange("(k p) n -> p k n", p=128), tg + "wd", D, f"cvwd{layer}", 512)
        em.dma("sp", Gx[:], C.GROW[layer, 1, 0, :].partition_broadcast(128), W=[tg + "Gx"])
        NX = ntok // 512
        ntile = NX + (1 if Cin is not None else 0)

        def nsub_of(j):
            return 4 if j < NX else 2

        def rows(j, s):
            if j < NX:
                r0 = j * 512 + s * 128
                return Xin[r0:r0 + 128, :], Xout[r0:r0 + 128, :]
            return Cin[s * 128:(s + 1) * 128, :], Cout[s * 128:(s + 1) * 128, :]

        def load(j):
            ns = nsub_of(j)
            src = Xin[j * 512:(j + 1) * 512, :] if j < NX else Cin
            em.dma("sp", xt[:, 0:ns, :], src.rearrange("(s p) d -> p s d", p=128), W=[tg + "xt"])

        def pro_a(j):
            ns = nsub_of(j)
            C.prologue_a(xt, tg + "xt", ns, xt, tg + "xt", ssq2, rs2, junk, tg + "p")

        def pro_b(j):
            path = 0 if j < NX else 1
            C.prologue_b(nsub_of(j), xt, tg + "xt", C.MODC[:, layer, 1, 0, path, :], C.MODC[:, layer, 1, 1, path, :],
                         hT, tg + "hT", [0, 1, 2, 3])

        def gateup(j):
            n = nsub_of(j) * 128
            for jj in range(NJ):
                b = 2 * (jj % 2)
                for gi, w_ in enumerate((wg, wu)):
                    for k in range(8):
                        em.op("pe", lambda k=k, b=b, gi=gi, w_=w_, jj=jj: T.matmul(
                            PS[:, b + gi, 0:n], lhsT=w_[:, k, jj * 128:(jj + 1) * 128], rhs=hT[:, k, 0:n],
                            start=(k == 0), stop=(k == 7)), R=[tg + "wg" + str(x) for x in {jj * 128 // 704, (jj * 128 + 127) // 704}] + [tg + "wu" + str(x) for x in {jj * 128 // 704, (jj * 128 + 127) // 704}] + [tg + "hT"], W=[("ps", b + gi)],
                              inc=(k == 7))
                em.op("act", lambda b=b, jj=jj: A.activation(out=hh[:, jj, 0:n], in_=PS[:, b, 0:n], func=AF.Silu),
                      R=[("ps", b)], W=[tg + f"hh{jj}"])
                em.op("dve", lambda b=b, jj=jj: V.tensor_tensor(out=hh[:, jj, 0:n], in0=hh[:, jj, 0:n], in1=PS[:, b + 1, 0:n],
                                                               op=ALU.mult), R=[("ps", b + 1), tg + f"hh{jj}"], W=[tg + f"hh{jj}"])

        def down(j, s):
            pb = 4 + 2 * (s % 2)
            for h in range(2):
                for jj in range(NJ):
                    em.op("pe", lambda jj=jj, h=h, s=s, pb=pb: T.matmul(
                        PS[:, pb + h, :], lhsT=hh[:, jj, s * 128:(s + 1) * 128], rhs=wd[:, jj, h * 512:(h + 1) * 512],
                        start=(jj == 0), stop=(jj == NJ - 1)), R=[tg + "wd" + str(h), tg + f"hh{jj}"], W=[("ps", pb + h)], inc=(jj == NJ - 1))

        def eload(j, s):
            em.dma("sp", est[s % 2][:], rows(j, s)[0], W=[tg + f"est{s % 2}"])

        def epi(j, s):
            pb = 4 + 2 * (s % 2)
            C.epilogue(pb, est[s % 2][:, :], tg + f"est{s % 2}", Gx[:, :], tg + "Gx", ssq, rs, junk, tt, tg)
            em.dma("pool", rows(j, s)[1], est[s % 2][:], R=[tg + f"est{s % 2}"], W=[(outkey, j, s)])

        load(0)
        pro_a(0)
        pro_b(0)
        for j in range(ntile):
            ns = nsub_of(j)
            if j == NX:
                em.dma("sp", Gx[:], C.GROW[layer, 1, 1, :].partition_broadcast(128), W=[tg + "Gx"])
            gateup(j)
            if layer == 0:
                cv_emit(C, 2)
            if j + 1 < ntile:
                load(j + 1)
                pro_a(j + 1)
            eload(j, 0)
            eload(j, 1)
            down(j, 0)
            down(j, 1)
            if j + 1 < ntile:
                pro_b(j + 1)
            epi(j, 0)
            epi(j, 1)
            if ns == 4:
                eload(j, 2)
                eload(j, 3)
                down(j, 2)
                down(j, 3)
                epi(j, 2)
                epi(j, 3)


PHASES += [("B", phase_B), ("D1", phase_D1),
           ("D2", lambda C: phase_FFN(C, 0, C.X1, C.C1, C.X2, C.C2, "X1", "X2"))]


def phase_E(C):
    nc, em, PS = C.nc, C.em, C.PS
    V, A, G, T = nc.vector, nc.scalar, nc.gpsimd, nc.tensor
    with ExitStack() as ps_:
        sb = lambda n, s, d=F32: ps_.enter_context(nc.sbuf_tensor("E" + n, list(s), d))
        wq = sb("wq", [128, 8, 3 * D], BF16)
        xt = [sb(f"xt{i}", [128, 4, D]) for i in range(2)]
        hT = [sb(f"hT{i}", [128, 8, 512], BF16) for i in range(2)]
        qk = [sb(f"qk{i}", [128, 16, 512], BF16) for i in range(2)]
        vst = [sb(f"vst{i}", [128, 4, D], BF16) for i in range(2)]
        ssq = sb("ssq", [128, 4]); rs = sb("rs", [128, 4]); junk = sb("junk", [128, D], BF16)
        _wload_bf(C, wq, C.WBF["wqkv"].rearrange("(k p) n -> p k n", p=128), "Ewq", 3 * D, "cvwqkv")
        QTv = C.QT.rearrange("(c p) t -> p c t", p=128); KTv = C.KT.rearrange("(c p) t -> p c t", p=128)
        NTL = NLT + 1
        evc = [0]

        def nsubE(j):
            return 2 if j == NLT else 4

        def load(j):
            sl = j % 2
            if j < NLT:
                em.dma("sp", xt[sl][:], C.X2[j * 512:(j + 1) * 512, :].rearrange("(s p) d -> p s d", p=128), W=[f"Ext{sl}"])
            else:
                em.dma("sp", xt[sl][:, 0:2, :], C.C2.rearrange("(s p) d -> p s d", p=128), W=[f"Ext{sl}"])

        def pro_a(j):
            sl = j % 2
            C.prologue_a(xt[sl], f"Ext{sl}", nsubE(j), xt[sl], f"Ext{sl}", ssq, rs, junk, "E")

        def pro_b(j):
            sl = j % 2
            path = 1 if j == NLT else 0
            C.prologue_b(nsubE(j), xt[sl], f"Ext{sl}", C.MODC[:, 1, 0, 0, path, :], C.MODC[:, 1, 0, 1, path, :],
                         hT[sl], f"EhT{sl}", [0, 1])

        def qkE(j):
            sl = j % 2
            isctx = j == NLT
            n = nsubE(j) * 128
            for oc in (range(8, 16) if isctx else range(16)):
                b = 2 + oc % 2
                for k in range(8):
                    em.op("pe", lambda k=k, b=b, oc=oc, n=n, sl=sl: T.matmul(
                        PS[:, b, 0:n], lhsT=wq[:, k, oc * 128:(oc + 1) * 128], rhs=hT[sl][:, k, 0:n],
                        start=(k == 0), stop=(k == 7)), R=["Ewq" + str(oc // 8), f"EhT{sl}"], W=[("ps", b)], inc=(k == 7))
                if oc < 8:
                    em.op("act", lambda b=b, oc=oc, sl=sl, n=n: A.activation(out=qk[sl][:, oc, 0:n], in_=PS[:, b, 0:n],
                                                                             func=AF.Copy, scale=0.125),
                          R=[("ps", b)], W=[f"Eqk{sl}"])
                else:
                    em.op("dve", lambda b=b, oc=oc, sl=sl, n=n: V.tensor_copy(out=qk[sl][:, oc, 0:n], in_=PS[:, b, 0:n]),
                          R=[("ps", b)], W=[f"Eqk{sl}"])
            if not isctx:
                em.dma("pool", QTv[:, :, j * 512:(j + 1) * 512], qk[sl][:, 0:8, :], R=[f"Eqk{sl}"], W=[("QT", j)])
                em.dma("pool", KTv[:, :, j * 512:(j + 1) * 512], qk[sl][:, 8:16, :], R=[f"Eqk{sl}"], W=[("KT", j)])
            else:
                em.dma("pool", KTv[:, :, NL:NL + LC], qk[sl][:, 8:16, 0:LC], R=[f"Eqk{sl}"], W=[("KT", j)])

        def vE(j):
            sl = j % 2
            isctx = j == NLT
            nsub = nsubE(j)
            for s in range(nsub):
                pb = 4 + 2 * (s % 2)
                for h in range(2):
                    for k in range(8):
                        em.op("pe", lambda k=k, h=h, s=s, pb=pb, sl=sl: T.matmul(
                            PS[:, pb + h, :], lhsT=hT[sl][:, k, s * 128:(s + 1) * 128], rhs=wq[:, k, 2048 + h * 512:2048 + (h + 1) * 512],
                            start=(k == 0), stop=(k == 7)), R=["Ewq2", f"EhT{sl}"], W=[("ps", pb + h)], inc=(k == 7))
                evc[0] += 1
                src = PS[:, pb:pb + 2, :].rearrange("p a b -> p (a b)")
                if evc[0] % 2:
                    em.op("dve", lambda s=s, sl=sl, src=src: V.tensor_copy(out=vst[sl][:, s, :], in_=src),
                          R=[("ps", pb), ("ps", pb + 1)], W=[f"Evst{sl}"])
                else:
                    em.op("act", lambda s=s, sl=sl, src=src: A.copy(out=vst[sl][:, s, :], in_=src),
                          R=[("ps", pb), ("ps", pb + 1)], W=[f"Evst{sl}"])
            t0 = NL if isctx else j * 512
            em.dma("pool", C.VD[t0:t0 + nsub * 128, :].rearrange("(s p) d -> p s d", p=128), vst[sl][:, 0:nsub, :],
                   R=[f"Evst{sl}"], W=[("VD", j)])

        load(0)
        load(1)
        pro_a(0)
        pro_b(0)
        for j in range(NTL):
            qkE(j)
            if j + 2 < NTL:
                load(j + 2)
            if j + 1 < NTL:
                pro_a(j + 1)
            vE(j)
            if j + 1 < NTL:
                pro_b(j + 1)


def phase_F(C):
    nc, em, PS = C.nc, C.em, C.PS
    V, A, G, T = nc.vector, nc.scalar, nc.gpsimd, nc.tensor
    with ExitStack() as ps_:
        sb = lambda n, s, d=F32: ps_.enter_context(nc.sbuf_tensor("AT" + n, list(s), d))
        wo = sb("wo", [128, 8, D], BF16)
        KTb = [sb(f"KTb{i}", [128, 8, 1024], BF16) for i in range(2)]
        Vb = [sb(f"Vb{i}", [128, 8, 16, 65], BF16) for i in range(2)]
        QTb = sb("QTb", [128, 8, 512], BF16); xt = sb("xt", [128, 4, D])
        KTc = sb("KTc", [128, 8, LC], BF16); Vc = sb("Vc", [128, 2, 16, 65], BF16)
        BTi = sb("BTi", [128, 16, 5, 128], BF16); BTe = sb("BTe", [128, 16, 8, 128], BF16)
        PT = [sb(f"PT{i}", [128, 1280], BF16) for i in range(2)]
        Ot = sb("Ot", [128, D]); OTt = sb("OTt", [128, 8, 128], BF16); rden = sb("rden", [128, 4])
        Gx = sb("Gx", [128, D]); ssq = sb("ssq", [128, 4]); rs = sb("rs", [128, 4]); junk = sb("junk", [128, D], BF16)
        tt = sb("tt", [128, D])
        _wload_bf(C, wo, C.WBF["wona"].rearrange("(k p) n -> p k n", p=128), "Awo", D, "cvwona")
        em.dma("sp", Gx[:], C.GROW[1, 0, 0, :].partition_broadcast(128), W=["AGx"])
        em.dma("sp", BTi[:], C.BT_d, W=["ABTi"])
        KTv = C.KT.rearrange("(c p) t -> p c t", p=128); QTv = C.QT.rearrange("(c p) t -> p c t", p=128)
        em.dma("sp", KTc[:], KTv[:, :, NL:NL + LC], W=["AKTc"])
        for i in range(2):
            em.op("pool", lambda i=i: G.memset(Vb[i][:, :, :, 64:65], 1.0), W=[f"AVb{i}"])
        em.op("pool", lambda: G.memset(Vc[:, :, :, 64:65], 1.0), W=["AVc"])
        for c in range(2):
            em.dma("sp", Vc[:, c, :, 0:64], C.VD[NL + c * 128:NL + (c + 1) * 128, :].rearrange("p (h d) -> p h d", d=64), W=["AVc"])

        NB = 8

        def kbof(blk):
            return 0 if blk == 0 else (52 if blk == NB - 1 else 8 * blk - 4)

        def load(blk, sl):
            if blk == 0:
                pieces = [(0, 2, 68 * 64), (2, 6, 0)]
            else:
                pieces = [(0, 8, kbof(blk) * 64)]
            for (c0, ncn, t0) in pieces:
                em.dma("sp", KTb[sl][:, :, c0 * 128:(c0 + ncn) * 128], KTv[:, :, t0:t0 + ncn * 128], W=[f"AKTb{sl}"])
                for c in range(ncn):
                    tt0 = t0 + c * 128
                    em.dma("sp", Vb[sl][:, c0 + c, :, 0:64], C.VD[tt0:tt0 + 128, :].rearrange("p (h d) -> p h d", d=64),
                           W=[f"AVb{sl}"])
        load(0, 0)
        for blk in range(NB):
            sl = blk % 2
            r0 = 8 * blk
            edge = blk in (0, NB - 1)
            eb = 0 if blk == 0 else 1
            nloc = 8 if edge else 5
            nch = nloc + 2
            em.dma("sp", QTb[:], QTv[:, :, r0 * 64:r0 * 64 + 512], W=["AQTb"])
            em.dma("sp", xt[:], C.X2[blk * 512:(blk + 1) * 512, :].rearrange("(s p) d -> p s d", p=128), W=["Axt"])
            if blk + 1 < NB:
                load(blk + 1, 1 - sl)

            def sbase(h):
                return 0 if edge else 2 * (h % 2)

            def cpos(cl, h):
                if edge:
                    return (cl // 4, (cl % 4) * 128)
                return (2 * (h % 2) + cl // 4, (cl % 4) * 128)

            def qk(i, h):
                off = 0 if edge else i
                if edge and h == 0:
                    em.dma("sp", BTe[:], C.BTE_d[eb, i], W=["ABTe"])
                j = h // 2; e = h % 2
                p0, p1 = 64 * e, 64 * e + 64
                for cl in range(nch):
                    bk, co = cpos(cl, h)
                    o = PS[:, bk, co:co + 128]
                    if cl < nloc:
                        lt = KTb[sl][p0:p1, j, (off + cl) * 128:(off + cl + 1) * 128]
                    else:
                        lt = KTc[p0:p1, j, (cl - nloc) * 128:(cl - nloc + 1) * 128]
                    last = cl == nch - 1
                    em.op("pe", lambda o=o, lt=lt, j=j, p0=p0, p1=p1, i=i, cl=cl, last=last: T.matmul(
                        o, lhsT=lt, rhs=QTb[p0:p1, j, i * 128:(i + 1) * 128], start=(cl % 4 == 0),
                        stop=(edge and last)), R=[f"AKTb{sl}", "AKTc", "AQTb"], W=[("ps", bk)], inc=(edge and last))
                    if edge:
                        if cl in (3, 7):
                            em.op("pe", lambda h=h, bk=bk, cl=cl: T.matmul(
                                PS[:, bk, :], lhsT=C.identb[:, :], rhs=BTe[:, h, cl - 3:cl + 1, :].rearrange("p a b -> p (a b)"),
                                start=False, stop=True), R=["identb", "ABTe"], W=[("ps", bk)], inc=False)
                    else:
                        if cl == 3:
                            em.op("pe", lambda h=h, bk=bk: T.matmul(
                                PS[:, bk, :], lhsT=C.identb[:, :], rhs=BTi[:, h, 0:4, :].rearrange("p a b -> p (a b)"),
                                start=False, stop=True), R=["identb", "ABTi"], W=[("ps", bk)], inc=False)
                        if cl == 6:
                            em.op("pe", lambda h=h, bk=bk: T.matmul(
                                PS[:, bk, 0:128], lhsT=C.identb[:, :], rhs=BTi[:, h, 4, :],
                                start=False, stop=True), R=["identb", "ABTi"], W=[("ps", bk)], inc=True)

            def ex(i, h):
                b0 = sbase(h)
                nb = 3 if edge else 2
                src = PS[:, b0:b0 + nb, :].rearrange("p a b -> p (a b)")[:, 0:nch * 128]
                em.op("act", lambda src=src, h=h: A.activation(out=PT[h % 2][:, 0:nch * 128], in_=src, func=AF.Exp),
                      R=[("ps", b0 + q) for q in range(nb)], W=[f"APT{h % 2}"])

            def pv(i, h):
                off = 0 if edge else i
                ob = 4 + (h // 4) % 2
                so = (h % 4) * 128
                for c in range(nch):
                    rhs = Vb[sl][:, off + c, h, :] if c < nloc else Vc[:, c - nloc, h, :]
                    em.op("pe", lambda c=c, rhs=rhs, h=h, ob=ob, so=so: T.matmul(
                        PS[:, ob, so:so + 65], lhsT=PT[h % 2][:, c * 128:(c + 1) * 128], rhs=rhs,
                        start=(c == 0), stop=(c == nch - 1)), R=[f"APT{h % 2}", f"AVb{sl}", "AVc"], W=[("ps", ob)],
                          inc=(c == nch - 1))
                if h % 4 == 3:
                    em.op("dve", lambda ob=ob: V.reciprocal(out=rden[:, 0:4], in_=PS[:, ob, 64:512:128]),
                          R=[("ps", ob)], W=["Arden"])
                    for hh in range(4):
                        hd = h - 3 + hh
                        em.op("dve", lambda ob=ob, hh=hh, hd=hd: V.tensor_scalar(
                            out=Ot[:, hd * 64:(hd + 1) * 64], in0=PS[:, ob, hh * 128:hh * 128 + 64],
                            scalar1=rden[:, hh:hh + 1], scalar2=None, op0=ALU.mult), R=[("ps", ob), "Arden"], W=["AOt"])

            def fin(i):
                for k in range(8):
                    em.op("pe", lambda k=k: T.transpose(PS[:, 6 + k // 4, (k % 4) * 128:(k % 4 + 1) * 128],
                                                        Ot[:, k * 128:(k + 1) * 128], C.ident[:]),
                          R=["AOt", "ident"], W=[("ps", 6 + k // 4)], inc=(k % 4 == 3))
                for hb in range(2):
                    em.op("act" if hb else "dve",
                          (lambda hb=hb: A.copy(out=OTt[:, 4 * hb:4 * hb + 4, :].rearrange("p a b -> p (a b)"), in_=PS[:, 6 + hb, :])) if hb else
                          (lambda hb=hb: V.tensor_copy(out=OTt[:, 4 * hb:4 * hb + 4, :].rearrange("p a b -> p (a b)"), in_=PS[:, 6 + hb, :])),
                          R=[("ps", 6 + hb)], W=["AOTt"])
                for hf in range(2):
                    for k in range(8):
                        em.op("pe", lambda k=k, hf=hf: T.matmul(PS[:, 6 + hf, :], lhsT=OTt[:, k, :], rhs=wo[:, k, hf * 512:(hf + 1) * 512],
                                                                start=(k == 0), stop=(k == 7)),
                              R=["AOTt", "Awo0"], W=[("ps", 6 + hf)], inc=(k == 7))
                C.epilogue(6, xt[:, i, :], "Axt", Gx[:, :], "AGx", ssq, rs, junk, tt, "A")

            items = [(i, h) for i in range(4) for h in range(16)]
            qk(*items[0])
            for n_, (i, h) in enumerate(items):
                nxt = items[n_ + 1] if n_ + 1 < len(items) else None
                if edge:
                    ex(i, h)
                    if nxt:
                        qk(*nxt)
                else:
                    if nxt:
                        qk(*nxt)
                    ex(i, h)
                pv(i, h)
                if h == 15:
                    fin(i)
            em.dma("pool", C.X3[blk * 512:(blk + 1) * 512, :].rearrange("(s p) d -> p s d", p=128), xt[:], R=["Axt"], W=[("X3", blk)])


PHASES += [("E", phase_E), ("F", phase_F),
           ("G", lambda C: phase_FFN(C, 1, C.X3, None, C.out_d, None, "X3", "OUT", ntok=4096))]
```

```python
import math
from contextlib import ExitStack
import numpy as np
import ml_dtypes
import concourse.bass as bass
import concourse.mybir as mybir
from concourse.bass_utils import run_bass_kernel_spmd

F32 = mybir.dt.float32
BF16 = mybir.dt.bfloat16
AF = mybir.ActivationFunctionType
ALU = mybir.AluOpType
NPBF = ml_dtypes.bfloat16

D = 1024
S = 8192
LC = 256
NT = S + LC
FH = 2816
NJ = FH // 128
EPS = 1e-6
NCORES = 8
NL = 4608
NLT = NL // 512
LBASE = 3584
NEG = -30000.0
QENG = ("pool", "dve", "pool", "dve")


class Em:
    LIMIT = 30000
    NDS = 8

    def __init__(self, nc, es):
        self.nc = nc
        self.es = es
        self.eng = dict(pe=nc.tensor, act=nc.scalar, dve=nc.vector, pool=nc.gpsimd, sp=nc.sync)
        self.sem = {}
        self.semkey = {}
        self.cnt = {}
        self.nsem = 0
        for e in self.eng:
            self._newsem(e)
        self.waited = {e: {} for e in self.eng}
        self.lastw = {}
        self.readers = {}
        self.dsem = {}
        self.ndma = {}
        for q in ("sp", "pool", "act"):
            self.dsem[q] = [es.enter_context(nc.semaphore(f"d{q}{i}")) for i in range(self.NDS)]
            self.ndma[q] = 0
        self.bg = []

    def _newsem(self, e):
        self.nsem += 1
        self.sem[e] = self.es.enter_context(self.nc.semaphore(f"s{e}{self.nsem}"))
        self.semkey[e] = (e, self.nsem)
        self.cnt[e] = 0

    def _deps(self, engine, R, W):
        deps = {}
        for k in list(R) + list(W):
            t = self.lastw.get(k)
            if t is not None:
                if t[0] not in deps or deps[t[0]][2] < t[2]:
                    deps[t[0]] = t
        for k in W:
            for t in self.readers.get(k, {}).values():
                if t[0] not in deps or deps[t[0]][2] < t[2]:
                    deps[t[0]] = t
        e = self.eng[engine]
        for sk, t in deps.items():
            if engine == "pe" and t[3] == "pe":
                continue
            if self.waited[engine].get(sk, 0) >= t[2]:
                continue
            e.wait_ge(t[1], t[2])
            self.waited[engine][sk] = t[2]

    def _record(self, tok, R, W):
        for k in W:
            self.lastw[k] = tok
            self.readers[k] = {}
        for k in R:
            d = self.readers.setdefault(k, {})
            if tok[0] not in d or d[tok[0]][2] < tok[2]:
                d[tok[0]] = tok

    def op(self, engine, fn, R=(), W=(), inc=True):
        self._deps(engine, R, W)
        ins = fn()
        if inc:
            ins.then_inc(self.sem[engine], 1)
            self.cnt[engine] += 1
            tok = (self.semkey[engine], self.sem[engine], self.cnt[engine], engine)
            self._record(tok, R, W)
            if self.cnt[engine] >= self.LIMIT:
                self._newsem(engine)
        else:
            tok = (self.semkey[engine], self.sem[engine], self.cnt[engine] + 1, engine)
            self._record(tok, R, W)
        return ins

    def dma(self, q, out, in_, R=(), W=(), **kw):
        self._deps(q, R, W)
        i = self.ndma[q]
        self.ndma[q] += 1
        sem = self.dsem[q][i % self.NDS]
        rnd = i // self.NDS
        sk = ("dma", q, i % self.NDS)
        if rnd > 0 and self.waited[q].get(sk, 0) < 16 * rnd:
            self.eng[q].wait_ge(sem, 16 * rnd)
            self.waited[q][sk] = 16 * rnd
        self.eng[q].dma_start(out=out, in_=in_, **kw).then_inc(sem, 16)
        tok = (sk, sem, 16 * (rnd + 1), None)
        self._record(tok, R, W)

    def dma_bg(self, q, out, in_, R=(), W=(), **kw):
        self._deps(q, R, W)
        sem = self.es.enter_context(self.nc.semaphore(f"bg{len(self.bg)}"))
        self.bg.append(sem)
        self.eng[q].dma_start(out=out, in_=in_, **kw).then_inc(sem, 16)
        self._record((("bg", len(self.bg)), sem, 16, None), R, W)

    def finish(self):
        sp = self.eng["sp"]
        for sem in self.bg:
            sp.wait_ge(sem, 16)
        for q in self.dsem:
            n = self.ndma[q]
            for s in range(self.NDS):
                uses = (n - s + self.NDS - 1) // self.NDS if n > s else 0
                if uses > 0:
                    sp.wait_ge(self.dsem[q][s], 16 * uses)
        for e in ("pe", "act", "dve", "pool"):
            if self.cnt[e] > 0:
                sp.wait_ge(self.sem[e], self.cnt[e])


def _tables():
    t = {}
    t["ident"] = np.eye(128, dtype=np.float32)
    t["identb"] = np.eye(128, dtype=np.float32).astype(NPBF)
    a = np.arange(128, dtype=np.float64)
    ang = 2 * np.pi * np.outer(a, a) / 128.0
    t["T1"] = np.concatenate([np.cos(ang), -np.sin(ang)], axis=1).astype(NPBF)
    t2 = np.arange(64, dtype=np.float64)[:, None, None]
    k1 = np.arange(128, dtype=np.float64)[None, :, None]
    k2 = np.arange(64, dtype=np.float64)[None, None, :]
    ph = 2 * np.pi * (t2 * k2 / 64.0 + t2 * k1 / 8192.0)
    Mc, Ms = np.cos(ph), np.sin(ph)
    t["M2"] = np.concatenate([Ms, Mc, -Ms], axis=2).astype(NPBF)
    c = np.arange(64, dtype=np.float64)
    angc = 2 * np.pi * np.outer(c, c) / 64.0
    Cc, Sc = np.cos(angc), np.sin(angc)
    z = np.zeros((64, 64))
    Cbd = np.block([[Cc, z], [z, Cc]])
    Sbd = np.block([[Sc, z], [z, Sc]])
    sx = 1.0 / math.sqrt(8192.0 * 64.0)
    t["CSf"] = (np.concatenate([Cbd, Sbd], axis=1) * sx).astype(NPBF)
    sc_ = 1.0 / math.sqrt(256.0 * 64.0)
    t["CSc"] = (np.concatenate([Cbd, Sbd], axis=1) * sc_).astype(NPBF)
    p = np.arange(256, dtype=np.float64)
    angp = 2 * np.pi * np.outer(p, p) / 256.0
    T256 = np.concatenate([np.cos(angp), -np.sin(angp)], axis=1)
    t["T256"] = T256.reshape(2, 128, 512).transpose(1, 0, 2).copy().astype(NPBF)
    return t


_TABLES = None


def _bias_tables(rpb):
    H = 16
    out = np.full((5, 5 * 128, H, 128), NEG, dtype=np.float32)
    kr = np.arange(10)[:, None, None, None]
    kc = np.arange(64)[None, :, None, None]
    qr = np.arange(2)[None, None, :, None]
    qc = np.arange(64)[None, None, None, :]
    cs = np.clip(qc - 8, 0, 48)
    for vi, (gp, ks) in enumerate([(0, 0), (2, 0), (60, 56), (124, 118), (126, 118)]):
        gq = gp + qr
        rs = np.clip(gq - 4, 0, 120)
        gk = ks + kr
        valid = (gk >= rs) & (gk < rs + 8) & (kc >= cs) & (kc < cs + 16)
        dr = np.clip(gk - gq + 7, 0, 14)
        dc = np.clip(kc - qc + 15, 0, 30)
        valid, dr, dc = np.broadcast_arrays(valid, dr, dc)
        vals = rpb[:, dr, dc]
        vals = np.where(valid[None], vals, NEG)
        out[vi] = vals.transpose(1, 2, 0, 3, 4).reshape(640, H, 128)
    bt = out.reshape(5, 5, 128, H, 128).transpose(0, 2, 3, 1, 4)
    return np.ascontiguousarray(bt).astype(NPBF)


def _local_rows(h):
    if h == 0:
        return np.arange(72)
    return np.concatenate([np.arange(64, 128), np.arange(56, 64)])


def _edge_tables(rpb, h):
    H = 16
    lr = _local_rows(h)
    out = np.empty((2, 4, 128, H, 8, 128), dtype=NPBF)
    kc = np.arange(64)[None, :, None, None]
    qr = np.arange(2)[None, None, :, None]
    qc = np.arange(64)[None, None, None, :]
    cs = np.clip(qc - 8, 0, 48)
    keyrows = [np.concatenate([np.arange(68, 72), np.arange(0, 12)]), np.arange(52, 68)]
    for eb, r0 in enumerate([0, 56]):
        gk = lr[keyrows[eb]][:, None, None, None]
        for i in range(4):
            gq = lr[r0 + 2 * i + np.arange(2)][None, None, :, None]
            rs = np.clip(gq - 4, 0, 120)
            valid = (gk >= rs) & (gk < rs + 8) & (kc >= cs) & (kc < cs + 16)
            dr = np.clip(gk - gq + 7, 0, 14)
            dc = np.clip(kc - qc + 15, 0, 30)
            valid, dr, dc = np.broadcast_arrays(valid, dr, dc)
            vals = np.where(valid[None], rpb[:, dr, dc], NEG)
            t = vals.transpose(1, 2, 0, 3, 4).reshape(8, 128, H, 128)
            out[eb, i] = t.transpose(1, 2, 0, 3).astype(NPBF)
    return out


def _col(v, nchunk):
    return np.ascontiguousarray(np.asarray(v, np.float32).reshape(nchunk, 128).T)


def build(stop="all", debug=False):
    nc = bass.Bass("TRN2", target_bir_lowering=False)

    def din(name, shape, dt=F32):
        return nc.dram_tensor(name, list(shape), dt, kind="ExternalInput").ap()

    skind = "ExternalOutput" if debug else "Internal"

    def dscr(name, shape, dt):
        return nc.dram_tensor(name, list(shape), dt, kind=skind).ap()

    x_d = din("x", [S, D]); xloc_d = din("xloc", [NL, D]); hsel_d = din("hsel", [128, 2]); ctx_d = din("ctx", [LC, D]); ccols_d = din("ccols", [128, 16])
    wmod_d = din("w_mod", [2, D, 6 * D]); bmodc_d = din("bmodc", [128, 2, 48]); bmod_d = din("b_mod", [2, 6 * D])
    gcols_d = din("gcols", [128, 2, 2, 8]); gpm_d = din("g_post_mix", [2, D]); gpf_d = din("g_post_ffn", [2, D])
    win_d = din("w_in_ab", [D, 1536]); woab_d = din("w_out_ab", [D, D])
    wg_d = din("w_ffn_gate", [2, D, FH]); wu_d = din("w_ffn_up", [2, D, FH]); wd_d = din("w_ffn_down", [2, FH, D])
    wqkv_d = din("w_qkv_na", [D, 3 * D]); wona_d = din("w_out_na", [D, D])
    wbd_d = din("wbd", [2, 2, 4, 128, 128]); lcols_d = din("lcols", [128, 4, 2, 3]); convc_d = din("convc", [128, 4, 5]); cdiag_d = din("cdiag", [4, 4, 128, 128])
    ident_d = din("ident", [128, 128]); identb_d = din("identb", [128, 128], BF16)
    T1_d = din("T1", [128, 256], BF16); M2_d = din("M2", [64, 128, 192], BF16); CSf_d = din("CSf", [128, 256], BF16)
    CSc_d = din("CSc", [128, 256], BF16); T256_d = din("T256", [128, 2, 512], BF16)
    BT_d = din("BT", [128, 16, 5, 128], BF16); BTE_d = din("BTE", [2, 4, 128, 16, 8, 128], BF16)
    out_d = nc.dram_tensor("out", [4096, D], F32, kind="ExternalOutput").ap()

    UG = dscr("UG", [8, 128, S], F32)
    UGc = dscr("UGc", [8, 128, LC], F32)
    MIXT = dscr("MIXT", [D, NT], BF16)
    GROW = dscr("GROW", [2, 2, 2, D], F32)
    X1 = dscr("X1", [NL, D], F32); C1 = dscr("C1", [LC, D], F32)
    X2 = dscr("X2", [NL, D], F32); C2 = dscr("C2", [LC, D], F32)
    X3 = dscr("X3", [NL, D], F32)
    QT = dscr("QT", [D, NL], BF16); KT = dscr("KT", [D, NL + LC], BF16); VD = dscr("VD", [NL + LC, D], BF16)

    WBF = {"woab": dscr("woab_bf", [D, D], BF16), "wona": dscr("wona_bf", [D, D], BF16),
           "wqkv": dscr("wqkv_bf", [D, 3 * D], BF16),
           "wg": dscr("wg_bf", [2, D, FH], BF16), "wu": dscr("wu_bf", [2, D, FH], BF16), "wd": dscr("wd_bf", [2, FH, D], BF16)}

    with ExitStack() as es:
        em = Em(nc, es)

        def barrier():
            for e in ("pe", "act", "dve", "pool", "sp"):
                eng = em.eng[e]
                for f in ("pe", "act", "dve", "pool"):
                    if f != e and em.cnt[f] > 0 and em.waited[e].get(em.semkey[f], 0) < em.cnt[f]:
                        eng.wait_ge(em.sem[f], em.cnt[f]); em.waited[e][em.semkey[f]] = em.cnt[f]
                for q in em.dsem:
                    n = em.ndma[q]
                    for s_ in range(em.NDS):
                        uses = (n - s_ + em.NDS - 1) // em.NDS if n > s_ else 0
                        sk = ("dma", q, s_)
                        if uses > 0 and em.waited[e].get(sk, 0) < 16 * uses:
                            eng.wait_ge(em.dsem[q][s_], 16 * uses); em.waited[e][sk] = 16 * uses

        PS = es.enter_context(nc.psum_tensor("PS", [128, 8, 512], F32))
        ident = es.enter_context(nc.sbuf_tensor("ident_s", [128, 128], F32))
        identb = es.enter_context(nc.sbuf_tensor("identb_s", [128, 128], BF16))
        MODC = es.enter_context(nc.sbuf_tensor("MODC", [128, 2, 2, 2, 2, 8], F32))
        mhalf = es.enter_context(nc.sbuf_tensor("mhalf", [128, 8], F32))
        em.dma("sp", ident[:], ident_d, W=["ident"])
        em.dma("sp", identb[:], identb_d, W=["identb"])
        em.op("dve", lambda: nc.vector.memset(mhalf[:], -0.5), W=["mhalf"])

        V = nc.vector; A = nc.scalar; G = nc.gpsimd; T = nc.tensor

        def pbank(b):
            return ("ps", b)

        def rstd_from_ssq(ssq, rs, n, tag):
            em.op("dve", lambda: V.tensor_scalar(out=rs[:, 0:n], in0=ssq[:, 0:n], scalar1=1.0 / D, scalar2=EPS,
                                                 op0=ALU.mult, op1=ALU.add), R=[tag + "ssq"], W=[tag + "rs"])
            em.op("pool", lambda: G.tensor_tensor(out=rs[:, 0:n], in0=rs[:, 0:n], in1=mhalf[:, 0:n], op=ALU.pow),
                  R=[tag + "rs", "mhalf"], W=[tag + "rs"])

        def prologue_a(xt, xkey, nsub, xs, xskey, ssq, rs, junk, tag):
            for s in range(nsub):
                em.op("act", lambda s=s: A.activation(out=junk[:, :], in_=xt[:, s, :], func=AF.Square,
                                                      accum_out=ssq[:, s:s + 1]),
                      R=[xkey], W=[tag + "junk", tag + "ssq"])
            rstd_from_ssq(ssq, rs, nsub, tag)
            for s in range(nsub):
                em.op("act", lambda s=s: A.activation(out=xs[:, s, :], in_=xt[:, s, :], func=AF.Copy,
                                                      scale=rs[:, s:s + 1]),
                      R=[xkey, tag + "rs"], W=[xskey])

        def prologue_b(nsub, xs, xskey, Acol, Bcol, hT, hkey, tb):
            for k in range(8):
                b = tb[k % len(tb)]
                for s in range(nsub):
                    em.op("pe", lambda s=s, k=k, b=b: T.transpose(PS[:, b, s * 128:(s + 1) * 128],
                                                                  xs[:, s, k * 128:(k + 1) * 128], ident[:]),
                          R=[xskey, "ident"], W=[pbank(b)], inc=(s == nsub - 1))
                em.op("act", lambda k=k, b=b: A.activation(out=hT[:, k, 0:nsub * 128], in_=PS[:, b, 0:nsub * 128],
                                                           func=AF.Identity, scale=Acol[:, k:k + 1],
                                                           bias=Bcol[:, k:k + 1]),
                      R=[pbank(b), "MODC"], W=[hkey])

        def prologue(xt, xkey, nsub, xs, xskey, Acol, Bcol, hT, hkey, ssq, rs, junk, tag, tb):
            prologue_a(xt, xkey, nsub, xs, xskey, ssq, rs, junk, tag)
            prologue_b(nsub, xs, xskey, Acol, Bcol, hT, hkey, tb)

        def epilogue(psb, xsub, xkey, Gb, gkey, ssq, rs, junk, tt, tag):
            yv = PS[:, psb:psb + 2, :]
            em.op("act", lambda: A.activation(out=junk[:, :], in_=yv, func=AF.Square, accum_out=ssq[:, 0:1]),
                  R=[pbank(psb), pbank(psb + 1)], W=[tag + "junk", tag + "ssq"])
            rstd_from_ssq(ssq, rs, 1, tag)
            em.op("dve", lambda: V.tensor_tensor(out=tt[:, :], in0=yv, in1=Gb, op=ALU.mult),
                  R=[pbank(psb), pbank(psb + 1), gkey], W=[tag + "tt"])
            em.op("dve", lambda: V.scalar_tensor_tensor(out=xsub, in0=tt[:, :], scalar=rs[:, 0:1], in1=xsub,
                                                        op0=ALU.mult, op1=ALU.add),
                  R=[tag + "tt", tag + "rs", xkey], W=[xkey])

        with ExitStack() as pes:
            sb = lambda n, s, d=F32: pes.enter_context(nc.sbuf_tensor(n, list(s), d))
            cc = sb("cc", [128, 16]); scc = sb("scc", [128, 16]); rhs2 = sb("rhs2", [128, 8, 2], BF16)
            bmc = sb("bmc", [128, 2, 48]); bm1 = sb("bm1", [128, 2, 48]); gco = sb("gco", [128, 2, 2, 8])
            bmrow = sb("bmrow", [2, 2, 2, D]); grow = sb("grow", [2, 2, 2, D]); grt = sb("grt", [2, 2, 2, D])
            wm = [sb(f"wm{i}", [128, 8, 512], BF16) for i in range(3)]
            em.dma("sp", cc[:], ccols_d, W=["cc"])
            em.dma("sp", bmc[:], bmodc_d, W=["bmc"])
            em.dma("sp", gco[:], gcols_d, W=["gco"])
            for l in range(2):
                for w_, (src, off) in enumerate([(bmod_d, 2 * D), (bmod_d, 5 * D)]):
                    em.dma("sp", bmrow[:, l, w_, :], src[l, off:off + D].partition_broadcast(2), W=["bmrow"])
                em.dma("sp", grow[:, l, 0, :], gpm_d[l, :].partition_broadcast(2), W=["grow"])
                em.dma("sp", grow[:, l, 1, :], gpf_d[l, :].partition_broadcast(2), W=["grow"])
            em.op("act", lambda: A.activation(out=scc[:], in_=cc[:], func=AF.Silu), R=["cc"], W=["scc"])
            em.op("dve", lambda: V.tensor_copy(out=rhs2[:, :, 0], in_=scc[:, 0:8]), R=["scc"], W=["rhs2"])
            em.op("dve", lambda: V.tensor_copy(out=rhs2[:, :, 1], in_=scc[:, 8:16]), R=["scc"], W=["rhs2"])
            em.op("dve", lambda: V.tensor_scalar(out=bm1[:], in0=bmc[:], scalar1=1.0, scalar2=None, op0=ALU.add),
                  R=["bmc"], W=["bm1"])
            it = 0
            for l in range(2):
                wsrc = wmod_d[l].rearrange("(k p) n -> p k n", p=128)
                for nb in range(12):
                    slot = it % 3; it += 1
                    wt = wm[slot]; wk = f"wm{slot}"
                    em.dma("pool", wt[:], wsrc[:, :, nb * 512:(nb + 1) * 512], W=[wk])
                    v = nb // 2; half = nb % 2
                    b = it % 4
                    if v in (2, 5):
                        w_ = 0 if v == 2 else 1
                        for k in range(8):
                            em.op("pe", lambda k=k, b=b, wt=wt: T.matmul(PS[0:2, b, :], lhsT=rhs2[:, k, :], rhs=wt[:, k, :],
                                                                        start=(k == 0), stop=(k == 7)),
                                  R=["rhs2", wk], W=[pbank(b)], inc=(k == 7))
                        dst = grt[:, l, w_, half * 512:(half + 1) * 512]
                        em.op("dve", lambda b=b, dst=dst, l=l, w_=w_, half=half: V.tensor_tensor(
                            out=dst, in0=PS[0:2, b, :], in1=bmrow[:, l, w_, half * 512:(half + 1) * 512], op=ALU.add),
                              R=[pbank(b), "bmrow"], W=["grt"])
                        em.op("dve", lambda dst=dst, l=l, w_=w_, half=half: V.tensor_tensor(
                            out=dst, in0=dst, in1=grow[:, l, w_, half * 512:(half + 1) * 512], op=ALU.mult),
                              R=["grt", "grow"], W=["grt"])
                        if half == 1:
                            em.dma("sp", GROW[l, w_, :, :], grt[:, l, w_, :], R=["grt"], W=[("GROW", l, w_)])
                    else:
                        sub = 0 if v < 2 else 1
                        isA = v in (1, 4)
                        for m in range(4):
                            ch = half * 4 + m
                            for k in range(8):
                                em.op("pe", lambda k=k, b=b, m=m, wt=wt: T.matmul(
                                    PS[:, b, 2 * m:2 * m + 2], lhsT=wt[:, k, m * 128:(m + 1) * 128], rhs=rhs2[:, k, :],
                                    start=(k == 0), stop=(k == 7)), R=["rhs2", wk], W=[pbank(b)], inc=(k == 7))
                            dst = MODC[:, l, sub, 0 if isA else 1, :, ch]
                            if isA:
                                em.op("dve", lambda b=b, m=m, dst=dst, l=l, v=v, ch=ch, sub=sub: V.tensor_scalar(
                                    out=dst, in0=PS[:, b, 2 * m:2 * m + 2], scalar1=bm1[:, l, v * 8 + ch:v * 8 + ch + 1],
                                    scalar2=gco[:, l, sub, ch:ch + 1], op0=ALU.add, op1=ALU.mult),
                                      R=[pbank(b), "bm1", "gco"], W=["MODC"])
                            else:
                                em.op("dve", lambda b=b, m=m, dst=dst, l=l, v=v, ch=ch: V.tensor_scalar(
                                    out=dst, in0=PS[:, b, 2 * m:2 * m + 2], scalar1=bmc[:, l, v * 8 + ch:v * 8 + ch + 1],
                                    scalar2=None, op0=ALU.add), R=[pbank(b), "bmc"], W=["MODC"])
            if debug:
                MODCd = nc.dram_tensor("MODCd", [128, 128], F32, kind="ExternalOutput").ap()
                em.dma("sp", MODCd, MODC[:].rearrange("p a b c d e -> p (a b c d e)"), R=["MODC"], W=["MODCd"])
            barrier()
        if stop == "0":
            em.finish()
            return nc
        C = type("C", (), {})()
        C.__dict__.update(locals())
        for name, fn in PHASES:
            fn(C)
            barrier()
            if stop == name:
                if hasattr(C, 'es_mix'):
                    C.es_mix.close()
                break
        em.finish()
    return nc


PHASES = []


def _host_inputs(inp, b):
    global _TABLES
    if _TABLES is None:
        _TABLES = _tables()
    f = lambda a: np.ascontiguousarray(np.asarray(a, dtype=np.float32))
    m = {}
    b, h = b // 2, b % 2
    m["x"] = f(inp["x"][b]); m["ctx"] = f(inp["ctx"][b])
    m["xloc"] = f(inp["x"][b].reshape(128, 64, D)[_local_rows(h)].reshape(NL, D))
    m["hsel"] = np.tile(np.array([[1.0 - h, float(h)]], np.float32), (128, 1))
    m["ccols"] = np.concatenate([_col(inp["c"][b], 8), _col(inp["c_ctx"], 8)], axis=1)
    m["w_mod"] = f(inp["w_mod"]); m["b_mod"] = f(inp["b_mod"])
    m["bmodc"] = np.ascontiguousarray(np.stack([_col(inp["b_mod"][l], 48) for l in range(2)], axis=1))
    m["gcols"] = np.ascontiguousarray(np.stack(
        [np.stack([_col(inp["g_pre_mix"][l], 8), _col(inp["g_pre_ffn"][l], 8)], axis=1) for l in range(2)], axis=1))
    m["g_post_mix"] = f(inp["g_post_mix"]); m["g_post_ffn"] = f(inp["g_post_ffn"])
    m["w_in_ab"] = f(inp["w_in_ab"][0]); m["w_out_ab"] = f(inp["w_out_ab"][0])
    m["w_ffn_gate"] = f(inp["w_ffn_gate"]); m["w_ffn_up"] = f(inp["w_ffn_up"]); m["w_ffn_down"] = f(inp["w_ffn_down"])
    m["w_qkv_na"] = f(inp["w_qkv_na"][0]); m["w_out_na"] = f(inp["w_out_na"][0])
    wbd = np.zeros((2, 2, 4, 128, 128), np.float32)
    for gi, key in enumerate(["lru_w_a", "lru_w_i"]):
        w = np.asarray(inp[key][0], np.float32)
        for d in range(2):
            for c in range(4):
                wbd[gi, d, c, 0:64, 0:64] = w[d, 2 * c]
                wbd[gi, d, c, 64:128, 64:128] = w[d, 2 * c + 1]
    m["wbd"] = wbd
    lc = np.zeros((128, 4, 2, 3), np.float32)
    for d in range(2):
        lc[:, :, d, 0] = _col(inp["lru_b_a"][0][d], 4)
        lc[:, :, d, 1] = _col(inp["lru_b_i"][0][d], 4)
        lc[:, :, d, 2] = _col(inp["lru_lam"][0][d], 4)
    m["lcols"] = lc
    cv = np.zeros((128, 4, 5), np.float32)
    for k in range(4):
        cv[:, :, k] = _col(inp["conv_w"][0][k], 4)
    cv[:, :, 4] = _col(inp["conv_b"][0], 4)
    m["convc"] = cv
    cd = np.zeros((4, 4, 128, 128), np.float32)
    ii = np.arange(128)
    for c in range(4):
        for k in range(4):
            cd[c, k, ii, ii] = cv[:, c, k]
    m["cdiag"] = cd
    for k in ("ident", "identb", "T1", "M2", "CSf", "CSc", "T256"):
        m[k] = _TABLES[k]
    rpb = np.asarray(inp["rpb_na"][0], np.float32)
    m["BT"] = np.ascontiguousarray(_bias_tables(rpb)[2])
    m["BTE"] = _edge_tables(rpb, h)
    return m


_NC_CACHE = {}


def kernel(**inputs):
    if "full" not in _NC_CACHE:
        _NC_CACHE["full"] = build()
    nc = _NC_CACHE["full"]
    in_maps = [_host_inputs(inputs, b) for b in range(NCORES)]
    res = run_bass_kernel_spmd(nc, in_maps, core_ids=list(range(NCORES)))
    out = np.empty((4, S, D), np.float32)
    for c in range(NCORES):
        b, h = c // 2, c % 2
        o = np.asarray(res.results[c]["out"], dtype=np.float32)
        out[b, 4096 * h:4096 * (h + 1)] = o
    return out


def _wload(C, dst, src_view, key, nk, ncols, step=512):
    for c0 in range(0, ncols, step):
        c1 = min(ncols, c0 + step)
        C.em.dma("pool", dst[:, :, c0:c1], src_view[:, :, c0:c1], W=[key])


def _wload_bf(C, dst, src_view, key, ncols, srckey, step=1024):
    cv_emit(C, 0, need=srckey)
    for bi, c0 in enumerate(range(0, ncols, step)):
        c1 = min(ncols, c0 + step)
        C.em.dma("sp", dst[:, :, c0:c1], src_view[:, :, c0:c1], R=C.cvkeys[srckey], W=[key + str(bi)])


def preconvert_init(C):
    jobs = [("woab", C.woab_d, C.WBF["woab"]), ("wg0", C.wg_d[0], C.WBF["wg"][0]), ("wu0", C.wu_d[0], C.WBF["wu"][0]),
            ("wd0", C.wd_d[0], C.WBF["wd"][0]), ("wqkv", C.wqkv_d, C.WBF["wqkv"]), ("wona", C.wona_d, C.WBF["wona"]),
            ("wg1", C.wg_d[1], C.WBF["wg"][1]), ("wu1", C.wu_d[1], C.WBF["wu"][1]), ("wd1", C.wd_d[1], C.WBF["wd"][1])]
    C.cvkeys = {}
    C.cvq = []
    for key, src, dst in jobs:
        rows, cols = src.shape
        rstep = 512 if rows % 512 == 0 else 704
        C.cvkeys["cv" + key] = []
        for r0 in range(0, rows, rstep):
            for c0 in range(0, cols, 1024):
                c1 = min(cols, c0 + 1024)
                k_ = f"cv{key}_{r0}_{c0}"
                C.cvkeys["cv" + key].append(k_)
                C.cvq.append(("cv" + key, k_, dst[r0:r0 + rstep, c0:c1], src[r0:r0 + rstep, c0:c1]))


def cv_emit(C, n=1, need=None):
    while C.cvq and (n > 0 or (need is not None and any(j[0] == need for j in C.cvq))):
        big, k_, dst, src = C.cvq.pop(0)
        C.em.dma_bg("pool", dst, src, W=[k_])
        n -= 1


def phase_A(C):
    nc, em, PS = C.nc, C.em, C.PS
    V, A, G, T = nc.vector, nc.scalar, nc.gpsimd, nc.tensor
    pes = C.es_mix = ExitStack()
    preconvert_init(C)
    C.F = pes.enter_context(nc.sbuf_tensor("Fbuf", [128, 64, 512], BF16))
    C.fTc = pes.enter_context(nc.sbuf_tensor("fTc", [128, 4, LC], BF16))
    F, fTc = C.F, C.fTc
    with ExitStack() as ps_:
        sb = lambda n, s, d=F32: ps_.enter_context(nc.sbuf_tensor(n, list(s), d))
        win = sb("win", [128, 8, 1536], BF16)
        xt = [sb(f"Axt{i}", [128, 4, D]) for i in range(2)]
        hT = [sb(f"AhT{i}", [128, 8, 512], BF16) for i in range(2)]
        ugst = [sb(f"Aug{i}", [128, 8, 512]) for i in range(2)]
        ssq = sb("Assq", [128, 4]); rs = sb("Ars", [128, 4]); junk = sb("Ajunk", [128, D], BF16)
        _wload(C, win, C.win_d.rearrange("(k p) n -> p k n", p=128), "win", 8, 1536)
        xsrc = C.x_d.rearrange("(t1 t2) d -> t1 t2 d", t2=64)
        Acol = C.MODC[:, 0, 0, 0, 0, :]; Bcol = C.MODC[:, 0, 0, 1, 0, :]
        AcolC = C.MODC[:, 0, 0, 0, 1, :]; BcolC = C.MODC[:, 0, 0, 1, 1, :]
        evc = [0]

        def loadA(j):
            sl = j % 2
            if j < 16:
                em.dma("sp", xt[sl][:], xsrc[:, 4 * j:4 * (j + 1), :], W=[f"Axt{sl}"])
            elif j == 16:
                em.dma("sp", xt[sl][:, 0:2, :], C.ctx_d.rearrange("(s p) d -> p s d", p=128), W=[f"Axt{sl}"])

        def nsubA(j):
            return 2 if j == 16 else 4

        def pro_a(j):
            sl = j % 2
            C.prologue_a(xt[sl], f"Axt{sl}", nsubA(j), xt[sl], f"Axt{sl}", ssq, rs, junk, "A")

        def pro_b(j):
            sl = j % 2
            isctx = j == 16
            C.prologue_b(nsubA(j), xt[sl], f"Axt{sl}", AcolC if isctx else Acol, BcolC if isctx else Bcol,
                         hT[sl], f"AhT{sl}", [0, 1])

        def ugA(j):
            sl = j % 2
            isctx = j == 16
            n = nsubA(j) * 128
            for oc in range(8):
                b = 2 + oc % 3
                for k in range(8):
                    em.op("pe", lambda k=k, b=b, oc=oc, sl=sl, n=n: T.matmul(
                        PS[:, b, 0:n], lhsT=win[:, k, oc * 128:(oc + 1) * 128], rhs=hT[sl][:, k, 0:n],
                        start=(k == 0), stop=(k == 7)), R=["win", f"AhT{sl}"], W=[("ps", b)], inc=(k == 7))
                evc[0] += 1
                if evc[0] % 2:
                    em.op("dve", lambda b=b, oc=oc, sl=sl, n=n: V.tensor_copy(out=ugst[sl][:, oc, 0:n], in_=PS[:, b, 0:n]),
                          R=[("ps", b)], W=[f"Aug{sl}"])
                else:
                    em.op("act", lambda b=b, oc=oc, sl=sl, n=n: A.copy(out=ugst[sl][:, oc, 0:n], in_=PS[:, b, 0:n]),
                          R=[("ps", b)], W=[f"Aug{sl}"])
            if isctx:
                em.dma("pool", C.UGc.rearrange("c p n -> p c n"), ugst[sl][:, :, 0:LC], R=[f"Aug{sl}"], W=["UGc"])
            else:
                em.dma("pool", C.UG[:, :, j * 512:(j + 1) * 512].rearrange("c p n -> p c n"), ugst[sl][:],
                       R=[f"Aug{sl}"], W=[("UG", j)])

        def fA(j):
            sl = j % 2
            if j == 16:
                for fc in range(4):
                    b = 5 + fc % 3
                    for k in range(8):
                        em.op("pe", lambda k=k, b=b, fc=fc, sl=sl: T.matmul(
                            PS[:, b, 0:LC], lhsT=win[:, k, 1024 + fc * 128:1024 + (fc + 1) * 128], rhs=hT[sl][:, k, 0:LC],
                            start=(k == 0), stop=(k == 7)), R=["win", f"AhT{sl}"], W=[("ps", b)], inc=(k == 7))
                    em.op("dve", lambda b=b, fc=fc: V.tensor_copy(out=fTc[:, fc, :], in_=PS[:, b, 0:LC]),
                          R=[("ps", b)], W=["fTc"])
            else:
                for s in range(4):
                    b = 5 + s % 3
                    for k in range(8):
                        em.op("pe", lambda k=k, b=b, s=s, sl=sl: T.matmul(
                            PS[:, b, :], lhsT=hT[sl][:, k, s * 128:(s + 1) * 128], rhs=win[:, k, 1024:1536],
                            start=(k == 0), stop=(k == 7)), R=["win", f"AhT{sl}"], W=[("ps", b)], inc=(k == 7))
                    evc[0] += 1
                    if evc[0] % 2:
                        em.op("dve", lambda b=b, s=s, j=j: V.tensor_copy(out=F[:, 4 * j + s, :], in_=PS[:, b, :]),
                              R=[("ps", b)], W=["F"])
                    else:
                        em.op("act", lambda b=b, s=s, j=j: A.copy(out=F[:, 4 * j + s, :], in_=PS[:, b, :]),
                              R=[("ps", b)], W=["F"])

        loadA(0)
        loadA(1)
        pro_a(0)
        pro_b(0)
        for j in range(17):
            ugA(j)
            cv_emit(C, 1)
            if j + 2 < 17:
                loadA(j + 2)
            if j + 1 < 17:
                pro_a(j + 1)
            fA(j)
            if j + 1 < 17:
                pro_b(j + 1)


def phase_C(C):
    nc, em, PS, F, fTc = C.nc, C.em, C.PS, C.F, C.fTc
    V, A, G, T = nc.vector, nc.scalar, nc.gpsimd, nc.tensor
    with ExitStack() as ps_:
        sb = lambda n, s, d=F32: ps_.enter_context(nc.sbuf_tensor(n, list(s), d))
        T1 = sb("T1s", [128, 256], BF16); M2 = sb("M2s", [64, 128, 192], BF16); CSf = sb("CSfs", [128, 256], BF16)
        CSc = sb("CScs", [128, 256], BF16); T256 = sb("T256s", [128, 2, 512], BF16)
        Ast = sb("Ast", [64, 64, 256], BF16); Y = sb("Ybuf", [128, 2, S], BF16)
        fst = [sb(f"fst{i}", [128, 2048], BF16) for i in range(2)]
        Gc = sb("Gcb", [128, 2, 256], BF16); fcs = sb("fcs", [128, LC], BF16)
        for dst, src, k in ((T1, C.T1_d, "T1"), (M2, C.M2_d, "M2"), (CSf, C.CSf_d, "CSf"), (CSc, C.CSc_d, "CSc"),
                            (T256, C.T256_d, "T256")):
            em.dma("sp", dst[:], src, W=[k])
        ev = 0
        for cc in range(4):
            for hc in range(2):
                for g4 in range(16):
                    b = 2 * (g4 % 2)
                    for q in range(4):
                        ch = cc * 128 + hc * 64 + g4 * 4 + q
                        em.op("pe", lambda b=b, q=q, ch=ch: T.matmul(
                            PS[0:64, b + q // 2, (q % 2) * 256:(q % 2) * 256 + 256], lhsT=F[:, :, ch], rhs=T1[:, :],
                            start=True, stop=True), R=["F", "T1"], W=[("ps", b), ("ps", b + 1)], inc=(q == 3))
                    ev += 1
                    dst = Ast[:, g4 * 4:(g4 + 1) * 4, :].rearrange("p a b -> p (a b)")
                    src = PS[0:64, b:b + 2, :].rearrange("p a b -> p (a b)")
                    if ev % 2:
                        em.op("dve", lambda dst=dst, src=src: V.tensor_copy(out=dst, in_=src),
                              R=[("ps", b), ("ps", b + 1)], W=["Ast"])
                    else:
                        em.op("act", lambda dst=dst, src=src: A.copy(out=dst, in_=src),
                              R=[("ps", b), ("ps", b + 1)], W=["Ast"])
                for kb in range(32):
                    b = 4 + kb % 2
                    for q in range(4):
                        k1 = kb * 4 + q
                        o = PS[hc * 64:(hc + 1) * 64, b, q * 128:(q + 1) * 128]
                        em.op("pe", lambda o=o, k1=k1: T.matmul(o, lhsT=Ast[:, :, k1], rhs=M2[:, k1, 64:192],
                                                                start=True, stop=False),
                              R=["Ast", "M2"], W=[("ps", b)], inc=False)
                        em.op("pe", lambda o=o, k1=k1: T.matmul(o, lhsT=Ast[:, :, 128 + k1], rhs=M2[:, k1, 0:128],
                                                                start=False, stop=True),
                              R=["Ast", "M2"], W=[("ps", b)], inc=(q == 3))
                    ev += 1
                    src = PS[hc * 64:(hc + 1) * 64, b, :].rearrange("p (k r c) -> p r c k", k=4, r=2)
                    dst = Y[hc * 64:(hc + 1) * 64, :, :].rearrange("p r (c k) -> p r c k", k=128)[:, :, :, kb * 4:(kb + 1) * 4]
                    if ev % 2:
                        em.op("dve", lambda dst=dst, src=src: V.tensor_copy(out=dst, in_=src), R=[("ps", b)], W=["Y"])
                    else:
                        em.op("act", lambda dst=dst, src=src: A.copy(out=dst, in_=src), R=[("ps", b)], W=["Y"])
            for tl in range(16):
                b = 6 + tl % 2
                em.op("pe", lambda b=b, tl=tl: T.matmul(PS[:, b, :], lhsT=CSf[:, 0:128], rhs=Y[:, 0, tl * 512:(tl + 1) * 512],
                                                        start=True, stop=False), R=["CSf", "Y"], W=[("ps", b)], inc=False)
                em.op("pe", lambda b=b, tl=tl: T.matmul(PS[:, b, :], lhsT=CSf[:, 128:256], rhs=Y[:, 1, tl * 512:(tl + 1) * 512],
                                                        start=False, stop=True), R=["CSf", "Y"], W=[("ps", b)], inc=True)
                fs = (tl // 4) % 2
                em.op("act", lambda b=b, tl=tl, fs=fs: A.copy(out=fst[fs][:, (tl % 4) * 512:(tl % 4 + 1) * 512], in_=PS[:, b, :]),
                      R=[("ps", b)], W=[f"fst{fs}"])
                if tl % 4 == 3:
                    t0 = (tl // 4) * 2048
                    em.dma("pool", C.MIXT[512 + cc * 128:512 + (cc + 1) * 128, t0:t0 + 2048], fst[fs][:],
                           R=[f"fst{fs}"], W=[("MIXT", 4 + cc)])
            for tc in range(2):
                em.op("pe", lambda tc=tc, cc=cc: T.matmul(PS[:, 0, 0:256], lhsT=fTc[:, cc, tc * 128:(tc + 1) * 128], rhs=CSc[:, :],
                                                          start=True, stop=True), R=["fTc", "CSc"], W=[("ps", 0)])
                em.op("dve", lambda tc=tc: V.tensor_copy(out=Gc[:, tc, :], in_=PS[:, 0, 0:256]), R=[("ps", 0)], W=["Gc"])
            for i_, (tc, part) in enumerate([(0, 0), (0, 1), (1, 0), (1, 1)]):
                em.op("pe", lambda i_=i_, tc=tc, part=part: T.matmul(
                    PS[:, 1, 0:256], lhsT=Gc[:, tc, part * 128:(part + 1) * 128], rhs=T256[:, tc, part * 256:(part + 1) * 256],
                    start=(i_ == 0), stop=(i_ == 3)), R=["Gc", "T256"], W=[("ps", 1)], inc=(i_ == 3))
            em.op("dve", lambda: V.tensor_copy(out=fcs[:, :], in_=PS[:, 1, 0:256]), R=[("ps", 1)], W=["fcs"])
            em.dma("pool", C.MIXT[512 + cc * 128:512 + (cc + 1) * 128, S:NT], fcs[:], R=["fcs"], W=[("MIXTc", 4 + cc)])
    C.es_mix.close()


PHASES += [("A", phase_A), ("C", phase_C)]


def phase_B(C):
    nc, em, PS = C.nc, C.em, C.PS
    V, A, G, T = nc.vector, nc.scalar, nc.gpsimd, nc.tensor
    TW = 2048
    with ExitStack() as ps_:
        sb = lambda n, s, d=F32: ps_.enter_context(nc.sbuf_tensor(n, list(s), d))
        bufA = sb("bufA", [128, NT]); bufB = sb("bufB", [128, 8460]); uc = sb("ucb_", [128, NT]); ucb = sb("ucbb", [128, NT], BF16)
        rt = sb("rt", [128, TW])
        it_ = [sb(f"it{i}", [128, TW]) for i in range(2)]; at = [sb(f"at{i}", [128, TW]) for i in range(2)]
        st = [sb(f"st{i}", [128, TW]) for i in range(2)]
        gtmp = [sb(f"gtmp{i}", [128, 1024]) for i in range(2)]
        wbd = sb("wbds", [128, 16, 128], BF16)
        cdg = sb("cdg", [128, 16, 128])
        lco = sb("lco", [128, 4, 2, 3]); cvc = sb("cvc", [128, 4, 5]); cA = sb("cA", [128, 4, 2]); carry = sb("carry", [128, 2])
        em.dma("pool", wbd[:], C.wbd_d.rearrange("g d c p n -> p (g d c) n"), W=["wbd"])
        em.dma("sp", cdg[:], C.cdiag_d.rearrange("c k p n -> p (c k) n"), W=["cdg"])
        em.dma("sp", lco[:], C.lcols_d, W=["lco"])
        em.dma("sp", cvc[:], C.convc_d, W=["cvc"])
        em.op("act", lambda: A.activation(out=cA[:], in_=lco[:, :, :, 2], func=AF.Exp, scale=-1.0), R=["lco"], W=["cA"])
        em.op("act", lambda: A.activation(out=cA[:], in_=cA[:], func=AF.Ln, bias=1.0), R=["cA"], W=["cA"])
        em.op("dve", lambda: V.tensor_scalar(out=cA[:], in0=cA[:], scalar1=-8.0, scalar2=None, op0=ALU.mult), R=["cA"], W=["cA"])
        em.op("dve", lambda: V.memset(bufB[:], 0.0), W=["bufB"])
        XO = 8200
        tiles = [(S, NT)] + [(i * TW, (i + 1) * TW) for i in range(4)]
        tix = 0
        for c in range(4):
            cv_emit(C, 2)
            em.dma("sp", bufA[:, 0:S], C.UG[c], W=["bufA"])
            em.dma("sp", bufA[:, S:NT], C.UGc[c], W=["bufA"])
            for qd in range(4):
                o_ = bufB[:, 2 + qd * 2048:2 + (qd + 1) * 2048].rearrange("p (a b) -> p a b", b=64)
                i_ = bufA[:, 0:S].rearrange("p (b a) -> p a b", a=128)[:, qd * 32:(qd + 1) * 32, :]
                eng = QENG[qd]
                fn = {"act": (lambda o_=o_, i_=i_: A.copy(out=o_, in_=i_)),
                      "dve": (lambda o_=o_, i_=i_: V.tensor_copy(out=o_, in_=i_)),
                      "pool": (lambda o_=o_, i_=i_: G.tensor_copy(out=o_, in_=i_))}[eng]
                em.op(eng, fn, R=["bufA"], W=[f"bufBq{qd}"])
            em.op("pool", lambda: G.tensor_copy(out=bufB[:, XO + 2:XO + 2 + LC], in_=bufA[:, S:NT]), R=["bufA"], W=["bufBc"])
            em.dma("sp", bufA[:, 0:S], C.UG[4 + c], W=["bufA"])
            em.dma("sp", bufA[:, S:NT], C.UGc[4 + c], W=["bufA"])
            for (o0, src0, n) in ((0, 0, S), (S, XO, LC)):
                for q0 in range(0, n, 2048):
                    nn = min(2048, n - q0)
                    nq = (nn + 511) // 512
                    pb = 4 * ((q0 // 2048) % 2)
                    qd = q0 // 2048
                    rk = ["bufBc", "bufB"] if n == LC else [f"bufBq{x}" for x in (qd - 1, qd, qd + 1) if 0 <= x < 4] + ["bufB"]
                    for q in range(nq):
                        w_ = min(512, nn - q * 512)
                        for k in range(4):
                            em.op("pe", lambda q=q, k=k, w_=w_, pb=pb, src0=src0, q0=q0, c=c: T.matmul(
                                PS[:, pb + q, 0:w_], lhsT=cdg[:, c * 4 + k, :],
                                rhs=bufB[:, src0 + q0 + q * 512 + k:src0 + q0 + q * 512 + k + w_],
                                start=(k == 0), stop=(k == 3)), R=["cdg"] + rk, W=[("ps", pb + q)], inc=(k == 3))
                    src = PS[:, pb:pb + 4, :].rearrange("p a b -> p (a b)")[:, 0:nn]
                    em.op("act", lambda src=src, o0=o0, q0=q0, nn=nn, c=c: A.activation(
                        out=uc[:, o0 + q0:o0 + q0 + nn], in_=src, func=AF.Identity, bias=cvc[:, c, 4:5]),
                          R=[("ps", pb + q) for q in range(4)] + ["cvc"], W=["uc"])
                    em.op("act", lambda src=src, o0=o0, q0=q0, nn=nn, c=c: A.activation(
                        out=ucb[:, o0 + q0:o0 + q0 + nn], in_=src, func=AF.Identity, bias=cvc[:, c, 4:5]),
                          R=[("ps", pb + q) for q in range(4)] + ["cvc"], W=["ucb"])

            def gelu_s1(lo, hi, p):
                n = hi - lo
                em.op("act", lambda: A.activation(out=gtmp[p][:, 0:n], in_=bufA[:, lo:hi], func=AF.Square),
                      R=["bufA"], W=[f"gt{p}"])
                em.op("pool", lambda: G.tensor_scalar(out=gtmp[p][:, 0:n], in0=gtmp[p][:, 0:n], scalar1=0.044715, scalar2=1.0,
                                                      op0=ALU.mult, op1=ALU.add), R=[f"gt{p}"], W=[f"gt{p}"])
                em.op("dve", lambda: V.tensor_tensor(out=gtmp[p][:, 0:n], in0=gtmp[p][:, 0:n], in1=bufA[:, lo:hi], op=ALU.mult),
                      R=[f"gt{p}", "bufA"], W=[f"gt{p}"])

            def gelu_s2(lo, hi, p):
                n = hi - lo
                em.op("act", lambda: A.activation(out=gtmp[p][:, 0:n], in_=gtmp[p][:, 0:n], func=AF.Sigmoid,
                                                  scale=1.5957691216057308), R=[f"gt{p}"], W=[f"gt{p}"])
                em.op("dve", lambda: V.tensor_tensor(out=bufA[:, lo:hi], in0=gtmp[p][:, 0:n], in1=bufA[:, lo:hi], op=ALU.mult),
                      R=[f"gt{p}", "bufA"], W=["bufA"])

            gpieces = [(S, NT)] + [(i * 1024, (i + 1) * 1024) for i in range(8)]
            gstate = {"s1": 0, "s2": 0}

            def gelu_push1():
                k = gstate["s1"]
                if k < len(gpieces):
                    gelu_s1(gpieces[k][0], gpieces[k][1], k % 2)
                    gstate["s1"] += 1

            def gelu_push2():
                k = gstate["s2"]
                if k < gstate["s1"]:
                    gelu_s2(gpieces[k][0], gpieces[k][1], k % 2)
                    gstate["s2"] += 1

            for d in range(2):
                order = [tiles[0]] + (tiles[1:] if d == 0 else tiles[1:][::-1])
                for ti, (lo, hi) in enumerate(order):
                    p = tix % 2
                    tix += 1
                    n = hi - lo
                    nq = (n + 511) // 512
                    for gi in range(2):
                        for q in range(nq):
                            w_ = min(512, n - q * 512)
                            em.op("pe", lambda gi=gi, q=q, w_=w_, lo=lo, d=d, c=c: T.matmul(
                                PS[:, gi * 4 + q, 0:w_], lhsT=wbd[:, gi * 8 + d * 4 + c, :], rhs=ucb[:, lo + q * 512:lo + q * 512 + w_],
                                start=True, stop=True), R=["wbd", "ucb"], W=[("ps", gi * 4 + q)], inc=(q == nq - 1))
                    pr = PS[:, 0:4, :].rearrange("p a b -> p (a b)")[:, 0:n]
                    pi = PS[:, 4:8, :].rearrange("p a b -> p (a b)")[:, 0:n]
                    if d == 0:
                        gelu_push2()
                        gelu_push2()
                    em.op("act", lambda pr=pr, n=n, c=c, d=d: A.activation(out=rt[:, 0:n], in_=pr, func=AF.Sigmoid,
                                                                           bias=lco[:, c, d, 0:1]),
                          R=[("ps", q) for q in range(4)] + ["lco"], W=["rt"])
                    em.op("act", lambda pi=pi, n=n, c=c, d=d, p=p: A.activation(out=it_[p][:, 0:n], in_=pi, func=AF.Sigmoid,
                                                                                bias=lco[:, c, d, 1:2]),
                          R=[("ps", 4 + q) for q in range(4)] + ["lco"], W=[f"it{p}"])
                    em.op("act", lambda n=n, c=c, d=d, p=p: A.activation(out=at[p][:, 0:n], in_=rt[:, 0:n], func=AF.Exp,
                                                                         scale=cA[:, c, d:d + 1]), R=["rt", "cA"], W=[f"at{p}"])
                    em.op("dve", lambda n=n, p=p: V.tensor_tensor(out=st[p][:, 0:n], in0=at[p][:, 0:n], in1=at[p][:, 0:n], op=ALU.mult),
                          R=[f"at{p}"], W=[f"st{p}"])
                    if d == 0:
                        gelu_push1()
                        gelu_push1()
                    em.op("act", lambda n=n, p=p: A.activation(out=st[p][:, 0:n], in_=st[p][:, 0:n], func=AF.Sqrt, scale=-1.0, bias=1.0),
                          R=[f"st{p}"], W=[f"st{p}"])
                    em.op("dve", lambda n=n, p=p: V.tensor_tensor(out=it_[p][:, 0:n], in0=it_[p][:, 0:n], in1=st[p][:, 0:n], op=ALU.mult),
                          R=[f"it{p}", f"st{p}"], W=[f"it{p}"])
                    em.op("dve", lambda n=n, lo=lo, hi=hi, p=p: V.tensor_tensor(out=it_[p][:, 0:n], in0=it_[p][:, 0:n], in1=uc[:, lo:hi],
                                                                               op=ALU.mult), R=[f"it{p}", "uc"], W=[f"it{p}"])
                    init = 0.0 if ti == 0 else carry[:, d:d + 1]
                    if d == 0:
                        em.op("dve", lambda n=n, lo=lo, hi=hi, init=init, p=p: V.tensor_tensor_scan(
                            out=bufB[:, lo:hi], data0=at[p][:, 0:n], data1=it_[p][:, 0:n], initial=init, op0=ALU.mult, op1=ALU.add),
                              R=[f"at{p}", f"it{p}", "carry", "bufB", "uc"], W=["bufB", "bufBq0", "bufBq1", "bufBq2", "bufBq3", "bufBc"])
                        em.op("dve", lambda hi=hi: V.tensor_copy(out=carry[:, 0:1], in_=bufB[:, hi - 1:hi]), R=["bufB"], W=["carry"])
                    else:
                        em.op("dve", lambda n=n, init=init, p=p: V.tensor_tensor_scan(
                            out=st[p][:, 0:n][:, ::-1], data0=at[p][:, 0:n][:, ::-1],
                            data1=it_[p][:, 0:n][:, ::-1], initial=init, op0=ALU.mult, op1=ALU.add),
                              R=[f"at{p}", f"it{p}", "carry", f"st{p}"], W=[f"st{p}"])
                        em.op("dve", lambda p=p: V.tensor_copy(out=carry[:, 1:2], in_=st[p][:, 0:1]), R=[f"st{p}"], W=["carry"])
                        em.op("dve", lambda n=n, lo=lo, hi=hi, p=p: V.tensor_tensor(out=bufB[:, lo:hi], in0=bufB[:, lo:hi], in1=st[p][:, 0:n],
                                                                                    op=ALU.add), R=["bufB", f"st{p}"], W=["bufB"])
            while gstate["s2"] < len(gpieces):
                gelu_push1()
                gelu_push2()
            em.op("dve", lambda: V.tensor_tensor(out=ucb[:, 0:S].rearrange("p (a b) -> p a b", b=64),
                                                 in0=bufB[:, 0:S].rearrange("p (a b) -> p a b", b=64),
                                                 in1=bufA[:, 0:S].rearrange("p (b a) -> p a b", a=128), op=ALU.mult),
                  R=["bufB", "bufBq0", "bufBq1", "bufBq2", "bufBq3", "bufBc", "bufA", "ucb"], W=["ucb"])
            em.op("dve", lambda: V.tensor_tensor(out=ucb[:, S:NT], in0=bufB[:, S:NT], in1=bufA[:, S:NT], op=ALU.mult),
                  R=["bufB", "bufBq0", "bufBq1", "bufBq2", "bufBq3", "bufBc", "bufA", "ucb"], W=["ucb"])
            em.dma("pool", C.MIXT[c * 128:(c + 1) * 128, :], ucb[:, :], R=["ucb"], W=[("MIXT", c)])
            if c < 3:
                em.op("dve", lambda: V.memset(bufB[:, 0:2], 0.0), R=["bufB"], W=["bufB", "bufBq0"])
                em.op("dve", lambda: V.memset(bufB[:, S:8460], 0.0), R=["bufB"], W=["bufB", "bufBq3", "bufBc"])


def _tok_tiles(ntok_tile):
    return None


def phase_D1(C, layer=0):
    nc, em, PS = C.nc, C.em, C.PS
    V, A, G, T = nc.vector, nc.scalar, nc.gpsimd, nc.tensor
    with ExitStack() as ps_:
        sb = lambda n, s, d=F32: ps_.enter_context(nc.sbuf_tensor(n, list(s), d))
        wo = sb("D1wo", [128, 8, D], BF16)
        xt = [sb(f"D1xt{i}", [128, 4, D]) for i in range(3)]
        mt = [sb(f"D1mt{i}", [128, 8, 512], BF16) for i in range(3)]
        mb = [sb(f"D1mb{i}", [128, 8, 512], BF16) for i in range(3)]
        hs = sb("D1hs", [128, 2])
        Gx = sb("D1Gx", [128, D]); Gc = sb("D1Gc", [128, D])
        ssq = sb("D1ssq", [128, 4]); rs = sb("D1rs", [128, 4]); junk = sb("D1junk", [128, D], BF16); tt = sb("D1tt", [128, D])
        _wload_bf(C, wo, C.WBF["woab"].rearrange("(k p) n -> p k n", p=128), "D1wo", D, "cvwoab")
        em.dma("sp", hs[:], C.hsel_d, W=["D1hs"])
        em.dma("sp", Gx[:], C.GROW[0, 0, 0, :].partition_broadcast(128), R=[("GROW", 0, 0)], W=["D1Gx"])
        em.dma("sp", Gc[:], C.GROW[0, 0, 1, :].partition_broadcast(128), R=[("GROW", 0, 0)], W=["D1Gc"])
        mixv = C.MIXT.rearrange("(k p) t -> p k t", p=128)
        NTL = NLT + 1

        def load(j, sl):
            if j < NLT:
                em.dma("sp", xt[sl][:], C.xloc_d[j * 512:(j + 1) * 512, :].rearrange("(s p) d -> p s d", p=128), W=[f"D1xt{sl}"])
                em.dma("sp", mt[sl][:], mixv[:, :, j * 512:(j + 1) * 512], W=[f"D1mt{sl}"])
                lb = 4096 + j * 512 if j < 8 else LBASE
                em.dma("sp", mb[sl][:], mixv[:, :, lb:lb + 512], W=[f"D1mb{sl}"])
                em.op("act", lambda sl=sl: A.activation(out=mt[sl][:].rearrange("p a b -> p (a b)"),
                                                        in_=mt[sl][:].rearrange("p a b -> p (a b)"), func=AF.Copy,
                                                        scale=hs[:, 0:1]), R=[f"D1mt{sl}", "D1hs"], W=[f"D1mt{sl}"])
                em.op("dve", lambda sl=sl: V.scalar_tensor_tensor(
                    out=mt[sl][:].rearrange("p a b -> p (a b)"), in0=mb[sl][:].rearrange("p a b -> p (a b)"), scalar=hs[:, 1:2],
                    in1=mt[sl][:].rearrange("p a b -> p (a b)"), op0=ALU.mult, op1=ALU.add),
                      R=[f"D1mt{sl}", f"D1mb{sl}", "D1hs"], W=[f"D1mt{sl}"])
            else:
                em.dma("sp", xt[sl][:, 0:2, :], C.ctx_d.rearrange("(s p) d -> p s d", p=128), W=[f"D1xt{sl}"])
                em.dma("sp", mt[sl][:, :, 0:LC], mixv[:, :, S:NT], W=[f"D1mt{sl}"])
        load(0, 0)
        load(1, 1)
        for j in range(NTL):
            sl = j % 3
            if j + 2 < NTL:
                load(j + 2, (j + 2) % 3)
            isx = j < NLT
            nsub = 4 if isx else 2
            for s in range(nsub):
                pb = 2 * (s % 4)
                for h in range(2):
                    for k in range(8):
                        em.op("pe", lambda k=k, h=h, s=s, sl=sl, pb=pb: T.matmul(
                            PS[:, pb + h, :], lhsT=mt[sl][:, k, s * 128:(s + 1) * 128], rhs=wo[:, k, h * 512:(h + 1) * 512],
                            start=(k == 0), stop=(k == 7)), R=["D1wo0", f"D1mt{sl}"], W=[("ps", pb + h)], inc=(k == 7))
                C.epilogue(pb, xt[sl][:, s, :], f"D1xt{sl}", (Gx if isx else Gc)[:, :], "D1Gx" if isx else "D1Gc",
                           ssq, rs, junk, tt, "D1")
            if isx:
                em.dma("pool", C.X1[j * 512:(j + 1) * 512, :].rearrange("(s p) d -> p s d", p=128), xt[sl][:],
                       R=[f"D1xt{sl}"], W=[("X1", j)])
            else:
                em.dma("pool", C.C1.rearrange("(s p) d -> p s d", p=128), xt[sl][:, 0:2, :], R=[f"D1xt{sl}"], W=["C1"])


def phase_FFN(C, layer, Xin, Cin, Xout, Cout, inkey, outkey, ntok=NL):
    nc, em, PS = C.nc, C.em, C.PS
    V, A, G, T = nc.vector, nc.scalar, nc.gpsimd, nc.tensor
    tg = f"F{layer}"
    with ExitStack() as ps_:
        sb = lambda n, s, d=F32: ps_.enter_context(nc.sbuf_tensor(tg + n, list(s), d))
        wg = sb("wg", [128, 8, FH], BF16); wu = sb("wu", [128, 8, FH], BF16); wd = sb("wd", [128, NJ, D], BF16)
        xt = sb("xt", [128, 4, D]); hT = sb("hT", [128, 8, 512], BF16); hh = sb("hh", [128, NJ, 512], BF16)
        est = [sb(f"est{i}", [128, D]) for i in range(2)]
        Gx = sb("Gx", [128, D]); tt = sb("tt", [128, D]); junk = sb("junk", [128, D], BF16)
        ssq = sb("ssq", [128, 4]); rs = sb("rs", [128, 4]); ssq2 = sb("ssq2", [128, 4]); rs2 = sb("rs2", [128, 4])
        NX = ntok // 512
        ntile = NX + (1 if Cin is not None else 0)

        def nsub_of(j):
            return 4 if j < NX else 2

        def rows(j, s):
            if j < NX:
                r0 = j * 512 + s * 128
                return Xin[r0:r0 + 128, :], Xout[r0:r0 + 128, :]
            return Cin[s * 128:(s + 1) * 128, :], Cout[s * 128:(s + 1) * 128, :]

        def load(j):
            ns = nsub_of(j)
            src = Xin[j * 512:(j + 1) * 512, :] if j < NX else Cin
            em.dma("sp", xt[:, 0:ns, :], src.rearrange("(s p) d -> p s d", p=128), W=[tg + "xt"])

        def pro_a(j):
            ns = nsub_of(j)
            C.prologue_a(xt, tg + "xt", ns, xt, tg + "xt", ssq2, rs2, junk, tg + "p")

        def pro_b(j):
            path = 0 if j < NX else 1
            C.prologue_b(nsub_of(j), xt, tg + "xt", C.MODC[:, layer, 1, 0, path, :], C.MODC[:, layer, 1, 1, path, :],
                         hT, tg + "hT", [0, 1, 2, 3])

        def gateup(j):
            n = nsub_of(j) * 128
            for jj in range(NJ):
                b = 2 * (jj % 2)
                for gi, w_ in enumerate((wg, wu)):
                    for k in range(8):
                        em.op("pe", lambda k=k, b=b, gi=gi, w_=w_, jj=jj: T.matmul(
                            PS[:, b + gi, 0:n], lhsT=w_[:, k, jj * 128:(jj + 1) * 128], rhs=hT[:, k, 0:n],
                            start=(k == 0), stop=(k == 7)), R=[tg + "wg" + str(x) for x in {jj * 128 // 704, (jj * 128 + 127) // 704}] + [tg + "wu" + str(x) for x in {jj * 128 // 704, (jj * 128 + 127) // 704}] + [tg + "hT"], W=[("ps", b + gi)],
                              inc=(k == 7))
                em.op("act", lambda b=b, jj=jj: A.activation(out=hh[:, jj, 0:n], in_=PS[:, b, 0:n], func=AF.Silu),
                      R=[("ps", b)], W=[tg + f"hh{jj}"])
                em.op("dve", lambda b=b, jj=jj: V.tensor_tensor(out=hh[:, jj, 0:n], in0=hh[:, jj, 0:n], in1=PS[:, b + 1, 0:n],
                                                               op=ALU.mult), R=[("ps", b + 1), tg + f"hh{jj}"], W=[tg + f"hh{jj}"])

        def down(j, s):
            pb = 4 + 2 * (s % 2)
            for h in range(2):
                for jj in range(NJ):
                    em.op("pe", lambda jj=jj, h=h, s=s, pb=pb: T.matmul(
                        PS[:, pb + h, :], lhsT=hh[:, jj, s * 128:(s + 1) * 128], rhs=wd[:, jj, h * 512:(h + 1) * 512],
                        start=(jj == 0), stop=(jj == NJ - 1)), R=[tg + "wd" + str(h), tg + f"hh{jj}"], W=[("ps", pb + h)], inc=(jj == NJ - 1))

        def eload(j, s):
            em.dma("sp", est[s % 2][:], rows(j, s)[0], W=[tg + f"est{s % 2}"])

        def epi(j, s):
            pb = 4 + 2 * (s % 2)
            C.epilogue(pb, est[s % 2][:, :], tg + f"est{s % 2}", Gx[:, :], tg + "Gx", ssq, rs, junk, tt, tg)
            em.dma("pool", rows(j, s)[1], est[s % 2][:], R=[tg + f"est{s % 2}"], W=[(outkey, j, s)])

        load(0)
        cv_emit(C, 0, need=f"cvwg{layer}"); cv_emit(C, 0, need=f"cvwu{layer}"); cv_emit(C, 0, need=f"cvwd{layer}")
        wgv = C.WBF["wg"][layer].rearrange("(k p) n -> p k n", p=128)
        wuv = C.WBF["wu"][layer].rearrange("(k p) n -> p k n", p=128)
        wdv = C.WBF["wd"][layer].rearrange("(k p) n -> p k n", p=128)
        for bi in range(4):
            em.dma("sp", wg[:, :, bi * 704:(bi + 1) * 704], wgv[:, :, bi * 704:(bi + 1) * 704], R=C.cvkeys[f"cvwg{layer}"], W=[tg + f"wg{bi}"])
            em.dma("sp", wu[:, :, bi * 704:(bi + 1) * 704], wuv[:, :, bi * 704:(bi + 1) * 704], R=C.cvkeys[f"cvwu{layer}"], W=[tg + f"wu{bi}"])
        for bi in range(2):
            em.dma("sp", wd[:, :, bi * 512:(bi + 1) * 512], wdv[:, :, bi * 512:(bi + 1) * 512], R=C.cvkeys[f"cvwd{layer}"], W=[tg + f"wd{bi}"])
        em.dma("sp", Gx[:], C.GROW[layer, 1, 0, :].partition_broadcast(128), W=[tg + "Gx"])
        pro_a(0)
        pro_b(0)
        for j in range(ntile):
            ns = nsub_of(j)
            if j == NX:
                em.dma("sp", Gx[:], C.GROW[layer, 1, 1, :].partition_broadcast(128), W=[tg + "Gx"])
            gateup(j)
            if layer == 0:
                cv_emit(C, 2)
            if j + 1 < ntile:
                load(j + 1)
                pro_a(j + 1)
            eload(j, 0)
            eload(j, 1)
            down(j, 0)
            down(j, 1)
            if j + 1 < ntile:
                pro_b(j + 1)
            epi(j, 0)
            epi(j, 1)
            if ns == 4:
                eload(j, 2)
                eload(j, 3)
                down(j, 2)
                down(j, 3)
                epi(j, 2)
                epi(j, 3)


PHASES += [("B", phase_B), ("D1", phase_D1),
           ("D2", lambda C: phase_FFN(C, 0, C.X1, C.C1, C.X2, C.C2, "X1", "X2"))]


def phase_E(C):
    nc, em, PS = C.nc, C.em, C.PS
    V, A, G, T = nc.vector, nc.scalar, nc.gpsimd, nc.tensor
    with ExitStack() as ps_:
        sb = lambda n, s, d=F32: ps_.enter_context(nc.sbuf_tensor("E" + n, list(s), d))
        wq = sb("wq", [128, 8, 3 * D], BF16)
        xt = [sb(f"xt{i}", [128, 4, D]) for i in range(2)]
        hT = [sb(f"hT{i}", [128, 8, 512], BF16) for i in range(2)]
        qk = [sb(f"qk{i}", [128, 16, 512], BF16) for i in range(2)]
        vst = [sb(f"vst{i}", [128, 4, D], BF16) for i in range(2)]
        ssq = sb("ssq", [128, 4]); rs = sb("rs", [128, 4]); junk = sb("junk", [128, D], BF16)
        QTv = C.QT.rearrange("(c p) t -> p c t", p=128); KTv = C.KT.rearrange("(c p) t -> p c t", p=128)
        NTL = NLT + 1
        evc = [0]

        def nsubE(j):
            return 2 if j == NLT else 4

        def load(j):
            sl = j % 2
            if j < NLT:
                em.dma("sp", xt[sl][:], C.X2[j * 512:(j + 1) * 512, :].rearrange("(s p) d -> p s d", p=128), W=[f"Ext{sl}"])
            else:
                em.dma("sp", xt[sl][:, 0:2, :], C.C2.rearrange("(s p) d -> p s d", p=128), W=[f"Ext{sl}"])

        def pro_a(j):
            sl = j % 2
            C.prologue_a(xt[sl], f"Ext{sl}", nsubE(j), xt[sl], f"Ext{sl}", ssq, rs, junk, "E")

        def pro_b(j):
            sl = j % 2
            path = 1 if j == NLT else 0
            C.prologue_b(nsubE(j), xt[sl], f"Ext{sl}", C.MODC[:, 1, 0, 0, path, :], C.MODC[:, 1, 0, 1, path, :],
                         hT[sl], f"EhT{sl}", [0, 1])

        def qkE(j):
            sl = j % 2
            isctx = j == NLT
            n = nsubE(j) * 128
            for oc in (range(8, 16) if isctx else range(16)):
                b = 2 + oc % 2
                for k in range(8):
                    em.op("pe", lambda k=k, b=b, oc=oc, n=n, sl=sl: T.matmul(
                        PS[:, b, 0:n], lhsT=wq[:, k, oc * 128:(oc + 1) * 128], rhs=hT[sl][:, k, 0:n],
                        start=(k == 0), stop=(k == 7)), R=["Ewq" + str(oc // 8), f"EhT{sl}"], W=[("ps", b)], inc=(k == 7))
                if oc < 8:
                    em.op("act", lambda b=b, oc=oc, sl=sl, n=n: A.activation(out=qk[sl][:, oc, 0:n], in_=PS[:, b, 0:n],
                                                                             func=AF.Copy, scale=0.125),
                          R=[("ps", b)], W=[f"Eqk{sl}"])
                else:
                    em.op("dve", lambda b=b, oc=oc, sl=sl, n=n: V.tensor_copy(out=qk[sl][:, oc, 0:n], in_=PS[:, b, 0:n]),
                          R=[("ps", b)], W=[f"Eqk{sl}"])
            if not isctx:
                em.dma("pool", QTv[:, :, j * 512:(j + 1) * 512], qk[sl][:, 0:8, :], R=[f"Eqk{sl}"], W=[("QT", j)])
                em.dma("pool", KTv[:, :, j * 512:(j + 1) * 512], qk[sl][:, 8:16, :], R=[f"Eqk{sl}"], W=[("KT", j)])
            else:
                em.dma("pool", KTv[:, :, NL:NL + LC], qk[sl][:, 8:16, 0:LC], R=[f"Eqk{sl}"], W=[("KT", j)])

        def vE(j):
            sl = j % 2
            isctx = j == NLT
            nsub = nsubE(j)
            for s in range(nsub):
                pb = 4 + 2 * (s % 2)
                for h in range(2):
                    for k in range(8):
                        em.op("pe", lambda k=k, h=h, s=s, pb=pb, sl=sl: T.matmul(
                            PS[:, pb + h, :], lhsT=hT[sl][:, k, s * 128:(s + 1) * 128], rhs=wq[:, k, 2048 + h * 512:2048 + (h + 1) * 512],
                            start=(k == 0), stop=(k == 7)), R=["Ewq2", f"EhT{sl}"], W=[("ps", pb + h)], inc=(k == 7))
                evc[0] += 1
                src = PS[:, pb:pb + 2, :].rearrange("p a b -> p (a b)")
                if evc[0] % 2:
                    em.op("dve", lambda s=s, sl=sl, src=src: V.tensor_copy(out=vst[sl][:, s, :], in_=src),
                          R=[("ps", pb), ("ps", pb + 1)], W=[f"Evst{sl}"])
                else:
                    em.op("act", lambda s=s, sl=sl, src=src: A.copy(out=vst[sl][:, s, :], in_=src),
                          R=[("ps", pb), ("ps", pb + 1)], W=[f"Evst{sl}"])
            t0 = NL if isctx else j * 512
            em.dma("pool", C.VD[t0:t0 + nsub * 128, :].rearrange("(s p) d -> p s d", p=128), vst[sl][:, 0:nsub, :],
                   R=[f"Evst{sl}"], W=[("VD", j)])

        load(0)
        _wload_bf(C, wq, C.WBF["wqkv"].rearrange("(k p) n -> p k n", p=128), "Ewq", 3 * D, "cvwqkv")
        load(1)
        pro_a(0)
        pro_b(0)
        for j in range(NTL):
            qkE(j)
            if j + 2 < NTL:
                load(j + 2)
            if j + 1 < NTL:
                pro_a(j + 1)
            vE(j)
            if j + 1 < NTL:
                pro_b(j + 1)


def phase_F(C):
    nc, em, PS = C.nc, C.em, C.PS
    V, A, G, T = nc.vector, nc.scalar, nc.gpsimd, nc.tensor
    with ExitStack() as ps_:
        sb = lambda n, s, d=F32: ps_.enter_context(nc.sbuf_tensor("AT" + n, list(s), d))
        wo = sb("wo", [128, 8, D], BF16)
        KTb = [sb(f"KTb{i}", [128, 8, 1024], BF16) for i in range(2)]
        Vb = [sb(f"Vb{i}", [128, 8, 16, 65], BF16) for i in range(2)]
        QTb = sb("QTb", [128, 8, 512], BF16); xt = sb("xt", [128, 4, D])
        KTc = sb("KTc", [128, 8, LC], BF16); Vc = sb("Vc", [128, 2, 16, 65], BF16)
        BTi = sb("BTi", [128, 16, 5, 128], BF16); BTe = sb("BTe", [128, 16, 8, 128], BF16)
        PT = [sb(f"PT{i}", [128, 1280], BF16) for i in range(2)]
        Ot = sb("Ot", [128, D]); OTt = sb("OTt", [128, 8, 128], BF16); rden = sb("rden", [128, 4])
        Gx = sb("Gx", [128, D]); ssq = sb("ssq", [128, 4]); rs = sb("rs", [128, 4]); junk = sb("junk", [128, D], BF16)
        tt = sb("tt", [128, D])
        KTv = C.KT.rearrange("(c p) t -> p c t", p=128); QTv = C.QT.rearrange("(c p) t -> p c t", p=128)
        em.dma("sp", KTc[:], KTv[:, :, NL:NL + LC], W=["AKTc"])
        for i in range(2):
            em.op("pool", lambda i=i: G.memset(Vb[i][:, :, :, 64:65], 1.0), W=[f"AVb{i}"])
        em.op("pool", lambda: G.memset(Vc[:, :, :, 64:65], 1.0), W=["AVc"])
        for c in range(2):
            em.dma("sp", Vc[:, c, :, 0:64], C.VD[NL + c * 128:NL + (c + 1) * 128, :].rearrange("p (h d) -> p h d", d=64), W=["AVc"])

        NB = 8

        def kbof(blk):
            return 0 if blk == 0 else (52 if blk == NB - 1 else 8 * blk - 4)

        def load(blk, sl):
            if blk == 0:
                pieces = [(0, 2, 68 * 64), (2, 6, 0)]
            else:
                pieces = [(0, 8, kbof(blk) * 64)]
            for (c0, ncn, t0) in pieces:
                em.dma("sp", KTb[sl][:, :, c0 * 128:(c0 + ncn) * 128], KTv[:, :, t0:t0 + ncn * 128], W=[f"AKTb{sl}"])
                for c in range(ncn):
                    tt0 = t0 + c * 128
                    em.dma("sp", Vb[sl][:, c0 + c, :, 0:64], C.VD[tt0:tt0 + 128, :].rearrange("p (h d) -> p h d", d=64),
                           W=[f"AVb{sl}"])
        load(0, 0)
        late = {"done": False}

        def late_loads():
            if not late["done"]:
                late["done"] = True
                em.dma("sp", BTi[:], C.BT_d, W=["ABTi"])
                _wload_bf(C, wo, C.WBF["wona"].rearrange("(k p) n -> p k n", p=128), "Awo", D, "cvwona")
                em.dma("sp", Gx[:], C.GROW[1, 0, 0, :].partition_broadcast(128), W=["AGx"])
        for blk in range(NB):
            sl = blk % 2
            r0 = 8 * blk
            edge = blk in (0, NB - 1)
            eb = 0 if blk == 0 else 1
            nloc = 8 if edge else 5
            nch = nloc + 2
            em.dma("sp", QTb[:], QTv[:, :, r0 * 64:r0 * 64 + 512], W=["AQTb"])
            em.dma("sp", xt[:], C.X2[blk * 512:(blk + 1) * 512, :].rearrange("(s p) d -> p s d", p=128), W=["Axt"])

            def sbase(h):
                return 0 if edge else 2 * (h % 2)

            def cpos(cl, h):
                if edge:
                    return (cl // 4, (cl % 4) * 128)
                return (2 * (h % 2) + cl // 4, (cl % 4) * 128)

            def qk(i, h):
                off = 0 if edge else i
                if edge and h == 0:
                    em.dma("sp", BTe[:], C.BTE_d[eb, i], W=["ABTe"])
                j = h // 2; e = h % 2
                p0, p1 = 64 * e, 64 * e + 64
                for cl in range(nch):
                    bk, co = cpos(cl, h)
                    o = PS[:, bk, co:co + 128]
                    if cl < nloc:
                        lt = KTb[sl][p0:p1, j, (off + cl) * 128:(off + cl + 1) * 128]
                    else:
                        lt = KTc[p0:p1, j, (cl - nloc) * 128:(cl - nloc + 1) * 128]
                    last = cl == nch - 1
                    em.op("pe", lambda o=o, lt=lt, j=j, p0=p0, p1=p1, i=i, cl=cl, last=last: T.matmul(
                        o, lhsT=lt, rhs=QTb[p0:p1, j, i * 128:(i + 1) * 128], start=(cl % 4 == 0),
                        stop=(edge and last)), R=[f"AKTb{sl}", "AKTc", "AQTb"], W=[("ps", bk)], inc=(edge and last))
                    if edge:
                        if cl in (3, 7):
                            em.op("pe", lambda h=h, bk=bk, cl=cl: T.matmul(
                                PS[:, bk, :], lhsT=C.identb[:, :], rhs=BTe[:, h, cl - 3:cl + 1, :].rearrange("p a b -> p (a b)"),
                                start=False, stop=True), R=["identb", "ABTe"], W=[("ps", bk)], inc=False)
                    else:
                        if cl == 3:
                            em.op("pe", lambda h=h, bk=bk: T.matmul(
                                PS[:, bk, :], lhsT=C.identb[:, :], rhs=BTi[:, h, 0:4, :].rearrange("p a b -> p (a b)"),
                                start=False, stop=True), R=["identb", "ABTi"], W=[("ps", bk)], inc=False)
                        if cl == 6:
                            em.op("pe", lambda h=h, bk=bk: T.matmul(
                                PS[:, bk, 0:128], lhsT=C.identb[:, :], rhs=BTi[:, h, 4, :],
                                start=False, stop=True), R=["identb", "ABTi"], W=[("ps", bk)], inc=True)

            def ex(i, h):
                b0 = sbase(h)
                nb = 3 if edge else 2
                src = PS[:, b0:b0 + nb, :].rearrange("p a b -> p (a b)")[:, 0:nch * 128]
                em.op("act", lambda src=src, h=h: A.activation(out=PT[h % 2][:, 0:nch * 128], in_=src, func=AF.Exp),
                      R=[("ps", b0 + q) for q in range(nb)], W=[f"APT{h % 2}"])

            def pv(i, h):
                off = 0 if edge else i
                ob = 4 + (h // 4) % 2
                so = (h % 4) * 128
                for c in range(nch):
                    rhs = Vb[sl][:, off + c, h, :] if c < nloc else Vc[:, c - nloc, h, :]
                    em.op("pe", lambda c=c, rhs=rhs, h=h, ob=ob, so=so: T.matmul(
                        PS[:, ob, so:so + 65], lhsT=PT[h % 2][:, c * 128:(c + 1) * 128], rhs=rhs,
                        start=(c == 0), stop=(c == nch - 1)), R=[f"APT{h % 2}", f"AVb{sl}", "AVc"], W=[("ps", ob)],
                          inc=(c == nch - 1))
                if h % 4 == 3:
                    em.op("dve", lambda ob=ob: V.reciprocal(out=rden[:, 0:4], in_=PS[:, ob, 64:512:128]),
                          R=[("ps", ob)], W=["Arden"])
                    for hh in range(4):
                        hd = h - 3 + hh
                        em.op("dve", lambda ob=ob, hh=hh, hd=hd: V.tensor_scalar(
                            out=Ot[:, hd * 64:(hd + 1) * 64], in0=PS[:, ob, hh * 128:hh * 128 + 64],
                            scalar1=rden[:, hh:hh + 1], scalar2=None, op0=ALU.mult), R=[("ps", ob), "Arden"], W=["AOt"])

            def fin(i):
                for k in range(8):
                    em.op("pe", lambda k=k: T.transpose(PS[:, 6 + k // 4, (k % 4) * 128:(k % 4 + 1) * 128],
                                                        Ot[:, k * 128:(k + 1) * 128], C.ident[:]),
                          R=["AOt", "ident"], W=[("ps", 6 + k // 4)], inc=(k % 4 == 3))
                for hb in range(2):
                    em.op("act" if hb else "dve",
                          (lambda hb=hb: A.copy(out=OTt[:, 4 * hb:4 * hb + 4, :].rearrange("p a b -> p (a b)"), in_=PS[:, 6 + hb, :])) if hb else
                          (lambda hb=hb: V.tensor_copy(out=OTt[:, 4 * hb:4 * hb + 4, :].rearrange("p a b -> p (a b)"), in_=PS[:, 6 + hb, :])),
                          R=[("ps", 6 + hb)], W=["AOTt"])
                for hf in range(2):
                    for k in range(8):
                        em.op("pe", lambda k=k, hf=hf: T.matmul(PS[:, 6 + hf, :], lhsT=OTt[:, k, :], rhs=wo[:, k, hf * 512:(hf + 1) * 512],
                                                                start=(k == 0), stop=(k == 7)),
                              R=["AOTt", "Awo0"], W=[("ps", 6 + hf)], inc=(k == 7))
                C.epilogue(6, xt[:, i, :], "Axt", Gx[:, :], "AGx", ssq, rs, junk, tt, "A")

            items = [(i, h) for i in range(4) for h in range(16)]
            qk(*items[0])
            late_loads()
            if blk + 1 < NB:
                load(blk + 1, 1 - sl)
            for n_, (i, h) in enumerate(items):
                nxt = items[n_ + 1] if n_ + 1 < len(items) else None
                if edge:
                    ex(i, h)
                    if nxt:
                        qk(*nxt)
                else:
                    if nxt:
                        qk(*nxt)
                    ex(i, h)
                pv(i, h)
                if h == 15:
                    fin(i)
            em.dma("pool", C.X3[blk * 512:(blk + 1) * 512, :].rearrange("(s p) d -> p s d", p=128), xt[:], R=["Axt"], W=[("X3", blk)])


PHASES += [("E", phase_E), ("F", phase_F),
           ("G", lambda C: phase_FFN(C, 1, C.X3, None, C.out_d, None, "X3", "OUT", ntok=4096))]
```

```python
import math
from contextlib import ExitStack
import numpy as np
import ml_dtypes
import concourse.bass as bass
import concourse.mybir as mybir
from concourse.bass_utils import run_bass_kernel_spmd

F32 = mybir.dt.float32
BF16 = mybir.dt.bfloat16
AF = mybir.ActivationFunctionType
ALU = mybir.AluOpType
NPBF = ml_dtypes.bfloat16

D = 1024
S = 8192
LC = 256
NT = S + LC
FH = 2816
NJ = FH // 128
EPS = 1e-6
NCORES = 8
NL = 4608
NLT = NL // 512
LBASE = 3584
NEG = -30000.0
QENG = ("pool", "dve", "pool", "dve")


class Em:
    LIMIT = 30000
    NDS = 8

    def __init__(self, nc, es):
        self.nc = nc
        self.es = es
        self.eng = dict(pe=nc.tensor, act=nc.scalar, dve=nc.vector, pool=nc.gpsimd, sp=nc.sync)
        self.sem = {}
        self.semkey = {}
        self.cnt = {}
        self.nsem = 0
        for e in self.eng:
            self._newsem(e)
        self.waited = {e: {} for e in self.eng}
        self.lastw = {}
        self.readers = {}
        self.dsem = {}
        self.ndma = {}
        for q in ("sp", "pool", "act"):
            self.dsem[q] = [es.enter_context(nc.semaphore(f"d{q}{i}")) for i in range(self.NDS)]
            self.ndma[q] = 0
        self.bg = []

    def _newsem(self, e):
        self.nsem += 1
        self.sem[e] = self.es.enter_context(self.nc.semaphore(f"s{e}{self.nsem}"))
        self.semkey[e] = (e, self.nsem)
        self.cnt[e] = 0

    def _deps(self, engine, R, W):
        deps = {}
        for k in list(R) + list(W):
            t = self.lastw.get(k)
            if t is None:
                continue
            for t_ in (t.values() if isinstance(t, dict) else (t,)):
                if t_[0] not in deps or deps[t_[0]][2] < t_[2]:
                    deps[t_[0]] = t_
        for k in W:
            for t in self.readers.get(k, {}).values():
                if t[0] not in deps or deps[t[0]][2] < t[2]:
                    deps[t[0]] = t
        e = self.eng[engine]
        for sk, t in deps.items():
            if engine == "pe" and t[3] == "pe":
                continue
            if self.waited[engine].get(sk, 0) >= t[2]:
                continue
            e.wait_ge(t[1], t[2])
            self.waited[engine][sk] = t[2]

    def _record(self, tok, R, W):
        for k in W:
            if tok[3] is None:
                cur = self.lastw.get(k)
                d = cur if isinstance(cur, dict) else {}
                d[tok[0]] = tok
                self.lastw[k] = d
            else:
                self.lastw[k] = tok
            self.readers[k] = {}
        for k in R:
            d = self.readers.setdefault(k, {})
            if tok[0] not in d or d[tok[0]][2] < tok[2]:
                d[tok[0]] = tok

    def op(self, engine, fn, R=(), W=(), inc=True):
        self._deps(engine, R, W)
        ins = fn()
        if inc:
            ins.then_inc(self.sem[engine], 1)
            self.cnt[engine] += 1
            tok = (self.semkey[engine], self.sem[engine], self.cnt[engine], engine)
            self._record(tok, R, W)
            if self.cnt[engine] >= self.LIMIT:
                self._newsem(engine)
        else:
            tok = (self.semkey[engine], self.sem[engine], self.cnt[engine] + 1, engine)
            self._record(tok, R, W)
        return ins

    def dma(self, q, out, in_, R=(), W=(), **kw):
        self._deps(q, R, W)
        i = self.ndma[q]
        self.ndma[q] += 1
        sem = self.dsem[q][i % self.NDS]
        rnd = i // self.NDS
        sk = ("dma", q, i % self.NDS)
        if rnd > 0 and self.waited[q].get(sk, 0) < 16 * rnd:
            self.eng[q].wait_ge(sem, 16 * rnd)
            self.waited[q][sk] = 16 * rnd
        self.eng[q].dma_start(out=out, in_=in_, **kw).then_inc(sem, 16)
        tok = (sk, sem, 16 * (rnd + 1), None)
        self._record(tok, R, W)

    def dma_bg(self, q, out, in_, R=(), W=(), **kw):
        self._deps(q, R, W)
        sem = self.es.enter_context(self.nc.semaphore(f"bg{len(self.bg)}"))
        self.bg.append(sem)
        self.eng[q].dma_start(out=out, in_=in_, **kw).then_inc(sem, 16)
        self._record((("bg", len(self.bg)), sem, 16, None), R, W)

    def finish(self):
        sp = self.eng["sp"]
        for sem in self.bg:
            sp.wait_ge(sem, 16)
        for q in self.dsem:
            n = self.ndma[q]
            for s in range(self.NDS):
                uses = (n - s + self.NDS - 1) // self.NDS if n > s else 0
                if uses > 0:
                    sp.wait_ge(self.dsem[q][s], 16 * uses)
        for e in ("pe", "act", "dve", "pool"):
            if self.cnt[e] > 0:
                sp.wait_ge(self.sem[e], self.cnt[e])


def _tables():
    t = {}
    t["ident"] = np.eye(128, dtype=np.float32)
    t["identb"] = np.eye(128, dtype=np.float32).astype(NPBF)
    a = np.arange(128, dtype=np.float64)
    ang = 2 * np.pi * np.outer(a, a) / 128.0
    t["T1"] = np.concatenate([np.cos(ang), -np.sin(ang)], axis=1).astype(NPBF)
    t2 = np.arange(64, dtype=np.float64)[:, None, None]
    k1 = np.arange(128, dtype=np.float64)[None, :, None]
    k2 = np.arange(64, dtype=np.float64)[None, None, :]
    ph = 2 * np.pi * (t2 * k2 / 64.0 + t2 * k1 / 8192.0)
    Mc, Ms = np.cos(ph), np.sin(ph)
    t["M2"] = np.concatenate([Ms, Mc, -Ms], axis=2).astype(NPBF)
    c = np.arange(64, dtype=np.float64)
    angc = 2 * np.pi * np.outer(c, c) / 64.0
    Cc, Sc = np.cos(angc), np.sin(angc)
    z = np.zeros((64, 64))
    Cbd = np.block([[Cc, z], [z, Cc]])
    Sbd = np.block([[Sc, z], [z, Sc]])
    sx = 1.0 / math.sqrt(8192.0 * 64.0)
    t["CSf"] = (np.concatenate([Cbd, Sbd], axis=1) * sx).astype(NPBF)
    sc_ = 1.0 / math.sqrt(256.0 * 64.0)
    t["CSc"] = (np.concatenate([Cbd, Sbd], axis=1) * sc_).astype(NPBF)
    p = np.arange(256, dtype=np.float64)
    angp = 2 * np.pi * np.outer(p, p) / 256.0
    T256 = np.concatenate([np.cos(angp), -np.sin(angp)], axis=1)
    t["T256"] = T256.reshape(2, 128, 512).transpose(1, 0, 2).copy().astype(NPBF)
    return t


_TABLES = None


def _bias_tables(rpb):
    H = 16
    out = np.full((5, 5 * 128, H, 128), NEG, dtype=np.float32)
    kr = np.arange(10)[:, None, None, None]
    kc = np.arange(64)[None, :, None, None]
    qr = np.arange(2)[None, None, :, None]
    qc = np.arange(64)[None, None, None, :]
    cs = np.clip(qc - 8, 0, 48)
    for vi, (gp, ks) in enumerate([(0, 0), (2, 0), (60, 56), (124, 118), (126, 118)]):
        gq = gp + qr
        rs = np.clip(gq - 4, 0, 120)
        gk = ks + kr
        valid = (gk >= rs) & (gk < rs + 8) & (kc >= cs) & (kc < cs + 16)
        dr = np.clip(gk - gq + 7, 0, 14)
        dc = np.clip(kc - qc + 15, 0, 30)
        valid, dr, dc = np.broadcast_arrays(valid, dr, dc)
        vals = rpb[:, dr, dc]
        vals = np.where(valid[None], vals, NEG)
        out[vi] = vals.transpose(1, 2, 0, 3, 4).reshape(640, H, 128)
    bt = out.reshape(5, 5, 128, H, 128).transpose(0, 2, 3, 1, 4)
    return np.ascontiguousarray(bt).astype(NPBF)


def _local_rows(h):
    if h == 0:
        return np.arange(72)
    return np.concatenate([np.arange(64, 128), np.arange(56, 64)])


def _edge_tables(rpb, h):
    H = 16
    lr = _local_rows(h)
    out = np.empty((2, 4, 128, H, 8, 128), dtype=NPBF)
    kc = np.arange(64)[None, :, None, None]
    qr = np.arange(2)[None, None, :, None]
    qc = np.arange(64)[None, None, None, :]
    cs = np.clip(qc - 8, 0, 48)
    keyrows = [np.concatenate([np.arange(68, 72), np.arange(0, 12)]), np.arange(52, 68)]
    for eb, r0 in enumerate([0, 56]):
        gk = lr[keyrows[eb]][:, None, None, None]
        for i in range(4):
            gq = lr[r0 + 2 * i + np.arange(2)][None, None, :, None]
            rs = np.clip(gq - 4, 0, 120)
            valid = (gk >= rs) & (gk < rs + 8) & (kc >= cs) & (kc < cs + 16)
            dr = np.clip(gk - gq + 7, 0, 14)
            dc = np.clip(kc - qc + 15, 0, 30)
            valid, dr, dc = np.broadcast_arrays(valid, dr, dc)
            vals = np.where(valid[None], rpb[:, dr, dc], NEG)
            t = vals.transpose(1, 2, 0, 3, 4).reshape(8, 128, H, 128)
            out[eb, i] = t.transpose(1, 2, 0, 3).astype(NPBF)
    return out


def _col(v, nchunk):
    return np.ascontiguousarray(np.asarray(v, np.float32).reshape(nchunk, 128).T)


def build(stop="all", debug=False):
    nc = bass.Bass("TRN2", target_bir_lowering=False)

    def din(name, shape, dt=F32):
        return nc.dram_tensor(name, list(shape), dt, kind="ExternalInput").ap()

    skind = "ExternalOutput" if debug else "Internal"

    def dscr(name, shape, dt):
        return nc.dram_tensor(name, list(shape), dt, kind=skind).ap()

    x_d = din("x", [S, D]); xloc_d = din("xloc", [NL, D]); hsel_d = din("hsel", [128, 2]); ctx_d = din("ctx", [LC, D]); ccols_d = din("ccols", [128, 16])
    wmod_d = din("w_mod", [2, D, 6 * D]); bmodc_d = din("bmodc", [128, 2, 48]); bmod_d = din("b_mod", [2, 6 * D])
    gcols_d = din("gcols", [128, 2, 2, 8]); gpm_d = din("g_post_mix", [2, D]); gpf_d = din("g_post_ffn", [2, D])
    win_d = din("w_in_ab", [D, 1536]); woab_d = din("w_out_ab", [D, D])
    wg_d = din("w_ffn_gate", [2, D, FH]); wu_d = din("w_ffn_up", [2, D, FH]); wd_d = din("w_ffn_down", [2, FH, D])
    wqkv_d = din("w_qkv_na", [D, 3 * D]); wona_d = din("w_out_na", [D, D])
    wbd_d = din("wbd", [2, 2, 4, 128, 128]); lcols_d = din("lcols", [128, 4, 2, 3]); convc_d = din("convc", [128, 4, 5]); cdiag_d = din("cdiag", [4, 4, 128, 128])
    ident_d = din("ident", [128, 128]); identb_d = din("identb", [128, 128], BF16)
    T1_d = din("T1", [128, 256], BF16); M2_d = din("M2", [64, 128, 192], BF16); CSf_d = din("CSf", [128, 256], BF16)
    CSc_d = din("CSc", [128, 256], BF16); T256_d = din("T256", [128, 2, 512], BF16)
    BT_d = din("BT", [128, 16, 5, 128], BF16); BTE_d = din("BTE", [2, 4, 128, 16, 8, 128], BF16)
    out_d = nc.dram_tensor("out", [4096, D], F32, kind="ExternalOutput").ap()

    UG = dscr("UG", [8, 128, S], F32)
    UGc = dscr("UGc", [8, 128, LC], F32)
    MIXT = dscr("MIXT", [D, NT], BF16)
    GROW = dscr("GROW", [2, 2, 2, D], F32)
    X1 = dscr("X1", [NL, D], F32); C1 = dscr("C1", [LC, D], F32)
    X2 = dscr("X2", [NL, D], F32); C2 = dscr("C2", [LC, D], F32)
    X3 = dscr("X3", [NL, D], F32)
    QT = dscr("QT", [D, NL], BF16); KT = dscr("KT", [D, NL + LC], BF16); VD = dscr("VD", [NL + LC, D], BF16)

    WBF = {"woab": dscr("woab_bf", [D, D], BF16), "wona": dscr("wona_bf", [D, D], BF16),
           "wqkv": dscr("wqkv_bf", [D, 3 * D], BF16),
           "wg": dscr("wg_bf", [2, D, FH], BF16), "wu": dscr("wu_bf", [2, D, FH], BF16), "wd": dscr("wd_bf", [2, FH, D], BF16)}

    with ExitStack() as es:
        em = Em(nc, es)

        def barrier():
            for e in ("pe", "act", "dve", "pool", "sp"):
                eng = em.eng[e]
                for f in ("pe", "act", "dve", "pool"):
                    if f != e and em.cnt[f] > 0 and em.waited[e].get(em.semkey[f], 0) < em.cnt[f]:
                        eng.wait_ge(em.sem[f], em.cnt[f]); em.waited[e][em.semkey[f]] = em.cnt[f]
                for q in em.dsem:
                    n = em.ndma[q]
                    for s_ in range(em.NDS):
                        uses = (n - s_ + em.NDS - 1) // em.NDS if n > s_ else 0
                        sk = ("dma", q, s_)
                        if uses > 0 and em.waited[e].get(sk, 0) < 16 * uses:
                            eng.wait_ge(em.dsem[q][s_], 16 * uses); em.waited[e][sk] = 16 * uses

        PS = es.enter_context(nc.psum_tensor("PS", [128, 8, 512], F32))
        ident = es.enter_context(nc.sbuf_tensor("ident_s", [128, 128], F32))
        identb = es.enter_context(nc.sbuf_tensor("identb_s", [128, 128], BF16))
        MODC = es.enter_context(nc.sbuf_tensor("MODC", [128, 2, 2, 2, 2, 8], F32))
        mhalf = es.enter_context(nc.sbuf_tensor("mhalf", [128, 8], F32))
        em.dma("sp", ident[:], ident_d, W=["ident"])
        em.dma("sp", identb[:], identb_d, W=["identb"])
        em.op("dve", lambda: nc.vector.memset(mhalf[:], -0.5), W=["mhalf"])

        V = nc.vector; A = nc.scalar; G = nc.gpsimd; T = nc.tensor

        def pbank(b):
            return ("ps", b)

        def rstd_from_ssq(ssq, rs, n, tag):
            em.op("dve", lambda: V.tensor_scalar(out=rs[:, 0:n], in0=ssq[:, 0:n], scalar1=1.0 / D, scalar2=EPS,
                                                 op0=ALU.mult, op1=ALU.add), R=[tag + "ssq"], W=[tag + "rs"])
            em.op("pool", lambda: G.tensor_tensor(out=rs[:, 0:n], in0=rs[:, 0:n], in1=mhalf[:, 0:n], op=ALU.pow),
                  R=[tag + "rs", "mhalf"], W=[tag + "rs"])

        def prologue_a(xt, xkey, nsub, xs, xskey, ssq, rs, junk, tag):
            for s in range(nsub):
                em.op("act", lambda s=s: A.activation(out=junk[:, :], in_=xt[:, s, :], func=AF.Square,
                                                      accum_out=ssq[:, s:s + 1]),
                      R=[xkey], W=[tag + "junk", tag + "ssq"])
            rstd_from_ssq(ssq, rs, nsub, tag)
            for s in range(nsub):
                em.op("act", lambda s=s: A.activation(out=xs[:, s, :], in_=xt[:, s, :], func=AF.Copy,
                                                      scale=rs[:, s:s + 1]),
                      R=[xkey, tag + "rs"], W=[xskey])

        def prologue_b(nsub, xs, xskey, Acol, Bcol, hT, hkey, tb):
            for k in range(8):
                b = tb[k % len(tb)]
                for s in range(nsub):
                    em.op("pe", lambda s=s, k=k, b=b: T.transpose(PS[:, b, s * 128:(s + 1) * 128],
                                                                  xs[:, s, k * 128:(k + 1) * 128], ident[:]),
                          R=[xskey, "ident"], W=[pbank(b)], inc=(s == nsub - 1))
                em.op("act", lambda k=k, b=b: A.activation(out=hT[:, k, 0:nsub * 128], in_=PS[:, b, 0:nsub * 128],
                                                           func=AF.Identity, scale=Acol[:, k:k + 1],
                                                           bias=Bcol[:, k:k + 1]),
                      R=[pbank(b), "MODC"], W=[hkey])

        def prologue(xt, xkey, nsub, xs, xskey, Acol, Bcol, hT, hkey, ssq, rs, junk, tag, tb):
            prologue_a(xt, xkey, nsub, xs, xskey, ssq, rs, junk, tag)
            prologue_b(nsub, xs, xskey, Acol, Bcol, hT, hkey, tb)

        def epilogue(psb, xsub, xkey, Gb, gkey, ssq, rs, junk, tt, tag):
            yv = PS[:, psb:psb + 2, :]
            em.op("act", lambda: A.activation(out=junk[:, :], in_=yv, func=AF.Square, accum_out=ssq[:, 0:1]),
                  R=[pbank(psb), pbank(psb + 1)], W=[tag + "junk", tag + "ssq"])
            rstd_from_ssq(ssq, rs, 1, tag)
            em.op("dve", lambda: V.tensor_tensor(out=tt[:, :], in0=yv, in1=Gb, op=ALU.mult),
                  R=[pbank(psb), pbank(psb + 1), gkey], W=[tag + "tt"])
            em.op("dve", lambda: V.scalar_tensor_tensor(out=xsub, in0=tt[:, :], scalar=rs[:, 0:1], in1=xsub,
                                                        op0=ALU.mult, op1=ALU.add),
                  R=[tag + "tt", tag + "rs", xkey], W=[xkey])

        with ExitStack() as pes:
            sb = lambda n, s, d=F32: pes.enter_context(nc.sbuf_tensor(n, list(s), d))
            cc = sb("cc", [128, 16]); scc = sb("scc", [128, 16]); rhs2 = sb("rhs2", [128, 8, 2], BF16)
            bmc = sb("bmc", [128, 2, 48]); bm1 = sb("bm1", [128, 2, 48]); gco = sb("gco", [128, 2, 2, 8])
            bmrow = sb("bmrow", [2, 2, 2, D]); grow = sb("grow", [2, 2, 2, D]); grt = sb("grt", [2, 2, 2, D])
            wm = [sb(f"wm{i}", [128, 8, 512], BF16) for i in range(3)]
            em.dma("sp", cc[:], ccols_d, W=["cc"])
            em.dma("sp", bmc[:], bmodc_d, W=["bmc"])
            em.dma("sp", gco[:], gcols_d, W=["gco"])
            for l in range(2):
                for w_, (src, off) in enumerate([(bmod_d, 2 * D), (bmod_d, 5 * D)]):
                    em.dma("sp", bmrow[:, l, w_, :], src[l, off:off + D].partition_broadcast(2), W=["bmrow"])
                em.dma("sp", grow[:, l, 0, :], gpm_d[l, :].partition_broadcast(2), W=["grow"])
                em.dma("sp", grow[:, l, 1, :], gpf_d[l, :].partition_broadcast(2), W=["grow"])
            em.op("act", lambda: A.activation(out=scc[:], in_=cc[:], func=AF.Silu), R=["cc"], W=["scc"])
            em.op("dve", lambda: V.tensor_copy(out=rhs2[:, :, 0], in_=scc[:, 0:8]), R=["scc"], W=["rhs2"])
            em.op("dve", lambda: V.tensor_copy(out=rhs2[:, :, 1], in_=scc[:, 8:16]), R=["scc"], W=["rhs2"])
            em.op("dve", lambda: V.tensor_scalar(out=bm1[:], in0=bmc[:], scalar1=1.0, scalar2=None, op0=ALU.add),
                  R=["bmc"], W=["bm1"])
            it = 0
            for l in range(2):
                wsrc = wmod_d[l].rearrange("(k p) n -> p k n", p=128)
                for nb in range(12):
                    slot = it % 3; it += 1
                    wt = wm[slot]; wk = f"wm{slot}"
                    em.dma("pool", wt[:], wsrc[:, :, nb * 512:(nb + 1) * 512], W=[wk])
                    v = nb // 2; half = nb % 2
                    b = it % 4
                    if v in (2, 5):
                        w_ = 0 if v == 2 else 1
                        for k in range(8):
                            em.op("pe", lambda k=k, b=b, wt=wt: T.matmul(PS[0:2, b, :], lhsT=rhs2[:, k, :], rhs=wt[:, k, :],
                                                                        start=(k == 0), stop=(k == 7)),
                                  R=["rhs2", wk], W=[pbank(b)], inc=(k == 7))
                        dst = grt[:, l, w_, half * 512:(half + 1) * 512]
                        em.op("dve", lambda b=b, dst=dst, l=l, w_=w_, half=half: V.tensor_tensor(
                            out=dst, in0=PS[0:2, b, :], in1=bmrow[:, l, w_, half * 512:(half + 1) * 512], op=ALU.add),
                              R=[pbank(b), "bmrow"], W=["grt"])
                        em.op("dve", lambda dst=dst, l=l, w_=w_, half=half: V.tensor_tensor(
                            out=dst, in0=dst, in1=grow[:, l, w_, half * 512:(half + 1) * 512], op=ALU.mult),
                              R=["grt", "grow"], W=["grt"])
                        if half == 1:
                            em.dma("sp", GROW[l, w_, :, :], grt[:, l, w_, :], R=["grt"], W=[("GROW", l, w_)])
                    else:
                        sub = 0 if v < 2 else 1
                        isA = v in (1, 4)
                        for m in range(4):
                            ch = half * 4 + m
                            for k in range(8):
                                em.op("pe", lambda k=k, b=b, m=m, wt=wt: T.matmul(
                                    PS[:, b, 2 * m:2 * m + 2], lhsT=wt[:, k, m * 128:(m + 1) * 128], rhs=rhs2[:, k, :],
                                    start=(k == 0), stop=(k == 7)), R=["rhs2", wk], W=[pbank(b)], inc=(k == 7))
                            dst = MODC[:, l, sub, 0 if isA else 1, :, ch]
                            if isA:
                                em.op("dve", lambda b=b, m=m, dst=dst, l=l, v=v, ch=ch, sub=sub: V.tensor_scalar(
                                    out=dst, in0=PS[:, b, 2 * m:2 * m + 2], scalar1=bm1[:, l, v * 8 + ch:v * 8 + ch + 1],
                                    scalar2=gco[:, l, sub, ch:ch + 1], op0=ALU.add, op1=ALU.mult),
                                      R=[pbank(b), "bm1", "gco"], W=["MODC"])
                            else:
                                em.op("dve", lambda b=b, m=m, dst=dst, l=l, v=v, ch=ch: V.tensor_scalar(
                                    out=dst, in0=PS[:, b, 2 * m:2 * m + 2], scalar1=bmc[:, l, v * 8 + ch:v * 8 + ch + 1],
                                    scalar2=None, op0=ALU.add), R=[pbank(b), "bmc"], W=["MODC"])
            if debug:
                MODCd = nc.dram_tensor("MODCd", [128, 128], F32, kind="ExternalOutput").ap()
                em.dma("sp", MODCd, MODC[:].rearrange("p a b c d e -> p (a b c d e)"), R=["MODC"], W=["MODCd"])
            barrier()
        if stop == "0":
            em.finish()
            return nc
        C = type("C", (), {})()
        C.__dict__.update(locals())
        for name, fn in PHASES:
            fn(C)
            barrier()
            if stop == name:
                if hasattr(C, 'es_mix'):
                    C.es_mix.close()
                break
        em.finish()
    return nc


PHASES = []


def _host_inputs(inp, b):
    global _TABLES
    if _TABLES is None:
        _TABLES = _tables()
    f = lambda a: np.ascontiguousarray(np.asarray(a, dtype=np.float32))
    m = {}
    b, h = b // 2, b % 2
    m["x"] = f(inp["x"][b]); m["ctx"] = f(inp["ctx"][b])
    m["xloc"] = f(inp["x"][b].reshape(128, 64, D)[_local_rows(h)].reshape(NL, D))
    m["hsel"] = np.tile(np.array([[1.0 - h, float(h)]], np.float32), (128, 1))
    m["ccols"] = np.concatenate([_col(inp["c"][b], 8), _col(inp["c_ctx"], 8)], axis=1)
    m["w_mod"] = f(inp["w_mod"]); m["b_mod"] = f(inp["b_mod"])
    m["bmodc"] = np.ascontiguousarray(np.stack([_col(inp["b_mod"][l], 48) for l in range(2)], axis=1))
    m["gcols"] = np.ascontiguousarray(np.stack(
        [np.stack([_col(inp["g_pre_mix"][l], 8), _col(inp["g_pre_ffn"][l], 8)], axis=1) for l in range(2)], axis=1))
    m["g_post_mix"] = f(inp["g_post_mix"]); m["g_post_ffn"] = f(inp["g_post_ffn"])
    m["w_in_ab"] = f(inp["w_in_ab"][0]); m["w_out_ab"] = f(inp["w_out_ab"][0])
    m["w_ffn_gate"] = f(inp["w_ffn_gate"]); m["w_ffn_up"] = f(inp["w_ffn_up"]); m["w_ffn_down"] = f(inp["w_ffn_down"])
    m["w_qkv_na"] = f(inp["w_qkv_na"][0]); m["w_out_na"] = f(inp["w_out_na"][0])
    wbd = np.zeros((2, 2, 4, 128, 128), np.float32)
    for gi, key in enumerate(["lru_w_a", "lru_w_i"]):
        w = np.asarray(inp[key][0], np.float32)
        for d in range(2):
            for c in range(4):
                wbd[gi, d, c, 0:64, 0:64] = w[d, 2 * c]
                wbd[gi, d, c, 64:128, 64:128] = w[d, 2 * c + 1]
    m["wbd"] = wbd
    lc = np.zeros((128, 4, 2, 3), np.float32)
    for d in range(2):
        lc[:, :, d, 0] = _col(inp["lru_b_a"][0][d], 4)
        lc[:, :, d, 1] = _col(inp["lru_b_i"][0][d], 4)
        lc[:, :, d, 2] = _col(inp["lru_lam"][0][d], 4)
    m["lcols"] = lc
    cv = np.zeros((128, 4, 5), np.float32)
    for k in range(4):
        cv[:, :, k] = _col(inp["conv_w"][0][k], 4)
    cv[:, :, 4] = _col(inp["conv_b"][0], 4)
    m["convc"] = cv
    cd = np.zeros((4, 4, 128, 128), np.float32)
    ii = np.arange(128)
    for c in range(4):
        for k in range(4):
            cd[c, k, ii, ii] = cv[:, c, k]
    m["cdiag"] = cd
    for k in ("ident", "identb", "T1", "M2", "CSf", "CSc", "T256"):
        m[k] = _TABLES[k]
    rpb = np.asarray(inp["rpb_na"][0], np.float32)
    m["BT"] = np.ascontiguousarray(_bias_tables(rpb)[2])
    m["BTE"] = _edge_tables(rpb, h)
    return m


_NC_CACHE = {}


def kernel(**inputs):
    if "full" not in _NC_CACHE:
        _NC_CACHE["full"] = build()
    nc = _NC_CACHE["full"]
    in_maps = [_host_inputs(inputs, b) for b in range(NCORES)]
    res = run_bass_kernel_spmd(nc, in_maps, core_ids=list(range(NCORES)))
    out = np.empty((4, S, D), np.float32)
    for c in range(NCORES):
        b, h = c // 2, c % 2
        o = np.asarray(res.results[c]["out"], dtype=np.float32)
        out[b, 4096 * h:4096 * (h + 1)] = o
    return out


def _wload(C, dst, src_view, key, nk, ncols, step=512):
    for bi, c0 in enumerate(range(0, ncols, step)):
        c1 = min(ncols, c0 + step)
        C.em.dma("pool", dst[:, :, c0:c1], src_view[:, :, c0:c1], W=[key + str(bi)])


def _wload_bf(C, dst, src_view, key, ncols, srckey, step=1024):
    cv_emit(C, 0, need=srckey)
    for bi, c0 in enumerate(range(0, ncols, step)):
        c1 = min(ncols, c0 + step)
        C.em.dma("sp", dst[:, :, c0:c1], src_view[:, :, c0:c1], R=C.cvkeys[srckey], W=[key + str(bi)])


def preconvert_init(C):
    jobs = [("woab", C.woab_d, C.WBF["woab"]), ("wg0", C.wg_d[0], C.WBF["wg"][0]), ("wu0", C.wu_d[0], C.WBF["wu"][0]),
            ("wd0", C.wd_d[0], C.WBF["wd"][0]), ("wqkv", C.wqkv_d, C.WBF["wqkv"]), ("wona", C.wona_d, C.WBF["wona"]),
            ("wg1", C.wg_d[1], C.WBF["wg"][1]), ("wu1", C.wu_d[1], C.WBF["wu"][1]), ("wd1", C.wd_d[1], C.WBF["wd"][1])]
    C.cvkeys = {}
    C.cvq = []
    for key, src, dst in jobs:
        rows, cols = src.shape
        rstep = 512 if rows % 512 == 0 else 704
        C.cvkeys["cv" + key] = []
        for r0 in range(0, rows, rstep):
            for c0 in range(0, cols, 1024):
                c1 = min(cols, c0 + 1024)
                k_ = f"cv{key}_{r0}_{c0}"
                C.cvkeys["cv" + key].append(k_)
                C.cvq.append(("cv" + key, k_, dst[r0:r0 + rstep, c0:c1], src[r0:r0 + rstep, c0:c1]))


def cv_emit(C, n=1, need=None):
    while C.cvq and (n > 0 or (need is not None and any(j[0] == need for j in C.cvq))):
        big, k_, dst, src = C.cvq.pop(0)
        C.em.dma_bg("pool", dst, src, W=[k_])
        n -= 1


def phase_A(C):
    nc, em, PS = C.nc, C.em, C.PS
    V, A, G, T = nc.vector, nc.scalar, nc.gpsimd, nc.tensor
    pes = C.es_mix = ExitStack()
    preconvert_init(C)
    C.F = pes.enter_context(nc.sbuf_tensor("Fbuf", [128, 64, 512], BF16))
    C.fTc = pes.enter_context(nc.sbuf_tensor("fTc", [128, 4, LC], BF16))
    F, fTc = C.F, C.fTc
    with ExitStack() as ps_:
        sb = lambda n, s, d=F32: ps_.enter_context(nc.sbuf_tensor(n, list(s), d))
        win = sb("win", [128, 8, 1536], BF16)
        xt = [sb(f"Axt{i}", [128, 4, D]) for i in range(2)]
        hT = [sb(f"AhT{i}", [128, 8, 512], BF16) for i in range(2)]
        ugst = [sb(f"Aug{i}", [128, 8, 512]) for i in range(2)]
        ssq = sb("Assq", [128, 4]); rs = sb("Ars", [128, 4]); junk = sb("Ajunk", [128, D], BF16)
        _wload(C, win, C.win_d.rearrange("(k p) n -> p k n", p=128), "win", 8, 1536)
        xsrc = C.x_d.rearrange("(t1 t2) d -> t1 t2 d", t2=64)
        Acol = C.MODC[:, 0, 0, 0, 0, :]; Bcol = C.MODC[:, 0, 0, 1, 0, :]
        AcolC = C.MODC[:, 0, 0, 0, 1, :]; BcolC = C.MODC[:, 0, 0, 1, 1, :]
        evc = [0]

        def loadA(j):
            sl = j % 2
            if j < 16:
                em.dma("sp", xt[sl][:], xsrc[:, 4 * j:4 * (j + 1), :], W=[f"Axt{sl}"])
            elif j == 16:
                em.dma("sp", xt[sl][:, 0:2, :], C.ctx_d.rearrange("(s p) d -> p s d", p=128), W=[f"Axt{sl}"])

        def nsubA(j):
            return 2 if j == 16 else 4

        def pro_a(j):
            sl = j % 2
            C.prologue_a(xt[sl], f"Axt{sl}", nsubA(j), xt[sl], f"Axt{sl}", ssq, rs, junk, "A")

        def pro_b(j):
            sl = j % 2
            isctx = j == 16
            C.prologue_b(nsubA(j), xt[sl], f"Axt{sl}", AcolC if isctx else Acol, BcolC if isctx else Bcol,
                         hT[sl], f"AhT{sl}", [0, 1])

        def ugA(j):
            sl = j % 2
            isctx = j == 16
            n = nsubA(j) * 128
            for oc in range(8):
                b = 2 + oc % 3
                for k in range(8):
                    em.op("pe", lambda k=k, b=b, oc=oc, sl=sl, n=n: T.matmul(
                        PS[:, b, 0:n], lhsT=win[:, k, oc * 128:(oc + 1) * 128], rhs=hT[sl][:, k, 0:n],
                        start=(k == 0), stop=(k == 7)), R=["win" + str(oc // 4), f"AhT{sl}"], W=[("ps", b)], inc=(k == 7))
                evc[0] += 1
                if evc[0] % 2:
                    em.op("dve", lambda b=b, oc=oc, sl=sl, n=n: V.tensor_copy(out=ugst[sl][:, oc, 0:n], in_=PS[:, b, 0:n]),
                          R=[("ps", b)], W=[f"Aug{sl}"])
                else:
                    em.op("act", lambda b=b, oc=oc, sl=sl, n=n: A.copy(out=ugst[sl][:, oc, 0:n], in_=PS[:, b, 0:n]),
                          R=[("ps", b)], W=[f"Aug{sl}"])
            if isctx:
                em.dma("pool", C.UGc.rearrange("c p n -> p c n"), ugst[sl][:, :, 0:LC], R=[f"Aug{sl}"], W=["UGc"])
            else:
                em.dma("pool", C.UG[:, :, j * 512:(j + 1) * 512].rearrange("c p n -> p c n"), ugst[sl][:],
                       R=[f"Aug{sl}"], W=[("UG", j)])

        def fA(j):
            sl = j % 2
            if j == 16:
                for fc in range(4):
                    b = 5 + fc % 3
                    for k in range(8):
                        em.op("pe", lambda k=k, b=b, fc=fc, sl=sl: T.matmul(
                            PS[:, b, 0:LC], lhsT=win[:, k, 1024 + fc * 128:1024 + (fc + 1) * 128], rhs=hT[sl][:, k, 0:LC],
                            start=(k == 0), stop=(k == 7)), R=["win2", f"AhT{sl}"], W=[("ps", b)], inc=(k == 7))
                    em.op("dve", lambda b=b, fc=fc: V.tensor_copy(out=fTc[:, fc, :], in_=PS[:, b, 0:LC]),
                          R=[("ps", b)], W=["fTc"])
            else:
                for s in range(4):
                    b = 5 + s % 3
                    for k in range(8):
                        em.op("pe", lambda k=k, b=b, s=s, sl=sl: T.matmul(
                            PS[:, b, :], lhsT=hT[sl][:, k, s * 128:(s + 1) * 128], rhs=win[:, k, 1024:1536],
                            start=(k == 0), stop=(k == 7)), R=["win2", f"AhT{sl}"], W=[("ps", b)], inc=(k == 7))
                    evc[0] += 1
                    if evc[0] % 2:
                        em.op("dve", lambda b=b, s=s, j=j: V.tensor_copy(out=F[:, 4 * j + s, :], in_=PS[:, b, :]),
                              R=[("ps", b)], W=["F"])
                    else:
                        em.op("act", lambda b=b, s=s, j=j: A.copy(out=F[:, 4 * j + s, :], in_=PS[:, b, :]),
                              R=[("ps", b)], W=["F"])

        loadA(0)
        loadA(1)
        pro_a(0)
        pro_b(0)
        for j in range(17):
            ugA(j)
            cv_emit(C, 1)
            if j + 2 < 17:
                loadA(j + 2)
            if j + 1 < 17:
                pro_a(j + 1)
            fA(j)
            if j + 1 < 17:
                pro_b(j + 1)


def phase_C(C):
    nc, em, PS, F, fTc = C.nc, C.em, C.PS, C.F, C.fTc
    V, A, G, T = nc.vector, nc.scalar, nc.gpsimd, nc.tensor
    with ExitStack() as ps_:
        sb = lambda n, s, d=F32: ps_.enter_context(nc.sbuf_tensor(n, list(s), d))
        T1 = sb("T1s", [128, 256], BF16); M2 = sb("M2s", [64, 128, 192], BF16); CSf = sb("CSfs", [128, 256], BF16)
        CSc = sb("CScs", [128, 256], BF16); T256 = sb("T256s", [128, 2, 512], BF16)
        Ast = sb("Ast", [64, 64, 256], BF16); Y = sb("Ybuf", [128, 2, S], BF16)
        fst = [sb(f"fst{i}", [128, 2048], BF16) for i in range(2)]
        Gc = sb("Gcb", [128, 2, 256], BF16); fcs = sb("fcs", [128, LC], BF16)
        for dst, src, k in ((T1, C.T1_d, "T1"), (M2, C.M2_d, "M2"), (CSf, C.CSf_d, "CSf"), (CSc, C.CSc_d, "CSc"),
                            (T256, C.T256_d, "T256")):
            em.dma("sp", dst[:], src, W=[k])
        ev = 0
        for cc in range(4):
            for hc in range(2):
                for g4 in range(16):
                    b = 2 * (g4 % 2)
                    for q in range(4):
                        ch = cc * 128 + hc * 64 + g4 * 4 + q
                        em.op("pe", lambda b=b, q=q, ch=ch: T.matmul(
                            PS[0:64, b + q // 2, (q % 2) * 256:(q % 2) * 256 + 256], lhsT=F[:, :, ch], rhs=T1[:, :],
                            start=True, stop=True), R=["F", "T1"], W=[("ps", b), ("ps", b + 1)], inc=(q == 3))
                    ev += 1
                    dst = Ast[:, g4 * 4:(g4 + 1) * 4, :].rearrange("p a b -> p (a b)")
                    src = PS[0:64, b:b + 2, :].rearrange("p a b -> p (a b)")
                    if ev % 2:
                        em.op("dve", lambda dst=dst, src=src: V.tensor_copy(out=dst, in_=src),
                              R=[("ps", b), ("ps", b + 1)], W=["Ast"])
                    else:
                        em.op("act", lambda dst=dst, src=src: A.copy(out=dst, in_=src),
                              R=[("ps", b), ("ps", b + 1)], W=["Ast"])
                for kb in range(32):
                    b = 4 + kb % 2
                    for q in range(4):
                        k1 = kb * 4 + q
                        o = PS[hc * 64:(hc + 1) * 64, b, q * 128:(q + 1) * 128]
                        em.op("pe", lambda o=o, k1=k1: T.matmul(o, lhsT=Ast[:, :, k1], rhs=M2[:, k1, 64:192],
                                                                start=True, stop=False),
                              R=["Ast", "M2"], W=[("ps", b)], inc=False)
                        em.op("pe", lambda o=o, k1=k1: T.matmul(o, lhsT=Ast[:, :, 128 + k1], rhs=M2[:, k1, 0:128],
                                                                start=False, stop=True),
                              R=["Ast", "M2"], W=[("ps", b)], inc=(q == 3))
                    ev += 1
                    src = PS[hc * 64:(hc + 1) * 64, b, :].rearrange("p (k r c) -> p r c k", k=4, r=2)
                    dst = Y[hc * 64:(hc + 1) * 64, :, :].rearrange("p r (c k) -> p r c k", k=128)[:, :, :, kb * 4:(kb + 1) * 4]
                    if ev % 2:
                        em.op("dve", lambda dst=dst, src=src: V.tensor_copy(out=dst, in_=src), R=[("ps", b)], W=["Y"])
                    else:
                        em.op("act", lambda dst=dst, src=src: A.copy(out=dst, in_=src), R=[("ps", b)], W=["Y"])
            for tl in range(16):
                b = 6 + tl % 2
                em.op("pe", lambda b=b, tl=tl: T.matmul(PS[:, b, :], lhsT=CSf[:, 0:128], rhs=Y[:, 0, tl * 512:(tl + 1) * 512],
                                                        start=True, stop=False), R=["CSf", "Y"], W=[("ps", b)], inc=False)
                em.op("pe", lambda b=b, tl=tl: T.matmul(PS[:, b, :], lhsT=CSf[:, 128:256], rhs=Y[:, 1, tl * 512:(tl + 1) * 512],
                                                        start=False, stop=True), R=["CSf", "Y"], W=[("ps", b)], inc=True)
                fs = (tl // 4) % 2
                em.op("act", lambda b=b, tl=tl, fs=fs: A.copy(out=fst[fs][:, (tl % 4) * 512:(tl % 4 + 1) * 512], in_=PS[:, b, :]),
                      R=[("ps", b)], W=[f"fst{fs}"])
                if tl % 4 == 3:
                    t0 = (tl // 4) * 2048
                    em.dma("pool", C.MIXT[512 + cc * 128:512 + (cc + 1) * 128, t0:t0 + 2048], fst[fs][:],
                           R=[f"fst{fs}"], W=[("MIXT", 4 + cc)])
            for tc in range(2):
                em.op("pe", lambda tc=tc, cc=cc: T.matmul(PS[:, 0, 0:256], lhsT=fTc[:, cc, tc * 128:(tc + 1) * 128], rhs=CSc[:, :],
                                                          start=True, stop=True), R=["fTc", "CSc"], W=[("ps", 0)])
                em.op("dve", lambda tc=tc: V.tensor_copy(out=Gc[:, tc, :], in_=PS[:, 0, 0:256]), R=[("ps", 0)], W=["Gc"])
            for i_, (tc, part) in enumerate([(0, 0), (0, 1), (1, 0), (1, 1)]):
                em.op("pe", lambda i_=i_, tc=tc, part=part: T.matmul(
                    PS[:, 1, 0:256], lhsT=Gc[:, tc, part * 128:(part + 1) * 128], rhs=T256[:, tc, part * 256:(part + 1) * 256],
                    start=(i_ == 0), stop=(i_ == 3)), R=["Gc", "T256"], W=[("ps", 1)], inc=(i_ == 3))
            em.op("dve", lambda: V.tensor_copy(out=fcs[:, :], in_=PS[:, 1, 0:256]), R=[("ps", 1)], W=["fcs"])
            em.dma("pool", C.MIXT[512 + cc * 128:512 + (cc + 1) * 128, S:NT], fcs[:], R=["fcs"], W=[("MIXTc", 4 + cc)])
    C.es_mix.close()


PHASES += [("A", phase_A), ("C", phase_C)]


def phase_B(C):
    nc, em, PS = C.nc, C.em, C.PS
    V, A, G, T = nc.vector, nc.scalar, nc.gpsimd, nc.tensor
    TW = 2048
    with ExitStack() as ps_:
        sb = lambda n, s, d=F32: ps_.enter_context(nc.sbuf_tensor(n, list(s), d))
        bufA = sb("bufA", [128, NT]); bufB = sb("bufB", [128, 8460]); uc = sb("ucb_", [128, NT]); ucb = sb("ucbb", [128, NT], BF16)
        rt = sb("rt", [128, TW])
        it_ = [sb(f"it{i}", [128, TW]) for i in range(2)]; at = [sb(f"at{i}", [128, TW]) for i in range(2)]
        st = [sb(f"st{i}", [128, TW]) for i in range(2)]
        gtmp = [sb(f"gtmp{i}", [128, 1024]) for i in range(2)]
        wbd = sb("wbds", [128, 16, 128], BF16)
        cdg = sb("cdg", [128, 16, 128])
        lco = sb("lco", [128, 4, 2, 3]); cvc = sb("cvc", [128, 4, 5]); cA = sb("cA", [128, 4, 2]); carry = sb("carry", [128, 2])
        em.dma("pool", wbd[:], C.wbd_d.rearrange("g d c p n -> p (g d c) n"), W=["wbd"])
        em.dma("sp", cdg[:], C.cdiag_d.rearrange("c k p n -> p (c k) n"), W=["cdg"])
        em.dma("sp", lco[:], C.lcols_d, W=["lco"])
        em.dma("sp", cvc[:], C.convc_d, W=["cvc"])
        em.op("act", lambda: A.activation(out=cA[:], in_=lco[:, :, :, 2], func=AF.Exp, scale=-1.0), R=["lco"], W=["cA"])
        em.op("act", lambda: A.activation(out=cA[:], in_=cA[:], func=AF.Ln, bias=1.0), R=["cA"], W=["cA"])
        em.op("dve", lambda: V.tensor_scalar(out=cA[:], in0=cA[:], scalar1=-8.0, scalar2=None, op0=ALU.mult), R=["cA"], W=["cA"])
        em.op("dve", lambda: V.memset(bufB[:], 0.0), W=["bufB"])
        XO = 8200
        tiles = [(S, NT)] + [(i * TW, (i + 1) * TW) for i in range(4)]
        tix = 0
        for c in range(4):
            cv_emit(C, 2)
            em.dma("sp", bufA[:, 0:S], C.UG[c], W=["bufA"])
            em.dma("sp", bufA[:, S:NT], C.UGc[c], W=["bufA"])
            for qd in range(4):
                o_ = bufB[:, 2 + qd * 2048:2 + (qd + 1) * 2048].rearrange("p (a b) -> p a b", b=64)
                i_ = bufA[:, 0:S].rearrange("p (b a) -> p a b", a=128)[:, qd * 32:(qd + 1) * 32, :]
                eng = QENG[qd]
                fn = {"act": (lambda o_=o_, i_=i_: A.copy(out=o_, in_=i_)),
                      "dve": (lambda o_=o_, i_=i_: V.tensor_copy(out=o_, in_=i_)),
                      "pool": (lambda o_=o_, i_=i_: G.tensor_copy(out=o_, in_=i_))}[eng]
                em.op(eng, fn, R=["bufA"], W=[f"bufBq{qd}"])
            em.op("pool", lambda: G.tensor_copy(out=bufB[:, XO + 2:XO + 2 + LC], in_=bufA[:, S:NT]), R=["bufA"], W=["bufBc"])
            em.dma("sp", bufA[:, 0:S], C.UG[4 + c], W=["bufA"])
            em.dma("sp", bufA[:, S:NT], C.UGc[4 + c], W=["bufA"])
            for (o0, src0, n) in ((0, 0, S), (S, XO, LC)):
                for q0 in range(0, n, 2048):
                    nn = min(2048, n - q0)
                    nq = (nn + 511) // 512
                    pb = 4 * ((q0 // 2048) % 2)
                    qd = q0 // 2048
                    rk = ["bufBc", "bufB"] if n == LC else [f"bufBq{x}" for x in (qd - 1, qd, qd + 1) if 0 <= x < 4] + ["bufB"]
                    for q in range(nq):
                        w_ = min(512, nn - q * 512)
                        for k in range(4):
                            em.op("pe", lambda q=q, k=k, w_=w_, pb=pb, src0=src0, q0=q0, c=c: T.matmul(
                                PS[:, pb + q, 0:w_], lhsT=cdg[:, c * 4 + k, :],
                                rhs=bufB[:, src0 + q0 + q * 512 + k:src0 + q0 + q * 512 + k + w_],
                                start=(k == 0), stop=(k == 3)), R=["cdg"] + rk, W=[("ps", pb + q)], inc=(k == 3))
                    src = PS[:, pb:pb + 4, :].rearrange("p a b -> p (a b)")[:, 0:nn]
                    em.op("act", lambda src=src, o0=o0, q0=q0, nn=nn, c=c: A.activation(
                        out=uc[:, o0 + q0:o0 + q0 + nn], in_=src, func=AF.Identity, bias=cvc[:, c, 4:5]),
                          R=[("ps", pb + q) for q in range(4)] + ["cvc"], W=["uc"])
                    em.op("act", lambda src=src, o0=o0, q0=q0, nn=nn, c=c: A.activation(
                        out=ucb[:, o0 + q0:o0 + q0 + nn], in_=src, func=AF.Identity, bias=cvc[:, c, 4:5]),
                          R=[("ps", pb + q) for q in range(4)] + ["cvc"], W=["ucb"])

            def gelu_s1(lo, hi, p):
                n = hi - lo
                em.op("act", lambda: A.activation(out=gtmp[p][:, 0:n], in_=bufA[:, lo:hi], func=AF.Square),
                      R=["bufA"], W=[f"gt{p}"])
                em.op("pool", lambda: G.tensor_scalar(out=gtmp[p][:, 0:n], in0=gtmp[p][:, 0:n], scalar1=0.044715, scalar2=1.0,
                                                      op0=ALU.mult, op1=ALU.add), R=[f"gt{p}"], W=[f"gt{p}"])
                em.op("dve", lambda: V.tensor_tensor(out=gtmp[p][:, 0:n], in0=gtmp[p][:, 0:n], in1=bufA[:, lo:hi], op=ALU.mult),
                      R=[f"gt{p}", "bufA"], W=[f"gt{p}"])

            def gelu_s2(lo, hi, p):
                n = hi - lo
                em.op("act", lambda: A.activation(out=gtmp[p][:, 0:n], in_=gtmp[p][:, 0:n], func=AF.Sigmoid,
                                                  scale=1.5957691216057308), R=[f"gt{p}"], W=[f"gt{p}"])
                em.op("dve", lambda: V.tensor_tensor(out=bufA[:, lo:hi], in0=gtmp[p][:, 0:n], in1=bufA[:, lo:hi], op=ALU.mult),
                      R=[f"gt{p}", "bufA"], W=["bufA"])

            gpieces = [(S, NT)] + [(i * 1024, (i + 1) * 1024) for i in range(8)]
            gstate = {"s1": 0, "s2": 0}

            def gelu_push1():
                k = gstate["s1"]
                if k < len(gpieces):
                    gelu_s1(gpieces[k][0], gpieces[k][1], k % 2)
                    gstate["s1"] += 1

            def gelu_push2():
                k = gstate["s2"]
                if k < gstate["s1"]:
                    gelu_s2(gpieces[k][0], gpieces[k][1], k % 2)
                    gstate["s2"] += 1

            for d in range(2):
                order = [tiles[0]] + (tiles[1:] if d == 0 else tiles[1:][::-1])
                for ti, (lo, hi) in enumerate(order):
                    p = tix % 2
                    tix += 1
                    n = hi - lo
                    nq = (n + 511) // 512
                    for gi in range(2):
                        for q in range(nq):
                            w_ = min(512, n - q * 512)
                            em.op("pe", lambda gi=gi, q=q, w_=w_, lo=lo, d=d, c=c: T.matmul(
                                PS[:, gi * 4 + q, 0:w_], lhsT=wbd[:, gi * 8 + d * 4 + c, :], rhs=ucb[:, lo + q * 512:lo + q * 512 + w_],
                                start=True, stop=True), R=["wbd", "ucb"], W=[("ps", gi * 4 + q)], inc=(q == nq - 1))
                    pr = PS[:, 0:4, :].rearrange("p a b -> p (a b)")[:, 0:n]
                    pi = PS[:, 4:8, :].rearrange("p a b -> p (a b)")[:, 0:n]
                    if d == 0:
                        gelu_push2()
                        gelu_push2()
                    em.op("act", lambda pr=pr, n=n, c=c, d=d: A.activation(out=rt[:, 0:n], in_=pr, func=AF.Sigmoid,
                                                                           bias=lco[:, c, d, 0:1]),
                          R=[("ps", q) for q in range(4)] + ["lco"], W=["rt"])
                    em.op("act", lambda pi=pi, n=n, c=c, d=d, p=p: A.activation(out=it_[p][:, 0:n], in_=pi, func=AF.Sigmoid,
                                                                                bias=lco[:, c, d, 1:2]),
                          R=[("ps", 4 + q) for q in range(4)] + ["lco"], W=[f"it{p}"])
                    em.op("act", lambda n=n, c=c, d=d, p=p: A.activation(out=at[p][:, 0:n], in_=rt[:, 0:n], func=AF.Exp,
                                                                         scale=cA[:, c, d:d + 1]), R=["rt", "cA"], W=[f"at{p}"])
                    em.op("dve", lambda n=n, p=p: V.tensor_tensor(out=st[p][:, 0:n], in0=at[p][:, 0:n], in1=at[p][:, 0:n], op=ALU.mult),
                          R=[f"at{p}"], W=[f"st{p}"])
                    if d == 0:
                        gelu_push1()
                        gelu_push1()
                    em.op("act", lambda n=n, p=p: A.activation(out=st[p][:, 0:n], in_=st[p][:, 0:n], func=AF.Sqrt, scale=-1.0, bias=1.0),
                          R=[f"st{p}"], W=[f"st{p}"])
                    em.op("dve", lambda n=n, p=p: V.tensor_tensor(out=it_[p][:, 0:n], in0=it_[p][:, 0:n], in1=st[p][:, 0:n], op=ALU.mult),
                          R=[f"it{p}", f"st{p}"], W=[f"it{p}"])
                    em.op("dve", lambda n=n, lo=lo, hi=hi, p=p: V.tensor_tensor(out=it_[p][:, 0:n], in0=it_[p][:, 0:n], in1=uc[:, lo:hi],
                                                                               op=ALU.mult), R=[f"it{p}", "uc"], W=[f"it{p}"])
                    init = 0.0 if ti == 0 else carry[:, d:d + 1]
                    if d == 0:
                        em.op("dve", lambda n=n, lo=lo, hi=hi, init=init, p=p: V.tensor_tensor_scan(
                            out=bufB[:, lo:hi], data0=at[p][:, 0:n], data1=it_[p][:, 0:n], initial=init, op0=ALU.mult, op1=ALU.add),
                              R=[f"at{p}", f"it{p}", "carry", "bufB", "uc"], W=["bufB", "bufBq0", "bufBq1", "bufBq2", "bufBq3", "bufBc"])
                        em.op("dve", lambda hi=hi: V.tensor_copy(out=carry[:, 0:1], in_=bufB[:, hi - 1:hi]), R=["bufB"], W=["carry"])
                    else:
                        em.op("dve", lambda n=n, init=init, p=p: V.tensor_tensor_scan(
                            out=st[p][:, 0:n][:, ::-1], data0=at[p][:, 0:n][:, ::-1],
                            data1=it_[p][:, 0:n][:, ::-1], initial=init, op0=ALU.mult, op1=ALU.add),
                              R=[f"at{p}", f"it{p}", "carry", f"st{p}"], W=[f"st{p}"])
                        em.op("dve", lambda p=p: V.tensor_copy(out=carry[:, 1:2], in_=st[p][:, 0:1]), R=[f"st{p}"], W=["carry"])
                        em.op("dve", lambda n=n, lo=lo, hi=hi, p=p: V.tensor_tensor(out=bufB[:, lo:hi], in0=bufB[:, lo:hi], in1=st[p][:, 0:n],
                                                                                    op=ALU.add), R=["bufB", f"st{p}"], W=["bufB"])
            while gstate["s2"] < len(gpieces):
                gelu_push1()
                gelu_push2()
            em.op("dve", lambda: V.tensor_tensor(out=ucb[:, 0:S].rearrange("p (a b) -> p a b", b=64),
                                                 in0=bufB[:, 0:S].rearrange("p (a b) -> p a b", b=64),
                                                 in1=bufA[:, 0:S].rearrange("p (b a) -> p a b", a=128), op=ALU.mult),
                  R=["bufB", "bufBq0", "bufBq1", "bufBq2", "bufBq3", "bufBc", "bufA", "ucb"], W=["ucb"])
            em.op("dve", lambda: V.tensor_tensor(out=ucb[:, S:NT], in0=bufB[:, S:NT], in1=bufA[:, S:NT], op=ALU.mult),
                  R=["bufB", "bufBq0", "bufBq1", "bufBq2", "bufBq3", "bufBc", "bufA", "ucb"], W=["ucb"])
            em.dma("pool", C.MIXT[c * 128:(c + 1) * 128, :], ucb[:, :], R=["ucb"], W=[("MIXT", c)])
            if c < 3:
                em.op("dve", lambda: V.memset(bufB[:, 0:2], 0.0), R=["bufB"], W=["bufB", "bufBq0"])
                em.op("dve", lambda: V.memset(bufB[:, S:8460], 0.0), R=["bufB"], W=["bufB", "bufBq3", "bufBc"])


def _tok_tiles(ntok_tile):
    return None


def phase_D1(C, layer=0):
    nc, em, PS = C.nc, C.em, C.PS
    V, A, G, T = nc.vector, nc.scalar, nc.gpsimd, nc.tensor
    with ExitStack() as ps_:
        sb = lambda n, s, d=F32: ps_.enter_context(nc.sbuf_tensor(n, list(s), d))
        wo = sb("D1wo", [128, 8, D], BF16)
        xt = [sb(f"D1xt{i}", [128, 4, D]) for i in range(3)]
        mt = [sb(f"D1mt{i}", [128, 8, 512], BF16) for i in range(3)]
        mb = [sb(f"D1mb{i}", [128, 8, 512], BF16) for i in range(3)]
        hs = sb("D1hs", [128, 2])
        Gx = sb("D1Gx", [128, D]); Gc = sb("D1Gc", [128, D])
        ssq = sb("D1ssq", [128, 4]); rs = sb("D1rs", [128, 4]); junk = sb("D1junk", [128, D], BF16); tt = sb("D1tt", [128, D])
        _wload_bf(C, wo, C.WBF["woab"].rearrange("(k p) n -> p k n", p=128), "D1wo", D, "cvwoab")
        em.dma("sp", hs[:], C.hsel_d, W=["D1hs"])
        em.dma("sp", Gx[:], C.GROW[0, 0, 0, :].partition_broadcast(128), R=[("GROW", 0, 0)], W=["D1Gx"])
        em.dma("sp", Gc[:], C.GROW[0, 0, 1, :].partition_broadcast(128), R=[("GROW", 0, 0)], W=["D1Gc"])
        mixv = C.MIXT.rearrange("(k p) t -> p k t", p=128)
        NTL = NLT + 1

        def load(j, sl):
            if j < NLT:
                em.dma("sp", xt[sl][:], C.xloc_d[j * 512:(j + 1) * 512, :].rearrange("(s p) d -> p s d", p=128), W=[f"D1xt{sl}"])
                em.dma("sp", mt[sl][:], mixv[:, :, j * 512:(j + 1) * 512], W=[f"D1mt{sl}"])
                lb = 4096 + j * 512 if j < 8 else LBASE
                em.dma("sp", mb[sl][:], mixv[:, :, lb:lb + 512], W=[f"D1mb{sl}"])
                em.op("act", lambda sl=sl: A.activation(out=mt[sl][:].rearrange("p a b -> p (a b)"),
                                                        in_=mt[sl][:].rearrange("p a b -> p (a b)"), func=AF.Copy,
                                                        scale=hs[:, 0:1]), R=[f"D1mt{sl}", "D1hs"], W=[f"D1mt{sl}"])
                em.op("dve", lambda sl=sl: V.scalar_tensor_tensor(
                    out=mt[sl][:].rearrange("p a b -> p (a b)"), in0=mb[sl][:].rearrange("p a b -> p (a b)"), scalar=hs[:, 1:2],
                    in1=mt[sl][:].rearrange("p a b -> p (a b)"), op0=ALU.mult, op1=ALU.add),
                      R=[f"D1mt{sl}", f"D1mb{sl}", "D1hs"], W=[f"D1mt{sl}"])
            else:
                em.dma("sp", xt[sl][:, 0:2, :], C.ctx_d.rearrange("(s p) d -> p s d", p=128), W=[f"D1xt{sl}"])
                em.dma("sp", mt[sl][:, :, 0:LC], mixv[:, :, S:NT], W=[f"D1mt{sl}"])
        load(0, 0)
        load(1, 1)
        for j in range(NTL):
            sl = j % 3
            if j + 2 < NTL:
                load(j + 2, (j + 2) % 3)
            isx = j < NLT
            nsub = 4 if isx else 2
            for s in range(nsub):
                pb = 2 * (s % 4)
                for h in range(2):
                    for k in range(8):
                        em.op("pe", lambda k=k, h=h, s=s, sl=sl, pb=pb: T.matmul(
                            PS[:, pb + h, :], lhsT=mt[sl][:, k, s * 128:(s + 1) * 128], rhs=wo[:, k, h * 512:(h + 1) * 512],
                            start=(k == 0), stop=(k == 7)), R=["D1wo0", f"D1mt{sl}"], W=[("ps", pb + h)], inc=(k == 7))
                C.epilogue(pb, xt[sl][:, s, :], f"D1xt{sl}", (Gx if isx else Gc)[:, :], "D1Gx" if isx else "D1Gc",
                           ssq, rs, junk, tt, "D1")
            if isx:
                em.dma("pool", C.X1[j * 512:(j + 1) * 512, :].rearrange("(s p) d -> p s d", p=128), xt[sl][:],
                       R=[f"D1xt{sl}"], W=[("X1", j)])
            else:
                em.dma("pool", C.C1.rearrange("(s p) d -> p s d", p=128), xt[sl][:, 0:2, :], R=[f"D1xt{sl}"], W=["C1"])


def phase_FFN(C, layer, Xin, Cin, Xout, Cout, inkey, outkey, ntok=NL):
    nc, em, PS = C.nc, C.em, C.PS
    V, A, G, T = nc.vector, nc.scalar, nc.gpsimd, nc.tensor
    tg = f"F{layer}"
    with ExitStack() as ps_:
        sb = lambda n, s, d=F32: ps_.enter_context(nc.sbuf_tensor(tg + n, list(s), d))
        wg = sb("wg", [128, 8, FH], BF16); wu = sb("wu", [128, 8, FH], BF16); wd = sb("wd", [128, NJ, D], BF16)
        xt = sb("xt", [128, 4, D]); hT = sb("hT", [128, 8, 512], BF16); hh = sb("hh", [128, NJ, 512], BF16)
        est = [sb(f"est{i}", [128, D]) for i in range(2)]
        Gx = sb("Gx", [128, D]); tt = sb("tt", [128, D]); junk = sb("junk", [128, D], BF16)
        ssq = sb("ssq", [128, 4]); rs = sb("rs", [128, 4]); ssq2 = sb("ssq2", [128, 4]); rs2 = sb("rs2", [128, 4])
        NX = ntok // 512
        ntile = NX + (1 if Cin is not None else 0)

        def nsub_of(j):
            return 4 if j < NX else 2

        def rows(j, s):
            if j < NX:
                r0 = j * 512 + s * 128
                return Xin[r0:r0 + 128, :], Xout[r0:r0 + 128, :]
            return Cin[s * 128:(s + 1) * 128, :], Cout[s * 128:(s + 1) * 128, :]

        def load(j):
            ns = nsub_of(j)
            src = Xin[j * 512:(j + 1) * 512, :] if j < NX else Cin
            em.dma("sp", xt[:, 0:ns, :], src.rearrange("(s p) d -> p s d", p=128), W=[tg + "xt"])

        def pro_a(j):
            ns = nsub_of(j)
            C.prologue_a(xt, tg + "xt", ns, xt, tg + "xt", ssq2, rs2, junk, tg + "p")

        def pro_b(j):
            path = 0 if j < NX else 1
            C.prologue_b(nsub_of(j), xt, tg + "xt", C.MODC[:, layer, 1, 0, path, :], C.MODC[:, layer, 1, 1, path, :],
                         hT, tg + "hT", [0, 1, 2, 3])

        def gateup(j):
            n = nsub_of(j) * 128
            for jj in range(NJ):
                b = 2 * (jj % 2)
                for gi, w_ in enumerate((wg, wu)):
                    for k in range(8):
                        em.op("pe", lambda k=k, b=b, gi=gi, w_=w_, jj=jj: T.matmul(
                            PS[:, b + gi, 0:n], lhsT=w_[:, k, jj * 128:(jj + 1) * 128], rhs=hT[:, k, 0:n],
                            start=(k == 0), stop=(k == 7)), R=[tg + "wg" + str(x) for x in {jj * 128 // 704, (jj * 128 + 127) // 704}] + [tg + "wu" + str(x) for x in {jj * 128 // 704, (jj * 128 + 127) // 704}] + [tg + "hT"], W=[("ps", b + gi)],
                              inc=(k == 7))
                em.op("act", lambda b=b, jj=jj: A.activation(out=hh[:, jj, 0:n], in_=PS[:, b, 0:n], func=AF.Silu),
                      R=[("ps", b)], W=[tg + f"hh{jj}"])
                em.op("dve", lambda b=b, jj=jj: V.tensor_tensor(out=hh[:, jj, 0:n], in0=hh[:, jj, 0:n], in1=PS[:, b + 1, 0:n],
                                                               op=ALU.mult), R=[("ps", b + 1), tg + f"hh{jj}"], W=[tg + f"hh{jj}"])

        def down(j, s):
            pb = 4 + 2 * (s % 2)
            for h in range(2):
                for jj in range(NJ):
                    em.op("pe", lambda jj=jj, h=h, s=s, pb=pb: T.matmul(
                        PS[:, pb + h, :], lhsT=hh[:, jj, s * 128:(s + 1) * 128], rhs=wd[:, jj, h * 512:(h + 1) * 512],
                        start=(jj == 0), stop=(jj == NJ - 1)), R=[tg + "wd" + str(h), tg + f"hh{jj}"], W=[("ps", pb + h)], inc=(jj == NJ - 1))

        def eload(j, s):
            em.dma("sp", est[s % 2][:], rows(j, s)[0], W=[tg + f"est{s % 2}"])

        def epi(j, s):
            pb = 4 + 2 * (s % 2)
            C.epilogue(pb, est[s % 2][:, :], tg + f"est{s % 2}", Gx[:, :], tg + "Gx", ssq, rs, junk, tt, tg)
            em.dma("pool", rows(j, s)[1], est[s % 2][:], R=[tg + f"est{s % 2}"], W=[(outkey, j, s)])

        load(0)
        cv_emit(C, 0, need=f"cvwg{layer}"); cv_emit(C, 0, need=f"cvwu{layer}"); cv_emit(C, 0, need=f"cvwd{layer}")
        wgv = C.WBF["wg"][layer].rearrange("(k p) n -> p k n", p=128)
        wuv = C.WBF["wu"][layer].rearrange("(k p) n -> p k n", p=128)
        wdv = C.WBF["wd"][layer].rearrange("(k p) n -> p k n", p=128)
        for bi in range(4):
            em.dma("sp", wg[:, :, bi * 704:(bi + 1) * 704], wgv[:, :, bi * 704:(bi + 1) * 704], R=C.cvkeys[f"cvwg{layer}"], W=[tg + f"wg{bi}"])
            em.dma("sp", wu[:, :, bi * 704:(bi + 1) * 704], wuv[:, :, bi * 704:(bi + 1) * 704], R=C.cvkeys[f"cvwu{layer}"], W=[tg + f"wu{bi}"])
        for bi in range(2):
            em.dma("sp", wd[:, :, bi * 512:(bi + 1) * 512], wdv[:, :, bi * 512:(bi + 1) * 512], R=C.cvkeys[f"cvwd{layer}"], W=[tg + f"wd{bi}"])
        em.dma("sp", Gx[:], C.GROW[layer, 1, 0, :].partition_broadcast(128), W=[tg + "Gx"])
        pro_a(0)
        pro_b(0)
        for j in range(ntile):
            ns = nsub_of(j)
            if j == NX:
                em.dma("sp", Gx[:], C.GROW[layer, 1, 1, :].partition_broadcast(128), W=[tg + "Gx"])
            gateup(j)
            if layer == 0:
                cv_emit(C, 2)
            if j + 1 < ntile:
                load(j + 1)
                pro_a(j + 1)
            eload(j, 0)
            eload(j, 1)
            down(j, 0)
            down(j, 1)
            if j + 1 < ntile:
                pro_b(j + 1)
            epi(j, 0)
            epi(j, 1)
            if ns == 4:
                eload(j, 2)
                eload(j, 3)
                down(j, 2)
                down(j, 3)
                epi(j, 2)
                epi(j, 3)


PHASES += [("B", phase_B), ("D1", phase_D1),
           ("D2", lambda C: phase_FFN(C, 0, C.X1, C.C1, C.X2, C.C2, "X1", "X2"))]


def phase_E(C):
    nc, em, PS = C.nc, C.em, C.PS
    V, A, G, T = nc.vector, nc.scalar, nc.gpsimd, nc.tensor
    with ExitStack() as ps_:
        sb = lambda n, s, d=F32: ps_.enter_context(nc.sbuf_tensor("E" + n, list(s), d))
        wq = sb("wq", [128, 8, 3 * D], BF16)
        xt = [sb(f"xt{i}", [128, 4, D]) for i in range(2)]
        hT = [sb(f"hT{i}", [128, 8, 512], BF16) for i in range(2)]
        qk = [sb(f"qk{i}", [128, 16, 512], BF16) for i in range(2)]
        vst = [sb(f"vst{i}", [128, 4, D], BF16) for i in range(2)]
        ssq = sb("ssq", [128, 4]); rs = sb("rs", [128, 4]); junk = sb("junk", [128, D], BF16)
        QTv = C.QT.rearrange("(c p) t -> p c t", p=128); KTv = C.KT.rearrange("(c p) t -> p c t", p=128)
        NTL = NLT + 1
        evc = [0]

        def nsubE(j):
            return 2 if j == NLT else 4

        def load(j):
            sl = j % 2
            if j < NLT:
                em.dma("sp", xt[sl][:], C.X2[j * 512:(j + 1) * 512, :].rearrange("(s p) d -> p s d", p=128), W=[f"Ext{sl}"])
            else:
                em.dma("sp", xt[sl][:, 0:2, :], C.C2.rearrange("(s p) d -> p s d", p=128), W=[f"Ext{sl}"])

        def pro_a(j):
            sl = j % 2
            C.prologue_a(xt[sl], f"Ext{sl}", nsubE(j), xt[sl], f"Ext{sl}", ssq, rs, junk, "E")

        def pro_b(j):
            sl = j % 2
            path = 1 if j == NLT else 0
            C.prologue_b(nsubE(j), xt[sl], f"Ext{sl}", C.MODC[:, 1, 0, 0, path, :], C.MODC[:, 1, 0, 1, path, :],
                         hT[sl], f"EhT{sl}", [0, 1])

        def qkE(j):
            sl = j % 2
            isctx = j == NLT
            n = nsubE(j) * 128
            for oc in (range(8, 16) if isctx else range(16)):
                b = 2 + oc % 2
                for k in range(8):
                    em.op("pe", lambda k=k, b=b, oc=oc, n=n, sl=sl: T.matmul(
                        PS[:, b, 0:n], lhsT=wq[:, k, oc * 128:(oc + 1) * 128], rhs=hT[sl][:, k, 0:n],
                        start=(k == 0), stop=(k == 7)), R=["Ewq" + str(oc // 8), f"EhT{sl}"], W=[("ps", b)], inc=(k == 7))
                if oc < 8:
                    em.op("act", lambda b=b, oc=oc, sl=sl, n=n: A.activation(out=qk[sl][:, oc, 0:n], in_=PS[:, b, 0:n],
                                                                             func=AF.Copy, scale=0.125),
                          R=[("ps", b)], W=[f"Eqk{sl}"])
                else:
                    em.op("dve", lambda b=b, oc=oc, sl=sl, n=n: V.tensor_copy(out=qk[sl][:, oc, 0:n], in_=PS[:, b, 0:n]),
                          R=[("ps", b)], W=[f"Eqk{sl}"])
            if not isctx:
                em.dma("pool", QTv[:, :, j * 512:(j + 1) * 512], qk[sl][:, 0:8, :], R=[f"Eqk{sl}"], W=[("QT", j)])
                em.dma("pool", KTv[:, :, j * 512:(j + 1) * 512], qk[sl][:, 8:16, :], R=[f"Eqk{sl}"], W=[("KT", j)])
            else:
                em.dma("pool", KTv[:, :, NL:NL + LC], qk[sl][:, 8:16, 0:LC], R=[f"Eqk{sl}"], W=[("KT", j)])

        def vE(j):
            sl = j % 2
            isctx = j == NLT
            nsub = nsubE(j)
            for s in range(nsub):
                pb = 4 + 2 * (s % 2)
                for h in range(2):
                    for k in range(8):
                        em.op("pe", lambda k=k, h=h, s=s, pb=pb, sl=sl: T.matmul(
                            PS[:, pb + h, :], lhsT=hT[sl][:, k, s * 128:(s + 1) * 128], rhs=wq[:, k, 2048 + h * 512:2048 + (h + 1) * 512],
                            start=(k == 0), stop=(k == 7)), R=["Ewq2", f"EhT{sl}"], W=[("ps", pb + h)], inc=(k == 7))
                evc[0] += 1
                src = PS[:, pb:pb + 2, :].rearrange("p a b -> p (a b)")
                if evc[0] % 2:
                    em.op("dve", lambda s=s, sl=sl, src=src: V.tensor_copy(out=vst[sl][:, s, :], in_=src),
                          R=[("ps", pb), ("ps", pb + 1)], W=[f"Evst{sl}"])
                else:
                    em.op("act", lambda s=s, sl=sl, src=src: A.copy(out=vst[sl][:, s, :], in_=src),
                          R=[("ps", pb), ("ps", pb + 1)], W=[f"Evst{sl}"])
            t0 = NL if isctx else j * 512
            em.dma("pool", C.VD[t0:t0 + nsub * 128, :].rearrange("(s p) d -> p s d", p=128), vst[sl][:, 0:nsub, :],
                   R=[f"Evst{sl}"], W=[("VD", j)])

        load(0)
        _wload_bf(C, wq, C.WBF["wqkv"].rearrange("(k p) n -> p k n", p=128), "Ewq", 3 * D, "cvwqkv")
        load(1)
        pro_a(0)
        pro_b(0)
        for j in range(NTL):
            qkE(j)
            if j + 2 < NTL:
                load(j + 2)
            if j + 1 < NTL:
                pro_a(j + 1)
            vE(j)
            if j + 1 < NTL:
                pro_b(j + 1)


def phase_F(C):
    nc, em, PS = C.nc, C.em, C.PS
    V, A, G, T = nc.vector, nc.scalar, nc.gpsimd, nc.tensor
    with ExitStack() as ps_:
        sb = lambda n, s, d=F32: ps_.enter_context(nc.sbuf_tensor("AT" + n, list(s), d))
        wo = sb("wo", [128, 8, D], BF16)
        KTb = [sb(f"KTb{i}", [128, 8, 1024], BF16) for i in range(2)]
        Vb = [sb(f"Vb{i}", [128, 8, 16, 65], BF16) for i in range(2)]
        QTb = sb("QTb", [128, 8, 512], BF16); xt = sb("xt", [128, 4, D])
        KTc = sb("KTc", [128, 8, LC], BF16); Vc = sb("Vc", [128, 2, 16, 65], BF16)
        BTi = sb("BTi", [128, 16, 5, 128], BF16); BTe = sb("BTe", [128, 16, 8, 128], BF16)
        PT = [sb(f"PT{i}", [128, 1280], BF16) for i in range(2)]
        Ot = sb("Ot", [128, D]); OTt = sb("OTt", [128, 8, 128], BF16); rden = sb("rden", [128, 4])
        Gx = sb("Gx", [128, D]); ssq = sb("ssq", [128, 4]); rs = sb("rs", [128, 4]); junk = sb("junk", [128, D], BF16)
        tt = sb("tt", [128, D])
        KTv = C.KT.rearrange("(c p) t -> p c t", p=128); QTv = C.QT.rearrange("(c p) t -> p c t", p=128)
        em.dma("sp", KTc[:], KTv[:, :, NL:NL + LC], W=["AKTc"])
        for i in range(2):
            em.op("pool", lambda i=i: G.memset(Vb[i][:, :, :, 64:65], 1.0), W=[f"AVb{i}"])
        em.op("pool", lambda: G.memset(Vc[:, :, :, 64:65], 1.0), W=["AVc"])
        for c in range(2):
            em.dma("sp", Vc[:, c, :, 0:64], C.VD[NL + c * 128:NL + (c + 1) * 128, :].rearrange("p (h d) -> p h d", d=64), W=["AVc"])

        NB = 8

        def kbof(blk):
            return 0 if blk == 0 else (52 if blk == NB - 1 else 8 * blk - 4)

        def load(blk, sl):
            if blk == 0:
                pieces = [(0, 2, 68 * 64), (2, 6, 0)]
            else:
                pieces = [(0, 8, kbof(blk) * 64)]
            for (c0, ncn, t0) in pieces:
                em.dma("sp", KTb[sl][:, :, c0 * 128:(c0 + ncn) * 128], KTv[:, :, t0:t0 + ncn * 128], W=[f"AKTb{sl}"])
                for c in range(ncn):
                    tt0 = t0 + c * 128
                    em.dma("sp", Vb[sl][:, c0 + c, :, 0:64], C.VD[tt0:tt0 + 128, :].rearrange("p (h d) -> p h d", d=64),
                           W=[f"AVb{sl}"])
        load(0, 0)
        late = {"done": False}

        def late_loads():
            if not late["done"]:
                late["done"] = True
                em.dma("sp", BTi[:], C.BT_d, W=["ABTi"])
                _wload_bf(C, wo, C.WBF["wona"].rearrange("(k p) n -> p k n", p=128), "Awo", D, "cvwona")
                em.dma("sp", Gx[:], C.GROW[1, 0, 0, :].partition_broadcast(128), W=["AGx"])
        for blk in range(NB):
            sl = blk % 2
            r0 = 8 * blk
            edge = blk in (0, NB - 1)
            eb = 0 if blk == 0 else 1
            nloc = 8 if edge else 5
            nch = nloc + 2
            em.dma("sp", QTb[:], QTv[:, :, r0 * 64:r0 * 64 + 512], W=["AQTb"])
            em.dma("sp", xt[:], C.X2[blk * 512:(blk + 1) * 512, :].rearrange("(s p) d -> p s d", p=128), W=["Axt"])

            def sbase(h):
                return 0 if edge else 2 * (h % 2)

            def cpos(cl, h):
                if edge:
                    return (cl // 4, (cl % 4) * 128)
                return (2 * (h % 2) + cl // 4, (cl % 4) * 128)

            def qk(i, h):
                off = 0 if edge else i
                if edge and h == 0:
                    em.dma("sp", BTe[:], C.BTE_d[eb, i], W=["ABTe"])
                j = h // 2; e = h % 2
                p0, p1 = 64 * e, 64 * e + 64
                for cl in range(nch):
                    bk, co = cpos(cl, h)
                    o = PS[:, bk, co:co + 128]
                    if cl < nloc:
                        lt = KTb[sl][p0:p1, j, (off + cl) * 128:(off + cl + 1) * 128]
                    else:
                        lt = KTc[p0:p1, j, (cl - nloc) * 128:(cl - nloc + 1) * 128]
                    last = cl == nch - 1
                    em.op("pe", lambda o=o, lt=lt, j=j, p0=p0, p1=p1, i=i, cl=cl, last=last: T.matmul(
                        o, lhsT=lt, rhs=QTb[p0:p1, j, i * 128:(i + 1) * 128], start=(cl % 4 == 0),
                        stop=(edge and last)), R=[f"AKTb{sl}", "AKTc", "AQTb"], W=[("ps", bk)], inc=(edge and last))
                    if edge:
                        if cl in (3, 7):
                            em.op("pe", lambda h=h, bk=bk, cl=cl: T.matmul(
                                PS[:, bk, :], lhsT=C.identb[:, :], rhs=BTe[:, h, cl - 3:cl + 1, :].rearrange("p a b -> p (a b)"),
                                start=False, stop=True), R=["identb", "ABTe"], W=[("ps", bk)], inc=False)
                    else:
                        if cl == 3:
                            em.op("pe", lambda h=h, bk=bk: T.matmul(
                                PS[:, bk, :], lhsT=C.identb[:, :], rhs=BTi[:, h, 0:4, :].rearrange("p a b -> p (a b)"),
                                start=False, stop=True), R=["identb", "ABTi"], W=[("ps", bk)], inc=False)
                        if cl == 6:
                            em.op("pe", lambda h=h, bk=bk: T.matmul(
                                PS[:, bk, 0:128], lhsT=C.identb[:, :], rhs=BTi[:, h, 4, :],
                                start=False, stop=True), R=["identb", "ABTi"], W=[("ps", bk)], inc=True)

            def ex(i, h):
                b0 = sbase(h)
                nb = 3 if edge else 2
                src = PS[:, b0:b0 + nb, :].rearrange("p a b -> p (a b)")[:, 0:nch * 128]
                em.op("act", lambda src=src, h=h: A.activation(out=PT[h % 2][:, 0:nch * 128], in_=src, func=AF.Exp),
                      R=[("ps", b0 + q) for q in range(nb)], W=[f"APT{h % 2}"])

            def pv(i, h):
                off = 0 if edge else i
                ob = 4 + (h // 4) % 2
                so = (h % 4) * 128
                for c in range(nch):
                    rhs = Vb[sl][:, off + c, h, :] if c < nloc else Vc[:, c - nloc, h, :]
                    em.op("pe", lambda c=c, rhs=rhs, h=h, ob=ob, so=so: T.matmul(
                        PS[:, ob, so:so + 65], lhsT=PT[h % 2][:, c * 128:(c + 1) * 128], rhs=rhs,
                        start=(c == 0), stop=(c == nch - 1)), R=[f"APT{h % 2}", f"AVb{sl}", "AVc"], W=[("ps", ob)],
                          inc=(c == nch - 1))
                if h % 4 == 3:
                    em.op("dve", lambda ob=ob: V.reciprocal(out=rden[:, 0:4], in_=PS[:, ob, 64:512:128]),
                          R=[("ps", ob)], W=["Arden"])
                    for hh in range(4):
                        hd = h - 3 + hh
                        em.op("dve", lambda ob=ob, hh=hh, hd=hd: V.tensor_scalar(
                            out=Ot[:, hd * 64:(hd + 1) * 64], in0=PS[:, ob, hh * 128:hh * 128 + 64],
                            scalar1=rden[:, hh:hh + 1], scalar2=None, op0=ALU.mult), R=[("ps", ob), "Arden"], W=["AOt"])

            def fin(i):
                for k in range(8):
                    em.op("pe", lambda k=k: T.transpose(PS[:, 6 + k // 4, (k % 4) * 128:(k % 4 + 1) * 128],
                                                        Ot[:, k * 128:(k + 1) * 128], C.ident[:]),
                          R=["AOt", "ident"], W=[("ps", 6 + k // 4)], inc=(k % 4 == 3))
                for hb in range(2):
                    em.op("act" if hb else "dve",
                          (lambda hb=hb: A.copy(out=OTt[:, 4 * hb:4 * hb + 4, :].rearrange("p a b -> p (a b)"), in_=PS[:, 6 + hb, :])) if hb else
                          (lambda hb=hb: V.tensor_copy(out=OTt[:, 4 * hb:4 * hb + 4, :].rearrange("p a b -> p (a b)"), in_=PS[:, 6 + hb, :])),
                          R=[("ps", 6 + hb)], W=["AOTt"])
                for hf in range(2):
                    for k in range(8):
                        em.op("pe", lambda k=k, hf=hf: T.matmul(PS[:, 6 + hf, :], lhsT=OTt[:, k, :], rhs=wo[:, k, hf * 512:(hf + 1) * 512],
                                                                start=(k == 0), stop=(k == 7)),
                              R=["AOTt", "Awo0"], W=[("ps", 6 + hf)], inc=(k == 7))
                C.epilogue(6, xt[:, i, :], "Axt", Gx[:, :], "AGx", ssq, rs, junk, tt, "A")

            items = [(i, h) for i in range(4) for h in range(16)]
            qk(*items[0])
            late_loads()
            if blk + 1 < NB:
                load(blk + 1, 1 - sl)
            for n_, (i, h) in enumerate(items):
                nxt = items[n_ + 1] if n_ + 1 < len(items) else None
                if edge:
                    ex(i, h)
                    if nxt:
                        qk(*nxt)
                else:
                    if nxt:
                        qk(*nxt)
                    ex(i, h)
                pv(i, h)
                if h == 15:
                    fin(i)
            em.dma("pool", C.X3[blk * 512:(blk + 1) * 512, :].rearrange("(s p) d -> p s d", p=128), xt[:], R=["Axt"], W=[("X3", blk)])


PHASES += [("E", phase_E), ("F", phase_F),
           ("G", lambda C: phase_FFN(C, 1, C.X3, None, C.out_d, None, "X3", "OUT", ntok=4096))]
```

```python
import math
from contextlib import ExitStack
import numpy as np
import ml_dtypes
import concourse.bass as bass
import concourse.mybir as mybir
from concourse.bass_utils import run_bass_kernel_spmd

F32 = mybir.dt.float32
BF16 = mybir.dt.bfloat16
AF = mybir.ActivationFunctionType
ALU = mybir.AluOpType
NPBF = ml_dtypes.bfloat16

D = 1024
S = 8192
LC = 256
NT = S + LC
FH = 2816
NJ = FH // 128
EPS = 1e-6
NCORES = 8
NL = 4608
NLT = NL // 512
LBASE = 3584
NEG = -30000.0
QENG = ("pool", "dve", "pool", "dve")


class Em:
    LIMIT = 30000
    NDS = 8

    def __init__(self, nc, es):
        self.nc = nc
        self.es = es
        self.eng = dict(pe=nc.tensor, act=nc.scalar, dve=nc.vector, pool=nc.gpsimd, sp=nc.sync)
        self.sem = {}
        self.semkey = {}
        self.cnt = {}
        self.nsem = 0
        for e in self.eng:
            self._newsem(e)
        self.waited = {e: {} for e in self.eng}
        self.lastw = {}
        self.readers = {}
        self.dsem = {}
        self.ndma = {}
        for q in ("sp", "pool", "act"):
            self.dsem[q] = [es.enter_context(nc.semaphore(f"d{q}{i}")) for i in range(self.NDS)]
            self.ndma[q] = 0
        self.bg = []

    def _newsem(self, e):
        self.nsem += 1
        self.sem[e] = self.es.enter_context(self.nc.semaphore(f"s{e}{self.nsem}"))
        self.semkey[e] = (e, self.nsem)
        self.cnt[e] = 0

    def _deps(self, engine, R, W):
        deps = {}
        for k in list(R) + list(W):
            t = self.lastw.get(k)
            if t is None:
                continue
            for t_ in (t.values() if isinstance(t, dict) else (t,)):
                if t_[0] not in deps or deps[t_[0]][2] < t_[2]:
                    deps[t_[0]] = t_
        for k in W:
            for t in self.readers.get(k, {}).values():
                if t[0] not in deps or deps[t[0]][2] < t[2]:
                    deps[t[0]] = t
        e = self.eng[engine]
        for sk, t in deps.items():
            if engine == "pe" and t[3] == "pe":
                continue
            if self.waited[engine].get(sk, 0) >= t[2]:
                continue
            e.wait_ge(t[1], t[2])
            self.waited[engine][sk] = t[2]

    def _record(self, tok, R, W):
        for k in W:
            if tok[3] is None:
                cur = self.lastw.get(k)
                d = cur if isinstance(cur, dict) else {}
                d[tok[0]] = tok
                self.lastw[k] = d
            else:
                self.lastw[k] = tok
            self.readers[k] = {}
        for k in R:
            d = self.readers.setdefault(k, {})
            if tok[0] not in d or d[tok[0]][2] < tok[2]:
                d[tok[0]] = tok

    def op(self, engine, fn, R=(), W=(), inc=True):
        self._deps(engine, R, W)
        ins = fn()
        if inc:
            ins.then_inc(self.sem[engine], 1)
            self.cnt[engine] += 1
            tok = (self.semkey[engine], self.sem[engine], self.cnt[engine], engine)
            self._record(tok, R, W)
            if self.cnt[engine] >= self.LIMIT:
                self._newsem(engine)
        else:
            tok = (self.semkey[engine], self.sem[engine], self.cnt[engine] + 1, engine)
            self._record(tok, R, W)
        return ins

    def dma(self, q, out, in_, R=(), W=(), **kw):
        self._deps(q, R, W)
        i = self.ndma[q]
        self.ndma[q] += 1
        sem = self.dsem[q][i % self.NDS]
        rnd = i // self.NDS
        sk = ("dma", q, i % self.NDS)
        if rnd > 0 and self.waited[q].get(sk, 0) < 16 * rnd:
            self.eng[q].wait_ge(sem, 16 * rnd)
            self.waited[q][sk] = 16 * rnd
        self.eng[q].dma_start(out=out, in_=in_, **kw).then_inc(sem, 16)
        tok = (sk, sem, 16 * (rnd + 1), None)
        self._record(tok, R, W)

    def dma_bg(self, q, out, in_, R=(), W=(), **kw):
        self._deps(q, R, W)
        sem = self.es.enter_context(self.nc.semaphore(f"bg{len(self.bg)}"))
        self.bg.append(sem)
        self.eng[q].dma_start(out=out, in_=in_, **kw).then_inc(sem, 16)
        self._record((("bg", len(self.bg)), sem, 16, None), R, W)

    def finish(self):
        sp = self.eng["sp"]
        for sem in self.bg:
            sp.wait_ge(sem, 16)
        for q in self.dsem:
            n = self.ndma[q]
            for s in range(self.NDS):
                uses = (n - s + self.NDS - 1) // self.NDS if n > s else 0
                if uses > 0:
                    sp.wait_ge(self.dsem[q][s], 16 * uses)
        for e in ("pe", "act", "dve", "pool"):
            if self.cnt[e] > 0:
                sp.wait_ge(self.sem[e], self.cnt[e])


def _tables():
    t = {}
    t["ident"] = np.eye(128, dtype=np.float32)
    t["identb"] = np.eye(128, dtype=np.float32).astype(NPBF)
    a = np.arange(128, dtype=np.float64)
    ang = 2 * np.pi * np.outer(a, a) / 128.0
    t["T1"] = np.concatenate([np.cos(ang), -np.sin(ang)], axis=1).astype(NPBF)
    t2 = np.arange(64, dtype=np.float64)[:, None, None]
    k1 = np.arange(128, dtype=np.float64)[None, :, None]
    k2 = np.arange(64, dtype=np.float64)[None, None, :]
    ph = 2 * np.pi * (t2 * k2 / 64.0 + t2 * k1 / 8192.0)
    Mc, Ms = np.cos(ph), np.sin(ph)
    t["M2"] = np.concatenate([Ms, Mc, -Ms], axis=2).astype(NPBF)
    c = np.arange(64, dtype=np.float64)
    angc = 2 * np.pi * np.outer(c, c) / 64.0
    Cc, Sc = np.cos(angc), np.sin(angc)
    z = np.zeros((64, 64))
    Cbd = np.block([[Cc, z], [z, Cc]])
    Sbd = np.block([[Sc, z], [z, Sc]])
    sx = 1.0 / math.sqrt(8192.0 * 64.0)
    t["CSf"] = (np.concatenate([Cbd, Sbd], axis=1) * sx).astype(NPBF)
    sc_ = 1.0 / math.sqrt(256.0 * 64.0)
    t["CSc"] = (np.concatenate([Cbd, Sbd], axis=1) * sc_).astype(NPBF)
    p = np.arange(256, dtype=np.float64)
    angp = 2 * np.pi * np.outer(p, p) / 256.0
    T256 = np.concatenate([np.cos(angp), -np.sin(angp)], axis=1)
    t["T256"] = T256.reshape(2, 128, 512).transpose(1, 0, 2).copy().astype(NPBF)
    return t


_TABLES = None


def _bias_tables(rpb):
    H = 16
    out = np.full((5, 5 * 128, H, 128), NEG, dtype=np.float32)
    kr = np.arange(10)[:, None, None, None]
    kc = np.arange(64)[None, :, None, None]
    qr = np.arange(2)[None, None, :, None]
    qc = np.arange(64)[None, None, None, :]
    cs = np.clip(qc - 8, 0, 48)
    for vi, (gp, ks) in enumerate([(0, 0), (2, 0), (60, 56), (124, 118), (126, 118)]):
        gq = gp + qr
        rs = np.clip(gq - 4, 0, 120)
        gk = ks + kr
        valid = (gk >= rs) & (gk < rs + 8) & (kc >= cs) & (kc < cs + 16)
        dr = np.clip(gk - gq + 7, 0, 14)
        dc = np.clip(kc - qc + 15, 0, 30)
        valid, dr, dc = np.broadcast_arrays(valid, dr, dc)
        vals = rpb[:, dr, dc]
        vals = np.where(valid[None], vals, NEG)
        out[vi] = vals.transpose(1, 2, 0, 3, 4).reshape(640, H, 128)
    bt = out.reshape(5, 5, 128, H, 128).transpose(0, 2, 3, 1, 4)
    return np.ascontiguousarray(bt).astype(NPBF)


def _local_rows(h):
    if h == 0:
        return np.arange(72)
    return np.concatenate([np.arange(64, 128), np.arange(56, 64)])


def _edge_tables(rpb, h):
    H = 16
    lr = _local_rows(h)
    out = np.empty((2, 4, 128, H, 8, 128), dtype=NPBF)
    kc = np.arange(64)[None, :, None, None]
    qr = np.arange(2)[None, None, :, None]
    qc = np.arange(64)[None, None, None, :]
    cs = np.clip(qc - 8, 0, 48)
    keyrows = [np.concatenate([np.arange(68, 72), np.arange(0, 12)]), np.arange(52, 68)]
    for eb, r0 in enumerate([0, 56]):
        gk = lr[keyrows[eb]][:, None, None, None]
        for i in range(4):
            gq = lr[r0 + 2 * i + np.arange(2)][None, None, :, None]
            rs = np.clip(gq - 4, 0, 120)
            valid = (gk >= rs) & (gk < rs + 8) & (kc >= cs) & (kc < cs + 16)
            dr = np.clip(gk - gq + 7, 0, 14)
            dc = np.clip(kc - qc + 15, 0, 30)
            valid, dr, dc = np.broadcast_arrays(valid, dr, dc)
            vals = np.where(valid[None], rpb[:, dr, dc], NEG)
            t = vals.transpose(1, 2, 0, 3, 4).reshape(8, 128, H, 128)
            out[eb, i] = t.transpose(1, 2, 0, 3).astype(NPBF)
    return out


def _col(v, nchunk):
    return np.ascontiguousarray(np.asarray(v, np.float32).reshape(nchunk, 128).T)


def build(stop="all", debug=False):
    nc = bass.Bass("TRN2", target_bir_lowering=False)

    def din(name, shape, dt=F32):
        return nc.dram_tensor(name, list(shape), dt, kind="ExternalInput").ap()

    skind = "ExternalOutput" if debug else "Internal"

    def dscr(name, shape, dt):
        return nc.dram_tensor(name, list(shape), dt, kind=skind).ap()

    x_d = din("x", [S, D]); xloc_d = din("xloc", [NL, D]); hsel_d = din("hsel", [128, 2]); ctx_d = din("ctx", [LC, D]); ccols_d = din("ccols", [128, 16])
    wmod_d = din("w_mod", [2, D, 6 * D]); bmodc_d = din("bmodc", [128, 2, 48]); bmod_d = din("b_mod", [2, 6 * D])
    gcols_d = din("gcols", [128, 2, 2, 8]); gpm_d = din("g_post_mix", [2, D]); gpf_d = din("g_post_ffn", [2, D])
    win_d = din("w_in_ab", [D, 1536]); woab_d = din("w_out_ab", [D, D])
    wg_d = din("w_ffn_gate", [2, D, FH]); wu_d = din("w_ffn_up", [2, D, FH]); wd_d = din("w_ffn_down", [2, FH, D])
    wqkv_d = din("w_qkv_na", [D, 3 * D]); wona_d = din("w_out_na", [D, D])
    wbd_d = din("wbd", [2, 2, 4, 128, 128]); lcols_d = din("lcols", [128, 4, 2, 3]); convc_d = din("convc", [128, 4, 5]); cdiag_d = din("cdiag", [4, 4, 128, 128])
    ident_d = din("ident", [128, 128]); identb_d = din("identb", [128, 128], BF16)
    T1_d = din("T1", [128, 256], BF16); M2_d = din("M2", [64, 128, 192], BF16); CSf_d = din("CSf", [128, 256], BF16)
    CSc_d = din("CSc", [128, 256], BF16); T256_d = din("T256", [128, 2, 512], BF16)
    BT_d = din("BT", [128, 16, 5, 128], BF16); BTE_d = din("BTE", [2, 4, 128, 16, 8, 128], BF16)
    out_d = nc.dram_tensor("out", [4096, D], F32, kind="ExternalOutput").ap()

    UG = dscr("UG", [8, 128, S], F32)
    UGc = dscr("UGc", [8, 128, LC], F32)
    MIXT = dscr("MIXT", [D, NT], BF16)
    GROW = dscr("GROW", [2, 2, 2, D], F32)
    X1 = dscr("X1", [NL, D], F32); C1 = dscr("C1", [LC, D], F32)
    X2 = dscr("X2", [NL, D], F32); C2 = dscr("C2", [LC, D], F32)
    X3 = dscr("X3", [NL, D], F32)
    QT = dscr("QT", [D, NL], BF16); KT = dscr("KT", [D, NL + LC], BF16); VD = dscr("VD", [NL + LC, D], BF16)

    WBF = {"woab": dscr("woab_bf", [D, D], BF16), "wona": dscr("wona_bf", [D, D], BF16),
           "wqkv": dscr("wqkv_bf", [D, 3 * D], BF16),
           "wg": dscr("wg_bf", [2, D, FH], BF16), "wu": dscr("wu_bf", [2, D, FH], BF16), "wd": dscr("wd_bf", [2, FH, D], BF16)}

    with ExitStack() as es:
        em = Em(nc, es)

        def barrier():
            for e in ("pe", "act", "dve", "pool", "sp"):
                eng = em.eng[e]
                for f in ("pe", "act", "dve", "pool"):
                    if f != e and em.cnt[f] > 0 and em.waited[e].get(em.semkey[f], 0) < em.cnt[f]:
                        eng.wait_ge(em.sem[f], em.cnt[f]); em.waited[e][em.semkey[f]] = em.cnt[f]
                for q in em.dsem:
                    n = em.ndma[q]
                    for s_ in range(em.NDS):
                        uses = (n - s_ + em.NDS - 1) // em.NDS if n > s_ else 0
                        sk = ("dma", q, s_)
                        if uses > 0 and em.waited[e].get(sk, 0) < 16 * uses:
                            eng.wait_ge(em.dsem[q][s_], 16 * uses); em.waited[e][sk] = 16 * uses

        PS = es.enter_context(nc.psum_tensor("PS", [128, 8, 512], F32))
        ident = es.enter_context(nc.sbuf_tensor("ident_s", [128, 128], F32))
        identb = es.enter_context(nc.sbuf_tensor("identb_s", [128, 128], BF16))
        MODC = es.enter_context(nc.sbuf_tensor("MODC", [128, 2, 2, 2, 2, 8], F32))
        mhalf = es.enter_context(nc.sbuf_tensor("mhalf", [128, 8], F32))
        em.dma("sp", ident[:], ident_d, W=["ident"])
        em.dma("sp", identb[:], identb_d, W=["identb"])
        em.op("dve", lambda: nc.vector.memset(mhalf[:], -0.5), W=["mhalf"])

        V = nc.vector; A = nc.scalar; G = nc.gpsimd; T = nc.tensor

        def pbank(b):
            return ("ps", b)

        def rstd_from_ssq(ssq, rs, n, tag):
            em.op("dve", lambda: V.tensor_scalar(out=rs[:, 0:n], in0=ssq[:, 0:n], scalar1=1.0 / D, scalar2=EPS,
                                                 op0=ALU.mult, op1=ALU.add), R=[tag + "ssq"], W=[tag + "rs"])
            em.op("pool", lambda: G.tensor_tensor(out=rs[:, 0:n], in0=rs[:, 0:n], in1=mhalf[:, 0:n], op=ALU.pow),
                  R=[tag + "rs", "mhalf"], W=[tag + "rs"])

        def prologue_a(xt, xkey, nsub, xs, xskey, ssq, rs, junk, tag):
            for s in range(nsub):
                em.op("act", lambda s=s: A.activation(out=junk[:, :], in_=xt[:, s, :], func=AF.Square,
                                                      accum_out=ssq[:, s:s + 1]),
                      R=[xkey], W=[tag + "junk", tag + "ssq"])
            rstd_from_ssq(ssq, rs, nsub, tag)
            for s in range(nsub):
                em.op("act", lambda s=s: A.activation(out=xs[:, s, :], in_=xt[:, s, :], func=AF.Copy,
                                                      scale=rs[:, s:s + 1]),
                      R=[xkey, tag + "rs"], W=[xskey])

        def prologue_b(nsub, xs, xskey, Acol, Bcol, hT, hkey, tb):
            for k in range(8):
                b = tb[k % len(tb)]
                for s in range(nsub):
                    em.op("pe", lambda s=s, k=k, b=b: T.transpose(PS[:, b, s * 128:(s + 1) * 128],
                                                                  xs[:, s, k * 128:(k + 1) * 128], ident[:]),
                          R=[xskey, "ident"], W=[pbank(b)], inc=(s == nsub - 1))
                em.op("act", lambda k=k, b=b: A.activation(out=hT[:, k, 0:nsub * 128], in_=PS[:, b, 0:nsub * 128],
                                                           func=AF.Identity, scale=Acol[:, k:k + 1],
                                                           bias=Bcol[:, k:k + 1]),
                      R=[pbank(b), "MODC"], W=[hkey])

        def prologue(xt, xkey, nsub, xs, xskey, Acol, Bcol, hT, hkey, ssq, rs, junk, tag, tb):
            prologue_a(xt, xkey, nsub, xs, xskey, ssq, rs, junk, tag)
            prologue_b(nsub, xs, xskey, Acol, Bcol, hT, hkey, tb)

        def epilogue(psb, xsub, xkey, Gb, gkey, ssq, rs, junk, tt, tag):
            yv = PS[:, psb:psb + 2, :]
            em.op("act", lambda: A.activation(out=junk[:, :], in_=yv, func=AF.Square, accum_out=ssq[:, 0:1]),
                  R=[pbank(psb), pbank(psb + 1)], W=[tag + "junk", tag + "ssq"])
            rstd_from_ssq(ssq, rs, 1, tag)
            em.op("dve", lambda: V.tensor_tensor(out=tt[:, :], in0=yv, in1=Gb, op=ALU.mult),
                  R=[pbank(psb), pbank(psb + 1), gkey], W=[tag + "tt"])
            em.op("dve", lambda: V.scalar_tensor_tensor(out=xsub, in0=tt[:, :], scalar=rs[:, 0:1], in1=xsub,
                                                        op0=ALU.mult, op1=ALU.add),
                  R=[tag + "tt", tag + "rs", xkey], W=[xkey])

        with ExitStack() as pes:
            sb = lambda n, s, d=F32: pes.enter_context(nc.sbuf_tensor(n, list(s), d))
            cc = sb("cc", [128, 16]); scc = sb("scc", [128, 16]); rhs2 = sb("rhs2", [128, 8, 2], BF16)
            bmc = sb("bmc", [128, 2, 48]); bm1 = sb("bm1", [128, 2, 48]); gco = sb("gco", [128, 2, 2, 8])
            bmrow = sb("bmrow", [2, 2, 2, D]); grow = sb("grow", [2, 2, 2, D]); grt = sb("grt", [2, 2, 2, D])
            wm = [sb(f"wm{i}", [128, 8, 512], BF16) for i in range(3)]
            em.dma("sp", cc[:], ccols_d, W=["cc"])
            em.dma("sp", bmc[:], bmodc_d, W=["bmc"])
            em.dma("sp", gco[:], gcols_d, W=["gco"])
            for l in range(2):
                for w_, (src, off) in enumerate([(bmod_d, 2 * D), (bmod_d, 5 * D)]):
                    em.dma("sp", bmrow[:, l, w_, :], src[l, off:off + D].partition_broadcast(2), W=["bmrow"])
                em.dma("sp", grow[:, l, 0, :], gpm_d[l, :].partition_broadcast(2), W=["grow"])
                em.dma("sp", grow[:, l, 1, :], gpf_d[l, :].partition_broadcast(2), W=["grow"])
            em.op("act", lambda: A.activation(out=scc[:], in_=cc[:], func=AF.Silu), R=["cc"], W=["scc"])
            em.op("dve", lambda: V.tensor_copy(out=rhs2[:, :, 0], in_=scc[:, 0:8]), R=["scc"], W=["rhs2"])
            em.op("dve", lambda: V.tensor_copy(out=rhs2[:, :, 1], in_=scc[:, 8:16]), R=["scc"], W=["rhs2"])
            em.op("dve", lambda: V.tensor_scalar(out=bm1[:], in0=bmc[:], scalar1=1.0, scalar2=None, op0=ALU.add),
                  R=["bmc"], W=["bm1"])
            it = 0
            for l in range(2):
                wsrc = wmod_d[l].rearrange("(k p) n -> p k n", p=128)
                for nb in range(12):
                    slot = it % 3; it += 1
                    wt = wm[slot]; wk = f"wm{slot}"
                    em.dma("pool", wt[:], wsrc[:, :, nb * 512:(nb + 1) * 512], W=[wk])
                    v = nb // 2; half = nb % 2
                    b = it % 4
                    if v in (2, 5):
                        w_ = 0 if v == 2 else 1
                        for k in range(8):
                            em.op("pe", lambda k=k, b=b, wt=wt: T.matmul(PS[0:2, b, :], lhsT=rhs2[:, k, :], rhs=wt[:, k, :],
                                                                        start=(k == 0), stop=(k == 7)),
                                  R=["rhs2", wk], W=[pbank(b)], inc=(k == 7))
                        dst = grt[:, l, w_, half * 512:(half + 1) * 512]
                        em.op("dve", lambda b=b, dst=dst, l=l, w_=w_, half=half: V.tensor_tensor(
                            out=dst, in0=PS[0:2, b, :], in1=bmrow[:, l, w_, half * 512:(half + 1) * 512], op=ALU.add),
                              R=[pbank(b), "bmrow"], W=["grt"])
                        em.op("dve", lambda dst=dst, l=l, w_=w_, half=half: V.tensor_tensor(
                            out=dst, in0=dst, in1=grow[:, l, w_, half * 512:(half + 1) * 512], op=ALU.mult),
                              R=["grt", "grow"], W=["grt"])
                        if half == 1:
                            em.dma("sp", GROW[l, w_, :, :], grt[:, l, w_, :], R=["grt"], W=[("GROW", l, w_)])
                    else:
                        sub = 0 if v < 2 else 1
                        isA = v in (1, 4)
                        for m in range(4):
                            ch = half * 4 + m
                            for k in range(8):
                                em.op("pe", lambda k=k, b=b, m=m, wt=wt: T.matmul(
                                    PS[:, b, 2 * m:2 * m + 2], lhsT=wt[:, k, m * 128:(m + 1) * 128], rhs=rhs2[:, k, :],
                                    start=(k == 0), stop=(k == 7)), R=["rhs2", wk], W=[pbank(b)], inc=(k == 7))
                            dst = MODC[:, l, sub, 0 if isA else 1, :, ch]
                            if isA:
                                em.op("dve", lambda b=b, m=m, dst=dst, l=l, v=v, ch=ch, sub=sub: V.tensor_scalar(
                                    out=dst, in0=PS[:, b, 2 * m:2 * m + 2], scalar1=bm1[:, l, v * 8 + ch:v * 8 + ch + 1],
                                    scalar2=gco[:, l, sub, ch:ch + 1], op0=ALU.add, op1=ALU.mult),
                                      R=[pbank(b), "bm1", "gco"], W=["MODC"])
                            else:
                                em.op("dve", lambda b=b, m=m, dst=dst, l=l, v=v, ch=ch: V.tensor_scalar(
                                    out=dst, in0=PS[:, b, 2 * m:2 * m + 2], scalar1=bmc[:, l, v * 8 + ch:v * 8 + ch + 1],
                                    scalar2=None, op0=ALU.add), R=[pbank(b), "bmc"], W=["MODC"])
            if debug:
                MODCd = nc.dram_tensor("MODCd", [128, 128], F32, kind="ExternalOutput").ap()
                em.dma("sp", MODCd, MODC[:].rearrange("p a b c d e -> p (a b c d e)"), R=["MODC"], W=["MODCd"])
            barrier()
        if stop == "0":
            em.finish()
            return nc
        C = type("C", (), {})()
        C.__dict__.update(locals())
        for name, fn in PHASES:
            fn(C)
            barrier()
            if stop == name:
                if hasattr(C, 'es_mix'):
                    C.es_mix.close()
                break
        em.finish()
    return nc


PHASES = []


def _host_inputs(inp, b):
    global _TABLES
    if _TABLES is None:
        _TABLES = _tables()
    f = lambda a: np.ascontiguousarray(np.asarray(a, dtype=np.float32))
    m = {}
    b, h = b // 2, b % 2
    m["x"] = f(inp["x"][b]); m["ctx"] = f(inp["ctx"][b])
    m["xloc"] = f(inp["x"][b].reshape(128, 64, D)[_local_rows(h)].reshape(NL, D))
    m["hsel"] = np.tile(np.array([[1.0 - h, float(h)]], np.float32), (128, 1))
    m["ccols"] = np.concatenate([_col(inp["c"][b], 8), _col(inp["c_ctx"], 8)], axis=1)
    m["w_mod"] = f(inp["w_mod"]); m["b_mod"] = f(inp["b_mod"])
    m["bmodc"] = np.ascontiguousarray(np.stack([_col(inp["b_mod"][l], 48) for l in range(2)], axis=1))
    m["gcols"] = np.ascontiguousarray(np.stack(
        [np.stack([_col(inp["g_pre_mix"][l], 8), _col(inp["g_pre_ffn"][l], 8)], axis=1) for l in range(2)], axis=1))
    m["g_post_mix"] = f(inp["g_post_mix"]); m["g_post_ffn"] = f(inp["g_post_ffn"])
    m["w_in_ab"] = f(inp["w_in_ab"][0]); m["w_out_ab"] = f(inp["w_out_ab"][0])
    m["w_ffn_gate"] = f(inp["w_ffn_gate"]); m["w_ffn_up"] = f(inp["w_ffn_up"]); m["w_ffn_down"] = f(inp["w_ffn_down"])
    m["w_qkv_na"] = f(inp["w_qkv_na"][0]); m["w_out_na"] = f(inp["w_out_na"][0])
    wbd = np.zeros((2, 2, 4, 128, 128), np.float32)
    for gi, key in enumerate(["lru_w_a", "lru_w_i"]):
        w = np.asarray(inp[key][0], np.float32)
        for d in range(2):
            for c in range(4):
                wbd[gi, d, c, 0:64, 0:64] = w[d, 2 * c]
                wbd[gi, d, c, 64:128, 64:128] = w[d, 2 * c + 1]
    m["wbd"] = wbd
    lc = np.zeros((128, 4, 2, 3), np.float32)
    for d in range(2):
        lc[:, :, d, 0] = _col(inp["lru_b_a"][0][d], 4)
        lc[:, :, d, 1] = _col(inp["lru_b_i"][0][d], 4)
        lc[:, :, d, 2] = _col(inp["lru_lam"][0][d], 4)
    m["lcols"] = lc
    cv = np.zeros((128, 4, 5), np.float32)
    for k in range(4):
        cv[:, :, k] = _col(inp["conv_w"][0][k], 4)
    cv[:, :, 4] = _col(inp["conv_b"][0], 4)
    m["convc"] = cv
    cd = np.zeros((4, 4, 128, 128), np.float32)
    ii = np.arange(128)
    for c in range(4):
        for k in range(4):
            cd[c, k, ii, ii] = cv[:, c, k]
    m["cdiag"] = cd
    for k in ("ident", "identb", "T1", "M2", "CSf", "CSc", "T256"):
        m[k] = _TABLES[k]
    rpb = np.asarray(inp["rpb_na"][0], np.float32)
    m["BT"] = np.ascontiguousarray(_bias_tables(rpb)[2])
    m["BTE"] = _edge_tables(rpb, h)
    return m


_NC_CACHE = {}


def kernel(**inputs):
    if "full" not in _NC_CACHE:
        _NC_CACHE["full"] = build()
    nc = _NC_CACHE["full"]
    in_maps = [_host_inputs(inputs, b) for b in range(NCORES)]
    res = run_bass_kernel_spmd(nc, in_maps, core_ids=list(range(NCORES)))
    out = np.empty((4, S, D), np.float32)
    for c in range(NCORES):
        b, h = c // 2, c % 2
        o = np.asarray(res.results[c]["out"], dtype=np.float32)
        out[b, 4096 * h:4096 * (h + 1)] = o
    return out


def _wload(C, dst, src_view, key, nk, ncols, step=512):
    for bi, c0 in enumerate(range(0, ncols, step)):
        c1 = min(ncols, c0 + step)
        C.em.dma("pool", dst[:, :, c0:c1], src_view[:, :, c0:c1], W=[key + str(bi)])


def _wload_bf(C, dst, src_view, key, ncols, srckey, step=1024):
    cv_emit(C, 0, need=srckey)
    for bi, c0 in enumerate(range(0, ncols, step)):
        c1 = min(ncols, c0 + step)
        C.em.dma("sp", dst[:, :, c0:c1], src_view[:, :, c0:c1], R=C.cvkeys[srckey], W=[key + str(bi)])


def preconvert_init(C):
    jobs = [("woab", C.woab_d, C.WBF["woab"]), ("wg0", C.wg_d[0], C.WBF["wg"][0]), ("wu0", C.wu_d[0], C.WBF["wu"][0]),
            ("wd0", C.wd_d[0], C.WBF["wd"][0]), ("wqkv", C.wqkv_d, C.WBF["wqkv"]), ("wona", C.wona_d, C.WBF["wona"]),
            ("wg1", C.wg_d[1], C.WBF["wg"][1]), ("wu1", C.wu_d[1], C.WBF["wu"][1]), ("wd1", C.wd_d[1], C.WBF["wd"][1])]
    C.cvkeys = {}
    C.cvq = []
    for key, src, dst in jobs:
        rows, cols = src.shape
        rstep = 512 if rows % 512 == 0 else 704
        C.cvkeys["cv" + key] = []
        for r0 in range(0, rows, rstep):
            for c0 in range(0, cols, 1024):
                c1 = min(cols, c0 + 1024)
                k_ = f"cv{key}_{r0}_{c0}"
                C.cvkeys["cv" + key].append(k_)
                C.cvq.append(("cv" + key, k_, dst[r0:r0 + rstep, c0:c1], src[r0:r0 + rstep, c0:c1]))


def cv_emit(C, n=1, need=None):
    while C.cvq and (n > 0 or (need is not None and any(j[0] == need for j in C.cvq))):
        big, k_, dst, src = C.cvq.pop(0)
        C.em.dma_bg("pool", dst, src, W=[k_])
        n -= 1


def phase_A(C):
    nc, em, PS = C.nc, C.em, C.PS
    V, A, G, T = nc.vector, nc.scalar, nc.gpsimd, nc.tensor
    pes = C.es_mix = ExitStack()
    preconvert_init(C)
    C.F = pes.enter_context(nc.sbuf_tensor("Fbuf", [128, 64, 512], BF16))
    C.fTc = pes.enter_context(nc.sbuf_tensor("fTc", [128, 4, LC], BF16))
    F, fTc = C.F, C.fTc
    with ExitStack() as ps_:
        sb = lambda n, s, d=F32: ps_.enter_context(nc.sbuf_tensor(n, list(s), d))
        win = sb("win", [128, 8, 1536], BF16)
        xt = [sb(f"Axt{i}", [128, 4, D]) for i in range(2)]
        hT = [sb(f"AhT{i}", [128, 8, 512], BF16) for i in range(2)]
        ugst = [sb(f"Aug{i}", [128, 8, 512]) for i in range(2)]
        ssq = sb("Assq", [128, 4]); rs = sb("Ars", [128, 4]); junk = sb("Ajunk", [128, D], BF16)
        _wload(C, win, C.win_d.rearrange("(k p) n -> p k n", p=128), "win", 8, 1536)
        xsrc = C.x_d.rearrange("(t1 t2) d -> t1 t2 d", t2=64)
        Acol = C.MODC[:, 0, 0, 0, 0, :]; Bcol = C.MODC[:, 0, 0, 1, 0, :]
        AcolC = C.MODC[:, 0, 0, 0, 1, :]; BcolC = C.MODC[:, 0, 0, 1, 1, :]
        evc = [0]

        def loadA(j):
            sl = j % 2
            if j < 16:
                em.dma("sp", xt[sl][:], xsrc[:, 4 * j:4 * (j + 1), :], W=[f"Axt{sl}"])
            elif j == 16:
                em.dma("sp", xt[sl][:, 0:2, :], C.ctx_d.rearrange("(s p) d -> p s d", p=128), W=[f"Axt{sl}"])

        def nsubA(j):
            return 2 if j == 16 else 4

        def pro_a(j):
            sl = j % 2
            C.prologue_a(xt[sl], f"Axt{sl}", nsubA(j), xt[sl], f"Axt{sl}", ssq, rs, junk, "A")

        def pro_b(j):
            sl = j % 2
            isctx = j == 16
            C.prologue_b(nsubA(j), xt[sl], f"Axt{sl}", AcolC if isctx else Acol, BcolC if isctx else Bcol,
                         hT[sl], f"AhT{sl}", [0, 1])

        def ugA(j):
            sl = j % 2
            isctx = j == 16
            n = nsubA(j) * 128
            for oc in range(8):
                b = 2 + oc % 3
                for k in range(8):
                    em.op("pe", lambda k=k, b=b, oc=oc, sl=sl, n=n: T.matmul(
                        PS[:, b, 0:n], lhsT=win[:, k, oc * 128:(oc + 1) * 128], rhs=hT[sl][:, k, 0:n],
                        start=(k == 0), stop=(k == 7)), R=["win" + str(oc // 4), f"AhT{sl}"], W=[("ps", b)], inc=(k == 7))
                evc[0] += 1
                if evc[0] % 2:
                    em.op("dve", lambda b=b, oc=oc, sl=sl, n=n: V.tensor_copy(out=ugst[sl][:, oc, 0:n], in_=PS[:, b, 0:n]),
                          R=[("ps", b)], W=[f"Aug{sl}"])
                else:
                    em.op("act", lambda b=b, oc=oc, sl=sl, n=n: A.copy(out=ugst[sl][:, oc, 0:n], in_=PS[:, b, 0:n]),
                          R=[("ps", b)], W=[f"Aug{sl}"])
            if isctx:
                em.dma("pool", C.UGc.rearrange("c p n -> p c n"), ugst[sl][:, :, 0:LC], R=[f"Aug{sl}"], W=["UGc"])
            else:
                em.dma("pool", C.UG[:, :, j * 512:(j + 1) * 512].rearrange("c p n -> p c n"), ugst[sl][:],
                       R=[f"Aug{sl}"], W=[("UG", j)])

        def fA(j):
            sl = j % 2
            if j == 16:
                for fc in range(4):
                    b = 5 + fc % 3
                    for k in range(8):
                        em.op("pe", lambda k=k, b=b, fc=fc, sl=sl: T.matmul(
                            PS[:, b, 0:LC], lhsT=win[:, k, 1024 + fc * 128:1024 + (fc + 1) * 128], rhs=hT[sl][:, k, 0:LC],
                            start=(k == 0), stop=(k == 7)), R=["win2", f"AhT{sl}"], W=[("ps", b)], inc=(k == 7))
                    em.op("dve", lambda b=b, fc=fc: V.tensor_copy(out=fTc[:, fc, :], in_=PS[:, b, 0:LC]),
                          R=[("ps", b)], W=["fTc"])
            else:
                for s in range(4):
                    b = 5 + s % 3
                    for k in range(8):
                        em.op("pe", lambda k=k, b=b, s=s, sl=sl: T.matmul(
                            PS[:, b, :], lhsT=hT[sl][:, k, s * 128:(s + 1) * 128], rhs=win[:, k, 1024:1536],
                            start=(k == 0), stop=(k == 7)), R=["win2", f"AhT{sl}"], W=[("ps", b)], inc=(k == 7))
                    evc[0] += 1
                    if evc[0] % 2:
                        em.op("dve", lambda b=b, s=s, j=j: V.tensor_copy(out=F[:, 4 * j + s, :], in_=PS[:, b, :]),
                              R=[("ps", b)], W=["F"])
                    else:
                        em.op("act", lambda b=b, s=s, j=j: A.copy(out=F[:, 4 * j + s, :], in_=PS[:, b, :]),
                              R=[("ps", b)], W=["F"])

        loadA(0)
        loadA(1)
        pro_a(0)
        pro_b(0)
        for j in range(17):
            ugA(j)
            cv_emit(C, 1)
            if j + 2 < 17:
                loadA(j + 2)
            if j + 1 < 17:
                pro_a(j + 1)
            fA(j)
            if j + 1 < 17:
                pro_b(j + 1)


def phase_C(C):
    nc, em, PS, F, fTc = C.nc, C.em, C.PS, C.F, C.fTc
    V, A, G, T = nc.vector, nc.scalar, nc.gpsimd, nc.tensor
    with ExitStack() as ps_:
        sb = lambda n, s, d=F32: ps_.enter_context(nc.sbuf_tensor(n, list(s), d))
        T1 = sb("T1s", [128, 256], BF16); M2 = sb("M2s", [64, 128, 192], BF16); CSf = sb("CSfs", [128, 256], BF16)
        CSc = sb("CScs", [128, 256], BF16); T256 = sb("T256s", [128, 2, 512], BF16)
        Ast = sb("Ast", [64, 64, 256], BF16); Y = sb("Ybuf", [128, 2, S], BF16)
        fst = [sb(f"fst{i}", [128, 2048], BF16) for i in range(2)]
        Gc = sb("Gcb", [128, 2, 256], BF16); fcs = sb("fcs", [128, LC], BF16)
        for dst, src, k in ((T1, C.T1_d, "T1"), (M2, C.M2_d, "M2"), (CSf, C.CSf_d, "CSf"), (CSc, C.CSc_d, "CSc"),
                            (T256, C.T256_d, "T256")):
            em.dma("sp", dst[:], src, W=[k])
        ev = 0
        for cc in range(4):
            for hc in range(2):
                for g4 in range(16):
                    b = 2 * (g4 % 2)
                    for q in range(4):
                        ch = cc * 128 + hc * 64 + g4 * 4 + q
                        em.op("pe", lambda b=b, q=q, ch=ch: T.matmul(
                            PS[0:64, b + q // 2, (q % 2) * 256:(q % 2) * 256 + 256], lhsT=F[:, :, ch], rhs=T1[:, :],
                            start=True, stop=True), R=["F", "T1"], W=[("ps", b), ("ps", b + 1)], inc=(q == 3))
                    ev += 1
                    dst = Ast[:, g4 * 4:(g4 + 1) * 4, :].rearrange("p a b -> p (a b)")
                    src = PS[0:64, b:b + 2, :].rearrange("p a b -> p (a b)")
                    if ev % 2:
                        em.op("dve", lambda dst=dst, src=src: V.tensor_copy(out=dst, in_=src),
                              R=[("ps", b), ("ps", b + 1)], W=["Ast"])
                    else:
                        em.op("act", lambda dst=dst, src=src: A.copy(out=dst, in_=src),
                              R=[("ps", b), ("ps", b + 1)], W=["Ast"])
                for kb in range(32):
                    b = 4 + kb % 2
                    for q in range(4):
                        k1 = kb * 4 + q
                        o = PS[hc * 64:(hc + 1) * 64, b, q * 128:(q + 1) * 128]
                        em.op("pe", lambda o=o, k1=k1: T.matmul(o, lhsT=Ast[:, :, k1], rhs=M2[:, k1, 64:192],
                                                                start=True, stop=False),
                              R=["Ast", "M2"], W=[("ps", b)], inc=False)
                        em.op("pe", lambda o=o, k1=k1: T.matmul(o, lhsT=Ast[:, :, 128 + k1], rhs=M2[:, k1, 0:128],
                                                                start=False, stop=True),
                              R=["Ast", "M2"], W=[("ps", b)], inc=(q == 3))
                    ev += 1
                    src = PS[hc * 64:(hc + 1) * 64, b, :].rearrange("p (k r c) -> p r c k", k=4, r=2)
                    dst = Y[hc * 64:(hc + 1) * 64, :, :].rearrange("p r (c k) -> p r c k", k=128)[:, :, :, kb * 4:(kb + 1) * 4]
                    if ev % 2:
                        em.op("dve", lambda dst=dst, src=src: V.tensor_copy(out=dst, in_=src), R=[("ps", b)], W=["Y"])
                    else:
                        em.op("act", lambda dst=dst, src=src: A.copy(out=dst, in_=src), R=[("ps", b)], W=["Y"])
            for tl in range(16):
                b = 6 + tl % 2
                em.op("pe", lambda b=b, tl=tl: T.matmul(PS[:, b, :], lhsT=CSf[:, 0:128], rhs=Y[:, 0, tl * 512:(tl + 1) * 512],
                                                        start=True, stop=False), R=["CSf", "Y"], W=[("ps", b)], inc=False)
                em.op("pe", lambda b=b, tl=tl: T.matmul(PS[:, b, :], lhsT=CSf[:, 128:256], rhs=Y[:, 1, tl * 512:(tl + 1) * 512],
                                                        start=False, stop=True), R=["CSf", "Y"], W=[("ps", b)], inc=True)
                fs = (tl // 4) % 2
                em.op("act", lambda b=b, tl=tl, fs=fs: A.copy(out=fst[fs][:, (tl % 4) * 512:(tl % 4 + 1) * 512], in_=PS[:, b, :]),
                      R=[("ps", b)], W=[f"fst{fs}"])
                if tl % 4 == 3:
                    t0 = (tl // 4) * 2048
                    em.dma("pool", C.MIXT[512 + cc * 128:512 + (cc + 1) * 128, t0:t0 + 2048], fst[fs][:],
                           R=[f"fst{fs}"], W=[("MIXT", 4 + cc)])
            for tc in range(2):
                em.op("pe", lambda tc=tc, cc=cc: T.matmul(PS[:, 0, 0:256], lhsT=fTc[:, cc, tc * 128:(tc + 1) * 128], rhs=CSc[:, :],
                                                          start=True, stop=True), R=["fTc", "CSc"], W=[("ps", 0)])
                em.op("dve", lambda tc=tc: V.tensor_copy(out=Gc[:, tc, :], in_=PS[:, 0, 0:256]), R=[("ps", 0)], W=["Gc"])
            for i_, (tc, part) in enumerate([(0, 0), (0, 1), (1, 0), (1, 1)]):
                em.op("pe", lambda i_=i_, tc=tc, part=part: T.matmul(
                    PS[:, 1, 0:256], lhsT=Gc[:, tc, part * 128:(part + 1) * 128], rhs=T256[:, tc, part * 256:(part + 1) * 256],
                    start=(i_ == 0), stop=(i_ == 3)), R=["Gc", "T256"], W=[("ps", 1)], inc=(i_ == 3))
            em.op("dve", lambda: V.tensor_copy(out=fcs[:, :], in_=PS[:, 1, 0:256]), R=[("ps", 1)], W=["fcs"])
            em.dma("pool", C.MIXT[512 + cc * 128:512 + (cc + 1) * 128, S:NT], fcs[:], R=["fcs"], W=[("MIXTc", 4 + cc)])
    C.es_mix.close()


PHASES += [("A", phase_A), ("C", phase_C)]


def phase_B(C):
    nc, em, PS = C.nc, C.em, C.PS
    V, A, G, T = nc.vector, nc.scalar, nc.gpsimd, nc.tensor
    TW = 2048
    with ExitStack() as ps_:
        sb = lambda n, s, d=F32: ps_.enter_context(nc.sbuf_tensor(n, list(s), d))
        bufA = sb("bufA", [128, NT]); bufB = sb("bufB", [128, 8460]); uc = sb("ucb_", [128, NT]); ucb = sb("ucbb", [128, NT], BF16)
        rt = sb("rt", [128, TW])
        it_ = [sb(f"it{i}", [128, TW]) for i in range(2)]; at = [sb(f"at{i}", [128, TW]) for i in range(2)]
        st = [sb(f"st{i}", [128, TW]) for i in range(2)]
        gtmp = [sb(f"gtmp{i}", [128, 1024]) for i in range(2)]
        wbd = sb("wbds", [128, 16, 128], BF16)
        cdg = sb("cdg", [128, 16, 128])
        lco = sb("lco", [128, 4, 2, 3]); cvc = sb("cvc", [128, 4, 5]); cA = sb("cA", [128, 4, 2]); carry = sb("carry", [128, 2])
        em.dma("pool", wbd[:], C.wbd_d.rearrange("g d c p n -> p (g d c) n"), W=["wbd"])
        em.dma("sp", cdg[:], C.cdiag_d.rearrange("c k p n -> p (c k) n"), W=["cdg"])
        em.dma("sp", lco[:], C.lcols_d, W=["lco"])
        em.dma("sp", cvc[:], C.convc_d, W=["cvc"])
        em.op("act", lambda: A.activation(out=cA[:], in_=lco[:, :, :, 2], func=AF.Exp, scale=-1.0), R=["lco"], W=["cA"])
        em.op("act", lambda: A.activation(out=cA[:], in_=cA[:], func=AF.Ln, bias=1.0), R=["cA"], W=["cA"])
        em.op("dve", lambda: V.tensor_scalar(out=cA[:], in0=cA[:], scalar1=-8.0, scalar2=None, op0=ALU.mult), R=["cA"], W=["cA"])
        em.op("dve", lambda: V.memset(bufB[:], 0.0), W=["bufB"])
        XO = 8200
        tiles = [(S, NT)] + [(i * TW, (i + 1) * TW) for i in range(4)]
        tix = 0
        for c in range(4):
            cv_emit(C, 2)
            em.dma("sp", bufA[:, 0:S], C.UG[c], W=["bufA"])
            em.dma("sp", bufA[:, S:NT], C.UGc[c], W=["bufA"])
            for qd in range(4):
                o_ = bufB[:, 2 + qd * 2048:2 + (qd + 1) * 2048].rearrange("p (a b) -> p a b", b=64)
                i_ = bufA[:, 0:S].rearrange("p (b a) -> p a b", a=128)[:, qd * 32:(qd + 1) * 32, :]
                eng = QENG[qd]
                fn = {"act": (lambda o_=o_, i_=i_: A.copy(out=o_, in_=i_)),
                      "dve": (lambda o_=o_, i_=i_: V.tensor_copy(out=o_, in_=i_)),
                      "pool": (lambda o_=o_, i_=i_: G.tensor_copy(out=o_, in_=i_))}[eng]
                em.op(eng, fn, R=["bufA"], W=[f"bufBq{qd}"])
            em.op("pool", lambda: G.tensor_copy(out=bufB[:, XO + 2:XO + 2 + LC], in_=bufA[:, S:NT]), R=["bufA"], W=["bufBc"])
            em.dma("sp", bufA[:, 0:S], C.UG[4 + c], W=["bufA"])
            em.dma("sp", bufA[:, S:NT], C.UGc[4 + c], W=["bufA"])
            for (o0, src0, n) in ((0, 0, S), (S, XO, LC)):
                for q0 in range(0, n, 2048):
                    nn = min(2048, n - q0)
                    nq = (nn + 511) // 512
                    pb = 4 * ((q0 // 2048) % 2)
                    qd = q0 // 2048
                    rk = ["bufBc", "bufB"] if n == LC else [f"bufBq{x}" for x in (qd - 1, qd, qd + 1) if 0 <= x < 4] + ["bufB"]
                    for q in range(nq):
                        w_ = min(512, nn - q * 512)
                        for k in range(4):
                            em.op("pe", lambda q=q, k=k, w_=w_, pb=pb, src0=src0, q0=q0, c=c: T.matmul(
                                PS[:, pb + q, 0:w_], lhsT=cdg[:, c * 4 + k, :],
                                rhs=bufB[:, src0 + q0 + q * 512 + k:src0 + q0 + q * 512 + k + w_],
                                start=(k == 0), stop=(k == 3)), R=["cdg"] + rk, W=[("ps", pb + q)], inc=(k == 3))
                    src = PS[:, pb:pb + 4, :].rearrange("p a b -> p (a b)")[:, 0:nn]
                    em.op("act", lambda src=src, o0=o0, q0=q0, nn=nn, c=c: A.activation(
                        out=uc[:, o0 + q0:o0 + q0 + nn], in_=src, func=AF.Identity, bias=cvc[:, c, 4:5]),
                          R=[("ps", pb + q) for q in range(4)] + ["cvc"], W=["uc"])
                    em.op("act", lambda src=src, o0=o0, q0=q0, nn=nn, c=c: A.activation(
                        out=ucb[:, o0 + q0:o0 + q0 + nn], in_=src, func=AF.Identity, bias=cvc[:, c, 4:5]),
                          R=[("ps", pb + q) for q in range(4)] + ["cvc"], W=["ucb"])

            def gelu_s1(lo, hi, p):
                n = hi - lo
                em.op("act", lambda: A.activation(out=gtmp[p][:, 0:n], in_=bufA[:, lo:hi], func=AF.Square),
                      R=["bufA"], W=[f"gt{p}"])
                em.op("pool", lambda: G.tensor_scalar(out=gtmp[p][:, 0:n], in0=gtmp[p][:, 0:n], scalar1=0.044715, scalar2=1.0,
                                                      op0=ALU.mult, op1=ALU.add), R=[f"gt{p}"], W=[f"gt{p}"])
                em.op("dve", lambda: V.tensor_tensor(out=gtmp[p][:, 0:n], in0=gtmp[p][:, 0:n], in1=bufA[:, lo:hi], op=ALU.mult),
                      R=[f"gt{p}", "bufA"], W=[f"gt{p}"])

            def gelu_s2(lo, hi, p):
                n = hi - lo
                em.op("act", lambda: A.activation(out=gtmp[p][:, 0:n], in_=gtmp[p][:, 0:n], func=AF.Sigmoid,
                                                  scale=1.5957691216057308), R=[f"gt{p}"], W=[f"gt{p}"])
                em.op("dve", lambda: V.tensor_tensor(out=bufA[:, lo:hi], in0=gtmp[p][:, 0:n], in1=bufA[:, lo:hi], op=ALU.mult),
                      R=[f"gt{p}", "bufA"], W=["bufA"])

            gpieces = [(S, NT)] + [(i * 1024, (i + 1) * 1024) for i in range(8)]
            gstate = {"s1": 0, "s2": 0}

            def gelu_push1():
                k = gstate["s1"]
                if k < len(gpieces):
                    gelu_s1(gpieces[k][0], gpieces[k][1], k % 2)
                    gstate["s1"] += 1

            def gelu_push2():
                k = gstate["s2"]
                if k < gstate["s1"]:
                    gelu_s2(gpieces[k][0], gpieces[k][1], k % 2)
                    gstate["s2"] += 1

            for d in range(2):
                order = [tiles[0]] + (tiles[1:] if d == 0 else tiles[1:][::-1])
                for ti, (lo, hi) in enumerate(order):
                    p = tix % 2
                    tix += 1
                    n = hi - lo
                    nq = (n + 511) // 512
                    for gi in range(2):
                        for q in range(nq):
                            w_ = min(512, n - q * 512)
                            em.op("pe", lambda gi=gi, q=q, w_=w_, lo=lo, d=d, c=c: T.matmul(
                                PS[:, gi * 4 + q, 0:w_], lhsT=wbd[:, gi * 8 + d * 4 + c, :], rhs=ucb[:, lo + q * 512:lo + q * 512 + w_],
                                start=True, stop=True), R=["wbd", "ucb"], W=[("ps", gi * 4 + q)], inc=(q == nq - 1))
                    pr = PS[:, 0:4, :].rearrange("p a b -> p (a b)")[:, 0:n]
                    pi = PS[:, 4:8, :].rearrange("p a b -> p (a b)")[:, 0:n]
                    if d == 0:
                        gelu_push2()
                        gelu_push2()
                    em.op("act", lambda pr=pr, n=n, c=c, d=d: A.activation(out=rt[:, 0:n], in_=pr, func=AF.Sigmoid,
                                                                           bias=lco[:, c, d, 0:1]),
                          R=[("ps", q) for q in range(4)] + ["lco"], W=["rt"])
                    em.op("act", lambda pi=pi, n=n, c=c, d=d, p=p: A.activation(out=it_[p][:, 0:n], in_=pi, func=AF.Sigmoid,
                                                                                bias=lco[:, c, d, 1:2]),
                          R=[("ps", 4 + q) for q in range(4)] + ["lco"], W=[f"it{p}"])
                    em.op("act", lambda n=n, c=c, d=d, p=p: A.activation(out=at[p][:, 0:n], in_=rt[:, 0:n], func=AF.Exp,
                                                                         scale=cA[:, c, d:d + 1]), R=["rt", "cA"], W=[f"at{p}"])
                    em.op("act", lambda n=n, p=p: A.activation(out=st[p][:, 0:n], in_=at[p][:, 0:n], func=AF.Square),
                          R=[f"at{p}"], W=[f"st{p}"])
                    if d == 0:
                        gelu_push1()
                        gelu_push1()
                    em.op("act", lambda n=n, p=p: A.activation(out=st[p][:, 0:n], in_=st[p][:, 0:n], func=AF.Sqrt, scale=-1.0, bias=1.0),
                          R=[f"st{p}"], W=[f"st{p}"])
                    em.op("dve", lambda n=n, p=p: V.tensor_tensor(out=it_[p][:, 0:n], in0=it_[p][:, 0:n], in1=st[p][:, 0:n], op=ALU.mult),
                          R=[f"it{p}", f"st{p}"], W=[f"it{p}"])
                    em.op("dve", lambda n=n, lo=lo, hi=hi, p=p: V.tensor_tensor(out=it_[p][:, 0:n], in0=it_[p][:, 0:n], in1=uc[:, lo:hi],
                                                                               op=ALU.mult), R=[f"it{p}", "uc"], W=[f"it{p}"])
                    init = 0.0 if ti == 0 else carry[:, d:d + 1]
                    if d == 0:
                        em.op("dve", lambda n=n, lo=lo, hi=hi, init=init, p=p: V.tensor_tensor_scan(
                            out=bufB[:, lo:hi], data0=at[p][:, 0:n], data1=it_[p][:, 0:n], initial=init, op0=ALU.mult, op1=ALU.add),
                              R=[f"at{p}", f"it{p}", "carry", "bufB", "uc"], W=["bufB", "bufBq0", "bufBq1", "bufBq2", "bufBq3", "bufBc"])
                        em.op("dve", lambda hi=hi: V.tensor_copy(out=carry[:, 0:1], in_=bufB[:, hi - 1:hi]), R=["bufB"], W=["carry"])
                    else:
                        em.op("dve", lambda n=n, init=init, p=p: V.tensor_tensor_scan(
                            out=st[p][:, 0:n][:, ::-1], data0=at[p][:, 0:n][:, ::-1],
                            data1=it_[p][:, 0:n][:, ::-1], initial=init, op0=ALU.mult, op1=ALU.add),
                              R=[f"at{p}", f"it{p}", "carry", f"st{p}"], W=[f"st{p}"])
                        em.op("dve", lambda p=p: V.tensor_copy(out=carry[:, 1:2], in_=st[p][:, 0:1]), R=[f"st{p}"], W=["carry"])
                        em.op("dve", lambda n=n, lo=lo, hi=hi, p=p: V.tensor_tensor(out=bufB[:, lo:hi], in0=bufB[:, lo:hi], in1=st[p][:, 0:n],
                                                                                    op=ALU.add), R=["bufB", f"st{p}"], W=["bufB"])
            while gstate["s2"] < len(gpieces):
                gelu_push1()
                gelu_push2()
            em.op("dve", lambda: V.tensor_tensor(out=ucb[:, 0:S].rearrange("p (a b) -> p a b", b=64),
                                                 in0=bufB[:, 0:S].rearrange("p (a b) -> p a b", b=64),
                                                 in1=bufA[:, 0:S].rearrange("p (b a) -> p a b", a=128), op=ALU.mult),
                  R=["bufB", "bufBq0", "bufBq1", "bufBq2", "bufBq3", "bufBc", "bufA", "ucb"], W=["ucb"])
            em.op("dve", lambda: V.tensor_tensor(out=ucb[:, S:NT], in0=bufB[:, S:NT], in1=bufA[:, S:NT], op=ALU.mult),
                  R=["bufB", "bufBq0", "bufBq1", "bufBq2", "bufBq3", "bufBc", "bufA", "ucb"], W=["ucb"])
            em.dma("pool", C.MIXT[c * 128:(c + 1) * 128, :], ucb[:, :], R=["ucb"], W=[("MIXT", c)])
            if c < 3:
                em.op("dve", lambda: V.memset(bufB[:, 0:2], 0.0), R=["bufB"], W=["bufB", "bufBq0"])
                em.op("dve", lambda: V.memset(bufB[:, S:8460], 0.0), R=["bufB"], W=["bufB", "bufBq3", "bufBc"])


def _tok_tiles(ntok_tile):
    return None


def phase_D1(C, layer=0):
    nc, em, PS = C.nc, C.em, C.PS
    V, A, G, T = nc.vector, nc.scalar, nc.gpsimd, nc.tensor
    with ExitStack() as ps_:
        sb = lambda n, s, d=F32: ps_.enter_context(nc.sbuf_tensor(n, list(s), d))
        wo = sb("D1wo", [128, 8, D], BF16)
        xt = [sb(f"D1xt{i}", [128, 4, D]) for i in range(3)]
        mt = [sb(f"D1mt{i}", [128, 8, 512], BF16) for i in range(3)]
        mb = [sb(f"D1mb{i}", [128, 8, 512], BF16) for i in range(3)]
        hs = sb("D1hs", [128, 2])
        Gx = sb("D1Gx", [128, D]); Gc = sb("D1Gc", [128, D])
        ssq = sb("D1ssq", [128, 4]); rs = sb("D1rs", [128, 4]); junk = sb("D1junk", [128, D], BF16); tt = sb("D1tt", [128, D])
        _wload_bf(C, wo, C.WBF["woab"].rearrange("(k p) n -> p k n", p=128), "D1wo", D, "cvwoab")
        em.dma("sp", hs[:], C.hsel_d, W=["D1hs"])
        em.dma("sp", Gx[:], C.GROW[0, 0, 0, :].partition_broadcast(128), R=[("GROW", 0, 0)], W=["D1Gx"])
        em.dma("sp", Gc[:], C.GROW[0, 0, 1, :].partition_broadcast(128), R=[("GROW", 0, 0)], W=["D1Gc"])
        mixv = C.MIXT.rearrange("(k p) t -> p k t", p=128)
        NTL = NLT + 1

        def load(j, sl):
            if j < NLT:
                em.dma("sp", xt[sl][:], C.xloc_d[j * 512:(j + 1) * 512, :].rearrange("(s p) d -> p s d", p=128), W=[f"D1xt{sl}"])
                em.dma("sp", mt[sl][:], mixv[:, :, j * 512:(j + 1) * 512], W=[f"D1mt{sl}"])
                lb = 4096 + j * 512 if j < 8 else LBASE
                em.dma("sp", mb[sl][:], mixv[:, :, lb:lb + 512], W=[f"D1mb{sl}"])
                em.op("act", lambda sl=sl: A.activation(out=mt[sl][:].rearrange("p a b -> p (a b)"),
                                                        in_=mt[sl][:].rearrange("p a b -> p (a b)"), func=AF.Copy,
                                                        scale=hs[:, 0:1]), R=[f"D1mt{sl}", "D1hs"], W=[f"D1mt{sl}"])
                em.op("dve", lambda sl=sl: V.scalar_tensor_tensor(
                    out=mt[sl][:].rearrange("p a b -> p (a b)"), in0=mb[sl][:].rearrange("p a b -> p (a b)"), scalar=hs[:, 1:2],
                    in1=mt[sl][:].rearrange("p a b -> p (a b)"), op0=ALU.mult, op1=ALU.add),
                      R=[f"D1mt{sl}", f"D1mb{sl}", "D1hs"], W=[f"D1mt{sl}"])
            else:
                em.dma("sp", xt[sl][:, 0:2, :], C.ctx_d.rearrange("(s p) d -> p s d", p=128), W=[f"D1xt{sl}"])
                em.dma("sp", mt[sl][:, :, 0:LC], mixv[:, :, S:NT], W=[f"D1mt{sl}"])
        load(0, 0)
        load(1, 1)
        for j in range(NTL):
            sl = j % 3
            if j + 2 < NTL:
                load(j + 2, (j + 2) % 3)
            isx = j < NLT
            nsub = 4 if isx else 2
            for s in range(nsub):
                pb = 2 * (s % 4)
                for h in range(2):
                    for k in range(8):
                        em.op("pe", lambda k=k, h=h, s=s, sl=sl, pb=pb: T.matmul(
                            PS[:, pb + h, :], lhsT=mt[sl][:, k, s * 128:(s + 1) * 128], rhs=wo[:, k, h * 512:(h + 1) * 512],
                            start=(k == 0), stop=(k == 7)), R=["D1wo0", f"D1mt{sl}"], W=[("ps", pb + h)], inc=(k == 7))
                C.epilogue(pb, xt[sl][:, s, :], f"D1xt{sl}", (Gx if isx else Gc)[:, :], "D1Gx" if isx else "D1Gc",
                           ssq, rs, junk, tt, "D1")
            if isx:
                em.dma("pool", C.X1[j * 512:(j + 1) * 512, :].rearrange("(s p) d -> p s d", p=128), xt[sl][:],
                       R=[f"D1xt{sl}"], W=[("X1", j)])
            else:
                em.dma("pool", C.C1.rearrange("(s p) d -> p s d", p=128), xt[sl][:, 0:2, :], R=[f"D1xt{sl}"], W=["C1"])


def phase_FFN(C, layer, Xin, Cin, Xout, Cout, inkey, outkey, ntok=NL):
    nc, em, PS = C.nc, C.em, C.PS
    V, A, G, T = nc.vector, nc.scalar, nc.gpsimd, nc.tensor
    tg = f"F{layer}"
    with ExitStack() as ps_:
        sb = lambda n, s, d=F32: ps_.enter_context(nc.sbuf_tensor(tg + n, list(s), d))
        wg = sb("wg", [128, 8, FH], BF16); wu = sb("wu", [128, 8, FH], BF16); wd = sb("wd", [128, NJ, D], BF16)
        xt = sb("xt", [128, 4, D]); hT = sb("hT", [128, 8, 512], BF16); hh = sb("hh", [128, NJ, 512], BF16)
        est = [sb(f"est{i}", [128, D]) for i in range(2)]
        Gx = sb("Gx", [128, D]); tt = sb("tt", [128, D]); junk = sb("junk", [128, D], BF16)
        ssq = sb("ssq", [128, 4]); rs = sb("rs", [128, 4]); ssq2 = sb("ssq2", [128, 4]); rs2 = sb("rs2", [128, 4])
        NX = ntok // 512
        ntile = NX + (1 if Cin is not None else 0)

        def nsub_of(j):
            return 4 if j < NX else 2

        def rows(j, s):
            if j < NX:
                r0 = j * 512 + s * 128
                return Xin[r0:r0 + 128, :], Xout[r0:r0 + 128, :]
            return Cin[s * 128:(s + 1) * 128, :], Cout[s * 128:(s + 1) * 128, :]

        def load(j):
            ns = nsub_of(j)
            src = Xin[j * 512:(j + 1) * 512, :] if j < NX else Cin
            em.dma("sp", xt[:, 0:ns, :], src.rearrange("(s p) d -> p s d", p=128), W=[tg + "xt"])

        def pro_a(j):
            ns = nsub_of(j)
            C.prologue_a(xt, tg + "xt", ns, xt, tg + "xt", ssq2, rs2, junk, tg + "p")

        def pro_b(j):
            path = 0 if j < NX else 1
            C.prologue_b(nsub_of(j), xt, tg + "xt", C.MODC[:, layer, 1, 0, path, :], C.MODC[:, layer, 1, 1, path, :],
                         hT, tg + "hT", [0, 1, 2, 3])

        def gateup(j):
            n = nsub_of(j) * 128
            for jj in range(NJ):
                b = 2 * (jj % 2)
                for gi, w_ in enumerate((wg, wu)):
                    for k in range(8):
                        em.op("pe", lambda k=k, b=b, gi=gi, w_=w_, jj=jj: T.matmul(
                            PS[:, b + gi, 0:n], lhsT=w_[:, k, jj * 128:(jj + 1) * 128], rhs=hT[:, k, 0:n],
                            start=(k == 0), stop=(k == 7)), R=[tg + "wg" + str(x) for x in {jj * 128 // 704, (jj * 128 + 127) // 704}] + [tg + "wu" + str(x) for x in {jj * 128 // 704, (jj * 128 + 127) // 704}] + [tg + "hT"], W=[("ps", b + gi)],
                              inc=(k == 7))
                em.op("act", lambda b=b, jj=jj: A.activation(out=hh[:, jj, 0:n], in_=PS[:, b, 0:n], func=AF.Silu),
                      R=[("ps", b)], W=[tg + f"hh{jj}"])
                em.op("dve", lambda b=b, jj=jj: V.tensor_tensor(out=hh[:, jj, 0:n], in0=hh[:, jj, 0:n], in1=PS[:, b + 1, 0:n],
                                                               op=ALU.mult), R=[("ps", b + 1), tg + f"hh{jj}"], W=[tg + f"hh{jj}"])

        def down(j, s):
            pb = 4 + 2 * (s % 2)
            for h in range(2):
                for jj in range(NJ):
                    em.op("pe", lambda jj=jj, h=h, s=s, pb=pb: T.matmul(
                        PS[:, pb + h, :], lhsT=hh[:, jj, s * 128:(s + 1) * 128], rhs=wd[:, jj, h * 512:(h + 1) * 512],
                        start=(jj == 0), stop=(jj == NJ - 1)), R=[tg + "wd" + str(h), tg + f"hh{jj}"], W=[("ps", pb + h)], inc=(jj == NJ - 1))

        def eload(j, s):
            em.dma("sp", est[s % 2][:], rows(j, s)[0], W=[tg + f"est{s % 2}"])

        def epi(j, s):
            pb = 4 + 2 * (s % 2)
            C.epilogue(pb, est[s % 2][:, :], tg + f"est{s % 2}", Gx[:, :], tg + "Gx", ssq, rs, junk, tt, tg)
            em.dma("pool", rows(j, s)[1], est[s % 2][:], R=[tg + f"est{s % 2}"], W=[(outkey, j, s)])

        load(0)
        cv_emit(C, 0, need=f"cvwg{layer}"); cv_emit(C, 0, need=f"cvwu{layer}"); cv_emit(C, 0, need=f"cvwd{layer}")
        wgv = C.WBF["wg"][layer].rearrange("(k p) n -> p k n", p=128)
        wuv = C.WBF["wu"][layer].rearrange("(k p) n -> p k n", p=128)
        wdv = C.WBF["wd"][layer].rearrange("(k p) n -> p k n", p=128)
        for bi in range(4):
            em.dma("sp", wg[:, :, bi * 704:(bi + 1) * 704], wgv[:, :, bi * 704:(bi + 1) * 704], R=C.cvkeys[f"cvwg{layer}"], W=[tg + f"wg{bi}"])
            em.dma("sp", wu[:, :, bi * 704:(bi + 1) * 704], wuv[:, :, bi * 704:(bi + 1) * 704], R=C.cvkeys[f"cvwu{layer}"], W=[tg + f"wu{bi}"])
        for bi in range(2):
            em.dma("sp", wd[:, :, bi * 512:(bi + 1) * 512], wdv[:, :, bi * 512:(bi + 1) * 512], R=C.cvkeys[f"cvwd{layer}"], W=[tg + f"wd{bi}"])
        em.dma("sp", Gx[:], C.GROW[layer, 1, 0, :].partition_broadcast(128), W=[tg + "Gx"])
        pro_a(0)
        pro_b(0)
        for j in range(ntile):
            ns = nsub_of(j)
            if j == NX:
                em.dma("sp", Gx[:], C.GROW[layer, 1, 1, :].partition_broadcast(128), W=[tg + "Gx"])
            gateup(j)
            if layer == 0:
                cv_emit(C, 2)
            if j + 1 < ntile:
                load(j + 1)
                pro_a(j + 1)
            eload(j, 0)
            eload(j, 1)
            down(j, 0)
            down(j, 1)
            if j + 1 < ntile:
                pro_b(j + 1)
            epi(j, 0)
            epi(j, 1)
            if ns == 4:
                eload(j, 2)
                eload(j, 3)
                down(j, 2)
                down(j, 3)
                epi(j, 2)
                epi(j, 3)


PHASES += [("B", phase_B), ("D1", phase_D1),
           ("D2", lambda C: phase_FFN(C, 0, C.X1, C.C1, C.X2, C.C2, "X1", "X2"))]


def phase_E(C):
    nc, em, PS = C.nc, C.em, C.PS
    V, A, G, T = nc.vector, nc.scalar, nc.gpsimd, nc.tensor
    with ExitStack() as ps_:
        sb = lambda n, s, d=F32: ps_.enter_context(nc.sbuf_tensor("E" + n, list(s), d))
        wq = sb("wq", [128, 8, 3 * D], BF16)
        xt = [sb(f"xt{i}", [128, 4, D]) for i in range(2)]
        hT = [sb(f"hT{i}", [128, 8, 512], BF16) for i in range(2)]
        qk = [sb(f"qk{i}", [128, 16, 512], BF16) for i in range(2)]
        vst = [sb(f"vst{i}", [128, 4, D], BF16) for i in range(2)]
        ssq = sb("ssq", [128, 4]); rs = sb("rs", [128, 4]); junk = sb("junk", [128, D], BF16)
        QTv = C.QT.rearrange("(c p) t -> p c t", p=128); KTv = C.KT.rearrange("(c p) t -> p c t", p=128)
        NTL = NLT + 1
        evc = [0]

        def nsubE(j):
            return 2 if j == NLT else 4

        def load(j):
            sl = j % 2
            if j < NLT:
                em.dma("sp", xt[sl][:], C.X2[j * 512:(j + 1) * 512, :].rearrange("(s p) d -> p s d", p=128), W=[f"Ext{sl}"])
            else:
                em.dma("sp", xt[sl][:, 0:2, :], C.C2.rearrange("(s p) d -> p s d", p=128), W=[f"Ext{sl}"])

        def pro_a(j):
            sl = j % 2
            C.prologue_a(xt[sl], f"Ext{sl}", nsubE(j), xt[sl], f"Ext{sl}", ssq, rs, junk, "E")

        def pro_b(j):
            sl = j % 2
            path = 1 if j == NLT else 0
            C.prologue_b(nsubE(j), xt[sl], f"Ext{sl}", C.MODC[:, 1, 0, 0, path, :], C.MODC[:, 1, 0, 1, path, :],
                         hT[sl], f"EhT{sl}", [0, 1])

        def qkE(j):
            sl = j % 2
            isctx = j == NLT
            n = nsubE(j) * 128
            for oc in (range(8, 16) if isctx else range(16)):
                b = 2 + oc % 2
                for k in range(8):
                    em.op("pe", lambda k=k, b=b, oc=oc, n=n, sl=sl: T.matmul(
                        PS[:, b, 0:n], lhsT=wq[:, k, oc * 128:(oc + 1) * 128], rhs=hT[sl][:, k, 0:n],
                        start=(k == 0), stop=(k == 7)), R=["Ewq" + str(oc // 8), f"EhT{sl}"], W=[("ps", b)], inc=(k == 7))
                if oc < 8:
                    em.op("act", lambda b=b, oc=oc, sl=sl, n=n: A.activation(out=qk[sl][:, oc, 0:n], in_=PS[:, b, 0:n],
                                                                             func=AF.Copy, scale=0.125),
                          R=[("ps", b)], W=[f"Eqk{sl}"])
                else:
                    em.op("dve", lambda b=b, oc=oc, sl=sl, n=n: V.tensor_copy(out=qk[sl][:, oc, 0:n], in_=PS[:, b, 0:n]),
                          R=[("ps", b)], W=[f"Eqk{sl}"])
            if not isctx:
                em.dma("pool", QTv[:, :, j * 512:(j + 1) * 512], qk[sl][:, 0:8, :], R=[f"Eqk{sl}"], W=[("QT", j)])
                em.dma("pool", KTv[:, :, j * 512:(j + 1) * 512], qk[sl][:, 8:16, :], R=[f"Eqk{sl}"], W=[("KT", j)])
            else:
                em.dma("pool", KTv[:, :, NL:NL + LC], qk[sl][:, 8:16, 0:LC], R=[f"Eqk{sl}"], W=[("KT", j)])

        def vE(j):
            sl = j % 2
            isctx = j == NLT
            nsub = nsubE(j)
            for s in range(nsub):
                pb = 4 + 2 * (s % 2)
                for h in range(2):
                    for k in range(8):
                        em.op("pe", lambda k=k, h=h, s=s, pb=pb, sl=sl: T.matmul(
                            PS[:, pb + h, :], lhsT=hT[sl][:, k, s * 128:(s + 1) * 128], rhs=wq[:, k, 2048 + h * 512:2048 + (h + 1) * 512],
                            start=(k == 0), stop=(k == 7)), R=["Ewq2", f"EhT{sl}"], W=[("ps", pb + h)], inc=(k == 7))
                evc[0] += 1
                src = PS[:, pb:pb + 2, :].rearrange("p a b -> p (a b)")
                if evc[0] % 2:
                    em.op("dve", lambda s=s, sl=sl, src=src: V.tensor_copy(out=vst[sl][:, s, :], in_=src),
                          R=[("ps", pb), ("ps", pb + 1)], W=[f"Evst{sl}"])
                else:
                    em.op("act", lambda s=s, sl=sl, src=src: A.copy(out=vst[sl][:, s, :], in_=src),
                          R=[("ps", pb), ("ps", pb + 1)], W=[f"Evst{sl}"])
            t0 = NL if isctx else j * 512
            em.dma("pool", C.VD[t0:t0 + nsub * 128, :].rearrange("(s p) d -> p s d", p=128), vst[sl][:, 0:nsub, :],
                   R=[f"Evst{sl}"], W=[("VD", j)])

        load(0)
        _wload_bf(C, wq, C.WBF["wqkv"].rearrange("(k p) n -> p k n", p=128), "Ewq", 3 * D, "cvwqkv")
        load(1)
        pro_a(0)
        pro_b(0)
        for j in range(NTL):
            qkE(j)
            if j + 2 < NTL:
                load(j + 2)
            if j + 1 < NTL:
                pro_a(j + 1)
            vE(j)
            if j + 1 < NTL:
                pro_b(j + 1)


def phase_F(C):
    nc, em, PS = C.nc, C.em, C.PS
    V, A, G, T = nc.vector, nc.scalar, nc.gpsimd, nc.tensor
    with ExitStack() as ps_:
        sb = lambda n, s, d=F32: ps_.enter_context(nc.sbuf_tensor("AT" + n, list(s), d))
        wo = sb("wo", [128, 8, D], BF16)
        KTb = [sb(f"KTb{i}", [128, 8, 1024], BF16) for i in range(2)]
        Vb = [sb(f"Vb{i}", [128, 8, 16, 65], BF16) for i in range(2)]
        QTb = sb("QTb", [128, 8, 512], BF16); xt = sb("xt", [128, 4, D])
        KTc = sb("KTc", [128, 8, LC], BF16); Vc = sb("Vc", [128, 2, 16, 65], BF16)
        BTi = sb("BTi", [128, 16, 5, 128], BF16); BTe = sb("BTe", [128, 16, 8, 128], BF16)
        PT = [sb(f"PT{i}", [128, 1280], BF16) for i in range(2)]
        Ot = sb("Ot", [128, D]); OTt = sb("OTt", [128, 8, 128], BF16); rden = sb("rden", [128, 4])
        Gx = sb("Gx", [128, D]); ssq = sb("ssq", [128, 4]); rs = sb("rs", [128, 4]); junk = sb("junk", [128, D], BF16)
        tt = sb("tt", [128, D])
        KTv = C.KT.rearrange("(c p) t -> p c t", p=128); QTv = C.QT.rearrange("(c p) t -> p c t", p=128)
        em.dma("sp", KTc[:], KTv[:, :, NL:NL + LC], W=["AKTc"])
        for i in range(2):
            em.op("pool", lambda i=i: G.memset(Vb[i][:, :, :, 64:65], 1.0), W=[f"AVb{i}"])
        em.op("pool", lambda: G.memset(Vc[:, :, :, 64:65], 1.0), W=["AVc"])
        for c in range(2):
            em.dma("sp", Vc[:, c, :, 0:64], C.VD[NL + c * 128:NL + (c + 1) * 128, :].rearrange("p (h d) -> p h d", d=64), W=["AVc"])

        NB = 8

        def kbof(blk):
            return 0 if blk == 0 else (52 if blk == NB - 1 else 8 * blk - 4)

        def load(blk, sl):
            if blk == 0:
                pieces = [(0, 2, 68 * 64), (2, 6, 0)]
            else:
                pieces = [(0, 8, kbof(blk) * 64)]
            for (c0, ncn, t0) in pieces:
                em.dma("sp", KTb[sl][:, :, c0 * 128:(c0 + ncn) * 128], KTv[:, :, t0:t0 + ncn * 128], W=[f"AKTb{sl}"])
                for c in range(ncn):
                    tt0 = t0 + c * 128
                    em.dma("sp", Vb[sl][:, c0 + c, :, 0:64], C.VD[tt0:tt0 + 128, :].rearrange("p (h d) -> p h d", d=64),
                           W=[f"AVb{sl}"])
        load(0, 0)
        late = {"done": False}

        def late_loads():
            if not late["done"]:
                late["done"] = True
                em.dma("sp", BTi[:], C.BT_d, W=["ABTi"])
                _wload_bf(C, wo, C.WBF["wona"].rearrange("(k p) n -> p k n", p=128), "Awo", D, "cvwona")
                em.dma("sp", Gx[:], C.GROW[1, 0, 0, :].partition_broadcast(128), W=["AGx"])
        for blk in range(NB):
            sl = blk % 2
            r0 = 8 * blk
            edge = blk in (0, NB - 1)
            eb = 0 if blk == 0 else 1
            nloc = 8 if edge else 5
            nch = nloc + 2
            em.dma("sp", QTb[:], QTv[:, :, r0 * 64:r0 * 64 + 512], W=["AQTb"])
            em.dma("sp", xt[:], C.X2[blk * 512:(blk + 1) * 512, :].rearrange("(s p) d -> p s d", p=128), W=["Axt"])

            def sbase(h):
                return 0 if edge else 2 * (h % 2)

            def cpos(cl, h):
                if edge:
                    return (cl // 4, (cl % 4) * 128)
                return (2 * (h % 2) + cl // 4, (cl % 4) * 128)

            def qk(i, h):
                off = 0 if edge else i
                if edge and h == 0:
                    em.dma("sp", BTe[:], C.BTE_d[eb, i], W=["ABTe"])
                j = h // 2; e = h % 2
                p0, p1 = 64 * e, 64 * e + 64
                for cl in range(nch):
                    bk, co = cpos(cl, h)
                    o = PS[:, bk, co:co + 128]
                    if cl < nloc:
                        lt = KTb[sl][p0:p1, j, (off + cl) * 128:(off + cl + 1) * 128]
                    else:
                        lt = KTc[p0:p1, j, (cl - nloc) * 128:(cl - nloc + 1) * 128]
                    last = cl == nch - 1
                    em.op("pe", lambda o=o, lt=lt, j=j, p0=p0, p1=p1, i=i, cl=cl, last=last: T.matmul(
                        o, lhsT=lt, rhs=QTb[p0:p1, j, i * 128:(i + 1) * 128], start=(cl % 4 == 0),
                        stop=(edge and last)), R=[f"AKTb{sl}", "AKTc", "AQTb"], W=[("ps", bk)], inc=(edge and last))
                    if edge:
                        if cl in (3, 7):
                            em.op("pe", lambda h=h, bk=bk, cl=cl: T.matmul(
                                PS[:, bk, :], lhsT=C.identb[:, :], rhs=BTe[:, h, cl - 3:cl + 1, :].rearrange("p a b -> p (a b)"),
                                start=False, stop=True), R=["identb", "ABTe"], W=[("ps", bk)], inc=False)
                    else:
                        if cl == 3:
                            em.op("pe", lambda h=h, bk=bk: T.matmul(
                                PS[:, bk, :], lhsT=C.identb[:, :], rhs=BTi[:, h, 0:4, :].rearrange("p a b -> p (a b)"),
                                start=False, stop=True), R=["identb", "ABTi"], W=[("ps", bk)], inc=False)
                        if cl == 6:
                            em.op("pe", lambda h=h, bk=bk: T.matmul(
                                PS[:, bk, 0:128], lhsT=C.identb[:, :], rhs=BTi[:, h, 4, :],
                                start=False, stop=True), R=["identb", "ABTi"], W=[("ps", bk)], inc=True)

            def ex(i, h):
                b0 = sbase(h)
                nb = 3 if edge else 2
                src = PS[:, b0:b0 + nb, :].rearrange("p a b -> p (a b)")[:, 0:nch * 128]
                em.op("act", lambda src=src, h=h: A.activation(out=PT[h % 2][:, 0:nch * 128], in_=src, func=AF.Exp),
                      R=[("ps", b0 + q) for q in range(nb)], W=[f"APT{h % 2}"])

            def pv(i, h):
                off = 0 if edge else i
                ob = 4 + (h // 4) % 2
                so = (h % 4) * 128
                for c in range(nch):
                    rhs = Vb[sl][:, off + c, h, :] if c < nloc else Vc[:, c - nloc, h, :]
                    em.op("pe", lambda c=c, rhs=rhs, h=h, ob=ob, so=so: T.matmul(
                        PS[:, ob, so:so + 65], lhsT=PT[h % 2][:, c * 128:(c + 1) * 128], rhs=rhs,
                        start=(c == 0), stop=(c == nch - 1)), R=[f"APT{h % 2}", f"AVb{sl}", "AVc"], W=[("ps", ob)],
                          inc=(c == nch - 1))
                if h % 4 == 3:
                    em.op("dve", lambda ob=ob: V.reciprocal(out=rden[:, 0:4], in_=PS[:, ob, 64:512:128]),
                          R=[("ps", ob)], W=["Arden"])
                    for hh in range(4):
                        hd = h - 3 + hh
                        em.op("dve", lambda ob=ob, hh=hh, hd=hd: V.tensor_scalar(
                            out=Ot[:, hd * 64:(hd + 1) * 64], in0=PS[:, ob, hh * 128:hh * 128 + 64],
                            scalar1=rden[:, hh:hh + 1], scalar2=None, op0=ALU.mult), R=[("ps", ob), "Arden"], W=["AOt"])

            def fin(i):
                for k in range(8):
                    em.op("pe", lambda k=k: T.transpose(PS[:, 6 + k // 4, (k % 4) * 128:(k % 4 + 1) * 128],
                                                        Ot[:, k * 128:(k + 1) * 128], C.ident[:]),
                          R=["AOt", "ident"], W=[("ps", 6 + k // 4)], inc=(k % 4 == 3))
                for hb in range(2):
                    em.op("act" if hb else "dve",
                          (lambda hb=hb: A.copy(out=OTt[:, 4 * hb:4 * hb + 4, :].rearrange("p a b -> p (a b)"), in_=PS[:, 6 + hb, :])) if hb else
                          (lambda hb=hb: V.tensor_copy(out=OTt[:, 4 * hb:4 * hb + 4, :].rearrange("p a b -> p (a b)"), in_=PS[:, 6 + hb, :])),
                          R=[("ps", 6 + hb)], W=["AOTt"])
                for hf in range(2):
                    for k in range(8):
                        em.op("pe", lambda k=k, hf=hf: T.matmul(PS[:, 6 + hf, :], lhsT=OTt[:, k, :], rhs=wo[:, k, hf * 512:(hf + 1) * 512],
                                                                start=(k == 0), stop=(k == 7)),
                              R=["AOTt", "Awo0"], W=[("ps", 6 + hf)], inc=(k == 7))
                C.epilogue(6, xt[:, i, :], "Axt", Gx[:, :], "AGx", ssq, rs, junk, tt, "A")

            items = [(i, h) for i in range(4) for h in range(16)]
            qk(*items[0])
            late_loads()
            if blk + 1 < NB:
                load(blk + 1, 1 - sl)
            for n_, (i, h) in enumerate(items):
                nxt = items[n_ + 1] if n_ + 1 < len(items) else None
                if edge:
                    ex(i, h)
                    if nxt:
                        qk(*nxt)
                else:
                    if nxt:
                        qk(*nxt)
                    ex(i, h)
                pv(i, h)
                if h == 15:
                    fin(i)
            em.dma("pool", C.X3[blk * 512:(blk + 1) * 512, :].rearrange("(s p) d -> p s d", p=128), xt[:], R=["Axt"], W=[("X3", blk)])


PHASES += [("E", phase_E), ("F", phase_F),
           ("G", lambda C: phase_FFN(C, 1, C.X3, None, C.out_d, None, "X3", "OUT", ntok=4096))]
```
